# Optimizing a Trainium2 kernel written in Bass

```python
import math
import jax, jax.numpy as jnp
from jax import lax
import numpy as np

D_MODEL = 1024
BATCH = 8
SEQ = 2048
DEPTH = 4
DEC_BATCH = 128
DEC_SEQ = 1
PAST_LEN = 16384
PAGE_SIZE = 128

N_MIXERS = 4
MIX_W = D_MODEL
GROUP_W = MIX_W // N_MIXERS
HEAD_DIM = 64
HG_HEADS = GROUP_W // HEAD_DIM
GDN_HEADS = GROUP_W // HEAD_DIM
SSD_HEADS = GROUP_W // HEAD_DIM
RET_HEADS = GROUP_W // HEAD_DIM
SSD_GROUPS = 2
SSD_STATE = 128
CONV_W = 4
GDN_CONV_DIM = 3 * GROUP_W
SSD_CONV_DIM = GROUP_W + 2 * SSD_GROUPS * SSD_STATE
N_META = 16
CHUNK = 64
ROPE_BASE = 10000.0
EPS = 1e-6
TINY = 1e-30
DT_MIN = 1e-3
DT_MAX = 1e-1
QK_SCALE = HEAD_DIM ** -0.5
IN_SIZES = (GROUP_W, GROUP_W, GROUP_W, GROUP_W,
            GDN_CONV_DIM, GROUP_W, GDN_HEADS, GDN_HEADS,
            SSD_CONV_DIM, GROUP_W, SSD_HEADS,
            GROUP_W, GROUP_W, GROUP_W, GROUP_W)
IN_DIM = sum(IN_SIZES)
F32 = jnp.float32

kernel_name = 'hymba_style_hgrn2_gdn_ssd_retention_step'


def group_rms_norm(x, w, groups):
    b, t, width = x.shape
    xg = x.astype(F32).reshape(b, t, groups, width // groups)
    xg = xg * lax.rsqrt(jnp.mean(xg * xg, axis=-1, keepdims=True) + EPS)
    return xg.reshape(b, t, width) * w


def rms_norm(x, w):
    return group_rms_norm(x, w, 1)


def group_norm(x, w, bias, groups):
    b, t, width = x.shape
    xg = x.astype(F32).reshape(b, t, groups, width // groups)
    mu = jnp.mean(xg, axis=-1, keepdims=True)
    xc = xg - mu
    xg = xc * lax.rsqrt(jnp.mean(xc * xc, axis=-1, keepdims=True) + EPS)
    return xg.reshape(b, t, width) * w + bias


def l2_normalize(x):
    return x * lax.rsqrt(jnp.sum(x * x, axis=-1, keepdims=True) + EPS)


def to_heads(x, n):
    b, t, w = x.shape
    return x.reshape(b, t, n, w // n).transpose(0, 2, 1, 3)


def from_heads(o):
    b, h, t, d = o.shape
    return o.transpose(0, 2, 1, 3).reshape(b, t, h * d)


def to_chunks(a):
    b, h, t = a.shape[:3]
    return jnp.moveaxis(a.reshape(b, h, t // CHUNK, CHUNK, *a.shape[3:]), 2, 0)


def from_chunks(o):
    o = jnp.moveaxis(o, 0, 2)
    return o.reshape(o.shape[0], o.shape[1], -1, o.shape[-1])


def masked_exp(diff, mask):
    return jnp.where(mask, jnp.exp(jnp.where(mask, diff, 0.0)), 0.0)


def causal_conv(u, ctx, w, b=None):
    full = jnp.concatenate([ctx.astype(u.dtype), u], axis=1)
    y = lax.conv_general_dilated(full, w[:, None, :].astype(u.dtype), window_strides=(1,), padding='VALID',
                                 dimension_numbers=('NWC', 'WIO', 'NWC'), feature_group_count=u.shape[-1])
    if b is not None:
        y = y + b
    return y, full[:, full.shape[1] - (CONV_W - 1):]


def rotary(x, pos):
    half = x.shape[-1] // 2
    inv_freq = 1.0 / (ROPE_BASE ** jnp.linspace(0.0, 1.0, half, dtype=F32))
    ang = pos[:, None] * inv_freq[None, :]
    cos, sin = jnp.cos(ang), jnp.sin(ang)
    x1, x2 = x[..., :half], x[..., half:]
    return jnp.concatenate([x1 * cos - x2 * sin, x2 * cos + x1 * sin], axis=-1)


def chunk_scalar_decay(S, xs):
    q, k, v, g = xs
    c = g.shape[-1]
    G = jnp.cumsum(g, axis=-1)
    causal = jnp.tril(jnp.ones((c, c), bool))
    decay = masked_exp(G[..., :, None] - G[..., None, :], causal)
    scores = jnp.einsum('bhik,bhjk->bhij', q, k) * decay
    o = (jnp.einsum('bhij,bhjv->bhiv', scores, v)
         + jnp.einsum('bhik,bhkv->bhiv', q * jnp.exp(G)[..., None], S))
    g_last = G[..., -1:]
    S_new = (jnp.exp(g_last)[..., None] * S
             + jnp.einsum('bhjk,bhjv->bhkv', k * jnp.exp(g_last - G)[..., None], v))
    return S_new, o


def step_scalar_decay(S, xs):
    q, k, v, g = xs
    S = jnp.exp(g)[..., None, None] * S + k[..., :, None] * v[..., None, :]
    return S, jnp.einsum('bhk,bhkv->bhv', q, S)


def chunk_vector_decay(S, xs):
    q, k, v, g = xs
    c = g.shape[2]
    G = jnp.cumsum(g, axis=2)
    causal = jnp.tril(jnp.ones((c, c), bool))
    diff = G[:, :, :, None, :] - G[:, :, None, :, :]
    decay = masked_exp(diff, causal[:, :, None])
    scores = jnp.einsum('bhik,bhjk,bhijk->bhij', q, k, decay)
    o = (jnp.einsum('bhij,bhjv->bhiv', scores, v)
         + jnp.einsum('bhik,bhkv->bhiv', q * jnp.exp(G), S))
    g_last = G[:, :, -1]
    S_new = (jnp.exp(g_last)[..., None] * S
             + jnp.einsum('bhjk,bhjv->bhkv', k * jnp.exp(g_last[:, :, None] - G), v))
    return S_new, o


def step_vector_decay(S, xs):
    q, k, v, g = xs
    S = jnp.exp(g)[..., None] * S + k[..., :, None] * v[..., None, :]
    return S, jnp.einsum('bhk,bhkv->bhv', q, S)


def chunk_gated_delta(S, xs):
    q, k, v, g, beta = xs
    c = g.shape[-1]
    G = jnp.cumsum(g, axis=-1)
    causal = jnp.tril(jnp.ones((c, c), bool))
    strict = jnp.tril(jnp.ones((c, c), bool), -1)
    decay = masked_exp(G[..., :, None] - G[..., None, :], causal)
    m = jnp.where(strict, jnp.einsum('bhik,bhjk->bhij', k, k) * decay * beta[..., :, None], 0.0)
    lhs = jnp.eye(c, dtype=m.dtype) + m
    rhs = jnp.concatenate([v * beta[..., None], k * (beta * jnp.exp(G))[..., None]], axis=-1)
    sol = lax.linalg.triangular_solve(lhs, rhs, left_side=True, lower=True, unit_diagonal=True)
    dv = v.shape[-1]
    u = sol[..., :dv] - jnp.einsum('bhik,bhkv->bhiv', sol[..., dv:], S)
    qk = jnp.einsum('bhik,bhjk->bhij', q, k) * decay
    o = (jnp.einsum('bhik,bhkv->bhiv', q * jnp.exp(G)[..., None], S)
         + jnp.einsum('bhij,bhjv->bhiv', qk, u))
    g_last = G[..., -1:]
    S_new = (jnp.exp(g_last)[..., None] * S
             + jnp.einsum('bhjk,bhjv->bhkv', k * jnp.exp(g_last - G)[..., None], u))
    return S_new, o


def step_gated_delta(S, xs):
    q, k, v, g, beta = xs
    S = jnp.exp(g)[..., None, None] * S
    u = beta[..., None] * (v - jnp.einsum('bhk,bhkv->bhv', k, S))
    S = S + k[..., :, None] * u[..., None, :]
    return S, jnp.einsum('bhk,bhkv->bhv', q, S)


def run_mixer(chunk_fn, step_fn, args, S0, prompt):
    if prompt:
        S, o_meta = chunk_fn(S0, tuple(a[:, :, :N_META] for a in args))
        S, o = lax.scan(chunk_fn, S, tuple(to_chunks(a[:, :, N_META:]) for a in args))
        return S, jnp.concatenate([o_meta, from_chunks(o)], axis=2)
    S, o = lax.scan(step_fn, S0, tuple(jnp.moveaxis(a, 2, 0) for a in args))
    return S, jnp.moveaxis(o, 0, 2)


def mixer(hn, pos, w_in_l, lb, hgrn_norm_w_l, gdn_conv_w_l, gdn_a_log_l, gdn_dt_bias_l, gdn_norm_w_l,
          ssd_conv_w_l, ssd_conv_b_l, ssd_a_log_l, ssd_dt_bias_l, ssd_d_l, ssd_norm_w_l,
          ret_norm_w_l, ret_norm_b_l, w_out_l,
          s_hgrn, s_gdn, gdn_ctx, s_ssd, ssd_ctx, s_ret, prompt):
    bsz, t, _ = hn.shape
    proj = jnp.einsum('btd,de->bte', hn, w_in_l)
    offsets = [int(o) for o in np.cumsum(IN_SIZES)[:-1]]
    (a_q, a_f, a_i, a_z, b_qkv, b_z, b_a, b_b, c_xbc, c_z, c_dt,
     d_q, d_k, d_v, d_z) = jnp.split(proj, offsets, axis=-1)

    a_k = (1.0 - lb) * jax.nn.sigmoid(-a_f)
    log_f = jnp.log(jnp.maximum(lb + (1.0 - lb) * jax.nn.sigmoid(a_f), TINY))
    s_hgrn, o_a = run_mixer(chunk_vector_decay, step_vector_decay,
                            (to_heads(jax.nn.silu(a_q), HG_HEADS) * QK_SCALE, to_heads(a_k, HG_HEADS),
                             to_heads(a_i, HG_HEADS), to_heads(log_f, HG_HEADS)), s_hgrn, prompt)
    y_a = group_rms_norm(from_heads(o_a), hgrn_norm_w_l, HG_HEADS) * jax.nn.silu(a_z)

    b_conv, gdn_ctx = causal_conv(b_qkv, gdn_ctx, gdn_conv_w_l)
    b_q, b_k, b_v = jnp.split(jax.nn.silu(b_conv), 3, axis=-1)
    b_g = jnp.swapaxes(-jnp.exp(gdn_a_log_l) * jax.nn.softplus(b_a + gdn_dt_bias_l), 1, 2)
    b_beta = jnp.swapaxes(jax.nn.sigmoid(b_b), 1, 2)
    s_gdn, o_b = run_mixer(chunk_gated_delta, step_gated_delta,
                           (l2_normalize(to_heads(b_q, GDN_HEADS)) * QK_SCALE, l2_normalize(to_heads(b_k, GDN_HEADS)),
                            to_heads(b_v, GDN_HEADS), b_g, b_beta), s_gdn, prompt)
    y_b = group_rms_norm(from_heads(o_b), gdn_norm_w_l, GDN_HEADS) * jax.nn.silu(b_z)

    c_conv, ssd_ctx = causal_conv(c_xbc, ssd_ctx, ssd_conv_w_l, ssd_conv_b_l)
    c_x, c_b, c_c = jnp.split(jax.nn.silu(c_conv), [GROUP_W, GROUP_W + SSD_GROUPS * SSD_STATE], axis=-1)
    dt_h = jnp.swapaxes(jax.nn.softplus(c_dt + ssd_dt_bias_l), 1, 2)
    x_h = to_heads(c_x, SSD_HEADS)
    b_h = jnp.repeat(to_heads(c_b, SSD_GROUPS), SSD_HEADS // SSD_GROUPS, axis=1)
    c_h = jnp.repeat(to_heads(c_c, SSD_GROUPS), SSD_HEADS // SSD_GROUPS, axis=1)
    s_ssd, o_c = run_mixer(chunk_scalar_decay, step_scalar_decay,
                           (c_h, b_h, x_h * dt_h[..., None], dt_h * (-jnp.exp(ssd_a_log_l))[:, None]),
                           s_ssd, prompt)
    o_c = o_c + ssd_d_l[:, None, None] * x_h
    y_c = group_rms_norm(from_heads(o_c) * jax.nn.silu(c_z), ssd_norm_w_l, SSD_GROUPS)

    ret_log_decay = jnp.log1p(-jnp.exp2(-5.0 - jnp.arange(RET_HEADS, dtype=F32)))
    ret_g = jnp.broadcast_to(ret_log_decay[None, :, None], (bsz, RET_HEADS, t))
    s_ret, o_d = run_mixer(chunk_scalar_decay, step_scalar_decay,
                           (rotary(to_heads(d_q, RET_HEADS), pos), rotary(to_heads(d_k, RET_HEADS), pos) * QK_SCALE,
                            to_heads(d_v, RET_HEADS), ret_g), s_ret, prompt)
    y_d = group_norm(from_heads(o_d), ret_norm_w_l, ret_norm_b_l, RET_HEADS) * jax.nn.silu(d_z)

    y = jnp.concatenate([y_a, y_b, y_c, y_d], axis=-1)
    out = jnp.einsum('bte,ed->btd', y, w_out_l)
    return out, (s_hgrn, s_gdn, gdn_ctx, s_ssd, ssd_ctx, s_ret)


def zero_states(bsz):
    return (jnp.zeros((bsz, HG_HEADS, HEAD_DIM, HEAD_DIM), F32),
            jnp.zeros((bsz, GDN_HEADS, HEAD_DIM, HEAD_DIM), F32),
            jnp.zeros((bsz, CONV_W - 1, GDN_CONV_DIM), F32),
            jnp.zeros((bsz, SSD_HEADS, SSD_STATE, HEAD_DIM), F32),
            jnp.zeros((bsz, CONV_W - 1, SSD_CONV_DIM), F32),
            jnp.zeros((bsz, RET_HEADS, HEAD_DIM, HEAD_DIM), F32))


def stack_states(states, i, like):
    return jnp.stack([s[i] for s in states]).astype(like.dtype)


def setup_inputs(seed: int = 0) -> dict:
    key = jax.random.key(seed)
    ks = jax.random.split(key, 28)

    def nrm(k, shape, scale):
        return scale * jax.random.normal(k, shape, F32)

    def gain(k, shape):
        return 1.0 + 0.05 * jax.random.normal(k, shape, F32)

    def a_log(k, n):
        return jnp.log(jax.random.uniform(k, (DEPTH, n), F32, 1.0, 16.0))

    def dt_bias(k, n):
        dt = jnp.exp(jax.random.uniform(k, (DEPTH, n), F32, math.log(DT_MIN), math.log(DT_MAX)))
        return dt + jnp.log(-jnp.expm1(-dt))

    return {
        'x_prompt': nrm(ks[0], (BATCH, SEQ, D_MODEL), 1.0),
        'x_sample': nrm(ks[1], (DEC_BATCH, DEC_SEQ, D_MODEL), 1.0),
        'state_hgrn': nrm(ks[2], (DEPTH, DEC_BATCH, HG_HEADS, HEAD_DIM, HEAD_DIM), 0.5),
        'state_gdn': nrm(ks[3], (DEPTH, DEC_BATCH, GDN_HEADS, HEAD_DIM, HEAD_DIM), 0.5),
        'state_gdn_conv': nrm(ks[4], (DEPTH, DEC_BATCH, CONV_W - 1, GDN_CONV_DIM), 1.0),
        'state_ssd': nrm(ks[5], (DEPTH, DEC_BATCH, SSD_HEADS, SSD_STATE, HEAD_DIM), 0.5),
        'state_ssd_conv': nrm(ks[6], (DEPTH, DEC_BATCH, CONV_W - 1, SSD_CONV_DIM), 1.0),
        'state_ret': nrm(ks[7], (DEPTH, DEC_BATCH, RET_HEADS, HEAD_DIM, HEAD_DIM), 2.0),
        'meta_tokens': nrm(ks[8], (N_META, D_MODEL), 1.0),
        'norm_w': gain(ks[9], (DEPTH, D_MODEL)),
        'w_in': nrm(ks[10], (DEPTH, D_MODEL, IN_DIM), D_MODEL ** -0.5),
        'hgrn_lb_logits': nrm(ks[11], (DEPTH, GROUP_W), 0.5),
        'hgrn_norm_w': gain(ks[12], (DEPTH, GROUP_W)),
        'gdn_conv_w': nrm(ks[13], (DEPTH, CONV_W, GDN_CONV_DIM), CONV_W ** -0.5),
        'gdn_a_log': a_log(ks[14], GDN_HEADS),
        'gdn_dt_bias': dt_bias(ks[15], GDN_HEADS),
        'gdn_norm_w': gain(ks[16], (DEPTH, GROUP_W)),
        'ssd_conv_w': nrm(ks[17], (DEPTH, CONV_W, SSD_CONV_DIM), CONV_W ** -0.5),
        'ssd_conv_b': nrm(ks[18], (DEPTH, SSD_CONV_DIM), 0.02),
        'ssd_a_log': a_log(ks[19], SSD_HEADS),
        'ssd_dt_bias': dt_bias(ks[20], SSD_HEADS),
        'ssd_d': gain(ks[21], (DEPTH, SSD_HEADS)),
        'ssd_norm_w': gain(ks[22], (DEPTH, GROUP_W)),
        'ret_norm_w': gain(ks[23], (DEPTH, GROUP_W)),
        'ret_norm_b': nrm(ks[24], (DEPTH, GROUP_W), 0.02),
        'w_out': nrm(ks[25], (DEPTH, MIX_W, D_MODEL), MIX_W ** -0.5),
        'final_norm_w': gain(ks[26], (D_MODEL,)),
    }


def reference(x_prompt, x_sample, state_hgrn, state_gdn, state_gdn_conv, state_ssd, state_ssd_conv, state_ret,
              meta_tokens, norm_w, w_in, hgrn_lb_logits, hgrn_norm_w, gdn_conv_w, gdn_a_log, gdn_dt_bias, gdn_norm_w,
              ssd_conv_w, ssd_conv_b, ssd_a_log, ssd_dt_bias, ssd_d, ssd_norm_w, ret_norm_w, ret_norm_b,
              w_out, final_norm_w):
    bsz = x_prompt.shape[0]
    lb_w = jax.nn.softmax(hgrn_lb_logits.astype(F32), axis=0)
    lb_all = jnp.maximum(jnp.cumsum(lb_w, axis=0) - lb_w[0], 0.0)

    meta = jnp.broadcast_to(meta_tokens.astype(F32)[None], (bsz, N_META, D_MODEL))
    hp = jnp.concatenate([meta, x_prompt.astype(F32)], axis=1)
    hs = x_sample.astype(F32)
    pos_p = jnp.arange(hp.shape[1], dtype=F32)
    pos_s = PAST_LEN + jnp.arange(hs.shape[1], dtype=F32)

    st_p, st_s = [], []
    for l in range(DEPTH):
        lw = (w_in[l], lb_all[l], hgrn_norm_w[l], gdn_conv_w[l], gdn_a_log[l], gdn_dt_bias[l], gdn_norm_w[l],
              ssd_conv_w[l], ssd_conv_b[l], ssd_a_log[l], ssd_dt_bias[l], ssd_d[l], ssd_norm_w[l],
              ret_norm_w[l], ret_norm_b[l], w_out[l])
        dp, sp = mixer(rms_norm(hp, norm_w[l]), pos_p, *lw, *zero_states(bsz), True)
        hp = hp + dp
        past = tuple(a[l].astype(F32) for a in (state_hgrn, state_gdn, state_gdn_conv,
                                                 state_ssd, state_ssd_conv, state_ret))
        ds, ss = mixer(rms_norm(hs, norm_w[l]), pos_s, *lw, *past, False)
        hs = hs + ds
        st_p.append(sp)
        st_s.append(ss)

    y_prompt = rms_norm(hp[:, N_META:], final_norm_w).astype(x_prompt.dtype)
    y_sample = rms_norm(hs, final_norm_w).astype(x_sample.dtype)
    return (y_prompt, y_sample,
            stack_states(st_p, 0, state_hgrn), stack_states(st_p, 1, state_gdn),
            stack_states(st_p, 2, state_gdn_conv), stack_states(st_p, 3, state_ssd),
            stack_states(st_p, 4, state_ssd_conv), stack_states(st_p, 5, state_ret),
            stack_states(st_s, 0, state_hgrn), stack_states(st_s, 1, state_gdn),
            stack_states(st_s, 2, state_gdn_conv), stack_states(st_s, 3, state_ssd),
            stack_states(st_s, 4, state_ssd_conv), stack_states(st_s, 5, state_ret))
```

```python
import contextlib
import math
import numpy as np
import concourse.bass as bass
import concourse.mybir as mybir
from concourse.bass_utils import run_bass_kernel_spmd

F32 = mybir.dt.float32
BF16 = mybir.dt.bfloat16
AF = mybir.ActivationFunctionType
ALU = mybir.AluOpType
AX = mybir.AxisListType

D = 1024
DEPTH = 4
SEQ = 2048
NT = 17
IN_DIM = 4108
EPS = 1e-6
QK = 0.125
NS = 16
NPRM = 1300
NEGV = -30000.0


class Prog:
    ENG = ("pe", "act", "dve", "pool", "sp")

    def __init__(self, nc, n_dma_sems=8):
        self.nc = nc
        self.ops = []
        self.cnt = {}
        self.clock = {e: {} for e in self.ENG}
        self.tok_clock = {}
        self.last_w = {}
        self.readers = {}
        self.n_dma = n_dma_sems
        self.dma_rr = {e: 0 for e in self.ENG}
        self.dma_last = {}
        import os
        self.pe_skip = not os.environ.get("PE_SELFWAIT")

    def _need(self, eng, tok, waits, force=False):
        key, idx = tok
        if key == "pe" and eng == "pe" and self.pe_skip and not force:
            return
        if self.clock[eng].get(key, 0) >= idx:
            return
        if waits.get(key, 0) < idx:
            waits[key] = idx

    def op(self, eng, fn, reads=(), writes=(), dma=False, pe_serial=False):
        waits = {}
        for b in reads:
            t = self.last_w.get(b)
            if t:
                self._need(eng, t, waits)
            if b.startswith("ps"):
                for r in self.readers.get(b, ()):
                    if r[0] != eng:
                        self._need(eng, r, waits)
        for b in writes:
            t = self.last_w.get(b)
            if t:
                self._need(eng, t, waits, force=pe_serial)
            for r in self.readers.get(b, ()):
                self._need(eng, r, waits)
        if dma:
            key = ("dma", eng, self.dma_rr[eng] % self.n_dma)
            self.dma_rr[eng] += 1
            prev = self.dma_last.get(key)
            if prev:
                self._need(eng, prev, waits)
        else:
            key = eng
        ck = self.clock[eng]
        for kk, ii in waits.items():
            for k2, i2 in self.tok_clock.get((kk, ii), {}).items():
                if ck.get(k2, 0) < i2:
                    ck[k2] = i2
            if ck.get(kk, 0) < ii:
                ck[kk] = ii
        self.cnt[key] = self.cnt.get(key, 0) + 1
        tok = (key, self.cnt[key])
        if dma:
            self.dma_last[key] = tok
            snap = dict(ck)
            snap[key] = tok[1]
            self.tok_clock[tok] = snap
        else:
            snap = dict(ck)
            snap[key] = tok[1]
            self.tok_clock[tok] = snap
        for b in writes:
            self.last_w[b] = tok
            self.readers[b] = []
        for b in reads:
            if b not in writes:
                self.readers.setdefault(b, []).append(tok)
        self.ops.append((eng, fn, list(waits.items()), tok, dma))
        return tok

    def emit(self, es, final_wait_eng="sp"):
        nc = self.nc
        import os
        km = int(os.environ.get("KMAX", "0"))
        if km:
            self.ops = self.ops[:km]
            self.dma_last = {}
            for (e_, f_, w_, tok_, d_) in self.ops:
                if d_:
                    self.dma_last[tok_[0]] = tok_
        needed = set()
        for (_, _, waits, _, _) in self.ops:
            for w in waits:
                needed.add(w)
        finals = []
        for k, t in self.dma_last.items():
            finals.append(t)
            needed.add(t)
        per_key = {}
        for (k, i) in needed:
            per_key.setdefault(k, []).append(i)
        sigcount = {}
        for k, lst in per_key.items():
            for n, i in enumerate(sorted(lst)):
                sigcount[(k, i)] = n + 1
        sems = {}
        for k in sorted(per_key.keys(), key=str):
            nm = "s_" + "_".join(str(x) for x in (k if isinstance(k, tuple) else (k,)))
            sems[k] = es.enter_context(nc.semaphore(nm))
        per_eng = {e: [] for e in self.ENG}
        for o in self.ops:
            per_eng[o[0]].append(o)
        blk = es.enter_context(nc.Block())

        def run(e, engobj):
            for (_, fn, waits, tok, dma) in per_eng[e]:
                for (k, i) in waits:
                    mult = 16 if isinstance(k, tuple) else 1
                    engobj.wait_ge(sems[k], sigcount[(k, i)] * mult)
                ins = fn(engobj)
                if tok in sigcount:
                    ins.then_inc(sems[tok[0]], 16 if dma else 1)
            if e == final_wait_eng:
                for t in finals:
                    engobj.wait_ge(sems[t[0]], sigcount[t] * 16)

        @blk.tensor
        def _(e):
            run("pe", e)

        @blk.scalar
        def _(e):
            run("act", e)

        @blk.vector
        def _(e):
            run("dve", e)

        @blk.gpsimd
        def _(e):
            run("pool", e)

        @blk.sync
        def _(e):
            run("sp", e)


def host_consts():
    idx = np.arange(128)
    ch = idx // 64
    same = ch[:, None] == ch[None, :]
    ident = np.eye(128, dtype=np.float32)
    maskT = (same & (idx[:, None] <= idx[None, :])).astype(np.float32)
    negT = np.where(maskT > 0, 0.0, NEGV).astype(np.float32)
    strict = (same & (idx[None, :] < idx[:, None]))
    negS = np.where(strict, 0.0, NEGV).astype(np.float32)
    mid = ch * 64 + 31
    uprime = (same & (idx[:, None] <= idx[None, :])).astype(np.float32) - \
             (same & (idx[:, None] <= mid[None, :])).astype(np.float32)
    urev = (same & (idx[:, None] > idx[None, :])).astype(np.float32)
    wc = np.zeros((128, 8), np.float32)
    wc[:, 0] = (idx <= 31)
    wc[:, 1] = (idx >= 64) & (idx <= 95)
    wc[:, 2] = (idx >= 32) & (idx <= 63)
    wc[:, 3] = (idx >= 96)
    wc[:, 4] = (idx <= 63)
    wc[:, 5] = (idx >= 64)
    blockones = same.astype(np.float32)
    lg = np.log1p(-np.exp2(-5.0 - np.arange(4, dtype=np.float64)))
    loc = idx % 64
    dt_ret = np.zeros((128, 4, 128), np.float64)
    for h in range(4):
        dt_ret[:, h, :] = np.where(maskT > 0, np.exp(lg[h] * (idx[None, :] - idx[:, None])), 0.0) * QK
    egq = np.zeros((128, 2, 128), np.float64)
    for hp in range(2):
        for hh in range(2):
            egq[hh * 64:(hh + 1) * 64, hp, :] = np.exp(lg[2 * hp + hh] * (loc[None, :] + 1))
    egrev64 = np.zeros((128, 4), np.float64)
    egrev16 = np.zeros((128, 4), np.float64)
    for h in range(4):
        egrev64[:, h] = np.exp(lg[h] * (63 - loc)) * QK
        egrev16[:, h] = np.exp(lg[h] * np.maximum(15 - idx, 0)) * QK
    egl = np.zeros((128, 2, 2), np.float64)
    for hp in range(2):
        for hh in range(2):
            egl[hh * 64:(hh + 1) * 64, hp, 0] = np.exp(lg[2 * hp + hh] * 16)
            egl[hh * 64:(hh + 1) * 64, hp, 1] = np.exp(lg[2 * hp + hh] * 64)
    sel = np.zeros((128, 4, 64), np.float32)
    selT = np.zeros((128, 4, 16), np.float32)
    gam64 = np.zeros((128, 4), np.float32)
    for h in range(4):
        for b in range(16):
            sel[b, h, h * 16 + b] = 1.0
            selT[h * 16 + b, h, b] = 1.0
            gam64[h * 16 + b, 0] = np.exp(lg[h])
    parts = [ident, maskT, negT, negS, uprime, urev, wc, blockones,
             dt_ret.reshape(128, 512), egq.reshape(128, 256), egrev64, egrev16, egl.reshape(128, 4),
             sel.reshape(128, 256), selT.reshape(128, 64), gam64]
    offs = {}
    names = ["ident", "maskT", "negT", "negS", "uprime", "urev", "wc", "blockones",
             "dt_ret", "egq", "egrev64", "egrev16", "egl", "sel", "selT", "gam64"]
    o = 0
    for nm, p in zip(names, parts):
        offs[nm] = (o, p.shape[1])
        o += p.shape[1]
    cst = np.concatenate([p.astype(np.float32) for p in parts], axis=1)
    half = 32
    inv_freq = (1.0 / (np.float32(10000.0) ** np.linspace(0.0, 1.0, half, dtype=np.float32))).astype(np.float32)
    rot = np.zeros((NT + 1, 128, 64), np.float32)
    for t in range(NT):
        pos = (np.arange(128) if t == 0 else 16 + (t - 1) * 128 + np.arange(128)).astype(np.float32)
        ang = (pos[:, None] * inv_freq[None, :]).astype(np.float32)
        rot[t, :, 0:32] = np.cos(ang)
        rot[t, :, 32:64] = np.sin(ang)
    ang = (np.full((128, 1), 16384.0, np.float32) * inv_freq[None, :]).astype(np.float32)
    rot[NT, :, 0:32] = np.cos(ang)
    rot[NT, :, 32:64] = np.sin(ang)
    return cst, offs, rot


CST, COFF, ROT = host_consts()
NCST = CST.shape[1]


def build_program(depth=DEPTH, ntiles=NT):
    nc = bass.Bass("TRN2", target_bir_lowering=False)

    def din(name, shape):
        return nc.dram_tensor(name, list(shape), F32, kind="ExternalInput").ap()

    def dout(name, shape):
        return nc.dram_tensor(name, list(shape), F32, kind="ExternalOutput").ap()

    xp = din("xp", [SEQ, D])
    meta = din("meta", [16, D])
    w_in = din("w_in", [DEPTH, D, IN_DIM])
    w_out = din("w_out", [DEPTH, D, D])
    cst_d = din("cst", [128, NCST])
    rot_d = din("rot", [NT + 1, 128, 64])
    prm_d = din("prm", [DEPTH, 128, NPRM])
    lgt_d = din("lgt", [128, 1024])
    finw_d = din("finw", [128, 1024])
    featp_d = din("featp", [128, 32 + 192])
    cbias_d = din("cbias", [1, DEPTH * 768])

    y_p = dout("y_p", [SEQ, D])
    st_hg = dout("st_hg", [DEPTH, 4, 64, 64])
    st_gd = dout("st_gd", [DEPTH, 4, 64, 64])
    st_gc = dout("st_gc", [DEPTH, 3, 768])
    st_sd = dout("st_sd", [DEPTH, 4, 128, 64])
    st_sc = dout("st_sc", [DEPTH, 3, 768])
    st_rt = dout("st_rt", [DEPTH, 4, 64, 64])
    xs_d = din("xs_in", [NS, D])
    si_hg = din("si_hg", [DEPTH, NS, 4, 64, 64]); si_gd = din("si_gd", [DEPTH, NS, 4, 64, 64])
    si_gc = din("si_gc", [DEPTH, NS, 3, 768]); si_sd = din("si_sd", [DEPTH, NS, 4, 128, 64])
    si_sc = din("si_sc", [DEPTH, NS, 3, 768]); si_rt = din("si_rt", [DEPTH, NS, 4, 64, 64])
    cwrow_d = din("cwrow", [DEPTH, 2, 4, NS, 768])
    cbrow_d = din("cbrow", [DEPTH, NS, 768])
    y_s = dout("y_s", [NS, D])
    so_hg = dout("so_hg", [DEPTH, NS, 4, 64, 64]); so_gd = dout("so_gd", [DEPTH, NS, 4, 64, 64])
    so_gc = dout("so_gc", [DEPTH, NS, 3, 768]); so_sd = dout("so_sd", [DEPTH, NS, 4, 128, 64])
    so_sc = dout("so_sc", [DEPTH, NS, 3, 768]); so_rt = dout("so_rt", [DEPTH, NS, 4, 64, 64])

    with contextlib.ExitStack() as es:
        def sb(name, shape, dt=F32):
            return es.enter_context(nc.sbuf_tensor("sb_" + name, list(shape), dt))

        P = Prog(nc)

        hscr = nc.dram_tensor("hscr", [NT * 128, D], F32, kind="Internal").ap()
        hb = [sb(f"hb{i}", [128, D]) for i in range(2)]
        win = sb("win", [128, 8, IN_DIM], BF16)
        wout = sb("wout", [128, 8, D], BF16)
        WCH = 1027
        wst = [sb(f"wst{i}", [128, WCH]) for i in range(2)]
        cst = sb("cst", [128, NCST])
        ident_bf = sb("ident_bf", [128, 128], BF16)
        bones_bf = sb("bones_bf", [128, 128], BF16)
        dtret_bf = sb("dtret_bf", [128, 4, 128], BF16)
        ones_bf = sb("ones_bf", [1, 128], BF16)
        prm = sb("prm", [128, NPRM])
        oml = sb("oml", [128, 4, 256])
        lgt = sb("lgt", [128, 4, 256])
        featp = sb("featp", [128, 32 + 192])
        dg = sb("dg", [128, 12, 4, 128], BF16)
        cbias_bf = sb("cbias_bf", [1, 768], BF16)
        nega = sb("nega", [128, 64])
        rot = [sb(f"rot{i}", [128, 64]) for i in range(2)]

        def C(name, rows=slice(0, 128), lo=0, hi=None):
            o, w = COFF[name]
            hi = w if hi is None else hi
            return cst[rows, o + lo:o + hi]

        psb = [es.enter_context(nc.psum_tensor(f"ps{i}", [128, 512], F32)) for i in range(8)]
        ps_rr = [0]

        def bank():
            i = ps_rr[0] % 4
            ps_rr[0] += 1
            return psb[i], f"ps{i}"

        import os as _os
        _AUDIT = bool(_os.environ.get("AUDIT"))
        _bad = set()

        def _chk(r, w, outs, ins):
            if not _AUDIT:
                return
            for grp, keys, what in ((outs, list(w), "W"), (ins, list(r) + list(w), "R")):
                for ap in grp:
                    nm = getattr(ap, "name", None)
                    if not isinstance(nm, str):
                        continue
                    key = nm[3:] if nm.startswith("sb_") else nm
                    if key not in keys:
                        import traceback
                        fr = traceback.extract_stack()[-3]
                        _bad.add((what, key, fr.lineno))

        _last_rb = {}

        def MM(out, lhsT, rhs, r, w, start=True, stop=True):
            skip = any(k in ("ps4", "ps5", "ps6", "ps7") for k in w)
            _chk(r, w, [out], [lhsT, rhs])
            rb = lhsT.base_partition()
            ser = False
            for k in w:
                if _last_rb.get(k, rb) != rb:
                    ser = True
                _last_rb[k] = rb
            P.op("pe", lambda e: e.matmul(out, lhsT=lhsT, rhs=rhs, start=start, stop=stop, skip_group_check=skip),
                 reads=r, writes=w, pe_serial=ser)

        def ACT(out, in_, func, r, w, scale=1.0, bias=None, accum=None):
            kw = {}
            if bias is not None:
                kw["bias"] = bias
            if accum is not None:
                kw["accum_out"] = accum
            if hasattr(bias, "name") and "eps_t" not in r:
                r = list(r) + ["eps_t"]
            _chk(r, w, [out] + ([accum] if accum is not None else []), [in_] + [x for x in (scale, bias) if hasattr(x, "name")])
            P.op("act", lambda e: e.activation(out=out, in_=in_, func=func, scale=scale, **kw), reads=r, writes=w)

        def TT(eng, out, in0, in1, op, r, w):
            _chk(r, w, [out], [in0, in1])
            P.op(eng, lambda e: e.tensor_tensor(out=out, in0=in0, in1=in1, op=op), reads=r, writes=w)

        def TS(eng, out, in0, s1, op0, r, w, s2=None, op1=None):
            _chk(r, w, [out], [in0] + [x for x in (s1, s2) if hasattr(x, "name")])
            if op1 is None:
                P.op(eng, lambda e: e.tensor_scalar(out=out, in0=in0, scalar1=s1, scalar2=None, op0=op0), reads=r, writes=w)
            else:
                P.op(eng, lambda e: e.tensor_scalar(out=out, in0=in0, scalar1=s1, scalar2=s2, op0=op0, op1=op1), reads=r, writes=w)

        def STT(out, in0, scalar, in1, op0, op1, r, w):
            _chk(r, w, [out], [in0, in1] + [x for x in (scalar,) if hasattr(x, "name")])
            P.op("dve", lambda e: e.scalar_tensor_tensor(out=out, in0=in0, scalar=scalar, in1=in1, op0=op0, op1=op1),
                 reads=r, writes=w)

        def RED(out, in_, r, w):
            _chk(r, w, [out], [in_])
            P.op("dve", lambda e: e.tensor_reduce(out=out, in_=in_, axis=AX.X, op=ALU.add), reads=r, writes=w)

        def RECIP(out, in_, r, w):
            _chk(r, w, [out], [in_])
            P.op("dve", lambda e: e.reciprocal(out=out, in_=in_), reads=r, writes=w)

        def CP(eng, out, in_, r, w):
            if eng == "act":
                ACT(out, in_, AF.Copy, r, w)
            else:
                _chk(r, w, [out], [in_])
                P.op(eng, lambda e: e.tensor_copy(out=out, in_=in_), reads=r, writes=w)

        def MEMSET(eng, ap, val, w):
            P.op(eng, lambda e: e.memset(ap, val), reads=(), writes=w)

        def DMA(out, in_, r, w, slow=False):
            if slow:
                P.op("sp", lambda e: e.dma_start(out=out, in_=in_, allow_slow_non_contiguous=True), reads=r, writes=w, dma=True)
            else:
                P.op("sp", lambda e: e.dma_start(out=out, in_=in_), reads=r, writes=w, dma=True)

        def sigmoid_from_exp(buf, key):
            TS("dve", buf, buf, 1.0, ALU.add, [key], [key])
            RECIP(buf, buf, [key], [key])

        def rsqrt_act(out, in_, scale, r, w):
            ACT(out, in_, AF.Ln, r, w, scale=scale, bias=eps_t[0:out.shape[0], 0:1])
            ACT(out, out, AF.Exp, w, w, scale=-0.5)

        eps_t = sb("eps_t", [128, 2])
        MEMSET("pool", eps_t[:, 0:1], EPS, ["eps_t"])
        MEMSET("pool", eps_t[:, 1:2], 1.0, ["eps_t"])
        DMA(cst[:], cst_d, [], ["cst"])
        DMA(lgt[:].rearrange("p a b -> p (a b)"), lgt_d, [], ["lgt"])
        DMA(featp[:], featp_d, [], ["featp"])
        CP("dve", ident_bf[:], C("ident"), ["cst"], ["ident_bf"])
        CP("dve", bones_bf[:], C("blockones"), ["cst"], ["bones_bf"])
        CP("dve", dtret_bf[:].rearrange("p a b -> p (a b)"), C("dt_ret"), ["cst"], ["dtret_bf"])
        MEMSET("pool", ones_bf[:], 1.0, ["ones_bf"])
        mx = wst[1][:, 0:256]
        TT("dve", mx, lgt[:, 0, :], lgt[:, 1, :], ALU.max, ["lgt"], ["wst1"])
        TT("dve", mx, mx, lgt[:, 2, :], ALU.max, ["lgt", "wst1"], ["wst1"])
        TT("dve", mx, mx, lgt[:, 3, :], ALU.max, ["lgt", "wst1"], ["wst1"])
        TT("dve", lgt[:], lgt[:], mx.unsqueeze(1).to_broadcast([128, 4, 256]), ALU.subtract, ["lgt", "wst1"], ["lgt"])
        ACT(lgt[:], lgt[:], AF.Exp, ["lgt"], ["lgt"])
        TT("dve", mx, lgt[:, 0, :], lgt[:, 1, :], ALU.add, ["lgt"], ["wst1"])
        TT("dve", mx, mx, lgt[:, 2, :], ALU.add, ["lgt", "wst1"], ["wst1"])
        TT("dve", mx, mx, lgt[:, 3, :], ALU.add, ["lgt", "wst1"], ["wst1"])
        RECIP(mx, mx, ["wst1"], ["wst1"])
        TT("dve", lgt[:], lgt[:], mx.unsqueeze(1).to_broadcast([128, 4, 256]), ALU.mult, ["lgt", "wst1"], ["lgt"])
        MEMSET("dve", oml[:, 0, :], 0.0, ["oml"])
        CP("dve", oml[:, 1, :], lgt[:, 1, :], ["lgt"], ["oml"])
        TT("dve", oml[:, 2, :], oml[:, 1, :], lgt[:, 2, :], ALU.add, ["lgt", "oml"], ["oml"])
        TT("dve", oml[:, 3, :], oml[:, 2, :], lgt[:, 3, :], ALU.add, ["lgt", "oml"], ["oml"])
        TS("dve", oml[:], oml[:], 0.0, ALU.max, ["oml"], ["oml"])
        TS("dve", oml[:], oml[:], -1.0, ALU.mult, ["oml"], ["oml"], s2=1.0, op1=ALU.add)
        DMA(lgt[:].rearrange("p a b -> p (a b)"), finw_d, ["lgt"], ["lgt"])
        finw = lgt[:].rearrange("p a b -> p (a b)")

        hn_bf = sb("hn_bf", [128, D], BF16)
        hnT = sb("hnT", [128, 8, 128], BF16)
        st4 = sb("st4", [128, 16])
        f1 = sb("f1", [128, 768])
        f2 = sb("f2", [128, 512])
        f3 = sb("f3", [128, 512])
        f4 = sb("f4", [128, 512])
        gate = sb("gate", [128, D])
        y_bf = sb("y_bf", [128, D], BF16)
        yT = sb("yT", [128, 8, 128], BF16)
        b1 = sb("b1", [128, 4, 128], BF16)
        qkT = sb("qkT", [128, 4, 128], BF16)
        AT = sb("AT", [128, 4, 128], BF16)
        v_bf = sb("v_bf", [128, 4, 64], BF16)
        kp_bf = sb("kp_bf", [128, 4, 128], BF16)
        qpT = sb("qpT", [128, 4, 128], BF16)
        ecs = sb("ecs", [128, 2, 8])
        S_A = sb("S_A", [128, 2, 64]); Sb_A = sb("Sb_A", [128, 2, 64], BF16); Sd_A = sb("Sd_A", [128, 2, 64])
        S_B = sb("S_B", [128, 2, 64]); Sb_B = sb("Sb_B", [128, 2, 64], BF16)
        S_C = sb("S_C", [128, 4, 64]); Sb_C = sb("Sb_C", [128, 4, 64], BF16)
        S_D = sb("S_D", [128, 2, 64]); Sb_D = sb("Sb_D", [128, 2, 64], BF16)
        tmpS = sb("tmpS", [128, 4, 64])
        uT = [sb(f"uT{i}", [128, 12, 131], BF16) for i in range(2)]
        cvst = sb("cvst", [128, 12, 3])
        xs = sb("xs", [128, 12, 128], BF16)
        xsf = sb("xsf", [128, 4, 128])
        g8 = sb("g8", [128, 64])
        gc = sb("gc", [128, 64])
        egc = sb("egc", [128, 64])
        beta = sb("beta", [128, 64])
        dtb = sb("dtb", [128, 64])
        nb = sb("nb", [128, 64])
        nb2 = sb("nb2", [128, 64])
        for _t, _k in ((g8, "g8"), (beta, "beta"), (nega, "nega")):
            MEMSET("pool", _t[:], 0.0, [_k])
        tt = sb("tt", [128, 8, 128])
        DT = sb("DT", [128, 8, 128], BF16)
        Dst = sb("Dst", [128, 4, 128], BF16)
        eGbc = sb("eGbc", [128, 8, 128])
        eGlB = sb("eGlB", [128, 2, 2])
        X_bf = [sb(f"X_bf{i}", [128, 4, 128], BF16) for i in range(2)]
        Y_bf = [sb(f"Y_bf{i}", [128, 4, 128], BF16) for i in range(2)]
        P_bf = sb("P_bf", [128, 4, 128], BF16)
        bv = sb("bv", [128, 4, 64])
        r_bf = sb("r_bf", [128, 4, 64], BF16)
        u_bf = sb("u_bf", [128, 4, 64], BF16)
        xd_bf = sb("xd_bf", [128, 4, 64], BF16)

        def load_layer(l):
            DMA(prm[:], prm_d[l], [], ["prm"])
            DMA(wst[0][0:1, 0:768], cbias_d[0:1, l * 768:(l + 1) * 768], [], ["wst0"])
            CP("pool", cbias_bf[:], wst[0][0:1, 0:768], ["wst0"], ["cbias_bf"])
            ACT(nega[:, 0:8], prm[:, 1280:1288], AF.Exp, ["prm"], ["nega"])
            TS("dve", nega[:, 0:64], nega[:, 0:64], -1.0, ALU.mult, ["nega"], ["nega"])
            for cv in range(2):
                for blk in range(6):
                    for w in range(4):
                        col = 32 + l * 48 + cv * 24 + blk * 4 + w
                        TS("pool", dg[:, cv * 6 + blk, w, :], ident_bf[:], featp[:, col:col + 1], ALU.mult,
                           ["ident_bf", "featp"], ["dg"])
            i = 0
            for kc in range(8):
                for c0 in range(0, IN_DIM, WCH):
                    st = wst[i % 2]; sk = f"wst{i % 2}"
                    DMA(st[:, 0:WCH], w_in[l, kc * 128:(kc + 1) * 128, c0:c0 + WCH], [], [sk])
                    TS("pool", win[:, kc, c0:c0 + WCH], st[:, 0:WCH], featp[:, l * 8 + kc:l * 8 + kc + 1], ALU.mult,
                       [sk, "featp"], ["win"])
                    i += 1
            for kc in range(8):
                st = wst[i % 2]; sk = f"wst{i % 2}"
                DMA(st[:, 0:1024], w_out[l, kc * 128:(kc + 1) * 128, :], [], [sk])
                CP("pool", wout[:, kc, :], st[:, 0:1024], [sk], ["wout"])
                i += 1
            for nm, S_, Sb_ in (("A", S_A, Sb_A), ("B", S_B, Sb_B), ("C", S_C, Sb_C), ("D", S_D, Sb_D)):
                MEMSET("pool", S_[:], 0.0, ["S_" + nm])
                MEMSET("pool", Sb_[:], 0.0, ["Sb_" + nm])
            MEMSET("pool", uT[0][:, :, 0:3], 0.0, ["uT0"])

        def tile_fwd(l, t):
            n = 16 if t == 0 else 128
            chunks = [(0, 16)] if t == 0 else [(0, 64), (64, 128)]
            nch = len(chunks)
            clen = chunks[0][1]
            last_tile = (t == ntiles - 1)
            hk = f"hb{t % 2}"
            ht = hb[t % 2][0:n, :]
            if l == 0:
                DMA(ht, meta if t == 0 else xp[(t - 1) * 128:t * 128, :], [], [hk])
            else:
                DMA(ht, hscr[t * 128:t * 128 + n, :], [f"hd{t}"], [hk])
            cur = uT[t % 2]; curk = f"uT{t % 2}"
            nxt = uT[(t + 1) % 2]; nxtk = f"uT{(t + 1) % 2}"
            rt = rot[t % 2]; rtk = f"rot{t % 2}"
            DMA(rt[:], rot_d[t], [], [rtk])

            def bc_h(ap2d, nh=4):
                return ap2d.unsqueeze(1).to_broadcast([ap2d.shape[0], nh, ap2d.shape[1]])

            def bc_l(ap2d, m):
                return ap2d.unsqueeze(2).to_broadcast([ap2d.shape[0], ap2d.shape[1], m])

            ACT(hn_bf[0:n, :], ht, AF.Square, [hk], ["hn_bf", "st4"], accum=st4[0:n, 0:1])
            rsqrt_act(st4[0:n, 1:2], st4[0:n, 0:1], 1.0 / D, ["st4"], ["st4"])
            ACT(hn_bf[0:n, :], ht, AF.Copy, [hk, "st4"], ["hn_bf"], scale=st4[0:n, 1:2])
            for half in range(2):
                pt, pk = bank()
                for kk in range(4):
                    kc = half * 4 + kk
                    MM(pt[:, kk * 128:kk * 128 + n], hn_bf[0:n, kc * 128:(kc + 1) * 128], ident_bf[0:n, 0:n],
                       ["hn_bf", "ident_bf"], [pk])
                CP("act" if half else "dve", hnT[:, half * 4:half * 4 + 4, 0:n],
                   pt[:, :].rearrange("p (a b) -> p a b", a=4)[:, :, 0:n], [pk], ["hnT"])

            def proj_tok(c0, c1, extra=None):
                pt, pk = bank()
                for kc in range(8):
                    MM(pt[0:n, 0:c1 - c0], hnT[:, kc, 0:n], win[:, kc, c0:c1], ["hnT", "win"], [pk],
                       start=(kc == 0), stop=(kc == 7 and extra is None))
                if extra is not None:
                    MM(pt[0:n, extra[0]:extra[0] + 4], C("ident", slice(0, n), 0, n), prm[0:n, extra[1]:extra[1] + 4],
                       ["cst", "prm"], [pk], start=False, stop=True)
                return pt, pk

            def proj_feat(cols0, nblk):
                pt, pk = bank()
                for b_ in range(nblk):
                    for kc in range(8):
                        MM(pt[:, b_ * 128:b_ * 128 + n], win[:, kc, cols0 + b_ * 128:cols0 + (b_ + 1) * 128], hnT[:, kc, 0:n],
                           ["hnT", "win"], [pk], start=(kc == 0), stop=(kc == 7))
                return pt, pk

            def v3(ps_ap, a):
                return ps_ap.rearrange("p (a b) -> p a b", a=a)

            pA0, kA0 = proj_tok(0, 512)
            pA1, kA1 = proj_tok(512, 1024)
            ACT(f1[0:n, 0:256], pA0[0:n, 0:256], AF.Exp, [kA0], ["f1"], scale=-1.0)
            ACT(f1[0:n, 256:512], pA0[0:n, 256:512], AF.Exp, [kA0], ["f1"])
            ACT(f1[0:n, 512:768], pA1[0:n, 256:512], AF.Exp, [kA1], ["f1"], scale=-1.0)
            sigmoid_from_exp(f1[0:n, :], "f1")
            STT(f2[0:n, 0:256], pA0[0:n, 0:256], QK, f1[0:n, 0:256], ALU.mult, ALU.mult, [kA0, "f1"], ["f2"])
            TT("dve", f2[0:n, 256:512], f1[0:n, 256:512], oml[0:n, l, :], ALU.mult, ["f1", "oml"], ["f2"])
            TT("pool", f1[0:n, 512:768], f1[0:n, 512:768], prm[0:n, 0:256], ALU.mult, ["f1", "prm"], ["f1"])
            TT("dve", gate[0:n, 0:256], pA1[0:n, 256:512], f1[0:n, 512:768], ALU.mult, [kA1, "f1"], ["gate"])
            ACT(v_bf[0:n, :, :].rearrange("p a b -> p (a b)"), pA1[0:n, 0:256], AF.Copy, [kA1], ["v_bf"])
            ACT(f3[0:n, 0:256], f2[0:n, 256:512], AF.Ln, ["f2"], ["f3"], scale=-1.0, bias=eps_t[0:n, 1:2])
            pG, kG = bank()
            MM(pG[0:n, 0:256], C("uprime", slice(0, n), 0, n), f3[0:n, 0:256], ["cst", "f3"], [kG])
            for hp in range(2):
                MM(pG[:, 256 + hp * 8:256 + hp * 8 + 8], f3[0:n, hp * 128:(hp + 1) * 128], C("wc", slice(0, n)),
                   ["cst", "f3"], [kG])
            ACT(f3[0:n, 0:256], pG[0:n, 0:256], AF.Exp, [kG], ["f3"])
            ACT(f3[0:n, 256:512], pG[0:n, 0:256], AF.Exp, [kG], ["f3"], scale=-1.0)
            ACT(ecs[:].rearrange("p a b -> p (a b)"), pG[:, 256:272], AF.Exp, [kG], ["ecs"])
            TT("dve", b1[0:n, 0:2, :].rearrange("p a b -> p (a b)"), f2[0:n, 0:256], f3[0:n, 0:256], ALU.mult,
               ["f2", "f3"], ["b1"])
            TT("dve", b1[0:n, 2:4, :].rearrange("p a b -> p (a b)"), f2[0:n, 256:512], f3[0:n, 256:512], ALU.mult,
               ["f2", "f3"], ["b1"])
            pT, kT = bank()
            for blk in range(4):
                MM(pT[:, blk * 128:blk * 128 + n], b1[0:n, blk, :], ident_bf[0:n, 0:n], ["b1", "ident_bf"], [kT])
            CP("act", qkT[:, :, 0:n], v3(pT[:, :], 4)[:, :, 0:n], [kT], ["qkT"])
            pS, kS = bank()
            for hd in range(4):
                hp, hh = hd // 2, hd % 2
                rows = slice(hh * 64, hh * 64 + 64)
                MM(pS[0:n, hd * 128:hd * 128 + n], qkT[rows, 2 + hp, 0:n], qkT[rows, hp, 0:n], ["qkT"], [kS])
            TT("dve", AT[0:n, :, 0:n], v3(pS[0:n, :], 4)[:, :, 0:n], bc_h(C("maskT", slice(0, n), 0, n)), ALU.mult,
               [kS, "cst"], ["AT"])
            pO, kO = psb[4], "ps4"
            pOD, kOD = psb[7], "ps7"
            for hd in range(4):
                MM(pO[0:n, hd * 64:hd * 64 + 64], AT[0:n, hd, 0:n], v_bf[0:n, hd, :], ["AT", "v_bf"], [kO],
                   start=(hd == 0), stop=False)
            for ci, (c0, c1) in enumerate(chunks):
                TT("dve", Sb_A[:], S_A[:], bc_l(ecs[:, :, ci], 64), ALU.mult, ["S_A", "ecs"], ["Sb_A"])
                TT("pool", Sd_A[:], S_A[:], bc_l(ecs[:, :, 4 + ci], 64), ALU.mult, ["S_A", "ecs"], ["Sd_A"])
                pK, kK = bank()
                for hd in range(4):
                    hp, hh = hd // 2, hd % 2
                    rows = slice(hh * 64, hh * 64 + 64)
                    MM(pO[c0:c1, hd * 64:hd * 64 + 64], qkT[rows, hp, c0:c1], Sb_A[rows, hp, :], ["qkT", "Sb_A"], [kO],
                       start=False, stop=True)
                    MM(pK[rows, hp * 64:hp * 64 + 64], b1[c0:c1, 2 + hp, hh * 64:hh * 64 + 64], v_bf[c0:c1, hd, :],
                       ["b1", "v_bf"], [kK])
                TT("dve", tmpS[:, 0:2, :], v3(pK[:, 0:128], 2), bc_l(ecs[:, :, 2 + ci], 64), ALU.mult, [kK, "ecs"], ["tmpS"])
                TT("dve", S_A[:], tmpS[:, 0:2, :], Sd_A[:], ALU.add, ["tmpS", "Sd_A"], ["S_A"])

            def head_norm(ps_ap, pskey, gcols, ycols):
                ACT(f4[0:n, 0:256], ps_ap, AF.Square, [pskey], ["f4"])
                RED(st4[0:n, 4:8], v3(f4[0:n, 0:256], 4), ["f4"], ["st4"])
                rsqrt_act(st4[0:n, 8:12], st4[0:n, 4:8], 1.0 / 64, ["st4"], ["st4"])
                TT("dve", v3(f4[0:n, 0:256], 4), v3(ps_ap, 4), bc_l(st4[0:n, 8:12], 64), ALU.mult, [pskey, "st4"], ["f4"])
                TT("dve", y_bf[0:n, ycols], f4[0:n, 0:256], gate[0:n, gcols], ALU.mult, ["f4", "gate"], ["y_bf"])

            head_norm(pO[0:n, 0:256], kO, slice(0, 256), slice(0, 256))

            pD0, kD0 = proj_tok(3084, 3596)
            pD1, kD1 = proj_tok(3596, 4108)
            cosb = rt[0:n, 0:32].unsqueeze(1).to_broadcast([n, 16, 32])
            sinb = rt[0:n, 32:64].unsqueeze(1).to_broadcast([n, 16, 32])
            qk4 = pD0[0:n, :].rearrange("p (a b) -> p a b", a=16)
            TT("dve", f1[0:n, 0:512].rearrange("p (a b) -> p a b", a=16), qk4, cosb, ALU.mult, [kD0, rtk], ["f1"])
            TT("dve", f2[0:n, 0:512].rearrange("p (a b) -> p a b", a=16), qk4, sinb, ALU.mult, [kD0, rtk], ["f2"])
            c4 = f1[0:n, 0:512].rearrange("p (a s b) -> p a s b", a=8, s=2)
            s4 = f2[0:n, 0:512].rearrange("p (a s b) -> p a s b", a=8, s=2)
            qkr = b1[0:n, :, :].rearrange("p a (s b) -> p a s b", s=4)
            qkr8 = b1[0:n, :, :].rearrange("p a b -> p (a b)").rearrange("p (a s b) -> p a s b", a=8, s=2)
            TT("dve", qkr8[:, :, 0, :], c4[:, :, 0, :], s4[:, :, 1, :], ALU.subtract, ["f1", "f2"], ["b1"])
            TT("dve", qkr8[:, :, 1, :], c4[:, :, 1, :], s4[:, :, 0, :], ALU.add, ["f1", "f2"], ["b1"])
            ACT(v_bf[0:n, :, :].rearrange("p a b -> p (a b)"), pD1[0:n, 0:256], AF.Copy, [kD1], ["v_bf"])
            ACT(f1[0:n, 512:768], pD1[0:n, 256:512], AF.Exp, [kD1], ["f1"], scale=-1.0)
            sigmoid_from_exp(f1[0:n, 512:768], "f1")
            TT("dve", gate[0:n, 768:1024], pD1[0:n, 256:512], f1[0:n, 512:768], ALU.mult, [kD1, "f1"], ["gate"])
            pT, kT = bank()
            for blk in range(4):
                MM(pT[:, blk * 128:blk * 128 + n], b1[0:n, blk, :], ident_bf[0:n, 0:n], ["b1", "ident_bf"], [kT])
            CP("act", qkT[:, :, 0:n], v3(pT[:, :], 4)[:, :, 0:n], [kT], ["qkT"])
            egq = C("egq").rearrange("p (a b) -> p a b", a=2)
            TT("dve", qpT[:, 0:2, 0:n], qkT[:, 0:2, 0:n], egq[:, :, 0:n], ALU.mult, ["qkT", "cst"], ["qpT"])
            egrev = C("egrev16" if t == 0 else "egrev64", slice(0, n))
            TT("dve", kp_bf[0:n, :, 0:64], b1[0:n, 2:4, :].rearrange("p a (s b) -> p (a s) b", s=2), bc_l(egrev, 64), ALU.mult,
               ["b1", "cst"], ["kp_bf"])
            pS, kS = bank()
            for hd in range(4):
                hp, hh = hd // 2, hd % 2
                rows = slice(hh * 64, hh * 64 + 64)
                MM(pS[0:n, hd * 128:hd * 128 + n], qkT[rows, 2 + hp, 0:n], qkT[rows, hp, 0:n], ["qkT"], [kS])
            TT("dve", AT[0:n, :, 0:n], v3(pS[0:n, :], 4)[:, :, 0:n], dtret_bf[0:n, :, 0:n], ALU.mult, [kS, "dtret_bf"], ["AT"])
            for hd in range(4):
                MM(pOD[0:n, hd * 64:hd * 64 + 64], AT[0:n, hd, 0:n], v_bf[0:n, hd, :], ["AT", "v_bf"], [kOD],
                   start=(hd == 0), stop=False)
            egl = C("egl").rearrange("p (a b) -> p a b", a=2)
            for ci, (c0, c1) in enumerate(chunks):
                pK, kK = bank()
                for hd in range(4):
                    hp, hh = hd // 2, hd % 2
                    rows = slice(hh * 64, hh * 64 + 64)
                    MM(pOD[c0:c1, hd * 64:hd * 64 + 64], qpT[rows, hp, c0:c1], Sb_D[rows, hp, :], ["qpT", "Sb_D"], [kOD],
                       start=False, stop=True)
                    MM(pK[rows, hp * 64:hp * 64 + 64], kp_bf[c0:c1, hd, 0:64], v_bf[c0:c1, hd, :], ["kp_bf", "v_bf"], [kK])
                TT("pool", tmpS[:, 0:2, :], S_D[:], bc_l(egl[:, :, (0 if t == 0 else 1)], 64), ALU.mult, ["S_D", "cst"], ["tmpS"])
                TT("dve", S_D[:], tmpS[:, 0:2, :], v3(pK[:, 0:128], 2), ALU.add, ["tmpS", kK], ["S_D"])
                CP("pool", Sb_D[:], S_D[:], ["S_D"], ["Sb_D"])
            oD = pOD[0:n, 0:256]
            kO_ = kOD
            RED(st4[0:n, 4:8], v3(oD, 4), [kO_], ["st4"])
            TS("dve", st4[0:n, 4:8], st4[0:n, 4:8], -1.0 / 64, ALU.mult, ["st4"], ["st4"])
            TT("dve", v3(f3[0:n, 0:256], 4), v3(oD, 4), bc_l(st4[0:n, 4:8], 64), ALU.add, [kO_, "st4"], ["f3"])
            ACT(f4[0:n, 0:256], f3[0:n, 0:256], AF.Square, ["f3"], ["f4"])
            RED(st4[0:n, 4:8], v3(f4[0:n, 0:256], 4), ["f4"], ["st4"])
            rsqrt_act(st4[0:n, 8:12], st4[0:n, 4:8], 1.0 / 64, ["st4"], ["st4"])
            TT("dve", v3(f3[0:n, 0:256], 4), v3(f3[0:n, 0:256], 4), bc_l(st4[0:n, 8:12], 64), ALU.mult, ["f3", "st4"], ["f3"])
            TT("pool", f3[0:n, 0:256], f3[0:n, 0:256], prm[0:n, 768:1024], ALU.mult, ["f3", "prm"], ["f3"])
            TT("pool", f3[0:n, 0:256], f3[0:n, 0:256], prm[0:n, 1024:1280], ALU.add, ["f3", "prm"], ["f3"])
            TT("dve", y_bf[0:n, 768:1024], f3[0:n, 0:256], gate[0:n, 768:1024], ALU.mult, ["f3", "gate"], ["y_bf"])

            pBz, kBz = proj_tok(1792, 2056, (256, 1288))
            pCz, kCz = proj_tok(2824, 3084, (256, 1292))
            ACT(f1[0:n, 512:768], pBz[0:n, 0:256], AF.Exp, [kBz], ["f1"], scale=-1.0)
            ACT(f2[0:n, 0:256], pCz[0:n, 0:256], AF.Exp, [kCz], ["f2"], scale=-1.0)
            ACT(beta[0:n, 0:4], pBz[0:n, 260:264], AF.Exp, [kBz], ["beta"], scale=-1.0)
            ACT(g8[0:n, 0:4], pBz[0:n, 256:260], AF.Exp, [kBz], ["g8"])
            ACT(g8[0:n, 4:8], pCz[0:n, 256:260], AF.Exp, [kCz], ["g8"])
            sigmoid_from_exp(f1[0:n, 512:768], "f1")
            sigmoid_from_exp(f2[0:n, 0:256], "f2")
            TT("pool", f1[0:n, 512:768], f1[0:n, 512:768], prm[0:n, 256:512], ALU.mult, ["f1", "prm"], ["f1"])
            TT("dve", gate[0:n, 256:512], pBz[0:n, 0:256], f1[0:n, 512:768], ALU.mult, [kBz, "f1"], ["gate"])
            TT("dve", gate[0:n, 512:768], pCz[0:n, 0:256], f2[0:n, 0:256], ALU.mult, [kCz, "f2"], ["gate"])
            for cv, cols0 in ((0, 1024), (1, 2056)):
                for part, (b0, nb_) in enumerate(((0, 4), (4, 2))):
                    pf, kf = proj_feat(cols0 + b0 * 128, nb_)
                    src = v3(pf[:, 0:nb_ * 128], nb_)[:, :, 0:n]
                    CP("act", cur[:, cv * 6 + b0:cv * 6 + b0 + nb_, 3:3 + n], src, [kf], [curk])
                    if last_tile:
                        CP("dve", cvst[:, cv * 6 + b0:cv * 6 + b0 + nb_, :], src[:, :, n - 3:n], [kf], ["cvst"])
            if not last_tile:
                CP("pool", nxt[:, :, 0:3], cur[:, :, n:n + 3], [curk], [nxtk])
            for cv in range(2):
                for part, (b0, nb_) in enumerate(((0, 4), (4, 2))):
                    pc, kc_ = bank()
                    for b_ in range(nb_):
                        blk = cv * 6 + b0 + b_
                        for w in range(4):
                            MM(pc[:, b_ * 128:b_ * 128 + n], dg[:, blk, w, :], cur[:, blk, w:w + n], ["dg", curk], [kc_],
                               start=(w == 0), stop=(w == 3 and cv == 0))
                        if cv == 1:
                            MM(pc[:, b_ * 128:b_ * 128 + n], cbias_bf[0:1, (b0 + b_) * 128:(b0 + b_ + 1) * 128], ones_bf[0:1, 0:n],
                               ["cbias_bf", "ones_bf"], [kc_], start=False, stop=True)
                    src = v3(pc[:, 0:nb_ * 128], nb_)[:, :, 0:n]
                    dstf = v3(f1[:, 0:nb_ * 128], nb_)[:, :, 0:n]
                    ACT(dstf, src, AF.Exp, [kc_], ["f1"], scale=-1.0)
                    sigmoid_from_exp(dstf, "f1")
                    if cv == 0 and part == 0:
                        TT("dve", xsf[:, :, 0:n], src, dstf, ALU.mult, [kc_, "f1"], ["xsf"])
                    else:
                        TT("dve", xs[:, cv * 6 + b0:cv * 6 + b0 + nb_, 0:n], src, dstf, ALU.mult, [kc_, "f1"], ["xs"])
            ACT(b1[:, :, 0:n], xsf[:, :, 0:n], AF.Square, ["xsf"], ["b1"])
            pN, kN = bank()
            for blk in range(4):
                MM(pN[:, blk * 128:blk * 128 + n], bones_bf[:], b1[:, blk, 0:n], ["bones_bf", "b1"], [kN])
            srcN = v3(pN[:, :], 4)[:, :, 0:n]
            dstN = v3(f2[:, 0:512], 4)[:, :, 0:n]
            ACT(dstN, srcN, AF.Ln, [kN], ["f2"], bias=eps_t[:, 0:1])
            ACT(dstN, dstN, AF.Exp, ["f2"], ["f2"], scale=-0.5)
            STT(xs[:, 0:2, 0:n], xsf[:, 0:2, 0:n], QK, dstN[:, 0:2, :], ALU.mult, ALU.mult, ["xsf", "f2"], ["xs"])
            TT("dve", xs[:, 2:4, 0:n], xsf[:, 2:4, 0:n], dstN[:, 2:4, :], ALU.mult, ["xsf", "f2"], ["xs"])

            ACT(g8[0:n, 0:8], g8[0:n, 0:8], AF.Ln, ["g8"], ["g8"], bias=eps_t[0:n, 1:2])
            CP("dve", dtb[0:n, :], g8[0:n, :], ["g8"], ["dtb"])
            TT("dve", g8[0:n, :], g8[0:n, :], nega[0:n, :], ALU.mult, ["g8", "nega"], ["g8"])
            sigmoid_from_exp(beta[0:n, :], "beta")
            pDc, kDc = bank()
            MM(pDc[0:n, 0:32], C("maskT", slice(0, n), 0, n), g8[0:n, 0:32], ["cst", "g8"], [kDc])
            MM(pDc[0:n, 32:64], C("urev", slice(0, n), 0, n), g8[0:n, 0:32], ["cst", "g8"], [kDc])
            CP("dve", gc[0:n, :], pDc[0:n, 0:64], [kDc], ["gc"])
            ACT(egc[0:n, :], gc[0:n, :], AF.Exp, ["gc"], ["egc"])
            for half in range(2):
                pB_, kB_ = bank()
                CP("dve", v3(f4[0:n, 0:512], 4), bc_l(g8[0:n, half * 4:half * 4 + 4], 128), ["g8"], ["f4"])
                for hd in range(4):
                    MM(pB_[:, hd * 128:hd * 128 + n], f4[0:n, hd * 128:(hd + 1) * 128],
                       C("maskT", slice(0, n), 0, n), ["cst", "f4"], [kB_])
                srcB = v3(pB_[:, :], 4)[:, :, 0:n]
                ACT(eGbc[:, half * 4:half * 4 + 4, 0:n], srcB, AF.Exp, [kB_], ["eGbc"])
                TT("dve", tt[0:n, half * 4:half * 4 + 4, 0:n], srcB[0:n], bc_l(gc[0:n, half * 4:half * 4 + 4], n), ALU.subtract,
                   [kB_, "gc"], ["tt"])
            if True:
                TT("dve", v3(f1[0:n, 0:512], 4)[:, :, 0:n], tt[0:n, 0:4, 0:n], bc_h(C("negS", slice(0, n), 0, n)), ALU.subtract,
                   ["tt", "cst"], ["f1"])
                ACT(Dst[0:n, :, 0:n], v3(f1[0:n, 0:512], 4)[:, :, 0:n], AF.Exp, ["f1"], ["Dst"], scale=-1.0)
                TT("dve", tt[0:n, :, 0:n], tt[0:n, :, 0:n], bc_h(C("negT", slice(0, n), 0, n), 8), ALU.add, ["tt", "cst"], ["tt"])
                ACT(DT[0:n, :, 0:n], tt[0:n, :, 0:n], AF.Exp, ["tt"], ["DT"])

            pS, kS = bank()
            pKK, kKK = bank()
            for hd in range(4):
                hp, hh = hd // 2, hd % 2
                rows = slice(hh * 64, hh * 64 + 64)
                MM(pS[0:n, hd * 128:hd * 128 + n], xs[rows, 2 + hp, 0:n], xs[rows, hp, 0:n], ["xs"], [kS])
                MM(pKK[0:n, hd * 128:hd * 128 + n], xs[rows, 2 + hp, 0:n], xs[rows, 2 + hp, 0:n], ["xs"], [kKK])
            TT("dve", AT[0:n, :, 0:n], v3(pS[0:n, :], 4)[:, :, 0:n], DT[0:n, 0:4, 0:n], ALU.mult, [kS, "DT"], ["AT"])
            TS("dve", nb[0:n, :], beta[0:n, :], -1.0, ALU.mult, ["beta"], ["nb"])
            TT("dve", v3(f1[0:n, 0:512], 4)[:, :, 0:n], v3(pKK[0:n, :], 4)[:, :, 0:n], Dst[0:n, :, 0:n], ALU.mult, [kKK, "Dst"], ["f1"])
            TT("dve", X_bf[0][0:n, :, 0:n], v3(f1[0:n, 0:512], 4)[:, :, 0:n], bc_l(nb[0:n, 0:4], n), ALU.mult,
               ["f1", "nb"], ["X_bf0"])
            pY, kY = bank()
            for hd in range(4):
                MM(pY[0:n, hd * 128:hd * 128 + n], X_bf[0][0:n, hd, 0:n], ident_bf[0:n, 0:n], ["X_bf0", "ident_bf"], [kY])
            CP("act", Y_bf[0][0:n, :, 0:n], v3(pY[0:n, :], 4)[:, :, 0:n], [kY], ["Y_bf0"])
            TT("dve", P_bf[0:n, :, 0:n], v3(pY[0:n, :], 4)[:, :, 0:n], bc_h(C("ident", slice(0, n), 0, n)), ALU.add,
               [kY, "cst"], ["P_bf"])
            nlev = int(math.ceil(math.log2(clen))) - 1
            ci_ = 0
            for lev in range(nlev):
                ni_ = 1 - ci_
                pX2, kX2 = bank()
                for hd in range(4):
                    MM(pX2[0:n, hd * 128:hd * 128 + n], Y_bf[ci_][0:n, hd, 0:n], X_bf[ci_][0:n, hd, 0:n],
                       [f"Y_bf{ci_}", f"X_bf{ci_}"], [kX2])
                CP("act", X_bf[ni_][0:n, :, 0:n], v3(pX2[0:n, :], 4)[:, :, 0:n], [kX2], [f"X_bf{ni_}"])
                if lev < nlev - 1:
                    pY2, kY2 = bank()
                    for hd in range(4):
                        MM(pY2[0:n, hd * 128:hd * 128 + n], X_bf[ci_][0:n, hd, 0:n], Y_bf[ci_][0:n, hd, 0:n],
                           [f"Y_bf{ci_}", f"X_bf{ci_}"], [kY2])
                    CP("dve", Y_bf[ni_][0:n, :, 0:n], v3(pY2[0:n, :], 4)[:, :, 0:n], [kY2], [f"Y_bf{ni_}"])
                pP, kP = bank()
                for hd in range(4):
                    MM(pP[0:n, hd * 128:hd * 128 + n], X_bf[ni_][0:n, hd, 0:n], P_bf[0:n, hd, 0:n], [f"X_bf{ni_}", "P_bf"], [kP])
                TT("dve", P_bf[0:n, :, 0:n], P_bf[0:n, :, 0:n], v3(pP[0:n, :], 4)[:, :, 0:n], ALU.add, ["P_bf", kP], ["P_bf"])
                ci_ = ni_
            pT, kT = bank()
            for blk in range(4):
                MM(pT[0:n, blk * 128:(blk + 1) * 128], xs[:, 2 + blk, 0:n], ident_bf[:, :], ["xs", "ident_bf"], [kT])
            TT("dve", kp_bf[0:n, :, 0:64], v3(pT[0:n, 0:256], 4), bc_l(egc[0:n, 32:36], 64), ALU.mult, [kT, "egc"], ["kp_bf"])
            TT("dve", bv[0:n, :, :], v3(pT[0:n, 256:512], 4), bc_l(beta[0:n, 0:4], 64), ALU.mult, [kT, "beta"], ["bv"])
            TT("dve", nb2[0:n, :], nb[0:n, :], egc[0:n, :], ALU.mult, ["nb", "egc"], ["nb2"])
            for hh in range(2):
                rows = slice(hh * 64, hh * 64 + 64)
                TT("pool", qpT[rows, 0:2, 0:n], xs[rows, 0:2, 0:n], eGbc[rows, hh:4:2, 0:n], ALU.mult, ["xs", "eGbc"], ["qpT"])
            pO2, kO2 = psb[5], "ps5"
            pOC, kOC = psb[6], "ps6"
            for ci, (c0, c1) in enumerate(chunks):
                pW, kW = bank()
                for hd in range(4):
                    hp, hh = hd // 2, hd % 2
                    rows = slice(hh * 64, hh * 64 + 64)
                    MM(pW[c0:c1, hd * 64:hd * 64 + 64], xs[rows, 2 + hp, c0:c1], Sb_B[rows, hp, :], ["xs", "Sb_B"], [kW])
                TT("dve", v3(f4[c0:c1, 0:256], 4), v3(pW[c0:c1, 0:256], 4), bc_l(nb2[c0:c1, 0:4], 64), ALU.mult,
                   [kW, "nb2"], ["f4"])
                TT("dve", r_bf[c0:c1, :, :], v3(f4[c0:c1, 0:256], 4), bv[c0:c1, :, :], ALU.add, ["f4", "bv"], ["r_bf"])
                pU, kU = bank()
                for hd in range(4):
                    MM(pU[c0:c1, hd * 64:hd * 64 + 64], P_bf[c0:c1, hd, c0:c1], r_bf[c0:c1, hd, :], ["P_bf", "r_bf"], [kU])
                CP("act", u_bf[c0:c1, :, :], v3(pU[c0:c1, 0:256], 4), [kU], ["u_bf"])
                pK, kK = bank()
                for hd in range(4):
                    hp, hh = hd // 2, hd % 2
                    rows = slice(hh * 64, hh * 64 + 64)
                    MM(pO2[c0:c1, hd * 64:hd * 64 + 64], AT[c0:c1, hd, c0:c1], u_bf[c0:c1, hd, :], ["AT", "u_bf"], [kO2],
                       start=(hd == 0), stop=False)
                    MM(pO2[c0:c1, hd * 64:hd * 64 + 64], qpT[rows, hp, c0:c1], Sb_B[rows, hp, :], ["qpT", "Sb_B"], [kO2],
                       start=False, stop=True)
                    MM(pK[rows, hp * 64:hp * 64 + 64], kp_bf[c0:c1, hd, 0:64], u_bf[c0:c1, hd, :], ["kp_bf", "u_bf"], [kK])
                for hh in range(2):
                    rows = slice(hh * 64, hh * 64 + 64)
                    TT("pool", tmpS[rows, 0:2, :], S_B[rows, :, :], eGbc[rows, hh:4:2, c1 - 1:c1].to_broadcast([64, 2, 64]), ALU.mult,
                       ["S_B", "eGbc"], ["tmpS"])
                TT("dve", S_B[:], tmpS[:, 0:2, :], v3(pK[:, 0:128], 2), ALU.add, ["tmpS", kK], ["S_B"])
                CP("pool", Sb_B[:], S_B[:], ["S_B"], ["Sb_B"])
            head_norm(pO2[0:n, 0:256], kO2, slice(256, 512), slice(256, 512))

            pS, kS = bank()
            for g in range(2):
                MM(pS[0:n, g * 128:g * 128 + n], xs[:, 8 + g, 0:n], xs[:, 10 + g, 0:n], ["xs"], [kS])
            for g in range(2):
                TT("dve", AT[0:n, 2 * g:2 * g + 2, 0:n], pS[0:n, g * 128:g * 128 + n].unsqueeze(1).to_broadcast([n, 2, n]),
                   DT[0:n, 4 + 2 * g:4 + 2 * g + 2, 0:n], ALU.mult, [kS, "DT"], ["AT"])
            pT, kT = bank()
            for blk in range(4):
                MM(pT[0:n, blk * 128:(blk + 1) * 128], xs[:, 6 + blk, 0:n], ident_bf[:, :], ["xs", "ident_bf"], [kT])
            TT("dve", v_bf[0:n, :, :], v3(pT[0:n, 0:256], 4), bc_l(dtb[0:n, 4:8], 64), ALU.mult, [kT, "dtb"], ["v_bf"])
            TT("dve", xd_bf[0:n, :, :], v3(pT[0:n, 0:256], 4), bc_l(prm[0:n, 1296:1300], 64), ALU.mult, [kT, "prm"], ["xd_bf"])
            for g in range(2):
                TT("dve", kp_bf[0:n, 2 * g:2 * g + 2, :], pT[0:n, 256 + g * 128:256 + (g + 1) * 128].unsqueeze(1).to_broadcast([n, 2, 128]),
                   bc_l(egc[0:n, 36 + 2 * g:36 + 2 * g + 2], 128), ALU.mult, [kT, "egc"], ["kp_bf"])
                TT("pool", qpT[:, 2 * g:2 * g + 2, 0:n], xs[:, 10 + g, 0:n].unsqueeze(1).to_broadcast([128, 2, n]),
                   eGbc[:, 4 + 2 * g:4 + 2 * g + 2, 0:n], ALU.mult, ["xs", "eGbc"], ["qpT"])
            for hd in range(4):
                MM(pOC[0:n, hd * 64:hd * 64 + 64], AT[0:n, hd, 0:n], v_bf[0:n, hd, :], ["AT", "v_bf"], [kOC],
                   start=(hd == 0), stop=False)
            MM(pOC[0:n, 0:256], ident_bf[0:n, 0:n], xd_bf[0:n, :, :].rearrange("p a b -> p (a b)"), ["ident_bf", "xd_bf"], [kOC],
               start=False, stop=False)
            for ci, (c0, c1) in enumerate(chunks):
                pK, kK = bank()
                for hd in range(4):
                    MM(pOC[c0:c1, hd * 64:hd * 64 + 64], qpT[:, hd, c0:c1], Sb_C[:, hd, :], ["qpT", "Sb_C"], [kOC],
                       start=False, stop=True)
                    MM(pK[:, hd * 64:hd * 64 + 64], kp_bf[c0:c1, hd, :], v_bf[c0:c1, hd, :], ["kp_bf", "v_bf"], [kK])
                TT("pool", tmpS[:, :, :], S_C[:], eGbc[:, 4:8, c1 - 1:c1].to_broadcast([128, 4, 64]), ALU.mult, ["S_C", "eGbc"], ["tmpS"])
                TT("dve", S_C[:], tmpS[:, :, :], v3(pK[:, 0:256], 4), ALU.add, ["tmpS", kK], ["S_C"])
                CP("pool", Sb_C[:], S_C[:], ["S_C"], ["Sb_C"])
            TT("dve", f3[0:n, 0:256], pOC[0:n, 0:256], gate[0:n, 512:768], ALU.mult, [kOC, "gate"], ["f3"])
            ACT(f4[0:n, 0:256], f3[0:n, 0:256], AF.Square, ["f3"], ["f4"])
            RED(st4[0:n, 4:6], v3(f4[0:n, 0:256], 2), ["f4"], ["st4"])
            rsqrt_act(st4[0:n, 8:10], st4[0:n, 4:6], 1.0 / 128, ["st4"], ["st4"])
            TT("dve", v3(f3[0:n, 0:256], 2), v3(f3[0:n, 0:256], 2), bc_l(st4[0:n, 8:10], 128), ALU.mult, ["f3", "st4"], ["f3"])
            TT("dve", y_bf[0:n, 512:768], f3[0:n, 0:256], prm[0:n, 512:768], ALU.mult, ["f3", "prm"], ["y_bf"])

            for half in range(2):
                pt, pk = bank()
                for kk in range(4):
                    kc = half * 4 + kk
                    MM(pt[:, kk * 128:kk * 128 + n], y_bf[0:n, kc * 128:(kc + 1) * 128], ident_bf[0:n, 0:n],
                       ["y_bf", "ident_bf"], [pk])
                CP("act" if half else "dve", yT[:, half * 4:half * 4 + 4, 0:n], v3(pt[:, :], 4)[:, :, 0:n], [pk], ["yT"])
            for cg in range(2):
                pt, pk = bank()
                for kc in range(8):
                    MM(pt[0:n, :], yT[:, kc, 0:n], wout[:, kc, cg * 512:(cg + 1) * 512], ["yT", "wout"], [pk],
                       start=(kc == 0), stop=(kc == 7))
                TT("dve", ht[:, cg * 512:(cg + 1) * 512], ht[:, cg * 512:(cg + 1) * 512], pt[0:n, :], ALU.add, [hk, pk], [hk])

            if last_tile:
                for nm, S_, dst in (("A", S_A, st_hg), ("B", S_B, st_gd), ("D", S_D, st_rt)):
                    for hh in range(2):
                        DMA(dst[l, hh:4:2, :, :].rearrange("a k v -> k a v"), S_[hh * 64:(hh + 1) * 64, :, :], ["S_" + nm], [])
                DMA(st_sd[l].rearrange("a k v -> k a v"), S_C[:, :, :], ["S_C"], [])
                for blk in range(6):
                    DMA(st_gc[l][:, blk * 128:(blk + 1) * 128].rearrange("w p -> p w"), cvst[:, blk, :], ["cvst"], [], slow=True)
                    DMA(st_sc[l][:, blk * 128:(blk + 1) * 128].rearrange("w p -> p w"), cvst[:, 6 + blk, :], ["cvst"], [], slow=True)
            if l < depth - 1:
                DMA(hscr[t * 128:t * 128 + n, :], ht, [hk], [f"hd{t}"])
            if l == depth - 1 and t > 0:
                ACT(hn_bf[0:n, :], ht, AF.Square, [hk], ["hn_bf", "st4"], accum=st4[0:n, 0:1])
                rsqrt_act(st4[0:n, 1:2], st4[0:n, 0:1], 1.0 / D, ["st4"], ["st4"])
                STT(ht, ht, st4[0:n, 1:2], finw[0:n, :], ALU.mult, ALU.mult, [hk, "st4", "lgt"], [hk])
                DMA(y_p[(t - 1) * 128:t * 128, :], ht, [hk], [])


        hs = sb("hs", [NS, D])
        DMA(hs[:, :], xs_d, [], ["hs"])

        def sample_fwd(l, last):
            n = NS
            DMA(rot[0][:], rot_d[NT], [], ["rot0"])
            rt = rot[0]; rtk = "rot0"

            def v3(ps_ap, a):
                return ps_ap.rearrange("p (a b) -> p a b", a=a)

            def bc_l(ap2d, m):
                return ap2d.unsqueeze(2).to_broadcast([ap2d.shape[0], ap2d.shape[1], m])

            ACT(hn_bf[0:n, :], hs[:, :], AF.Square, ["hs"], ["hn_bf", "st4"], accum=st4[0:n, 0:1])
            rsqrt_act(st4[0:n, 1:2], st4[0:n, 0:1], 1.0 / D, ["st4"], ["st4"])
            ACT(hn_bf[0:n, :], hs[:, :], AF.Copy, ["hs", "st4"], ["hn_bf"], scale=st4[0:n, 1:2])
            for half in range(2):
                pt, pk = bank()
                for kk in range(4):
                    kc = half * 4 + kk
                    MM(pt[:, kk * 128:kk * 128 + n], hn_bf[0:n, kc * 128:(kc + 1) * 128], ident_bf[0:n, 0:n],
                       ["hn_bf", "ident_bf"], [pk])
                CP("act", hnT[:, half * 4:half * 4 + 4, 0:n], v3(pt[:, :], 4)[:, :, 0:n], [pk], ["hnT"])

            def proj_tok(c0, c1, extra=None):
                pt, pk = bank()
                for kc in range(8):
                    MM(pt[0:n, 0:c1 - c0], hnT[:, kc, 0:n], win[:, kc, c0:c1], ["hnT", "win"], [pk],
                       start=(kc == 0), stop=(kc == 7 and extra is None))
                if extra is not None:
                    MM(pt[0:n, extra[0]:extra[0] + 4], C("ident", slice(0, n), 0, n), prm[0:n, extra[1]:extra[1] + 4],
                       ["cst", "prm"], [pk], start=False, stop=True)
                return pt, pk

            sel = C("sel").rearrange("p (a b) -> p a b", a=4)
            selT = C("selT").rearrange("p (a b) -> p a b", a=4)
            pvs = f4
            Sbuf = [tt, eGbc]; Skey = ["tt", "eGbc"]
            Tbuf = gate; Tkey = "gate"
            slot = [0]

            def select(fields):
                pv, pvk = bank()
                first = True
                for (c0, wd, fn) in fields:
                    for hd in range(4):
                        ap, key = fn(hd)
                        P.op("pe", (lambda o_, l_, r_, st_: (lambda e: e.matmul(o_, lhsT=l_, rhs=r_, start=st_, stop=False,
                                                                                 skip_group_check=True)))(
                            pv[0:64, c0:c0 + wd], sel[0:n, hd, :], ap, first), reads=["cst", key], writes=[pvk])
                        first = False
                wtot = max(c0 + wd for (c0, wd, _) in fields)
                CP("dve", pvs[0:64, 0:wtot], pv[0:64, 0:wtot], [pvk], ["f4"])

            def unselect(o_ap, okey):
                po, pok = bank()
                for hd in range(4):
                    MM(po[0:n, hd * 64:hd * 64 + 64], selT[0:64, hd, :], o_ap, ["cst", okey], [pok])
                return po, pok

            def state_io(st_in, st_out, K):
                ks = 16
                for k0 in range(0, K, ks):
                    yield k0, ks

            def load_slice(st_in, k0, ks):
                i = slot[0] % 2
                slot[0] += 1
                Sv = Sbuf[i][0:64, :, :].rearrange("p a b -> p (a b)")[:, 0:ks * 64].rearrange("p (k v) -> p k v", k=ks)
                for hd in range(4):
                    DMA(Sv[hd * 16:(hd + 1) * 16, :, :], st_in[l, :, hd, k0:k0 + ks, :], [], [Skey[i]])
                return Sv, Skey[i]

            def store_slice(st_out, Sv, sk, k0, ks):
                for hd in range(4):
                    DMA(st_out[l, :, hd, k0:k0 + ks, :], Sv[hd * 16:(hd + 1) * 16, :, :], [sk], [])

            o_sb = tmpS[0:64, 0, :]; w_sb = tmpS[0:64, 1, :]; op_sb = tmpS[0:64, 2, :]; u_sb = tmpS[0:64, 3, :]

            def Tview(ks):
                return Tbuf[0:64, 0:ks * 64].rearrange("p (k v) -> p k v", k=ks)

            def q_reduce(Sv, sk, q_ap, k0, ks, first, acc):
                T = Tview(ks)
                TT("dve", T, Sv, bc_l(q_ap[:, k0:k0 + ks], 64), ALU.mult, [sk, "f4"], [Tkey])
                RED(op_sb, T.rearrange("p k v -> p v k"), [Tkey], ["tmpS"])
                if first:
                    CP("dve", acc, op_sb, ["tmpS"], ["tmpS"])
                else:
                    TT("dve", acc, acc, op_sb, ALU.add, ["tmpS"], ["tmpS"])

            def step_plain(st_in, st_out, K, q_ap, k_ap, v_ap, vec_f=None, sc=None):
                for k0, ks in state_io(st_in, st_out, K):
                    Sv, sk = load_slice(st_in, k0, ks)
                    T = Tview(ks)
                    TT("dve", T, bc_l(k_ap[:, k0:k0 + ks], 64), v_ap.unsqueeze(1).to_broadcast([64, ks, 64]), ALU.mult,
                       ["f4", "tmpS"], [Tkey])
                    if vec_f is not None:
                        TT("dve", Sv, Sv, bc_l(vec_f[:, k0:k0 + ks], 64), ALU.mult, [sk, "f4"], [sk])
                        TT("dve", Sv, Sv, T, ALU.add, [sk, Tkey], [sk])
                    else:
                        STT(Sv, Sv, sc, T, ALU.mult, ALU.add, [sk, Tkey, "f4", "cst"], [sk])
                    store_slice(st_out, Sv, sk, k0, ks)
                    q_reduce(Sv, sk, q_ap, k0, ks, k0 == 0, o_sb)

            def head_norm(ps_ap, pskey, gcols, ycols):
                ACT(f3[0:n, 256:512], ps_ap, AF.Square, [pskey], ["f3"])
                RED(st4[0:n, 4:8], v3(f3[0:n, 256:512], 4), ["f3"], ["st4"])
                rsqrt_act(st4[0:n, 8:12], st4[0:n, 4:8], 1.0 / 64, ["st4"], ["st4"])
                TT("dve", v3(f3[0:n, 256:512], 4), v3(ps_ap, 4), bc_l(st4[0:n, 8:12], 64), ALU.mult, [pskey, "st4"], ["f3"])
                TT("dve", y_bf[0:n, ycols], f3[0:n, 256:512], hb[1][0:n, gcols], ALU.mult, ["f3", "hb1"], ["y_bf"])

            gs = hb[1]; gsk = "hb1"

            pA0, kA0 = proj_tok(0, 512)
            pA1, kA1 = proj_tok(512, 1024)
            ACT(f1[0:n, 0:256], pA0[0:n, 0:256], AF.Exp, [kA0], ["f1"], scale=-1.0)
            ACT(f1[0:n, 256:512], pA0[0:n, 256:512], AF.Exp, [kA0], ["f1"])
            ACT(f1[0:n, 512:768], pA1[0:n, 256:512], AF.Exp, [kA1], ["f1"], scale=-1.0)
            sigmoid_from_exp(f1[0:n, :], "f1")
            STT(f2[0:n, 0:256], pA0[0:n, 0:256], QK, f1[0:n, 0:256], ALU.mult, ALU.mult, [kA0, "f1"], ["f2"])
            TT("dve", f2[0:n, 256:512], f1[0:n, 256:512], oml[0:n, l, :], ALU.mult, ["f1", "oml"], ["f2"])
            TT("dve", f1[0:n, 512:768], f1[0:n, 512:768], prm[0:n, 0:256], ALU.mult, ["f1", "prm"], ["f1"])
            TT("dve", gs[0:n, 0:256], pA1[0:n, 256:512], f1[0:n, 512:768], ALU.mult, [kA1, "f1"], [gsk])
            CP("dve", f3[0:n, 0:256], pA1[0:n, 0:256], [kA1], ["f3"])
            TS("dve", f1[0:n, 0:256], f2[0:n, 256:512], -1.0, ALU.mult, ["f2"], ["f1"], s2=1.0, op1=ALU.add)
            select([(0, 64, lambda hd: (f2[0:n, hd * 64:hd * 64 + 64], "f2")),
                    (64, 64, lambda hd: (f2[0:n, 256 + hd * 64:256 + hd * 64 + 64], "f2")),
                    (128, 64, lambda hd: (f3[0:n, hd * 64:hd * 64 + 64], "f3")),
                    (192, 64, lambda hd: (f1[0:n, hd * 64:hd * 64 + 64], "f1"))])
            step_plain(si_hg, so_hg, 64, pvs[0:64, 0:64], pvs[0:64, 64:128], pvs[0:64, 128:192], vec_f=pvs[0:64, 192:256])
            po, pok = unselect(o_sb, "tmpS")
            head_norm(po[0:n, 0:256], pok, slice(0, 256), slice(0, 256))

            pD0, kD0 = proj_tok(3084, 3596)
            pD1, kD1 = proj_tok(3596, 4108)
            cosb = rt[0:n, 0:32].unsqueeze(1).to_broadcast([n, 16, 32])
            sinb = rt[0:n, 32:64].unsqueeze(1).to_broadcast([n, 16, 32])
            qk4 = pD0[0:n, :].rearrange("p (a b) -> p a b", a=16)
            TT("dve", f1[0:n, 0:512].rearrange("p (a b) -> p a b", a=16), qk4, cosb, ALU.mult, [kD0, rtk], ["f1"])
            TT("dve", f2[0:n, 0:512].rearrange("p (a b) -> p a b", a=16), qk4, sinb, ALU.mult, [kD0, rtk], ["f2"])
            c4 = f1[0:n, 0:512].rearrange("p (a s b) -> p a s b", a=8, s=2)
            s4 = f2[0:n, 0:512].rearrange("p (a s b) -> p a s b", a=8, s=2)
            r4 = f3[0:n, 0:512].rearrange("p (a s b) -> p a s b", a=8, s=2)
            TT("dve", r4[:, :, 0, :], c4[:, :, 0, :], s4[:, :, 1, :], ALU.subtract, ["f1", "f2"], ["f3"])
            TT("dve", r4[:, :, 1, :], c4[:, :, 1, :], s4[:, :, 0, :], ALU.add, ["f1", "f2"], ["f3"])
            TS("dve", f3[0:n, 256:512], f3[0:n, 256:512], QK, ALU.mult, ["f3"], ["f3"])
            CP("dve", f1[0:n, 0:256], pD1[0:n, 0:256], [kD1], ["f1"])
            ACT(f1[0:n, 512:768], pD1[0:n, 256:512], AF.Exp, [kD1], ["f1"], scale=-1.0)
            sigmoid_from_exp(f1[0:n, 512:768], "f1")
            TT("dve", gs[0:n, 768:1024], pD1[0:n, 256:512], f1[0:n, 512:768], ALU.mult, [kD1, "f1"], [gsk])
            select([(0, 64, lambda hd: (f3[0:n, hd * 64:hd * 64 + 64], "f3")),
                    (64, 64, lambda hd: (f3[0:n, 256 + hd * 64:256 + hd * 64 + 64], "f3")),
                    (128, 64, lambda hd: (f1[0:n, hd * 64:hd * 64 + 64], "f1"))])
            step_plain(si_rt, so_rt, 64, pvs[0:64, 0:64], pvs[0:64, 64:128], pvs[0:64, 128:192], sc=C("gam64", slice(0, 64), 0, 1))
            po, pok = unselect(o_sb, "tmpS")
            oD = po[0:n, 0:256]
            RED(st4[0:n, 4:8], v3(oD, 4), [pok], ["st4"])
            TS("dve", st4[0:n, 4:8], st4[0:n, 4:8], -1.0 / 64, ALU.mult, ["st4"], ["st4"])
            TT("dve", v3(f3[0:n, 0:256], 4), v3(oD, 4), bc_l(st4[0:n, 4:8], 64), ALU.add, [pok, "st4"], ["f3"])
            ACT(f3[0:n, 256:512], f3[0:n, 0:256], AF.Square, ["f3"], ["f3"])
            RED(st4[0:n, 4:8], v3(f3[0:n, 256:512], 4), ["f3"], ["st4"])
            rsqrt_act(st4[0:n, 8:12], st4[0:n, 4:8], 1.0 / 64, ["st4"], ["st4"])
            TT("dve", v3(f3[0:n, 0:256], 4), v3(f3[0:n, 0:256], 4), bc_l(st4[0:n, 8:12], 64), ALU.mult, ["f3", "st4"], ["f3"])
            TT("dve", f3[0:n, 0:256], f3[0:n, 0:256], prm[0:n, 768:1024], ALU.mult, ["f3", "prm"], ["f3"])
            TT("dve", f3[0:n, 0:256], f3[0:n, 0:256], prm[0:n, 1024:1280], ALU.add, ["f3", "prm"], ["f3"])
            TT("dve", y_bf[0:n, 768:1024], f3[0:n, 0:256], gs[0:n, 768:1024], ALU.mult, ["f3", gsk], ["y_bf"])

            pBz, kBz = proj_tok(1792, 2056, (256, 1288))
            ACT(f1[0:n, 512:768], pBz[0:n, 0:256], AF.Exp, [kBz], ["f1"], scale=-1.0)
            ACT(beta[0:n, 0:4], pBz[0:n, 260:264], AF.Exp, [kBz], ["beta"], scale=-1.0)
            ACT(g8[0:n, 0:4], pBz[0:n, 256:260], AF.Exp, [kBz], ["g8"])
            sigmoid_from_exp(f1[0:n, 512:768], "f1")
            TT("dve", f1[0:n, 512:768], f1[0:n, 512:768], prm[0:n, 256:512], ALU.mult, ["f1", "prm"], ["f1"])
            TT("dve", gs[0:n, 256:512], pBz[0:n, 0:256], f1[0:n, 512:768], ALU.mult, [kBz, "f1"], [gsk])
            pCz, kCz = proj_tok(2824, 3084, (256, 1292))
            ACT(f1[0:n, 512:768], pCz[0:n, 0:256], AF.Exp, [kCz], ["f1"], scale=-1.0)
            ACT(g8[0:n, 4:8], pCz[0:n, 256:260], AF.Exp, [kCz], ["g8"])
            sigmoid_from_exp(f1[0:n, 512:768], "f1")
            TT("dve", gs[0:n, 512:768], pCz[0:n, 0:256], f1[0:n, 512:768], ALU.mult, [kCz, "f1"], [gsk])
            ACT(g8[0:n, 0:8], g8[0:n, 0:8], AF.Ln, ["g8"], ["g8"], bias=eps_t[0:n, 1:2])
            CP("dve", dtb[0:n, :], g8[0:n, :], ["g8"], ["dtb"])
            TT("dve", g8[0:n, :], g8[0:n, :], nega[0:n, :], ALU.mult, ["g8", "nega"], ["g8"])
            ACT(egc[0:n, 0:8], g8[0:n, 0:8], AF.Exp, ["g8"], ["egc"])
            sigmoid_from_exp(beta[0:n, :], "beta")

            def conv_tok(cv, groups, st_in, st_out, bias):
                U = hb[0]; Uk = "hb0"
                for (c0, c1, o0) in groups:
                    pu, puk = proj_tok(c0, c1)
                    CP("dve", U[0:n, o0:o0 + (c1 - c0)], pu[0:n, 0:c1 - c0], [puk], [Uk])
                DMA(st_out[l, :, 2, :], U[0:n, 0:768], [Uk], [])
                Wt = eGbc[0:n, :, :].rearrange("p a b -> p (a b)")[:, 0:768]
                Ct = tt[0:n, :, :].rearrange("p a b -> p (a b)")[:, 0:768]
                DMA(Wt, cwrow_d[l, cv, 3], [], ["eGbc"])
                TT("dve", f1[0:n, 0:768], U[0:n, 0:768], Wt, ALU.mult, [Uk, "eGbc"], ["f1"])
                for w in range(3):
                    DMA(Ct, st_in[l, :, w, :], [], ["tt"])
                    DMA(Wt, cwrow_d[l, cv, w], [], ["eGbc"])
                    if w >= 1:
                        DMA(st_out[l, :, w - 1, :], Ct, ["tt"], [])
                    TT("dve", Wt, Ct, Wt, ALU.mult, ["tt", "eGbc"], ["eGbc"])
                    TT("dve", f1[0:n, 0:768], f1[0:n, 0:768], Wt, ALU.add, ["f1", "eGbc"], ["f1"])
                if bias:
                    DMA(Ct, cbrow_d[l], [], ["tt"])
                    TT("dve", f1[0:n, 0:768], f1[0:n, 0:768], Ct, ALU.add, ["f1", "tt"], ["f1"])
                ACT(Ct, f1[0:n, 0:768], AF.Exp, ["f1"], ["tt"], scale=-1.0)
                sigmoid_from_exp(Ct, "tt")
                TT("dve", f1[0:n, 0:768], f1[0:n, 0:768], Ct, ALU.mult, ["f1", "tt"], ["f1"])

            conv_tok(0, [(1024, 1536, 0), (1536, 1792, 512)], si_gc, so_gc, False)
            Ct = tt[0:n, :, :].rearrange("p a b -> p (a b)")[:, 0:512]
            ACT(Ct, f1[0:n, 0:512], AF.Square, ["f1"], ["tt"])
            RED(nb[0:n, 0:8], v3(Ct, 8), ["tt"], ["nb"])
            ACT(nb[0:n, 8:16], nb[0:n, 0:8], AF.Ln, ["nb"], ["nb"], bias=eps_t[0:n, 0:1])
            ACT(nb[0:n, 8:16], nb[0:n, 8:16], AF.Exp, ["nb"], ["nb"], scale=-0.5)
            TT("dve", v3(f1[0:n, 0:512], 8), v3(f1[0:n, 0:512], 8), bc_l(nb[0:n, 8:16], 64), ALU.mult, ["f1", "nb"], ["f1"])
            TS("dve", f1[0:n, 0:256], f1[0:n, 0:256], QK, ALU.mult, ["f1"], ["f1"])
            select([(0, 64, lambda hd: (f1[0:n, hd * 64:hd * 64 + 64], "f1")),
                    (64, 64, lambda hd: (f1[0:n, 256 + hd * 64:256 + hd * 64 + 64], "f1")),
                    (128, 64, lambda hd: (f1[0:n, 512 + hd * 64:512 + hd * 64 + 64], "f1")),
                    (192, 1, lambda hd: (egc[0:n, hd:hd + 1], "egc")),
                    (193, 1, lambda hd: (beta[0:n, hd:hd + 1], "beta"))])
            qB, kB, vB = pvs[0:64, 0:64], pvs[0:64, 64:128], pvs[0:64, 128:192]
            egB, btB = pvs[0:64, 192:193], pvs[0:64, 193:194]
            for k0, ks in state_io(si_gd, so_gd, 64):
                Sv, sk = load_slice(si_gd, k0, ks)
                T = Tview(ks)
                TT("dve", T, Sv, bc_l(kB[:, k0:k0 + ks], 64), ALU.mult, [sk, "f4"], [Tkey])
                RED(op_sb, T.rearrange("p k v -> p v k"), [Tkey], ["tmpS"])
                if k0 == 0:
                    CP("dve", w_sb, op_sb, ["tmpS"], ["tmpS"])
                else:
                    TT("dve", w_sb, w_sb, op_sb, ALU.add, ["tmpS"], ["tmpS"])
            TS("dve", w_sb, w_sb, egB, ALU.mult, ["tmpS", "f4"], ["tmpS"])
            TT("dve", u_sb, vB, w_sb, ALU.subtract, ["f4", "tmpS"], ["tmpS"])
            TS("dve", u_sb, u_sb, btB, ALU.mult, ["tmpS", "f4"], ["tmpS"])
            step_plain(si_gd, so_gd, 64, qB, kB, u_sb, sc=egB)
            po, pok = unselect(o_sb, "tmpS")
            head_norm(po[0:n, 0:256], pok, slice(256, 512), slice(256, 512))

            conv_tok(1, [(2056, 2568, 0), (2568, 2824, 512)], si_sc, so_sc, True)
            TT("dve", v3(f2[0:n, 0:256], 4), v3(f1[0:n, 0:256], 4), bc_l(dtb[0:n, 4:8], 64), ALU.mult, ["f1", "dtb"], ["f2"])
            select([(0, 128, lambda hd: (f1[0:n, 512 + (hd // 2) * 128:512 + (hd // 2) * 128 + 128], "f1")),
                    (128, 128, lambda hd: (f1[0:n, 256 + (hd // 2) * 128:256 + (hd // 2) * 128 + 128], "f1")),
                    (256, 64, lambda hd: (f2[0:n, hd * 64:hd * 64 + 64], "f2")),
                    (320, 1, lambda hd: (egc[0:n, 4 + hd:5 + hd], "egc"))])
            step_plain(si_sd, so_sd, 128, pvs[0:64, 0:128], pvs[0:64, 128:256], pvs[0:64, 256:320], sc=pvs[0:64, 320:321])
            po, pok = unselect(o_sb, "tmpS")
            TT("dve", v3(f3[0:n, 0:256], 4), v3(f1[0:n, 0:256], 4), bc_l(prm[0:n, 1296:1300], 64), ALU.mult, ["f1", "prm"], ["f3"])
            TT("dve", f3[0:n, 0:256], f3[0:n, 0:256], po[0:n, 0:256], ALU.add, ["f3", pok], ["f3"])
            TT("dve", f3[0:n, 0:256], f3[0:n, 0:256], gs[0:n, 512:768], ALU.mult, ["f3", gsk], ["f3"])
            ACT(f3[0:n, 256:512], f3[0:n, 0:256], AF.Square, ["f3"], ["f3"])
            RED(st4[0:n, 4:6], v3(f3[0:n, 256:512], 2), ["f3"], ["st4"])
            rsqrt_act(st4[0:n, 8:10], st4[0:n, 4:6], 1.0 / 128, ["st4"], ["st4"])
            TT("dve", v3(f3[0:n, 0:256], 2), v3(f3[0:n, 0:256], 2), bc_l(st4[0:n, 8:10], 128), ALU.mult, ["f3", "st4"], ["f3"])
            TT("dve", y_bf[0:n, 512:768], f3[0:n, 0:256], prm[0:n, 512:768], ALU.mult, ["f3", "prm"], ["y_bf"])

            for half in range(2):
                pt, pk = bank()
                for kk in range(4):
                    kc = half * 4 + kk
                    MM(pt[:, kk * 128:kk * 128 + n], y_bf[0:n, kc * 128:(kc + 1) * 128], ident_bf[0:n, 0:n],
                       ["y_bf", "ident_bf"], [pk])
                CP("act", yT[:, half * 4:half * 4 + 4, 0:n], v3(pt[:, :], 4)[:, :, 0:n], [pk], ["yT"])
            for cg in range(2):
                pt, pk = bank()
                for kc in range(8):
                    MM(pt[0:n, :], yT[:, kc, 0:n], wout[:, kc, cg * 512:(cg + 1) * 512], ["yT", "wout"], [pk],
                       start=(kc == 0), stop=(kc == 7))
                TT("dve", hs[:, cg * 512:(cg + 1) * 512], hs[:, cg * 512:(cg + 1) * 512], pt[0:n, :], ALU.add, ["hs", pk], ["hs"])
            if last:
                ACT(hn_bf[0:n, :], hs[:, :], AF.Square, ["hs"], ["hn_bf", "st4"], accum=st4[0:n, 0:1])
                rsqrt_act(st4[0:n, 1:2], st4[0:n, 0:1], 1.0 / D, ["st4"], ["st4"])
                STT(hs[:, :], hs[:, :], st4[0:n, 1:2], finw[0:n, :], ALU.mult, ALU.mult, ["hs", "st4", "lgt"], ["hs"])
                DMA(y_s, hs[:, :], ["hs"], [])

        for l in range(depth):
            load_layer(l)
            sample_fwd(l, l == depth - 1)
            for t in range(ntiles):
                tile_fwd(l, t)

        if _AUDIT:
            for b_ in sorted(_bad, key=str):
                print("AUDIT missing key:", b_)
        P.emit(es)
    return nc


_NC_CACHE = {}


def _prep_inputs(inp, c):
    f = np.float32
    prm = np.zeros((DEPTH, 128, NPRM), f)
    for l in range(DEPTH):
        row = np.concatenate([inp["hgrn_norm_w"][l], inp["gdn_norm_w"][l], inp["ssd_norm_w"][l], inp["ret_norm_w"][l],
                              inp["ret_norm_b"][l], inp["gdn_a_log"][l], inp["ssd_a_log"][l], inp["gdn_dt_bias"][l],
                              inp["ssd_dt_bias"][l], inp["ssd_d"][l]]).astype(f)
        prm[l] = np.broadcast_to(row[None, :], (128, NPRM))
    lgt = np.ascontiguousarray(np.broadcast_to(inp["hgrn_lb_logits"].reshape(1, 1024), (128, 1024))).astype(f)
    finw = np.ascontiguousarray(np.broadcast_to(inp["final_norm_w"].reshape(1, 1024), (128, 1024))).astype(f)
    featp = np.zeros((128, 32 + 192), f)
    featp[:, 0:32] = inp["norm_w"].reshape(DEPTH, 8, 128).transpose(2, 0, 1).reshape(128, 32)
    for l in range(DEPTH):
        for cv, key in enumerate(("gdn_conv_w", "ssd_conv_w")):
            w = inp[key][l].reshape(4, 6, 128)
            featp[:, 32 + l * 48 + cv * 24:32 + l * 48 + cv * 24 + 24] = w.transpose(2, 1, 0).reshape(128, 24)
    cbias = np.ascontiguousarray(inp["ssd_conv_b"].reshape(1, DEPTH * 768)).astype(f)
    cw = np.stack([inp["gdn_conv_w"], inp["ssd_conv_w"]], 1).astype(f)
    cwrow = np.ascontiguousarray(np.broadcast_to(cw[:, :, :, None, :], (DEPTH, 2, 4, NS, 768)))
    cbrow = np.ascontiguousarray(np.broadcast_to(inp["ssd_conv_b"].astype(f)[:, None, :], (DEPTH, NS, 768)))
    return {
        "xp": np.ascontiguousarray(inp["x_prompt"][c]).astype(f),
        "meta": np.ascontiguousarray(inp["meta_tokens"]).astype(f),
        "w_in": np.ascontiguousarray(inp["w_in"]).astype(f),
        "w_out": np.ascontiguousarray(inp["w_out"]).astype(f),
        "cst": CST, "rot": ROT, "prm": prm, "lgt": lgt, "finw": finw, "featp": featp, "cbias": cbias,
        "cwrow": cwrow, "cbrow": cbrow, **_sample_inputs(inp, c),
    }


def _sample_inputs(inp, c):
    f = np.float32
    sl = slice(c * NS, (c + 1) * NS)
    return {
        "xs_in": np.ascontiguousarray(inp["x_sample"][sl, 0, :]).astype(f),
        "si_hg": np.ascontiguousarray(inp["state_hgrn"][:, sl]).astype(f),
        "si_gd": np.ascontiguousarray(inp["state_gdn"][:, sl]).astype(f),
        "si_gc": np.ascontiguousarray(inp["state_gdn_conv"][:, sl]).astype(f),
        "si_sd": np.ascontiguousarray(inp["state_ssd"][:, sl]).astype(f),
        "si_sc": np.ascontiguousarray(inp["state_ssd_conv"][:, sl]).astype(f),
        "si_rt": np.ascontiguousarray(inp["state_ret"][:, sl]).astype(f),
    }


def kernel(**inp):
    inp = {k: np.asarray(v) for k, v in inp.items()}
    if "nc" not in _NC_CACHE:
        _NC_CACHE["nc"] = build_program()
    nc = _NC_CACHE["nc"]
    shared = None
    in_maps = []
    for c in range(8):
        m = _prep_inputs(inp, c) if shared is None else dict(shared)
        if shared is None:
            shared = m
        else:
            m["xp"] = np.ascontiguousarray(inp["x_prompt"][c]).astype(np.float32)
            m.update(_sample_inputs(inp, c))
        in_maps.append(m)
    res = run_bass_kernel_spmd(nc, in_maps, core_ids=list(range(8)))
    R = res.results
    y_prompt = np.stack([R[c]["y_p"] for c in range(8)], 0)
    def stk(name):
        return np.ascontiguousarray(np.stack([R[c][name] for c in range(8)], 1))
    y_sample = np.concatenate([R[c]["y_s"] for c in range(8)], 0)[:, None, :]
    def cat(name):
        return np.ascontiguousarray(np.concatenate([R[c][name] for c in range(8)], 1))
    outs = (y_prompt, np.ascontiguousarray(y_sample),
            stk("st_hg"), stk("st_gd"), stk("st_gc"), stk("st_sd"), stk("st_sc"), stk("st_rt"),
            cat("so_hg"), cat("so_gd"), cat("so_gc"), cat("so_sd"), cat("so_sc"), cat("so_rt"))
    return outs
```

```python
import contextlib
import math
import numpy as np
import concourse.bass as bass
import concourse.mybir as mybir
from concourse.bass_utils import run_bass_kernel_spmd

F32 = mybir.dt.float32
BF16 = mybir.dt.bfloat16
AF = mybir.ActivationFunctionType
ALU = mybir.AluOpType
AX = mybir.AxisListType

D = 1024
DEPTH = 4
SEQ = 2048
NT = 17
IN_DIM = 4108
EPS = 1e-6
QK = 0.125
NS = 16
NPRM = 1300
NEGV = -30000.0
import os as _os0
EMBED_WAIT = not _os0.environ.get("NO_EMBED")


class Prog:
    ENG = ("pe", "act", "dve", "pool", "sp")

    def __init__(self, nc, n_dma_sems=8):
        self.nc = nc
        self.ops = []
        self.cnt = {}
        self.clock = {e: {} for e in self.ENG}
        self.tok_clock = {}
        self.last_w = {}
        self.readers = {}
        self.n_dma = n_dma_sems
        self.dma_rr = {e: 0 for e in self.ENG}
        self.dma_last = {}
        import os
        self.pe_skip = not os.environ.get("PE_SELFWAIT")
        self.strict_same = not os.environ.get("RELAX_SAME")

    def _need(self, eng, tok, waits, force=False):
        key, idx = tok
        if key == "pe" and eng == "pe" and self.pe_skip and not force:
            return
        if self.clock[eng].get(key, 0) >= idx:
            return
        if waits.get(key, 0) < idx:
            waits[key] = idx

    def op(self, eng, fn, reads=(), writes=(), dma=False, pe_serial=False):
        waits = {}
        for b in reads:
            t = self.last_w.get(b)
            if t:
                self._need(eng, t, waits)
            if b.startswith("ps"):
                for r in self.readers.get(b, ()):
                    if r[0] != eng:
                        self._need(eng, r, waits)
        for b in writes:
            t = self.last_w.get(b)
            if t and (t[0] != eng or pe_serial or dma or self.strict_same):
                self._need(eng, t, waits, force=pe_serial)
            for r in self.readers.get(b, ()):
                if r[0] != eng or dma or self.strict_same:
                    self._need(eng, r, waits)
        if dma:
            key = ("dma", eng, self.dma_rr[eng] % self.n_dma)
            self.dma_rr[eng] += 1
            prev = self.dma_last.get(key)
            if prev:
                self._need(eng, prev, waits)
        else:
            key = eng
        ck = self.clock[eng]
        for kk, ii in waits.items():
            for k2, i2 in self.tok_clock.get((kk, ii), {}).items():
                if ck.get(k2, 0) < i2:
                    ck[k2] = i2
            if ck.get(kk, 0) < ii:
                ck[kk] = ii
        self.cnt[key] = self.cnt.get(key, 0) + 1
        tok = (key, self.cnt[key])
        if dma:
            self.dma_last[key] = tok
            snap = dict(ck)
            snap[key] = tok[1]
            self.tok_clock[tok] = snap
        else:
            snap = dict(ck)
            snap[key] = tok[1]
            self.tok_clock[tok] = snap
        for b in writes:
            self.last_w[b] = tok
            self.readers[b] = []
        for b in reads:
            if b not in writes:
                self.readers.setdefault(b, []).append(tok)
        self.ops.append((eng, fn, list(waits.items()), tok, dma))
        return tok

    def emit(self, es, final_wait_eng="sp"):
        nc = self.nc
        import os
        km = int(os.environ.get("KMAX", "0"))
        if km:
            self.ops = self.ops[:km]
            self.dma_last = {}
            for (e_, f_, w_, tok_, d_) in self.ops:
                if d_:
                    self.dma_last[tok_[0]] = tok_
        needed = set()
        for (_, _, waits, _, _) in self.ops:
            for w in waits:
                needed.add(w)
        finals = []
        for k, t in self.dma_last.items():
            finals.append(t)
            needed.add(t)
        per_key = {}
        for (k, i) in needed:
            per_key.setdefault(k, []).append(i)
        sigcount = {}
        for k, lst in per_key.items():
            for n, i in enumerate(sorted(lst)):
                sigcount[(k, i)] = n + 1
        sems = {}
        for k in sorted(per_key.keys(), key=str):
            nm = "s_" + "_".join(str(x) for x in (k if isinstance(k, tuple) else (k,)))
            sems[k] = es.enter_context(nc.semaphore(nm))
        per_eng = {e: [] for e in self.ENG}
        for o in self.ops:
            per_eng[o[0]].append(o)
        blk = es.enter_context(nc.Block())

        def run(e, engobj):
            for (_, fn, waits, tok, dma) in per_eng[e]:
                emb = None
                if waits and EMBED_WAIT and not dma:
                    emb = waits[-1]
                    waits = waits[:-1]
                for (k, i) in waits:
                    mult = 16 if isinstance(k, tuple) else 1
                    engobj.wait_ge(sems[k], sigcount[(k, i)] * mult)
                ins = fn(engobj)
                if emb is not None:
                    k, i = emb
                    ins._wait_ge(sems[k], sigcount[(k, i)] * (16 if isinstance(k, tuple) else 1))
                if tok in sigcount:
                    ins.then_inc(sems[tok[0]], 16 if dma else 1)
            if e == final_wait_eng:
                for t in finals:
                    engobj.wait_ge(sems[t[0]], sigcount[t] * 16)

        @blk.tensor
        def _(e):
            run("pe", e)

        @blk.scalar
        def _(e):
            run("act", e)

        @blk.vector
        def _(e):
            run("dve", e)

        @blk.gpsimd
        def _(e):
            run("pool", e)

        @blk.sync
        def _(e):
            run("sp", e)


def host_consts():
    idx = np.arange(128)
    ch = idx // 64
    same = ch[:, None] == ch[None, :]
    ident = np.eye(128, dtype=np.float32)
    maskT = (same & (idx[:, None] <= idx[None, :])).astype(np.float32)
    negT = np.where(maskT > 0, 0.0, NEGV).astype(np.float32)
    strict = (same & (idx[None, :] < idx[:, None]))
    negS = np.where(strict, 0.0, NEGV).astype(np.float32)
    mid = ch * 64 + 31
    uprime = (same & (idx[:, None] <= idx[None, :])).astype(np.float32) - \
             (same & (idx[:, None] <= mid[None, :])).astype(np.float32)
    urev = (same & (idx[:, None] > idx[None, :])).astype(np.float32)
    wc = np.zeros((128, 8), np.float32)
    wc[:, 0] = (idx <= 31)
    wc[:, 1] = (idx >= 64) & (idx <= 95)
    wc[:, 2] = (idx >= 32) & (idx <= 63)
    wc[:, 3] = (idx >= 96)
    wc[:, 4] = (idx <= 63)
    wc[:, 5] = (idx >= 64)
    blockones = same.astype(np.float32)
    lg = np.log1p(-np.exp2(-5.0 - np.arange(4, dtype=np.float64)))
    loc = idx % 64
    dt_ret = np.zeros((128, 4, 128), np.float64)
    for h in range(4):
        dt_ret[:, h, :] = np.where(maskT > 0, np.exp(lg[h] * (idx[None, :] - idx[:, None])), 0.0) * QK
    egq = np.zeros((128, 2, 128), np.float64)
    for hp in range(2):
        for hh in range(2):
            egq[hh * 64:(hh + 1) * 64, hp, :] = np.exp(lg[2 * hp + hh] * (loc[None, :] + 1))
    egrev64 = np.zeros((128, 4), np.float64)
    egrev16 = np.zeros((128, 4), np.float64)
    for h in range(4):
        egrev64[:, h] = np.exp(lg[h] * (63 - loc)) * QK
        egrev16[:, h] = np.exp(lg[h] * np.maximum(15 - idx, 0)) * QK
    egl = np.zeros((128, 2, 2), np.float64)
    for hp in range(2):
        for hh in range(2):
            egl[hh * 64:(hh + 1) * 64, hp, 0] = np.exp(lg[2 * hp + hh] * 16)
            egl[hh * 64:(hh + 1) * 64, hp, 1] = np.exp(lg[2 * hp + hh] * 64)
    sel = np.zeros((128, 4, 64), np.float32)
    selT = np.zeros((128, 4, 16), np.float32)
    gam64 = np.zeros((128, 4), np.float32)
    for h in range(4):
        for b in range(16):
            sel[b, h, h * 16 + b] = 1.0
            selT[h * 16 + b, h, b] = 1.0
            gam64[h * 16 + b, 0] = np.exp(lg[h])
    parts = [ident, maskT, negT, negS, uprime, urev, wc, blockones,
             dt_ret.reshape(128, 512), egq.reshape(128, 256), egrev64, egrev16, egl.reshape(128, 4),
             sel.reshape(128, 256), selT.reshape(128, 64), gam64]
    offs = {}
    names = ["ident", "maskT", "negT", "negS", "uprime", "urev", "wc", "blockones",
             "dt_ret", "egq", "egrev64", "egrev16", "egl", "sel", "selT", "gam64"]
    o = 0
    for nm, p in zip(names, parts):
        offs[nm] = (o, p.shape[1])
        o += p.shape[1]
    cst = np.concatenate([p.astype(np.float32) for p in parts], axis=1)
    half = 32
    inv_freq = (1.0 / (np.float32(10000.0) ** np.linspace(0.0, 1.0, half, dtype=np.float32))).astype(np.float32)
    rot = np.zeros((NT + 1, 128, 64), np.float32)
    for t in range(NT):
        pos = (np.arange(128) if t == 0 else 16 + (t - 1) * 128 + np.arange(128)).astype(np.float32)
        ang = (pos[:, None] * inv_freq[None, :]).astype(np.float32)
        rot[t, :, 0:32] = np.cos(ang)
        rot[t, :, 32:64] = np.sin(ang)
    ang = (np.full((128, 1), 16384.0, np.float32) * inv_freq[None, :]).astype(np.float32)
    rot[NT, :, 0:32] = np.cos(ang)
    rot[NT, :, 32:64] = np.sin(ang)
    return cst, offs, rot


CST, COFF, ROT = host_consts()
NCST = CST.shape[1]


def build_program(depth=DEPTH, ntiles=NT):
    nc = bass.Bass("TRN2", target_bir_lowering=False)

    def din(name, shape):
        return nc.dram_tensor(name, list(shape), F32, kind="ExternalInput").ap()

    def dout(name, shape):
        return nc.dram_tensor(name, list(shape), F32, kind="ExternalOutput").ap()

    xp = din("xp", [SEQ, D])
    meta = din("meta", [16, D])
    w_in = din("w_in", [DEPTH, D, IN_DIM])
    w_out = din("w_out", [DEPTH, D, D])
    cst_d = din("cst", [128, NCST])
    rot_d = din("rot", [NT + 1, 128, 64])
    prm_d = din("prm", [DEPTH, 128, NPRM])
    lgt_d = din("lgt", [128, 1024])
    finw_d = din("finw", [128, 1024])
    featp_d = din("featp", [128, 32 + 192])
    cbias_d = din("cbias", [1, DEPTH * 768])

    y_p = dout("y_p", [SEQ, D])
    st_hg = dout("st_hg", [DEPTH, 4, 64, 64])
    st_gd = dout("st_gd", [DEPTH, 4, 64, 64])
    st_gc = dout("st_gc", [DEPTH, 3, 768])
    st_sd = dout("st_sd", [DEPTH, 4, 128, 64])
    st_sc = dout("st_sc", [DEPTH, 3, 768])
    st_rt = dout("st_rt", [DEPTH, 4, 64, 64])
    xs_d = din("xs_in", [NS, D])
    si_hg = din("si_hg", [DEPTH, NS, 4, 64, 64]); si_gd = din("si_gd", [DEPTH, NS, 4, 64, 64])
    si_gc = din("si_gc", [DEPTH, NS, 3, 768]); si_sd = din("si_sd", [DEPTH, NS, 4, 128, 64])
    si_sc = din("si_sc", [DEPTH, NS, 3, 768]); si_rt = din("si_rt", [DEPTH, NS, 4, 64, 64])
    cwrow_d = din("cwrow", [DEPTH, 2, 4, NS, 768])
    cbrow_d = din("cbrow", [DEPTH, NS, 768])
    y_s = dout("y_s", [NS, D])
    so_hg = dout("so_hg", [DEPTH, NS, 4, 64, 64]); so_gd = dout("so_gd", [DEPTH, NS, 4, 64, 64])
    so_gc = dout("so_gc", [DEPTH, NS, 3, 768]); so_sd = dout("so_sd", [DEPTH, NS, 4, 128, 64])
    so_sc = dout("so_sc", [DEPTH, NS, 3, 768]); so_rt = dout("so_rt", [DEPTH, NS, 4, 64, 64])

    with contextlib.ExitStack() as es:
        def sb(name, shape, dt=F32):
            return es.enter_context(nc.sbuf_tensor("sb_" + name, list(shape), dt))

        P = Prog(nc)

        hscr = nc.dram_tensor("hscr", [NT * 128, D], F32, kind="Internal").ap()
        hb = [sb(f"hb{i}", [128, D]) for i in range(2)]
        win = sb("win", [128, 8, IN_DIM], BF16)
        wout = sb("wout", [128, 8, D], BF16)
        WCH = 1027
        wst = [sb(f"wst{i}", [128, WCH]) for i in range(2)]
        cst = sb("cst", [128, NCST])
        ident_bf = sb("ident_bf", [128, 128], BF16)
        bones_bf = sb("bones_bf", [128, 128], BF16)
        dtret_bf = sb("dtret_bf", [128, 4, 128], BF16)
        ones_bf = sb("ones_bf", [1, 128], BF16)
        prm = sb("prm", [128, NPRM])
        oml = sb("oml", [128, 4, 256])
        lgt = sb("lgt", [128, 4, 256])
        featp = sb("featp", [128, 32 + 192])
        dg = sb("dg", [128, 12, 4, 128], BF16)
        cbias_bf = sb("cbias_bf", [1, 768], BF16)
        nega = sb("nega", [128, 64])
        rot = [sb(f"rot{i}", [128, 64]) for i in range(2)]

        def C(name, rows=slice(0, 128), lo=0, hi=None):
            o, w = COFF[name]
            hi = w if hi is None else hi
            return cst[rows, o + lo:o + hi]

        psb = [es.enter_context(nc.psum_tensor(f"ps{i}", [128, 512], F32)) for i in range(8)]
        ps_rr = [0]

        def bank():
            i = ps_rr[0] % 4
            ps_rr[0] += 1
            return psb[i], f"ps{i}"

        import os as _os
        _AUDIT = bool(_os.environ.get("AUDIT"))
        _bad = set()

        def _chk(r, w, outs, ins):
            if not _AUDIT:
                return
            for grp, keys, what in ((outs, list(w), "W"), (ins, list(r) + list(w), "R")):
                for ap in grp:
                    nm = getattr(ap, "name", None)
                    if not isinstance(nm, str):
                        continue
                    key = nm[3:] if nm.startswith("sb_") else nm
                    if key not in keys:
                        import traceback
                        fr = traceback.extract_stack()[-3]
                        _bad.add((what, key, fr.lineno))

        _last_rb = {}

        def MM(out, lhsT, rhs, r, w, start=True, stop=True):
            skip = any(k in ("ps4", "ps5", "ps6", "ps7") for k in w)
            _chk(r, w, [out], [lhsT, rhs])
            rb = lhsT.base_partition()
            ser = False
            for k in w:
                if _last_rb.get(k, rb) != rb:
                    ser = True
                _last_rb[k] = rb
            P.op("pe", lambda e: e.matmul(out, lhsT=lhsT, rhs=rhs, start=start, stop=stop, skip_group_check=skip),
                 reads=r, writes=w, pe_serial=ser)

        def ACT(out, in_, func, r, w, scale=1.0, bias=None, accum=None):
            kw = {}
            if bias is not None:
                kw["bias"] = bias
            if accum is not None:
                kw["accum_out"] = accum
            if hasattr(bias, "name") and "eps_t" not in r:
                r = list(r) + ["eps_t"]
            _chk(r, w, [out] + ([accum] if accum is not None else []), [in_] + [x for x in (scale, bias) if hasattr(x, "name")])
            P.op("act", lambda e: e.activation(out=out, in_=in_, func=func, scale=scale, **kw), reads=r, writes=w)

        def TT(eng, out, in0, in1, op, r, w):
            _chk(r, w, [out], [in0, in1])
            P.op(eng, lambda e: e.tensor_tensor(out=out, in0=in0, in1=in1, op=op), reads=r, writes=w)

        def TS(eng, out, in0, s1, op0, r, w, s2=None, op1=None):
            _chk(r, w, [out], [in0] + [x for x in (s1, s2) if hasattr(x, "name")])
            if op1 is None:
                P.op(eng, lambda e: e.tensor_scalar(out=out, in0=in0, scalar1=s1, scalar2=None, op0=op0), reads=r, writes=w)
            else:
                P.op(eng, lambda e: e.tensor_scalar(out=out, in0=in0, scalar1=s1, scalar2=s2, op0=op0, op1=op1), reads=r, writes=w)

        def STT(out, in0, scalar, in1, op0, op1, r, w):
            _chk(r, w, [out], [in0, in1] + [x for x in (scalar,) if hasattr(x, "name")])
            P.op("dve", lambda e: e.scalar_tensor_tensor(out=out, in0=in0, scalar=scalar, in1=in1, op0=op0, op1=op1),
                 reads=r, writes=w)

        def RED(out, in_, r, w):
            _chk(r, w, [out], [in_])
            P.op("dve", lambda e: e.tensor_reduce(out=out, in_=in_, axis=AX.X, op=ALU.add), reads=r, writes=w)

        def RECIP(out, in_, r, w):
            _chk(r, w, [out], [in_])
            P.op("dve", lambda e: e.reciprocal(out=out, in_=in_), reads=r, writes=w)

        def CP(eng, out, in_, r, w):
            if eng == "act":
                ACT(out, in_, AF.Copy, r, w)
            else:
                _chk(r, w, [out], [in_])
                P.op(eng, lambda e: e.tensor_copy(out=out, in_=in_), reads=r, writes=w)

        def MEMSET(eng, ap, val, w):
            P.op(eng, lambda e: e.memset(ap, val), reads=(), writes=w)

        def DMA(out, in_, r, w, slow=False):
            if slow:
                P.op("sp", lambda e: e.dma_start(out=out, in_=in_, allow_slow_non_contiguous=True), reads=r, writes=w, dma=True)
            else:
                P.op("sp", lambda e: e.dma_start(out=out, in_=in_), reads=r, writes=w, dma=True)

        def sigmoid_from_exp(buf, key):
            TS("dve", buf, buf, 1.0, ALU.add, [key], [key])
            RECIP(buf, buf, [key], [key])

        def rsqrt_act(out, in_, scale, r, w):
            ACT(out, in_, AF.Ln, r, w, scale=scale, bias=eps_t[0:out.shape[0], 0:1])
            ACT(out, out, AF.Exp, w, w, scale=-0.5)

        eps_t = sb("eps_t", [128, 2])
        MEMSET("pool", eps_t[:, 0:1], EPS, ["eps_t"])
        MEMSET("pool", eps_t[:, 1:2], 1.0, ["eps_t"])
        DMA(cst[:], cst_d, [], ["cst"])
        DMA(lgt[:].rearrange("p a b -> p (a b)"), lgt_d, [], ["lgt"])
        DMA(featp[:], featp_d, [], ["featp"])
        CP("dve", ident_bf[:], C("ident"), ["cst"], ["ident_bf"])
        CP("dve", bones_bf[:], C("blockones"), ["cst"], ["bones_bf"])
        CP("dve", dtret_bf[:].rearrange("p a b -> p (a b)"), C("dt_ret"), ["cst"], ["dtret_bf"])
        MEMSET("pool", ones_bf[:], 1.0, ["ones_bf"])
        mx = wst[1][:, 0:256]
        TT("dve", mx, lgt[:, 0, :], lgt[:, 1, :], ALU.max, ["lgt"], ["wst1"])
        TT("dve", mx, mx, lgt[:, 2, :], ALU.max, ["lgt", "wst1"], ["wst1"])
        TT("dve", mx, mx, lgt[:, 3, :], ALU.max, ["lgt", "wst1"], ["wst1"])
        TT("dve", lgt[:], lgt[:], mx.unsqueeze(1).to_broadcast([128, 4, 256]), ALU.subtract, ["lgt", "wst1"], ["lgt"])
        ACT(lgt[:], lgt[:], AF.Exp, ["lgt"], ["lgt"])
        TT("dve", mx, lgt[:, 0, :], lgt[:, 1, :], ALU.add, ["lgt"], ["wst1"])
        TT("dve", mx, mx, lgt[:, 2, :], ALU.add, ["lgt", "wst1"], ["wst1"])
        TT("dve", mx, mx, lgt[:, 3, :], ALU.add, ["lgt", "wst1"], ["wst1"])
        RECIP(mx, mx, ["wst1"], ["wst1"])
        TT("dve", lgt[:], lgt[:], mx.unsqueeze(1).to_broadcast([128, 4, 256]), ALU.mult, ["lgt", "wst1"], ["lgt"])
        MEMSET("dve", oml[:, 0, :], 0.0, ["oml"])
        CP("dve", oml[:, 1, :], lgt[:, 1, :], ["lgt"], ["oml"])
        TT("dve", oml[:, 2, :], oml[:, 1, :], lgt[:, 2, :], ALU.add, ["lgt", "oml"], ["oml"])
        TT("dve", oml[:, 3, :], oml[:, 2, :], lgt[:, 3, :], ALU.add, ["lgt", "oml"], ["oml"])
        TS("dve", oml[:], oml[:], 0.0, ALU.max, ["oml"], ["oml"])
        TS("dve", oml[:], oml[:], -1.0, ALU.mult, ["oml"], ["oml"], s2=1.0, op1=ALU.add)
        DMA(lgt[:].rearrange("p a b -> p (a b)"), finw_d, ["lgt"], ["lgt"])
        finw = lgt[:].rearrange("p a b -> p (a b)")

        hn_bf = sb("hn_bf", [128, D], BF16)
        hnT = sb("hnT", [128, 8, 128], BF16)
        st4 = sb("st4", [128, 16])
        f1 = sb("f1", [128, 768])
        f2 = sb("f2", [128, 512])
        f3 = sb("f3", [128, 512])
        f4 = sb("f4", [128, 512])
        gate = sb("gate", [128, D])
        y_bf = sb("y_bf", [128, D], BF16)
        yT = sb("yT", [128, 8, 128], BF16)
        b1 = sb("b1", [128, 4, 128], BF16)
        qkT = sb("qkT", [128, 4, 128], BF16)
        AT = sb("AT", [128, 4, 128], BF16)
        v_bf = sb("v_bf", [128, 4, 64], BF16)
        kp_bf = sb("kp_bf", [128, 4, 128], BF16)
        qpT = sb("qpT", [128, 4, 128], BF16)
        ecs = sb("ecs", [128, 2, 8])
        S_A = sb("S_A", [128, 2, 64]); Sb_A = sb("Sb_A", [128, 2, 64], BF16); Sd_A = sb("Sd_A", [128, 2, 64])
        S_B = sb("S_B", [128, 2, 64]); Sb_B = sb("Sb_B", [128, 2, 64], BF16)
        S_C = sb("S_C", [128, 4, 64]); Sb_C = sb("Sb_C", [128, 4, 64], BF16)
        S_D = sb("S_D", [128, 2, 64]); Sb_D = sb("Sb_D", [128, 2, 64], BF16)
        tmpS = sb("tmpS", [128, 4, 64])
        uT = [sb(f"uT{i}", [128, 12, 131], BF16) for i in range(2)]
        cvst = sb("cvst", [128, 12, 3])
        xs = sb("xs", [128, 12, 128], BF16)
        xsf = sb("xsf", [128, 4, 128])
        g8 = sb("g8", [128, 64])
        gc = sb("gc", [128, 64])
        egc = sb("egc", [128, 64])
        beta = sb("beta", [128, 64])
        dtb = sb("dtb", [128, 64])
        nb = sb("nb", [128, 64])
        nb2 = sb("nb2", [128, 64])
        for _t, _k in ((g8, "g8"), (beta, "beta"), (nega, "nega")):
            MEMSET("pool", _t[:], 0.0, [_k])
        tt = sb("tt", [128, 8, 128])
        DT = sb("DT", [128, 8, 128], BF16)
        Dst = sb("Dst", [128, 4, 128], BF16)
        eGbc = sb("eGbc", [128, 8, 128])
        eGlB = sb("eGlB", [128, 2, 2])
        X_bf = [sb(f"X_bf{i}", [128, 4, 128], BF16) for i in range(2)]
        Y_bf = [sb(f"Y_bf{i}", [128, 4, 128], BF16) for i in range(2)]
        P_bf = sb("P_bf", [128, 4, 128], BF16)
        bv = sb("bv", [128, 4, 64])
        r_bf = sb("r_bf", [128, 4, 64], BF16)
        u_bf = sb("u_bf", [128, 4, 64], BF16)
        xd_bf = sb("xd_bf", [128, 4, 64], BF16)

        def load_layer(l):
            DMA(prm[:], prm_d[l], [], ["prm"])
            DMA(wst[0][0:1, 0:768], cbias_d[0:1, l * 768:(l + 1) * 768], [], ["wst0"])
            CP("pool", cbias_bf[:], wst[0][0:1, 0:768], ["wst0"], ["cbias_bf"])
            ACT(nega[:, 0:8], prm[:, 1280:1288], AF.Exp, ["prm"], ["nega"])
            TS("dve", nega[:, 0:64], nega[:, 0:64], -1.0, ALU.mult, ["nega"], ["nega"])
            for cv in range(2):
                for blk in range(6):
                    for w in range(4):
                        col = 32 + l * 48 + cv * 24 + blk * 4 + w
                        if (blk + w) % 2 == 0:
                            ACT(dg[:, cv * 6 + blk, w, :], ident_bf[:], AF.Copy, ["ident_bf", "featp"], ["dg"],
                                scale=featp[:, col:col + 1])
                        else:
                            TS("dve", dg[:, cv * 6 + blk, w, :], ident_bf[:], featp[:, col:col + 1], ALU.mult,
                               ["ident_bf", "featp"], ["dg"])
            i = 0
            for kc in range(8):
                for c0 in range(0, IN_DIM, WCH):
                    st = wst[i % 2]; sk = f"wst{i % 2}"
                    DMA(st[:, 0:WCH], w_in[l, kc * 128:(kc + 1) * 128, c0:c0 + WCH], [], [sk])
                    if i % 2 == 0:
                        ACT(win[:, kc, c0:c0 + WCH], st[:, 0:WCH], AF.Copy, [sk, "featp"], ["win"],
                            scale=featp[:, l * 8 + kc:l * 8 + kc + 1])
                    else:
                        TS("dve", win[:, kc, c0:c0 + WCH], st[:, 0:WCH], featp[:, l * 8 + kc:l * 8 + kc + 1], ALU.mult,
                           [sk, "featp"], ["win"])
                    i += 1
            for kc in range(8):
                st = wst[i % 2]; sk = f"wst{i % 2}"
                DMA(st[:, 0:1024], w_out[l, kc * 128:(kc + 1) * 128, :], [], [sk])
                CP("act" if i % 2 == 0 else "dve", wout[:, kc, :], st[:, 0:1024], [sk], ["wout"])
                i += 1
            for nm, S_, Sb_ in (("A", S_A, Sb_A), ("B", S_B, Sb_B), ("C", S_C, Sb_C), ("D", S_D, Sb_D)):
                MEMSET("pool", S_[:], 0.0, ["S_" + nm])
                MEMSET("pool", Sb_[:], 0.0, ["Sb_" + nm])
            MEMSET("pool", uT[0][:, :, 0:3], 0.0, ["uT0"])

        def tile_fwd(l, t):
            n = 16 if t == 0 else 128
            chunks = [(0, 16)] if t == 0 else [(0, 64), (64, 128)]
            nch = len(chunks)
            clen = chunks[0][1]
            last_tile = (t == ntiles - 1)
            hk = f"hb{t % 2}"
            ht = hb[t % 2][0:n, :]
            if l == 0:
                DMA(ht, meta if t == 0 else xp[(t - 1) * 128:t * 128, :], [], [hk])
            else:
                DMA(ht, hscr[t * 128:t * 128 + n, :], [f"hd{t}"], [hk])
            cur = uT[t % 2]; curk = f"uT{t % 2}"
            nxt = uT[(t + 1) % 2]; nxtk = f"uT{(t + 1) % 2}"
            rt = rot[t % 2]; rtk = f"rot{t % 2}"
            DMA(rt[:], rot_d[t], [], [rtk])

            def bc_h(ap2d, nh=4):
                return ap2d.unsqueeze(1).to_broadcast([ap2d.shape[0], nh, ap2d.shape[1]])

            def bc_l(ap2d, m):
                return ap2d.unsqueeze(2).to_broadcast([ap2d.shape[0], ap2d.shape[1], m])

            ACT(hn_bf[0:n, :], ht, AF.Square, [hk], ["hn_bf", "st4"], accum=st4[0:n, 0:1])
            rsqrt_act(st4[0:n, 1:2], st4[0:n, 0:1], 1.0 / D, ["st4"], ["st4"])
            ACT(hn_bf[0:n, :], ht, AF.Copy, [hk, "st4"], ["hn_bf"], scale=st4[0:n, 1:2])
            for half in range(2):
                pt, pk = bank()
                for kk in range(4):
                    kc = half * 4 + kk
                    MM(pt[:, kk * 128:kk * 128 + n], hn_bf[0:n, kc * 128:(kc + 1) * 128], ident_bf[0:n, 0:n],
                       ["hn_bf", "ident_bf"], [pk])
                CP("act" if half else "dve", hnT[:, half * 4:half * 4 + 4, 0:n],
                   pt[:, :].rearrange("p (a b) -> p a b", a=4)[:, :, 0:n], [pk], ["hnT"])

            def proj_tok(c0, c1, extra=None):
                pt, pk = bank()
                for kc in range(8):
                    MM(pt[0:n, 0:c1 - c0], hnT[:, kc, 0:n], win[:, kc, c0:c1], ["hnT", "win"], [pk],
                       start=(kc == 0), stop=(kc == 7 and extra is None))
                if extra is not None:
                    MM(pt[0:n, extra[0]:extra[0] + 4], C("ident", slice(0, n), 0, n), prm[0:n, extra[1]:extra[1] + 4],
                       ["cst", "prm"], [pk], start=False, stop=True)
                return pt, pk

            def proj_feat(cols0, nblk):
                pt, pk = bank()
                for b_ in range(nblk):
                    for kc in range(8):
                        MM(pt[:, b_ * 128:b_ * 128 + n], win[:, kc, cols0 + b_ * 128:cols0 + (b_ + 1) * 128], hnT[:, kc, 0:n],
                           ["hnT", "win"], [pk], start=(kc == 0), stop=(kc == 7))
                return pt, pk

            def v3(ps_ap, a):
                return ps_ap.rearrange("p (a b) -> p a b", a=a)

            pA0, kA0 = proj_tok(0, 512)
            pA1, kA1 = proj_tok(512, 1024)
            ACT(f1[0:n, 0:256], pA0[0:n, 0:256], AF.Exp, [kA0], ["f1"], scale=-1.0)
            ACT(f1[0:n, 256:512], pA0[0:n, 256:512], AF.Exp, [kA0], ["f1"])
            ACT(f1[0:n, 512:768], pA1[0:n, 256:512], AF.Exp, [kA1], ["f1"], scale=-1.0)
            sigmoid_from_exp(f1[0:n, :], "f1")
            STT(f2[0:n, 0:256], pA0[0:n, 0:256], QK, f1[0:n, 0:256], ALU.mult, ALU.mult, [kA0, "f1"], ["f2"])
            TT("dve", f2[0:n, 256:512], f1[0:n, 256:512], oml[0:n, l, :], ALU.mult, ["f1", "oml"], ["f2"])
            TT("pool", f1[0:n, 512:768], f1[0:n, 512:768], prm[0:n, 0:256], ALU.mult, ["f1", "prm"], ["f1"])
            TT("dve", gate[0:n, 0:256], pA1[0:n, 256:512], f1[0:n, 512:768], ALU.mult, [kA1, "f1"], ["gate"])
            ACT(v_bf[0:n, :, :].rearrange("p a b -> p (a b)"), pA1[0:n, 0:256], AF.Copy, [kA1], ["v_bf"])
            ACT(f3[0:n, 0:256], f2[0:n, 256:512], AF.Ln, ["f2"], ["f3"], scale=-1.0, bias=eps_t[0:n, 1:2])
            pG, kG = bank()
            MM(pG[0:n, 0:256], C("uprime", slice(0, n), 0, n), f3[0:n, 0:256], ["cst", "f3"], [kG])
            for hp in range(2):
                MM(pG[:, 256 + hp * 8:256 + hp * 8 + 8], f3[0:n, hp * 128:(hp + 1) * 128], C("wc", slice(0, n)),
                   ["cst", "f3"], [kG])
            ACT(f3[0:n, 0:256], pG[0:n, 0:256], AF.Exp, [kG], ["f3"])
            ACT(f3[0:n, 256:512], pG[0:n, 0:256], AF.Exp, [kG], ["f3"], scale=-1.0)
            ACT(ecs[:].rearrange("p a b -> p (a b)"), pG[:, 256:272], AF.Exp, [kG], ["ecs"])
            TT("dve", b1[0:n, 0:2, :].rearrange("p a b -> p (a b)"), f2[0:n, 0:256], f3[0:n, 0:256], ALU.mult,
               ["f2", "f3"], ["b1"])
            TT("dve", b1[0:n, 2:4, :].rearrange("p a b -> p (a b)"), f2[0:n, 256:512], f3[0:n, 256:512], ALU.mult,
               ["f2", "f3"], ["b1"])
            pT, kT = bank()
            for blk in range(4):
                MM(pT[:, blk * 128:blk * 128 + n], b1[0:n, blk, :], ident_bf[0:n, 0:n], ["b1", "ident_bf"], [kT])
            CP("act", qkT[:, :, 0:n], v3(pT[:, :], 4)[:, :, 0:n], [kT], ["qkT"])
            pS, kS = bank()
            for hd in (0, 2, 1, 3):
                hp, hh = hd // 2, hd % 2
                rows = slice(hh * 64, hh * 64 + 64)
                MM(pS[0:n, hd * 128:hd * 128 + n], qkT[rows, 2 + hp, 0:n], qkT[rows, hp, 0:n], ["qkT"], [kS])
            TT("dve", AT[0:n, :, 0:n], v3(pS[0:n, :], 4)[:, :, 0:n], bc_h(C("maskT", slice(0, n), 0, n)), ALU.mult,
               [kS, "cst"], ["AT"])
            pO, kO = psb[4], "ps4"
            pOD, kOD = psb[7], "ps7"
            for hd in (0, 2, 1, 3):
                MM(pO[0:n, hd * 64:hd * 64 + 64], AT[0:n, hd, 0:n], v_bf[0:n, hd, :], ["AT", "v_bf"], [kO],
                   start=(hd == 0), stop=False)
            for ci, (c0, c1) in enumerate(chunks):
                TT("dve", Sb_A[:], S_A[:], bc_l(ecs[:, :, ci], 64), ALU.mult, ["S_A", "ecs"], ["Sb_A"])
                TT("pool", Sd_A[:], S_A[:], bc_l(ecs[:, :, 4 + ci], 64), ALU.mult, ["S_A", "ecs"], ["Sd_A"])
                pK, kK = bank()
                for hd in (0, 2, 1, 3):
                    hp, hh = hd // 2, hd % 2
                    rows = slice(hh * 64, hh * 64 + 64)
                    MM(pO[c0:c1, hd * 64:hd * 64 + 64], qkT[rows, hp, c0:c1], Sb_A[rows, hp, :], ["qkT", "Sb_A"], [kO],
                       start=False, stop=True)
                    MM(pK[rows, hp * 64:hp * 64 + 64], b1[c0:c1, 2 + hp, hh * 64:hh * 64 + 64], v_bf[c0:c1, hd, :],
                       ["b1", "v_bf"], [kK])
                TT("dve", tmpS[:, 0:2, :], v3(pK[:, 0:128], 2), bc_l(ecs[:, :, 2 + ci], 64), ALU.mult, [kK, "ecs"], ["tmpS"])
                TT("dve", S_A[:], tmpS[:, 0:2, :], Sd_A[:], ALU.add, ["tmpS", "Sd_A"], ["S_A"])

            def head_norm(ps_ap, pskey, gcols, ycols):
                ACT(f4[0:n, 0:256], ps_ap, AF.Square, [pskey], ["f4"])
                RED(st4[0:n, 4:8], v3(f4[0:n, 0:256], 4), ["f4"], ["st4"])
                rsqrt_act(st4[0:n, 8:12], st4[0:n, 4:8], 1.0 / 64, ["st4"], ["st4"])
                TT("dve", v3(f4[0:n, 0:256], 4), v3(ps_ap, 4), bc_l(st4[0:n, 8:12], 64), ALU.mult, [pskey, "st4"], ["f4"])
                TT("dve", y_bf[0:n, ycols], f4[0:n, 0:256], gate[0:n, gcols], ALU.mult, ["f4", "gate"], ["y_bf"])

            head_norm(pO[0:n, 0:256], kO, slice(0, 256), slice(0, 256))

            pD0, kD0 = proj_tok(3084, 3596)
            pD1, kD1 = proj_tok(3596, 4108)
            cosb = rt[0:n, 0:32].unsqueeze(1).to_broadcast([n, 16, 32])
            sinb = rt[0:n, 32:64].unsqueeze(1).to_broadcast([n, 16, 32])
            qk4 = pD0[0:n, :].rearrange("p (a b) -> p a b", a=16)
            TT("dve", f1[0:n, 0:512].rearrange("p (a b) -> p a b", a=16), qk4, cosb, ALU.mult, [kD0, rtk], ["f1"])
            TT("dve", f2[0:n, 0:512].rearrange("p (a b) -> p a b", a=16), qk4, sinb, ALU.mult, [kD0, rtk], ["f2"])
            c4 = f1[0:n, 0:512].rearrange("p (a s b) -> p a s b", a=8, s=2)
            s4 = f2[0:n, 0:512].rearrange("p (a s b) -> p a s b", a=8, s=2)
            qkr = b1[0:n, :, :].rearrange("p a (s b) -> p a s b", s=4)
            qkr8 = b1[0:n, :, :].rearrange("p a b -> p (a b)").rearrange("p (a s b) -> p a s b", a=8, s=2)
            TT("dve", qkr8[:, :, 0, :], c4[:, :, 0, :], s4[:, :, 1, :], ALU.subtract, ["f1", "f2"], ["b1"])
            TT("dve", qkr8[:, :, 1, :], c4[:, :, 1, :], s4[:, :, 0, :], ALU.add, ["f1", "f2"], ["b1"])
            ACT(v_bf[0:n, :, :].rearrange("p a b -> p (a b)"), pD1[0:n, 0:256], AF.Copy, [kD1], ["v_bf"])
            ACT(f1[0:n, 512:768], pD1[0:n, 256:512], AF.Exp, [kD1], ["f1"], scale=-1.0)
            sigmoid_from_exp(f1[0:n, 512:768], "f1")
            TT("dve", gate[0:n, 768:1024], pD1[0:n, 256:512], f1[0:n, 512:768], ALU.mult, [kD1, "f1"], ["gate"])
            pT, kT = bank()
            for blk in range(4):
                MM(pT[:, blk * 128:blk * 128 + n], b1[0:n, blk, :], ident_bf[0:n, 0:n], ["b1", "ident_bf"], [kT])
            CP("act", qkT[:, :, 0:n], v3(pT[:, :], 4)[:, :, 0:n], [kT], ["qkT"])
            egq = C("egq").rearrange("p (a b) -> p a b", a=2)
            TT("dve", qpT[:, 0:2, 0:n], qkT[:, 0:2, 0:n], egq[:, :, 0:n], ALU.mult, ["qkT", "cst"], ["qpT"])
            egrev = C("egrev16" if t == 0 else "egrev64", slice(0, n))
            TT("dve", kp_bf[0:n, :, 0:64], b1[0:n, 2:4, :].rearrange("p a (s b) -> p (a s) b", s=2), bc_l(egrev, 64), ALU.mult,
               ["b1", "cst"], ["kp_bf"])
            pS, kS = bank()
            for hd in (0, 2, 1, 3):
                hp, hh = hd // 2, hd % 2
                rows = slice(hh * 64, hh * 64 + 64)
                MM(pS[0:n, hd * 128:hd * 128 + n], qkT[rows, 2 + hp, 0:n], qkT[rows, hp, 0:n], ["qkT"], [kS])
            TT("dve", AT[0:n, :, 0:n], v3(pS[0:n, :], 4)[:, :, 0:n], dtret_bf[0:n, :, 0:n], ALU.mult, [kS, "dtret_bf"], ["AT"])
            for hd in (0, 2, 1, 3):
                MM(pOD[0:n, hd * 64:hd * 64 + 64], AT[0:n, hd, 0:n], v_bf[0:n, hd, :], ["AT", "v_bf"], [kOD],
                   start=(hd == 0), stop=False)
            egl = C("egl").rearrange("p (a b) -> p a b", a=2)
            for ci, (c0, c1) in enumerate(chunks):
                pK, kK = bank()
                for hd in (0, 2, 1, 3):
                    hp, hh = hd // 2, hd % 2
                    rows = slice(hh * 64, hh * 64 + 64)
                    MM(pOD[c0:c1, hd * 64:hd * 64 + 64], qpT[rows, hp, c0:c1], Sb_D[rows, hp, :], ["qpT", "Sb_D"], [kOD],
                       start=False, stop=True)
                    MM(pK[rows, hp * 64:hp * 64 + 64], kp_bf[c0:c1, hd, 0:64], v_bf[c0:c1, hd, :], ["kp_bf", "v_bf"], [kK])
                TT("pool", tmpS[:, 0:2, :], S_D[:], bc_l(egl[:, :, (0 if t == 0 else 1)], 64), ALU.mult, ["S_D", "cst"], ["tmpS"])
                TT("dve", S_D[:], tmpS[:, 0:2, :], v3(pK[:, 0:128], 2), ALU.add, ["tmpS", kK], ["S_D"])
                CP("pool", Sb_D[:], S_D[:], ["S_D"], ["Sb_D"])
            oD = pOD[0:n, 0:256]
            kO_ = kOD
            RED(st4[0:n, 4:8], v3(oD, 4), [kO_], ["st4"])
            TS("dve", st4[0:n, 4:8], st4[0:n, 4:8], -1.0 / 64, ALU.mult, ["st4"], ["st4"])
            TT("dve", v3(f3[0:n, 0:256], 4), v3(oD, 4), bc_l(st4[0:n, 4:8], 64), ALU.add, [kO_, "st4"], ["f3"])
            ACT(f4[0:n, 0:256], f3[0:n, 0:256], AF.Square, ["f3"], ["f4"])
            RED(st4[0:n, 4:8], v3(f4[0:n, 0:256], 4), ["f4"], ["st4"])
            rsqrt_act(st4[0:n, 8:12], st4[0:n, 4:8], 1.0 / 64, ["st4"], ["st4"])
            TT("dve", v3(f3[0:n, 0:256], 4), v3(f3[0:n, 0:256], 4), bc_l(st4[0:n, 8:12], 64), ALU.mult, ["f3", "st4"], ["f3"])
            TT("pool", f3[0:n, 0:256], f3[0:n, 0:256], prm[0:n, 768:1024], ALU.mult, ["f3", "prm"], ["f3"])
            TT("pool", f3[0:n, 0:256], f3[0:n, 0:256], prm[0:n, 1024:1280], ALU.add, ["f3", "prm"], ["f3"])
            TT("dve", y_bf[0:n, 768:1024], f3[0:n, 0:256], gate[0:n, 768:1024], ALU.mult, ["f3", "gate"], ["y_bf"])

            pBz, kBz = proj_tok(1792, 2056, (256, 1288))
            pCz, kCz = proj_tok(2824, 3084, (256, 1292))
            ACT(f1[0:n, 512:768], pBz[0:n, 0:256], AF.Exp, [kBz], ["f1"], scale=-1.0)
            ACT(f2[0:n, 0:256], pCz[0:n, 0:256], AF.Exp, [kCz], ["f2"], scale=-1.0)
            ACT(beta[0:n, 0:4], pBz[0:n, 260:264], AF.Exp, [kBz], ["beta"], scale=-1.0)
            ACT(g8[0:n, 0:4], pBz[0:n, 256:260], AF.Exp, [kBz], ["g8"])
            ACT(g8[0:n, 4:8], pCz[0:n, 256:260], AF.Exp, [kCz], ["g8"])
            sigmoid_from_exp(f1[0:n, 512:768], "f1")
            sigmoid_from_exp(f2[0:n, 0:256], "f2")
            TT("pool", f1[0:n, 512:768], f1[0:n, 512:768], prm[0:n, 256:512], ALU.mult, ["f1", "prm"], ["f1"])
            TT("dve", gate[0:n, 256:512], pBz[0:n, 0:256], f1[0:n, 512:768], ALU.mult, [kBz, "f1"], ["gate"])
            TT("dve", gate[0:n, 512:768], pCz[0:n, 0:256], f2[0:n, 0:256], ALU.mult, [kCz, "f2"], ["gate"])
            for cv, cols0 in ((0, 1024), (1, 2056)):
                for part, (b0, nb_) in enumerate(((0, 4), (4, 2))):
                    pf, kf = proj_feat(cols0 + b0 * 128, nb_)
                    src = v3(pf[:, 0:nb_ * 128], nb_)[:, :, 0:n]
                    CP("act", cur[:, cv * 6 + b0:cv * 6 + b0 + nb_, 3:3 + n], src, [kf], [curk])
                    if last_tile:
                        CP("dve", cvst[:, cv * 6 + b0:cv * 6 + b0 + nb_, :], src[:, :, n - 3:n], [kf], ["cvst"])
            if not last_tile:
                CP("pool", nxt[:, :, 0:3], cur[:, :, n:n + 3], [curk], [nxtk])
            for cv in range(2):
                for part, (b0, nb_) in enumerate(((0, 4), (4, 2))):
                    pc, kc_ = bank()
                    for b_ in range(nb_):
                        blk = cv * 6 + b0 + b_
                        for w in range(4):
                            MM(pc[:, b_ * 128:b_ * 128 + n], dg[:, blk, w, :], cur[:, blk, w:w + n], ["dg", curk], [kc_],
                               start=(w == 0), stop=(w == 3 and cv == 0))
                        if cv == 1:
                            MM(pc[:, b_ * 128:b_ * 128 + n], cbias_bf[0:1, (b0 + b_) * 128:(b0 + b_ + 1) * 128], ones_bf[0:1, 0:n],
                               ["cbias_bf", "ones_bf"], [kc_], start=False, stop=True)
                    src = v3(pc[:, 0:nb_ * 128], nb_)[:, :, 0:n]
                    dstf = v3(f1[:, 0:nb_ * 128], nb_)[:, :, 0:n]
                    ACT(dstf, src, AF.Exp, [kc_], ["f1"], scale=-1.0)
                    sigmoid_from_exp(dstf, "f1")
                    if cv == 0 and part == 0:
                        TT("dve", xsf[:, :, 0:n], src, dstf, ALU.mult, [kc_, "f1"], ["xsf"])
                    else:
                        TT("dve", xs[:, cv * 6 + b0:cv * 6 + b0 + nb_, 0:n], src, dstf, ALU.mult, [kc_, "f1"], ["xs"])
            ACT(b1[:, :, 0:n], xsf[:, :, 0:n], AF.Square, ["xsf"], ["b1"])
            pN, kN = bank()
            for blk in range(4):
                MM(pN[:, blk * 128:blk * 128 + n], bones_bf[:], b1[:, blk, 0:n], ["bones_bf", "b1"], [kN])
            srcN = v3(pN[:, :], 4)[:, :, 0:n]
            dstN = v3(f2[:, 0:512], 4)[:, :, 0:n]
            ACT(dstN, srcN, AF.Ln, [kN], ["f2"], bias=eps_t[:, 0:1])
            ACT(dstN, dstN, AF.Exp, ["f2"], ["f2"], scale=-0.5)
            STT(xs[:, 0:2, 0:n], xsf[:, 0:2, 0:n], QK, dstN[:, 0:2, :], ALU.mult, ALU.mult, ["xsf", "f2"], ["xs"])
            TT("dve", xs[:, 2:4, 0:n], xsf[:, 2:4, 0:n], dstN[:, 2:4, :], ALU.mult, ["xsf", "f2"], ["xs"])

            ACT(g8[0:n, 0:8], g8[0:n, 0:8], AF.Ln, ["g8"], ["g8"], bias=eps_t[0:n, 1:2])
            CP("dve", dtb[0:n, :], g8[0:n, :], ["g8"], ["dtb"])
            TT("dve", g8[0:n, :], g8[0:n, :], nega[0:n, :], ALU.mult, ["g8", "nega"], ["g8"])
            sigmoid_from_exp(beta[0:n, :], "beta")
            pDc, kDc = bank()
            MM(pDc[0:n, 0:32], C("maskT", slice(0, n), 0, n), g8[0:n, 0:32], ["cst", "g8"], [kDc])
            MM(pDc[0:n, 32:64], C("urev", slice(0, n), 0, n), g8[0:n, 0:32], ["cst", "g8"], [kDc])
            CP("dve", gc[0:n, :], pDc[0:n, 0:64], [kDc], ["gc"])
            ACT(egc[0:n, :], gc[0:n, :], AF.Exp, ["gc"], ["egc"])
            for half in range(2):
                pB_, kB_ = bank()
                CP("dve", v3(f4[0:n, 0:512], 4), bc_l(g8[0:n, half * 4:half * 4 + 4], 128), ["g8"], ["f4"])
                for hd in (0, 2, 1, 3):
                    MM(pB_[:, hd * 128:hd * 128 + n], f4[0:n, hd * 128:(hd + 1) * 128],
                       C("maskT", slice(0, n), 0, n), ["cst", "f4"], [kB_])
                srcB = v3(pB_[:, :], 4)[:, :, 0:n]
                ACT(eGbc[:, half * 4:half * 4 + 4, 0:n], srcB, AF.Exp, [kB_], ["eGbc"])
                TT("dve", tt[0:n, half * 4:half * 4 + 4, 0:n], srcB[0:n], bc_l(gc[0:n, half * 4:half * 4 + 4], n), ALU.subtract,
                   [kB_, "gc"], ["tt"])
            if True:
                TT("dve", v3(f1[0:n, 0:512], 4)[:, :, 0:n], tt[0:n, 0:4, 0:n], bc_h(C("negS", slice(0, n), 0, n)), ALU.subtract,
                   ["tt", "cst"], ["f1"])
                ACT(Dst[0:n, :, 0:n], v3(f1[0:n, 0:512], 4)[:, :, 0:n], AF.Exp, ["f1"], ["Dst"], scale=-1.0)
                TT("dve", tt[0:n, :, 0:n], tt[0:n, :, 0:n], bc_h(C("negT", slice(0, n), 0, n), 8), ALU.add, ["tt", "cst"], ["tt"])
                ACT(DT[0:n, :, 0:n], tt[0:n, :, 0:n], AF.Exp, ["tt"], ["DT"])

            pS, kS = bank()
            pKK, kKK = bank()
            for hd in (0, 2, 1, 3):
                hp, hh = hd // 2, hd % 2
                rows = slice(hh * 64, hh * 64 + 64)
                MM(pS[0:n, hd * 128:hd * 128 + n], xs[rows, 2 + hp, 0:n], xs[rows, hp, 0:n], ["xs"], [kS])
                MM(pKK[0:n, hd * 128:hd * 128 + n], xs[rows, 2 + hp, 0:n], xs[rows, 2 + hp, 0:n], ["xs"], [kKK])
            TT("dve", AT[0:n, :, 0:n], v3(pS[0:n, :], 4)[:, :, 0:n], DT[0:n, 0:4, 0:n], ALU.mult, [kS, "DT"], ["AT"])
            TS("dve", nb[0:n, :], beta[0:n, :], -1.0, ALU.mult, ["beta"], ["nb"])
            TT("dve", v3(f1[0:n, 0:512], 4)[:, :, 0:n], v3(pKK[0:n, :], 4)[:, :, 0:n], Dst[0:n, :, 0:n], ALU.mult, [kKK, "Dst"], ["f1"])
            TT("dve", X_bf[0][0:n, :, 0:n], v3(f1[0:n, 0:512], 4)[:, :, 0:n], bc_l(nb[0:n, 0:4], n), ALU.mult,
               ["f1", "nb"], ["X_bf0"])
            pY, kY = bank()
            for hd in (0, 2, 1, 3):
                MM(pY[0:n, hd * 128:hd * 128 + n], X_bf[0][0:n, hd, 0:n], ident_bf[0:n, 0:n], ["X_bf0", "ident_bf"], [kY])
            CP("act", Y_bf[0][0:n, :, 0:n], v3(pY[0:n, :], 4)[:, :, 0:n], [kY], ["Y_bf0"])
            TT("dve", P_bf[0:n, :, 0:n], v3(pY[0:n, :], 4)[:, :, 0:n], bc_h(C("ident", slice(0, n), 0, n)), ALU.add,
               [kY, "cst"], ["P_bf"])
            nlev = int(math.ceil(math.log2(clen))) - 1
            ci_ = 0
            for lev in range(nlev):
                ni_ = 1 - ci_
                pX2, kX2 = bank()
                for hd in (0, 2, 1, 3):
                    MM(pX2[0:n, hd * 128:hd * 128 + n], Y_bf[ci_][0:n, hd, 0:n], X_bf[ci_][0:n, hd, 0:n],
                       [f"Y_bf{ci_}", f"X_bf{ci_}"], [kX2])
                CP("act", X_bf[ni_][0:n, :, 0:n], v3(pX2[0:n, :], 4)[:, :, 0:n], [kX2], [f"X_bf{ni_}"])
                if lev < nlev - 1:
                    pY2, kY2 = bank()
                    for hd in (0, 2, 1, 3):
                        MM(pY2[0:n, hd * 128:hd * 128 + n], X_bf[ci_][0:n, hd, 0:n], Y_bf[ci_][0:n, hd, 0:n],
                           [f"Y_bf{ci_}", f"X_bf{ci_}"], [kY2])
                    CP("dve", Y_bf[ni_][0:n, :, 0:n], v3(pY2[0:n, :], 4)[:, :, 0:n], [kY2], [f"Y_bf{ni_}"])
                pP, kP = bank()
                for hd in (0, 2, 1, 3):
                    MM(pP[0:n, hd * 128:hd * 128 + n], X_bf[ni_][0:n, hd, 0:n], P_bf[0:n, hd, 0:n], [f"X_bf{ni_}", "P_bf"], [kP])
                TT("dve", P_bf[0:n, :, 0:n], P_bf[0:n, :, 0:n], v3(pP[0:n, :], 4)[:, :, 0:n], ALU.add, ["P_bf", kP], ["P_bf"])
                ci_ = ni_
            pT, kT = bank()
            for blk in range(4):
                MM(pT[0:n, blk * 128:(blk + 1) * 128], xs[:, 2 + blk, 0:n], ident_bf[:, :], ["xs", "ident_bf"], [kT])
            TT("dve", kp_bf[0:n, :, 0:64], v3(pT[0:n, 0:256], 4), bc_l(egc[0:n, 32:36], 64), ALU.mult, [kT, "egc"], ["kp_bf"])
            TT("dve", bv[0:n, :, :], v3(pT[0:n, 256:512], 4), bc_l(beta[0:n, 0:4], 64), ALU.mult, [kT, "beta"], ["bv"])
            TT("dve", nb2[0:n, :], nb[0:n, :], egc[0:n, :], ALU.mult, ["nb", "egc"], ["nb2"])
            for hh in range(2):
                rows = slice(hh * 64, hh * 64 + 64)
                TT("pool", qpT[rows, 0:2, 0:n], xs[rows, 0:2, 0:n], eGbc[rows, hh:4:2, 0:n], ALU.mult, ["xs", "eGbc"], ["qpT"])
            pO2, kO2 = psb[5], "ps5"
            pOC, kOC = psb[6], "ps6"
            for ci, (c0, c1) in enumerate(chunks):
                pW, kW = bank()
                for hd in (0, 2, 1, 3):
                    hp, hh = hd // 2, hd % 2
                    rows = slice(hh * 64, hh * 64 + 64)
                    MM(pW[c0:c1, hd * 64:hd * 64 + 64], xs[rows, 2 + hp, c0:c1], Sb_B[rows, hp, :], ["xs", "Sb_B"], [kW])
                TT("dve", v3(f4[c0:c1, 0:256], 4), v3(pW[c0:c1, 0:256], 4), bc_l(nb2[c0:c1, 0:4], 64), ALU.mult,
                   [kW, "nb2"], ["f4"])
                TT("dve", r_bf[c0:c1, :, :], v3(f4[c0:c1, 0:256], 4), bv[c0:c1, :, :], ALU.add, ["f4", "bv"], ["r_bf"])
                pU, kU = bank()
                for hd in (0, 2, 1, 3):
                    MM(pU[c0:c1, hd * 64:hd * 64 + 64], P_bf[c0:c1, hd, c0:c1], r_bf[c0:c1, hd, :], ["P_bf", "r_bf"], [kU])
                CP("act", u_bf[c0:c1, :, :], v3(pU[c0:c1, 0:256], 4), [kU], ["u_bf"])
                pK, kK = bank()
                for hd in (0, 2, 1, 3):
                    hp, hh = hd // 2, hd % 2
                    rows = slice(hh * 64, hh * 64 + 64)
                    MM(pO2[c0:c1, hd * 64:hd * 64 + 64], AT[c0:c1, hd, c0:c1], u_bf[c0:c1, hd, :], ["AT", "u_bf"], [kO2],
                       start=(hd == 0), stop=False)
                    MM(pO2[c0:c1, hd * 64:hd * 64 + 64], qpT[rows, hp, c0:c1], Sb_B[rows, hp, :], ["qpT", "Sb_B"], [kO2],
                       start=False, stop=True)
                    MM(pK[rows, hp * 64:hp * 64 + 64], kp_bf[c0:c1, hd, 0:64], u_bf[c0:c1, hd, :], ["kp_bf", "u_bf"], [kK])
                for hh in range(2):
                    rows = slice(hh * 64, hh * 64 + 64)
                    TT("pool", tmpS[rows, 0:2, :], S_B[rows, :, :], eGbc[rows, hh:4:2, c1 - 1:c1].to_broadcast([64, 2, 64]), ALU.mult,
                       ["S_B", "eGbc"], ["tmpS"])
                TT("dve", S_B[:], tmpS[:, 0:2, :], v3(pK[:, 0:128], 2), ALU.add, ["tmpS", kK], ["S_B"])
                CP("pool", Sb_B[:], S_B[:], ["S_B"], ["Sb_B"])
            head_norm(pO2[0:n, 0:256], kO2, slice(256, 512), slice(256, 512))

            pS, kS = bank()
            for g in range(2):
                MM(pS[0:n, g * 128:g * 128 + n], xs[:, 8 + g, 0:n], xs[:, 10 + g, 0:n], ["xs"], [kS])
            for g in range(2):
                TT("dve", AT[0:n, 2 * g:2 * g + 2, 0:n], pS[0:n, g * 128:g * 128 + n].unsqueeze(1).to_broadcast([n, 2, n]),
                   DT[0:n, 4 + 2 * g:4 + 2 * g + 2, 0:n], ALU.mult, [kS, "DT"], ["AT"])
            pT, kT = bank()
            for blk in range(4):
                MM(pT[0:n, blk * 128:(blk + 1) * 128], xs[:, 6 + blk, 0:n], ident_bf[:, :], ["xs", "ident_bf"], [kT])
            TT("dve", v_bf[0:n, :, :], v3(pT[0:n, 0:256], 4), bc_l(dtb[0:n, 4:8], 64), ALU.mult, [kT, "dtb"], ["v_bf"])
            TT("dve", xd_bf[0:n, :, :], v3(pT[0:n, 0:256], 4), bc_l(prm[0:n, 1296:1300], 64), ALU.mult, [kT, "prm"], ["xd_bf"])
            for g in range(2):
                TT("dve", kp_bf[0:n, 2 * g:2 * g + 2, :], pT[0:n, 256 + g * 128:256 + (g + 1) * 128].unsqueeze(1).to_broadcast([n, 2, 128]),
                   bc_l(egc[0:n, 36 + 2 * g:36 + 2 * g + 2], 128), ALU.mult, [kT, "egc"], ["kp_bf"])
                TT("pool", qpT[:, 2 * g:2 * g + 2, 0:n], xs[:, 10 + g, 0:n].unsqueeze(1).to_broadcast([128, 2, n]),
                   eGbc[:, 4 + 2 * g:4 + 2 * g + 2, 0:n], ALU.mult, ["xs", "eGbc"], ["qpT"])
            for hd in (0, 2, 1, 3):
                MM(pOC[0:n, hd * 64:hd * 64 + 64], AT[0:n, hd, 0:n], v_bf[0:n, hd, :], ["AT", "v_bf"], [kOC],
                   start=(hd == 0), stop=False)
            MM(pOC[0:n, 0:256], ident_bf[0:n, 0:n], xd_bf[0:n, :, :].rearrange("p a b -> p (a b)"), ["ident_bf", "xd_bf"], [kOC],
               start=False, stop=False)
            for ci, (c0, c1) in enumerate(chunks):
                pK, kK = bank()
                for hd in (0, 2, 1, 3):
                    MM(pOC[c0:c1, hd * 64:hd * 64 + 64], qpT[:, hd, c0:c1], Sb_C[:, hd, :], ["qpT", "Sb_C"], [kOC],
                       start=False, stop=True)
                    MM(pK[:, hd * 64:hd * 64 + 64], kp_bf[c0:c1, hd, :], v_bf[c0:c1, hd, :], ["kp_bf", "v_bf"], [kK])
                TT("pool", tmpS[:, :, :], S_C[:], eGbc[:, 4:8, c1 - 1:c1].to_broadcast([128, 4, 64]), ALU.mult, ["S_C", "eGbc"], ["tmpS"])
                TT("dve", S_C[:], tmpS[:, :, :], v3(pK[:, 0:256], 4), ALU.add, ["tmpS", kK], ["S_C"])
                CP("pool", Sb_C[:], S_C[:], ["S_C"], ["Sb_C"])
            TT("dve", f3[0:n, 0:256], pOC[0:n, 0:256], gate[0:n, 512:768], ALU.mult, [kOC, "gate"], ["f3"])
            ACT(f4[0:n, 0:256], f3[0:n, 0:256], AF.Square, ["f3"], ["f4"])
            RED(st4[0:n, 4:6], v3(f4[0:n, 0:256], 2), ["f4"], ["st4"])
            rsqrt_act(st4[0:n, 8:10], st4[0:n, 4:6], 1.0 / 128, ["st4"], ["st4"])
            TT("dve", v3(f3[0:n, 0:256], 2), v3(f3[0:n, 0:256], 2), bc_l(st4[0:n, 8:10], 128), ALU.mult, ["f3", "st4"], ["f3"])
            TT("dve", y_bf[0:n, 512:768], f3[0:n, 0:256], prm[0:n, 512:768], ALU.mult, ["f3", "prm"], ["y_bf"])

            for half in range(2):
                pt, pk = bank()
                for kk in range(4):
                    kc = half * 4 + kk
                    MM(pt[:, kk * 128:kk * 128 + n], y_bf[0:n, kc * 128:(kc + 1) * 128], ident_bf[0:n, 0:n],
                       ["y_bf", "ident_bf"], [pk])
                CP("act" if half else "dve", yT[:, half * 4:half * 4 + 4, 0:n], v3(pt[:, :], 4)[:, :, 0:n], [pk], ["yT"])
            for cg in range(2):
                pt, pk = bank()
                for kc in range(8):
                    MM(pt[0:n, :], yT[:, kc, 0:n], wout[:, kc, cg * 512:(cg + 1) * 512], ["yT", "wout"], [pk],
                       start=(kc == 0), stop=(kc == 7))
                TT("dve", ht[:, cg * 512:(cg + 1) * 512], ht[:, cg * 512:(cg + 1) * 512], pt[0:n, :], ALU.add, [hk, pk], [hk])

            if last_tile:
                for nm, S_, dst in (("A", S_A, st_hg), ("B", S_B, st_gd), ("D", S_D, st_rt)):
                    for hh in range(2):
                        DMA(dst[l, hh:4:2, :, :].rearrange("a k v -> k a v"), S_[hh * 64:(hh + 1) * 64, :, :], ["S_" + nm], [])
                DMA(st_sd[l].rearrange("a k v -> k a v"), S_C[:, :, :], ["S_C"], [])
                for blk in range(6):
                    DMA(st_gc[l][:, blk * 128:(blk + 1) * 128].rearrange("w p -> p w"), cvst[:, blk, :], ["cvst"], [], slow=True)
                    DMA(st_sc[l][:, blk * 128:(blk + 1) * 128].rearrange("w p -> p w"), cvst[:, 6 + blk, :], ["cvst"], [], slow=True)
            if l < depth - 1:
                DMA(hscr[t * 128:t * 128 + n, :], ht, [hk], [f"hd{t}"])
            if l == depth - 1 and t > 0:
                ACT(hn_bf[0:n, :], ht, AF.Square, [hk], ["hn_bf", "st4"], accum=st4[0:n, 0:1])
                rsqrt_act(st4[0:n, 1:2], st4[0:n, 0:1], 1.0 / D, ["st4"], ["st4"])
                STT(ht, ht, st4[0:n, 1:2], finw[0:n, :], ALU.mult, ALU.mult, [hk, "st4", "lgt"], [hk])
                DMA(y_p[(t - 1) * 128:t * 128, :], ht, [hk], [])


        hs = sb("hs", [NS, D])
        DMA(hs[:, :], xs_d, [], ["hs"])

        def sample_fwd(l, last):
            n = NS
            DMA(rot[0][:], rot_d[NT], [], ["rot0"])
            rt = rot[0]; rtk = "rot0"

            def v3(ps_ap, a):
                return ps_ap.rearrange("p (a b) -> p a b", a=a)

            def bc_l(ap2d, m):
                return ap2d.unsqueeze(2).to_broadcast([ap2d.shape[0], ap2d.shape[1], m])

            ACT(hn_bf[0:n, :], hs[:, :], AF.Square, ["hs"], ["hn_bf", "st4"], accum=st4[0:n, 0:1])
            rsqrt_act(st4[0:n, 1:2], st4[0:n, 0:1], 1.0 / D, ["st4"], ["st4"])
            ACT(hn_bf[0:n, :], hs[:, :], AF.Copy, ["hs", "st4"], ["hn_bf"], scale=st4[0:n, 1:2])
            for half in range(2):
                pt, pk = bank()
                for kk in range(4):
                    kc = half * 4 + kk
                    MM(pt[:, kk * 128:kk * 128 + n], hn_bf[0:n, kc * 128:(kc + 1) * 128], ident_bf[0:n, 0:n],
                       ["hn_bf", "ident_bf"], [pk])
                CP("act", hnT[:, half * 4:half * 4 + 4, 0:n], v3(pt[:, :], 4)[:, :, 0:n], [pk], ["hnT"])

            def proj_tok(c0, c1, extra=None):
                pt, pk = bank()
                for kc in range(8):
                    MM(pt[0:n, 0:c1 - c0], hnT[:, kc, 0:n], win[:, kc, c0:c1], ["hnT", "win"], [pk],
                       start=(kc == 0), stop=(kc == 7 and extra is None))
                if extra is not None:
                    MM(pt[0:n, extra[0]:extra[0] + 4], C("ident", slice(0, n), 0, n), prm[0:n, extra[1]:extra[1] + 4],
                       ["cst", "prm"], [pk], start=False, stop=True)
                return pt, pk

            sel = C("sel").rearrange("p (a b) -> p a b", a=4)
            selT = C("selT").rearrange("p (a b) -> p a b", a=4)
            pvs = f4
            Sbuf = [tt, eGbc]; Skey = ["tt", "eGbc"]
            Tbuf = gate; Tkey = "gate"
            slot = [0]

            def select(fields):
                pv, pvk = bank()
                first = True
                for (c0, wd, fn) in fields:
                    for hd in range(4):
                        ap, key = fn(hd)
                        P.op("pe", (lambda o_, l_, r_, st_: (lambda e: e.matmul(o_, lhsT=l_, rhs=r_, start=st_, stop=False,
                                                                                 skip_group_check=True)))(
                            pv[0:64, c0:c0 + wd], sel[0:n, hd, :], ap, first), reads=["cst", key], writes=[pvk])
                        first = False
                wtot = max(c0 + wd for (c0, wd, _) in fields)
                CP("dve", pvs[0:64, 0:wtot], pv[0:64, 0:wtot], [pvk], ["f4"])

            def unselect(o_ap, okey):
                po, pok = bank()
                for hd in range(4):
                    MM(po[0:n, hd * 64:hd * 64 + 64], selT[0:64, hd, :], o_ap, ["cst", okey], [pok])
                return po, pok

            def state_io(st_in, st_out, K):
                ks = 16
                for k0 in range(0, K, ks):
                    yield k0, ks

            def load_slice(st_in, k0, ks):
                i = slot[0] % 2
                slot[0] += 1
                Sv = Sbuf[i][0:64, :, :].rearrange("p a b -> p (a b)")[:, 0:ks * 64].rearrange("p (k v) -> p k v", k=ks)
                for hd in range(4):
                    DMA(Sv[hd * 16:(hd + 1) * 16, :, :], st_in[l, :, hd, k0:k0 + ks, :], [], [Skey[i]])
                return Sv, Skey[i]

            def store_slice(st_out, Sv, sk, k0, ks):
                for hd in range(4):
                    DMA(st_out[l, :, hd, k0:k0 + ks, :], Sv[hd * 16:(hd + 1) * 16, :, :], [sk], [])

            o_sb = tmpS[0:64, 0, :]; w_sb = tmpS[0:64, 1, :]; op_sb = tmpS[0:64, 2, :]; u_sb = tmpS[0:64, 3, :]

            def Tview(ks):
                return Tbuf[0:64, 0:ks * 64].rearrange("p (k v) -> p k v", k=ks)

            def q_reduce(Sv, sk, q_ap, k0, ks, first, acc):
                T = Tview(ks)
                TT("dve", T, Sv, bc_l(q_ap[:, k0:k0 + ks], 64), ALU.mult, [sk, "f4"], [Tkey])
                RED(op_sb, T.rearrange("p k v -> p v k"), [Tkey], ["tmpS"])
                if first:
                    CP("dve", acc, op_sb, ["tmpS"], ["tmpS"])
                else:
                    TT("dve", acc, acc, op_sb, ALU.add, ["tmpS"], ["tmpS"])

            def step_plain(st_in, st_out, K, q_ap, k_ap, v_ap, vec_f=None, sc=None):
                for k0, ks in state_io(st_in, st_out, K):
                    Sv, sk = load_slice(st_in, k0, ks)
                    T = Tview(ks)
                    TT("dve", T, bc_l(k_ap[:, k0:k0 + ks], 64), v_ap.unsqueeze(1).to_broadcast([64, ks, 64]), ALU.mult,
                       ["f4", "tmpS"], [Tkey])
                    if vec_f is not None:
                        TT("dve", Sv, Sv, bc_l(vec_f[:, k0:k0 + ks], 64), ALU.mult, [sk, "f4"], [sk])
                        TT("dve", Sv, Sv, T, ALU.add, [sk, Tkey], [sk])
                    else:
                        STT(Sv, Sv, sc, T, ALU.mult, ALU.add, [sk, Tkey, "f4", "cst"], [sk])
                    store_slice(st_out, Sv, sk, k0, ks)
                    q_reduce(Sv, sk, q_ap, k0, ks, k0 == 0, o_sb)

            def head_norm(ps_ap, pskey, gcols, ycols):
                ACT(f3[0:n, 256:512], ps_ap, AF.Square, [pskey], ["f3"])
                RED(st4[0:n, 4:8], v3(f3[0:n, 256:512], 4), ["f3"], ["st4"])
                rsqrt_act(st4[0:n, 8:12], st4[0:n, 4:8], 1.0 / 64, ["st4"], ["st4"])
                TT("dve", v3(f3[0:n, 256:512], 4), v3(ps_ap, 4), bc_l(st4[0:n, 8:12], 64), ALU.mult, [pskey, "st4"], ["f3"])
                TT("dve", y_bf[0:n, ycols], f3[0:n, 256:512], hb[1][0:n, gcols], ALU.mult, ["f3", "hb1"], ["y_bf"])

            gs = hb[1]; gsk = "hb1"

            pA0, kA0 = proj_tok(0, 512)
            pA1, kA1 = proj_tok(512, 1024)
            ACT(f1[0:n, 0:256], pA0[0:n, 0:256], AF.Exp, [kA0], ["f1"], scale=-1.0)
            ACT(f1[0:n, 256:512], pA0[0:n, 256:512], AF.Exp, [kA0], ["f1"])
            ACT(f1[0:n, 512:768], pA1[0:n, 256:512], AF.Exp, [kA1], ["f1"], scale=-1.0)
            sigmoid_from_exp(f1[0:n, :], "f1")
            STT(f2[0:n, 0:256], pA0[0:n, 0:256], QK, f1[0:n, 0:256], ALU.mult, ALU.mult, [kA0, "f1"], ["f2"])
            TT("dve", f2[0:n, 256:512], f1[0:n, 256:512], oml[0:n, l, :], ALU.mult, ["f1", "oml"], ["f2"])
            TT("dve", f1[0:n, 512:768], f1[0:n, 512:768], prm[0:n, 0:256], ALU.mult, ["f1", "prm"], ["f1"])
            TT("dve", gs[0:n, 0:256], pA1[0:n, 256:512], f1[0:n, 512:768], ALU.mult, [kA1, "f1"], [gsk])
            CP("dve", f3[0:n, 0:256], pA1[0:n, 0:256], [kA1], ["f3"])
            TS("dve", f1[0:n, 0:256], f2[0:n, 256:512], -1.0, ALU.mult, ["f2"], ["f1"], s2=1.0, op1=ALU.add)
            select([(0, 64, lambda hd: (f2[0:n, hd * 64:hd * 64 + 64], "f2")),
                    (64, 64, lambda hd: (f2[0:n, 256 + hd * 64:256 + hd * 64 + 64], "f2")),
                    (128, 64, lambda hd: (f3[0:n, hd * 64:hd * 64 + 64], "f3")),
                    (192, 64, lambda hd: (f1[0:n, hd * 64:hd * 64 + 64], "f1"))])
            step_plain(si_hg, so_hg, 64, pvs[0:64, 0:64], pvs[0:64, 64:128], pvs[0:64, 128:192], vec_f=pvs[0:64, 192:256])
            po, pok = unselect(o_sb, "tmpS")
            head_norm(po[0:n, 0:256], pok, slice(0, 256), slice(0, 256))

            pD0, kD0 = proj_tok(3084, 3596)
            pD1, kD1 = proj_tok(3596, 4108)
            cosb = rt[0:n, 0:32].unsqueeze(1).to_broadcast([n, 16, 32])
            sinb = rt[0:n, 32:64].unsqueeze(1).to_broadcast([n, 16, 32])
            qk4 = pD0[0:n, :].rearrange("p (a b) -> p a b", a=16)
            TT("dve", f1[0:n, 0:512].rearrange("p (a b) -> p a b", a=16), qk4, cosb, ALU.mult, [kD0, rtk], ["f1"])
            TT("dve", f2[0:n, 0:512].rearrange("p (a b) -> p a b", a=16), qk4, sinb, ALU.mult, [kD0, rtk], ["f2"])
            c4 = f1[0:n, 0:512].rearrange("p (a s b) -> p a s b", a=8, s=2)
            s4 = f2[0:n, 0:512].rearrange("p (a s b) -> p a s b", a=8, s=2)
            r4 = f3[0:n, 0:512].rearrange("p (a s b) -> p a s b", a=8, s=2)
            TT("dve", r4[:, :, 0, :], c4[:, :, 0, :], s4[:, :, 1, :], ALU.subtract, ["f1", "f2"], ["f3"])
            TT("dve", r4[:, :, 1, :], c4[:, :, 1, :], s4[:, :, 0, :], ALU.add, ["f1", "f2"], ["f3"])
            TS("dve", f3[0:n, 256:512], f3[0:n, 256:512], QK, ALU.mult, ["f3"], ["f3"])
            CP("dve", f1[0:n, 0:256], pD1[0:n, 0:256], [kD1], ["f1"])
            ACT(f1[0:n, 512:768], pD1[0:n, 256:512], AF.Exp, [kD1], ["f1"], scale=-1.0)
            sigmoid_from_exp(f1[0:n, 512:768], "f1")
            TT("dve", gs[0:n, 768:1024], pD1[0:n, 256:512], f1[0:n, 512:768], ALU.mult, [kD1, "f1"], [gsk])
            select([(0, 64, lambda hd: (f3[0:n, hd * 64:hd * 64 + 64], "f3")),
                    (64, 64, lambda hd: (f3[0:n, 256 + hd * 64:256 + hd * 64 + 64], "f3")),
                    (128, 64, lambda hd: (f1[0:n, hd * 64:hd * 64 + 64], "f1"))])
            step_plain(si_rt, so_rt, 64, pvs[0:64, 0:64], pvs[0:64, 64:128], pvs[0:64, 128:192], sc=C("gam64", slice(0, 64), 0, 1))
            po, pok = unselect(o_sb, "tmpS")
            oD = po[0:n, 0:256]
            RED(st4[0:n, 4:8], v3(oD, 4), [pok], ["st4"])
            TS("dve", st4[0:n, 4:8], st4[0:n, 4:8], -1.0 / 64, ALU.mult, ["st4"], ["st4"])
            TT("dve", v3(f3[0:n, 0:256], 4), v3(oD, 4), bc_l(st4[0:n, 4:8], 64), ALU.add, [pok, "st4"], ["f3"])
            ACT(f3[0:n, 256:512], f3[0:n, 0:256], AF.Square, ["f3"], ["f3"])
            RED(st4[0:n, 4:8], v3(f3[0:n, 256:512], 4), ["f3"], ["st4"])
            rsqrt_act(st4[0:n, 8:12], st4[0:n, 4:8], 1.0 / 64, ["st4"], ["st4"])
            TT("dve", v3(f3[0:n, 0:256], 4), v3(f3[0:n, 0:256], 4), bc_l(st4[0:n, 8:12], 64), ALU.mult, ["f3", "st4"], ["f3"])
            TT("dve", f3[0:n, 0:256], f3[0:n, 0:256], prm[0:n, 768:1024], ALU.mult, ["f3", "prm"], ["f3"])
            TT("dve", f3[0:n, 0:256], f3[0:n, 0:256], prm[0:n, 1024:1280], ALU.add, ["f3", "prm"], ["f3"])
            TT("dve", y_bf[0:n, 768:1024], f3[0:n, 0:256], gs[0:n, 768:1024], ALU.mult, ["f3", gsk], ["y_bf"])

            pBz, kBz = proj_tok(1792, 2056, (256, 1288))
            ACT(f1[0:n, 512:768], pBz[0:n, 0:256], AF.Exp, [kBz], ["f1"], scale=-1.0)
            ACT(beta[0:n, 0:4], pBz[0:n, 260:264], AF.Exp, [kBz], ["beta"], scale=-1.0)
            ACT(g8[0:n, 0:4], pBz[0:n, 256:260], AF.Exp, [kBz], ["g8"])
            sigmoid_from_exp(f1[0:n, 512:768], "f1")
            TT("dve", f1[0:n, 512:768], f1[0:n, 512:768], prm[0:n, 256:512], ALU.mult, ["f1", "prm"], ["f1"])
            TT("dve", gs[0:n, 256:512], pBz[0:n, 0:256], f1[0:n, 512:768], ALU.mult, [kBz, "f1"], [gsk])
            pCz, kCz = proj_tok(2824, 3084, (256, 1292))
            ACT(f1[0:n, 512:768], pCz[0:n, 0:256], AF.Exp, [kCz], ["f1"], scale=-1.0)
            ACT(g8[0:n, 4:8], pCz[0:n, 256:260], AF.Exp, [kCz], ["g8"])
            sigmoid_from_exp(f1[0:n, 512:768], "f1")
            TT("dve", gs[0:n, 512:768], pCz[0:n, 0:256], f1[0:n, 512:768], ALU.mult, [kCz, "f1"], [gsk])
            ACT(g8[0:n, 0:8], g8[0:n, 0:8], AF.Ln, ["g8"], ["g8"], bias=eps_t[0:n, 1:2])
            CP("dve", dtb[0:n, :], g8[0:n, :], ["g8"], ["dtb"])
            TT("dve", g8[0:n, :], g8[0:n, :], nega[0:n, :], ALU.mult, ["g8", "nega"], ["g8"])
            ACT(egc[0:n, 0:8], g8[0:n, 0:8], AF.Exp, ["g8"], ["egc"])
            sigmoid_from_exp(beta[0:n, :], "beta")

            def conv_tok(cv, groups, st_in, st_out, bias):
                U = hb[0]; Uk = "hb0"
                for (c0, c1, o0) in groups:
                    pu, puk = proj_tok(c0, c1)
                    CP("dve", U[0:n, o0:o0 + (c1 - c0)], pu[0:n, 0:c1 - c0], [puk], [Uk])
                DMA(st_out[l, :, 2, :], U[0:n, 0:768], [Uk], [])
                Wt = eGbc[0:n, :, :].rearrange("p a b -> p (a b)")[:, 0:768]
                Ct = tt[0:n, :, :].rearrange("p a b -> p (a b)")[:, 0:768]
                DMA(Wt, cwrow_d[l, cv, 3], [], ["eGbc"])
                TT("dve", f1[0:n, 0:768], U[0:n, 0:768], Wt, ALU.mult, [Uk, "eGbc"], ["f1"])
                for w in range(3):
                    DMA(Ct, st_in[l, :, w, :], [], ["tt"])
                    DMA(Wt, cwrow_d[l, cv, w], [], ["eGbc"])
                    if w >= 1:
                        DMA(st_out[l, :, w - 1, :], Ct, ["tt"], [])
                    TT("dve", Wt, Ct, Wt, ALU.mult, ["tt", "eGbc"], ["eGbc"])
                    TT("dve", f1[0:n, 0:768], f1[0:n, 0:768], Wt, ALU.add, ["f1", "eGbc"], ["f1"])
                if bias:
                    DMA(Ct, cbrow_d[l], [], ["tt"])
                    TT("dve", f1[0:n, 0:768], f1[0:n, 0:768], Ct, ALU.add, ["f1", "tt"], ["f1"])
                ACT(Ct, f1[0:n, 0:768], AF.Exp, ["f1"], ["tt"], scale=-1.0)
                sigmoid_from_exp(Ct, "tt")
                TT("dve", f1[0:n, 0:768], f1[0:n, 0:768], Ct, ALU.mult, ["f1", "tt"], ["f1"])

            conv_tok(0, [(1024, 1536, 0), (1536, 1792, 512)], si_gc, so_gc, False)
            Ct = tt[0:n, :, :].rearrange("p a b -> p (a b)")[:, 0:512]
            ACT(Ct, f1[0:n, 0:512], AF.Square, ["f1"], ["tt"])
            RED(nb[0:n, 0:8], v3(Ct, 8), ["tt"], ["nb"])
            ACT(nb[0:n, 8:16], nb[0:n, 0:8], AF.Ln, ["nb"], ["nb"], bias=eps_t[0:n, 0:1])
            ACT(nb[0:n, 8:16], nb[0:n, 8:16], AF.Exp, ["nb"], ["nb"], scale=-0.5)
            TT("dve", v3(f1[0:n, 0:512], 8), v3(f1[0:n, 0:512], 8), bc_l(nb[0:n, 8:16], 64), ALU.mult, ["f1", "nb"], ["f1"])
            TS("dve", f1[0:n, 0:256], f1[0:n, 0:256], QK, ALU.mult, ["f1"], ["f1"])
            select([(0, 64, lambda hd: (f1[0:n, hd * 64:hd * 64 + 64], "f1")),
                    (64, 64, lambda hd: (f1[0:n, 256 + hd * 64:256 + hd * 64 + 64], "f1")),
                    (128, 64, lambda hd: (f1[0:n, 512 + hd * 64:512 + hd * 64 + 64], "f1")),
                    (192, 1, lambda hd: (egc[0:n, hd:hd + 1], "egc")),
                    (193, 1, lambda hd: (beta[0:n, hd:hd + 1], "beta"))])
            qB, kB, vB = pvs[0:64, 0:64], pvs[0:64, 64:128], pvs[0:64, 128:192]
            egB, btB = pvs[0:64, 192:193], pvs[0:64, 193:194]
            for k0, ks in state_io(si_gd, so_gd, 64):
                Sv, sk = load_slice(si_gd, k0, ks)
                T = Tview(ks)
                TT("dve", T, Sv, bc_l(kB[:, k0:k0 + ks], 64), ALU.mult, [sk, "f4"], [Tkey])
                RED(op_sb, T.rearrange("p k v -> p v k"), [Tkey], ["tmpS"])
                if k0 == 0:
                    CP("dve", w_sb, op_sb, ["tmpS"], ["tmpS"])
                else:
                    TT("dve", w_sb, w_sb, op_sb, ALU.add, ["tmpS"], ["tmpS"])
            TS("dve", w_sb, w_sb, egB, ALU.mult, ["tmpS", "f4"], ["tmpS"])
            TT("dve", u_sb, vB, w_sb, ALU.subtract, ["f4", "tmpS"], ["tmpS"])
            TS("dve", u_sb, u_sb, btB, ALU.mult, ["tmpS", "f4"], ["tmpS"])
            step_plain(si_gd, so_gd, 64, qB, kB, u_sb, sc=egB)
            po, pok = unselect(o_sb, "tmpS")
            head_norm(po[0:n, 0:256], pok, slice(256, 512), slice(256, 512))

            conv_tok(1, [(2056, 2568, 0), (2568, 2824, 512)], si_sc, so_sc, True)
            TT("dve", v3(f2[0:n, 0:256], 4), v3(f1[0:n, 0:256], 4), bc_l(dtb[0:n, 4:8], 64), ALU.mult, ["f1", "dtb"], ["f2"])
            select([(0, 128, lambda hd: (f1[0:n, 512 + (hd // 2) * 128:512 + (hd // 2) * 128 + 128], "f1")),
                    (128, 128, lambda hd: (f1[0:n, 256 + (hd // 2) * 128:256 + (hd // 2) * 128 + 128], "f1")),
                    (256, 64, lambda hd: (f2[0:n, hd * 64:hd * 64 + 64], "f2")),
                    (320, 1, lambda hd: (egc[0:n, 4 + hd:5 + hd], "egc"))])
            step_plain(si_sd, so_sd, 128, pvs[0:64, 0:128], pvs[0:64, 128:256], pvs[0:64, 256:320], sc=pvs[0:64, 320:321])
            po, pok = unselect(o_sb, "tmpS")
            TT("dve", v3(f3[0:n, 0:256], 4), v3(f1[0:n, 0:256], 4), bc_l(prm[0:n, 1296:1300], 64), ALU.mult, ["f1", "prm"], ["f3"])
            TT("dve", f3[0:n, 0:256], f3[0:n, 0:256], po[0:n, 0:256], ALU.add, ["f3", pok], ["f3"])
            TT("dve", f3[0:n, 0:256], f3[0:n, 0:256], gs[0:n, 512:768], ALU.mult, ["f3", gsk], ["f3"])
            ACT(f3[0:n, 256:512], f3[0:n, 0:256], AF.Square, ["f3"], ["f3"])
            RED(st4[0:n, 4:6], v3(f3[0:n, 256:512], 2), ["f3"], ["st4"])
            rsqrt_act(st4[0:n, 8:10], st4[0:n, 4:6], 1.0 / 128, ["st4"], ["st4"])
            TT("dve", v3(f3[0:n, 0:256], 2), v3(f3[0:n, 0:256], 2), bc_l(st4[0:n, 8:10], 128), ALU.mult, ["f3", "st4"], ["f3"])
            TT("dve", y_bf[0:n, 512:768], f3[0:n, 0:256], prm[0:n, 512:768], ALU.mult, ["f3", "prm"], ["y_bf"])

            for half in range(2):
                pt, pk = bank()
                for kk in range(4):
                    kc = half * 4 + kk
                    MM(pt[:, kk * 128:kk * 128 + n], y_bf[0:n, kc * 128:(kc + 1) * 128], ident_bf[0:n, 0:n],
                       ["y_bf", "ident_bf"], [pk])
                CP("act", yT[:, half * 4:half * 4 + 4, 0:n], v3(pt[:, :], 4)[:, :, 0:n], [pk], ["yT"])
            for cg in range(2):
                pt, pk = bank()
                for kc in range(8):
                    MM(pt[0:n, :], yT[:, kc, 0:n], wout[:, kc, cg * 512:(cg + 1) * 512], ["yT", "wout"], [pk],
                       start=(kc == 0), stop=(kc == 7))
                TT("dve", hs[:, cg * 512:(cg + 1) * 512], hs[:, cg * 512:(cg + 1) * 512], pt[0:n, :], ALU.add, ["hs", pk], ["hs"])
            if last:
                ACT(hn_bf[0:n, :], hs[:, :], AF.Square, ["hs"], ["hn_bf", "st4"], accum=st4[0:n, 0:1])
                rsqrt_act(st4[0:n, 1:2], st4[0:n, 0:1], 1.0 / D, ["st4"], ["st4"])
                STT(hs[:, :], hs[:, :], st4[0:n, 1:2], finw[0:n, :], ALU.mult, ALU.mult, ["hs", "st4", "lgt"], ["hs"])
                DMA(y_s, hs[:, :], ["hs"], [])

        for l in range(depth):
            load_layer(l)
            if not _os0.environ.get("NO_SAMPLE"):
                sample_fwd(l, l == depth - 1)
            for t in range(ntiles):
                tile_fwd(l, t)

        if _AUDIT:
            for b_ in sorted(_bad, key=str):
                print("AUDIT missing key:", b_)
        P.emit(es)
    return nc


_NC_CACHE = {}


def _prep_inputs(inp, c):
    f = np.float32
    prm = np.zeros((DEPTH, 128, NPRM), f)
    for l in range(DEPTH):
        row = np.concatenate([inp["hgrn_norm_w"][l], inp["gdn_norm_w"][l], inp["ssd_norm_w"][l], inp["ret_norm_w"][l],
                              inp["ret_norm_b"][l], inp["gdn_a_log"][l], inp["ssd_a_log"][l], inp["gdn_dt_bias"][l],
                              inp["ssd_dt_bias"][l], inp["ssd_d"][l]]).astype(f)
        prm[l] = np.broadcast_to(row[None, :], (128, NPRM))
    lgt = np.ascontiguousarray(np.broadcast_to(inp["hgrn_lb_logits"].reshape(1, 1024), (128, 1024))).astype(f)
    finw = np.ascontiguousarray(np.broadcast_to(inp["final_norm_w"].reshape(1, 1024), (128, 1024))).astype(f)
    featp = np.zeros((128, 32 + 192), f)
    featp[:, 0:32] = inp["norm_w"].reshape(DEPTH, 8, 128).transpose(2, 0, 1).reshape(128, 32)
    for l in range(DEPTH):
        for cv, key in enumerate(("gdn_conv_w", "ssd_conv_w")):
            w = inp[key][l].reshape(4, 6, 128)
            featp[:, 32 + l * 48 + cv * 24:32 + l * 48 + cv * 24 + 24] = w.transpose(2, 1, 0).reshape(128, 24)
    cbias = np.ascontiguousarray(inp["ssd_conv_b"].reshape(1, DEPTH * 768)).astype(f)
    cw = np.stack([inp["gdn_conv_w"], inp["ssd_conv_w"]], 1).astype(f)
    cwrow = np.ascontiguousarray(np.broadcast_to(cw[:, :, :, None, :], (DEPTH, 2, 4, NS, 768)))
    cbrow = np.ascontiguousarray(np.broadcast_to(inp["ssd_conv_b"].astype(f)[:, None, :], (DEPTH, NS, 768)))
    return {
        "xp": np.ascontiguousarray(inp["x_prompt"][c]).astype(f),
        "meta": np.ascontiguousarray(inp["meta_tokens"]).astype(f),
        "w_in": np.ascontiguousarray(inp["w_in"]).astype(f),
        "w_out": np.ascontiguousarray(inp["w_out"]).astype(f),
        "cst": CST, "rot": ROT, "prm": prm, "lgt": lgt, "finw": finw, "featp": featp, "cbias": cbias,
        "cwrow": cwrow, "cbrow": cbrow, **_sample_inputs(inp, c),
    }


def _sample_inputs(inp, c):
    f = np.float32
    sl = slice(c * NS, (c + 1) * NS)
    return {
        "xs_in": np.ascontiguousarray(inp["x_sample"][sl, 0, :]).astype(f),
        "si_hg": np.ascontiguousarray(inp["state_hgrn"][:, sl]).astype(f),
        "si_gd": np.ascontiguousarray(inp["state_gdn"][:, sl]).astype(f),
        "si_gc": np.ascontiguousarray(inp["state_gdn_conv"][:, sl]).astype(f),
        "si_sd": np.ascontiguousarray(inp["state_ssd"][:, sl]).astype(f),
        "si_sc": np.ascontiguousarray(inp["state_ssd_conv"][:, sl]).astype(f),
        "si_rt": np.ascontiguousarray(inp["state_ret"][:, sl]).astype(f),
    }


def kernel(**inp):
    inp = {k: np.asarray(v) for k, v in inp.items()}
    if "nc" not in _NC_CACHE:
        _NC_CACHE["nc"] = build_program()
    nc = _NC_CACHE["nc"]
    shared = None
    in_maps = []
    for c in range(8):
        m = _prep_inputs(inp, c) if shared is None else dict(shared)
        if shared is None:
            shared = m
        else:
            m["xp"] = np.ascontiguousarray(inp["x_prompt"][c]).astype(np.float32)
            m.update(_sample_inputs(inp, c))
        in_maps.append(m)
    res = run_bass_kernel_spmd(nc, in_maps, core_ids=list(range(8)))
    R = res.results
    y_prompt = np.stack([R[c]["y_p"] for c in range(8)], 0)
    def stk(name):
        return np.ascontiguousarray(np.stack([R[c][name] for c in range(8)], 1))
    y_sample = np.concatenate([R[c]["y_s"] for c in range(8)], 0)[:, None, :]
    def cat(name):
        return np.ascontiguousarray(np.concatenate([R[c][name] for c in range(8)], 1))
    outs = (y_prompt, np.ascontiguousarray(y_sample),
            stk("st_hg"), stk("st_gd"), stk("st_gc"), stk("st_sd"), stk("st_sc"), stk("st_rt"),
            cat("so_hg"), cat("so_gd"), cat("so_gc"), cat("so_sd"), cat("so_sc"), cat("so_rt"))
    return outs
```

```python
import contextlib
import math
import numpy as np
import concourse.bass as bass
import concourse.mybir as mybir
from concourse.bass_utils import run_bass_kernel_spmd

F32 = mybir.dt.float32
BF16 = mybir.dt.bfloat16
AF = mybir.ActivationFunctionType
ALU = mybir.AluOpType
AX = mybir.AxisListType

D = 1024
DEPTH = 4
SEQ = 2048
NT = 17
IN_DIM = 4108
EPS = 1e-6
QK = 0.125
NS = 16
NPRM = 1300
NEGV = -30000.0
import os as _os0
EMBED_WAIT = not _os0.environ.get("NO_EMBED")
ANNOTATE = bool(_os0.environ.get("ANNOTATE"))


class Prog:
    ENG = ("pe", "act", "dve", "pool", "sp")

    def __init__(self, nc, n_dma_sems=8):
        self.nc = nc
        self.ops = []
        self.cnt = {}
        self.clock = {e: {} for e in self.ENG}
        self.tok_clock = {}
        self.last_w = {}
        self.readers = {}
        self.n_dma = n_dma_sems
        self.dma_rr = {e: 0 for e in self.ENG}
        self.dma_last = {}
        self.anns = []
        import os
        self.pe_skip = not os.environ.get("PE_SELFWAIT")
        self.strict_same = not os.environ.get("RELAX_SAME")

    def _need(self, eng, tok, waits, force=False):
        key, idx = tok
        if key == "pe" and eng == "pe" and self.pe_skip and not force:
            return
        if self.clock[eng].get(key, 0) >= idx:
            return
        if waits.get(key, 0) < idx:
            waits[key] = idx

    def op(self, eng, fn, reads=(), writes=(), dma=False, pe_serial=False):
        waits = {}
        for b in reads:
            t = self.last_w.get(b)
            if t:
                self._need(eng, t, waits)
            if b.startswith("ps"):
                for r in self.readers.get(b, ()):
                    if r[0] != eng:
                        self._need(eng, r, waits)
        for b in writes:
            t = self.last_w.get(b)
            if t and (t[0] != eng or pe_serial or dma or self.strict_same):
                self._need(eng, t, waits, force=pe_serial)
            for r in self.readers.get(b, ()):
                if r[0] != eng or dma or self.strict_same:
                    self._need(eng, r, waits)
        if dma:
            key = ("dma", eng, self.dma_rr[eng] % self.n_dma)
            self.dma_rr[eng] += 1
            prev = self.dma_last.get(key)
            if prev:
                self._need(eng, prev, waits)
        else:
            key = eng
        ck = self.clock[eng]
        for kk, ii in waits.items():
            for k2, i2 in self.tok_clock.get((kk, ii), {}).items():
                if ck.get(k2, 0) < i2:
                    ck[k2] = i2
            if ck.get(kk, 0) < ii:
                ck[kk] = ii
        self.cnt[key] = self.cnt.get(key, 0) + 1
        tok = (key, self.cnt[key])
        if dma:
            self.dma_last[key] = tok
            snap = dict(ck)
            snap[key] = tok[1]
            self.tok_clock[tok] = snap
        else:
            snap = dict(ck)
            snap[key] = tok[1]
            self.tok_clock[tok] = snap
        for b in writes:
            self.last_w[b] = tok
            self.readers[b] = []
        for b in reads:
            if b not in writes:
                self.readers.setdefault(b, []).append(tok)
        ann = None
        if ANNOTATE:
            import sys as _sys
            f = _sys._getframe(1)
            while f:
                if f.f_code.co_name in ("tile_fwd", "sample_fwd", "load_layer"):
                    ann = "L%d" % f.f_lineno
                    break
                f = f.f_back
        self.anns.append(ann)
        self.ops.append((eng, fn, list(waits.items()), tok, dma))
        return tok

    def emit(self, es, final_wait_eng="sp"):
        nc = self.nc
        import os
        km = int(os.environ.get("KMAX", "0"))
        if km:
            self.ops = self.ops[:km]
            self.dma_last = {}
            for (e_, f_, w_, tok_, d_) in self.ops:
                if d_:
                    self.dma_last[tok_[0]] = tok_
        needed = set()
        for (_, _, waits, _, _) in self.ops:
            for w in waits:
                needed.add(w)
        finals = []
        for k, t in self.dma_last.items():
            finals.append(t)
            needed.add(t)
        per_key = {}
        for (k, i) in needed:
            per_key.setdefault(k, []).append(i)
        sigcount = {}
        for k, lst in per_key.items():
            for n, i in enumerate(sorted(lst)):
                sigcount[(k, i)] = n + 1
        sems = {}
        for k in sorted(per_key.keys(), key=str):
            nm = "s_" + "_".join(str(x) for x in (k if isinstance(k, tuple) else (k,)))
            sems[k] = es.enter_context(nc.semaphore(nm))
        per_eng = {e: [] for e in self.ENG}
        for j, o in enumerate(self.ops):
            per_eng[o[0]].append(o + (self.anns[j] if j < len(self.anns) else None,))
        blk = es.enter_context(nc.Block())

        def run(e, engobj):
            for (_, fn, waits, tok, dma, ann) in per_eng[e]:
                emb = None
                if waits and EMBED_WAIT and not dma:
                    emb = waits[-1]
                    waits = waits[:-1]
                for (k, i) in waits:
                    mult = 16 if isinstance(k, tuple) else 1
                    engobj.wait_ge(sems[k], sigcount[(k, i)] * mult)
                ins = fn(engobj)
                if ann is not None:
                    ins.annotate(ann)
                if emb is not None:
                    k, i = emb
                    ins._wait_ge(sems[k], sigcount[(k, i)] * (16 if isinstance(k, tuple) else 1))
                if tok in sigcount:
                    ins.then_inc(sems[tok[0]], 16 if dma else 1)
            if e == final_wait_eng:
                for t in finals:
                    engobj.wait_ge(sems[t[0]], sigcount[t] * 16)

        @blk.tensor
        def _(e):
            run("pe", e)

        @blk.scalar
        def _(e):
            run("act", e)

        @blk.vector
        def _(e):
            run("dve", e)

        @blk.gpsimd
        def _(e):
            run("pool", e)

        @blk.sync
        def _(e):
            run("sp", e)


def host_consts():
    idx = np.arange(128)
    ch = idx // 64
    same = ch[:, None] == ch[None, :]
    ident = np.eye(128, dtype=np.float32)
    maskT = (same & (idx[:, None] <= idx[None, :])).astype(np.float32)
    negT = np.where(maskT > 0, 0.0, NEGV).astype(np.float32)
    strict = (same & (idx[None, :] < idx[:, None]))
    negS = np.where(strict, 0.0, NEGV).astype(np.float32)
    mid = ch * 64 + 31
    uprime = (same & (idx[:, None] <= idx[None, :])).astype(np.float32) - \
             (same & (idx[:, None] <= mid[None, :])).astype(np.float32)
    urev = (same & (idx[:, None] > idx[None, :])).astype(np.float32)
    wc = np.zeros((128, 8), np.float32)
    wc[:, 0] = (idx <= 31)
    wc[:, 1] = (idx >= 64) & (idx <= 95)
    wc[:, 2] = (idx >= 32) & (idx <= 63)
    wc[:, 3] = (idx >= 96)
    wc[:, 4] = (idx <= 63)
    wc[:, 5] = (idx >= 64)
    blockones = same.astype(np.float32)
    lg = np.log1p(-np.exp2(-5.0 - np.arange(4, dtype=np.float64)))
    loc = idx % 64
    dt_ret = np.zeros((128, 4, 128), np.float64)
    for h in range(4):
        dt_ret[:, h, :] = np.where(maskT > 0, np.exp(lg[h] * (idx[None, :] - idx[:, None])), 0.0) * QK
    egq = np.zeros((128, 2, 128), np.float64)
    for hp in range(2):
        for hh in range(2):
            egq[hh * 64:(hh + 1) * 64, hp, :] = np.exp(lg[2 * hp + hh] * (loc[None, :] + 1))
    egrev64 = np.zeros((128, 4), np.float64)
    egrev16 = np.zeros((128, 4), np.float64)
    for h in range(4):
        egrev64[:, h] = np.exp(lg[h] * (63 - loc)) * QK
        egrev16[:, h] = np.exp(lg[h] * np.maximum(15 - idx, 0)) * QK
    egl = np.zeros((128, 2, 2), np.float64)
    for hp in range(2):
        for hh in range(2):
            egl[hh * 64:(hh + 1) * 64, hp, 0] = np.exp(lg[2 * hp + hh] * 16)
            egl[hh * 64:(hh + 1) * 64, hp, 1] = np.exp(lg[2 * hp + hh] * 64)
    sel = np.zeros((128, 4, 64), np.float32)
    selT = np.zeros((128, 4, 16), np.float32)
    gam64 = np.zeros((128, 4), np.float32)
    for h in range(4):
        for b in range(16):
            sel[b, h, h * 16 + b] = 1.0
            selT[h * 16 + b, h, b] = 1.0
            gam64[h * 16 + b, 0] = np.exp(lg[h])
    parts = [ident, maskT, negT, negS, uprime, urev, wc, blockones,
             dt_ret.reshape(128, 512), egq.reshape(128, 256), egrev64, egrev16, egl.reshape(128, 4),
             sel.reshape(128, 256), selT.reshape(128, 64), gam64]
    offs = {}
    names = ["ident", "maskT", "negT", "negS", "uprime", "urev", "wc", "blockones",
             "dt_ret", "egq", "egrev64", "egrev16", "egl", "sel", "selT", "gam64"]
    o = 0
    for nm, p in zip(names, parts):
        offs[nm] = (o, p.shape[1])
        o += p.shape[1]
    cst = np.concatenate([p.astype(np.float32) for p in parts], axis=1)
    half = 32
    inv_freq = (1.0 / (np.float32(10000.0) ** np.linspace(0.0, 1.0, half, dtype=np.float32))).astype(np.float32)
    rot = np.zeros((NT + 1, 128, 64), np.float32)
    for t in range(NT):
        pos = (np.arange(128) if t == 0 else 16 + (t - 1) * 128 + np.arange(128)).astype(np.float32)
        ang = (pos[:, None] * inv_freq[None, :]).astype(np.float32)
        rot[t, :, 0:32] = np.cos(ang)
        rot[t, :, 32:64] = np.sin(ang)
    ang = (np.full((128, 1), 16384.0, np.float32) * inv_freq[None, :]).astype(np.float32)
    rot[NT, :, 0:32] = np.cos(ang)
    rot[NT, :, 32:64] = np.sin(ang)
    return cst, offs, rot


CST, COFF, ROT = host_consts()
NCST = CST.shape[1]


def build_program(depth=DEPTH, ntiles=NT):
    nc = bass.Bass("TRN2", target_bir_lowering=False)

    def din(name, shape):
        return nc.dram_tensor(name, list(shape), F32, kind="ExternalInput").ap()

    def dout(name, shape):
        return nc.dram_tensor(name, list(shape), F32, kind="ExternalOutput").ap()

    xp = din("xp", [SEQ, D])
    meta = din("meta", [16, D])
    w_in = din("w_in", [DEPTH, D, IN_DIM])
    w_out = din("w_out", [DEPTH, D, D])
    cst_d = din("cst", [128, NCST])
    rot_d = din("rot", [NT + 1, 128, 64])
    prm_d = din("prm", [DEPTH, 128, NPRM])
    lgt_d = din("lgt", [128, 1024])
    finw_d = din("finw", [128, 1024])
    featp_d = din("featp", [128, 32 + 192])
    cbias_d = din("cbias", [1, DEPTH * 768])

    y_p = dout("y_p", [SEQ, D])
    st_hg = dout("st_hg", [DEPTH, 4, 64, 64])
    st_gd = dout("st_gd", [DEPTH, 4, 64, 64])
    st_gc = dout("st_gc", [DEPTH, 3, 768])
    st_sd = dout("st_sd", [DEPTH, 4, 128, 64])
    st_sc = dout("st_sc", [DEPTH, 3, 768])
    st_rt = dout("st_rt", [DEPTH, 4, 64, 64])
    xs_d = din("xs_in", [NS, D])
    si_hg = din("si_hg", [DEPTH, NS, 4, 64, 64]); si_gd = din("si_gd", [DEPTH, NS, 4, 64, 64])
    si_gc = din("si_gc", [DEPTH, NS, 3, 768]); si_sd = din("si_sd", [DEPTH, NS, 4, 128, 64])
    si_sc = din("si_sc", [DEPTH, NS, 3, 768]); si_rt = din("si_rt", [DEPTH, NS, 4, 64, 64])
    cwrow_d = din("cwrow", [DEPTH, 2, 4, NS, 768])
    cbrow_d = din("cbrow", [DEPTH, NS, 768])
    y_s = dout("y_s", [NS, D])
    so_hg = dout("so_hg", [DEPTH, NS, 4, 64, 64]); so_gd = dout("so_gd", [DEPTH, NS, 4, 64, 64])
    so_gc = dout("so_gc", [DEPTH, NS, 3, 768]); so_sd = dout("so_sd", [DEPTH, NS, 4, 128, 64])
    so_sc = dout("so_sc", [DEPTH, NS, 3, 768]); so_rt = dout("so_rt", [DEPTH, NS, 4, 64, 64])

    with contextlib.ExitStack() as es:
        def sb(name, shape, dt=F32):
            return es.enter_context(nc.sbuf_tensor("sb_" + name, list(shape), dt))

        P = Prog(nc)

        hscr = nc.dram_tensor("hscr", [NT * 128, D], F32, kind="Internal").ap()
        hb = [sb(f"hb{i}", [128, D]) for i in range(2)]
        win = sb("win", [128, 8, IN_DIM], BF16)
        wout = sb("wout", [128, 8, D], BF16)
        WCH = 1027
        wst = [sb(f"wst{i}", [128, WCH]) for i in range(2)]
        cst = sb("cst", [128, NCST])
        ident_bf = sb("ident_bf", [128, 128], BF16)
        bones_bf = sb("bones_bf", [128, 128], BF16)
        dtret_bf = sb("dtret_bf", [128, 4, 128], BF16)
        ones_bf = sb("ones_bf", [1, 128], BF16)
        prm = sb("prm", [128, NPRM])
        oml = sb("oml", [128, 4, 256])
        lgt = sb("lgt", [128, 4, 256])
        featp = sb("featp", [128, 32 + 192])
        dg = sb("dg", [128, 12, 4, 128], BF16)
        cbias_bf = sb("cbias_bf", [1, 768], BF16)
        nega = sb("nega", [128, 64])
        rot = [sb(f"rot{i}", [128, 64]) for i in range(2)]

        def C(name, rows=slice(0, 128), lo=0, hi=None):
            o, w = COFF[name]
            hi = w if hi is None else hi
            return cst[rows, o + lo:o + hi]

        psb = [es.enter_context(nc.psum_tensor(f"ps{i}", [128, 512], F32)) for i in range(8)]
        ps_rr = [0]

        def bank():
            i = ps_rr[0] % 4 if ps_rr[0] < 0 else (0, 1, 2, 3, 6, 7)[ps_rr[0] % 6]
            ps_rr[0] += 1
            return psb[i], f"ps{i}"

        import os as _os
        _AUDIT = bool(_os.environ.get("AUDIT"))
        _bad = set()

        def _chk(r, w, outs, ins):
            if not _AUDIT:
                return
            for grp, keys, what in ((outs, list(w), "W"), (ins, list(r) + list(w), "R")):
                for ap in grp:
                    nm = getattr(ap, "name", None)
                    if not isinstance(nm, str):
                        continue
                    key = nm[3:] if nm.startswith("sb_") else nm
                    if key not in keys:
                        import traceback
                        fr = traceback.extract_stack()[-3]
                        _bad.add((what, key, fr.lineno))

        _last_rb = {}

        def MM(out, lhsT, rhs, r, w, start=True, stop=True):
            skip = any(k in ("ps4", "ps5") for k in w)
            _chk(r, w, [out], [lhsT, rhs])
            rb = lhsT.base_partition()
            ser = False
            for k in w:
                if _last_rb.get(k, rb) != rb:
                    ser = True
                _last_rb[k] = rb
            P.op("pe", lambda e: e.matmul(out, lhsT=lhsT, rhs=rhs, start=start, stop=stop, skip_group_check=skip),
                 reads=r, writes=w, pe_serial=ser)

        def ACT(out, in_, func, r, w, scale=1.0, bias=None, accum=None):
            kw = {}
            if bias is not None:
                kw["bias"] = bias
            if accum is not None:
                kw["accum_out"] = accum
            if hasattr(bias, "name") and "eps_t" not in r:
                r = list(r) + ["eps_t"]
            _chk(r, w, [out] + ([accum] if accum is not None else []), [in_] + [x for x in (scale, bias) if hasattr(x, "name")])
            P.op("act", lambda e: e.activation(out=out, in_=in_, func=func, scale=scale, **kw), reads=r, writes=w)

        def TT(eng, out, in0, in1, op, r, w):
            _chk(r, w, [out], [in0, in1])
            P.op(eng, lambda e: e.tensor_tensor(out=out, in0=in0, in1=in1, op=op), reads=r, writes=w)

        def TS(eng, out, in0, s1, op0, r, w, s2=None, op1=None):
            _chk(r, w, [out], [in0] + [x for x in (s1, s2) if hasattr(x, "name")])
            if op1 is None:
                P.op(eng, lambda e: e.tensor_scalar(out=out, in0=in0, scalar1=s1, scalar2=None, op0=op0), reads=r, writes=w)
            else:
                P.op(eng, lambda e: e.tensor_scalar(out=out, in0=in0, scalar1=s1, scalar2=s2, op0=op0, op1=op1), reads=r, writes=w)

        def STT(out, in0, scalar, in1, op0, op1, r, w):
            _chk(r, w, [out], [in0, in1] + [x for x in (scalar,) if hasattr(x, "name")])
            P.op("dve", lambda e: e.scalar_tensor_tensor(out=out, in0=in0, scalar=scalar, in1=in1, op0=op0, op1=op1),
                 reads=r, writes=w)

        def RED(out, in_, r, w):
            _chk(r, w, [out], [in_])
            P.op("dve", lambda e: e.tensor_reduce(out=out, in_=in_, axis=AX.X, op=ALU.add), reads=r, writes=w)

        def RECIP(out, in_, r, w):
            _chk(r, w, [out], [in_])
            P.op("dve", lambda e: e.reciprocal(out=out, in_=in_), reads=r, writes=w)

        def CP(eng, out, in_, r, w):
            if eng == "act":
                ACT(out, in_, AF.Copy, r, w)
            else:
                _chk(r, w, [out], [in_])
                P.op(eng, lambda e: e.tensor_copy(out=out, in_=in_), reads=r, writes=w)

        def MEMSET(eng, ap, val, w):
            P.op(eng, lambda e: e.memset(ap, val), reads=(), writes=w)

        def DMA(out, in_, r, w, slow=False):
            if slow:
                P.op("sp", lambda e: e.dma_start(out=out, in_=in_, allow_slow_non_contiguous=True), reads=r, writes=w, dma=True)
            else:
                P.op("sp", lambda e: e.dma_start(out=out, in_=in_), reads=r, writes=w, dma=True)

        def sigmoid_from_exp(buf, key):
            TS("dve", buf, buf, 1.0, ALU.add, [key], [key])
            RECIP(buf, buf, [key], [key])

        def rsqrt_act(out, in_, scale, r, w):
            ACT(out, in_, AF.Ln, r, w, scale=scale, bias=eps_t[0:out.shape[0], 0:1])
            ACT(out, out, AF.Exp, w, w, scale=-0.5)

        eps_t = sb("eps_t", [128, 2])
        MEMSET("pool", eps_t[:, 0:1], EPS, ["eps_t"])
        MEMSET("pool", eps_t[:, 1:2], 1.0, ["eps_t"])
        DMA(cst[:], cst_d, [], ["cst"])
        DMA(lgt[:].rearrange("p a b -> p (a b)"), lgt_d, [], ["lgt"])
        DMA(featp[:], featp_d, [], ["featp"])
        CP("dve", ident_bf[:], C("ident"), ["cst"], ["ident_bf"])
        CP("dve", bones_bf[:], C("blockones"), ["cst"], ["bones_bf"])
        CP("dve", dtret_bf[:].rearrange("p a b -> p (a b)"), C("dt_ret"), ["cst"], ["dtret_bf"])
        MEMSET("pool", ones_bf[:], 1.0, ["ones_bf"])
        mx = wst[1][:, 0:256]
        TT("dve", mx, lgt[:, 0, :], lgt[:, 1, :], ALU.max, ["lgt"], ["wst1"])
        TT("dve", mx, mx, lgt[:, 2, :], ALU.max, ["lgt", "wst1"], ["wst1"])
        TT("dve", mx, mx, lgt[:, 3, :], ALU.max, ["lgt", "wst1"], ["wst1"])
        TT("dve", lgt[:], lgt[:], mx.unsqueeze(1).to_broadcast([128, 4, 256]), ALU.subtract, ["lgt", "wst1"], ["lgt"])
        ACT(lgt[:], lgt[:], AF.Exp, ["lgt"], ["lgt"])
        TT("dve", mx, lgt[:, 0, :], lgt[:, 1, :], ALU.add, ["lgt"], ["wst1"])
        TT("dve", mx, mx, lgt[:, 2, :], ALU.add, ["lgt", "wst1"], ["wst1"])
        TT("dve", mx, mx, lgt[:, 3, :], ALU.add, ["lgt", "wst1"], ["wst1"])
        RECIP(mx, mx, ["wst1"], ["wst1"])
        TT("dve", lgt[:], lgt[:], mx.unsqueeze(1).to_broadcast([128, 4, 256]), ALU.mult, ["lgt", "wst1"], ["lgt"])
        MEMSET("dve", oml[:, 0, :], 0.0, ["oml"])
        CP("dve", oml[:, 1, :], lgt[:, 1, :], ["lgt"], ["oml"])
        TT("dve", oml[:, 2, :], oml[:, 1, :], lgt[:, 2, :], ALU.add, ["lgt", "oml"], ["oml"])
        TT("dve", oml[:, 3, :], oml[:, 2, :], lgt[:, 3, :], ALU.add, ["lgt", "oml"], ["oml"])
        TS("dve", oml[:], oml[:], 0.0, ALU.max, ["oml"], ["oml"])
        TS("dve", oml[:], oml[:], -1.0, ALU.mult, ["oml"], ["oml"], s2=1.0, op1=ALU.add)
        DMA(lgt[:].rearrange("p a b -> p (a b)"), finw_d, ["lgt"], ["lgt"])
        finw = lgt[:].rearrange("p a b -> p (a b)")

        hn_bf = sb("hn_bf", [128, D], BF16)
        hnTs = [sb(f"hnT{i}", [128, 8, 128], BF16) for i in range(2)]
        hnT = hnTs[1]
        st0 = sb("st0", [128, 2])
        st4 = sb("st4", [128, 16])
        f1 = sb("f1", [128, 768])
        f2 = sb("f2", [128, 512])
        f3 = sb("f3", [128, 512])
        f4 = sb("f4", [128, 512])
        gate = sb("gate", [128, D])
        y_bf = sb("y_bf", [128, D], BF16)
        yT = sb("yT", [128, 8, 128], BF16)
        b1 = sb("b1", [128, 4, 128], BF16)
        qkT = sb("qkT", [128, 4, 128], BF16)
        AT = sb("AT", [128, 4, 128], BF16)
        v_bf = sb("v_bf", [128, 4, 64], BF16)
        kp_bf = sb("kp_bf", [128, 4, 128], BF16)
        qpT = sb("qpT", [128, 4, 128], BF16)
        ecs = sb("ecs", [128, 2, 8])
        S_A = sb("S_A", [128, 2, 64]); Sb_A = sb("Sb_A", [128, 2, 64], BF16); Sd_A = sb("Sd_A", [128, 2, 64])
        S_B = sb("S_B", [128, 2, 64]); Sb_B = sb("Sb_B", [128, 2, 64], BF16)
        S_C = sb("S_C", [128, 4, 64]); Sb_C = sb("Sb_C", [128, 4, 64], BF16)
        S_D = sb("S_D", [128, 2, 64]); Sb_D = sb("Sb_D", [128, 2, 64], BF16)
        tmpS = sb("tmpS", [128, 4, 64])
        uT = [sb(f"uT{i}", [128, 12, 131], BF16) for i in range(2)]
        cvst = sb("cvst", [128, 12, 3])
        xs = sb("xs", [128, 12, 128], BF16)
        xsf = sb("xsf", [128, 4, 128])
        g8 = sb("g8", [128, 64])
        gc = sb("gc", [128, 64])
        egc = sb("egc", [128, 64])
        beta = sb("beta", [128, 64])
        dtb = sb("dtb", [128, 64])
        nb = sb("nb", [128, 64])
        nb2 = sb("nb2", [128, 64])
        for _t, _k in ((g8, "g8"), (beta, "beta"), (nega, "nega")):
            MEMSET("pool", _t[:], 0.0, [_k])
        tt = sb("tt", [128, 8, 128])
        DT = sb("DT", [128, 8, 128], BF16)
        Dst = sb("Dst", [128, 4, 128], BF16)
        eGbc = sb("eGbc", [128, 8, 128])
        eGlB = sb("eGlB", [128, 2, 2])
        X_bf = [sb(f"X_bf{i}", [128, 4, 128], BF16) for i in range(2)]
        Y_bf = [sb(f"Y_bf{i}", [128, 4, 128], BF16) for i in range(2)]
        P_bf = sb("P_bf", [128, 4, 128], BF16)
        bv = sb("bv", [128, 4, 64])
        r_bf = sb("r_bf", [128, 4, 64], BF16)
        u_bf = sb("u_bf", [128, 4, 64], BF16)
        xd_bf = sb("xd_bf", [128, 4, 64], BF16)

        def load_layer(l):
            DMA(prm[:], prm_d[l], [], ["prm"])
            DMA(wst[0][0:1, 0:768], cbias_d[0:1, l * 768:(l + 1) * 768], [], ["wst0"])
            CP("pool", cbias_bf[:], wst[0][0:1, 0:768], ["wst0"], ["cbias_bf"])
            ACT(nega[:, 0:8], prm[:, 1280:1288], AF.Exp, ["prm"], ["nega"])
            TS("dve", nega[:, 0:64], nega[:, 0:64], -1.0, ALU.mult, ["nega"], ["nega"])
            for cv in range(2):
                for blk in range(6):
                    for w in range(4):
                        col = 32 + l * 48 + cv * 24 + blk * 4 + w
                        if (blk + w) % 2 == 0:
                            ACT(dg[:, cv * 6 + blk, w, :], ident_bf[:], AF.Copy, ["ident_bf", "featp"], ["dg"],
                                scale=featp[:, col:col + 1])
                        else:
                            TS("dve", dg[:, cv * 6 + blk, w, :], ident_bf[:], featp[:, col:col + 1], ALU.mult,
                               ["ident_bf", "featp"], ["dg"])
            i = 0
            for kc in range(8):
                for c0 in range(0, IN_DIM, WCH):
                    st = wst[i % 2]; sk = f"wst{i % 2}"
                    DMA(st[:, 0:WCH], w_in[l, kc * 128:(kc + 1) * 128, c0:c0 + WCH], [], [sk])
                    if i % 2 == 0:
                        ACT(win[:, kc, c0:c0 + WCH], st[:, 0:WCH], AF.Copy, [sk, "featp"], ["win"],
                            scale=featp[:, l * 8 + kc:l * 8 + kc + 1])
                    else:
                        TS("dve", win[:, kc, c0:c0 + WCH], st[:, 0:WCH], featp[:, l * 8 + kc:l * 8 + kc + 1], ALU.mult,
                           [sk, "featp"], ["win"])
                    i += 1
            for kc in range(8):
                st = wst[i % 2]; sk = f"wst{i % 2}"
                DMA(st[:, 0:1024], w_out[l, kc * 128:(kc + 1) * 128, :], [], [sk])
                CP("act" if i % 2 == 0 else "dve", wout[:, kc, :], st[:, 0:1024], [sk], ["wout"])
                i += 1
            for nm, S_, Sb_ in (("A", S_A, Sb_A), ("B", S_B, Sb_B), ("C", S_C, Sb_C), ("D", S_D, Sb_D)):
                MEMSET("pool", S_[:], 0.0, ["S_" + nm])
                MEMSET("pool", Sb_[:], 0.0, ["Sb_" + nm])
            MEMSET("pool", uT[0][:, :, 0:3], 0.0, ["uT0"])

        def stage0(l, t):
            n = 16 if t == 0 else 128
            hk = f"hb{t % 2}"
            ht = hb[t % 2][0:n, :]
            hnT = hnTs[t % 2]; hnTk = f"hnT{t % 2}"
            if l == 0:
                DMA(ht, meta if t == 0 else xp[(t - 1) * 128:t * 128, :], [], [hk])
            else:
                DMA(ht, hscr[t * 128:t * 128 + n, :], [f"hd{t}"], [hk])
            ACT(hn_bf[0:n, :], ht, AF.Square, [hk], ["hn_bf", "st0"], accum=st0[0:n, 0:1])
            ACT(st0[0:n, 1:2], st0[0:n, 0:1], AF.Ln, ["st0"], ["st0"], scale=1.0 / D, bias=eps_t[0:n, 0:1])
            ACT(st0[0:n, 1:2], st0[0:n, 1:2], AF.Exp, ["st0"], ["st0"], scale=-0.5)
            ACT(hn_bf[0:n, :], ht, AF.Copy, [hk, "st0"], ["hn_bf"], scale=st0[0:n, 1:2])
            for half in range(2):
                pt, pk = bank()
                for kk in range(4):
                    kc = half * 4 + kk
                    MM(pt[:, kk * 128:kk * 128 + n], hn_bf[0:n, kc * 128:(kc + 1) * 128], ident_bf[0:n, 0:n],
                       ["hn_bf", "ident_bf"], [pk])
                CP("act" if half else "dve", hnT[:, half * 4:half * 4 + 4, 0:n],
                   pt[:, :].rearrange("p (a b) -> p a b", a=4)[:, :, 0:n], [pk], [hnTk])

        def tile_fwd(l, t, mid_hook=None):
            n = 16 if t == 0 else 128
            chunks = [(0, 16)] if t == 0 else [(0, 64), (64, 128)]
            nch = len(chunks)
            clen = chunks[0][1]
            last_tile = (t == ntiles - 1)
            hk = f"hb{t % 2}"
            ht = hb[t % 2][0:n, :]
            hnT = hnTs[t % 2]; hnTk = f"hnT{t % 2}"
            cur = uT[t % 2]; curk = f"uT{t % 2}"
            nxt = uT[(t + 1) % 2]; nxtk = f"uT{(t + 1) % 2}"
            rt = rot[t % 2]; rtk = f"rot{t % 2}"
            DMA(rt[:], rot_d[t], [], [rtk])

            def bc_h(ap2d, nh=4):
                return ap2d.unsqueeze(1).to_broadcast([ap2d.shape[0], nh, ap2d.shape[1]])

            def bc_l(ap2d, m):
                return ap2d.unsqueeze(2).to_broadcast([ap2d.shape[0], ap2d.shape[1], m])

            def proj_tok(c0, c1, extra=None):
                pt, pk = bank()
                for kc in range(8):
                    MM(pt[0:n, 0:c1 - c0], hnT[:, kc, 0:n], win[:, kc, c0:c1], [hnTk, "win"], [pk],
                       start=(kc == 0), stop=(kc == 7 and extra is None))
                if extra is not None:
                    MM(pt[0:n, extra[0]:extra[0] + 4], C("ident", slice(0, n), 0, n), prm[0:n, extra[1]:extra[1] + 4],
                       ["cst", "prm"], [pk], start=False, stop=True)
                return pt, pk

            def proj_feat(cols0, nblk):
                pt, pk = bank()
                for b_ in range(nblk):
                    for kc in range(8):
                        MM(pt[:, b_ * 128:b_ * 128 + n], win[:, kc, cols0 + b_ * 128:cols0 + (b_ + 1) * 128], hnT[:, kc, 0:n],
                           [hnTk, "win"], [pk], start=(kc == 0), stop=(kc == 7))
                return pt, pk

            def v3(ps_ap, a):
                return ps_ap.rearrange("p (a b) -> p a b", a=a)

            pA0, kA0 = proj_tok(0, 512)
            pA1, kA1 = proj_tok(512, 1024)
            ACT(f1[0:n, 0:256], pA0[0:n, 0:256], AF.Exp, [kA0], ["f1"], scale=-1.0)
            ACT(f1[0:n, 256:512], pA0[0:n, 256:512], AF.Exp, [kA0], ["f1"])
            ACT(f1[0:n, 512:768], pA1[0:n, 256:512], AF.Exp, [kA1], ["f1"], scale=-1.0)
            sigmoid_from_exp(f1[0:n, :], "f1")
            STT(f2[0:n, 0:256], pA0[0:n, 0:256], QK, f1[0:n, 0:256], ALU.mult, ALU.mult, [kA0, "f1"], ["f2"])
            TT("dve", f2[0:n, 256:512], f1[0:n, 256:512], oml[0:n, l, :], ALU.mult, ["f1", "oml"], ["f2"])
            TT("dve", f1[0:n, 512:768], f1[0:n, 512:768], prm[0:n, 0:256], ALU.mult, ["f1", "prm"], ["f1"])
            TT("dve", gate[0:n, 0:256], pA1[0:n, 256:512], f1[0:n, 512:768], ALU.mult, [kA1, "f1"], ["gate"])
            ACT(v_bf[0:n, :, :].rearrange("p a b -> p (a b)"), pA1[0:n, 0:256], AF.Copy, [kA1], ["v_bf"])
            ACT(f3[0:n, 0:256], f2[0:n, 256:512], AF.Ln, ["f2"], ["f3"], scale=-1.0, bias=eps_t[0:n, 1:2])
            pG, kG = bank()
            MM(pG[0:n, 0:256], C("uprime", slice(0, n), 0, n), f3[0:n, 0:256], ["cst", "f3"], [kG])
            for hp in range(2):
                MM(pG[:, 256 + hp * 8:256 + hp * 8 + 8], f3[0:n, hp * 128:(hp + 1) * 128], C("wc", slice(0, n)),
                   ["cst", "f3"], [kG])
            ACT(f3[0:n, 0:256], pG[0:n, 0:256], AF.Exp, [kG], ["f3"])
            ACT(f3[0:n, 256:512], pG[0:n, 0:256], AF.Exp, [kG], ["f3"], scale=-1.0)
            ACT(ecs[:].rearrange("p a b -> p (a b)"), pG[:, 256:272], AF.Exp, [kG], ["ecs"])
            TT("dve", b1[0:n, 0:2, :].rearrange("p a b -> p (a b)"), f2[0:n, 0:256], f3[0:n, 0:256], ALU.mult,
               ["f2", "f3"], ["b1"])
            TT("dve", b1[0:n, 2:4, :].rearrange("p a b -> p (a b)"), f2[0:n, 256:512], f3[0:n, 256:512], ALU.mult,
               ["f2", "f3"], ["b1"])
            pT, kT = bank()
            for blk in range(4):
                MM(pT[:, blk * 128:blk * 128 + n], b1[0:n, blk, :], ident_bf[0:n, 0:n], ["b1", "ident_bf"], [kT])
            CP("act", qkT[:, :, 0:n], v3(pT[:, :], 4)[:, :, 0:n], [kT], ["qkT"])
            pS, kS = bank()
            for hd in (0, 2, 1, 3):
                hp, hh = hd // 2, hd % 2
                rows = slice(hh * 64, hh * 64 + 64)
                MM(pS[0:n, hd * 128:hd * 128 + n], qkT[rows, 2 + hp, 0:n], qkT[rows, hp, 0:n], ["qkT"], [kS])
            TT("dve", AT[0:n, :, 0:n], v3(pS[0:n, :], 4)[:, :, 0:n], bc_h(C("maskT", slice(0, n), 0, n)), ALU.mult,
               [kS, "cst"], ["AT"])
            pO, kO = psb[4], "ps4"
            pOD, kOD = psb[4], "ps4"
            for hd in (0, 2, 1, 3):
                MM(pO[0:n, hd * 64:hd * 64 + 64], AT[0:n, hd, 0:n], v_bf[0:n, hd, :], ["AT", "v_bf"], [kO],
                   start=(hd == 0), stop=False)
            for ci, (c0, c1) in enumerate(chunks):
                TT("dve", Sb_A[:], S_A[:], bc_l(ecs[:, :, ci], 64), ALU.mult, ["S_A", "ecs"], ["Sb_A"])
                TT("dve", Sd_A[:], S_A[:], bc_l(ecs[:, :, 4 + ci], 64), ALU.mult, ["S_A", "ecs"], ["Sd_A"])
                pK, kK = bank()
                for hd in (0, 2, 1, 3):
                    hp, hh = hd // 2, hd % 2
                    rows = slice(hh * 64, hh * 64 + 64)
                    MM(pO[c0:c1, hd * 64:hd * 64 + 64], qkT[rows, hp, c0:c1], Sb_A[rows, hp, :], ["qkT", "Sb_A"], [kO],
                       start=False, stop=True)
                    MM(pK[rows, hp * 64:hp * 64 + 64], b1[c0:c1, 2 + hp, hh * 64:hh * 64 + 64], v_bf[c0:c1, hd, :],
                       ["b1", "v_bf"], [kK])
                TT("dve", tmpS[:, 0:2, :], v3(pK[:, 0:128], 2), bc_l(ecs[:, :, 2 + ci], 64), ALU.mult, [kK, "ecs"], ["tmpS"])
                TT("dve", S_A[:], tmpS[:, 0:2, :], Sd_A[:], ALU.add, ["tmpS", "Sd_A"], ["S_A"])

            def head_norm(ps_ap, pskey, gcols, ycols):
                ACT(f4[0:n, 0:256], ps_ap, AF.Square, [pskey], ["f4"])
                RED(st4[0:n, 4:8], v3(f4[0:n, 0:256], 4), ["f4"], ["st4"])
                rsqrt_act(st4[0:n, 8:12], st4[0:n, 4:8], 1.0 / 64, ["st4"], ["st4"])
                TT("dve", v3(f4[0:n, 0:256], 4), v3(ps_ap, 4), bc_l(st4[0:n, 8:12], 64), ALU.mult, [pskey, "st4"], ["f4"])
                TT("dve", y_bf[0:n, ycols], f4[0:n, 0:256], gate[0:n, gcols], ALU.mult, ["f4", "gate"], ["y_bf"])

            head_norm(pO[0:n, 0:256], kO, slice(0, 256), slice(0, 256))
            if mid_hook is not None:
                mid_hook()

            pD0, kD0 = proj_tok(3084, 3596)
            pD1, kD1 = proj_tok(3596, 4108)
            cosb = rt[0:n, 0:32].unsqueeze(1).to_broadcast([n, 16, 32])
            sinb = rt[0:n, 32:64].unsqueeze(1).to_broadcast([n, 16, 32])
            qk4 = pD0[0:n, :].rearrange("p (a b) -> p a b", a=16)
            TT("dve", f1[0:n, 0:512].rearrange("p (a b) -> p a b", a=16), qk4, cosb, ALU.mult, [kD0, rtk], ["f1"])
            TT("dve", f2[0:n, 0:512].rearrange("p (a b) -> p a b", a=16), qk4, sinb, ALU.mult, [kD0, rtk], ["f2"])
            c4 = f1[0:n, 0:512].rearrange("p (a s b) -> p a s b", a=8, s=2)
            s4 = f2[0:n, 0:512].rearrange("p (a s b) -> p a s b", a=8, s=2)
            qkr = b1[0:n, :, :].rearrange("p a (s b) -> p a s b", s=4)
            qkr8 = b1[0:n, :, :].rearrange("p a b -> p (a b)").rearrange("p (a s b) -> p a s b", a=8, s=2)
            TT("dve", qkr8[:, :, 0, :], c4[:, :, 0, :], s4[:, :, 1, :], ALU.subtract, ["f1", "f2"], ["b1"])
            TT("dve", qkr8[:, :, 1, :], c4[:, :, 1, :], s4[:, :, 0, :], ALU.add, ["f1", "f2"], ["b1"])
            ACT(v_bf[0:n, :, :].rearrange("p a b -> p (a b)"), pD1[0:n, 0:256], AF.Copy, [kD1], ["v_bf"])
            ACT(f1[0:n, 512:768], pD1[0:n, 256:512], AF.Exp, [kD1], ["f1"], scale=-1.0)
            sigmoid_from_exp(f1[0:n, 512:768], "f1")
            TT("dve", gate[0:n, 768:1024], pD1[0:n, 256:512], f1[0:n, 512:768], ALU.mult, [kD1, "f1"], ["gate"])
            pT, kT = bank()
            for blk in range(4):
                MM(pT[:, blk * 128:blk * 128 + n], b1[0:n, blk, :], ident_bf[0:n, 0:n], ["b1", "ident_bf"], [kT])
            CP("act", qkT[:, :, 0:n], v3(pT[:, :], 4)[:, :, 0:n], [kT], ["qkT"])
            egq = C("egq").rearrange("p (a b) -> p a b", a=2)
            TT("dve", qpT[:, 0:2, 0:n], qkT[:, 0:2, 0:n], egq[:, :, 0:n], ALU.mult, ["qkT", "cst"], ["qpT"])
            egrev = C("egrev16" if t == 0 else "egrev64", slice(0, n))
            TT("dve", kp_bf[0:n, :, 0:64], b1[0:n, 2:4, :].rearrange("p a (s b) -> p (a s) b", s=2), bc_l(egrev, 64), ALU.mult,
               ["b1", "cst"], ["kp_bf"])
            pS, kS = bank()
            for hd in (0, 2, 1, 3):
                hp, hh = hd // 2, hd % 2
                rows = slice(hh * 64, hh * 64 + 64)
                MM(pS[0:n, hd * 128:hd * 128 + n], qkT[rows, 2 + hp, 0:n], qkT[rows, hp, 0:n], ["qkT"], [kS])
            TT("dve", AT[0:n, :, 0:n], v3(pS[0:n, :], 4)[:, :, 0:n], dtret_bf[0:n, :, 0:n], ALU.mult, [kS, "dtret_bf"], ["AT"])
            for hd in (0, 2, 1, 3):
                MM(pOD[0:n, 256 + hd * 64:256 + hd * 64 + 64], AT[0:n, hd, 0:n], v_bf[0:n, hd, :], ["AT", "v_bf"], [kOD],
                   start=(hd == 0), stop=False)
            egl = C("egl").rearrange("p (a b) -> p a b", a=2)
            for ci, (c0, c1) in enumerate(chunks):
                pK, kK = bank()
                for hd in (0, 2, 1, 3):
                    hp, hh = hd // 2, hd % 2
                    rows = slice(hh * 64, hh * 64 + 64)
                    MM(pOD[c0:c1, 256 + hd * 64:256 + hd * 64 + 64], qpT[rows, hp, c0:c1], Sb_D[rows, hp, :], ["qpT", "Sb_D"], [kOD],
                       start=False, stop=True)
                    MM(pK[rows, hp * 64:hp * 64 + 64], kp_bf[c0:c1, hd, 0:64], v_bf[c0:c1, hd, :], ["kp_bf", "v_bf"], [kK])
                TT("dve", tmpS[:, 0:2, :], S_D[:], bc_l(egl[:, :, (0 if t == 0 else 1)], 64), ALU.mult, ["S_D", "cst"], ["tmpS"])
                TT("dve", S_D[:], tmpS[:, 0:2, :], v3(pK[:, 0:128], 2), ALU.add, ["tmpS", kK], ["S_D"])
                CP("act", Sb_D[:], S_D[:], ["S_D"], ["Sb_D"])
            oD = pOD[0:n, 256:512]
            kO_ = kOD
            RED(st4[0:n, 4:8], v3(oD, 4), [kO_], ["st4"])
            TS("dve", st4[0:n, 4:8], st4[0:n, 4:8], -1.0 / 64, ALU.mult, ["st4"], ["st4"])
            TT("dve", v3(f3[0:n, 0:256], 4), v3(oD, 4), bc_l(st4[0:n, 4:8], 64), ALU.add, [kO_, "st4"], ["f3"])
            ACT(f4[0:n, 0:256], f3[0:n, 0:256], AF.Square, ["f3"], ["f4"])
            RED(st4[0:n, 4:8], v3(f4[0:n, 0:256], 4), ["f4"], ["st4"])
            rsqrt_act(st4[0:n, 8:12], st4[0:n, 4:8], 1.0 / 64, ["st4"], ["st4"])
            TT("dve", v3(f3[0:n, 0:256], 4), v3(f3[0:n, 0:256], 4), bc_l(st4[0:n, 8:12], 64), ALU.mult, ["f3", "st4"], ["f3"])
            TT("dve", f3[0:n, 0:256], f3[0:n, 0:256], prm[0:n, 768:1024], ALU.mult, ["f3", "prm"], ["f3"])
            TT("dve", f3[0:n, 0:256], f3[0:n, 0:256], prm[0:n, 1024:1280], ALU.add, ["f3", "prm"], ["f3"])
            TT("dve", y_bf[0:n, 768:1024], f3[0:n, 0:256], gate[0:n, 768:1024], ALU.mult, ["f3", "gate"], ["y_bf"])

            pBz, kBz = proj_tok(1792, 2056, (256, 1288))
            pCz, kCz = proj_tok(2824, 3084, (256, 1292))
            ACT(f1[0:n, 512:768], pBz[0:n, 0:256], AF.Exp, [kBz], ["f1"], scale=-1.0)
            ACT(f2[0:n, 0:256], pCz[0:n, 0:256], AF.Exp, [kCz], ["f2"], scale=-1.0)
            ACT(beta[0:n, 0:4], pBz[0:n, 260:264], AF.Exp, [kBz], ["beta"], scale=-1.0)
            ACT(g8[0:n, 0:4], pBz[0:n, 256:260], AF.Exp, [kBz], ["g8"])
            ACT(g8[0:n, 4:8], pCz[0:n, 256:260], AF.Exp, [kCz], ["g8"])
            sigmoid_from_exp(f1[0:n, 512:768], "f1")
            sigmoid_from_exp(f2[0:n, 0:256], "f2")
            TT("dve", f1[0:n, 512:768], f1[0:n, 512:768], prm[0:n, 256:512], ALU.mult, ["f1", "prm"], ["f1"])
            TT("dve", gate[0:n, 256:512], pBz[0:n, 0:256], f1[0:n, 512:768], ALU.mult, [kBz, "f1"], ["gate"])
            TT("dve", gate[0:n, 512:768], pCz[0:n, 0:256], f2[0:n, 0:256], ALU.mult, [kCz, "f2"], ["gate"])
            for cv, cols0 in ((0, 1024), (1, 2056)):
                for part, (b0, nb_) in enumerate(((0, 4), (4, 2))):
                    pf, kf = proj_feat(cols0 + b0 * 128, nb_)
                    src = v3(pf[:, 0:nb_ * 128], nb_)[:, :, 0:n]
                    CP("act", cur[:, cv * 6 + b0:cv * 6 + b0 + nb_, 3:3 + n], src, [kf], [curk])
                    if last_tile:
                        CP("dve", cvst[:, cv * 6 + b0:cv * 6 + b0 + nb_, :], src[:, :, n - 3:n], [kf], ["cvst"])
            if not last_tile:
                CP("pool", nxt[:, :, 0:3], cur[:, :, n:n + 3], [curk], [nxtk])
            for cv in range(2):
                for part, (b0, nb_) in enumerate(((0, 4), (4, 2))):
                    pc, kc_ = bank()
                    for b_ in range(nb_):
                        blk = cv * 6 + b0 + b_
                        for w in range(4):
                            MM(pc[:, b_ * 128:b_ * 128 + n], dg[:, blk, w, :], cur[:, blk, w:w + n], ["dg", curk], [kc_],
                               start=(w == 0), stop=(w == 3 and cv == 0))
                        if cv == 1:
                            MM(pc[:, b_ * 128:b_ * 128 + n], cbias_bf[0:1, (b0 + b_) * 128:(b0 + b_ + 1) * 128], ones_bf[0:1, 0:n],
                               ["cbias_bf", "ones_bf"], [kc_], start=False, stop=True)
                    src = v3(pc[:, 0:nb_ * 128], nb_)[:, :, 0:n]
                    dstf = v3(f1[:, 0:nb_ * 128], nb_)[:, :, 0:n]
                    ACT(dstf, src, AF.Exp, [kc_], ["f1"], scale=-1.0)
                    sigmoid_from_exp(dstf, "f1")
                    if cv == 0 and part == 0:
                        TT("dve", xsf[:, :, 0:n], src, dstf, ALU.mult, [kc_, "f1"], ["xsf"])
                    else:
                        TT("dve", xs[:, cv * 6 + b0:cv * 6 + b0 + nb_, 0:n], src, dstf, ALU.mult, [kc_, "f1"], ["xs"])
            ACT(b1[:, :, 0:n], xsf[:, :, 0:n], AF.Square, ["xsf"], ["b1"])
            pN, kN = bank()
            for blk in range(4):
                MM(pN[:, blk * 128:blk * 128 + n], bones_bf[:], b1[:, blk, 0:n], ["bones_bf", "b1"], [kN])
            srcN = v3(pN[:, :], 4)[:, :, 0:n]
            dstN = v3(f2[:, 0:512], 4)[:, :, 0:n]
            ACT(dstN, srcN, AF.Ln, [kN], ["f2"], bias=eps_t[:, 0:1])
            ACT(dstN, dstN, AF.Exp, ["f2"], ["f2"], scale=-0.5)
            STT(xs[:, 0:2, 0:n], xsf[:, 0:2, 0:n], QK, dstN[:, 0:2, :], ALU.mult, ALU.mult, ["xsf", "f2"], ["xs"])
            TT("dve", xs[:, 2:4, 0:n], xsf[:, 2:4, 0:n], dstN[:, 2:4, :], ALU.mult, ["xsf", "f2"], ["xs"])

            ACT(g8[0:n, 0:8], g8[0:n, 0:8], AF.Ln, ["g8"], ["g8"], bias=eps_t[0:n, 1:2])
            CP("dve", dtb[0:n, :], g8[0:n, :], ["g8"], ["dtb"])
            TT("dve", g8[0:n, :], g8[0:n, :], nega[0:n, :], ALU.mult, ["g8", "nega"], ["g8"])
            sigmoid_from_exp(beta[0:n, :], "beta")
            pDc, kDc = bank()
            MM(pDc[0:n, 0:32], C("maskT", slice(0, n), 0, n), g8[0:n, 0:32], ["cst", "g8"], [kDc])
            MM(pDc[0:n, 32:64], C("urev", slice(0, n), 0, n), g8[0:n, 0:32], ["cst", "g8"], [kDc])
            CP("dve", gc[0:n, :], pDc[0:n, 0:64], [kDc], ["gc"])
            ACT(egc[0:n, :], gc[0:n, :], AF.Exp, ["gc"], ["egc"])
            for half in range(2):
                pB_, kB_ = bank()
                CP("dve", v3(f4[0:n, 0:512], 4), bc_l(g8[0:n, half * 4:half * 4 + 4], 128), ["g8"], ["f4"])
                for hd in (0, 2, 1, 3):
                    MM(pB_[:, hd * 128:hd * 128 + n], f4[0:n, hd * 128:(hd + 1) * 128],
                       C("maskT", slice(0, n), 0, n), ["cst", "f4"], [kB_])
                srcB = v3(pB_[:, :], 4)[:, :, 0:n]
                ACT(eGbc[:, half * 4:half * 4 + 4, 0:n], srcB, AF.Exp, [kB_], ["eGbc"])
                TT("dve", tt[0:n, half * 4:half * 4 + 4, 0:n], srcB[0:n], bc_l(gc[0:n, half * 4:half * 4 + 4], n), ALU.subtract,
                   [kB_, "gc"], ["tt"])
            if True:
                TT("dve", v3(f1[0:n, 0:512], 4)[:, :, 0:n], tt[0:n, 0:4, 0:n], bc_h(C("negS", slice(0, n), 0, n)), ALU.subtract,
                   ["tt", "cst"], ["f1"])
                ACT(Dst[0:n, :, 0:n], v3(f1[0:n, 0:512], 4)[:, :, 0:n], AF.Exp, ["f1"], ["Dst"], scale=-1.0)
                TT("dve", tt[0:n, :, 0:n], tt[0:n, :, 0:n], bc_h(C("negT", slice(0, n), 0, n), 8), ALU.add, ["tt", "cst"], ["tt"])
                ACT(DT[0:n, :, 0:n], tt[0:n, :, 0:n], AF.Exp, ["tt"], ["DT"])

            pS, kS = bank()
            pKK, kKK = bank()
            for hd in (0, 2, 1, 3):
                hp, hh = hd // 2, hd % 2
                rows = slice(hh * 64, hh * 64 + 64)
                MM(pS[0:n, hd * 128:hd * 128 + n], xs[rows, 2 + hp, 0:n], xs[rows, hp, 0:n], ["xs"], [kS])
                MM(pKK[0:n, hd * 128:hd * 128 + n], xs[rows, 2 + hp, 0:n], xs[rows, 2 + hp, 0:n], ["xs"], [kKK])
            TT("dve", AT[0:n, :, 0:n], v3(pS[0:n, :], 4)[:, :, 0:n], DT[0:n, 0:4, 0:n], ALU.mult, [kS, "DT"], ["AT"])
            TS("dve", nb[0:n, :], beta[0:n, :], -1.0, ALU.mult, ["beta"], ["nb"])
            TT("dve", v3(f1[0:n, 0:512], 4)[:, :, 0:n], v3(pKK[0:n, :], 4)[:, :, 0:n], Dst[0:n, :, 0:n], ALU.mult, [kKK, "Dst"], ["f1"])
            TT("dve", X_bf[0][0:n, :, 0:n], v3(f1[0:n, 0:512], 4)[:, :, 0:n], bc_l(nb[0:n, 0:4], n), ALU.mult,
               ["f1", "nb"], ["X_bf0"])
            pY, kY = bank()
            for hd in (0, 2, 1, 3):
                MM(pY[0:n, hd * 128:hd * 128 + n], X_bf[0][0:n, hd, 0:n], ident_bf[0:n, 0:n], ["X_bf0", "ident_bf"], [kY])
            CP("act", Y_bf[0][0:n, :, 0:n], v3(pY[0:n, :], 4)[:, :, 0:n], [kY], ["Y_bf0"])
            TT("dve", P_bf[0:n, :, 0:n], v3(pY[0:n, :], 4)[:, :, 0:n], bc_h(C("ident", slice(0, n), 0, n)), ALU.add,
               [kY, "cst"], ["P_bf"])
            nlev = int(math.ceil(math.log2(clen))) - 1
            ci_ = 0
            for lev in range(nlev):
                ni_ = 1 - ci_
                pX2, kX2 = bank()
                for hd in (0, 2, 1, 3):
                    MM(pX2[0:n, hd * 128:hd * 128 + n], Y_bf[ci_][0:n, hd, 0:n], X_bf[ci_][0:n, hd, 0:n],
                       [f"Y_bf{ci_}", f"X_bf{ci_}"], [kX2])
                CP("act", X_bf[ni_][0:n, :, 0:n], v3(pX2[0:n, :], 4)[:, :, 0:n], [kX2], [f"X_bf{ni_}"])
                if lev < nlev - 1:
                    pY2, kY2 = bank()
                    for hd in (0, 2, 1, 3):
                        MM(pY2[0:n, hd * 128:hd * 128 + n], X_bf[ci_][0:n, hd, 0:n], Y_bf[ci_][0:n, hd, 0:n],
                           [f"Y_bf{ci_}", f"X_bf{ci_}"], [kY2])
                    CP("dve", Y_bf[ni_][0:n, :, 0:n], v3(pY2[0:n, :], 4)[:, :, 0:n], [kY2], [f"Y_bf{ni_}"])
                pP, kP = bank()
                for hd in (0, 2, 1, 3):
                    MM(pP[0:n, hd * 128:hd * 128 + n], X_bf[ni_][0:n, hd, 0:n], P_bf[0:n, hd, 0:n], [f"X_bf{ni_}", "P_bf"], [kP])
                TT("dve", P_bf[0:n, :, 0:n], P_bf[0:n, :, 0:n], v3(pP[0:n, :], 4)[:, :, 0:n], ALU.add, ["P_bf", kP], ["P_bf"])
                ci_ = ni_
            pT, kT = bank()
            for blk in range(4):
                MM(pT[0:n, blk * 128:(blk + 1) * 128], xs[:, 2 + blk, 0:n], ident_bf[:, :], ["xs", "ident_bf"], [kT])
            TT("dve", kp_bf[0:n, :, 0:64], v3(pT[0:n, 0:256], 4), bc_l(egc[0:n, 32:36], 64), ALU.mult, [kT, "egc"], ["kp_bf"])
            TT("dve", bv[0:n, :, :], v3(pT[0:n, 256:512], 4), bc_l(beta[0:n, 0:4], 64), ALU.mult, [kT, "beta"], ["bv"])
            TT("dve", nb2[0:n, :], nb[0:n, :], egc[0:n, :], ALU.mult, ["nb", "egc"], ["nb2"])
            for hh in range(2):
                rows = slice(hh * 64, hh * 64 + 64)
                TT("dve", qpT[rows, 0:2, 0:n], xs[rows, 0:2, 0:n], eGbc[rows, hh:4:2, 0:n], ALU.mult, ["xs", "eGbc"], ["qpT"])
            pO2, kO2 = psb[5], "ps5"
            pOC, kOC = psb[5], "ps5"
            for ci, (c0, c1) in enumerate(chunks):
                pW, kW = bank()
                for hd in (0, 2, 1, 3):
                    hp, hh = hd // 2, hd % 2
                    rows = slice(hh * 64, hh * 64 + 64)
                    MM(pW[c0:c1, hd * 64:hd * 64 + 64], xs[rows, 2 + hp, c0:c1], Sb_B[rows, hp, :], ["xs", "Sb_B"], [kW])
                TT("dve", v3(f4[c0:c1, 0:256], 4), v3(pW[c0:c1, 0:256], 4), bc_l(nb2[c0:c1, 0:4], 64), ALU.mult,
                   [kW, "nb2"], ["f4"])
                TT("dve", r_bf[c0:c1, :, :], v3(f4[c0:c1, 0:256], 4), bv[c0:c1, :, :], ALU.add, ["f4", "bv"], ["r_bf"])
                pU, kU = bank()
                for hd in (0, 2, 1, 3):
                    MM(pU[c0:c1, hd * 64:hd * 64 + 64], P_bf[c0:c1, hd, c0:c1], r_bf[c0:c1, hd, :], ["P_bf", "r_bf"], [kU])
                CP("act", u_bf[c0:c1, :, :], v3(pU[c0:c1, 0:256], 4), [kU], ["u_bf"])
                pK, kK = bank()
                for hd in (0, 2, 1, 3):
                    hp, hh = hd // 2, hd % 2
                    rows = slice(hh * 64, hh * 64 + 64)
                    MM(pO2[c0:c1, hd * 64:hd * 64 + 64], AT[c0:c1, hd, c0:c1], u_bf[c0:c1, hd, :], ["AT", "u_bf"], [kO2],
                       start=(hd == 0), stop=False)
                    MM(pO2[c0:c1, hd * 64:hd * 64 + 64], qpT[rows, hp, c0:c1], Sb_B[rows, hp, :], ["qpT", "Sb_B"], [kO2],
                       start=False, stop=True)
                    MM(pK[rows, hp * 64:hp * 64 + 64], kp_bf[c0:c1, hd, 0:64], u_bf[c0:c1, hd, :], ["kp_bf", "u_bf"], [kK])
                for hh in range(2):
                    rows = slice(hh * 64, hh * 64 + 64)
                    TT("dve", tmpS[rows, 0:2, :], S_B[rows, :, :], eGbc[rows, hh:4:2, c1 - 1:c1].to_broadcast([64, 2, 64]), ALU.mult,
                       ["S_B", "eGbc"], ["tmpS"])
                TT("dve", S_B[:], tmpS[:, 0:2, :], v3(pK[:, 0:128], 2), ALU.add, ["tmpS", kK], ["S_B"])
                CP("act", Sb_B[:], S_B[:], ["S_B"], ["Sb_B"])
            head_norm(pO2[0:n, 0:256], kO2, slice(256, 512), slice(256, 512))

            pS, kS = bank()
            for g in range(2):
                MM(pS[0:n, g * 128:g * 128 + n], xs[:, 8 + g, 0:n], xs[:, 10 + g, 0:n], ["xs"], [kS])
            for g in range(2):
                TT("dve", AT[0:n, 2 * g:2 * g + 2, 0:n], pS[0:n, g * 128:g * 128 + n].unsqueeze(1).to_broadcast([n, 2, n]),
                   DT[0:n, 4 + 2 * g:4 + 2 * g + 2, 0:n], ALU.mult, [kS, "DT"], ["AT"])
            pT, kT = bank()
            for blk in range(4):
                MM(pT[0:n, blk * 128:(blk + 1) * 128], xs[:, 6 + blk, 0:n], ident_bf[:, :], ["xs", "ident_bf"], [kT])
            TT("dve", v_bf[0:n, :, :], v3(pT[0:n, 0:256], 4), bc_l(dtb[0:n, 4:8], 64), ALU.mult, [kT, "dtb"], ["v_bf"])
            TT("dve", xd_bf[0:n, :, :], v3(pT[0:n, 0:256], 4), bc_l(prm[0:n, 1296:1300], 64), ALU.mult, [kT, "prm"], ["xd_bf"])
            for g in range(2):
                TT("dve", kp_bf[0:n, 2 * g:2 * g + 2, :], pT[0:n, 256 + g * 128:256 + (g + 1) * 128].unsqueeze(1).to_broadcast([n, 2, 128]),
                   bc_l(egc[0:n, 36 + 2 * g:36 + 2 * g + 2], 128), ALU.mult, [kT, "egc"], ["kp_bf"])
                TT("dve", qpT[:, 2 * g:2 * g + 2, 0:n], xs[:, 10 + g, 0:n].unsqueeze(1).to_broadcast([128, 2, n]),
                   eGbc[:, 4 + 2 * g:4 + 2 * g + 2, 0:n], ALU.mult, ["xs", "eGbc"], ["qpT"])
            for hd in (0, 2, 1, 3):
                MM(pOC[0:n, 256 + hd * 64:256 + hd * 64 + 64], AT[0:n, hd, 0:n], v_bf[0:n, hd, :], ["AT", "v_bf"], [kOC],
                   start=(hd == 0), stop=False)
            MM(pOC[0:n, 256:512], ident_bf[0:n, 0:n], xd_bf[0:n, :, :].rearrange("p a b -> p (a b)"), ["ident_bf", "xd_bf"], [kOC],
               start=False, stop=False)
            for ci, (c0, c1) in enumerate(chunks):
                pK, kK = bank()
                for hd in (0, 2, 1, 3):
                    MM(pOC[c0:c1, 256 + hd * 64:256 + hd * 64 + 64], qpT[:, hd, c0:c1], Sb_C[:, hd, :], ["qpT", "Sb_C"], [kOC],
                       start=False, stop=True)
                    MM(pK[:, hd * 64:hd * 64 + 64], kp_bf[c0:c1, hd, :], v_bf[c0:c1, hd, :], ["kp_bf", "v_bf"], [kK])
                TT("dve", tmpS[:, :, :], S_C[:], eGbc[:, 4:8, c1 - 1:c1].to_broadcast([128, 4, 64]), ALU.mult, ["S_C", "eGbc"], ["tmpS"])
                TT("dve", S_C[:], tmpS[:, :, :], v3(pK[:, 0:256], 4), ALU.add, ["tmpS", kK], ["S_C"])
                CP("act", Sb_C[:], S_C[:], ["S_C"], ["Sb_C"])
            TT("dve", f3[0:n, 0:256], pOC[0:n, 256:512], gate[0:n, 512:768], ALU.mult, [kOC, "gate"], ["f3"])
            ACT(f4[0:n, 0:256], f3[0:n, 0:256], AF.Square, ["f3"], ["f4"])
            RED(st4[0:n, 4:6], v3(f4[0:n, 0:256], 2), ["f4"], ["st4"])
            rsqrt_act(st4[0:n, 8:10], st4[0:n, 4:6], 1.0 / 128, ["st4"], ["st4"])
            TT("dve", v3(f3[0:n, 0:256], 2), v3(f3[0:n, 0:256], 2), bc_l(st4[0:n, 8:10], 128), ALU.mult, ["f3", "st4"], ["f3"])
            TT("dve", y_bf[0:n, 512:768], f3[0:n, 0:256], prm[0:n, 512:768], ALU.mult, ["f3", "prm"], ["y_bf"])

            for half in range(2):
                pt, pk = bank()
                for kk in range(4):
                    kc = half * 4 + kk
                    MM(pt[:, kk * 128:kk * 128 + n], y_bf[0:n, kc * 128:(kc + 1) * 128], ident_bf[0:n, 0:n],
                       ["y_bf", "ident_bf"], [pk])
                CP("act" if half else "dve", yT[:, half * 4:half * 4 + 4, 0:n], v3(pt[:, :], 4)[:, :, 0:n], [pk], ["yT"])
            for cg in range(2):
                pt, pk = bank()
                for kc in range(8):
                    MM(pt[0:n, :], yT[:, kc, 0:n], wout[:, kc, cg * 512:(cg + 1) * 512], ["yT", "wout"], [pk],
                       start=(kc == 0), stop=(kc == 7))
                TT("dve", ht[:, cg * 512:(cg + 1) * 512], ht[:, cg * 512:(cg + 1) * 512], pt[0:n, :], ALU.add, [hk, pk], [hk])

            if last_tile:
                for nm, S_, dst in (("A", S_A, st_hg), ("B", S_B, st_gd), ("D", S_D, st_rt)):
                    for hh in range(2):
                        DMA(dst[l, hh:4:2, :, :].rearrange("a k v -> k a v"), S_[hh * 64:(hh + 1) * 64, :, :], ["S_" + nm], [])
                DMA(st_sd[l].rearrange("a k v -> k a v"), S_C[:, :, :], ["S_C"], [])
                for blk in range(6):
                    DMA(st_gc[l][:, blk * 128:(blk + 1) * 128].rearrange("w p -> p w"), cvst[:, blk, :], ["cvst"], [], slow=True)
                    DMA(st_sc[l][:, blk * 128:(blk + 1) * 128].rearrange("w p -> p w"), cvst[:, 6 + blk, :], ["cvst"], [], slow=True)
            if l < depth - 1:
                DMA(hscr[t * 128:t * 128 + n, :], ht, [hk], [f"hd{t}"])
            if l == depth - 1 and t > 0:
                ACT(hn_bf[0:n, :], ht, AF.Square, [hk], ["hn_bf", "st4"], accum=st4[0:n, 0:1])
                rsqrt_act(st4[0:n, 1:2], st4[0:n, 0:1], 1.0 / D, ["st4"], ["st4"])
                STT(ht, ht, st4[0:n, 1:2], finw[0:n, :], ALU.mult, ALU.mult, [hk, "st4", "lgt"], [hk])
                DMA(y_p[(t - 1) * 128:t * 128, :], ht, [hk], [])


        hs = sb("hs", [NS, D])
        DMA(hs[:, :], xs_d, [], ["hs"])

        def sample_fwd(l, last):
            n = NS
            DMA(rot[0][:], rot_d[NT], [], ["rot0"])
            rt = rot[0]; rtk = "rot0"

            def v3(ps_ap, a):
                return ps_ap.rearrange("p (a b) -> p a b", a=a)

            def bc_l(ap2d, m):
                return ap2d.unsqueeze(2).to_broadcast([ap2d.shape[0], ap2d.shape[1], m])

            ACT(hn_bf[0:n, :], hs[:, :], AF.Square, ["hs"], ["hn_bf", "st4"], accum=st4[0:n, 0:1])
            rsqrt_act(st4[0:n, 1:2], st4[0:n, 0:1], 1.0 / D, ["st4"], ["st4"])
            ACT(hn_bf[0:n, :], hs[:, :], AF.Copy, ["hs", "st4"], ["hn_bf"], scale=st4[0:n, 1:2])
            for half in range(2):
                pt, pk = bank()
                for kk in range(4):
                    kc = half * 4 + kk
                    MM(pt[:, kk * 128:kk * 128 + n], hn_bf[0:n, kc * 128:(kc + 1) * 128], ident_bf[0:n, 0:n],
                       ["hn_bf", "ident_bf"], [pk])
                CP("act", hnT[:, half * 4:half * 4 + 4, 0:n], v3(pt[:, :], 4)[:, :, 0:n], [pk], ["hnT1"])

            def proj_tok(c0, c1, extra=None):
                pt, pk = bank()
                for kc in range(8):
                    MM(pt[0:n, 0:c1 - c0], hnT[:, kc, 0:n], win[:, kc, c0:c1], ["hnT1", "win"], [pk],
                       start=(kc == 0), stop=(kc == 7 and extra is None))
                if extra is not None:
                    MM(pt[0:n, extra[0]:extra[0] + 4], C("ident", slice(0, n), 0, n), prm[0:n, extra[1]:extra[1] + 4],
                       ["cst", "prm"], [pk], start=False, stop=True)
                return pt, pk

            sel = C("sel").rearrange("p (a b) -> p a b", a=4)
            selT = C("selT").rearrange("p (a b) -> p a b", a=4)
            pvs = f4
            Sbuf = [tt, eGbc]; Skey = ["tt", "eGbc"]
            Tbuf = gate; Tkey = "gate"
            slot = [0]

            def select(fields):
                pv, pvk = bank()
                first = True
                for (c0, wd, fn) in fields:
                    for hd in range(4):
                        ap, key = fn(hd)
                        P.op("pe", (lambda o_, l_, r_, st_: (lambda e: e.matmul(o_, lhsT=l_, rhs=r_, start=st_, stop=False,
                                                                                 skip_group_check=True)))(
                            pv[0:64, c0:c0 + wd], sel[0:n, hd, :], ap, first), reads=["cst", key], writes=[pvk])
                        first = False
                wtot = max(c0 + wd for (c0, wd, _) in fields)
                CP("dve", pvs[0:64, 0:wtot], pv[0:64, 0:wtot], [pvk], ["f4"])

            def unselect(o_ap, okey):
                po, pok = bank()
                for hd in range(4):
                    MM(po[0:n, hd * 64:hd * 64 + 64], selT[0:64, hd, :], o_ap, ["cst", okey], [pok])
                return po, pok

            def state_io(st_in, st_out, K):
                ks = 16
                for k0 in range(0, K, ks):
                    yield k0, ks

            def load_slice(st_in, k0, ks):
                i = slot[0] % 2
                slot[0] += 1
                Sv = Sbuf[i][0:64, :, :].rearrange("p a b -> p (a b)")[:, 0:ks * 64].rearrange("p (k v) -> p k v", k=ks)
                for hd in range(4):
                    DMA(Sv[hd * 16:(hd + 1) * 16, :, :], st_in[l, :, hd, k0:k0 + ks, :], [], [Skey[i]])
                return Sv, Skey[i]

            def store_slice(st_out, Sv, sk, k0, ks):
                for hd in range(4):
                    DMA(st_out[l, :, hd, k0:k0 + ks, :], Sv[hd * 16:(hd + 1) * 16, :, :], [sk], [])

            o_sb = tmpS[0:64, 0, :]; w_sb = tmpS[0:64, 1, :]; op_sb = tmpS[0:64, 2, :]; u_sb = tmpS[0:64, 3, :]

            def Tview(ks):
                return Tbuf[0:64, 0:ks * 64].rearrange("p (k v) -> p k v", k=ks)

            def q_reduce(Sv, sk, q_ap, k0, ks, first, acc):
                T = Tview(ks)
                TT("dve", T, Sv, bc_l(q_ap[:, k0:k0 + ks], 64), ALU.mult, [sk, "f4"], [Tkey])
                RED(op_sb, T.rearrange("p k v -> p v k"), [Tkey], ["tmpS"])
                if first:
                    CP("dve", acc, op_sb, ["tmpS"], ["tmpS"])
                else:
                    TT("dve", acc, acc, op_sb, ALU.add, ["tmpS"], ["tmpS"])

            def step_plain(st_in, st_out, K, q_ap, k_ap, v_ap, vec_f=None, sc=None):
                for k0, ks in state_io(st_in, st_out, K):
                    Sv, sk = load_slice(st_in, k0, ks)
                    T = Tview(ks)
                    TT("dve", T, bc_l(k_ap[:, k0:k0 + ks], 64), v_ap.unsqueeze(1).to_broadcast([64, ks, 64]), ALU.mult,
                       ["f4", "tmpS"], [Tkey])
                    if vec_f is not None:
                        TT("dve", Sv, Sv, bc_l(vec_f[:, k0:k0 + ks], 64), ALU.mult, [sk, "f4"], [sk])
                        TT("dve", Sv, Sv, T, ALU.add, [sk, Tkey], [sk])
                    else:
                        STT(Sv, Sv, sc, T, ALU.mult, ALU.add, [sk, Tkey, "f4", "cst"], [sk])
                    store_slice(st_out, Sv, sk, k0, ks)
                    q_reduce(Sv, sk, q_ap, k0, ks, k0 == 0, o_sb)

            def head_norm(ps_ap, pskey, gcols, ycols):
                ACT(f3[0:n, 256:512], ps_ap, AF.Square, [pskey], ["f3"])
                RED(st4[0:n, 4:8], v3(f3[0:n, 256:512], 4), ["f3"], ["st4"])
                rsqrt_act(st4[0:n, 8:12], st4[0:n, 4:8], 1.0 / 64, ["st4"], ["st4"])
                TT("dve", v3(f3[0:n, 256:512], 4), v3(ps_ap, 4), bc_l(st4[0:n, 8:12], 64), ALU.mult, [pskey, "st4"], ["f3"])
                TT("dve", y_bf[0:n, ycols], f3[0:n, 256:512], hb[1][0:n, gcols], ALU.mult, ["f3", "hb1"], ["y_bf"])

            gs = hb[1]; gsk = "hb1"

            pA0, kA0 = proj_tok(0, 512)
            pA1, kA1 = proj_tok(512, 1024)
            ACT(f1[0:n, 0:256], pA0[0:n, 0:256], AF.Exp, [kA0], ["f1"], scale=-1.0)
            ACT(f1[0:n, 256:512], pA0[0:n, 256:512], AF.Exp, [kA0], ["f1"])
            ACT(f1[0:n, 512:768], pA1[0:n, 256:512], AF.Exp, [kA1], ["f1"], scale=-1.0)
            sigmoid_from_exp(f1[0:n, :], "f1")
            STT(f2[0:n, 0:256], pA0[0:n, 0:256], QK, f1[0:n, 0:256], ALU.mult, ALU.mult, [kA0, "f1"], ["f2"])
            TT("dve", f2[0:n, 256:512], f1[0:n, 256:512], oml[0:n, l, :], ALU.mult, ["f1", "oml"], ["f2"])
            TT("dve", f1[0:n, 512:768], f1[0:n, 512:768], prm[0:n, 0:256], ALU.mult, ["f1", "prm"], ["f1"])
            TT("dve", gs[0:n, 0:256], pA1[0:n, 256:512], f1[0:n, 512:768], ALU.mult, [kA1, "f1"], [gsk])
            CP("dve", f3[0:n, 0:256], pA1[0:n, 0:256], [kA1], ["f3"])
            TS("dve", f1[0:n, 0:256], f2[0:n, 256:512], -1.0, ALU.mult, ["f2"], ["f1"], s2=1.0, op1=ALU.add)
            select([(0, 64, lambda hd: (f2[0:n, hd * 64:hd * 64 + 64], "f2")),
                    (64, 64, lambda hd: (f2[0:n, 256 + hd * 64:256 + hd * 64 + 64], "f2")),
                    (128, 64, lambda hd: (f3[0:n, hd * 64:hd * 64 + 64], "f3")),
                    (192, 64, lambda hd: (f1[0:n, hd * 64:hd * 64 + 64], "f1"))])
            step_plain(si_hg, so_hg, 64, pvs[0:64, 0:64], pvs[0:64, 64:128], pvs[0:64, 128:192], vec_f=pvs[0:64, 192:256])
            po, pok = unselect(o_sb, "tmpS")
            head_norm(po[0:n, 0:256], pok, slice(0, 256), slice(0, 256))

            pD0, kD0 = proj_tok(3084, 3596)
            pD1, kD1 = proj_tok(3596, 4108)
            cosb = rt[0:n, 0:32].unsqueeze(1).to_broadcast([n, 16, 32])
            sinb = rt[0:n, 32:64].unsqueeze(1).to_broadcast([n, 16, 32])
            qk4 = pD0[0:n, :].rearrange("p (a b) -> p a b", a=16)
            TT("dve", f1[0:n, 0:512].rearrange("p (a b) -> p a b", a=16), qk4, cosb, ALU.mult, [kD0, rtk], ["f1"])
            TT("dve", f2[0:n, 0:512].rearrange("p (a b) -> p a b", a=16), qk4, sinb, ALU.mult, [kD0, rtk], ["f2"])
            c4 = f1[0:n, 0:512].rearrange("p (a s b) -> p a s b", a=8, s=2)
            s4 = f2[0:n, 0:512].rearrange("p (a s b) -> p a s b", a=8, s=2)
            r4 = f3[0:n, 0:512].rearrange("p (a s b) -> p a s b", a=8, s=2)
            TT("dve", r4[:, :, 0, :], c4[:, :, 0, :], s4[:, :, 1, :], ALU.subtract, ["f1", "f2"], ["f3"])
            TT("dve", r4[:, :, 1, :], c4[:, :, 1, :], s4[:, :, 0, :], ALU.add, ["f1", "f2"], ["f3"])
            TS("dve", f3[0:n, 256:512], f3[0:n, 256:512], QK, ALU.mult, ["f3"], ["f3"])
            CP("dve", f1[0:n, 0:256], pD1[0:n, 0:256], [kD1], ["f1"])
            ACT(f1[0:n, 512:768], pD1[0:n, 256:512], AF.Exp, [kD1], ["f1"], scale=-1.0)
            sigmoid_from_exp(f1[0:n, 512:768], "f1")
            TT("dve", gs[0:n, 768:1024], pD1[0:n, 256:512], f1[0:n, 512:768], ALU.mult, [kD1, "f1"], [gsk])
            select([(0, 64, lambda hd: (f3[0:n, hd * 64:hd * 64 + 64], "f3")),
                    (64, 64, lambda hd: (f3[0:n, 256 + hd * 64:256 + hd * 64 + 64], "f3")),
                    (128, 64, lambda hd: (f1[0:n, hd * 64:hd * 64 + 64], "f1"))])
            step_plain(si_rt, so_rt, 64, pvs[0:64, 0:64], pvs[0:64, 64:128], pvs[0:64, 128:192], sc=C("gam64", slice(0, 64), 0, 1))
            po, pok = unselect(o_sb, "tmpS")
            oD = po[0:n, 0:256]
            RED(st4[0:n, 4:8], v3(oD, 4), [pok], ["st4"])
            TS("dve", st4[0:n, 4:8], st4[0:n, 4:8], -1.0 / 64, ALU.mult, ["st4"], ["st4"])
            TT("dve", v3(f3[0:n, 0:256], 4), v3(oD, 4), bc_l(st4[0:n, 4:8], 64), ALU.add, [pok, "st4"], ["f3"])
            ACT(f3[0:n, 256:512], f3[0:n, 0:256], AF.Square, ["f3"], ["f3"])
            RED(st4[0:n, 4:8], v3(f3[0:n, 256:512], 4), ["f3"], ["st4"])
            rsqrt_act(st4[0:n, 8:12], st4[0:n, 4:8], 1.0 / 64, ["st4"], ["st4"])
            TT("dve", v3(f3[0:n, 0:256], 4), v3(f3[0:n, 0:256], 4), bc_l(st4[0:n, 8:12], 64), ALU.mult, ["f3", "st4"], ["f3"])
            TT("dve", f3[0:n, 0:256], f3[0:n, 0:256], prm[0:n, 768:1024], ALU.mult, ["f3", "prm"], ["f3"])
            TT("dve", f3[0:n, 0:256], f3[0:n, 0:256], prm[0:n, 1024:1280], ALU.add, ["f3", "prm"], ["f3"])
            TT("dve", y_bf[0:n, 768:1024], f3[0:n, 0:256], gs[0:n, 768:1024], ALU.mult, ["f3", gsk], ["y_bf"])

            pBz, kBz = proj_tok(1792, 2056, (256, 1288))
            ACT(f1[0:n, 512:768], pBz[0:n, 0:256], AF.Exp, [kBz], ["f1"], scale=-1.0)
            ACT(beta[0:n, 0:4], pBz[0:n, 260:264], AF.Exp, [kBz], ["beta"], scale=-1.0)
            ACT(g8[0:n, 0:4], pBz[0:n, 256:260], AF.Exp, [kBz], ["g8"])
            sigmoid_from_exp(f1[0:n, 512:768], "f1")
            TT("dve", f1[0:n, 512:768], f1[0:n, 512:768], prm[0:n, 256:512], ALU.mult, ["f1", "prm"], ["f1"])
            TT("dve", gs[0:n, 256:512], pBz[0:n, 0:256], f1[0:n, 512:768], ALU.mult, [kBz, "f1"], [gsk])
            pCz, kCz = proj_tok(2824, 3084, (256, 1292))
            ACT(f1[0:n, 512:768], pCz[0:n, 0:256], AF.Exp, [kCz], ["f1"], scale=-1.0)
            ACT(g8[0:n, 4:8], pCz[0:n, 256:260], AF.Exp, [kCz], ["g8"])
            sigmoid_from_exp(f1[0:n, 512:768], "f1")
            TT("dve", gs[0:n, 512:768], pCz[0:n, 0:256], f1[0:n, 512:768], ALU.mult, [kCz, "f1"], [gsk])
            ACT(g8[0:n, 0:8], g8[0:n, 0:8], AF.Ln, ["g8"], ["g8"], bias=eps_t[0:n, 1:2])
            CP("dve", dtb[0:n, :], g8[0:n, :], ["g8"], ["dtb"])
            TT("dve", g8[0:n, :], g8[0:n, :], nega[0:n, :], ALU.mult, ["g8", "nega"], ["g8"])
            ACT(egc[0:n, 0:8], g8[0:n, 0:8], AF.Exp, ["g8"], ["egc"])
            sigmoid_from_exp(beta[0:n, :], "beta")

            def conv_tok(cv, groups, st_in, st_out, bias):
                U = hb[0]; Uk = "hb0"
                for (c0, c1, o0) in groups:
                    pu, puk = proj_tok(c0, c1)
                    CP("dve", U[0:n, o0:o0 + (c1 - c0)], pu[0:n, 0:c1 - c0], [puk], [Uk])
                DMA(st_out[l, :, 2, :], U[0:n, 0:768], [Uk], [])
                Wt = eGbc[0:n, :, :].rearrange("p a b -> p (a b)")[:, 0:768]
                Ct = tt[0:n, :, :].rearrange("p a b -> p (a b)")[:, 0:768]
                DMA(Wt, cwrow_d[l, cv, 3], [], ["eGbc"])
                TT("dve", f1[0:n, 0:768], U[0:n, 0:768], Wt, ALU.mult, [Uk, "eGbc"], ["f1"])
                for w in range(3):
                    DMA(Ct, st_in[l, :, w, :], [], ["tt"])
                    DMA(Wt, cwrow_d[l, cv, w], [], ["eGbc"])
                    if w >= 1:
                        DMA(st_out[l, :, w - 1, :], Ct, ["tt"], [])
                    TT("dve", Wt, Ct, Wt, ALU.mult, ["tt", "eGbc"], ["eGbc"])
                    TT("dve", f1[0:n, 0:768], f1[0:n, 0:768], Wt, ALU.add, ["f1", "eGbc"], ["f1"])
                if bias:
                    DMA(Ct, cbrow_d[l], [], ["tt"])
                    TT("dve", f1[0:n, 0:768], f1[0:n, 0:768], Ct, ALU.add, ["f1", "tt"], ["f1"])
                ACT(Ct, f1[0:n, 0:768], AF.Exp, ["f1"], ["tt"], scale=-1.0)
                sigmoid_from_exp(Ct, "tt")
                TT("dve", f1[0:n, 0:768], f1[0:n, 0:768], Ct, ALU.mult, ["f1", "tt"], ["f1"])

            conv_tok(0, [(1024, 1536, 0), (1536, 1792, 512)], si_gc, so_gc, False)
            Ct = tt[0:n, :, :].rearrange("p a b -> p (a b)")[:, 0:512]
            ACT(Ct, f1[0:n, 0:512], AF.Square, ["f1"], ["tt"])
            RED(nb[0:n, 0:8], v3(Ct, 8), ["tt"], ["nb"])
            ACT(nb[0:n, 8:16], nb[0:n, 0:8], AF.Ln, ["nb"], ["nb"], bias=eps_t[0:n, 0:1])
            ACT(nb[0:n, 8:16], nb[0:n, 8:16], AF.Exp, ["nb"], ["nb"], scale=-0.5)
            TT("dve", v3(f1[0:n, 0:512], 8), v3(f1[0:n, 0:512], 8), bc_l(nb[0:n, 8:16], 64), ALU.mult, ["f1", "nb"], ["f1"])
            TS("dve", f1[0:n, 0:256], f1[0:n, 0:256], QK, ALU.mult, ["f1"], ["f1"])
            select([(0, 64, lambda hd: (f1[0:n, hd * 64:hd * 64 + 64], "f1")),
                    (64, 64, lambda hd: (f1[0:n, 256 + hd * 64:256 + hd * 64 + 64], "f1")),
                    (128, 64, lambda hd: (f1[0:n, 512 + hd * 64:512 + hd * 64 + 64], "f1")),
                    (192, 1, lambda hd: (egc[0:n, hd:hd + 1], "egc")),
                    (193, 1, lambda hd: (beta[0:n, hd:hd + 1], "beta"))])
            qB, kB, vB = pvs[0:64, 0:64], pvs[0:64, 64:128], pvs[0:64, 128:192]
            egB, btB = pvs[0:64, 192:193], pvs[0:64, 193:194]
            for k0, ks in state_io(si_gd, so_gd, 64):
                Sv, sk = load_slice(si_gd, k0, ks)
                T = Tview(ks)
                TT("dve", T, Sv, bc_l(kB[:, k0:k0 + ks], 64), ALU.mult, [sk, "f4"], [Tkey])
                RED(op_sb, T.rearrange("p k v -> p v k"), [Tkey], ["tmpS"])
                if k0 == 0:
                    CP("dve", w_sb, op_sb, ["tmpS"], ["tmpS"])
                else:
                    TT("dve", w_sb, w_sb, op_sb, ALU.add, ["tmpS"], ["tmpS"])
            TS("dve", w_sb, w_sb, egB, ALU.mult, ["tmpS", "f4"], ["tmpS"])
            TT("dve", u_sb, vB, w_sb, ALU.subtract, ["f4", "tmpS"], ["tmpS"])
            TS("dve", u_sb, u_sb, btB, ALU.mult, ["tmpS", "f4"], ["tmpS"])
            step_plain(si_gd, so_gd, 64, qB, kB, u_sb, sc=egB)
            po, pok = unselect(o_sb, "tmpS")
            head_norm(po[0:n, 0:256], pok, slice(256, 512), slice(256, 512))

            conv_tok(1, [(2056, 2568, 0), (2568, 2824, 512)], si_sc, so_sc, True)
            TT("dve", v3(f2[0:n, 0:256], 4), v3(f1[0:n, 0:256], 4), bc_l(dtb[0:n, 4:8], 64), ALU.mult, ["f1", "dtb"], ["f2"])
            select([(0, 128, lambda hd: (f1[0:n, 512 + (hd // 2) * 128:512 + (hd // 2) * 128 + 128], "f1")),
                    (128, 128, lambda hd: (f1[0:n, 256 + (hd // 2) * 128:256 + (hd // 2) * 128 + 128], "f1")),
                    (256, 64, lambda hd: (f2[0:n, hd * 64:hd * 64 + 64], "f2")),
                    (320, 1, lambda hd: (egc[0:n, 4 + hd:5 + hd], "egc"))])
            step_plain(si_sd, so_sd, 128, pvs[0:64, 0:128], pvs[0:64, 128:256], pvs[0:64, 256:320], sc=pvs[0:64, 320:321])
            po, pok = unselect(o_sb, "tmpS")
            TT("dve", v3(f3[0:n, 0:256], 4), v3(f1[0:n, 0:256], 4), bc_l(prm[0:n, 1296:1300], 64), ALU.mult, ["f1", "prm"], ["f3"])
            TT("dve", f3[0:n, 0:256], f3[0:n, 0:256], po[0:n, 0:256], ALU.add, ["f3", pok], ["f3"])
            TT("dve", f3[0:n, 0:256], f3[0:n, 0:256], gs[0:n, 512:768], ALU.mult, ["f3", gsk], ["f3"])
            ACT(f3[0:n, 256:512], f3[0:n, 0:256], AF.Square, ["f3"], ["f3"])
            RED(st4[0:n, 4:6], v3(f3[0:n, 256:512], 2), ["f3"], ["st4"])
            rsqrt_act(st4[0:n, 8:10], st4[0:n, 4:6], 1.0 / 128, ["st4"], ["st4"])
            TT("dve", v3(f3[0:n, 0:256], 2), v3(f3[0:n, 0:256], 2), bc_l(st4[0:n, 8:10], 128), ALU.mult, ["f3", "st4"], ["f3"])
            TT("dve", y_bf[0:n, 512:768], f3[0:n, 0:256], prm[0:n, 512:768], ALU.mult, ["f3", "prm"], ["y_bf"])

            for half in range(2):
                pt, pk = bank()
                for kk in range(4):
                    kc = half * 4 + kk
                    MM(pt[:, kk * 128:kk * 128 + n], y_bf[0:n, kc * 128:(kc + 1) * 128], ident_bf[0:n, 0:n],
                       ["y_bf", "ident_bf"], [pk])
                CP("act", yT[:, half * 4:half * 4 + 4, 0:n], v3(pt[:, :], 4)[:, :, 0:n], [pk], ["yT"])
            for cg in range(2):
                pt, pk = bank()
                for kc in range(8):
                    MM(pt[0:n, :], yT[:, kc, 0:n], wout[:, kc, cg * 512:(cg + 1) * 512], ["yT", "wout"], [pk],
                       start=(kc == 0), stop=(kc == 7))
                TT("dve", hs[:, cg * 512:(cg + 1) * 512], hs[:, cg * 512:(cg + 1) * 512], pt[0:n, :], ALU.add, ["hs", pk], ["hs"])
            if last:
                ACT(hn_bf[0:n, :], hs[:, :], AF.Square, ["hs"], ["hn_bf", "st4"], accum=st4[0:n, 0:1])
                rsqrt_act(st4[0:n, 1:2], st4[0:n, 0:1], 1.0 / D, ["st4"], ["st4"])
                STT(hs[:, :], hs[:, :], st4[0:n, 1:2], finw[0:n, :], ALU.mult, ALU.mult, ["hs", "st4", "lgt"], ["hs"])
                DMA(y_s, hs[:, :], ["hs"], [])

        for l in range(depth):
            load_layer(l)
            if not _os0.environ.get("NO_SAMPLE"):
                sample_fwd(l, l == depth - 1)
            if ntiles > 0:
                stage0(l, 0)
            for t in range(ntiles):
                tile_fwd(l, t, (lambda l_=l, t_=t: stage0(l_, t_ + 1)) if t + 1 < ntiles else None)

        if _AUDIT:
            for b_ in sorted(_bad, key=str):
                print("AUDIT missing key:", b_)
        P.emit(es)
    return nc


_NC_CACHE = {}


def _prep_inputs(inp, c):
    f = np.float32
    prm = np.zeros((DEPTH, 128, NPRM), f)
    for l in range(DEPTH):
        row = np.concatenate([inp["hgrn_norm_w"][l], inp["gdn_norm_w"][l], inp["ssd_norm_w"][l], inp["ret_norm_w"][l],
                              inp["ret_norm_b"][l], inp["gdn_a_log"][l], inp["ssd_a_log"][l], inp["gdn_dt_bias"][l],
                              inp["ssd_dt_bias"][l], inp["ssd_d"][l]]).astype(f)
        prm[l] = np.broadcast_to(row[None, :], (128, NPRM))
    lgt = np.ascontiguousarray(np.broadcast_to(inp["hgrn_lb_logits"].reshape(1, 1024), (128, 1024))).astype(f)
    finw = np.ascontiguousarray(np.broadcast_to(inp["final_norm_w"].reshape(1, 1024), (128, 1024))).astype(f)
    featp = np.zeros((128, 32 + 192), f)
    featp[:, 0:32] = inp["norm_w"].reshape(DEPTH, 8, 128).transpose(2, 0, 1).reshape(128, 32)
    for l in range(DEPTH):
        for cv, key in enumerate(("gdn_conv_w", "ssd_conv_w")):
            w = inp[key][l].reshape(4, 6, 128)
            featp[:, 32 + l * 48 + cv * 24:32 + l * 48 + cv * 24 + 24] = w.transpose(2, 1, 0).reshape(128, 24)
    cbias = np.ascontiguousarray(inp["ssd_conv_b"].reshape(1, DEPTH * 768)).astype(f)
    cw = np.stack([inp["gdn_conv_w"], inp["ssd_conv_w"]], 1).astype(f)
    cwrow = np.ascontiguousarray(np.broadcast_to(cw[:, :, :, None, :], (DEPTH, 2, 4, NS, 768)))
    cbrow = np.ascontiguousarray(np.broadcast_to(inp["ssd_conv_b"].astype(f)[:, None, :], (DEPTH, NS, 768)))
    return {
        "xp": np.ascontiguousarray(inp["x_prompt"][c]).astype(f),
        "meta": np.ascontiguousarray(inp["meta_tokens"]).astype(f),
        "w_in": np.ascontiguousarray(inp["w_in"]).astype(f),
        "w_out": np.ascontiguousarray(inp["w_out"]).astype(f),
        "cst": CST, "rot": ROT, "prm": prm, "lgt": lgt, "finw": finw, "featp": featp, "cbias": cbias,
        "cwrow": cwrow, "cbrow": cbrow, **_sample_inputs(inp, c),
    }


def _sample_inputs(inp, c):
    f = np.float32
    sl = slice(c * NS, (c + 1) * NS)
    return {
        "xs_in": np.ascontiguousarray(inp["x_sample"][sl, 0, :]).astype(f),
        "si_hg": np.ascontiguousarray(inp["state_hgrn"][:, sl]).astype(f),
        "si_gd": np.ascontiguousarray(inp["state_gdn"][:, sl]).astype(f),
        "si_gc": np.ascontiguousarray(inp["state_gdn_conv"][:, sl]).astype(f),
        "si_sd": np.ascontiguousarray(inp["state_ssd"][:, sl]).astype(f),
        "si_sc": np.ascontiguousarray(inp["state_ssd_conv"][:, sl]).astype(f),
        "si_rt": np.ascontiguousarray(inp["state_ret"][:, sl]).astype(f),
    }


def kernel(**inp):
    inp = {k: np.asarray(v) for k, v in inp.items()}
    if "nc" not in _NC_CACHE:
        _NC_CACHE["nc"] = build_program()
    nc = _NC_CACHE["nc"]
    shared = None
    in_maps = []
    for c in range(8):
        m = _prep_inputs(inp, c) if shared is None else dict(shared)
        if shared is None:
            shared = m
        else:
            m["xp"] = np.ascontiguousarray(inp["x_prompt"][c]).astype(np.float32)
            m.update(_sample_inputs(inp, c))
        in_maps.append(m)
    res = run_bass_kernel_spmd(nc, in_maps, core_ids=list(range(8)))
    R = res.results
    y_prompt = np.stack([R[c]["y_p"] for c in range(8)], 0)
    def stk(name):
        return np.ascontiguousarray(np.stack([R[c][name] for c in range(8)], 1))
    y_sample = np.concatenate([R[c]["y_s"] for c in range(8)], 0)[:, None, :]
    def cat(name):
        return np.ascontiguousarray(np.concatenate([R[c][name] for c in range(8)], 1))
    outs = (y_prompt, np.ascontiguousarray(y_sample),
            stk("st_hg"), stk("st_gd"), stk("st_gc"), stk("st_sd"), stk("st_sc"), stk("st_rt"),
            cat("so_hg"), cat("so_gd"), cat("so_gc"), cat("so_sd"), cat("so_sc"), cat("so_rt"))
    return outs
```

```python
import contextlib
import math
import numpy as np
import concourse.bass as bass
import concourse.mybir as mybir
from concourse.bass_utils import run_bass_kernel_spmd

F32 = mybir.dt.float32
BF16 = mybir.dt.bfloat16
AF = mybir.ActivationFunctionType
ALU = mybir.AluOpType
AX = mybir.AxisListType

D = 1024
DEPTH = 4
SEQ = 2048
NT = 17
IN_DIM = 4108
EPS = 1e-6
QK = 0.125
NS = 16
NPRM = 1300
NEGV = -30000.0
import os as _os0
EMBED_WAIT = not _os0.environ.get("NO_EMBED")
ANNOTATE = bool(_os0.environ.get("ANNOTATE"))


class Prog:
    ENG = ("pe", "act", "dve", "pool", "sp")

    def __init__(self, nc, n_dma_sems=8):
        self.nc = nc
        self.ops = []
        self.cnt = {}
        self.clock = {e: {} for e in self.ENG}
        self.tok_clock = {}
        self.last_w = {}
        self.readers = {}
        self.n_dma = n_dma_sems
        self.dma_rr = {e: 0 for e in self.ENG}
        self.dma_last = {}
        self.anns = []
        import os
        self.pe_skip = not os.environ.get("PE_SELFWAIT")
        self.strict_same = not os.environ.get("RELAX_SAME")

    def _need(self, eng, tok, waits, force=False):
        key, idx = tok
        if key == "pe" and eng == "pe" and self.pe_skip and not force:
            return
        if self.clock[eng].get(key, 0) >= idx:
            return
        if waits.get(key, 0) < idx:
            waits[key] = idx

    def op(self, eng, fn, reads=(), writes=(), dma=False, pe_serial=False):
        waits = {}
        for b in reads:
            t = self.last_w.get(b)
            if t:
                self._need(eng, t, waits)
            if b.startswith("ps"):
                for r in self.readers.get(b, ()):
                    if r[0] != eng:
                        self._need(eng, r, waits)
        for b in writes:
            t = self.last_w.get(b)
            if t and (t[0] != eng or pe_serial or dma or self.strict_same):
                self._need(eng, t, waits, force=pe_serial)
            for r in self.readers.get(b, ()):
                if r[0] != eng or dma or self.strict_same:
                    self._need(eng, r, waits)
        if dma:
            key = ("dma", eng, self.dma_rr[eng] % self.n_dma)
            self.dma_rr[eng] += 1
            prev = self.dma_last.get(key)
            if prev:
                self._need(eng, prev, waits)
        else:
            key = eng
        ck = self.clock[eng]
        for kk, ii in waits.items():
            for k2, i2 in self.tok_clock.get((kk, ii), {}).items():
                if ck.get(k2, 0) < i2:
                    ck[k2] = i2
            if ck.get(kk, 0) < ii:
                ck[kk] = ii
        self.cnt[key] = self.cnt.get(key, 0) + 1
        tok = (key, self.cnt[key])
        if dma:
            self.dma_last[key] = tok
            snap = dict(ck)
            snap[key] = tok[1]
            self.tok_clock[tok] = snap
        else:
            snap = dict(ck)
            snap[key] = tok[1]
            self.tok_clock[tok] = snap
        for b in writes:
            self.last_w[b] = tok
            self.readers[b] = []
        for b in reads:
            if b not in writes:
                self.readers.setdefault(b, []).append(tok)
        ann = None
        if ANNOTATE:
            import sys as _sys
            f = _sys._getframe(1)
            while f:
                if f.f_code.co_name in ("tile_fwd", "sample_fwd", "load_layer"):
                    ann = "L%d" % f.f_lineno
                    break
                f = f.f_back
        self.anns.append(ann)
        self.ops.append((eng, fn, list(waits.items()), tok, dma))
        return tok

    def emit(self, es, final_wait_eng="sp"):
        nc = self.nc
        import os
        km = int(os.environ.get("KMAX", "0"))
        if km:
            self.ops = self.ops[:km]
            self.dma_last = {}
            for (e_, f_, w_, tok_, d_) in self.ops:
                if d_:
                    self.dma_last[tok_[0]] = tok_
        needed = set()
        for (_, _, waits, _, _) in self.ops:
            for w in waits:
                needed.add(w)
        finals = []
        for k, t in self.dma_last.items():
            finals.append(t)
            needed.add(t)
        per_key = {}
        for (k, i) in needed:
            per_key.setdefault(k, []).append(i)
        sigcount = {}
        for k, lst in per_key.items():
            for n, i in enumerate(sorted(lst)):
                sigcount[(k, i)] = n + 1
        sems = {}
        for k in sorted(per_key.keys(), key=str):
            nm = "s_" + "_".join(str(x) for x in (k if isinstance(k, tuple) else (k,)))
            sems[k] = es.enter_context(nc.semaphore(nm))
        per_eng = {e: [] for e in self.ENG}
        for j, o in enumerate(self.ops):
            per_eng[o[0]].append(o + (self.anns[j] if j < len(self.anns) else None,))
        blk = es.enter_context(nc.Block())

        def run(e, engobj):
            for (_, fn, waits, tok, dma, ann) in per_eng[e]:
                emb = None
                if waits and EMBED_WAIT and not dma:
                    emb = waits[-1]
                    waits = waits[:-1]
                for (k, i) in waits:
                    mult = 16 if isinstance(k, tuple) else 1
                    engobj.wait_ge(sems[k], sigcount[(k, i)] * mult)
                ins = fn(engobj)
                if ann is not None:
                    ins.annotate(ann)
                if emb is not None:
                    k, i = emb
                    ins._wait_ge(sems[k], sigcount[(k, i)] * (16 if isinstance(k, tuple) else 1))
                if tok in sigcount:
                    ins.then_inc(sems[tok[0]], 16 if dma else 1)
            if e == final_wait_eng:
                for t in finals:
                    engobj.wait_ge(sems[t[0]], sigcount[t] * 16)

        @blk.tensor
        def _(e):
            run("pe", e)

        @blk.scalar
        def _(e):
            run("act", e)

        @blk.vector
        def _(e):
            run("dve", e)

        @blk.gpsimd
        def _(e):
            run("pool", e)

        @blk.sync
        def _(e):
            run("sp", e)


def host_consts():
    idx = np.arange(128)
    ch = idx // 64
    same = ch[:, None] == ch[None, :]
    ident = np.eye(128, dtype=np.float32)
    maskT = (same & (idx[:, None] <= idx[None, :])).astype(np.float32)
    negT = np.where(maskT > 0, 0.0, NEGV).astype(np.float32)
    strict = (same & (idx[None, :] < idx[:, None]))
    negS = np.where(strict, 0.0, NEGV).astype(np.float32)
    mid = ch * 64 + 31
    uprime = (same & (idx[:, None] <= idx[None, :])).astype(np.float32) - \
             (same & (idx[:, None] <= mid[None, :])).astype(np.float32)
    urev = (same & (idx[:, None] > idx[None, :])).astype(np.float32)
    wc = np.zeros((128, 8), np.float32)
    wc[:, 0] = (idx <= 31)
    wc[:, 1] = (idx >= 64) & (idx <= 95)
    wc[:, 2] = (idx >= 32) & (idx <= 63)
    wc[:, 3] = (idx >= 96)
    wc[:, 4] = (idx <= 63)
    wc[:, 5] = (idx >= 64)
    blockones = same.astype(np.float32)
    lg = np.log1p(-np.exp2(-5.0 - np.arange(4, dtype=np.float64)))
    loc = idx % 64
    dt_ret = np.zeros((128, 4, 128), np.float64)
    for h in range(4):
        dt_ret[:, h, :] = np.where(maskT > 0, np.exp(lg[h] * (idx[None, :] - idx[:, None])), 0.0) * QK
    egq = np.zeros((128, 2, 128), np.float64)
    for hp in range(2):
        for hh in range(2):
            egq[hh * 64:(hh + 1) * 64, hp, :] = np.exp(lg[2 * hp + hh] * (loc[None, :] + 1))
    egrev64 = np.zeros((128, 4), np.float64)
    egrev16 = np.zeros((128, 4), np.float64)
    for h in range(4):
        egrev64[:, h] = np.exp(lg[h] * (63 - loc)) * QK
        egrev16[:, h] = np.exp(lg[h] * np.maximum(15 - idx, 0)) * QK
    egl = np.zeros((128, 2, 2), np.float64)
    for hp in range(2):
        for hh in range(2):
            egl[hh * 64:(hh + 1) * 64, hp, 0] = np.exp(lg[2 * hp + hh] * 16)
            egl[hh * 64:(hh + 1) * 64, hp, 1] = np.exp(lg[2 * hp + hh] * 64)
    sel = np.zeros((128, 4, 64), np.float32)
    selT = np.zeros((128, 4, 16), np.float32)
    gam64 = np.zeros((128, 4), np.float32)
    for h in range(4):
        for b in range(16):
            sel[b, h, h * 16 + b] = 1.0
            selT[h * 16 + b, h, b] = 1.0
            gam64[h * 16 + b, 0] = np.exp(lg[h])
    parts = [ident, maskT, negT, negS, uprime, urev, wc, blockones,
             dt_ret.reshape(128, 512), egq.reshape(128, 256), egrev64, egrev16, egl.reshape(128, 4),
             sel.reshape(128, 256), selT.reshape(128, 64), gam64]
    offs = {}
    names = ["ident", "maskT", "negT", "negS", "uprime", "urev", "wc", "blockones",
             "dt_ret", "egq", "egrev64", "egrev16", "egl", "sel", "selT", "gam64"]
    o = 0
    for nm, p in zip(names, parts):
        offs[nm] = (o, p.shape[1])
        o += p.shape[1]
    cst = np.concatenate([p.astype(np.float32) for p in parts], axis=1)
    half = 32
    inv_freq = (1.0 / (np.float32(10000.0) ** np.linspace(0.0, 1.0, half, dtype=np.float32))).astype(np.float32)
    rot = np.zeros((NT + 1, 128, 64), np.float32)
    for t in range(NT):
        pos = (np.arange(128) if t == 0 else 16 + (t - 1) * 128 + np.arange(128)).astype(np.float32)
        ang = (pos[:, None] * inv_freq[None, :]).astype(np.float32)
        rot[t, :, 0:32] = np.cos(ang)
        rot[t, :, 32:64] = np.sin(ang)
    ang = (np.full((128, 1), 16384.0, np.float32) * inv_freq[None, :]).astype(np.float32)
    rot[NT, :, 0:32] = np.cos(ang)
    rot[NT, :, 32:64] = np.sin(ang)
    return cst, offs, rot


CST, COFF, ROT = host_consts()
NCST = CST.shape[1]


def build_program(depth=DEPTH, ntiles=NT):
    nc = bass.Bass("TRN2", target_bir_lowering=False)

    def din(name, shape):
        return nc.dram_tensor(name, list(shape), F32, kind="ExternalInput").ap()

    def dout(name, shape):
        return nc.dram_tensor(name, list(shape), F32, kind="ExternalOutput").ap()

    xp = din("xp", [SEQ, D])
    meta = din("meta", [16, D])
    w_in = din("w_in", [DEPTH, D, IN_DIM])
    w_out = din("w_out", [DEPTH, D, D])
    cst_d = din("cst", [128, NCST])
    rot_d = din("rot", [NT + 1, 128, 64])
    prm_d = din("prm", [DEPTH, 128, NPRM])
    lgt_d = din("lgt", [128, 1024])
    finw_d = din("finw", [128, 1024])
    featp_d = din("featp", [128, 32 + 192])
    cbias_d = din("cbias", [1, DEPTH * 768])

    y_p = dout("y_p", [SEQ, D])
    st_hg = dout("st_hg", [DEPTH, 4, 64, 64])
    st_gd = dout("st_gd", [DEPTH, 4, 64, 64])
    st_gc = dout("st_gc", [DEPTH, 3, 768])
    st_sd = dout("st_sd", [DEPTH, 4, 128, 64])
    st_sc = dout("st_sc", [DEPTH, 3, 768])
    st_rt = dout("st_rt", [DEPTH, 4, 64, 64])
    xs_d = din("xs_in", [NS, D])
    si_hg = din("si_hg", [DEPTH, NS, 4, 64, 64]); si_gd = din("si_gd", [DEPTH, NS, 4, 64, 64])
    si_gc = din("si_gc", [DEPTH, NS, 3, 768]); si_sd = din("si_sd", [DEPTH, NS, 4, 128, 64])
    si_sc = din("si_sc", [DEPTH, NS, 3, 768]); si_rt = din("si_rt", [DEPTH, NS, 4, 64, 64])
    cwrow_d = din("cwrow", [DEPTH, 2, 4, NS, 768])
    cbrow_d = din("cbrow", [DEPTH, NS, 768])
    y_s = dout("y_s", [NS, D])
    so_hg = dout("so_hg", [DEPTH, NS, 4, 64, 64]); so_gd = dout("so_gd", [DEPTH, NS, 4, 64, 64])
    so_gc = dout("so_gc", [DEPTH, NS, 3, 768]); so_sd = dout("so_sd", [DEPTH, NS, 4, 128, 64])
    so_sc = dout("so_sc", [DEPTH, NS, 3, 768]); so_rt = dout("so_rt", [DEPTH, NS, 4, 64, 64])

    with contextlib.ExitStack() as es:
        def sb(name, shape, dt=F32):
            return es.enter_context(nc.sbuf_tensor("sb_" + name, list(shape), dt))

        P = Prog(nc)

        hscr = nc.dram_tensor("hscr", [NT * 128, D], F32, kind="Internal").ap()
        hb = [sb(f"hb{i}", [128, D]) for i in range(2)]
        win = sb("win", [128, 8, IN_DIM], BF16)
        wout = sb("wout", [128, 8, D], BF16)
        WCH = 1027
        wst = [sb(f"wst{i}", [128, WCH]) for i in range(2)]
        cst = sb("cst", [128, NCST])
        ident_bf = sb("ident_bf", [128, 128], BF16)
        bones_bf = sb("bones_bf", [128, 128], BF16)
        dtret_bf = sb("dtret_bf", [128, 4, 128], BF16)
        ones_bf = sb("ones_bf", [1, 128], BF16)
        prm = sb("prm", [128, NPRM])
        oml = sb("oml", [128, 4, 256])
        lgt = sb("lgt", [128, 4, 256])
        featp = sb("featp", [128, 32 + 192])
        dg = sb("dg", [128, 12, 4, 128], BF16)
        cbias_bf = sb("cbias_bf", [1, 768], BF16)
        nega = sb("nega", [128, 64])
        rot = [sb(f"rot{i}", [128, 64]) for i in range(2)]

        def C(name, rows=slice(0, 128), lo=0, hi=None):
            o, w = COFF[name]
            hi = w if hi is None else hi
            return cst[rows, o + lo:o + hi]

        psb = [es.enter_context(nc.psum_tensor(f"ps{i}", [128, 512], F32)) for i in range(8)]
        ps_rr = [0]

        def bank():
            i = ps_rr[0] % 4 if ps_rr[0] < 0 else (0, 1, 2, 3, 6, 7)[ps_rr[0] % 6]
            ps_rr[0] += 1
            return psb[i], f"ps{i}"

        import os as _os
        _AUDIT = bool(_os.environ.get("AUDIT"))
        _bad = set()

        def _chk(r, w, outs, ins):
            if not _AUDIT:
                return
            for grp, keys, what in ((outs, list(w), "W"), (ins, list(r) + list(w), "R")):
                for ap in grp:
                    nm = getattr(ap, "name", None)
                    if not isinstance(nm, str):
                        continue
                    key = nm[3:] if nm.startswith("sb_") else nm
                    if key not in keys:
                        import traceback
                        fr = traceback.extract_stack()[-3]
                        _bad.add((what, key, fr.lineno))

        _last_rb = {}

        def MM(out, lhsT, rhs, r, w, start=True, stop=True):
            skip = any(k in ("ps4", "ps5") for k in w)
            _chk(r, w, [out], [lhsT, rhs])
            rb = lhsT.base_partition()
            ser = False
            for k in w:
                if _last_rb.get(k, rb) != rb:
                    ser = True
                _last_rb[k] = rb
            P.op("pe", lambda e: e.matmul(out, lhsT=lhsT, rhs=rhs, start=start, stop=stop, skip_group_check=skip),
                 reads=r, writes=w, pe_serial=ser)

        def ACT(out, in_, func, r, w, scale=1.0, bias=None, accum=None):
            kw = {}
            if bias is not None:
                kw["bias"] = bias
            if accum is not None:
                kw["accum_out"] = accum
            if hasattr(bias, "name") and "eps_t" not in r:
                r = list(r) + ["eps_t"]
            _chk(r, w, [out] + ([accum] if accum is not None else []), [in_] + [x for x in (scale, bias) if hasattr(x, "name")])
            P.op("act", lambda e: e.activation(out=out, in_=in_, func=func, scale=scale, **kw), reads=r, writes=w)

        def TT(eng, out, in0, in1, op, r, w):
            _chk(r, w, [out], [in0, in1])
            P.op(eng, lambda e: e.tensor_tensor(out=out, in0=in0, in1=in1, op=op), reads=r, writes=w)

        def TS(eng, out, in0, s1, op0, r, w, s2=None, op1=None):
            _chk(r, w, [out], [in0] + [x for x in (s1, s2) if hasattr(x, "name")])
            if op1 is None:
                P.op(eng, lambda e: e.tensor_scalar(out=out, in0=in0, scalar1=s1, scalar2=None, op0=op0), reads=r, writes=w)
            else:
                P.op(eng, lambda e: e.tensor_scalar(out=out, in0=in0, scalar1=s1, scalar2=s2, op0=op0, op1=op1), reads=r, writes=w)

        def STT(out, in0, scalar, in1, op0, op1, r, w):
            _chk(r, w, [out], [in0, in1] + [x for x in (scalar,) if hasattr(x, "name")])
            P.op("dve", lambda e: e.scalar_tensor_tensor(out=out, in0=in0, scalar=scalar, in1=in1, op0=op0, op1=op1),
                 reads=r, writes=w)

        def RED(out, in_, r, w):
            _chk(r, w, [out], [in_])
            P.op("dve", lambda e: e.tensor_reduce(out=out, in_=in_, axis=AX.X, op=ALU.add), reads=r, writes=w)

        def RECIP(out, in_, r, w):
            _chk(r, w, [out], [in_])
            P.op("dve", lambda e: e.reciprocal(out=out, in_=in_), reads=r, writes=w)

        def CP(eng, out, in_, r, w):
            if eng == "act":
                ACT(out, in_, AF.Copy, r, w)
            else:
                _chk(r, w, [out], [in_])
                P.op(eng, lambda e: e.tensor_copy(out=out, in_=in_), reads=r, writes=w)

        def MEMSET(eng, ap, val, w):
            P.op(eng, lambda e: e.memset(ap, val), reads=(), writes=w)

        def DMA(out, in_, r, w, slow=False):
            if slow:
                P.op("sp", lambda e: e.dma_start(out=out, in_=in_, allow_slow_non_contiguous=True), reads=r, writes=w, dma=True)
            else:
                P.op("sp", lambda e: e.dma_start(out=out, in_=in_), reads=r, writes=w, dma=True)

        def sigmoid_from_exp(buf, key):
            TS("dve", buf, buf, 1.0, ALU.add, [key], [key])
            RECIP(buf, buf, [key], [key])

        def rsqrt_act(out, in_, scale, r, w):
            ACT(out, in_, AF.Ln, r, w, scale=scale, bias=eps_t[0:out.shape[0], 0:1])
            ACT(out, out, AF.Exp, w, w, scale=-0.5)

        eps_t = sb("eps_t", [128, 2])
        MEMSET("pool", eps_t[:, 0:1], EPS, ["eps_t"])
        MEMSET("pool", eps_t[:, 1:2], 1.0, ["eps_t"])
        DMA(cst[:], cst_d, [], ["cst"])
        DMA(lgt[:].rearrange("p a b -> p (a b)"), lgt_d, [], ["lgt"])
        DMA(featp[:], featp_d, [], ["featp"])
        CP("dve", ident_bf[:], C("ident"), ["cst"], ["ident_bf"])
        CP("dve", bones_bf[:], C("blockones"), ["cst"], ["bones_bf"])
        CP("dve", dtret_bf[:].rearrange("p a b -> p (a b)"), C("dt_ret"), ["cst"], ["dtret_bf"])
        MEMSET("pool", ones_bf[:], 1.0, ["ones_bf"])
        mx = wst[1][:, 0:256]
        TT("dve", mx, lgt[:, 0, :], lgt[:, 1, :], ALU.max, ["lgt"], ["wst1"])
        TT("dve", mx, mx, lgt[:, 2, :], ALU.max, ["lgt", "wst1"], ["wst1"])
        TT("dve", mx, mx, lgt[:, 3, :], ALU.max, ["lgt", "wst1"], ["wst1"])
        TT("dve", lgt[:], lgt[:], mx.unsqueeze(1).to_broadcast([128, 4, 256]), ALU.subtract, ["lgt", "wst1"], ["lgt"])
        ACT(lgt[:], lgt[:], AF.Exp, ["lgt"], ["lgt"])
        TT("dve", mx, lgt[:, 0, :], lgt[:, 1, :], ALU.add, ["lgt"], ["wst1"])
        TT("dve", mx, mx, lgt[:, 2, :], ALU.add, ["lgt", "wst1"], ["wst1"])
        TT("dve", mx, mx, lgt[:, 3, :], ALU.add, ["lgt", "wst1"], ["wst1"])
        RECIP(mx, mx, ["wst1"], ["wst1"])
        TT("dve", lgt[:], lgt[:], mx.unsqueeze(1).to_broadcast([128, 4, 256]), ALU.mult, ["lgt", "wst1"], ["lgt"])
        MEMSET("dve", oml[:, 0, :], 0.0, ["oml"])
        CP("dve", oml[:, 1, :], lgt[:, 1, :], ["lgt"], ["oml"])
        TT("dve", oml[:, 2, :], oml[:, 1, :], lgt[:, 2, :], ALU.add, ["lgt", "oml"], ["oml"])
        TT("dve", oml[:, 3, :], oml[:, 2, :], lgt[:, 3, :], ALU.add, ["lgt", "oml"], ["oml"])
        TS("dve", oml[:], oml[:], 0.0, ALU.max, ["oml"], ["oml"])
        TS("dve", oml[:], oml[:], -1.0, ALU.mult, ["oml"], ["oml"], s2=1.0, op1=ALU.add)
        DMA(lgt[:].rearrange("p a b -> p (a b)"), finw_d, ["lgt"], ["lgt"])
        finw = lgt[:].rearrange("p a b -> p (a b)")

        hn_bf = sb("hn_bf", [128, D], BF16)
        hnTs = [sb(f"hnT{i}", [128, 8, 128], BF16) for i in range(2)]
        hnT = hnTs[1]
        st0 = sb("st0", [128, 2])
        st4 = sb("st4", [128, 16])
        f1 = sb("f1", [128, 768])
        f2 = sb("f2", [128, 512])
        f3 = sb("f3", [128, 512])
        f4 = sb("f4", [128, 512])
        gate = sb("gate", [128, D])
        y_bf = sb("y_bf", [128, D], BF16)
        yT = sb("yT", [128, 8, 128], BF16)
        b1 = sb("b1", [128, 4, 128], BF16)
        qkT = sb("qkT", [128, 4, 128], BF16)
        AT = sb("AT", [128, 4, 128], BF16)
        AT2 = sb("AT2", [128, 4, 128], BF16)
        v_bf = sb("v_bf", [128, 4, 64], BF16)
        kp_bf = sb("kp_bf", [128, 4, 128], BF16)
        qpT = sb("qpT", [128, 4, 128], BF16)
        ecs = sb("ecs", [128, 2, 8])
        S_A = sb("S_A", [128, 2, 64]); Sb_A = sb("Sb_A", [128, 2, 64], BF16); Sd_A = sb("Sd_A", [128, 2, 64])
        S_B = sb("S_B", [128, 2, 64]); Sb_B = sb("Sb_B", [128, 2, 64], BF16)
        S_C = sb("S_C", [128, 4, 64]); Sb_C = sb("Sb_C", [128, 4, 64], BF16)
        S_D = sb("S_D", [128, 2, 64]); Sb_D = sb("Sb_D", [128, 2, 64], BF16)
        tmpS = sb("tmpS", [128, 4, 64])
        uT = [sb(f"uT{i}", [128, 12, 131], BF16) for i in range(2)]
        cvst = sb("cvst", [128, 12, 3])
        xs = sb("xs", [128, 12, 128], BF16)
        xsf = sb("xsf", [128, 4, 128])
        g8 = sb("g8", [128, 64])
        gc = sb("gc", [128, 64])
        egc = sb("egc", [128, 64])
        beta = sb("beta", [128, 64])
        dtb = sb("dtb", [128, 64])
        nb = sb("nb", [128, 64])
        nb2 = sb("nb2", [128, 64])
        for _t, _k in ((g8, "g8"), (beta, "beta"), (nega, "nega")):
            MEMSET("pool", _t[:], 0.0, [_k])
        tt = sb("tt", [128, 8, 128])
        DT = sb("DT", [128, 8, 128], BF16)
        Dst = sb("Dst", [128, 4, 128], BF16)
        eGbc = sb("eGbc", [128, 8, 128])
        eGlB = sb("eGlB", [128, 2, 2])
        X_bf = [sb(f"X_bf{i}", [128, 4, 128], BF16) for i in range(2)]
        Y_bf = [sb(f"Y_bf{i}", [128, 4, 128], BF16) for i in range(2)]
        P_bf = sb("P_bf", [128, 4, 128], BF16)
        bv = sb("bv", [128, 4, 64])
        r_bf = sb("r_bf", [128, 4, 64], BF16)
        u_bf = sb("u_bf", [128, 4, 64], BF16)
        xd_bf = sb("xd_bf", [128, 4, 64], BF16)

        def load_layer(l):
            DMA(prm[:], prm_d[l], [], ["prm"])
            DMA(wst[0][0:1, 0:768], cbias_d[0:1, l * 768:(l + 1) * 768], [], ["wst0"])
            CP("pool", cbias_bf[:], wst[0][0:1, 0:768], ["wst0"], ["cbias_bf"])
            ACT(nega[:, 0:8], prm[:, 1280:1288], AF.Exp, ["prm"], ["nega"])
            TS("dve", nega[:, 0:64], nega[:, 0:64], -1.0, ALU.mult, ["nega"], ["nega"])
            for cv in range(2):
                for blk in range(6):
                    for w in range(4):
                        col = 32 + l * 48 + cv * 24 + blk * 4 + w
                        if (blk + w) % 2 == 0:
                            ACT(dg[:, cv * 6 + blk, w, :], ident_bf[:], AF.Copy, ["ident_bf", "featp"], ["dg"],
                                scale=featp[:, col:col + 1])
                        else:
                            TS("dve", dg[:, cv * 6 + blk, w, :], ident_bf[:], featp[:, col:col + 1], ALU.mult,
                               ["ident_bf", "featp"], ["dg"])
            i = 0
            for kc in range(8):
                for c0 in range(0, IN_DIM, WCH):
                    st = wst[i % 2]; sk = f"wst{i % 2}"
                    DMA(st[:, 0:WCH], w_in[l, kc * 128:(kc + 1) * 128, c0:c0 + WCH], [], [sk])
                    if i % 2 == 0:
                        ACT(win[:, kc, c0:c0 + WCH], st[:, 0:WCH], AF.Copy, [sk, "featp"], ["win"],
                            scale=featp[:, l * 8 + kc:l * 8 + kc + 1])
                    else:
                        TS("dve", win[:, kc, c0:c0 + WCH], st[:, 0:WCH], featp[:, l * 8 + kc:l * 8 + kc + 1], ALU.mult,
                           [sk, "featp"], ["win"])
                    i += 1
            for kc in range(8):
                st = wst[i % 2]; sk = f"wst{i % 2}"
                DMA(st[:, 0:1024], w_out[l, kc * 128:(kc + 1) * 128, :], [], [sk])
                CP("act" if i % 2 == 0 else "dve", wout[:, kc, :], st[:, 0:1024], [sk], ["wout"])
                i += 1
            for nm, S_, Sb_ in (("A", S_A, Sb_A), ("B", S_B, Sb_B), ("C", S_C, Sb_C), ("D", S_D, Sb_D)):
                MEMSET("pool", S_[:], 0.0, ["S_" + nm])
                MEMSET("pool", Sb_[:], 0.0, ["Sb_" + nm])
            MEMSET("pool", uT[0][:, :, 0:3], 0.0, ["uT0"])

        def stage0(l, t):
            n = 16 if t == 0 else 128
            hk = f"hb{t % 2}"
            ht = hb[t % 2][0:n, :]
            hnT = hnTs[t % 2]; hnTk = f"hnT{t % 2}"
            if l == 0:
                DMA(ht, meta if t == 0 else xp[(t - 1) * 128:t * 128, :], [], [hk])
            else:
                DMA(ht, hscr[t * 128:t * 128 + n, :], [f"hd{t}"], [hk])
            ACT(hn_bf[0:n, :], ht, AF.Square, [hk], ["hn_bf", "st0"], accum=st0[0:n, 0:1])
            ACT(st0[0:n, 1:2], st0[0:n, 0:1], AF.Ln, ["st0"], ["st0"], scale=1.0 / D, bias=eps_t[0:n, 0:1])
            ACT(st0[0:n, 1:2], st0[0:n, 1:2], AF.Exp, ["st0"], ["st0"], scale=-0.5)
            ACT(hn_bf[0:n, :], ht, AF.Copy, [hk, "st0"], ["hn_bf"], scale=st0[0:n, 1:2])

        def stage0_pe(l, t):
            n = 16 if t == 0 else 128
            hnT = hnTs[t % 2]; hnTk = f"hnT{t % 2}"
            for half in range(2):
                pt, pk = bank()
                for kk in range(4):
                    kc = half * 4 + kk
                    MM(pt[:, kk * 128:kk * 128 + n], hn_bf[0:n, kc * 128:(kc + 1) * 128], ident_bf[0:n, 0:n],
                       ["hn_bf", "ident_bf"], [pk])
                CP("act" if half else "dve", hnT[:, half * 4:half * 4 + 4, 0:n],
                   pt[:, :].rearrange("p (a b) -> p a b", a=4)[:, :, 0:n], [pk], [hnTk])

        def tile_fwd(l, t, mid_hook=None, late_hook=None):
            n = 16 if t == 0 else 128
            chunks = [(0, 16)] if t == 0 else [(0, 64), (64, 128)]
            nch = len(chunks)
            clen = chunks[0][1]
            last_tile = (t == ntiles - 1)
            hk = f"hb{t % 2}"
            ht = hb[t % 2][0:n, :]
            hnT = hnTs[t % 2]; hnTk = f"hnT{t % 2}"
            cur = uT[t % 2]; curk = f"uT{t % 2}"
            pO2, kO2 = psb[5], "ps5"
            pOC, kOC = psb[5], "ps5"
            nxt = uT[(t + 1) % 2]; nxtk = f"uT{(t + 1) % 2}"
            rt = rot[t % 2]; rtk = f"rot{t % 2}"
            DMA(rt[:], rot_d[t], [], [rtk])

            def bc_h(ap2d, nh=4):
                return ap2d.unsqueeze(1).to_broadcast([ap2d.shape[0], nh, ap2d.shape[1]])

            def bc_l(ap2d, m):
                return ap2d.unsqueeze(2).to_broadcast([ap2d.shape[0], ap2d.shape[1], m])

            def proj_tok(c0, c1, extra=None):
                pt, pk = bank()
                for kc in range(8):
                    MM(pt[0:n, 0:c1 - c0], hnT[:, kc, 0:n], win[:, kc, c0:c1], [hnTk, "win"], [pk],
                       start=(kc == 0), stop=(kc == 7 and extra is None))
                if extra is not None:
                    MM(pt[0:n, extra[0]:extra[0] + 4], C("ident", slice(0, n), 0, n), prm[0:n, extra[1]:extra[1] + 4],
                       ["cst", "prm"], [pk], start=False, stop=True)
                return pt, pk

            def proj_feat(cols0, nblk):
                pt, pk = bank()
                for b_ in range(nblk):
                    for kc in range(8):
                        MM(pt[:, b_ * 128:b_ * 128 + n], win[:, kc, cols0 + b_ * 128:cols0 + (b_ + 1) * 128], hnT[:, kc, 0:n],
                           [hnTk, "win"], [pk], start=(kc == 0), stop=(kc == 7))
                return pt, pk

            def v3(ps_ap, a):
                return ps_ap.rearrange("p (a b) -> p a b", a=a)

            pA0, kA0 = proj_tok(0, 512)
            pA1, kA1 = proj_tok(512, 1024)
            ACT(f1[0:n, 0:256], pA0[0:n, 0:256], AF.Exp, [kA0], ["f1"], scale=-1.0)
            ACT(f1[0:n, 256:512], pA0[0:n, 256:512], AF.Exp, [kA0], ["f1"])
            ACT(f1[0:n, 512:768], pA1[0:n, 256:512], AF.Exp, [kA1], ["f1"], scale=-1.0)
            sigmoid_from_exp(f1[0:n, :], "f1")
            STT(f2[0:n, 0:256], pA0[0:n, 0:256], QK, f1[0:n, 0:256], ALU.mult, ALU.mult, [kA0, "f1"], ["f2"])
            TT("dve", f2[0:n, 256:512], f1[0:n, 256:512], oml[0:n, l, :], ALU.mult, ["f1", "oml"], ["f2"])
            TT("dve", f1[0:n, 512:768], f1[0:n, 512:768], prm[0:n, 0:256], ALU.mult, ["f1", "prm"], ["f1"])
            TT("dve", gate[0:n, 0:256], pA1[0:n, 256:512], f1[0:n, 512:768], ALU.mult, [kA1, "f1"], ["gate"])
            ACT(v_bf[0:n, :, :].rearrange("p a b -> p (a b)"), pA1[0:n, 0:256], AF.Copy, [kA1], ["v_bf"])
            ACT(f3[0:n, 0:256], f2[0:n, 256:512], AF.Ln, ["f2"], ["f3"], scale=-1.0, bias=eps_t[0:n, 1:2])
            pG, kG = bank()
            MM(pG[0:n, 0:256], C("uprime", slice(0, n), 0, n), f3[0:n, 0:256], ["cst", "f3"], [kG])
            for hp in range(2):
                MM(pG[:, 256 + hp * 8:256 + hp * 8 + 8], f3[0:n, hp * 128:(hp + 1) * 128], C("wc", slice(0, n)),
                   ["cst", "f3"], [kG])
            ACT(f3[0:n, 0:256], pG[0:n, 0:256], AF.Exp, [kG], ["f3"])
            ACT(f3[0:n, 256:512], pG[0:n, 0:256], AF.Exp, [kG], ["f3"], scale=-1.0)
            ACT(ecs[:].rearrange("p a b -> p (a b)"), pG[:, 256:272], AF.Exp, [kG], ["ecs"])
            TT("dve", b1[0:n, 0:2, :].rearrange("p a b -> p (a b)"), f2[0:n, 0:256], f3[0:n, 0:256], ALU.mult,
               ["f2", "f3"], ["b1"])
            TT("dve", b1[0:n, 2:4, :].rearrange("p a b -> p (a b)"), f2[0:n, 256:512], f3[0:n, 256:512], ALU.mult,
               ["f2", "f3"], ["b1"])
            pT, kT = bank()
            for blk in range(4):
                MM(pT[:, blk * 128:blk * 128 + n], b1[0:n, blk, :], ident_bf[0:n, 0:n], ["b1", "ident_bf"], [kT])
            CP("act", qkT[:, :, 0:n], v3(pT[:, :], 4)[:, :, 0:n], [kT], ["qkT"])
            pS, kS = bank()
            for hd in (0, 2, 1, 3):
                hp, hh = hd // 2, hd % 2
                rows = slice(hh * 64, hh * 64 + 64)
                MM(pS[0:n, hd * 128:hd * 128 + n], qkT[rows, 2 + hp, 0:n], qkT[rows, hp, 0:n], ["qkT"], [kS])
            TT("dve", AT[0:n, :, 0:n], v3(pS[0:n, :], 4)[:, :, 0:n], bc_h(C("maskT", slice(0, n), 0, n)), ALU.mult,
               [kS, "cst"], ["AT"])
            pO, kO = psb[4], "ps4"
            pOD, kOD = psb[4], "ps4"
            for hd in (0, 2, 1, 3):
                MM(pO[0:n, hd * 64:hd * 64 + 64], AT[0:n, hd, 0:n], v_bf[0:n, hd, :], ["AT", "v_bf"], [kO],
                   start=(hd == 0), stop=False)
            for ci, (c0, c1) in enumerate(chunks):
                TT("dve", Sb_A[:], S_A[:], bc_l(ecs[:, :, ci], 64), ALU.mult, ["S_A", "ecs"], ["Sb_A"])
                TT("dve", Sd_A[:], S_A[:], bc_l(ecs[:, :, 4 + ci], 64), ALU.mult, ["S_A", "ecs"], ["Sd_A"])
                pK, kK = bank()
                for hd in (0, 2, 1, 3):
                    hp, hh = hd // 2, hd % 2
                    rows = slice(hh * 64, hh * 64 + 64)
                    MM(pO[c0:c1, hd * 64:hd * 64 + 64], qkT[rows, hp, c0:c1], Sb_A[rows, hp, :], ["qkT", "Sb_A"], [kO],
                       start=False, stop=True)
                    MM(pK[rows, hp * 64:hp * 64 + 64], b1[c0:c1, 2 + hp, hh * 64:hh * 64 + 64], v_bf[c0:c1, hd, :],
                       ["b1", "v_bf"], [kK])
                TT("dve", tmpS[:, 0:2, :], v3(pK[:, 0:128], 2), bc_l(ecs[:, :, 2 + ci], 64), ALU.mult, [kK, "ecs"], ["tmpS"])
                TT("dve", S_A[:], tmpS[:, 0:2, :], Sd_A[:], ALU.add, ["tmpS", "Sd_A"], ["S_A"])

            def head_norm(ps_ap, pskey, gcols, ycols):
                ACT(f4[0:n, 0:256], ps_ap, AF.Square, [pskey], ["f4"])
                RED(st4[0:n, 4:8], v3(f4[0:n, 0:256], 4), ["f4"], ["st4"])
                rsqrt_act(st4[0:n, 8:12], st4[0:n, 4:8], 1.0 / 64, ["st4"], ["st4"])
                TT("dve", v3(f4[0:n, 0:256], 4), v3(ps_ap, 4), bc_l(st4[0:n, 8:12], 64), ALU.mult, [pskey, "st4"], ["f4"])
                TT("dve", y_bf[0:n, ycols], f4[0:n, 0:256], gate[0:n, gcols], ALU.mult, ["f4", "gate"], ["y_bf"])

            head_norm(pO[0:n, 0:256], kO, slice(0, 256), slice(0, 256))
            if mid_hook is not None:
                mid_hook()

            pD0, kD0 = proj_tok(3084, 3596)
            pD1, kD1 = proj_tok(3596, 4108)
            cosb = rt[0:n, 0:32].unsqueeze(1).to_broadcast([n, 16, 32])
            sinb = rt[0:n, 32:64].unsqueeze(1).to_broadcast([n, 16, 32])
            qk4 = pD0[0:n, :].rearrange("p (a b) -> p a b", a=16)
            TT("dve", f1[0:n, 0:512].rearrange("p (a b) -> p a b", a=16), qk4, cosb, ALU.mult, [kD0, rtk], ["f1"])
            TT("dve", f2[0:n, 0:512].rearrange("p (a b) -> p a b", a=16), qk4, sinb, ALU.mult, [kD0, rtk], ["f2"])
            c4 = f1[0:n, 0:512].rearrange("p (a s b) -> p a s b", a=8, s=2)
            s4 = f2[0:n, 0:512].rearrange("p (a s b) -> p a s b", a=8, s=2)
            qkr = b1[0:n, :, :].rearrange("p a (s b) -> p a s b", s=4)
            qkr8 = b1[0:n, :, :].rearrange("p a b -> p (a b)").rearrange("p (a s b) -> p a s b", a=8, s=2)
            TT("dve", qkr8[:, :, 0, :], c4[:, :, 0, :], s4[:, :, 1, :], ALU.subtract, ["f1", "f2"], ["b1"])
            TT("dve", qkr8[:, :, 1, :], c4[:, :, 1, :], s4[:, :, 0, :], ALU.add, ["f1", "f2"], ["b1"])
            ACT(v_bf[0:n, :, :].rearrange("p a b -> p (a b)"), pD1[0:n, 0:256], AF.Copy, [kD1], ["v_bf"])
            ACT(f1[0:n, 512:768], pD1[0:n, 256:512], AF.Exp, [kD1], ["f1"], scale=-1.0)
            sigmoid_from_exp(f1[0:n, 512:768], "f1")
            TT("dve", gate[0:n, 768:1024], pD1[0:n, 256:512], f1[0:n, 512:768], ALU.mult, [kD1, "f1"], ["gate"])
            pT, kT = bank()
            for blk in range(4):
                MM(pT[:, blk * 128:blk * 128 + n], b1[0:n, blk, :], ident_bf[0:n, 0:n], ["b1", "ident_bf"], [kT])
            CP("act", qkT[:, :, 0:n], v3(pT[:, :], 4)[:, :, 0:n], [kT], ["qkT"])
            egq = C("egq").rearrange("p (a b) -> p a b", a=2)
            TT("dve", qpT[:, 0:2, 0:n], qkT[:, 0:2, 0:n], egq[:, :, 0:n], ALU.mult, ["qkT", "cst"], ["qpT"])
            egrev = C("egrev16" if t == 0 else "egrev64", slice(0, n))
            TT("dve", kp_bf[0:n, :, 0:64], b1[0:n, 2:4, :].rearrange("p a (s b) -> p (a s) b", s=2), bc_l(egrev, 64), ALU.mult,
               ["b1", "cst"], ["kp_bf"])
            pS, kS = bank()
            for hd in (0, 2, 1, 3):
                hp, hh = hd // 2, hd % 2
                rows = slice(hh * 64, hh * 64 + 64)
                MM(pS[0:n, hd * 128:hd * 128 + n], qkT[rows, 2 + hp, 0:n], qkT[rows, hp, 0:n], ["qkT"], [kS])
            TT("dve", AT[0:n, :, 0:n], v3(pS[0:n, :], 4)[:, :, 0:n], dtret_bf[0:n, :, 0:n], ALU.mult, [kS, "dtret_bf"], ["AT"])
            for hd in (0, 2, 1, 3):
                MM(pOD[0:n, 256 + hd * 64:256 + hd * 64 + 64], AT[0:n, hd, 0:n], v_bf[0:n, hd, :], ["AT", "v_bf"], [kOD],
                   start=(hd == 0), stop=False)
            egl = C("egl").rearrange("p (a b) -> p a b", a=2)
            for ci, (c0, c1) in enumerate(chunks):
                pK, kK = bank()
                for hd in (0, 2, 1, 3):
                    hp, hh = hd // 2, hd % 2
                    rows = slice(hh * 64, hh * 64 + 64)
                    MM(pOD[c0:c1, 256 + hd * 64:256 + hd * 64 + 64], qpT[rows, hp, c0:c1], Sb_D[rows, hp, :], ["qpT", "Sb_D"], [kOD],
                       start=False, stop=True)
                    MM(pK[rows, hp * 64:hp * 64 + 64], kp_bf[c0:c1, hd, 0:64], v_bf[c0:c1, hd, :], ["kp_bf", "v_bf"], [kK])
                TT("dve", tmpS[:, 0:2, :], S_D[:], bc_l(egl[:, :, (0 if t == 0 else 1)], 64), ALU.mult, ["S_D", "cst"], ["tmpS"])
                TT("dve", S_D[:], tmpS[:, 0:2, :], v3(pK[:, 0:128], 2), ALU.add, ["tmpS", kK], ["S_D"])
                CP("act", Sb_D[:], S_D[:], ["S_D"], ["Sb_D"])
            oD = pOD[0:n, 256:512]
            kO_ = kOD
            RED(st4[0:n, 4:8], v3(oD, 4), [kO_], ["st4"])
            TS("dve", st4[0:n, 4:8], st4[0:n, 4:8], -1.0 / 64, ALU.mult, ["st4"], ["st4"])
            TT("dve", v3(f3[0:n, 0:256], 4), v3(oD, 4), bc_l(st4[0:n, 4:8], 64), ALU.add, [kO_, "st4"], ["f3"])
            ACT(f4[0:n, 0:256], f3[0:n, 0:256], AF.Square, ["f3"], ["f4"])
            RED(st4[0:n, 4:8], v3(f4[0:n, 0:256], 4), ["f4"], ["st4"])
            rsqrt_act(st4[0:n, 8:12], st4[0:n, 4:8], 1.0 / 64, ["st4"], ["st4"])
            TT("dve", v3(f3[0:n, 0:256], 4), v3(f3[0:n, 0:256], 4), bc_l(st4[0:n, 8:12], 64), ALU.mult, ["f3", "st4"], ["f3"])
            TT("dve", f3[0:n, 0:256], f3[0:n, 0:256], prm[0:n, 768:1024], ALU.mult, ["f3", "prm"], ["f3"])
            TT("dve", f3[0:n, 0:256], f3[0:n, 0:256], prm[0:n, 1024:1280], ALU.add, ["f3", "prm"], ["f3"])
            TT("dve", y_bf[0:n, 768:1024], f3[0:n, 0:256], gate[0:n, 768:1024], ALU.mult, ["f3", "gate"], ["y_bf"])

            pBz, kBz = proj_tok(1792, 2056, (256, 1288))
            pCz, kCz = proj_tok(2824, 3084, (256, 1292))
            ACT(f1[0:n, 512:768], pBz[0:n, 0:256], AF.Exp, [kBz], ["f1"], scale=-1.0)
            ACT(f2[0:n, 0:256], pCz[0:n, 0:256], AF.Exp, [kCz], ["f2"], scale=-1.0)
            ACT(beta[0:n, 0:4], pBz[0:n, 260:264], AF.Exp, [kBz], ["beta"], scale=-1.0)
            ACT(g8[0:n, 0:4], pBz[0:n, 256:260], AF.Exp, [kBz], ["g8"])
            ACT(g8[0:n, 4:8], pCz[0:n, 256:260], AF.Exp, [kCz], ["g8"])
            sigmoid_from_exp(f1[0:n, 512:768], "f1")
            sigmoid_from_exp(f2[0:n, 0:256], "f2")
            TT("dve", f1[0:n, 512:768], f1[0:n, 512:768], prm[0:n, 256:512], ALU.mult, ["f1", "prm"], ["f1"])
            TT("dve", gate[0:n, 256:512], pBz[0:n, 0:256], f1[0:n, 512:768], ALU.mult, [kBz, "f1"], ["gate"])
            TT("dve", gate[0:n, 512:768], pCz[0:n, 0:256], f2[0:n, 0:256], ALU.mult, [kCz, "f2"], ["gate"])
            for cv, cols0 in ((0, 1024), (1, 2056)):
                for part, (b0, nb_) in enumerate(((0, 4), (4, 2))):
                    pf, kf = proj_feat(cols0 + b0 * 128, nb_)
                    src = v3(pf[:, 0:nb_ * 128], nb_)[:, :, 0:n]
                    CP("act", cur[:, cv * 6 + b0:cv * 6 + b0 + nb_, 3:3 + n], src, [kf], [curk])
                    if last_tile:
                        CP("dve", cvst[:, cv * 6 + b0:cv * 6 + b0 + nb_, :], src[:, :, n - 3:n], [kf], ["cvst"])
            if not last_tile:
                CP("pool", nxt[:, :, 0:3], cur[:, :, n:n + 3], [curk], [nxtk])
            for cv in range(2):
                for part, (b0, nb_) in enumerate(((0, 4), (4, 2))):
                    pc, kc_ = bank()
                    for b_ in range(nb_):
                        blk = cv * 6 + b0 + b_
                        for w in range(4):
                            MM(pc[:, b_ * 128:b_ * 128 + n], dg[:, blk, w, :], cur[:, blk, w:w + n], ["dg", curk], [kc_],
                               start=(w == 0), stop=(w == 3 and cv == 0))
                        if cv == 1:
                            MM(pc[:, b_ * 128:b_ * 128 + n], cbias_bf[0:1, (b0 + b_) * 128:(b0 + b_ + 1) * 128], ones_bf[0:1, 0:n],
                               ["cbias_bf", "ones_bf"], [kc_], start=False, stop=True)
                    src = v3(pc[:, 0:nb_ * 128], nb_)[:, :, 0:n]
                    dstf = v3(f1[:, 0:nb_ * 128], nb_)[:, :, 0:n]
                    ACT(dstf, src, AF.Exp, [kc_], ["f1"], scale=-1.0)
                    sigmoid_from_exp(dstf, "f1")
                    if cv == 0 and part == 0:
                        TT("dve", xsf[:, :, 0:n], src, dstf, ALU.mult, [kc_, "f1"], ["xsf"])
                    else:
                        TT("dve", xs[:, cv * 6 + b0:cv * 6 + b0 + nb_, 0:n], src, dstf, ALU.mult, [kc_, "f1"], ["xs"])
            ACT(b1[:, :, 0:n], xsf[:, :, 0:n], AF.Square, ["xsf"], ["b1"])
            pN, kN = bank()
            for blk in range(4):
                MM(pN[:, blk * 128:blk * 128 + n], bones_bf[:], b1[:, blk, 0:n], ["bones_bf", "b1"], [kN])
            srcN = v3(pN[:, :], 4)[:, :, 0:n]
            dstN = v3(f2[:, 0:512], 4)[:, :, 0:n]
            ACT(dstN, srcN, AF.Ln, [kN], ["f2"], bias=eps_t[:, 0:1])
            ACT(dstN, dstN, AF.Exp, ["f2"], ["f2"], scale=-0.5)
            STT(xs[:, 0:2, 0:n], xsf[:, 0:2, 0:n], QK, dstN[:, 0:2, :], ALU.mult, ALU.mult, ["xsf", "f2"], ["xs"])
            TT("dve", xs[:, 2:4, 0:n], xsf[:, 2:4, 0:n], dstN[:, 2:4, :], ALU.mult, ["xsf", "f2"], ["xs"])

            ACT(g8[0:n, 0:8], g8[0:n, 0:8], AF.Ln, ["g8"], ["g8"], bias=eps_t[0:n, 1:2])
            CP("dve", dtb[0:n, :], g8[0:n, :], ["g8"], ["dtb"])
            TT("dve", g8[0:n, :], g8[0:n, :], nega[0:n, :], ALU.mult, ["g8", "nega"], ["g8"])
            sigmoid_from_exp(beta[0:n, :], "beta")
            pDc, kDc = bank()
            MM(pDc[0:n, 0:32], C("maskT", slice(0, n), 0, n), g8[0:n, 0:32], ["cst", "g8"], [kDc])
            MM(pDc[0:n, 32:64], C("urev", slice(0, n), 0, n), g8[0:n, 0:32], ["cst", "g8"], [kDc])
            CP("dve", gc[0:n, :], pDc[0:n, 0:64], [kDc], ["gc"])
            ACT(egc[0:n, :], gc[0:n, :], AF.Exp, ["gc"], ["egc"])
            for half in range(2):
                pB_, kB_ = bank()
                CP("dve", v3(f4[0:n, 0:512], 4), bc_l(g8[0:n, half * 4:half * 4 + 4], 128), ["g8"], ["f4"])
                for hd in (0, 2, 1, 3):
                    MM(pB_[:, hd * 128:hd * 128 + n], f4[0:n, hd * 128:(hd + 1) * 128],
                       C("maskT", slice(0, n), 0, n), ["cst", "f4"], [kB_])
                srcB = v3(pB_[:, :], 4)[:, :, 0:n]
                ACT(eGbc[:, half * 4:half * 4 + 4, 0:n], srcB, AF.Exp, [kB_], ["eGbc"])
                TT("dve", tt[0:n, half * 4:half * 4 + 4, 0:n], srcB[0:n], bc_l(gc[0:n, half * 4:half * 4 + 4], n), ALU.subtract,
                   [kB_, "gc"], ["tt"])
            if True:
                TT("dve", v3(f1[0:n, 0:512], 4)[:, :, 0:n], tt[0:n, 0:4, 0:n], bc_h(C("negS", slice(0, n), 0, n)), ALU.subtract,
                   ["tt", "cst"], ["f1"])
                ACT(Dst[0:n, :, 0:n], v3(f1[0:n, 0:512], 4)[:, :, 0:n], AF.Exp, ["f1"], ["Dst"], scale=-1.0)
                TT("dve", tt[0:n, :, 0:n], tt[0:n, :, 0:n], bc_h(C("negT", slice(0, n), 0, n), 8), ALU.add, ["tt", "cst"], ["tt"])
                ACT(DT[0:n, :, 0:n], tt[0:n, :, 0:n], AF.Exp, ["tt"], ["DT"])

            pS, kS = bank()
            pKK, kKK = bank()
            for hd in (0, 2, 1, 3):
                hp, hh = hd // 2, hd % 2
                rows = slice(hh * 64, hh * 64 + 64)
                MM(pS[0:n, hd * 128:hd * 128 + n], xs[rows, 2 + hp, 0:n], xs[rows, hp, 0:n], ["xs"], [kS])
                MM(pKK[0:n, hd * 128:hd * 128 + n], xs[rows, 2 + hp, 0:n], xs[rows, 2 + hp, 0:n], ["xs"], [kKK])
            TT("dve", AT[0:n, :, 0:n], v3(pS[0:n, :], 4)[:, :, 0:n], DT[0:n, 0:4, 0:n], ALU.mult, [kS, "DT"], ["AT"])
            TS("dve", nb[0:n, :], beta[0:n, :], -1.0, ALU.mult, ["beta"], ["nb"])
            TT("dve", v3(f1[0:n, 0:512], 4)[:, :, 0:n], v3(pKK[0:n, :], 4)[:, :, 0:n], Dst[0:n, :, 0:n], ALU.mult, [kKK, "Dst"], ["f1"])
            TT("dve", X_bf[0][0:n, :, 0:n], v3(f1[0:n, 0:512], 4)[:, :, 0:n], bc_l(nb[0:n, 0:4], n), ALU.mult,
               ["f1", "nb"], ["X_bf0"])
            def t_chain():
                pY, kY = bank()
                for hd in (0, 2, 1, 3):
                    MM(pY[0:n, hd * 128:hd * 128 + n], X_bf[0][0:n, hd, 0:n], ident_bf[0:n, 0:n], ["X_bf0", "ident_bf"], [kY])
                CP("act", Y_bf[0][0:n, :, 0:n], v3(pY[0:n, :], 4)[:, :, 0:n], [kY], ["Y_bf0"])
                TT("dve", P_bf[0:n, :, 0:n], v3(pY[0:n, :], 4)[:, :, 0:n], bc_h(C("ident", slice(0, n), 0, n)), ALU.add,
                   [kY, "cst"], ["P_bf"])
                yield
                nlev = int(math.ceil(math.log2(clen))) - 1
                ci_ = 0
                for lev in range(nlev):
                    ni_ = 1 - ci_
                    pX2, kX2 = bank()
                    for hd in (0, 2, 1, 3):
                        MM(pX2[0:n, hd * 128:hd * 128 + n], Y_bf[ci_][0:n, hd, 0:n], X_bf[ci_][0:n, hd, 0:n],
                           [f"Y_bf{ci_}", f"X_bf{ci_}"], [kX2])
                    CP("act", X_bf[ni_][0:n, :, 0:n], v3(pX2[0:n, :], 4)[:, :, 0:n], [kX2], [f"X_bf{ni_}"])
                    if lev < nlev - 1:
                        pY2, kY2 = bank()
                        for hd in (0, 2, 1, 3):
                            MM(pY2[0:n, hd * 128:hd * 128 + n], X_bf[ci_][0:n, hd, 0:n], Y_bf[ci_][0:n, hd, 0:n],
                               [f"Y_bf{ci_}", f"X_bf{ci_}"], [kY2])
                        CP("dve", Y_bf[ni_][0:n, :, 0:n], v3(pY2[0:n, :], 4)[:, :, 0:n], [kY2], [f"Y_bf{ni_}"])
                    pP, kP = bank()
                    for hd in (0, 2, 1, 3):
                        MM(pP[0:n, hd * 128:hd * 128 + n], X_bf[ni_][0:n, hd, 0:n], P_bf[0:n, hd, 0:n], [f"X_bf{ni_}", "P_bf"], [kP])
                    TT("dve", P_bf[0:n, :, 0:n], P_bf[0:n, :, 0:n], v3(pP[0:n, :], 4)[:, :, 0:n], ALU.add, ["P_bf", kP], ["P_bf"])
                    ci_ = ni_
                    yield

            tgen = t_chain()

            def tstep():
                next(tgen, None)

            tstep()

            pS, kS = bank()
            for g in range(2):
                MM(pS[0:n, g * 128:g * 128 + n], xs[:, 8 + g, 0:n], xs[:, 10 + g, 0:n], ["xs"], [kS])
            for g in range(2):
                TT("dve", AT2[0:n, 2 * g:2 * g + 2, 0:n], pS[0:n, g * 128:g * 128 + n].unsqueeze(1).to_broadcast([n, 2, n]),
                   DT[0:n, 4 + 2 * g:4 + 2 * g + 2, 0:n], ALU.mult, [kS, "DT"], ["AT2"])
            tstep()
            pT, kT = bank()
            for blk in range(4):
                MM(pT[0:n, blk * 128:(blk + 1) * 128], xs[:, 6 + blk, 0:n], ident_bf[:, :], ["xs", "ident_bf"], [kT])
            TT("dve", v_bf[0:n, :, :], v3(pT[0:n, 0:256], 4), bc_l(dtb[0:n, 4:8], 64), ALU.mult, [kT, "dtb"], ["v_bf"])
            TT("dve", xd_bf[0:n, :, :], v3(pT[0:n, 0:256], 4), bc_l(prm[0:n, 1296:1300], 64), ALU.mult, [kT, "prm"], ["xd_bf"])
            for g in range(2):
                TT("dve", kp_bf[0:n, 2 * g:2 * g + 2, :], pT[0:n, 256 + g * 128:256 + (g + 1) * 128].unsqueeze(1).to_broadcast([n, 2, 128]),
                   bc_l(egc[0:n, 36 + 2 * g:36 + 2 * g + 2], 128), ALU.mult, [kT, "egc"], ["kp_bf"])
                TT("dve", qpT[:, 2 * g:2 * g + 2, 0:n], xs[:, 10 + g, 0:n].unsqueeze(1).to_broadcast([128, 2, n]),
                   eGbc[:, 4 + 2 * g:4 + 2 * g + 2, 0:n], ALU.mult, ["xs", "eGbc"], ["qpT"])
            for hd in (0, 2, 1, 3):
                MM(pOC[0:n, 256 + hd * 64:256 + hd * 64 + 64], AT2[0:n, hd, 0:n], v_bf[0:n, hd, :], ["AT2", "v_bf"], [kOC],
                   start=(hd == 0), stop=False)
            MM(pOC[0:n, 256:512], ident_bf[0:n, 0:n], xd_bf[0:n, :, :].rearrange("p a b -> p (a b)"), ["ident_bf", "xd_bf"], [kOC],
               start=False, stop=False)
            for ci, (c0, c1) in enumerate(chunks):
                tstep()
                pK, kK = bank()
                for hd in (0, 2, 1, 3):
                    MM(pOC[c0:c1, 256 + hd * 64:256 + hd * 64 + 64], qpT[:, hd, c0:c1], Sb_C[:, hd, :], ["qpT", "Sb_C"], [kOC],
                       start=False, stop=True)
                    MM(pK[:, hd * 64:hd * 64 + 64], kp_bf[c0:c1, hd, :], v_bf[c0:c1, hd, :], ["kp_bf", "v_bf"], [kK])
                TT("dve", tmpS[:, :, :], S_C[:], eGbc[:, 4:8, c1 - 1:c1].to_broadcast([128, 4, 64]), ALU.mult, ["S_C", "eGbc"], ["tmpS"])
                TT("dve", S_C[:], tmpS[:, :, :], v3(pK[:, 0:256], 4), ALU.add, ["tmpS", kK], ["S_C"])
                CP("act", Sb_C[:], S_C[:], ["S_C"], ["Sb_C"])
            tstep()
            TT("dve", f3[0:n, 0:256], pOC[0:n, 256:512], gate[0:n, 512:768], ALU.mult, [kOC, "gate"], ["f3"])
            ACT(f4[0:n, 0:256], f3[0:n, 0:256], AF.Square, ["f3"], ["f4"])
            RED(st4[0:n, 4:6], v3(f4[0:n, 0:256], 2), ["f4"], ["st4"])
            rsqrt_act(st4[0:n, 8:10], st4[0:n, 4:6], 1.0 / 128, ["st4"], ["st4"])
            TT("dve", v3(f3[0:n, 0:256], 2), v3(f3[0:n, 0:256], 2), bc_l(st4[0:n, 8:10], 128), ALU.mult, ["f3", "st4"], ["f3"])
            TT("dve", y_bf[0:n, 512:768], f3[0:n, 0:256], prm[0:n, 512:768], ALU.mult, ["f3", "prm"], ["y_bf"])

            for _ in tgen:
                pass

            pT, kT = bank()
            for blk in range(4):
                MM(pT[0:n, blk * 128:(blk + 1) * 128], xs[:, 2 + blk, 0:n], ident_bf[:, :], ["xs", "ident_bf"], [kT])
            TT("dve", kp_bf[0:n, :, 0:64], v3(pT[0:n, 0:256], 4), bc_l(egc[0:n, 32:36], 64), ALU.mult, [kT, "egc"], ["kp_bf"])
            TT("dve", bv[0:n, :, :], v3(pT[0:n, 256:512], 4), bc_l(beta[0:n, 0:4], 64), ALU.mult, [kT, "beta"], ["bv"])
            TT("dve", nb2[0:n, :], nb[0:n, :], egc[0:n, :], ALU.mult, ["nb", "egc"], ["nb2"])
            for hh in range(2):
                rows = slice(hh * 64, hh * 64 + 64)
                TT("dve", qpT[rows, 0:2, 0:n], xs[rows, 0:2, 0:n], eGbc[rows, hh:4:2, 0:n], ALU.mult, ["xs", "eGbc"], ["qpT"])
            for ci, (c0, c1) in enumerate(chunks):
                pW, kW = bank()
                for hd in (0, 2, 1, 3):
                    hp, hh = hd // 2, hd % 2
                    rows = slice(hh * 64, hh * 64 + 64)
                    MM(pW[c0:c1, hd * 64:hd * 64 + 64], xs[rows, 2 + hp, c0:c1], Sb_B[rows, hp, :], ["xs", "Sb_B"], [kW])
                TT("dve", v3(f4[c0:c1, 0:256], 4), v3(pW[c0:c1, 0:256], 4), bc_l(nb2[c0:c1, 0:4], 64), ALU.mult,
                   [kW, "nb2"], ["f4"])
                TT("dve", r_bf[c0:c1, :, :], v3(f4[c0:c1, 0:256], 4), bv[c0:c1, :, :], ALU.add, ["f4", "bv"], ["r_bf"])
                pU, kU = bank()
                for hd in (0, 2, 1, 3):
                    MM(pU[c0:c1, hd * 64:hd * 64 + 64], P_bf[c0:c1, hd, c0:c1], r_bf[c0:c1, hd, :], ["P_bf", "r_bf"], [kU])
                CP("act", u_bf[c0:c1, :, :], v3(pU[c0:c1, 0:256], 4), [kU], ["u_bf"])
                pK, kK = bank()
                for hd in (0, 2, 1, 3):
                    hp, hh = hd // 2, hd % 2
                    rows = slice(hh * 64, hh * 64 + 64)
                    MM(pO2[c0:c1, hd * 64:hd * 64 + 64], AT[c0:c1, hd, c0:c1], u_bf[c0:c1, hd, :], ["AT", "u_bf"], [kO2],
                       start=(hd == 0), stop=False)
                    MM(pO2[c0:c1, hd * 64:hd * 64 + 64], qpT[rows, hp, c0:c1], Sb_B[rows, hp, :], ["qpT", "Sb_B"], [kO2],
                       start=False, stop=True)
                    MM(pK[rows, hp * 64:hp * 64 + 64], kp_bf[c0:c1, hd, 0:64], u_bf[c0:c1, hd, :], ["kp_bf", "u_bf"], [kK])
                for hh in range(2):
                    rows = slice(hh * 64, hh * 64 + 64)
                    TT("dve", tmpS[rows, 0:2, :], S_B[rows, :, :], eGbc[rows, hh:4:2, c1 - 1:c1].to_broadcast([64, 2, 64]), ALU.mult,
                       ["S_B", "eGbc"], ["tmpS"])
                TT("dve", S_B[:], tmpS[:, 0:2, :], v3(pK[:, 0:128], 2), ALU.add, ["tmpS", kK], ["S_B"])
                CP("act", Sb_B[:], S_B[:], ["S_B"], ["Sb_B"])
            head_norm(pO2[0:n, 0:256], kO2, slice(256, 512), slice(256, 512))

            if late_hook is not None:
                late_hook()
            for half in range(2):
                pt, pk = bank()
                for kk in range(4):
                    kc = half * 4 + kk
                    MM(pt[:, kk * 128:kk * 128 + n], y_bf[0:n, kc * 128:(kc + 1) * 128], ident_bf[0:n, 0:n],
                       ["y_bf", "ident_bf"], [pk])
                CP("act" if half else "dve", yT[:, half * 4:half * 4 + 4, 0:n], v3(pt[:, :], 4)[:, :, 0:n], [pk], ["yT"])
            for cg in range(2):
                pt, pk = bank()
                for kc in range(8):
                    MM(pt[0:n, :], yT[:, kc, 0:n], wout[:, kc, cg * 512:(cg + 1) * 512], ["yT", "wout"], [pk],
                       start=(kc == 0), stop=(kc == 7))
                TT("dve", ht[:, cg * 512:(cg + 1) * 512], ht[:, cg * 512:(cg + 1) * 512], pt[0:n, :], ALU.add, [hk, pk], [hk])

            if last_tile:
                for nm, S_, dst in (("A", S_A, st_hg), ("B", S_B, st_gd), ("D", S_D, st_rt)):
                    for hh in range(2):
                        DMA(dst[l, hh:4:2, :, :].rearrange("a k v -> k a v"), S_[hh * 64:(hh + 1) * 64, :, :], ["S_" + nm], [])
                DMA(st_sd[l].rearrange("a k v -> k a v"), S_C[:, :, :], ["S_C"], [])
                for blk in range(6):
                    DMA(st_gc[l][:, blk * 128:(blk + 1) * 128].rearrange("w p -> p w"), cvst[:, blk, :], ["cvst"], [], slow=True)
                    DMA(st_sc[l][:, blk * 128:(blk + 1) * 128].rearrange("w p -> p w"), cvst[:, 6 + blk, :], ["cvst"], [], slow=True)
            if l < depth - 1:
                DMA(hscr[t * 128:t * 128 + n, :], ht, [hk], [f"hd{t}"])
            if l == depth - 1 and t > 0:
                ACT(hn_bf[0:n, :], ht, AF.Square, [hk], ["hn_bf", "st4"], accum=st4[0:n, 0:1])
                rsqrt_act(st4[0:n, 1:2], st4[0:n, 0:1], 1.0 / D, ["st4"], ["st4"])
                STT(ht, ht, st4[0:n, 1:2], finw[0:n, :], ALU.mult, ALU.mult, [hk, "st4", "lgt"], [hk])
                DMA(y_p[(t - 1) * 128:t * 128, :], ht, [hk], [])


        hs = sb("hs", [NS, D])
        DMA(hs[:, :], xs_d, [], ["hs"])

        def sample_fwd(l, last):
            n = NS
            DMA(rot[0][:], rot_d[NT], [], ["rot0"])
            rt = rot[0]; rtk = "rot0"

            def v3(ps_ap, a):
                return ps_ap.rearrange("p (a b) -> p a b", a=a)

            def bc_l(ap2d, m):
                return ap2d.unsqueeze(2).to_broadcast([ap2d.shape[0], ap2d.shape[1], m])

            ACT(hn_bf[0:n, :], hs[:, :], AF.Square, ["hs"], ["hn_bf", "st4"], accum=st4[0:n, 0:1])
            rsqrt_act(st4[0:n, 1:2], st4[0:n, 0:1], 1.0 / D, ["st4"], ["st4"])
            ACT(hn_bf[0:n, :], hs[:, :], AF.Copy, ["hs", "st4"], ["hn_bf"], scale=st4[0:n, 1:2])
            for half in range(2):
                pt, pk = bank()
                for kk in range(4):
                    kc = half * 4 + kk
                    MM(pt[:, kk * 128:kk * 128 + n], hn_bf[0:n, kc * 128:(kc + 1) * 128], ident_bf[0:n, 0:n],
                       ["hn_bf", "ident_bf"], [pk])
                CP("act", hnT[:, half * 4:half * 4 + 4, 0:n], v3(pt[:, :], 4)[:, :, 0:n], [pk], ["hnT1"])

            def proj_tok(c0, c1, extra=None):
                pt, pk = bank()
                for kc in range(8):
                    MM(pt[0:n, 0:c1 - c0], hnT[:, kc, 0:n], win[:, kc, c0:c1], ["hnT1", "win"], [pk],
                       start=(kc == 0), stop=(kc == 7 and extra is None))
                if extra is not None:
                    MM(pt[0:n, extra[0]:extra[0] + 4], C("ident", slice(0, n), 0, n), prm[0:n, extra[1]:extra[1] + 4],
                       ["cst", "prm"], [pk], start=False, stop=True)
                return pt, pk

            sel = C("sel").rearrange("p (a b) -> p a b", a=4)
            selT = C("selT").rearrange("p (a b) -> p a b", a=4)
            pvs = f4
            Sbuf = [tt, eGbc]; Skey = ["tt", "eGbc"]
            Tbuf = gate; Tkey = "gate"
            slot = [0]

            def select(fields):
                pv, pvk = bank()
                first = True
                for (c0, wd, fn) in fields:
                    for hd in range(4):
                        ap, key = fn(hd)
                        P.op("pe", (lambda o_, l_, r_, st_: (lambda e: e.matmul(o_, lhsT=l_, rhs=r_, start=st_, stop=False,
                                                                                 skip_group_check=True)))(
                            pv[0:64, c0:c0 + wd], sel[0:n, hd, :], ap, first), reads=["cst", key], writes=[pvk])
                        first = False
                wtot = max(c0 + wd for (c0, wd, _) in fields)
                CP("dve", pvs[0:64, 0:wtot], pv[0:64, 0:wtot], [pvk], ["f4"])

            def unselect(o_ap, okey):
                po, pok = bank()
                for hd in range(4):
                    MM(po[0:n, hd * 64:hd * 64 + 64], selT[0:64, hd, :], o_ap, ["cst", okey], [pok])
                return po, pok

            def state_io(st_in, st_out, K):
                ks = 16
                for k0 in range(0, K, ks):
                    yield k0, ks

            def load_slice(st_in, k0, ks):
                i = slot[0] % 2
                slot[0] += 1
                Sv = Sbuf[i][0:64, :, :].rearrange("p a b -> p (a b)")[:, 0:ks * 64].rearrange("p (k v) -> p k v", k=ks)
                for hd in range(4):
                    DMA(Sv[hd * 16:(hd + 1) * 16, :, :], st_in[l, :, hd, k0:k0 + ks, :], [], [Skey[i]])
                return Sv, Skey[i]

            def store_slice(st_out, Sv, sk, k0, ks):
                for hd in range(4):
                    DMA(st_out[l, :, hd, k0:k0 + ks, :], Sv[hd * 16:(hd + 1) * 16, :, :], [sk], [])

            o_sb = tmpS[0:64, 0, :]; w_sb = tmpS[0:64, 1, :]; op_sb = tmpS[0:64, 2, :]; u_sb = tmpS[0:64, 3, :]

            def Tview(ks):
                return Tbuf[0:64, 0:ks * 64].rearrange("p (k v) -> p k v", k=ks)

            def q_reduce(Sv, sk, q_ap, k0, ks, first, acc):
                T = Tview(ks)
                TT("dve", T, Sv, bc_l(q_ap[:, k0:k0 + ks], 64), ALU.mult, [sk, "f4"], [Tkey])
                RED(op_sb, T.rearrange("p k v -> p v k"), [Tkey], ["tmpS"])
                if first:
                    CP("dve", acc, op_sb, ["tmpS"], ["tmpS"])
                else:
                    TT("dve", acc, acc, op_sb, ALU.add, ["tmpS"], ["tmpS"])

            def step_plain(st_in, st_out, K, q_ap, k_ap, v_ap, vec_f=None, sc=None):
                for k0, ks in state_io(st_in, st_out, K):
                    Sv, sk = load_slice(st_in, k0, ks)
                    T = Tview(ks)
                    TT("dve", T, bc_l(k_ap[:, k0:k0 + ks], 64), v_ap.unsqueeze(1).to_broadcast([64, ks, 64]), ALU.mult,
                       ["f4", "tmpS"], [Tkey])
                    if vec_f is not None:
                        TT("dve", Sv, Sv, bc_l(vec_f[:, k0:k0 + ks], 64), ALU.mult, [sk, "f4"], [sk])
                        TT("dve", Sv, Sv, T, ALU.add, [sk, Tkey], [sk])
                    else:
                        STT(Sv, Sv, sc, T, ALU.mult, ALU.add, [sk, Tkey, "f4", "cst"], [sk])
                    store_slice(st_out, Sv, sk, k0, ks)
                    q_reduce(Sv, sk, q_ap, k0, ks, k0 == 0, o_sb)

            def head_norm(ps_ap, pskey, gcols, ycols):
                ACT(f3[0:n, 256:512], ps_ap, AF.Square, [pskey], ["f3"])
                RED(st4[0:n, 4:8], v3(f3[0:n, 256:512], 4), ["f3"], ["st4"])
                rsqrt_act(st4[0:n, 8:12], st4[0:n, 4:8], 1.0 / 64, ["st4"], ["st4"])
                TT("dve", v3(f3[0:n, 256:512], 4), v3(ps_ap, 4), bc_l(st4[0:n, 8:12], 64), ALU.mult, [pskey, "st4"], ["f3"])
                TT("dve", y_bf[0:n, ycols], f3[0:n, 256:512], hb[1][0:n, gcols], ALU.mult, ["f3", "hb1"], ["y_bf"])

            gs = hb[1]; gsk = "hb1"

            pA0, kA0 = proj_tok(0, 512)
            pA1, kA1 = proj_tok(512, 1024)
            ACT(f1[0:n, 0:256], pA0[0:n, 0:256], AF.Exp, [kA0], ["f1"], scale=-1.0)
            ACT(f1[0:n, 256:512], pA0[0:n, 256:512], AF.Exp, [kA0], ["f1"])
            ACT(f1[0:n, 512:768], pA1[0:n, 256:512], AF.Exp, [kA1], ["f1"], scale=-1.0)
            sigmoid_from_exp(f1[0:n, :], "f1")
            STT(f2[0:n, 0:256], pA0[0:n, 0:256], QK, f1[0:n, 0:256], ALU.mult, ALU.mult, [kA0, "f1"], ["f2"])
            TT("dve", f2[0:n, 256:512], f1[0:n, 256:512], oml[0:n, l, :], ALU.mult, ["f1", "oml"], ["f2"])
            TT("dve", f1[0:n, 512:768], f1[0:n, 512:768], prm[0:n, 0:256], ALU.mult, ["f1", "prm"], ["f1"])
            TT("dve", gs[0:n, 0:256], pA1[0:n, 256:512], f1[0:n, 512:768], ALU.mult, [kA1, "f1"], [gsk])
            CP("dve", f3[0:n, 0:256], pA1[0:n, 0:256], [kA1], ["f3"])
            TS("dve", f1[0:n, 0:256], f2[0:n, 256:512], -1.0, ALU.mult, ["f2"], ["f1"], s2=1.0, op1=ALU.add)
            select([(0, 64, lambda hd: (f2[0:n, hd * 64:hd * 64 + 64], "f2")),
                    (64, 64, lambda hd: (f2[0:n, 256 + hd * 64:256 + hd * 64 + 64], "f2")),
                    (128, 64, lambda hd: (f3[0:n, hd * 64:hd * 64 + 64], "f3")),
                    (192, 64, lambda hd: (f1[0:n, hd * 64:hd * 64 + 64], "f1"))])
            step_plain(si_hg, so_hg, 64, pvs[0:64, 0:64], pvs[0:64, 64:128], pvs[0:64, 128:192], vec_f=pvs[0:64, 192:256])
            po, pok = unselect(o_sb, "tmpS")
            head_norm(po[0:n, 0:256], pok, slice(0, 256), slice(0, 256))

            pD0, kD0 = proj_tok(3084, 3596)
            pD1, kD1 = proj_tok(3596, 4108)
            cosb = rt[0:n, 0:32].unsqueeze(1).to_broadcast([n, 16, 32])
            sinb = rt[0:n, 32:64].unsqueeze(1).to_broadcast([n, 16, 32])
            qk4 = pD0[0:n, :].rearrange("p (a b) -> p a b", a=16)
            TT("dve", f1[0:n, 0:512].rearrange("p (a b) -> p a b", a=16), qk4, cosb, ALU.mult, [kD0, rtk], ["f1"])
            TT("dve", f2[0:n, 0:512].rearrange("p (a b) -> p a b", a=16), qk4, sinb, ALU.mult, [kD0, rtk], ["f2"])
            c4 = f1[0:n, 0:512].rearrange("p (a s b) -> p a s b", a=8, s=2)
            s4 = f2[0:n, 0:512].rearrange("p (a s b) -> p a s b", a=8, s=2)
            r4 = f3[0:n, 0:512].rearrange("p (a s b) -> p a s b", a=8, s=2)
            TT("dve", r4[:, :, 0, :], c4[:, :, 0, :], s4[:, :, 1, :], ALU.subtract, ["f1", "f2"], ["f3"])
            TT("dve", r4[:, :, 1, :], c4[:, :, 1, :], s4[:, :, 0, :], ALU.add, ["f1", "f2"], ["f3"])
            TS("dve", f3[0:n, 256:512], f3[0:n, 256:512], QK, ALU.mult, ["f3"], ["f3"])
            CP("dve", f1[0:n, 0:256], pD1[0:n, 0:256], [kD1], ["f1"])
            ACT(f1[0:n, 512:768], pD1[0:n, 256:512], AF.Exp, [kD1], ["f1"], scale=-1.0)
            sigmoid_from_exp(f1[0:n, 512:768], "f1")
            TT("dve", gs[0:n, 768:1024], pD1[0:n, 256:512], f1[0:n, 512:768], ALU.mult, [kD1, "f1"], [gsk])
            select([(0, 64, lambda hd: (f3[0:n, hd * 64:hd * 64 + 64], "f3")),
                    (64, 64, lambda hd: (f3[0:n, 256 + hd * 64:256 + hd * 64 + 64], "f3")),
                    (128, 64, lambda hd: (f1[0:n, hd * 64:hd * 64 + 64], "f1"))])
            step_plain(si_rt, so_rt, 64, pvs[0:64, 0:64], pvs[0:64, 64:128], pvs[0:64, 128:192], sc=C("gam64", slice(0, 64), 0, 1))
            po, pok = unselect(o_sb, "tmpS")
            oD = po[0:n, 0:256]
            RED(st4[0:n, 4:8], v3(oD, 4), [pok], ["st4"])
            TS("dve", st4[0:n, 4:8], st4[0:n, 4:8], -1.0 / 64, ALU.mult, ["st4"], ["st4"])
            TT("dve", v3(f3[0:n, 0:256], 4), v3(oD, 4), bc_l(st4[0:n, 4:8], 64), ALU.add, [pok, "st4"], ["f3"])
            ACT(f3[0:n, 256:512], f3[0:n, 0:256], AF.Square, ["f3"], ["f3"])
            RED(st4[0:n, 4:8], v3(f3[0:n, 256:512], 4), ["f3"], ["st4"])
            rsqrt_act(st4[0:n, 8:12], st4[0:n, 4:8], 1.0 / 64, ["st4"], ["st4"])
            TT("dve", v3(f3[0:n, 0:256], 4), v3(f3[0:n, 0:256], 4), bc_l(st4[0:n, 8:12], 64), ALU.mult, ["f3", "st4"], ["f3"])
            TT("dve", f3[0:n, 0:256], f3[0:n, 0:256], prm[0:n, 768:1024], ALU.mult, ["f3", "prm"], ["f3"])
            TT("dve", f3[0:n, 0:256], f3[0:n, 0:256], prm[0:n, 1024:1280], ALU.add, ["f3", "prm"], ["f3"])
            TT("dve", y_bf[0:n, 768:1024], f3[0:n, 0:256], gs[0:n, 768:1024], ALU.mult, ["f3", gsk], ["y_bf"])

            pBz, kBz = proj_tok(1792, 2056, (256, 1288))
            ACT(f1[0:n, 512:768], pBz[0:n, 0:256], AF.Exp, [kBz], ["f1"], scale=-1.0)
            ACT(beta[0:n, 0:4], pBz[0:n, 260:264], AF.Exp, [kBz], ["beta"], scale=-1.0)
            ACT(g8[0:n, 0:4], pBz[0:n, 256:260], AF.Exp, [kBz], ["g8"])
            sigmoid_from_exp(f1[0:n, 512:768], "f1")
            TT("dve", f1[0:n, 512:768], f1[0:n, 512:768], prm[0:n, 256:512], ALU.mult, ["f1", "prm"], ["f1"])
            TT("dve", gs[0:n, 256:512], pBz[0:n, 0:256], f1[0:n, 512:768], ALU.mult, [kBz, "f1"], [gsk])
            pCz, kCz = proj_tok(2824, 3084, (256, 1292))
            ACT(f1[0:n, 512:768], pCz[0:n, 0:256], AF.Exp, [kCz], ["f1"], scale=-1.0)
            ACT(g8[0:n, 4:8], pCz[0:n, 256:260], AF.Exp, [kCz], ["g8"])
            sigmoid_from_exp(f1[0:n, 512:768], "f1")
            TT("dve", gs[0:n, 512:768], pCz[0:n, 0:256], f1[0:n, 512:768], ALU.mult, [kCz, "f1"], [gsk])
            ACT(g8[0:n, 0:8], g8[0:n, 0:8], AF.Ln, ["g8"], ["g8"], bias=eps_t[0:n, 1:2])
            CP("dve", dtb[0:n, :], g8[0:n, :], ["g8"], ["dtb"])
            TT("dve", g8[0:n, :], g8[0:n, :], nega[0:n, :], ALU.mult, ["g8", "nega"], ["g8"])
            ACT(egc[0:n, 0:8], g8[0:n, 0:8], AF.Exp, ["g8"], ["egc"])
            sigmoid_from_exp(beta[0:n, :], "beta")

            def conv_tok(cv, groups, st_in, st_out, bias):
                U = hb[0]; Uk = "hb0"
                for (c0, c1, o0) in groups:
                    pu, puk = proj_tok(c0, c1)
                    CP("dve", U[0:n, o0:o0 + (c1 - c0)], pu[0:n, 0:c1 - c0], [puk], [Uk])
                DMA(st_out[l, :, 2, :], U[0:n, 0:768], [Uk], [])
                Wt = eGbc[0:n, :, :].rearrange("p a b -> p (a b)")[:, 0:768]
                Ct = tt[0:n, :, :].rearrange("p a b -> p (a b)")[:, 0:768]
                DMA(Wt, cwrow_d[l, cv, 3], [], ["eGbc"])
                TT("dve", f1[0:n, 0:768], U[0:n, 0:768], Wt, ALU.mult, [Uk, "eGbc"], ["f1"])
                for w in range(3):
                    DMA(Ct, st_in[l, :, w, :], [], ["tt"])
                    DMA(Wt, cwrow_d[l, cv, w], [], ["eGbc"])
                    if w >= 1:
                        DMA(st_out[l, :, w - 1, :], Ct, ["tt"], [])
                    TT("dve", Wt, Ct, Wt, ALU.mult, ["tt", "eGbc"], ["eGbc"])
                    TT("dve", f1[0:n, 0:768], f1[0:n, 0:768], Wt, ALU.add, ["f1", "eGbc"], ["f1"])
                if bias:
                    DMA(Ct, cbrow_d[l], [], ["tt"])
                    TT("dve", f1[0:n, 0:768], f1[0:n, 0:768], Ct, ALU.add, ["f1", "tt"], ["f1"])
                ACT(Ct, f1[0:n, 0:768], AF.Exp, ["f1"], ["tt"], scale=-1.0)
                sigmoid_from_exp(Ct, "tt")
                TT("dve", f1[0:n, 0:768], f1[0:n, 0:768], Ct, ALU.mult, ["f1", "tt"], ["f1"])

            conv_tok(0, [(1024, 1536, 0), (1536, 1792, 512)], si_gc, so_gc, False)
            Ct = tt[0:n, :, :].rearrange("p a b -> p (a b)")[:, 0:512]
            ACT(Ct, f1[0:n, 0:512], AF.Square, ["f1"], ["tt"])
            RED(nb[0:n, 0:8], v3(Ct, 8), ["tt"], ["nb"])
            ACT(nb[0:n, 8:16], nb[0:n, 0:8], AF.Ln, ["nb"], ["nb"], bias=eps_t[0:n, 0:1])
            ACT(nb[0:n, 8:16], nb[0:n, 8:16], AF.Exp, ["nb"], ["nb"], scale=-0.5)
            TT("dve", v3(f1[0:n, 0:512], 8), v3(f1[0:n, 0:512], 8), bc_l(nb[0:n, 8:16], 64), ALU.mult, ["f1", "nb"], ["f1"])
            TS("dve", f1[0:n, 0:256], f1[0:n, 0:256], QK, ALU.mult, ["f1"], ["f1"])
            select([(0, 64, lambda hd: (f1[0:n, hd * 64:hd * 64 + 64], "f1")),
                    (64, 64, lambda hd: (f1[0:n, 256 + hd * 64:256 + hd * 64 + 64], "f1")),
                    (128, 64, lambda hd: (f1[0:n, 512 + hd * 64:512 + hd * 64 + 64], "f1")),
                    (192, 1, lambda hd: (egc[0:n, hd:hd + 1], "egc")),
                    (193, 1, lambda hd: (beta[0:n, hd:hd + 1], "beta"))])
            qB, kB, vB = pvs[0:64, 0:64], pvs[0:64, 64:128], pvs[0:64, 128:192]
            egB, btB = pvs[0:64, 192:193], pvs[0:64, 193:194]
            for k0, ks in state_io(si_gd, so_gd, 64):
                Sv, sk = load_slice(si_gd, k0, ks)
                T = Tview(ks)
                TT("dve", T, Sv, bc_l(kB[:, k0:k0 + ks], 64), ALU.mult, [sk, "f4"], [Tkey])
                RED(op_sb, T.rearrange("p k v -> p v k"), [Tkey], ["tmpS"])
                if k0 == 0:
                    CP("dve", w_sb, op_sb, ["tmpS"], ["tmpS"])
                else:
                    TT("dve", w_sb, w_sb, op_sb, ALU.add, ["tmpS"], ["tmpS"])
            TS("dve", w_sb, w_sb, egB, ALU.mult, ["tmpS", "f4"], ["tmpS"])
            TT("dve", u_sb, vB, w_sb, ALU.subtract, ["f4", "tmpS"], ["tmpS"])
            TS("dve", u_sb, u_sb, btB, ALU.mult, ["tmpS", "f4"], ["tmpS"])
            step_plain(si_gd, so_gd, 64, qB, kB, u_sb, sc=egB)
            po, pok = unselect(o_sb, "tmpS")
            head_norm(po[0:n, 0:256], pok, slice(256, 512), slice(256, 512))

            conv_tok(1, [(2056, 2568, 0), (2568, 2824, 512)], si_sc, so_sc, True)
            TT("dve", v3(f2[0:n, 0:256], 4), v3(f1[0:n, 0:256], 4), bc_l(dtb[0:n, 4:8], 64), ALU.mult, ["f1", "dtb"], ["f2"])
            select([(0, 128, lambda hd: (f1[0:n, 512 + (hd // 2) * 128:512 + (hd // 2) * 128 + 128], "f1")),
                    (128, 128, lambda hd: (f1[0:n, 256 + (hd // 2) * 128:256 + (hd // 2) * 128 + 128], "f1")),
                    (256, 64, lambda hd: (f2[0:n, hd * 64:hd * 64 + 64], "f2")),
                    (320, 1, lambda hd: (egc[0:n, 4 + hd:5 + hd], "egc"))])
            step_plain(si_sd, so_sd, 128, pvs[0:64, 0:128], pvs[0:64, 128:256], pvs[0:64, 256:320], sc=pvs[0:64, 320:321])
            po, pok = unselect(o_sb, "tmpS")
            TT("dve", v3(f3[0:n, 0:256], 4), v3(f1[0:n, 0:256], 4), bc_l(prm[0:n, 1296:1300], 64), ALU.mult, ["f1", "prm"], ["f3"])
            TT("dve", f3[0:n, 0:256], f3[0:n, 0:256], po[0:n, 0:256], ALU.add, ["f3", pok], ["f3"])
            TT("dve", f3[0:n, 0:256], f3[0:n, 0:256], gs[0:n, 512:768], ALU.mult, ["f3", gsk], ["f3"])
            ACT(f3[0:n, 256:512], f3[0:n, 0:256], AF.Square, ["f3"], ["f3"])
            RED(st4[0:n, 4:6], v3(f3[0:n, 256:512], 2), ["f3"], ["st4"])
            rsqrt_act(st4[0:n, 8:10], st4[0:n, 4:6], 1.0 / 128, ["st4"], ["st4"])
            TT("dve", v3(f3[0:n, 0:256], 2), v3(f3[0:n, 0:256], 2), bc_l(st4[0:n, 8:10], 128), ALU.mult, ["f3", "st4"], ["f3"])
            TT("dve", y_bf[0:n, 512:768], f3[0:n, 0:256], prm[0:n, 512:768], ALU.mult, ["f3", "prm"], ["y_bf"])

            for half in range(2):
                pt, pk = bank()
                for kk in range(4):
                    kc = half * 4 + kk
                    MM(pt[:, kk * 128:kk * 128 + n], y_bf[0:n, kc * 128:(kc + 1) * 128], ident_bf[0:n, 0:n],
                       ["y_bf", "ident_bf"], [pk])
                CP("act", yT[:, half * 4:half * 4 + 4, 0:n], v3(pt[:, :], 4)[:, :, 0:n], [pk], ["yT"])
            for cg in range(2):
                pt, pk = bank()
                for kc in range(8):
                    MM(pt[0:n, :], yT[:, kc, 0:n], wout[:, kc, cg * 512:(cg + 1) * 512], ["yT", "wout"], [pk],
                       start=(kc == 0), stop=(kc == 7))
                TT("dve", hs[:, cg * 512:(cg + 1) * 512], hs[:, cg * 512:(cg + 1) * 512], pt[0:n, :], ALU.add, ["hs", pk], ["hs"])
            if last:
                ACT(hn_bf[0:n, :], hs[:, :], AF.Square, ["hs"], ["hn_bf", "st4"], accum=st4[0:n, 0:1])
                rsqrt_act(st4[0:n, 1:2], st4[0:n, 0:1], 1.0 / D, ["st4"], ["st4"])
                STT(hs[:, :], hs[:, :], st4[0:n, 1:2], finw[0:n, :], ALU.mult, ALU.mult, ["hs", "st4", "lgt"], ["hs"])
                DMA(y_s, hs[:, :], ["hs"], [])

        for l in range(depth):
            load_layer(l)
            if not _os0.environ.get("NO_SAMPLE"):
                sample_fwd(l, l == depth - 1)
            if ntiles > 0:
                stage0(l, 0)
                stage0_pe(l, 0)
            for t in range(ntiles):
                more = t + 1 < ntiles
                tile_fwd(l, t, (lambda l_=l, t_=t: stage0(l_, t_ + 1)) if more else None,
                         (lambda l_=l, t_=t: stage0_pe(l_, t_ + 1)) if more else None)

        if _AUDIT:
            for b_ in sorted(_bad, key=str):
                print("AUDIT missing key:", b_)
        P.emit(es)
    return nc


_NC_CACHE = {}


def _prep_inputs(inp, c):
    f = np.float32
    prm = np.zeros((DEPTH, 128, NPRM), f)
    for l in range(DEPTH):
        row = np.concatenate([inp["hgrn_norm_w"][l], inp["gdn_norm_w"][l], inp["ssd_norm_w"][l], inp["ret_norm_w"][l],
                              inp["ret_norm_b"][l], inp["gdn_a_log"][l], inp["ssd_a_log"][l], inp["gdn_dt_bias"][l],
                              inp["ssd_dt_bias"][l], inp["ssd_d"][l]]).astype(f)
        prm[l] = np.broadcast_to(row[None, :], (128, NPRM))
    lgt = np.ascontiguousarray(np.broadcast_to(inp["hgrn_lb_logits"].reshape(1, 1024), (128, 1024))).astype(f)
    finw = np.ascontiguousarray(np.broadcast_to(inp["final_norm_w"].reshape(1, 1024), (128, 1024))).astype(f)
    featp = np.zeros((128, 32 + 192), f)
    featp[:, 0:32] = inp["norm_w"].reshape(DEPTH, 8, 128).transpose(2, 0, 1).reshape(128, 32)
    for l in range(DEPTH):
        for cv, key in enumerate(("gdn_conv_w", "ssd_conv_w")):
            w = inp[key][l].reshape(4, 6, 128)
            featp[:, 32 + l * 48 + cv * 24:32 + l * 48 + cv * 24 + 24] = w.transpose(2, 1, 0).reshape(128, 24)
    cbias = np.ascontiguousarray(inp["ssd_conv_b"].reshape(1, DEPTH * 768)).astype(f)
    cw = np.stack([inp["gdn_conv_w"], inp["ssd_conv_w"]], 1).astype(f)
    cwrow = np.ascontiguousarray(np.broadcast_to(cw[:, :, :, None, :], (DEPTH, 2, 4, NS, 768)))
    cbrow = np.ascontiguousarray(np.broadcast_to(inp["ssd_conv_b"].astype(f)[:, None, :], (DEPTH, NS, 768)))
    return {
        "xp": np.ascontiguousarray(inp["x_prompt"][c]).astype(f),
        "meta": np.ascontiguousarray(inp["meta_tokens"]).astype(f),
        "w_in": np.ascontiguousarray(inp["w_in"]).astype(f),
        "w_out": np.ascontiguousarray(inp["w_out"]).astype(f),
        "cst": CST, "rot": ROT, "prm": prm, "lgt": lgt, "finw": finw, "featp": featp, "cbias": cbias,
        "cwrow": cwrow, "cbrow": cbrow, **_sample_inputs(inp, c),
    }


def _sample_inputs(inp, c):
    f = np.float32
    sl = slice(c * NS, (c + 1) * NS)
    return {
        "xs_in": np.ascontiguousarray(inp["x_sample"][sl, 0, :]).astype(f),
        "si_hg": np.ascontiguousarray(inp["state_hgrn"][:, sl]).astype(f),
        "si_gd": np.ascontiguousarray(inp["state_gdn"][:, sl]).astype(f),
        "si_gc": np.ascontiguousarray(inp["state_gdn_conv"][:, sl]).astype(f),
        "si_sd": np.ascontiguousarray(inp["state_ssd"][:, sl]).astype(f),
        "si_sc": np.ascontiguousarray(inp["state_ssd_conv"][:, sl]).astype(f),
        "si_rt": np.ascontiguousarray(inp["state_ret"][:, sl]).astype(f),
    }


def kernel(**inp):
    inp = {k: np.asarray(v) for k, v in inp.items()}
    if "nc" not in _NC_CACHE:
        _NC_CACHE["nc"] = build_program()
    nc = _NC_CACHE["nc"]
    shared = None
    in_maps = []
    for c in range(8):
        m = _prep_inputs(inp, c) if shared is None else dict(shared)
        if shared is None:
            shared = m
        else:
            m["xp"] = np.ascontiguousarray(inp["x_prompt"][c]).astype(np.float32)
            m.update(_sample_inputs(inp, c))
        in_maps.append(m)
    res = run_bass_kernel_spmd(nc, in_maps, core_ids=list(range(8)))
    R = res.results
    y_prompt = np.stack([R[c]["y_p"] for c in range(8)], 0)
    def stk(name):
        return np.ascontiguousarray(np.stack([R[c][name] for c in range(8)], 1))
    y_sample = np.concatenate([R[c]["y_s"] for c in range(8)], 0)[:, None, :]
    def cat(name):
        return np.ascontiguousarray(np.concatenate([R[c][name] for c in range(8)], 1))
    outs = (y_prompt, np.ascontiguousarray(y_sample),
            stk("st_hg"), stk("st_gd"), stk("st_gc"), stk("st_sd"), stk("st_sc"), stk("st_rt"),
            cat("so_hg"), cat("so_gd"), cat("so_gc"), cat("so_sd"), cat("so_sc"), cat("so_rt"))
    return outs
```

```python
import contextlib
import math
import numpy as np
import concourse.bass as bass
import concourse.mybir as mybir
from concourse.bass_utils import run_bass_kernel_spmd

F32 = mybir.dt.float32
BF16 = mybir.dt.bfloat16
AF = mybir.ActivationFunctionType
ALU = mybir.AluOpType
AX = mybir.AxisListType

D = 1024
DEPTH = 4
SEQ = 2048
NT = 17
IN_DIM = 4108
EPS = 1e-6
QK = 0.125
NS = 16
NPRM = 1300
NEGV = -30000.0
import os as _os0
EMBED_WAIT = not _os0.environ.get("NO_EMBED")
ANNOTATE = bool(_os0.environ.get("ANNOTATE"))


class Prog:
    ENG = ("pe", "act", "dve", "pool", "sp")

    def __init__(self, nc, n_dma_sems=8):
        self.nc = nc
        self.ops = []
        self.cnt = {}
        self.clock = {e: {} for e in self.ENG}
        self.tok_clock = {}
        self.last_w = {}
        self.readers = {}
        self.n_dma = n_dma_sems
        self.dma_rr = {e: 0 for e in self.ENG}
        self.dma_last = {}
        self.anns = []
        import os
        self.pe_skip = not os.environ.get("PE_SELFWAIT")
        self.strict_same = not os.environ.get("RELAX_SAME")

    def _need(self, eng, tok, waits, force=False):
        key, idx = tok
        if key == "pe" and eng == "pe" and self.pe_skip and not force:
            return
        if self.clock[eng].get(key, 0) >= idx:
            return
        if waits.get(key, 0) < idx:
            waits[key] = idx

    def op(self, eng, fn, reads=(), writes=(), dma=False, pe_serial=False):
        waits = {}
        for b in reads:
            t = self.last_w.get(b)
            if t:
                self._need(eng, t, waits)
            if b.startswith("ps"):
                for r in self.readers.get(b, ()):
                    if r[0] != eng:
                        self._need(eng, r, waits)
        for b in writes:
            t = self.last_w.get(b)
            if t and (t[0] != eng or pe_serial or dma or self.strict_same):
                self._need(eng, t, waits, force=pe_serial)
            for r in self.readers.get(b, ()):
                if r[0] != eng or dma or self.strict_same:
                    self._need(eng, r, waits)
        if dma:
            key = ("dma", eng, self.dma_rr[eng] % self.n_dma)
            self.dma_rr[eng] += 1
            prev = self.dma_last.get(key)
            if prev:
                self._need(eng, prev, waits)
        else:
            key = eng
        ck = self.clock[eng]
        for kk, ii in waits.items():
            for k2, i2 in self.tok_clock.get((kk, ii), {}).items():
                if ck.get(k2, 0) < i2:
                    ck[k2] = i2
            if ck.get(kk, 0) < ii:
                ck[kk] = ii
        self.cnt[key] = self.cnt.get(key, 0) + 1
        tok = (key, self.cnt[key])
        if dma:
            self.dma_last[key] = tok
            snap = dict(ck)
            snap[key] = tok[1]
            self.tok_clock[tok] = snap
        else:
            snap = dict(ck)
            snap[key] = tok[1]
            self.tok_clock[tok] = snap
        for b in writes:
            self.last_w[b] = tok
            self.readers[b] = []
        for b in reads:
            if b not in writes:
                self.readers.setdefault(b, []).append(tok)
        ann = None
        if ANNOTATE:
            import sys as _sys
            f = _sys._getframe(1)
            while f:
                if f.f_code.co_name in ("tile_fwd", "sample_fwd", "load_layer"):
                    ann = "L%d" % f.f_lineno
                    break
                f = f.f_back
        self.anns.append(ann)
        self.ops.append((eng, fn, list(waits.items()), tok, dma))
        return tok

    def emit(self, es, final_wait_eng="sp"):
        nc = self.nc
        import os
        km = int(os.environ.get("KMAX", "0"))
        if km:
            self.ops = self.ops[:km]
            self.dma_last = {}
            for (e_, f_, w_, tok_, d_) in self.ops:
                if d_:
                    self.dma_last[tok_[0]] = tok_
        needed = set()
        for (_, _, waits, _, _) in self.ops:
            for w in waits:
                needed.add(w)
        finals = []
        for k, t in self.dma_last.items():
            finals.append(t)
            needed.add(t)
        per_key = {}
        for (k, i) in needed:
            per_key.setdefault(k, []).append(i)
        sigcount = {}
        for k, lst in per_key.items():
            for n, i in enumerate(sorted(lst)):
                sigcount[(k, i)] = n + 1
        sems = {}
        for k in sorted(per_key.keys(), key=str):
            nm = "s_" + "_".join(str(x) for x in (k if isinstance(k, tuple) else (k,)))
            sems[k] = es.enter_context(nc.semaphore(nm))
        per_eng = {e: [] for e in self.ENG}
        for j, o in enumerate(self.ops):
            per_eng[o[0]].append(o + (self.anns[j] if j < len(self.anns) else None,))
        blk = es.enter_context(nc.Block())

        def run(e, engobj):
            for (_, fn, waits, tok, dma, ann) in per_eng[e]:
                emb = None
                if waits and EMBED_WAIT and not dma:
                    emb = waits[-1]
                    waits = waits[:-1]
                for (k, i) in waits:
                    mult = 16 if isinstance(k, tuple) else 1
                    engobj.wait_ge(sems[k], sigcount[(k, i)] * mult)
                ins = fn(engobj)
                if ann is not None:
                    ins.annotate(ann)
                if emb is not None:
                    k, i = emb
                    ins._wait_ge(sems[k], sigcount[(k, i)] * (16 if isinstance(k, tuple) else 1))
                if tok in sigcount:
                    ins.then_inc(sems[tok[0]], 16 if dma else 1)
            if e == final_wait_eng:
                for t in finals:
                    engobj.wait_ge(sems[t[0]], sigcount[t] * 16)

        @blk.tensor
        def _(e):
            run("pe", e)

        @blk.scalar
        def _(e):
            run("act", e)

        @blk.vector
        def _(e):
            run("dve", e)

        @blk.gpsimd
        def _(e):
            run("pool", e)

        @blk.sync
        def _(e):
            run("sp", e)


def host_consts():
    idx = np.arange(128)
    ch = idx // 64
    same = ch[:, None] == ch[None, :]
    ident = np.eye(128, dtype=np.float32)
    maskT = (same & (idx[:, None] <= idx[None, :])).astype(np.float32)
    negT = np.where(maskT > 0, 0.0, NEGV).astype(np.float32)
    strict = (same & (idx[None, :] < idx[:, None]))
    negS = np.where(strict, 0.0, NEGV).astype(np.float32)
    mid = ch * 64 + 31
    uprime = (same & (idx[:, None] <= idx[None, :])).astype(np.float32) - \
             (same & (idx[:, None] <= mid[None, :])).astype(np.float32)
    urev = (same & (idx[:, None] > idx[None, :])).astype(np.float32)
    wc = np.zeros((128, 8), np.float32)
    wc[:, 0] = (idx <= 31)
    wc[:, 1] = (idx >= 64) & (idx <= 95)
    wc[:, 2] = (idx >= 32) & (idx <= 63)
    wc[:, 3] = (idx >= 96)
    wc[:, 4] = (idx <= 63)
    wc[:, 5] = (idx >= 64)
    blockones = same.astype(np.float32)
    lg = np.log1p(-np.exp2(-5.0 - np.arange(4, dtype=np.float64)))
    loc = idx % 64
    dt_ret = np.zeros((128, 4, 128), np.float64)
    for h in range(4):
        dt_ret[:, h, :] = np.where(maskT > 0, np.exp(lg[h] * (idx[None, :] - idx[:, None])), 0.0) * QK
    egq = np.zeros((128, 2, 128), np.float64)
    for hp in range(2):
        for hh in range(2):
            egq[hh * 64:(hh + 1) * 64, hp, :] = np.exp(lg[2 * hp + hh] * (loc[None, :] + 1))
    egrev64 = np.zeros((128, 4), np.float64)
    egrev16 = np.zeros((128, 4), np.float64)
    for h in range(4):
        egrev64[:, h] = np.exp(lg[h] * (63 - loc)) * QK
        egrev16[:, h] = np.exp(lg[h] * np.maximum(15 - idx, 0)) * QK
    egl = np.zeros((128, 2, 2), np.float64)
    for hp in range(2):
        for hh in range(2):
            egl[hh * 64:(hh + 1) * 64, hp, 0] = np.exp(lg[2 * hp + hh] * 16)
            egl[hh * 64:(hh + 1) * 64, hp, 1] = np.exp(lg[2 * hp + hh] * 64)
    sel = np.zeros((128, 4, 64), np.float32)
    selT = np.zeros((128, 4, 16), np.float32)
    gam64 = np.zeros((128, 4), np.float32)
    for h in range(4):
        for b in range(16):
            sel[b, h, h * 16 + b] = 1.0
            selT[h * 16 + b, h, b] = 1.0
            gam64[h * 16 + b, 0] = np.exp(lg[h])
    parts = [ident, maskT, negT, negS, uprime, urev, wc, blockones,
             dt_ret.reshape(128, 512), egq.reshape(128, 256), egrev64, egrev16, egl.reshape(128, 4),
             sel.reshape(128, 256), selT.reshape(128, 64), gam64]
    offs = {}
    names = ["ident", "maskT", "negT", "negS", "uprime", "urev", "wc", "blockones",
             "dt_ret", "egq", "egrev64", "egrev16", "egl", "sel", "selT", "gam64"]
    o = 0
    for nm, p in zip(names, parts):
        offs[nm] = (o, p.shape[1])
        o += p.shape[1]
    cst = np.concatenate([p.astype(np.float32) for p in parts], axis=1)
    half = 32
    inv_freq = (1.0 / (np.float32(10000.0) ** np.linspace(0.0, 1.0, half, dtype=np.float32))).astype(np.float32)
    rot = np.zeros((NT + 1, 128, 64), np.float32)
    for t in range(NT):
        pos = (np.arange(128) if t == 0 else 16 + (t - 1) * 128 + np.arange(128)).astype(np.float32)
        ang = (pos[:, None] * inv_freq[None, :]).astype(np.float32)
        rot[t, :, 0:32] = np.cos(ang)
        rot[t, :, 32:64] = np.sin(ang)
    ang = (np.full((128, 1), 16384.0, np.float32) * inv_freq[None, :]).astype(np.float32)
    rot[NT, :, 0:32] = np.cos(ang)
    rot[NT, :, 32:64] = np.sin(ang)
    return cst, offs, rot


CST, COFF, ROT = host_consts()
NCST = CST.shape[1]


def build_program(depth=DEPTH, ntiles=NT):
    nc = bass.Bass("TRN2", target_bir_lowering=False)

    def din(name, shape):
        return nc.dram_tensor(name, list(shape), F32, kind="ExternalInput").ap()

    def dout(name, shape):
        return nc.dram_tensor(name, list(shape), F32, kind="ExternalOutput").ap()

    xp = din("xp", [SEQ, D])
    meta = din("meta", [16, D])
    w_in = din("w_in", [DEPTH, D, IN_DIM])
    w_out = din("w_out", [DEPTH, D, D])
    cst_d = din("cst", [128, NCST])
    rot_d = din("rot", [NT + 1, 128, 64])
    prm_d = din("prm", [DEPTH, 128, NPRM])
    lgt_d = din("lgt", [128, 1024])
    finw_d = din("finw", [128, 1024])
    featp_d = din("featp", [128, 32 + 192])
    cbias_d = din("cbias", [1, DEPTH * 768])

    y_p = dout("y_p", [SEQ, D])
    st_hg = dout("st_hg", [DEPTH, 4, 64, 64])
    st_gd = dout("st_gd", [DEPTH, 4, 64, 64])
    st_gc = dout("st_gc", [DEPTH, 3, 768])
    st_sd = dout("st_sd", [DEPTH, 4, 128, 64])
    st_sc = dout("st_sc", [DEPTH, 3, 768])
    st_rt = dout("st_rt", [DEPTH, 4, 64, 64])
    xs_d = din("xs_in", [NS, D])
    si_hg = din("si_hg", [DEPTH, NS, 4, 64, 64]); si_gd = din("si_gd", [DEPTH, NS, 4, 64, 64])
    si_gc = din("si_gc", [DEPTH, NS, 3, 768]); si_sd = din("si_sd", [DEPTH, NS, 4, 128, 64])
    si_sc = din("si_sc", [DEPTH, NS, 3, 768]); si_rt = din("si_rt", [DEPTH, NS, 4, 64, 64])
    cwrow_d = din("cwrow", [DEPTH, 2, 4, NS, 768])
    cbrow_d = din("cbrow", [DEPTH, NS, 768])
    y_s = dout("y_s", [NS, D])
    so_hg = dout("so_hg", [DEPTH, NS, 4, 64, 64]); so_gd = dout("so_gd", [DEPTH, NS, 4, 64, 64])
    so_gc = dout("so_gc", [DEPTH, NS, 3, 768]); so_sd = dout("so_sd", [DEPTH, NS, 4, 128, 64])
    so_sc = dout("so_sc", [DEPTH, NS, 3, 768]); so_rt = dout("so_rt", [DEPTH, NS, 4, 64, 64])

    with contextlib.ExitStack() as es:
        def sb(name, shape, dt=F32):
            return es.enter_context(nc.sbuf_tensor("sb_" + name, list(shape), dt))

        P = Prog(nc)

        hscr = nc.dram_tensor("hscr", [NT * 128, D], F32, kind="Internal").ap()
        hb = [sb(f"hb{i}", [128, D]) for i in range(2)]
        win = sb("win", [128, 8, IN_DIM], BF16)
        wout = sb("wout", [128, 8, D], BF16)
        WCH = 1027
        wst = [sb(f"wst{i}", [128, WCH]) for i in range(2)]
        cst = sb("cst", [128, NCST])
        ident_bf = sb("ident_bf", [128, 128], BF16)
        bones_bf = sb("bones_bf", [128, 128], BF16)
        dtret_bf = sb("dtret_bf", [128, 4, 128], BF16)
        ones_bf = sb("ones_bf", [1, 128], BF16)
        prm = sb("prm", [128, NPRM])
        oml = sb("oml", [128, 4, 256])
        lgt = sb("lgt", [128, 4, 256])
        featp = sb("featp", [128, 32 + 192])
        dg = sb("dg", [128, 12, 4, 128], BF16)
        cbias_bf = sb("cbias_bf", [1, 768], BF16)
        nega = sb("nega", [128, 64])
        rot = [sb(f"rot{i}", [128, 64]) for i in range(2)]

        def C(name, rows=slice(0, 128), lo=0, hi=None):
            o, w = COFF[name]
            hi = w if hi is None else hi
            return cst[rows, o + lo:o + hi]

        psb = [es.enter_context(nc.psum_tensor(f"ps{i}", [128, 512], F32)) for i in range(8)]
        ps_rr = [0]

        def bank():
            i = ps_rr[0] % 4 if ps_rr[0] < 0 else (0, 1, 2, 3, 6, 7)[ps_rr[0] % 6]
            ps_rr[0] += 1
            return psb[i], f"ps{i}"

        import os as _os
        _AUDIT = bool(_os.environ.get("AUDIT"))
        _bad = set()

        def _chk(r, w, outs, ins):
            if not _AUDIT:
                return
            for grp, keys, what in ((outs, list(w), "W"), (ins, list(r) + list(w), "R")):
                for ap in grp:
                    nm = getattr(ap, "name", None)
                    if not isinstance(nm, str):
                        continue
                    key = nm[3:] if nm.startswith("sb_") else nm
                    if key not in keys:
                        import traceback
                        fr = traceback.extract_stack()[-3]
                        _bad.add((what, key, fr.lineno))

        _last_rb = {}

        def MM(out, lhsT, rhs, r, w, start=True, stop=True):
            skip = any(k in ("ps4", "ps5") for k in w)
            _chk(r, w, [out], [lhsT, rhs])
            rb = lhsT.base_partition()
            ser = False
            for k in w:
                if _last_rb.get(k, rb) != rb:
                    ser = True
                _last_rb[k] = rb
            P.op("pe", lambda e: e.matmul(out, lhsT=lhsT, rhs=rhs, start=start, stop=stop, skip_group_check=skip),
                 reads=r, writes=w, pe_serial=ser)

        def ACT(out, in_, func, r, w, scale=1.0, bias=None, accum=None):
            kw = {}
            if bias is not None:
                kw["bias"] = bias
            if accum is not None:
                kw["accum_out"] = accum
            if hasattr(bias, "name") and "eps_t" not in r:
                r = list(r) + ["eps_t"]
            _chk(r, w, [out] + ([accum] if accum is not None else []), [in_] + [x for x in (scale, bias) if hasattr(x, "name")])
            P.op("act", lambda e: e.activation(out=out, in_=in_, func=func, scale=scale, **kw), reads=r, writes=w)

        def TT(eng, out, in0, in1, op, r, w):
            _chk(r, w, [out], [in0, in1])
            P.op(eng, lambda e: e.tensor_tensor(out=out, in0=in0, in1=in1, op=op), reads=r, writes=w)

        def TS(eng, out, in0, s1, op0, r, w, s2=None, op1=None):
            _chk(r, w, [out], [in0] + [x for x in (s1, s2) if hasattr(x, "name")])
            if op1 is None:
                P.op(eng, lambda e: e.tensor_scalar(out=out, in0=in0, scalar1=s1, scalar2=None, op0=op0), reads=r, writes=w)
            else:
                P.op(eng, lambda e: e.tensor_scalar(out=out, in0=in0, scalar1=s1, scalar2=s2, op0=op0, op1=op1), reads=r, writes=w)

        def STT(out, in0, scalar, in1, op0, op1, r, w):
            _chk(r, w, [out], [in0, in1] + [x for x in (scalar,) if hasattr(x, "name")])
            P.op("dve", lambda e: e.scalar_tensor_tensor(out=out, in0=in0, scalar=scalar, in1=in1, op0=op0, op1=op1),
                 reads=r, writes=w)

        def RED(out, in_, r, w):
            _chk(r, w, [out], [in_])
            P.op("dve", lambda e: e.tensor_reduce(out=out, in_=in_, axis=AX.X, op=ALU.add), reads=r, writes=w)

        def RECIP(out, in_, r, w):
            _chk(r, w, [out], [in_])
            P.op("dve", lambda e: e.reciprocal(out=out, in_=in_), reads=r, writes=w)

        def CP(eng, out, in_, r, w):
            if eng == "act":
                ACT(out, in_, AF.Copy, r, w)
            else:
                _chk(r, w, [out], [in_])
                P.op(eng, lambda e: e.tensor_copy(out=out, in_=in_), reads=r, writes=w)

        def MEMSET(eng, ap, val, w):
            P.op(eng, lambda e: e.memset(ap, val), reads=(), writes=w)

        def DMA(out, in_, r, w, slow=False):
            if slow:
                P.op("sp", lambda e: e.dma_start(out=out, in_=in_, allow_slow_non_contiguous=True), reads=r, writes=w, dma=True)
            else:
                P.op("sp", lambda e: e.dma_start(out=out, in_=in_), reads=r, writes=w, dma=True)

        def sigmoid_from_exp(buf, key):
            ACT(buf, buf, AF.Ln, [key], [key], bias=eps_t[0:buf.shape[0], 1:2])
            ACT(buf, buf, AF.Exp, [key], [key], scale=-1.0)

        def rsqrt_act(out, in_, scale, r, w):
            ACT(out, in_, AF.Ln, r, w, scale=scale, bias=eps_t[0:out.shape[0], 0:1])
            ACT(out, out, AF.Exp, w, w, scale=-0.5)

        eps_t = sb("eps_t", [128, 2])
        MEMSET("pool", eps_t[:, 0:1], EPS, ["eps_t"])
        MEMSET("pool", eps_t[:, 1:2], 1.0, ["eps_t"])
        DMA(cst[:], cst_d, [], ["cst"])
        DMA(lgt[:].rearrange("p a b -> p (a b)"), lgt_d, [], ["lgt"])
        DMA(featp[:], featp_d, [], ["featp"])
        CP("dve", ident_bf[:], C("ident"), ["cst"], ["ident_bf"])
        CP("dve", bones_bf[:], C("blockones"), ["cst"], ["bones_bf"])
        CP("dve", dtret_bf[:].rearrange("p a b -> p (a b)"), C("dt_ret"), ["cst"], ["dtret_bf"])
        MEMSET("pool", ones_bf[:], 1.0, ["ones_bf"])
        mx = wst[1][:, 0:256]
        TT("dve", mx, lgt[:, 0, :], lgt[:, 1, :], ALU.max, ["lgt"], ["wst1"])
        TT("dve", mx, mx, lgt[:, 2, :], ALU.max, ["lgt", "wst1"], ["wst1"])
        TT("dve", mx, mx, lgt[:, 3, :], ALU.max, ["lgt", "wst1"], ["wst1"])
        TT("dve", lgt[:], lgt[:], mx.unsqueeze(1).to_broadcast([128, 4, 256]), ALU.subtract, ["lgt", "wst1"], ["lgt"])
        ACT(lgt[:], lgt[:], AF.Exp, ["lgt"], ["lgt"])
        TT("dve", mx, lgt[:, 0, :], lgt[:, 1, :], ALU.add, ["lgt"], ["wst1"])
        TT("dve", mx, mx, lgt[:, 2, :], ALU.add, ["lgt", "wst1"], ["wst1"])
        TT("dve", mx, mx, lgt[:, 3, :], ALU.add, ["lgt", "wst1"], ["wst1"])
        RECIP(mx, mx, ["wst1"], ["wst1"])
        TT("dve", lgt[:], lgt[:], mx.unsqueeze(1).to_broadcast([128, 4, 256]), ALU.mult, ["lgt", "wst1"], ["lgt"])
        MEMSET("dve", oml[:, 0, :], 0.0, ["oml"])
        CP("dve", oml[:, 1, :], lgt[:, 1, :], ["lgt"], ["oml"])
        TT("dve", oml[:, 2, :], oml[:, 1, :], lgt[:, 2, :], ALU.add, ["lgt", "oml"], ["oml"])
        TT("dve", oml[:, 3, :], oml[:, 2, :], lgt[:, 3, :], ALU.add, ["lgt", "oml"], ["oml"])
        TS("dve", oml[:], oml[:], 0.0, ALU.max, ["oml"], ["oml"])
        TS("dve", oml[:], oml[:], -1.0, ALU.mult, ["oml"], ["oml"], s2=1.0, op1=ALU.add)
        DMA(lgt[:].rearrange("p a b -> p (a b)"), finw_d, ["lgt"], ["lgt"])
        finw = lgt[:].rearrange("p a b -> p (a b)")

        hn_bf = sb("hn_bf", [128, D], BF16)
        hnTs = [sb(f"hnT{i}", [128, 8, 128], BF16) for i in range(2)]
        hnT = hnTs[1]
        st0 = sb("st0", [128, 2])
        st4 = sb("st4", [128, 16])
        f1 = sb("f1", [128, 768])
        f2 = sb("f2", [128, 512])
        f3 = sb("f3", [128, 512])
        f4 = sb("f4", [128, 512])
        gate = sb("gate", [128, D])
        y_bf = sb("y_bf", [128, D], BF16)
        yT = sb("yT", [128, 8, 128], BF16)
        b1 = sb("b1", [128, 4, 128], BF16)
        qkT = sb("qkT", [128, 4, 128], BF16)
        AT = sb("AT", [128, 4, 128], BF16)
        AT2 = sb("AT2", [128, 4, 128], BF16)
        v_bf = sb("v_bf", [128, 4, 64], BF16)
        kp_bf = sb("kp_bf", [128, 4, 128], BF16)
        qpT = sb("qpT", [128, 4, 128], BF16)
        ecs = sb("ecs", [128, 2, 8])
        S_A = sb("S_A", [128, 2, 64]); Sb_A = sb("Sb_A", [128, 2, 64], BF16); Sd_A = sb("Sd_A", [128, 2, 64])
        S_B = sb("S_B", [128, 2, 64]); Sb_B = sb("Sb_B", [128, 2, 64], BF16)
        S_C = sb("S_C", [128, 4, 64]); Sb_C = sb("Sb_C", [128, 4, 64], BF16)
        S_D = sb("S_D", [128, 2, 64]); Sb_D = sb("Sb_D", [128, 2, 64], BF16)
        tmpS = sb("tmpS", [128, 4, 64])
        uT = [sb(f"uT{i}", [128, 12, 131], BF16) for i in range(2)]
        cvst = sb("cvst", [128, 12, 3])
        xs = sb("xs", [128, 12, 128], BF16)
        xsf = sb("xsf", [128, 4, 128])
        g8 = sb("g8", [128, 64])
        gc = sb("gc", [128, 64])
        egc = sb("egc", [128, 64])
        beta = sb("beta", [128, 64])
        dtb = sb("dtb", [128, 64])
        nb = sb("nb", [128, 64])
        nb2 = sb("nb2", [128, 64])
        for _t, _k in ((g8, "g8"), (beta, "beta"), (nega, "nega")):
            MEMSET("pool", _t[:], 0.0, [_k])
        tt = sb("tt", [128, 8, 128])
        DT = sb("DT", [128, 8, 128], BF16)
        Dst = sb("Dst", [128, 4, 128], BF16)
        eGbc = sb("eGbc", [128, 8, 128])
        eGlB = sb("eGlB", [128, 2, 2])
        X_bf = [sb(f"X_bf{i}", [128, 4, 128], BF16) for i in range(2)]
        Y_bf = [sb(f"Y_bf{i}", [128, 4, 128], BF16) for i in range(2)]
        P_bf = sb("P_bf", [128, 4, 128], BF16)
        bv = sb("bv", [128, 4, 64])
        r_bf = sb("r_bf", [128, 4, 64], BF16)
        u_bf = sb("u_bf", [128, 4, 64], BF16)
        xd_bf = sb("xd_bf", [128, 4, 64], BF16)

        def load_layer(l):
            DMA(prm[:], prm_d[l], [], ["prm"])
            DMA(wst[0][0:1, 0:768], cbias_d[0:1, l * 768:(l + 1) * 768], [], ["wst0"])
            CP("pool", cbias_bf[:], wst[0][0:1, 0:768], ["wst0"], ["cbias_bf"])
            ACT(nega[:, 0:8], prm[:, 1280:1288], AF.Exp, ["prm"], ["nega"])
            TS("dve", nega[:, 0:64], nega[:, 0:64], -1.0, ALU.mult, ["nega"], ["nega"])
            for cv in range(2):
                for blk in range(6):
                    for w in range(4):
                        col = 32 + l * 48 + cv * 24 + blk * 4 + w
                        if (blk + w) % 2 == 0:
                            ACT(dg[:, cv * 6 + blk, w, :], ident_bf[:], AF.Copy, ["ident_bf", "featp"], ["dg"],
                                scale=featp[:, col:col + 1])
                        else:
                            TS("dve", dg[:, cv * 6 + blk, w, :], ident_bf[:], featp[:, col:col + 1], ALU.mult,
                               ["ident_bf", "featp"], ["dg"])
            i = 0
            for kc in range(8):
                for c0 in range(0, IN_DIM, WCH):
                    st = wst[i % 2]; sk = f"wst{i % 2}"
                    DMA(st[:, 0:WCH], w_in[l, kc * 128:(kc + 1) * 128, c0:c0 + WCH], [], [sk])
                    if i % 2 == 0:
                        ACT(win[:, kc, c0:c0 + WCH], st[:, 0:WCH], AF.Copy, [sk, "featp"], ["win"],
                            scale=featp[:, l * 8 + kc:l * 8 + kc + 1])
                    else:
                        TS("dve", win[:, kc, c0:c0 + WCH], st[:, 0:WCH], featp[:, l * 8 + kc:l * 8 + kc + 1], ALU.mult,
                           [sk, "featp"], ["win"])
                    i += 1
            for kc in range(8):
                st = wst[i % 2]; sk = f"wst{i % 2}"
                DMA(st[:, 0:1024], w_out[l, kc * 128:(kc + 1) * 128, :], [], [sk])
                CP("act" if i % 2 == 0 else "dve", wout[:, kc, :], st[:, 0:1024], [sk], ["wout"])
                i += 1
            for nm, S_, Sb_ in (("A", S_A, Sb_A), ("B", S_B, Sb_B), ("C", S_C, Sb_C), ("D", S_D, Sb_D)):
                MEMSET("pool", S_[:], 0.0, ["S_" + nm])
                MEMSET("pool", Sb_[:], 0.0, ["Sb_" + nm])
            MEMSET("pool", uT[0][:, :, 0:3], 0.0, ["uT0"])

        def stage0(l, t):
            n = 16 if t == 0 else 128
            hk = f"hb{t % 2}"
            ht = hb[t % 2][0:n, :]
            hnT = hnTs[t % 2]; hnTk = f"hnT{t % 2}"
            if l == 0:
                DMA(ht, meta if t == 0 else xp[(t - 1) * 128:t * 128, :], [], [hk])
            else:
                DMA(ht, hscr[t * 128:t * 128 + n, :], [f"hd{t}"], [hk])
            ACT(hn_bf[0:n, :], ht, AF.Square, [hk], ["hn_bf", "st0"], accum=st0[0:n, 0:1])
            ACT(st0[0:n, 1:2], st0[0:n, 0:1], AF.Ln, ["st0"], ["st0"], scale=1.0 / D, bias=eps_t[0:n, 0:1])
            ACT(st0[0:n, 1:2], st0[0:n, 1:2], AF.Exp, ["st0"], ["st0"], scale=-0.5)
            ACT(hn_bf[0:n, :], ht, AF.Copy, [hk, "st0"], ["hn_bf"], scale=st0[0:n, 1:2])

        def stage0_pe(l, t):
            n = 16 if t == 0 else 128
            hnT = hnTs[t % 2]; hnTk = f"hnT{t % 2}"
            for half in range(2):
                pt, pk = bank()
                for kk in range(4):
                    kc = half * 4 + kk
                    MM(pt[:, kk * 128:kk * 128 + n], hn_bf[0:n, kc * 128:(kc + 1) * 128], ident_bf[0:n, 0:n],
                       ["hn_bf", "ident_bf"], [pk])
                CP("act" if half else "dve", hnT[:, half * 4:half * 4 + 4, 0:n],
                   pt[:, :].rearrange("p (a b) -> p a b", a=4)[:, :, 0:n], [pk], [hnTk])

        def tile_fwd(l, t, mid_hook=None, late_hook=None):
            n = 16 if t == 0 else 128
            chunks = [(0, 16)] if t == 0 else [(0, 64), (64, 128)]
            nch = len(chunks)
            clen = chunks[0][1]
            last_tile = (t == ntiles - 1)
            hk = f"hb{t % 2}"
            ht = hb[t % 2][0:n, :]
            hnT = hnTs[t % 2]; hnTk = f"hnT{t % 2}"
            cur = uT[t % 2]; curk = f"uT{t % 2}"
            pO2, kO2 = psb[5], "ps5"
            pOC, kOC = psb[5], "ps5"
            nxt = uT[(t + 1) % 2]; nxtk = f"uT{(t + 1) % 2}"
            rt = rot[t % 2]; rtk = f"rot{t % 2}"
            DMA(rt[:], rot_d[t], [], [rtk])

            def bc_h(ap2d, nh=4):
                return ap2d.unsqueeze(1).to_broadcast([ap2d.shape[0], nh, ap2d.shape[1]])

            def bc_l(ap2d, m):
                return ap2d.unsqueeze(2).to_broadcast([ap2d.shape[0], ap2d.shape[1], m])

            def proj_tok(c0, c1, extra=None):
                pt, pk = bank()
                for kc in range(8):
                    MM(pt[0:n, 0:c1 - c0], hnT[:, kc, 0:n], win[:, kc, c0:c1], [hnTk, "win"], [pk],
                       start=(kc == 0), stop=(kc == 7 and extra is None))
                if extra is not None:
                    MM(pt[0:n, extra[0]:extra[0] + 4], C("ident", slice(0, n), 0, n), prm[0:n, extra[1]:extra[1] + 4],
                       ["cst", "prm"], [pk], start=False, stop=True)
                return pt, pk

            def proj_feat(cols0, nblk):
                pt, pk = bank()
                for b_ in range(nblk):
                    for kc in range(8):
                        MM(pt[:, b_ * 128:b_ * 128 + n], win[:, kc, cols0 + b_ * 128:cols0 + (b_ + 1) * 128], hnT[:, kc, 0:n],
                           [hnTk, "win"], [pk], start=(kc == 0), stop=(kc == 7))
                return pt, pk

            def v3(ps_ap, a):
                return ps_ap.rearrange("p (a b) -> p a b", a=a)

            pA0, kA0 = proj_tok(0, 512)
            pA1, kA1 = proj_tok(512, 1024)
            ACT(f1[0:n, 0:256], pA0[0:n, 0:256], AF.Exp, [kA0], ["f1"], scale=-1.0)
            ACT(f1[0:n, 256:512], pA0[0:n, 256:512], AF.Exp, [kA0], ["f1"])
            ACT(f1[0:n, 512:768], pA1[0:n, 256:512], AF.Exp, [kA1], ["f1"], scale=-1.0)
            sigmoid_from_exp(f1[0:n, :], "f1")
            STT(f2[0:n, 0:256], pA0[0:n, 0:256], QK, f1[0:n, 0:256], ALU.mult, ALU.mult, [kA0, "f1"], ["f2"])
            TT("dve", f2[0:n, 256:512], f1[0:n, 256:512], oml[0:n, l, :], ALU.mult, ["f1", "oml"], ["f2"])
            TT("dve", f1[0:n, 512:768], f1[0:n, 512:768], prm[0:n, 0:256], ALU.mult, ["f1", "prm"], ["f1"])
            TT("dve", gate[0:n, 0:256], pA1[0:n, 256:512], f1[0:n, 512:768], ALU.mult, [kA1, "f1"], ["gate"])
            ACT(v_bf[0:n, :, :].rearrange("p a b -> p (a b)"), pA1[0:n, 0:256], AF.Copy, [kA1], ["v_bf"])
            ACT(f3[0:n, 0:256], f2[0:n, 256:512], AF.Ln, ["f2"], ["f3"], scale=-1.0, bias=eps_t[0:n, 1:2])
            pG, kG = bank()
            MM(pG[0:n, 0:256], C("uprime", slice(0, n), 0, n), f3[0:n, 0:256], ["cst", "f3"], [kG])
            for hp in range(2):
                MM(pG[:, 256 + hp * 8:256 + hp * 8 + 8], f3[0:n, hp * 128:(hp + 1) * 128], C("wc", slice(0, n)),
                   ["cst", "f3"], [kG])
            ACT(f3[0:n, 0:256], pG[0:n, 0:256], AF.Exp, [kG], ["f3"])
            ACT(f3[0:n, 256:512], pG[0:n, 0:256], AF.Exp, [kG], ["f3"], scale=-1.0)
            ACT(ecs[:].rearrange("p a b -> p (a b)"), pG[:, 256:272], AF.Exp, [kG], ["ecs"])
            TT("dve", b1[0:n, 0:2, :].rearrange("p a b -> p (a b)"), f2[0:n, 0:256], f3[0:n, 0:256], ALU.mult,
               ["f2", "f3"], ["b1"])
            TT("dve", b1[0:n, 2:4, :].rearrange("p a b -> p (a b)"), f2[0:n, 256:512], f3[0:n, 256:512], ALU.mult,
               ["f2", "f3"], ["b1"])
            pT, kT = bank()
            for blk in range(4):
                MM(pT[:, blk * 128:blk * 128 + n], b1[0:n, blk, :], ident_bf[0:n, 0:n], ["b1", "ident_bf"], [kT])
            CP("act", qkT[:, :, 0:n], v3(pT[:, :], 4)[:, :, 0:n], [kT], ["qkT"])
            pS, kS = bank()
            for hd in (0, 2, 1, 3):
                hp, hh = hd // 2, hd % 2
                rows = slice(hh * 64, hh * 64 + 64)
                MM(pS[0:n, hd * 128:hd * 128 + n], qkT[rows, 2 + hp, 0:n], qkT[rows, hp, 0:n], ["qkT"], [kS])
            TT("dve", AT[0:n, :, 0:n], v3(pS[0:n, :], 4)[:, :, 0:n], bc_h(C("maskT", slice(0, n), 0, n)), ALU.mult,
               [kS, "cst"], ["AT"])
            pO, kO = psb[4], "ps4"
            pOD, kOD = psb[4], "ps4"
            for hd in (0, 2, 1, 3):
                MM(pO[0:n, hd * 64:hd * 64 + 64], AT[0:n, hd, 0:n], v_bf[0:n, hd, :], ["AT", "v_bf"], [kO],
                   start=(hd == 0), stop=False)
            for ci, (c0, c1) in enumerate(chunks):
                TT("dve", Sb_A[:], S_A[:], bc_l(ecs[:, :, ci], 64), ALU.mult, ["S_A", "ecs"], ["Sb_A"])
                TT("dve", Sd_A[:], S_A[:], bc_l(ecs[:, :, 4 + ci], 64), ALU.mult, ["S_A", "ecs"], ["Sd_A"])
                pK, kK = bank()
                for hd in (0, 2, 1, 3):
                    hp, hh = hd // 2, hd % 2
                    rows = slice(hh * 64, hh * 64 + 64)
                    MM(pO[c0:c1, hd * 64:hd * 64 + 64], qkT[rows, hp, c0:c1], Sb_A[rows, hp, :], ["qkT", "Sb_A"], [kO],
                       start=False, stop=True)
                    MM(pK[rows, hp * 64:hp * 64 + 64], b1[c0:c1, 2 + hp, hh * 64:hh * 64 + 64], v_bf[c0:c1, hd, :],
                       ["b1", "v_bf"], [kK])
                TT("dve", tmpS[:, 0:2, :], v3(pK[:, 0:128], 2), bc_l(ecs[:, :, 2 + ci], 64), ALU.mult, [kK, "ecs"], ["tmpS"])
                TT("dve", S_A[:], tmpS[:, 0:2, :], Sd_A[:], ALU.add, ["tmpS", "Sd_A"], ["S_A"])

            def head_norm(ps_ap, pskey, gcols, ycols):
                ACT(f4[0:n, 0:256], ps_ap, AF.Square, [pskey], ["f4"])
                RED(st4[0:n, 4:8], v3(f4[0:n, 0:256], 4), ["f4"], ["st4"])
                rsqrt_act(st4[0:n, 8:12], st4[0:n, 4:8], 1.0 / 64, ["st4"], ["st4"])
                TT("dve", v3(f4[0:n, 0:256], 4), v3(ps_ap, 4), bc_l(st4[0:n, 8:12], 64), ALU.mult, [pskey, "st4"], ["f4"])
                TT("dve", y_bf[0:n, ycols], f4[0:n, 0:256], gate[0:n, gcols], ALU.mult, ["f4", "gate"], ["y_bf"])

            head_norm(pO[0:n, 0:256], kO, slice(0, 256), slice(0, 256))
            if mid_hook is not None:
                mid_hook()

            pD0, kD0 = proj_tok(3084, 3596)
            pD1, kD1 = proj_tok(3596, 4108)
            cosb = rt[0:n, 0:32].unsqueeze(1).to_broadcast([n, 16, 32])
            sinb = rt[0:n, 32:64].unsqueeze(1).to_broadcast([n, 16, 32])
            qk4 = pD0[0:n, :].rearrange("p (a b) -> p a b", a=16)
            TT("dve", f1[0:n, 0:512].rearrange("p (a b) -> p a b", a=16), qk4, cosb, ALU.mult, [kD0, rtk], ["f1"])
            TT("dve", f2[0:n, 0:512].rearrange("p (a b) -> p a b", a=16), qk4, sinb, ALU.mult, [kD0, rtk], ["f2"])
            c4 = f1[0:n, 0:512].rearrange("p (a s b) -> p a s b", a=8, s=2)
            s4 = f2[0:n, 0:512].rearrange("p (a s b) -> p a s b", a=8, s=2)
            qkr = b1[0:n, :, :].rearrange("p a (s b) -> p a s b", s=4)
            qkr8 = b1[0:n, :, :].rearrange("p a b -> p (a b)").rearrange("p (a s b) -> p a s b", a=8, s=2)
            TT("dve", qkr8[:, :, 0, :], c4[:, :, 0, :], s4[:, :, 1, :], ALU.subtract, ["f1", "f2"], ["b1"])
            TT("dve", qkr8[:, :, 1, :], c4[:, :, 1, :], s4[:, :, 0, :], ALU.add, ["f1", "f2"], ["b1"])
            ACT(v_bf[0:n, :, :].rearrange("p a b -> p (a b)"), pD1[0:n, 0:256], AF.Copy, [kD1], ["v_bf"])
            ACT(f1[0:n, 512:768], pD1[0:n, 256:512], AF.Exp, [kD1], ["f1"], scale=-1.0)
            sigmoid_from_exp(f1[0:n, 512:768], "f1")
            TT("dve", gate[0:n, 768:1024], pD1[0:n, 256:512], f1[0:n, 512:768], ALU.mult, [kD1, "f1"], ["gate"])
            pT, kT = bank()
            for blk in range(4):
                MM(pT[:, blk * 128:blk * 128 + n], b1[0:n, blk, :], ident_bf[0:n, 0:n], ["b1", "ident_bf"], [kT])
            CP("act", qkT[:, :, 0:n], v3(pT[:, :], 4)[:, :, 0:n], [kT], ["qkT"])
            egq = C("egq").rearrange("p (a b) -> p a b", a=2)
            TT("dve", qpT[:, 0:2, 0:n], qkT[:, 0:2, 0:n], egq[:, :, 0:n], ALU.mult, ["qkT", "cst"], ["qpT"])
            egrev = C("egrev16" if t == 0 else "egrev64", slice(0, n))
            TT("dve", kp_bf[0:n, :, 0:64], b1[0:n, 2:4, :].rearrange("p a (s b) -> p (a s) b", s=2), bc_l(egrev, 64), ALU.mult,
               ["b1", "cst"], ["kp_bf"])
            pS, kS = bank()
            for hd in (0, 2, 1, 3):
                hp, hh = hd // 2, hd % 2
                rows = slice(hh * 64, hh * 64 + 64)
                MM(pS[0:n, hd * 128:hd * 128 + n], qkT[rows, 2 + hp, 0:n], qkT[rows, hp, 0:n], ["qkT"], [kS])
            TT("dve", AT[0:n, :, 0:n], v3(pS[0:n, :], 4)[:, :, 0:n], dtret_bf[0:n, :, 0:n], ALU.mult, [kS, "dtret_bf"], ["AT"])
            for hd in (0, 2, 1, 3):
                MM(pOD[0:n, 256 + hd * 64:256 + hd * 64 + 64], AT[0:n, hd, 0:n], v_bf[0:n, hd, :], ["AT", "v_bf"], [kOD],
                   start=(hd == 0), stop=False)
            egl = C("egl").rearrange("p (a b) -> p a b", a=2)
            for ci, (c0, c1) in enumerate(chunks):
                pK, kK = bank()
                for hd in (0, 2, 1, 3):
                    hp, hh = hd // 2, hd % 2
                    rows = slice(hh * 64, hh * 64 + 64)
                    MM(pOD[c0:c1, 256 + hd * 64:256 + hd * 64 + 64], qpT[rows, hp, c0:c1], Sb_D[rows, hp, :], ["qpT", "Sb_D"], [kOD],
                       start=False, stop=True)
                    MM(pK[rows, hp * 64:hp * 64 + 64], kp_bf[c0:c1, hd, 0:64], v_bf[c0:c1, hd, :], ["kp_bf", "v_bf"], [kK])
                TT("dve", tmpS[:, 0:2, :], S_D[:], bc_l(egl[:, :, (0 if t == 0 else 1)], 64), ALU.mult, ["S_D", "cst"], ["tmpS"])
                TT("dve", S_D[:], tmpS[:, 0:2, :], v3(pK[:, 0:128], 2), ALU.add, ["tmpS", kK], ["S_D"])
                CP("act", Sb_D[:], S_D[:], ["S_D"], ["Sb_D"])
            oD = pOD[0:n, 256:512]
            kO_ = kOD
            RED(st4[0:n, 4:8], v3(oD, 4), [kO_], ["st4"])
            TS("dve", st4[0:n, 4:8], st4[0:n, 4:8], -1.0 / 64, ALU.mult, ["st4"], ["st4"])
            TT("dve", v3(f3[0:n, 0:256], 4), v3(oD, 4), bc_l(st4[0:n, 4:8], 64), ALU.add, [kO_, "st4"], ["f3"])
            ACT(f4[0:n, 0:256], f3[0:n, 0:256], AF.Square, ["f3"], ["f4"])
            RED(st4[0:n, 4:8], v3(f4[0:n, 0:256], 4), ["f4"], ["st4"])
            rsqrt_act(st4[0:n, 8:12], st4[0:n, 4:8], 1.0 / 64, ["st4"], ["st4"])
            TT("dve", v3(f3[0:n, 0:256], 4), v3(f3[0:n, 0:256], 4), bc_l(st4[0:n, 8:12], 64), ALU.mult, ["f3", "st4"], ["f3"])
            TT("dve", f3[0:n, 0:256], f3[0:n, 0:256], prm[0:n, 768:1024], ALU.mult, ["f3", "prm"], ["f3"])
            TT("dve", f3[0:n, 0:256], f3[0:n, 0:256], prm[0:n, 1024:1280], ALU.add, ["f3", "prm"], ["f3"])
            TT("dve", y_bf[0:n, 768:1024], f3[0:n, 0:256], gate[0:n, 768:1024], ALU.mult, ["f3", "gate"], ["y_bf"])

            pBz, kBz = proj_tok(1792, 2056, (256, 1288))
            pCz, kCz = proj_tok(2824, 3084, (256, 1292))
            ACT(f1[0:n, 512:768], pBz[0:n, 0:256], AF.Exp, [kBz], ["f1"], scale=-1.0)
            ACT(f2[0:n, 0:256], pCz[0:n, 0:256], AF.Exp, [kCz], ["f2"], scale=-1.0)
            ACT(beta[0:n, 0:4], pBz[0:n, 260:264], AF.Exp, [kBz], ["beta"], scale=-1.0)
            ACT(g8[0:n, 0:4], pBz[0:n, 256:260], AF.Exp, [kBz], ["g8"])
            ACT(g8[0:n, 4:8], pCz[0:n, 256:260], AF.Exp, [kCz], ["g8"])
            sigmoid_from_exp(f1[0:n, 512:768], "f1")
            sigmoid_from_exp(f2[0:n, 0:256], "f2")
            TT("dve", f1[0:n, 512:768], f1[0:n, 512:768], prm[0:n, 256:512], ALU.mult, ["f1", "prm"], ["f1"])
            TT("dve", gate[0:n, 256:512], pBz[0:n, 0:256], f1[0:n, 512:768], ALU.mult, [kBz, "f1"], ["gate"])
            TT("dve", gate[0:n, 512:768], pCz[0:n, 0:256], f2[0:n, 0:256], ALU.mult, [kCz, "f2"], ["gate"])
            for cv, cols0 in ((0, 1024), (1, 2056)):
                for part, (b0, nb_) in enumerate(((0, 4), (4, 2))):
                    pf, kf = proj_feat(cols0 + b0 * 128, nb_)
                    src = v3(pf[:, 0:nb_ * 128], nb_)[:, :, 0:n]
                    CP("act", cur[:, cv * 6 + b0:cv * 6 + b0 + nb_, 3:3 + n], src, [kf], [curk])
                    if last_tile:
                        CP("dve", cvst[:, cv * 6 + b0:cv * 6 + b0 + nb_, :], src[:, :, n - 3:n], [kf], ["cvst"])
            if not last_tile:
                CP("pool", nxt[:, :, 0:3], cur[:, :, n:n + 3], [curk], [nxtk])
            for cv in range(2):
                for part, (b0, nb_) in enumerate(((0, 4), (4, 2))):
                    pc, kc_ = bank()
                    for b_ in range(nb_):
                        blk = cv * 6 + b0 + b_
                        for w in range(4):
                            MM(pc[:, b_ * 128:b_ * 128 + n], dg[:, blk, w, :], cur[:, blk, w:w + n], ["dg", curk], [kc_],
                               start=(w == 0), stop=(w == 3 and cv == 0))
                        if cv == 1:
                            MM(pc[:, b_ * 128:b_ * 128 + n], cbias_bf[0:1, (b0 + b_) * 128:(b0 + b_ + 1) * 128], ones_bf[0:1, 0:n],
                               ["cbias_bf", "ones_bf"], [kc_], start=False, stop=True)
                    src = v3(pc[:, 0:nb_ * 128], nb_)[:, :, 0:n]
                    dstf = v3(f1[:, 0:nb_ * 128], nb_)[:, :, 0:n]
                    ACT(dstf, src, AF.Exp, [kc_], ["f1"], scale=-1.0)
                    sigmoid_from_exp(dstf, "f1")
                    if cv == 0 and part == 0:
                        TT("dve", xsf[:, :, 0:n], src, dstf, ALU.mult, [kc_, "f1"], ["xsf"])
                    else:
                        TT("dve", xs[:, cv * 6 + b0:cv * 6 + b0 + nb_, 0:n], src, dstf, ALU.mult, [kc_, "f1"], ["xs"])
            ACT(b1[:, :, 0:n], xsf[:, :, 0:n], AF.Square, ["xsf"], ["b1"])
            pN, kN = bank()
            for blk in range(4):
                MM(pN[:, blk * 128:blk * 128 + n], bones_bf[:], b1[:, blk, 0:n], ["bones_bf", "b1"], [kN])
            srcN = v3(pN[:, :], 4)[:, :, 0:n]
            dstN = v3(f2[:, 0:512], 4)[:, :, 0:n]
            ACT(dstN, srcN, AF.Ln, [kN], ["f2"], bias=eps_t[:, 0:1])
            ACT(dstN, dstN, AF.Exp, ["f2"], ["f2"], scale=-0.5)
            STT(xs[:, 0:2, 0:n], xsf[:, 0:2, 0:n], QK, dstN[:, 0:2, :], ALU.mult, ALU.mult, ["xsf", "f2"], ["xs"])
            TT("dve", xs[:, 2:4, 0:n], xsf[:, 2:4, 0:n], dstN[:, 2:4, :], ALU.mult, ["xsf", "f2"], ["xs"])

            ACT(g8[0:n, 0:8], g8[0:n, 0:8], AF.Ln, ["g8"], ["g8"], bias=eps_t[0:n, 1:2])
            CP("dve", dtb[0:n, :], g8[0:n, :], ["g8"], ["dtb"])
            TT("dve", g8[0:n, :], g8[0:n, :], nega[0:n, :], ALU.mult, ["g8", "nega"], ["g8"])
            sigmoid_from_exp(beta[0:n, :], "beta")
            pDc, kDc = bank()
            MM(pDc[0:n, 0:32], C("maskT", slice(0, n), 0, n), g8[0:n, 0:32], ["cst", "g8"], [kDc])
            MM(pDc[0:n, 32:64], C("urev", slice(0, n), 0, n), g8[0:n, 0:32], ["cst", "g8"], [kDc])
            CP("dve", gc[0:n, :], pDc[0:n, 0:64], [kDc], ["gc"])
            ACT(egc[0:n, :], gc[0:n, :], AF.Exp, ["gc"], ["egc"])
            for half in range(2):
                pB_, kB_ = bank()
                CP("dve", v3(f4[0:n, 0:512], 4), bc_l(g8[0:n, half * 4:half * 4 + 4], 128), ["g8"], ["f4"])
                for hd in (0, 2, 1, 3):
                    MM(pB_[:, hd * 128:hd * 128 + n], f4[0:n, hd * 128:(hd + 1) * 128],
                       C("maskT", slice(0, n), 0, n), ["cst", "f4"], [kB_])
                srcB = v3(pB_[:, :], 4)[:, :, 0:n]
                ACT(eGbc[:, half * 4:half * 4 + 4, 0:n], srcB, AF.Exp, [kB_], ["eGbc"])
                TT("dve", tt[0:n, half * 4:half * 4 + 4, 0:n], srcB[0:n], bc_l(gc[0:n, half * 4:half * 4 + 4], n), ALU.subtract,
                   [kB_, "gc"], ["tt"])
            if True:
                TT("dve", v3(f1[0:n, 0:512], 4)[:, :, 0:n], tt[0:n, 0:4, 0:n], bc_h(C("negS", slice(0, n), 0, n)), ALU.subtract,
                   ["tt", "cst"], ["f1"])
                ACT(Dst[0:n, :, 0:n], v3(f1[0:n, 0:512], 4)[:, :, 0:n], AF.Exp, ["f1"], ["Dst"], scale=-1.0)
                TT("dve", tt[0:n, :, 0:n], tt[0:n, :, 0:n], bc_h(C("negT", slice(0, n), 0, n), 8), ALU.add, ["tt", "cst"], ["tt"])
                ACT(DT[0:n, :, 0:n], tt[0:n, :, 0:n], AF.Exp, ["tt"], ["DT"])

            pS, kS = bank()
            pKK, kKK = bank()
            for hd in (0, 2, 1, 3):
                hp, hh = hd // 2, hd % 2
                rows = slice(hh * 64, hh * 64 + 64)
                MM(pS[0:n, hd * 128:hd * 128 + n], xs[rows, 2 + hp, 0:n], xs[rows, hp, 0:n], ["xs"], [kS])
                MM(pKK[0:n, hd * 128:hd * 128 + n], xs[rows, 2 + hp, 0:n], xs[rows, 2 + hp, 0:n], ["xs"], [kKK])
            TT("dve", AT[0:n, :, 0:n], v3(pS[0:n, :], 4)[:, :, 0:n], DT[0:n, 0:4, 0:n], ALU.mult, [kS, "DT"], ["AT"])
            TS("dve", nb[0:n, :], beta[0:n, :], -1.0, ALU.mult, ["beta"], ["nb"])
            TT("dve", v3(f1[0:n, 0:512], 4)[:, :, 0:n], v3(pKK[0:n, :], 4)[:, :, 0:n], Dst[0:n, :, 0:n], ALU.mult, [kKK, "Dst"], ["f1"])
            TT("dve", X_bf[0][0:n, :, 0:n], v3(f1[0:n, 0:512], 4)[:, :, 0:n], bc_l(nb[0:n, 0:4], n), ALU.mult,
               ["f1", "nb"], ["X_bf0"])
            def t_chain():
                pY, kY = bank()
                for hd in (0, 2, 1, 3):
                    MM(pY[0:n, hd * 128:hd * 128 + n], X_bf[0][0:n, hd, 0:n], ident_bf[0:n, 0:n], ["X_bf0", "ident_bf"], [kY])
                CP("act", Y_bf[0][0:n, :, 0:n], v3(pY[0:n, :], 4)[:, :, 0:n], [kY], ["Y_bf0"])
                TT("dve", P_bf[0:n, :, 0:n], v3(pY[0:n, :], 4)[:, :, 0:n], bc_h(C("ident", slice(0, n), 0, n)), ALU.add,
                   [kY, "cst"], ["P_bf"])
                yield
                nlev = int(math.ceil(math.log2(clen))) - 1
                ci_ = 0
                for lev in range(nlev):
                    ni_ = 1 - ci_
                    pX2, kX2 = bank()
                    for hd in (0, 2, 1, 3):
                        MM(pX2[0:n, hd * 128:hd * 128 + n], Y_bf[ci_][0:n, hd, 0:n], X_bf[ci_][0:n, hd, 0:n],
                           [f"Y_bf{ci_}", f"X_bf{ci_}"], [kX2])
                    CP("act", X_bf[ni_][0:n, :, 0:n], v3(pX2[0:n, :], 4)[:, :, 0:n], [kX2], [f"X_bf{ni_}"])
                    if lev < nlev - 1:
                        pY2, kY2 = bank()
                        for hd in (0, 2, 1, 3):
                            MM(pY2[0:n, hd * 128:hd * 128 + n], X_bf[ci_][0:n, hd, 0:n], Y_bf[ci_][0:n, hd, 0:n],
                               [f"Y_bf{ci_}", f"X_bf{ci_}"], [kY2])
                        CP("dve", Y_bf[ni_][0:n, :, 0:n], v3(pY2[0:n, :], 4)[:, :, 0:n], [kY2], [f"Y_bf{ni_}"])
                    pP, kP = bank()
                    for hd in (0, 2, 1, 3):
                        MM(pP[0:n, hd * 128:hd * 128 + n], X_bf[ni_][0:n, hd, 0:n], P_bf[0:n, hd, 0:n], [f"X_bf{ni_}", "P_bf"], [kP])
                    TT("dve", P_bf[0:n, :, 0:n], P_bf[0:n, :, 0:n], v3(pP[0:n, :], 4)[:, :, 0:n], ALU.add, ["P_bf", kP], ["P_bf"])
                    ci_ = ni_
                    yield

            tgen = t_chain()

            def tstep():
                next(tgen, None)

            tstep()

            pS, kS = bank()
            for g in range(2):
                MM(pS[0:n, g * 128:g * 128 + n], xs[:, 8 + g, 0:n], xs[:, 10 + g, 0:n], ["xs"], [kS])
            for g in range(2):
                TT("dve", AT2[0:n, 2 * g:2 * g + 2, 0:n], pS[0:n, g * 128:g * 128 + n].unsqueeze(1).to_broadcast([n, 2, n]),
                   DT[0:n, 4 + 2 * g:4 + 2 * g + 2, 0:n], ALU.mult, [kS, "DT"], ["AT2"])
            tstep()
            pT, kT = bank()
            for blk in range(4):
                MM(pT[0:n, blk * 128:(blk + 1) * 128], xs[:, 6 + blk, 0:n], ident_bf[:, :], ["xs", "ident_bf"], [kT])
            TT("dve", v_bf[0:n, :, :], v3(pT[0:n, 0:256], 4), bc_l(dtb[0:n, 4:8], 64), ALU.mult, [kT, "dtb"], ["v_bf"])
            TT("dve", xd_bf[0:n, :, :], v3(pT[0:n, 0:256], 4), bc_l(prm[0:n, 1296:1300], 64), ALU.mult, [kT, "prm"], ["xd_bf"])
            for g in range(2):
                TT("dve", kp_bf[0:n, 2 * g:2 * g + 2, :], pT[0:n, 256 + g * 128:256 + (g + 1) * 128].unsqueeze(1).to_broadcast([n, 2, 128]),
                   bc_l(egc[0:n, 36 + 2 * g:36 + 2 * g + 2], 128), ALU.mult, [kT, "egc"], ["kp_bf"])
                TT("dve", qpT[:, 2 * g:2 * g + 2, 0:n], xs[:, 10 + g, 0:n].unsqueeze(1).to_broadcast([128, 2, n]),
                   eGbc[:, 4 + 2 * g:4 + 2 * g + 2, 0:n], ALU.mult, ["xs", "eGbc"], ["qpT"])
            for hd in (0, 2, 1, 3):
                MM(pOC[0:n, 256 + hd * 64:256 + hd * 64 + 64], AT2[0:n, hd, 0:n], v_bf[0:n, hd, :], ["AT2", "v_bf"], [kOC],
                   start=(hd == 0), stop=False)
            MM(pOC[0:n, 256:512], ident_bf[0:n, 0:n], xd_bf[0:n, :, :].rearrange("p a b -> p (a b)"), ["ident_bf", "xd_bf"], [kOC],
               start=False, stop=False)
            for ci, (c0, c1) in enumerate(chunks):
                tstep()
                pK, kK = bank()
                for hd in (0, 2, 1, 3):
                    MM(pOC[c0:c1, 256 + hd * 64:256 + hd * 64 + 64], qpT[:, hd, c0:c1], Sb_C[:, hd, :], ["qpT", "Sb_C"], [kOC],
                       start=False, stop=True)
                    MM(pK[:, hd * 64:hd * 64 + 64], kp_bf[c0:c1, hd, :], v_bf[c0:c1, hd, :], ["kp_bf", "v_bf"], [kK])
                TT("dve", tmpS[:, :, :], S_C[:], eGbc[:, 4:8, c1 - 1:c1].to_broadcast([128, 4, 64]), ALU.mult, ["S_C", "eGbc"], ["tmpS"])
                TT("dve", S_C[:], tmpS[:, :, :], v3(pK[:, 0:256], 4), ALU.add, ["tmpS", kK], ["S_C"])
                CP("act", Sb_C[:], S_C[:], ["S_C"], ["Sb_C"])
            tstep()
            TT("dve", f3[0:n, 0:256], pOC[0:n, 256:512], gate[0:n, 512:768], ALU.mult, [kOC, "gate"], ["f3"])
            ACT(f4[0:n, 0:256], f3[0:n, 0:256], AF.Square, ["f3"], ["f4"])
            RED(st4[0:n, 4:6], v3(f4[0:n, 0:256], 2), ["f4"], ["st4"])
            rsqrt_act(st4[0:n, 8:10], st4[0:n, 4:6], 1.0 / 128, ["st4"], ["st4"])
            TT("dve", v3(f3[0:n, 0:256], 2), v3(f3[0:n, 0:256], 2), bc_l(st4[0:n, 8:10], 128), ALU.mult, ["f3", "st4"], ["f3"])
            TT("dve", y_bf[0:n, 512:768], f3[0:n, 0:256], prm[0:n, 512:768], ALU.mult, ["f3", "prm"], ["y_bf"])

            for _ in tgen:
                pass

            pT, kT = bank()
            for blk in range(4):
                MM(pT[0:n, blk * 128:(blk + 1) * 128], xs[:, 2 + blk, 0:n], ident_bf[:, :], ["xs", "ident_bf"], [kT])
            TT("dve", kp_bf[0:n, :, 0:64], v3(pT[0:n, 0:256], 4), bc_l(egc[0:n, 32:36], 64), ALU.mult, [kT, "egc"], ["kp_bf"])
            TT("dve", bv[0:n, :, :], v3(pT[0:n, 256:512], 4), bc_l(beta[0:n, 0:4], 64), ALU.mult, [kT, "beta"], ["bv"])
            TT("dve", nb2[0:n, :], nb[0:n, :], egc[0:n, :], ALU.mult, ["nb", "egc"], ["nb2"])
            for hh in range(2):
                rows = slice(hh * 64, hh * 64 + 64)
                TT("dve", qpT[rows, 0:2, 0:n], xs[rows, 0:2, 0:n], eGbc[rows, hh:4:2, 0:n], ALU.mult, ["xs", "eGbc"], ["qpT"])
            for ci, (c0, c1) in enumerate(chunks):
                pW, kW = bank()
                for hd in (0, 2, 1, 3):
                    hp, hh = hd // 2, hd % 2
                    rows = slice(hh * 64, hh * 64 + 64)
                    MM(pW[c0:c1, hd * 64:hd * 64 + 64], xs[rows, 2 + hp, c0:c1], Sb_B[rows, hp, :], ["xs", "Sb_B"], [kW])
                TT("dve", v3(f4[c0:c1, 0:256], 4), v3(pW[c0:c1, 0:256], 4), bc_l(nb2[c0:c1, 0:4], 64), ALU.mult,
                   [kW, "nb2"], ["f4"])
                TT("dve", r_bf[c0:c1, :, :], v3(f4[c0:c1, 0:256], 4), bv[c0:c1, :, :], ALU.add, ["f4", "bv"], ["r_bf"])
                pU, kU = bank()
                for hd in (0, 2, 1, 3):
                    MM(pU[c0:c1, hd * 64:hd * 64 + 64], P_bf[c0:c1, hd, c0:c1], r_bf[c0:c1, hd, :], ["P_bf", "r_bf"], [kU])
                CP("act", u_bf[c0:c1, :, :], v3(pU[c0:c1, 0:256], 4), [kU], ["u_bf"])
                pK, kK = bank()
                for hd in (0, 2, 1, 3):
                    hp, hh = hd // 2, hd % 2
                    rows = slice(hh * 64, hh * 64 + 64)
                    MM(pO2[c0:c1, hd * 64:hd * 64 + 64], AT[c0:c1, hd, c0:c1], u_bf[c0:c1, hd, :], ["AT", "u_bf"], [kO2],
                       start=(hd == 0), stop=False)
                    MM(pO2[c0:c1, hd * 64:hd * 64 + 64], qpT[rows, hp, c0:c1], Sb_B[rows, hp, :], ["qpT", "Sb_B"], [kO2],
                       start=False, stop=True)
                    MM(pK[rows, hp * 64:hp * 64 + 64], kp_bf[c0:c1, hd, 0:64], u_bf[c0:c1, hd, :], ["kp_bf", "u_bf"], [kK])
                for hh in range(2):
                    rows = slice(hh * 64, hh * 64 + 64)
                    TT("dve", tmpS[rows, 0:2, :], S_B[rows, :, :], eGbc[rows, hh:4:2, c1 - 1:c1].to_broadcast([64, 2, 64]), ALU.mult,
                       ["S_B", "eGbc"], ["tmpS"])
                TT("dve", S_B[:], tmpS[:, 0:2, :], v3(pK[:, 0:128], 2), ALU.add, ["tmpS", kK], ["S_B"])
                CP("act", Sb_B[:], S_B[:], ["S_B"], ["Sb_B"])
            head_norm(pO2[0:n, 0:256], kO2, slice(256, 512), slice(256, 512))

            if late_hook is not None:
                late_hook()
            for half in range(2):
                pt, pk = bank()
                for kk in range(4):
                    kc = half * 4 + kk
                    MM(pt[:, kk * 128:kk * 128 + n], y_bf[0:n, kc * 128:(kc + 1) * 128], ident_bf[0:n, 0:n],
                       ["y_bf", "ident_bf"], [pk])
                CP("act" if half else "dve", yT[:, half * 4:half * 4 + 4, 0:n], v3(pt[:, :], 4)[:, :, 0:n], [pk], ["yT"])
            for cg in range(2):
                pt, pk = bank()
                for kc in range(8):
                    MM(pt[0:n, :], yT[:, kc, 0:n], wout[:, kc, cg * 512:(cg + 1) * 512], ["yT", "wout"], [pk],
                       start=(kc == 0), stop=(kc == 7))
                TT("dve", ht[:, cg * 512:(cg + 1) * 512], ht[:, cg * 512:(cg + 1) * 512], pt[0:n, :], ALU.add, [hk, pk], [hk])

            if last_tile:
                for nm, S_, dst in (("A", S_A, st_hg), ("B", S_B, st_gd), ("D", S_D, st_rt)):
                    for hh in range(2):
                        DMA(dst[l, hh:4:2, :, :].rearrange("a k v -> k a v"), S_[hh * 64:(hh + 1) * 64, :, :], ["S_" + nm], [])
                DMA(st_sd[l].rearrange("a k v -> k a v"), S_C[:, :, :], ["S_C"], [])
                for blk in range(6):
                    DMA(st_gc[l][:, blk * 128:(blk + 1) * 128].rearrange("w p -> p w"), cvst[:, blk, :], ["cvst"], [], slow=True)
                    DMA(st_sc[l][:, blk * 128:(blk + 1) * 128].rearrange("w p -> p w"), cvst[:, 6 + blk, :], ["cvst"], [], slow=True)
            if l < depth - 1:
                DMA(hscr[t * 128:t * 128 + n, :], ht, [hk], [f"hd{t}"])
            if l == depth - 1 and t > 0:
                ACT(hn_bf[0:n, :], ht, AF.Square, [hk], ["hn_bf", "st4"], accum=st4[0:n, 0:1])
                rsqrt_act(st4[0:n, 1:2], st4[0:n, 0:1], 1.0 / D, ["st4"], ["st4"])
                STT(ht, ht, st4[0:n, 1:2], finw[0:n, :], ALU.mult, ALU.mult, [hk, "st4", "lgt"], [hk])
                DMA(y_p[(t - 1) * 128:t * 128, :], ht, [hk], [])


        hs = sb("hs", [NS, D])
        DMA(hs[:, :], xs_d, [], ["hs"])

        def sample_fwd(l, last):
            n = NS
            DMA(rot[0][:], rot_d[NT], [], ["rot0"])
            rt = rot[0]; rtk = "rot0"

            def v3(ps_ap, a):
                return ps_ap.rearrange("p (a b) -> p a b", a=a)

            def bc_l(ap2d, m):
                return ap2d.unsqueeze(2).to_broadcast([ap2d.shape[0], ap2d.shape[1], m])

            ACT(hn_bf[0:n, :], hs[:, :], AF.Square, ["hs"], ["hn_bf", "st4"], accum=st4[0:n, 0:1])
            rsqrt_act(st4[0:n, 1:2], st4[0:n, 0:1], 1.0 / D, ["st4"], ["st4"])
            ACT(hn_bf[0:n, :], hs[:, :], AF.Copy, ["hs", "st4"], ["hn_bf"], scale=st4[0:n, 1:2])
            for half in range(2):
                pt, pk = bank()
                for kk in range(4):
                    kc = half * 4 + kk
                    MM(pt[:, kk * 128:kk * 128 + n], hn_bf[0:n, kc * 128:(kc + 1) * 128], ident_bf[0:n, 0:n],
                       ["hn_bf", "ident_bf"], [pk])
                CP("act", hnT[:, half * 4:half * 4 + 4, 0:n], v3(pt[:, :], 4)[:, :, 0:n], [pk], ["hnT1"])

            def proj_tok(c0, c1, extra=None):
                pt, pk = bank()
                for kc in range(8):
                    MM(pt[0:n, 0:c1 - c0], hnT[:, kc, 0:n], win[:, kc, c0:c1], ["hnT1", "win"], [pk],
                       start=(kc == 0), stop=(kc == 7 and extra is None))
                if extra is not None:
                    MM(pt[0:n, extra[0]:extra[0] + 4], C("ident", slice(0, n), 0, n), prm[0:n, extra[1]:extra[1] + 4],
                       ["cst", "prm"], [pk], start=False, stop=True)
                return pt, pk

            sel = C("sel").rearrange("p (a b) -> p a b", a=4)
            selT = C("selT").rearrange("p (a b) -> p a b", a=4)
            pvs = f4
            Sbuf = [tt, eGbc]; Skey = ["tt", "eGbc"]
            Tbuf = gate; Tkey = "gate"
            slot = [0]

            def select(fields):
                pv, pvk = bank()
                first = True
                for (c0, wd, fn) in fields:
                    for hd in range(4):
                        ap, key = fn(hd)
                        P.op("pe", (lambda o_, l_, r_, st_: (lambda e: e.matmul(o_, lhsT=l_, rhs=r_, start=st_, stop=False,
                                                                                 skip_group_check=True)))(
                            pv[0:64, c0:c0 + wd], sel[0:n, hd, :], ap, first), reads=["cst", key], writes=[pvk])
                        first = False
                wtot = max(c0 + wd for (c0, wd, _) in fields)
                CP("dve", pvs[0:64, 0:wtot], pv[0:64, 0:wtot], [pvk], ["f4"])

            def unselect(o_ap, okey):
                po, pok = bank()
                for hd in range(4):
                    MM(po[0:n, hd * 64:hd * 64 + 64], selT[0:64, hd, :], o_ap, ["cst", okey], [pok])
                return po, pok

            def state_io(st_in, st_out, K):
                ks = 16
                for k0 in range(0, K, ks):
                    yield k0, ks

            def load_slice(st_in, k0, ks):
                i = slot[0] % 2
                slot[0] += 1
                Sv = Sbuf[i][0:64, :, :].rearrange("p a b -> p (a b)")[:, 0:ks * 64].rearrange("p (k v) -> p k v", k=ks)
                for hd in range(4):
                    DMA(Sv[hd * 16:(hd + 1) * 16, :, :], st_in[l, :, hd, k0:k0 + ks, :], [], [Skey[i]])
                return Sv, Skey[i]

            def store_slice(st_out, Sv, sk, k0, ks):
                for hd in range(4):
                    DMA(st_out[l, :, hd, k0:k0 + ks, :], Sv[hd * 16:(hd + 1) * 16, :, :], [sk], [])

            o_sb = tmpS[0:64, 0, :]; w_sb = tmpS[0:64, 1, :]; op_sb = tmpS[0:64, 2, :]; u_sb = tmpS[0:64, 3, :]

            def Tview(ks):
                return Tbuf[0:64, 0:ks * 64].rearrange("p (k v) -> p k v", k=ks)

            def q_reduce(Sv, sk, q_ap, k0, ks, first, acc):
                T = Tview(ks)
                TT("dve", T, Sv, bc_l(q_ap[:, k0:k0 + ks], 64), ALU.mult, [sk, "f4"], [Tkey])
                RED(op_sb, T.rearrange("p k v -> p v k"), [Tkey], ["tmpS"])
                if first:
                    CP("dve", acc, op_sb, ["tmpS"], ["tmpS"])
                else:
                    TT("dve", acc, acc, op_sb, ALU.add, ["tmpS"], ["tmpS"])

            def step_plain(st_in, st_out, K, q_ap, k_ap, v_ap, vec_f=None, sc=None):
                for k0, ks in state_io(st_in, st_out, K):
                    Sv, sk = load_slice(st_in, k0, ks)
                    T = Tview(ks)
                    TT("dve", T, bc_l(k_ap[:, k0:k0 + ks], 64), v_ap.unsqueeze(1).to_broadcast([64, ks, 64]), ALU.mult,
                       ["f4", "tmpS"], [Tkey])
                    if vec_f is not None:
                        TT("dve", Sv, Sv, bc_l(vec_f[:, k0:k0 + ks], 64), ALU.mult, [sk, "f4"], [sk])
                        TT("dve", Sv, Sv, T, ALU.add, [sk, Tkey], [sk])
                    else:
                        STT(Sv, Sv, sc, T, ALU.mult, ALU.add, [sk, Tkey, "f4", "cst"], [sk])
                    store_slice(st_out, Sv, sk, k0, ks)
                    q_reduce(Sv, sk, q_ap, k0, ks, k0 == 0, o_sb)

            def head_norm(ps_ap, pskey, gcols, ycols):
                ACT(f3[0:n, 256:512], ps_ap, AF.Square, [pskey], ["f3"])
                RED(st4[0:n, 4:8], v3(f3[0:n, 256:512], 4), ["f3"], ["st4"])
                rsqrt_act(st4[0:n, 8:12], st4[0:n, 4:8], 1.0 / 64, ["st4"], ["st4"])
                TT("dve", v3(f3[0:n, 256:512], 4), v3(ps_ap, 4), bc_l(st4[0:n, 8:12], 64), ALU.mult, [pskey, "st4"], ["f3"])
                TT("dve", y_bf[0:n, ycols], f3[0:n, 256:512], hb[1][0:n, gcols], ALU.mult, ["f3", "hb1"], ["y_bf"])

            gs = hb[1]; gsk = "hb1"

            pA0, kA0 = proj_tok(0, 512)
            pA1, kA1 = proj_tok(512, 1024)
            ACT(f1[0:n, 0:256], pA0[0:n, 0:256], AF.Exp, [kA0], ["f1"], scale=-1.0)
            ACT(f1[0:n, 256:512], pA0[0:n, 256:512], AF.Exp, [kA0], ["f1"])
            ACT(f1[0:n, 512:768], pA1[0:n, 256:512], AF.Exp, [kA1], ["f1"], scale=-1.0)
            sigmoid_from_exp(f1[0:n, :], "f1")
            STT(f2[0:n, 0:256], pA0[0:n, 0:256], QK, f1[0:n, 0:256], ALU.mult, ALU.mult, [kA0, "f1"], ["f2"])
            TT("dve", f2[0:n, 256:512], f1[0:n, 256:512], oml[0:n, l, :], ALU.mult, ["f1", "oml"], ["f2"])
            TT("dve", f1[0:n, 512:768], f1[0:n, 512:768], prm[0:n, 0:256], ALU.mult, ["f1", "prm"], ["f1"])
            TT("dve", gs[0:n, 0:256], pA1[0:n, 256:512], f1[0:n, 512:768], ALU.mult, [kA1, "f1"], [gsk])
            CP("dve", f3[0:n, 0:256], pA1[0:n, 0:256], [kA1], ["f3"])
            TS("dve", f1[0:n, 0:256], f2[0:n, 256:512], -1.0, ALU.mult, ["f2"], ["f1"], s2=1.0, op1=ALU.add)
            select([(0, 64, lambda hd: (f2[0:n, hd * 64:hd * 64 + 64], "f2")),
                    (64, 64, lambda hd: (f2[0:n, 256 + hd * 64:256 + hd * 64 + 64], "f2")),
                    (128, 64, lambda hd: (f3[0:n, hd * 64:hd * 64 + 64], "f3")),
                    (192, 64, lambda hd: (f1[0:n, hd * 64:hd * 64 + 64], "f1"))])
            step_plain(si_hg, so_hg, 64, pvs[0:64, 0:64], pvs[0:64, 64:128], pvs[0:64, 128:192], vec_f=pvs[0:64, 192:256])
            po, pok = unselect(o_sb, "tmpS")
            head_norm(po[0:n, 0:256], pok, slice(0, 256), slice(0, 256))

            pD0, kD0 = proj_tok(3084, 3596)
            pD1, kD1 = proj_tok(3596, 4108)
            cosb = rt[0:n, 0:32].unsqueeze(1).to_broadcast([n, 16, 32])
            sinb = rt[0:n, 32:64].unsqueeze(1).to_broadcast([n, 16, 32])
            qk4 = pD0[0:n, :].rearrange("p (a b) -> p a b", a=16)
            TT("dve", f1[0:n, 0:512].rearrange("p (a b) -> p a b", a=16), qk4, cosb, ALU.mult, [kD0, rtk], ["f1"])
            TT("dve", f2[0:n, 0:512].rearrange("p (a b) -> p a b", a=16), qk4, sinb, ALU.mult, [kD0, rtk], ["f2"])
            c4 = f1[0:n, 0:512].rearrange("p (a s b) -> p a s b", a=8, s=2)
            s4 = f2[0:n, 0:512].rearrange("p (a s b) -> p a s b", a=8, s=2)
            r4 = f3[0:n, 0:512].rearrange("p (a s b) -> p a s b", a=8, s=2)
            TT("dve", r4[:, :, 0, :], c4[:, :, 0, :], s4[:, :, 1, :], ALU.subtract, ["f1", "f2"], ["f3"])
            TT("dve", r4[:, :, 1, :], c4[:, :, 1, :], s4[:, :, 0, :], ALU.add, ["f1", "f2"], ["f3"])
            TS("dve", f3[0:n, 256:512], f3[0:n, 256:512], QK, ALU.mult, ["f3"], ["f3"])
            CP("dve", f1[0:n, 0:256], pD1[0:n, 0:256], [kD1], ["f1"])
            ACT(f1[0:n, 512:768], pD1[0:n, 256:512], AF.Exp, [kD1], ["f1"], scale=-1.0)
            sigmoid_from_exp(f1[0:n, 512:768], "f1")
            TT("dve", gs[0:n, 768:1024], pD1[0:n, 256:512], f1[0:n, 512:768], ALU.mult, [kD1, "f1"], [gsk])
            select([(0, 64, lambda hd: (f3[0:n, hd * 64:hd * 64 + 64], "f3")),
                    (64, 64, lambda hd: (f3[0:n, 256 + hd * 64:256 + hd * 64 + 64], "f3")),
                    (128, 64, lambda hd: (f1[0:n, hd * 64:hd * 64 + 64], "f1"))])
            step_plain(si_rt, so_rt, 64, pvs[0:64, 0:64], pvs[0:64, 64:128], pvs[0:64, 128:192], sc=C("gam64", slice(0, 64), 0, 1))
            po, pok = unselect(o_sb, "tmpS")
            oD = po[0:n, 0:256]
            RED(st4[0:n, 4:8], v3(oD, 4), [pok], ["st4"])
            TS("dve", st4[0:n, 4:8], st4[0:n, 4:8], -1.0 / 64, ALU.mult, ["st4"], ["st4"])
            TT("dve", v3(f3[0:n, 0:256], 4), v3(oD, 4), bc_l(st4[0:n, 4:8], 64), ALU.add, [pok, "st4"], ["f3"])
            ACT(f3[0:n, 256:512], f3[0:n, 0:256], AF.Square, ["f3"], ["f3"])
            RED(st4[0:n, 4:8], v3(f3[0:n, 256:512], 4), ["f3"], ["st4"])
            rsqrt_act(st4[0:n, 8:12], st4[0:n, 4:8], 1.0 / 64, ["st4"], ["st4"])
            TT("dve", v3(f3[0:n, 0:256], 4), v3(f3[0:n, 0:256], 4), bc_l(st4[0:n, 8:12], 64), ALU.mult, ["f3", "st4"], ["f3"])
            TT("dve", f3[0:n, 0:256], f3[0:n, 0:256], prm[0:n, 768:1024], ALU.mult, ["f3", "prm"], ["f3"])
            TT("dve", f3[0:n, 0:256], f3[0:n, 0:256], prm[0:n, 1024:1280], ALU.add, ["f3", "prm"], ["f3"])
            TT("dve", y_bf[0:n, 768:1024], f3[0:n, 0:256], gs[0:n, 768:1024], ALU.mult, ["f3", gsk], ["y_bf"])

            pBz, kBz = proj_tok(1792, 2056, (256, 1288))
            ACT(f1[0:n, 512:768], pBz[0:n, 0:256], AF.Exp, [kBz], ["f1"], scale=-1.0)
            ACT(beta[0:n, 0:4], pBz[0:n, 260:264], AF.Exp, [kBz], ["beta"], scale=-1.0)
            ACT(g8[0:n, 0:4], pBz[0:n, 256:260], AF.Exp, [kBz], ["g8"])
            sigmoid_from_exp(f1[0:n, 512:768], "f1")
            TT("dve", f1[0:n, 512:768], f1[0:n, 512:768], prm[0:n, 256:512], ALU.mult, ["f1", "prm"], ["f1"])
            TT("dve", gs[0:n, 256:512], pBz[0:n, 0:256], f1[0:n, 512:768], ALU.mult, [kBz, "f1"], [gsk])
            pCz, kCz = proj_tok(2824, 3084, (256, 1292))
            ACT(f1[0:n, 512:768], pCz[0:n, 0:256], AF.Exp, [kCz], ["f1"], scale=-1.0)
            ACT(g8[0:n, 4:8], pCz[0:n, 256:260], AF.Exp, [kCz], ["g8"])
            sigmoid_from_exp(f1[0:n, 512:768], "f1")
            TT("dve", gs[0:n, 512:768], pCz[0:n, 0:256], f1[0:n, 512:768], ALU.mult, [kCz, "f1"], [gsk])
            ACT(g8[0:n, 0:8], g8[0:n, 0:8], AF.Ln, ["g8"], ["g8"], bias=eps_t[0:n, 1:2])
            CP("dve", dtb[0:n, :], g8[0:n, :], ["g8"], ["dtb"])
            TT("dve", g8[0:n, :], g8[0:n, :], nega[0:n, :], ALU.mult, ["g8", "nega"], ["g8"])
            ACT(egc[0:n, 0:8], g8[0:n, 0:8], AF.Exp, ["g8"], ["egc"])
            sigmoid_from_exp(beta[0:n, :], "beta")

            def conv_tok(cv, groups, st_in, st_out, bias):
                U = hb[0]; Uk = "hb0"
                for (c0, c1, o0) in groups:
                    pu, puk = proj_tok(c0, c1)
                    CP("dve", U[0:n, o0:o0 + (c1 - c0)], pu[0:n, 0:c1 - c0], [puk], [Uk])
                DMA(st_out[l, :, 2, :], U[0:n, 0:768], [Uk], [])
                Wt = eGbc[0:n, :, :].rearrange("p a b -> p (a b)")[:, 0:768]
                Ct = tt[0:n, :, :].rearrange("p a b -> p (a b)")[:, 0:768]
                DMA(Wt, cwrow_d[l, cv, 3], [], ["eGbc"])
                TT("dve", f1[0:n, 0:768], U[0:n, 0:768], Wt, ALU.mult, [Uk, "eGbc"], ["f1"])
                for w in range(3):
                    DMA(Ct, st_in[l, :, w, :], [], ["tt"])
                    DMA(Wt, cwrow_d[l, cv, w], [], ["eGbc"])
                    if w >= 1:
                        DMA(st_out[l, :, w - 1, :], Ct, ["tt"], [])
                    TT("dve", Wt, Ct, Wt, ALU.mult, ["tt", "eGbc"], ["eGbc"])
                    TT("dve", f1[0:n, 0:768], f1[0:n, 0:768], Wt, ALU.add, ["f1", "eGbc"], ["f1"])
                if bias:
                    DMA(Ct, cbrow_d[l], [], ["tt"])
                    TT("dve", f1[0:n, 0:768], f1[0:n, 0:768], Ct, ALU.add, ["f1", "tt"], ["f1"])
                ACT(Ct, f1[0:n, 0:768], AF.Exp, ["f1"], ["tt"], scale=-1.0)
                sigmoid_from_exp(Ct, "tt")
                TT("dve", f1[0:n, 0:768], f1[0:n, 0:768], Ct, ALU.mult, ["f1", "tt"], ["f1"])

            conv_tok(0, [(1024, 1536, 0), (1536, 1792, 512)], si_gc, so_gc, False)
            Ct = tt[0:n, :, :].rearrange("p a b -> p (a b)")[:, 0:512]
            ACT(Ct, f1[0:n, 0:512], AF.Square, ["f1"], ["tt"])
            RED(nb[0:n, 0:8], v3(Ct, 8), ["tt"], ["nb"])
            ACT(nb[0:n, 8:16], nb[0:n, 0:8], AF.Ln, ["nb"], ["nb"], bias=eps_t[0:n, 0:1])
            ACT(nb[0:n, 8:16], nb[0:n, 8:16], AF.Exp, ["nb"], ["nb"], scale=-0.5)
            TT("dve", v3(f1[0:n, 0:512], 8), v3(f1[0:n, 0:512], 8), bc_l(nb[0:n, 8:16], 64), ALU.mult, ["f1", "nb"], ["f1"])
            TS("dve", f1[0:n, 0:256], f1[0:n, 0:256], QK, ALU.mult, ["f1"], ["f1"])
            select([(0, 64, lambda hd: (f1[0:n, hd * 64:hd * 64 + 64], "f1")),
                    (64, 64, lambda hd: (f1[0:n, 256 + hd * 64:256 + hd * 64 + 64], "f1")),
                    (128, 64, lambda hd: (f1[0:n, 512 + hd * 64:512 + hd * 64 + 64], "f1")),
                    (192, 1, lambda hd: (egc[0:n, hd:hd + 1], "egc")),
                    (193, 1, lambda hd: (beta[0:n, hd:hd + 1], "beta"))])
            qB, kB, vB = pvs[0:64, 0:64], pvs[0:64, 64:128], pvs[0:64, 128:192]
            egB, btB = pvs[0:64, 192:193], pvs[0:64, 193:194]
            for k0, ks in state_io(si_gd, so_gd, 64):
                Sv, sk = load_slice(si_gd, k0, ks)
                T = Tview(ks)
                TT("dve", T, Sv, bc_l(kB[:, k0:k0 + ks], 64), ALU.mult, [sk, "f4"], [Tkey])
                RED(op_sb, T.rearrange("p k v -> p v k"), [Tkey], ["tmpS"])
                if k0 == 0:
                    CP("dve", w_sb, op_sb, ["tmpS"], ["tmpS"])
                else:
                    TT("dve", w_sb, w_sb, op_sb, ALU.add, ["tmpS"], ["tmpS"])
            TS("dve", w_sb, w_sb, egB, ALU.mult, ["tmpS", "f4"], ["tmpS"])
            TT("dve", u_sb, vB, w_sb, ALU.subtract, ["f4", "tmpS"], ["tmpS"])
            TS("dve", u_sb, u_sb, btB, ALU.mult, ["tmpS", "f4"], ["tmpS"])
            step_plain(si_gd, so_gd, 64, qB, kB, u_sb, sc=egB)
            po, pok = unselect(o_sb, "tmpS")
            head_norm(po[0:n, 0:256], pok, slice(256, 512), slice(256, 512))

            conv_tok(1, [(2056, 2568, 0), (2568, 2824, 512)], si_sc, so_sc, True)
            TT("dve", v3(f2[0:n, 0:256], 4), v3(f1[0:n, 0:256], 4), bc_l(dtb[0:n, 4:8], 64), ALU.mult, ["f1", "dtb"], ["f2"])
            select([(0, 128, lambda hd: (f1[0:n, 512 + (hd // 2) * 128:512 + (hd // 2) * 128 + 128], "f1")),
                    (128, 128, lambda hd: (f1[0:n, 256 + (hd // 2) * 128:256 + (hd // 2) * 128 + 128], "f1")),
                    (256, 64, lambda hd: (f2[0:n, hd * 64:hd * 64 + 64], "f2")),
                    (320, 1, lambda hd: (egc[0:n, 4 + hd:5 + hd], "egc"))])
            step_plain(si_sd, so_sd, 128, pvs[0:64, 0:128], pvs[0:64, 128:256], pvs[0:64, 256:320], sc=pvs[0:64, 320:321])
            po, pok = unselect(o_sb, "tmpS")
            TT("dve", v3(f3[0:n, 0:256], 4), v3(f1[0:n, 0:256], 4), bc_l(prm[0:n, 1296:1300], 64), ALU.mult, ["f1", "prm"], ["f3"])
            TT("dve", f3[0:n, 0:256], f3[0:n, 0:256], po[0:n, 0:256], ALU.add, ["f3", pok], ["f3"])
            TT("dve", f3[0:n, 0:256], f3[0:n, 0:256], gs[0:n, 512:768], ALU.mult, ["f3", gsk], ["f3"])
            ACT(f3[0:n, 256:512], f3[0:n, 0:256], AF.Square, ["f3"], ["f3"])
            RED(st4[0:n, 4:6], v3(f3[0:n, 256:512], 2), ["f3"], ["st4"])
            rsqrt_act(st4[0:n, 8:10], st4[0:n, 4:6], 1.0 / 128, ["st4"], ["st4"])
            TT("dve", v3(f3[0:n, 0:256], 2), v3(f3[0:n, 0:256], 2), bc_l(st4[0:n, 8:10], 128), ALU.mult, ["f3", "st4"], ["f3"])
            TT("dve", y_bf[0:n, 512:768], f3[0:n, 0:256], prm[0:n, 512:768], ALU.mult, ["f3", "prm"], ["y_bf"])

            for half in range(2):
                pt, pk = bank()
                for kk in range(4):
                    kc = half * 4 + kk
                    MM(pt[:, kk * 128:kk * 128 + n], y_bf[0:n, kc * 128:(kc + 1) * 128], ident_bf[0:n, 0:n],
                       ["y_bf", "ident_bf"], [pk])
                CP("act", yT[:, half * 4:half * 4 + 4, 0:n], v3(pt[:, :], 4)[:, :, 0:n], [pk], ["yT"])
            for cg in range(2):
                pt, pk = bank()
                for kc in range(8):
                    MM(pt[0:n, :], yT[:, kc, 0:n], wout[:, kc, cg * 512:(cg + 1) * 512], ["yT", "wout"], [pk],
                       start=(kc == 0), stop=(kc == 7))
                TT("dve", hs[:, cg * 512:(cg + 1) * 512], hs[:, cg * 512:(cg + 1) * 512], pt[0:n, :], ALU.add, ["hs", pk], ["hs"])
            if last:
                ACT(hn_bf[0:n, :], hs[:, :], AF.Square, ["hs"], ["hn_bf", "st4"], accum=st4[0:n, 0:1])
                rsqrt_act(st4[0:n, 1:2], st4[0:n, 0:1], 1.0 / D, ["st4"], ["st4"])
                STT(hs[:, :], hs[:, :], st4[0:n, 1:2], finw[0:n, :], ALU.mult, ALU.mult, ["hs", "st4", "lgt"], ["hs"])
                DMA(y_s, hs[:, :], ["hs"], [])

        for l in range(depth):
            load_layer(l)
            if not _os0.environ.get("NO_SAMPLE"):
                sample_fwd(l, l == depth - 1)
            if ntiles > 0:
                stage0(l, 0)
                stage0_pe(l, 0)
            for t in range(ntiles):
                more = t + 1 < ntiles
                tile_fwd(l, t, (lambda l_=l, t_=t: stage0(l_, t_ + 1)) if more else None,
                         (lambda l_=l, t_=t: stage0_pe(l_, t_ + 1)) if more else None)

        if _AUDIT:
            for b_ in sorted(_bad, key=str):
                print("AUDIT missing key:", b_)
        P.emit(es)
    return nc


_NC_CACHE = {}


def _prep_inputs(inp, c):
    f = np.float32
    prm = np.zeros((DEPTH, 128, NPRM), f)
    for l in range(DEPTH):
        row = np.concatenate([inp["hgrn_norm_w"][l], inp["gdn_norm_w"][l], inp["ssd_norm_w"][l], inp["ret_norm_w"][l],
                              inp["ret_norm_b"][l], inp["gdn_a_log"][l], inp["ssd_a_log"][l], inp["gdn_dt_bias"][l],
                              inp["ssd_dt_bias"][l], inp["ssd_d"][l]]).astype(f)
        prm[l] = np.broadcast_to(row[None, :], (128, NPRM))
    lgt = np.ascontiguousarray(np.broadcast_to(inp["hgrn_lb_logits"].reshape(1, 1024), (128, 1024))).astype(f)
    finw = np.ascontiguousarray(np.broadcast_to(inp["final_norm_w"].reshape(1, 1024), (128, 1024))).astype(f)
    featp = np.zeros((128, 32 + 192), f)
    featp[:, 0:32] = inp["norm_w"].reshape(DEPTH, 8, 128).transpose(2, 0, 1).reshape(128, 32)
    for l in range(DEPTH):
        for cv, key in enumerate(("gdn_conv_w", "ssd_conv_w")):
            w = inp[key][l].reshape(4, 6, 128)
            featp[:, 32 + l * 48 + cv * 24:32 + l * 48 + cv * 24 + 24] = w.transpose(2, 1, 0).reshape(128, 24)
    cbias = np.ascontiguousarray(inp["ssd_conv_b"].reshape(1, DEPTH * 768)).astype(f)
    cw = np.stack([inp["gdn_conv_w"], inp["ssd_conv_w"]], 1).astype(f)
    cwrow = np.ascontiguousarray(np.broadcast_to(cw[:, :, :, None, :], (DEPTH, 2, 4, NS, 768)))
    cbrow = np.ascontiguousarray(np.broadcast_to(inp["ssd_conv_b"].astype(f)[:, None, :], (DEPTH, NS, 768)))
    return {
        "xp": np.ascontiguousarray(inp["x_prompt"][c]).astype(f),
        "meta": np.ascontiguousarray(inp["meta_tokens"]).astype(f),
        "w_in": np.ascontiguousarray(inp["w_in"]).astype(f),
        "w_out": np.ascontiguousarray(inp["w_out"]).astype(f),
        "cst": CST, "rot": ROT, "prm": prm, "lgt": lgt, "finw": finw, "featp": featp, "cbias": cbias,
        "cwrow": cwrow, "cbrow": cbrow, **_sample_inputs(inp, c),
    }


def _sample_inputs(inp, c):
    f = np.float32
    sl = slice(c * NS, (c + 1) * NS)
    return {
        "xs_in": np.ascontiguousarray(inp["x_sample"][sl, 0, :]).astype(f),
        "si_hg": np.ascontiguousarray(inp["state_hgrn"][:, sl]).astype(f),
        "si_gd": np.ascontiguousarray(inp["state_gdn"][:, sl]).astype(f),
        "si_gc": np.ascontiguousarray(inp["state_gdn_conv"][:, sl]).astype(f),
        "si_sd": np.ascontiguousarray(inp["state_ssd"][:, sl]).astype(f),
        "si_sc": np.ascontiguousarray(inp["state_ssd_conv"][:, sl]).astype(f),
        "si_rt": np.ascontiguousarray(inp["state_ret"][:, sl]).astype(f),
    }


def kernel(**inp):
    inp = {k: np.asarray(v) for k, v in inp.items()}
    if "nc" not in _NC_CACHE:
        _NC_CACHE["nc"] = build_program()
    nc = _NC_CACHE["nc"]
    shared = None
    in_maps = []
    for c in range(8):
        m = _prep_inputs(inp, c) if shared is None else dict(shared)
        if shared is None:
            shared = m
        else:
            m["xp"] = np.ascontiguousarray(inp["x_prompt"][c]).astype(np.float32)
            m.update(_sample_inputs(inp, c))
        in_maps.append(m)
    res = run_bass_kernel_spmd(nc, in_maps, core_ids=list(range(8)))
    R = res.results
    y_prompt = np.stack([R[c]["y_p"] for c in range(8)], 0)
    def stk(name):
        return np.ascontiguousarray(np.stack([R[c][name] for c in range(8)], 1))
    y_sample = np.concatenate([R[c]["y_s"] for c in range(8)], 0)[:, None, :]
    def cat(name):
        return np.ascontiguousarray(np.concatenate([R[c][name] for c in range(8)], 1))
    outs = (y_prompt, np.ascontiguousarray(y_sample),
            stk("st_hg"), stk("st_gd"), stk("st_gc"), stk("st_sd"), stk("st_sc"), stk("st_rt"),
            cat("so_hg"), cat("so_gd"), cat("so_gc"), cat("so_sd"), cat("so_sc"), cat("so_rt"))
    return outs
```

```python
import contextlib
import math
import numpy as np
import concourse.bass as bass
import concourse.mybir as mybir
from concourse.bass_utils import run_bass_kernel_spmd

F32 = mybir.dt.float32
BF16 = mybir.dt.bfloat16
AF = mybir.ActivationFunctionType
ALU = mybir.AluOpType
AX = mybir.AxisListType

D = 1024
DEPTH = 4
SEQ = 2048
NT = 17
IN_DIM = 4108
EPS = 1e-6
QK = 0.125
NS = 16
NPRM = 1300
NEGV = -30000.0
import os as _os0
EMBED_WAIT = not _os0.environ.get("NO_EMBED")
ANNOTATE = bool(_os0.environ.get("ANNOTATE"))


class Prog:
    ENG = ("pe", "act", "dve", "pool", "sp")

    def __init__(self, nc, n_dma_sems=8):
        self.nc = nc
        self.ops = []
        self.cnt = {}
        self.clock = {e: {} for e in self.ENG}
        self.tok_clock = {}
        self.last_w = {}
        self.readers = {}
        self.n_dma = n_dma_sems
        self.dma_rr = {e: 0 for e in self.ENG}
        self.dma_last = {}
        self.anns = []
        import os
        self.pe_skip = not os.environ.get("PE_SELFWAIT")
        self.strict_same = not os.environ.get("RELAX_SAME")

    def _need(self, eng, tok, waits, force=False):
        key, idx = tok
        if key == "pe" and eng == "pe" and self.pe_skip and not force:
            return
        if self.clock[eng].get(key, 0) >= idx:
            return
        if waits.get(key, 0) < idx:
            waits[key] = idx

    def op(self, eng, fn, reads=(), writes=(), dma=False, pe_serial=False):
        waits = {}
        for b in reads:
            t = self.last_w.get(b)
            if t:
                self._need(eng, t, waits)
            if b.startswith("ps"):
                for r in self.readers.get(b, ()):
                    if r[0] != eng:
                        self._need(eng, r, waits)
        for b in writes:
            t = self.last_w.get(b)
            if t and (t[0] != eng or pe_serial or dma or self.strict_same):
                self._need(eng, t, waits, force=pe_serial)
            for r in self.readers.get(b, ()):
                if r[0] != eng or dma or self.strict_same:
                    self._need(eng, r, waits)
        if dma:
            key = ("dma", eng, self.dma_rr[eng] % self.n_dma)
            self.dma_rr[eng] += 1
            prev = self.dma_last.get(key)
            if prev:
                self._need(eng, prev, waits)
        else:
            key = eng
        ck = self.clock[eng]
        for kk, ii in waits.items():
            for k2, i2 in self.tok_clock.get((kk, ii), {}).items():
                if ck.get(k2, 0) < i2:
                    ck[k2] = i2
            if ck.get(kk, 0) < ii:
                ck[kk] = ii
        self.cnt[key] = self.cnt.get(key, 0) + 1
        tok = (key, self.cnt[key])
        if dma:
            self.dma_last[key] = tok
            snap = dict(ck)
            snap[key] = tok[1]
            self.tok_clock[tok] = snap
        else:
            snap = dict(ck)
            snap[key] = tok[1]
            self.tok_clock[tok] = snap
        for b in writes:
            self.last_w[b] = tok
            self.readers[b] = []
        for b in reads:
            if b not in writes:
                self.readers.setdefault(b, []).append(tok)
        ann = None
        if ANNOTATE:
            import sys as _sys
            f = _sys._getframe(1)
            while f:
                if f.f_code.co_name in ("tile_fwd", "sample_fwd", "load_layer"):
                    ann = "L%d" % f.f_lineno
                    break
                f = f.f_back
        self.anns.append(ann)
        self.ops.append((eng, fn, list(waits.items()), tok, dma))
        return tok

    def emit(self, es, final_wait_eng="sp"):
        nc = self.nc
        import os
        km = int(os.environ.get("KMAX", "0"))
        if km:
            self.ops = self.ops[:km]
            self.dma_last = {}
            for (e_, f_, w_, tok_, d_) in self.ops:
                if d_:
                    self.dma_last[tok_[0]] = tok_
        needed = set()
        for (_, _, waits, _, _) in self.ops:
            for w in waits:
                needed.add(w)
        finals = []
        for k, t in self.dma_last.items():
            finals.append(t)
            needed.add(t)
        per_key = {}
        for (k, i) in needed:
            per_key.setdefault(k, []).append(i)
        sigcount = {}
        for k, lst in per_key.items():
            for n, i in enumerate(sorted(lst)):
                sigcount[(k, i)] = n + 1
        sems = {}
        for k in sorted(per_key.keys(), key=str):
            nm = "s_" + "_".join(str(x) for x in (k if isinstance(k, tuple) else (k,)))
            sems[k] = es.enter_context(nc.semaphore(nm))
        per_eng = {e: [] for e in self.ENG}
        for j, o in enumerate(self.ops):
            per_eng[o[0]].append(o + (self.anns[j] if j < len(self.anns) else None,))
        blk = es.enter_context(nc.Block())

        def run(e, engobj):
            for (_, fn, waits, tok, dma, ann) in per_eng[e]:
                emb = None
                if waits and EMBED_WAIT and not dma:
                    emb = waits[-1]
                    waits = waits[:-1]
                for (k, i) in waits:
                    mult = 16 if isinstance(k, tuple) else 1
                    engobj.wait_ge(sems[k], sigcount[(k, i)] * mult)
                ins = fn(engobj)
                if ann is not None:
                    ins.annotate(ann)
                if emb is not None:
                    k, i = emb
                    ins._wait_ge(sems[k], sigcount[(k, i)] * (16 if isinstance(k, tuple) else 1))
                if tok in sigcount:
                    ins.then_inc(sems[tok[0]], 16 if dma else 1)
            if e == final_wait_eng:
                for t in finals:
                    engobj.wait_ge(sems[t[0]], sigcount[t] * 16)

        @blk.tensor
        def _(e):
            run("pe", e)

        @blk.scalar
        def _(e):
            run("act", e)

        @blk.vector
        def _(e):
            run("dve", e)

        @blk.gpsimd
        def _(e):
            run("pool", e)

        @blk.sync
        def _(e):
            run("sp", e)


def host_consts():
    idx = np.arange(128)
    ch = idx // 64
    same = ch[:, None] == ch[None, :]
    ident = np.eye(128, dtype=np.float32)
    maskT = (same & (idx[:, None] <= idx[None, :])).astype(np.float32)
    negT = np.where(maskT > 0, 0.0, NEGV).astype(np.float32)
    strict = (same & (idx[None, :] < idx[:, None]))
    negS = np.where(strict, 0.0, NEGV).astype(np.float32)
    mid = ch * 64 + 31
    uprime = (same & (idx[:, None] <= idx[None, :])).astype(np.float32) - \
             (same & (idx[:, None] <= mid[None, :])).astype(np.float32)
    urev = (same & (idx[:, None] > idx[None, :])).astype(np.float32)
    wc = np.zeros((128, 8), np.float32)
    wc[:, 0] = (idx <= 31)
    wc[:, 1] = (idx >= 64) & (idx <= 95)
    wc[:, 2] = (idx >= 32) & (idx <= 63)
    wc[:, 3] = (idx >= 96)
    wc[:, 4] = (idx <= 63)
    wc[:, 5] = (idx >= 64)
    blockones = same.astype(np.float32)
    lg = np.log1p(-np.exp2(-5.0 - np.arange(4, dtype=np.float64)))
    loc = idx % 64
    dt_ret = np.zeros((128, 4, 128), np.float64)
    for h in range(4):
        dt_ret[:, h, :] = np.where(maskT > 0, np.exp(lg[h] * (idx[None, :] - idx[:, None])), 0.0) * QK
    egq = np.zeros((128, 2, 128), np.float64)
    for hp in range(2):
        for hh in range(2):
            egq[hh * 64:(hh + 1) * 64, hp, :] = np.exp(lg[2 * hp + hh] * (loc[None, :] + 1))
    egrev64 = np.zeros((128, 4), np.float64)
    egrev16 = np.zeros((128, 4), np.float64)
    for h in range(4):
        egrev64[:, h] = np.exp(lg[h] * (63 - loc)) * QK
        egrev16[:, h] = np.exp(lg[h] * np.maximum(15 - idx, 0)) * QK
    egl = np.zeros((128, 2, 2), np.float64)
    for hp in range(2):
        for hh in range(2):
            egl[hh * 64:(hh + 1) * 64, hp, 0] = np.exp(lg[2 * hp + hh] * 16)
            egl[hh * 64:(hh + 1) * 64, hp, 1] = np.exp(lg[2 * hp + hh] * 64)
    sel = np.zeros((128, 4, 64), np.float32)
    selT = np.zeros((128, 4, 16), np.float32)
    gam64 = np.zeros((128, 4), np.float32)
    for h in range(4):
        for b in range(16):
            sel[b, h, h * 16 + b] = 1.0
            selT[h * 16 + b, h, b] = 1.0
            gam64[h * 16 + b, 0] = np.exp(lg[h])
    parts = [ident, maskT, negT, negS, uprime, urev, wc, blockones,
             dt_ret.reshape(128, 512), egq.reshape(128, 256), egrev64, egrev16, egl.reshape(128, 4),
             sel.reshape(128, 256), selT.reshape(128, 64), gam64]
    offs = {}
    names = ["ident", "maskT", "negT", "negS", "uprime", "urev", "wc", "blockones",
             "dt_ret", "egq", "egrev64", "egrev16", "egl", "sel", "selT", "gam64"]
    o = 0
    for nm, p in zip(names, parts):
        offs[nm] = (o, p.shape[1])
        o += p.shape[1]
    cst = np.concatenate([p.astype(np.float32) for p in parts], axis=1)
    half = 32
    inv_freq = (1.0 / (np.float32(10000.0) ** np.linspace(0.0, 1.0, half, dtype=np.float32))).astype(np.float32)
    rot = np.zeros((NT + 1, 128, 64), np.float32)
    for t in range(NT):
        pos = (np.arange(128) if t == 0 else 16 + (t - 1) * 128 + np.arange(128)).astype(np.float32)
        ang = (pos[:, None] * inv_freq[None, :]).astype(np.float32)
        rot[t, :, 0:32] = np.cos(ang)
        rot[t, :, 32:64] = np.sin(ang)
    ang = (np.full((128, 1), 16384.0, np.float32) * inv_freq[None, :]).astype(np.float32)
    rot[NT, :, 0:32] = np.cos(ang)
    rot[NT, :, 32:64] = np.sin(ang)
    return cst, offs, rot


CST, COFF, ROT = host_consts()
NCST = CST.shape[1]


def build_program(depth=DEPTH, ntiles=NT):
    nc = bass.Bass("TRN2", target_bir_lowering=False)

    def din(name, shape):
        return nc.dram_tensor(name, list(shape), F32, kind="ExternalInput").ap()

    def dout(name, shape):
        return nc.dram_tensor(name, list(shape), F32, kind="ExternalOutput").ap()

    xp = din("xp", [SEQ, D])
    meta = din("meta", [16, D])
    w_in = din("w_in", [DEPTH, D, IN_DIM])
    w_out = din("w_out", [DEPTH, D, D])
    cst_d = din("cst", [128, NCST])
    rot_d = din("rot", [NT + 1, 128, 64])
    prm_d = din("prm", [DEPTH, 128, NPRM])
    lgt_d = din("lgt", [128, 1024])
    finw_d = din("finw", [128, 1024])
    featp_d = din("featp", [128, 32 + 192])
    cbias_d = din("cbias", [1, DEPTH * 768])

    y_p = dout("y_p", [SEQ, D])
    st_hg = dout("st_hg", [DEPTH, 4, 64, 64])
    st_gd = dout("st_gd", [DEPTH, 4, 64, 64])
    st_gc = dout("st_gc", [DEPTH, 3, 768])
    st_sd = dout("st_sd", [DEPTH, 4, 128, 64])
    st_sc = dout("st_sc", [DEPTH, 3, 768])
    st_rt = dout("st_rt", [DEPTH, 4, 64, 64])
    xs_d = din("xs_in", [NS, D])
    si_hg = din("si_hg", [DEPTH, NS, 4, 64, 64]); si_gd = din("si_gd", [DEPTH, NS, 4, 64, 64])
    si_gc = din("si_gc", [DEPTH, NS, 3, 768]); si_sd = din("si_sd", [DEPTH, NS, 4, 128, 64])
    si_sc = din("si_sc", [DEPTH, NS, 3, 768]); si_rt = din("si_rt", [DEPTH, NS, 4, 64, 64])
    cwrow_d = din("cwrow", [DEPTH, 2, 4, NS, 768])
    cbrow_d = din("cbrow", [DEPTH, NS, 768])
    y_s = dout("y_s", [NS, D])
    so_hg = dout("so_hg", [DEPTH, NS, 4, 64, 64]); so_gd = dout("so_gd", [DEPTH, NS, 4, 64, 64])
    so_gc = dout("so_gc", [DEPTH, NS, 3, 768]); so_sd = dout("so_sd", [DEPTH, NS, 4, 128, 64])
    so_sc = dout("so_sc", [DEPTH, NS, 3, 768]); so_rt = dout("so_rt", [DEPTH, NS, 4, 64, 64])

    with contextlib.ExitStack() as es:
        def sb(name, shape, dt=F32):
            return es.enter_context(nc.sbuf_tensor("sb_" + name, list(shape), dt))

        P = Prog(nc)

        hscr = nc.dram_tensor("hscr", [NT * 128, D], F32, kind="Internal").ap()
        hb = [sb(f"hb{i}", [128, D]) for i in range(2)]
        win = sb("win", [128, 8, IN_DIM], BF16)
        wout = sb("wout", [128, 8, D], BF16)
        WCH = 1027
        wst = [sb(f"wst{i}", [128, WCH]) for i in range(2)]
        cst = sb("cst", [128, NCST])
        ident_bf = sb("ident_bf", [128, 128], BF16)
        bones_bf = sb("bones_bf", [128, 128], BF16)
        maskT_bf = sb("maskT_bf", [128, 128], BF16)
        ghi = sb("ghi", [128, 4, 128], BF16)
        glo = sb("glo", [128, 4, 128], BF16)
        dtret_bf = sb("dtret_bf", [128, 4, 128], BF16)
        ones_bf = sb("ones_bf", [1, 128], BF16)
        prm = sb("prm", [128, NPRM])
        oml = sb("oml", [128, 4, 256])
        lgt = sb("lgt", [128, 4, 256])
        featp = sb("featp", [128, 32 + 192])
        dg = sb("dg", [128, 12, 4, 128], BF16)
        cbias_bf = sb("cbias_bf", [1, 768], BF16)
        nega = sb("nega", [128, 64])
        rot = [sb(f"rot{i}", [128, 64]) for i in range(2)]

        def C(name, rows=slice(0, 128), lo=0, hi=None):
            o, w = COFF[name]
            hi = w if hi is None else hi
            return cst[rows, o + lo:o + hi]

        psb = [es.enter_context(nc.psum_tensor(f"ps{i}", [128, 512], F32)) for i in range(8)]
        ps_rr = [0]

        def bank():
            i = ps_rr[0] % 4 if ps_rr[0] < 0 else (0, 1, 2, 3, 6, 7)[ps_rr[0] % 6]
            ps_rr[0] += 1
            return psb[i], f"ps{i}"

        import os as _os
        _AUDIT = bool(_os.environ.get("AUDIT"))
        _bad = set()

        def _chk(r, w, outs, ins):
            if not _AUDIT:
                return
            for grp, keys, what in ((outs, list(w), "W"), (ins, list(r) + list(w), "R")):
                for ap in grp:
                    nm = getattr(ap, "name", None)
                    if not isinstance(nm, str):
                        continue
                    key = nm[3:] if nm.startswith("sb_") else nm
                    if key not in keys:
                        import traceback
                        fr = traceback.extract_stack()[-3]
                        _bad.add((what, key, fr.lineno))

        _last_rb = {}

        def MM(out, lhsT, rhs, r, w, start=True, stop=True):
            skip = any(k in ("ps4", "ps5") for k in w)
            _chk(r, w, [out], [lhsT, rhs])
            rb = lhsT.base_partition()
            ser = False
            for k in w:
                if _last_rb.get(k, rb) != rb:
                    ser = True
                _last_rb[k] = rb
            P.op("pe", lambda e: e.matmul(out, lhsT=lhsT, rhs=rhs, start=start, stop=stop, skip_group_check=skip),
                 reads=r, writes=w, pe_serial=ser)

        def ACT(out, in_, func, r, w, scale=1.0, bias=None, accum=None):
            kw = {}
            if bias is not None:
                kw["bias"] = bias
            if accum is not None:
                kw["accum_out"] = accum
            if hasattr(bias, "name") and "eps_t" not in r:
                r = list(r) + ["eps_t"]
            _chk(r, w, [out] + ([accum] if accum is not None else []), [in_] + [x for x in (scale, bias) if hasattr(x, "name")])
            P.op("act", lambda e: e.activation(out=out, in_=in_, func=func, scale=scale, **kw), reads=r, writes=w)

        def TT(eng, out, in0, in1, op, r, w):
            _chk(r, w, [out], [in0, in1])
            P.op(eng, lambda e: e.tensor_tensor(out=out, in0=in0, in1=in1, op=op), reads=r, writes=w)

        def TS(eng, out, in0, s1, op0, r, w, s2=None, op1=None):
            _chk(r, w, [out], [in0] + [x for x in (s1, s2) if hasattr(x, "name")])
            if op1 is None:
                P.op(eng, lambda e: e.tensor_scalar(out=out, in0=in0, scalar1=s1, scalar2=None, op0=op0), reads=r, writes=w)
            else:
                P.op(eng, lambda e: e.tensor_scalar(out=out, in0=in0, scalar1=s1, scalar2=s2, op0=op0, op1=op1), reads=r, writes=w)

        def STT(out, in0, scalar, in1, op0, op1, r, w):
            _chk(r, w, [out], [in0, in1] + [x for x in (scalar,) if hasattr(x, "name")])
            P.op("dve", lambda e: e.scalar_tensor_tensor(out=out, in0=in0, scalar=scalar, in1=in1, op0=op0, op1=op1),
                 reads=r, writes=w)

        def RED(out, in_, r, w):
            _chk(r, w, [out], [in_])
            P.op("dve", lambda e: e.tensor_reduce(out=out, in_=in_, axis=AX.X, op=ALU.add), reads=r, writes=w)

        def RECIP(out, in_, r, w):
            _chk(r, w, [out], [in_])
            P.op("dve", lambda e: e.reciprocal(out=out, in_=in_), reads=r, writes=w)

        def CP(eng, out, in_, r, w):
            if eng == "act":
                ACT(out, in_, AF.Copy, r, w)
            else:
                _chk(r, w, [out], [in_])
                P.op(eng, lambda e: e.tensor_copy(out=out, in_=in_), reads=r, writes=w)

        def MEMSET(eng, ap, val, w):
            P.op(eng, lambda e: e.memset(ap, val), reads=(), writes=w)

        def DMA(out, in_, r, w, slow=False):
            if slow:
                P.op("sp", lambda e: e.dma_start(out=out, in_=in_, allow_slow_non_contiguous=True), reads=r, writes=w, dma=True)
            else:
                P.op("sp", lambda e: e.dma_start(out=out, in_=in_), reads=r, writes=w, dma=True)

        def sigmoid_from_exp(buf, key):
            ACT(buf, buf, AF.Ln, [key], [key], bias=eps_t[0:buf.shape[0], 1:2])
            ACT(buf, buf, AF.Exp, [key], [key], scale=-1.0)

        def rsqrt_act(out, in_, scale, r, w):
            ACT(out, in_, AF.Ln, r, w, scale=scale, bias=eps_t[0:out.shape[0], 0:1])
            ACT(out, out, AF.Exp, w, w, scale=-0.5)

        eps_t = sb("eps_t", [128, 2])
        MEMSET("pool", eps_t[:, 0:1], EPS, ["eps_t"])
        MEMSET("pool", eps_t[:, 1:2], 1.0, ["eps_t"])
        DMA(cst[:], cst_d, [], ["cst"])
        DMA(lgt[:].rearrange("p a b -> p (a b)"), lgt_d, [], ["lgt"])
        DMA(featp[:], featp_d, [], ["featp"])
        CP("dve", ident_bf[:], C("ident"), ["cst"], ["ident_bf"])
        CP("dve", bones_bf[:], C("blockones"), ["cst"], ["bones_bf"])
        CP("dve", maskT_bf[:], C("maskT"), ["cst"], ["maskT_bf"])
        CP("dve", dtret_bf[:].rearrange("p a b -> p (a b)"), C("dt_ret"), ["cst"], ["dtret_bf"])
        MEMSET("pool", ones_bf[:], 1.0, ["ones_bf"])
        mx = wst[1][:, 0:256]
        TT("dve", mx, lgt[:, 0, :], lgt[:, 1, :], ALU.max, ["lgt"], ["wst1"])
        TT("dve", mx, mx, lgt[:, 2, :], ALU.max, ["lgt", "wst1"], ["wst1"])
        TT("dve", mx, mx, lgt[:, 3, :], ALU.max, ["lgt", "wst1"], ["wst1"])
        TT("dve", lgt[:], lgt[:], mx.unsqueeze(1).to_broadcast([128, 4, 256]), ALU.subtract, ["lgt", "wst1"], ["lgt"])
        ACT(lgt[:], lgt[:], AF.Exp, ["lgt"], ["lgt"])
        TT("dve", mx, lgt[:, 0, :], lgt[:, 1, :], ALU.add, ["lgt"], ["wst1"])
        TT("dve", mx, mx, lgt[:, 2, :], ALU.add, ["lgt", "wst1"], ["wst1"])
        TT("dve", mx, mx, lgt[:, 3, :], ALU.add, ["lgt", "wst1"], ["wst1"])
        RECIP(mx, mx, ["wst1"], ["wst1"])
        TT("dve", lgt[:], lgt[:], mx.unsqueeze(1).to_broadcast([128, 4, 256]), ALU.mult, ["lgt", "wst1"], ["lgt"])
        MEMSET("dve", oml[:, 0, :], 0.0, ["oml"])
        CP("dve", oml[:, 1, :], lgt[:, 1, :], ["lgt"], ["oml"])
        TT("dve", oml[:, 2, :], oml[:, 1, :], lgt[:, 2, :], ALU.add, ["lgt", "oml"], ["oml"])
        TT("dve", oml[:, 3, :], oml[:, 2, :], lgt[:, 3, :], ALU.add, ["lgt", "oml"], ["oml"])
        TS("dve", oml[:], oml[:], 0.0, ALU.max, ["oml"], ["oml"])
        TS("dve", oml[:], oml[:], -1.0, ALU.mult, ["oml"], ["oml"], s2=1.0, op1=ALU.add)
        DMA(lgt[:].rearrange("p a b -> p (a b)"), finw_d, ["lgt"], ["lgt"])
        finw = lgt[:].rearrange("p a b -> p (a b)")

        hn_bf = sb("hn_bf", [128, D], BF16)
        hnTs = [sb(f"hnT{i}", [128, 8, 128], BF16) for i in range(2)]
        hnT = hnTs[1]
        st0 = sb("st0", [128, 2])
        st4 = sb("st4", [128, 16])
        f1 = sb("f1", [128, 768])
        f2 = sb("f2", [128, 512])
        f3 = sb("f3", [128, 512])
        f4 = sb("f4", [128, 512])
        gate = sb("gate", [128, D])
        y_bf = sb("y_bf", [128, D], BF16)
        yT = sb("yT", [128, 8, 128], BF16)
        b1 = sb("b1", [128, 4, 128], BF16)
        qkT = sb("qkT", [128, 4, 128], BF16)
        AT = sb("AT", [128, 4, 128], BF16)
        AT2 = sb("AT2", [128, 4, 128], BF16)
        v_bf = sb("v_bf", [128, 4, 64], BF16)
        kp_bf = sb("kp_bf", [128, 4, 128], BF16)
        qpT = sb("qpT", [128, 4, 128], BF16)
        ecs = sb("ecs", [128, 2, 8])
        S_A = sb("S_A", [128, 2, 64]); Sb_A = sb("Sb_A", [128, 2, 64], BF16); Sd_A = sb("Sd_A", [128, 2, 64])
        S_B = sb("S_B", [128, 2, 64]); Sb_B = sb("Sb_B", [128, 2, 64], BF16)
        S_C = sb("S_C", [128, 4, 64]); Sb_C = sb("Sb_C", [128, 4, 64], BF16)
        S_D = sb("S_D", [128, 2, 64]); Sb_D = sb("Sb_D", [128, 2, 64], BF16)
        tmpS = sb("tmpS", [128, 4, 64])
        uT = [sb(f"uT{i}", [128, 12, 131], BF16) for i in range(2)]
        cvst = sb("cvst", [128, 12, 3])
        xs = sb("xs", [128, 12, 128], BF16)
        xsf = sb("xsf", [128, 4, 128])
        g8 = sb("g8", [128, 64])
        gc = sb("gc", [128, 64])
        egc = sb("egc", [128, 64])
        beta = sb("beta", [128, 64])
        dtb = sb("dtb", [128, 64])
        nb = sb("nb", [128, 64])
        nb2 = sb("nb2", [128, 64])
        for _t, _k in ((g8, "g8"), (beta, "beta"), (nega, "nega")):
            MEMSET("pool", _t[:], 0.0, [_k])
        tt = sb("tt", [128, 8, 128])
        DT = sb("DT", [128, 8, 128], BF16)
        Dst = sb("Dst", [128, 4, 128], BF16)
        eGbc = sb("eGbc", [128, 8, 128])
        eGlB = sb("eGlB", [128, 2, 2])
        X_bf = [sb(f"X_bf{i}", [128, 4, 128], BF16) for i in range(2)]
        Y_bf = [sb(f"Y_bf{i}", [128, 4, 128], BF16) for i in range(2)]
        P_bf = sb("P_bf", [128, 4, 128], BF16)
        bv = sb("bv", [128, 4, 64])
        r_bf = sb("r_bf", [128, 4, 64], BF16)
        u_bf = sb("u_bf", [128, 4, 64], BF16)
        xd_bf = sb("xd_bf", [128, 4, 64], BF16)

        def load_layer(l):
            DMA(prm[:], prm_d[l], [], ["prm"])
            DMA(wst[0][0:1, 0:768], cbias_d[0:1, l * 768:(l + 1) * 768], [], ["wst0"])
            CP("pool", cbias_bf[:], wst[0][0:1, 0:768], ["wst0"], ["cbias_bf"])
            ACT(nega[:, 0:8], prm[:, 1280:1288], AF.Exp, ["prm"], ["nega"])
            TS("dve", nega[:, 0:64], nega[:, 0:64], -1.0, ALU.mult, ["nega"], ["nega"])
            for cv in range(2):
                for blk in range(6):
                    for w in range(4):
                        col = 32 + l * 48 + cv * 24 + blk * 4 + w
                        if (blk + w) % 2 == 0:
                            ACT(dg[:, cv * 6 + blk, w, :], ident_bf[:], AF.Copy, ["ident_bf", "featp"], ["dg"],
                                scale=featp[:, col:col + 1])
                        else:
                            TS("dve", dg[:, cv * 6 + blk, w, :], ident_bf[:], featp[:, col:col + 1], ALU.mult,
                               ["ident_bf", "featp"], ["dg"])
            i = 0
            for kc in range(8):
                for c0 in range(0, IN_DIM, WCH):
                    st = wst[i % 2]; sk = f"wst{i % 2}"
                    DMA(st[:, 0:WCH], w_in[l, kc * 128:(kc + 1) * 128, c0:c0 + WCH], [], [sk])
                    if i % 2 == 0:
                        ACT(win[:, kc, c0:c0 + WCH], st[:, 0:WCH], AF.Copy, [sk, "featp"], ["win"],
                            scale=featp[:, l * 8 + kc:l * 8 + kc + 1])
                    else:
                        TS("dve", win[:, kc, c0:c0 + WCH], st[:, 0:WCH], featp[:, l * 8 + kc:l * 8 + kc + 1], ALU.mult,
                           [sk, "featp"], ["win"])
                    i += 1
            for kc in range(8):
                st = wst[i % 2]; sk = f"wst{i % 2}"
                DMA(st[:, 0:1024], w_out[l, kc * 128:(kc + 1) * 128, :], [], [sk])
                CP("act" if i % 2 == 0 else "dve", wout[:, kc, :], st[:, 0:1024], [sk], ["wout"])
                i += 1
            for nm, S_, Sb_ in (("A", S_A, Sb_A), ("B", S_B, Sb_B), ("C", S_C, Sb_C), ("D", S_D, Sb_D)):
                MEMSET("pool", S_[:], 0.0, ["S_" + nm])
                MEMSET("pool", Sb_[:], 0.0, ["Sb_" + nm])
            MEMSET("pool", uT[0][:, :, 0:3], 0.0, ["uT0"])

        def stage0(l, t):
            n = 16 if t == 0 else 128
            hk = f"hb{t % 2}"
            ht = hb[t % 2][0:n, :]
            hnT = hnTs[t % 2]; hnTk = f"hnT{t % 2}"
            if l == 0:
                DMA(ht, meta if t == 0 else xp[(t - 1) * 128:t * 128, :], [], [hk])
            else:
                DMA(ht, hscr[t * 128:t * 128 + n, :], [f"hd{t}"], [hk])
            ACT(hn_bf[0:n, :], ht, AF.Square, [hk], ["hn_bf", "st0"], accum=st0[0:n, 0:1])
            ACT(st0[0:n, 1:2], st0[0:n, 0:1], AF.Ln, ["st0"], ["st0"], scale=1.0 / D, bias=eps_t[0:n, 0:1])
            ACT(st0[0:n, 1:2], st0[0:n, 1:2], AF.Exp, ["st0"], ["st0"], scale=-0.5)
            ACT(hn_bf[0:n, :], ht, AF.Copy, [hk, "st0"], ["hn_bf"], scale=st0[0:n, 1:2])

        def stage0_pe(l, t):
            n = 16 if t == 0 else 128
            hnT = hnTs[t % 2]; hnTk = f"hnT{t % 2}"
            for half in range(2):
                pt, pk = bank()
                for kk in range(4):
                    kc = half * 4 + kk
                    MM(pt[:, kk * 128:kk * 128 + n], hn_bf[0:n, kc * 128:(kc + 1) * 128], ident_bf[0:n, 0:n],
                       ["hn_bf", "ident_bf"], [pk])
                CP("act" if half else "dve", hnT[:, half * 4:half * 4 + 4, 0:n],
                   pt[:, :].rearrange("p (a b) -> p a b", a=4)[:, :, 0:n], [pk], [hnTk])

        def tile_fwd(l, t, mid_hook=None, late_hook=None):
            n = 16 if t == 0 else 128
            chunks = [(0, 16)] if t == 0 else [(0, 64), (64, 128)]
            nch = len(chunks)
            clen = chunks[0][1]
            last_tile = (t == ntiles - 1)
            hk = f"hb{t % 2}"
            ht = hb[t % 2][0:n, :]
            hnT = hnTs[t % 2]; hnTk = f"hnT{t % 2}"
            cur = uT[t % 2]; curk = f"uT{t % 2}"
            pO2, kO2 = psb[5], "ps5"
            pOC, kOC = psb[5], "ps5"
            nxt = uT[(t + 1) % 2]; nxtk = f"uT{(t + 1) % 2}"
            rt = rot[t % 2]; rtk = f"rot{t % 2}"
            DMA(rt[:], rot_d[t], [], [rtk])

            def bc_h(ap2d, nh=4):
                return ap2d.unsqueeze(1).to_broadcast([ap2d.shape[0], nh, ap2d.shape[1]])

            def bc_l(ap2d, m):
                return ap2d.unsqueeze(2).to_broadcast([ap2d.shape[0], ap2d.shape[1], m])

            def proj_tok(c0, c1, extra=None):
                pt, pk = bank()
                for kc in range(8):
                    MM(pt[0:n, 0:c1 - c0], hnT[:, kc, 0:n], win[:, kc, c0:c1], [hnTk, "win"], [pk],
                       start=(kc == 0), stop=(kc == 7 and extra is None))
                if extra is not None:
                    MM(pt[0:n, extra[0]:extra[0] + 4], C("ident", slice(0, n), 0, n), prm[0:n, extra[1]:extra[1] + 4],
                       ["cst", "prm"], [pk], start=False, stop=True)
                return pt, pk

            def proj_feat(cols0, nblk):
                pt, pk = bank()
                for b_ in range(nblk):
                    for kc in range(8):
                        MM(pt[:, b_ * 128:b_ * 128 + n], win[:, kc, cols0 + b_ * 128:cols0 + (b_ + 1) * 128], hnT[:, kc, 0:n],
                           [hnTk, "win"], [pk], start=(kc == 0), stop=(kc == 7))
                return pt, pk

            def v3(ps_ap, a):
                return ps_ap.rearrange("p (a b) -> p a b", a=a)

            pA0, kA0 = proj_tok(0, 512)
            pA1, kA1 = proj_tok(512, 1024)
            ACT(f1[0:n, 0:256], pA0[0:n, 0:256], AF.Exp, [kA0], ["f1"], scale=-1.0)
            ACT(f1[0:n, 256:512], pA0[0:n, 256:512], AF.Exp, [kA0], ["f1"])
            ACT(f1[0:n, 512:768], pA1[0:n, 256:512], AF.Exp, [kA1], ["f1"], scale=-1.0)
            sigmoid_from_exp(f1[0:n, :], "f1")
            STT(f2[0:n, 0:256], pA0[0:n, 0:256], QK, f1[0:n, 0:256], ALU.mult, ALU.mult, [kA0, "f1"], ["f2"])
            TT("dve", f2[0:n, 256:512], f1[0:n, 256:512], oml[0:n, l, :], ALU.mult, ["f1", "oml"], ["f2"])
            TT("dve", f1[0:n, 512:768], f1[0:n, 512:768], prm[0:n, 0:256], ALU.mult, ["f1", "prm"], ["f1"])
            TT("dve", gate[0:n, 0:256], pA1[0:n, 256:512], f1[0:n, 512:768], ALU.mult, [kA1, "f1"], ["gate"])
            ACT(v_bf[0:n, :, :].rearrange("p a b -> p (a b)"), pA1[0:n, 0:256], AF.Copy, [kA1], ["v_bf"])
            ACT(f3[0:n, 0:256], f2[0:n, 256:512], AF.Ln, ["f2"], ["f3"], scale=-1.0, bias=eps_t[0:n, 1:2])
            pG, kG = bank()
            MM(pG[0:n, 0:256], C("uprime", slice(0, n), 0, n), f3[0:n, 0:256], ["cst", "f3"], [kG])
            for hp in range(2):
                MM(pG[:, 256 + hp * 8:256 + hp * 8 + 8], f3[0:n, hp * 128:(hp + 1) * 128], C("wc", slice(0, n)),
                   ["cst", "f3"], [kG])
            ACT(f3[0:n, 0:256], pG[0:n, 0:256], AF.Exp, [kG], ["f3"])
            ACT(f3[0:n, 256:512], pG[0:n, 0:256], AF.Exp, [kG], ["f3"], scale=-1.0)
            ACT(ecs[:].rearrange("p a b -> p (a b)"), pG[:, 256:272], AF.Exp, [kG], ["ecs"])
            TT("dve", b1[0:n, 0:2, :].rearrange("p a b -> p (a b)"), f2[0:n, 0:256], f3[0:n, 0:256], ALU.mult,
               ["f2", "f3"], ["b1"])
            TT("dve", b1[0:n, 2:4, :].rearrange("p a b -> p (a b)"), f2[0:n, 256:512], f3[0:n, 256:512], ALU.mult,
               ["f2", "f3"], ["b1"])
            pT, kT = bank()
            for blk in range(4):
                MM(pT[:, blk * 128:blk * 128 + n], b1[0:n, blk, :], ident_bf[0:n, 0:n], ["b1", "ident_bf"], [kT])
            CP("act", qkT[:, :, 0:n], v3(pT[:, :], 4)[:, :, 0:n], [kT], ["qkT"])
            pS, kS = bank()
            for hd in (0, 2, 1, 3):
                hp, hh = hd // 2, hd % 2
                rows = slice(hh * 64, hh * 64 + 64)
                MM(pS[0:n, hd * 128:hd * 128 + n], qkT[rows, 2 + hp, 0:n], qkT[rows, hp, 0:n], ["qkT"], [kS])
            TT("dve", AT[0:n, :, 0:n], v3(pS[0:n, :], 4)[:, :, 0:n], bc_h(C("maskT", slice(0, n), 0, n)), ALU.mult,
               [kS, "cst"], ["AT"])
            pO, kO = psb[4], "ps4"
            pOD, kOD = psb[4], "ps4"
            for hd in (0, 2, 1, 3):
                MM(pO[0:n, hd * 64:hd * 64 + 64], AT[0:n, hd, 0:n], v_bf[0:n, hd, :], ["AT", "v_bf"], [kO],
                   start=(hd == 0), stop=False)
            for ci, (c0, c1) in enumerate(chunks):
                TT("dve", Sb_A[:], S_A[:], bc_l(ecs[:, :, ci], 64), ALU.mult, ["S_A", "ecs"], ["Sb_A"])
                TT("dve", Sd_A[:], S_A[:], bc_l(ecs[:, :, 4 + ci], 64), ALU.mult, ["S_A", "ecs"], ["Sd_A"])
                pK, kK = bank()
                for hd in (0, 2, 1, 3):
                    hp, hh = hd // 2, hd % 2
                    rows = slice(hh * 64, hh * 64 + 64)
                    MM(pO[c0:c1, hd * 64:hd * 64 + 64], qkT[rows, hp, c0:c1], Sb_A[rows, hp, :], ["qkT", "Sb_A"], [kO],
                       start=False, stop=True)
                    MM(pK[rows, hp * 64:hp * 64 + 64], b1[c0:c1, 2 + hp, hh * 64:hh * 64 + 64], v_bf[c0:c1, hd, :],
                       ["b1", "v_bf"], [kK])
                TT("dve", tmpS[:, 0:2, :], v3(pK[:, 0:128], 2), bc_l(ecs[:, :, 2 + ci], 64), ALU.mult, [kK, "ecs"], ["tmpS"])
                TT("dve", S_A[:], tmpS[:, 0:2, :], Sd_A[:], ALU.add, ["tmpS", "Sd_A"], ["S_A"])

            def head_norm(ps_ap, pskey, gcols, ycols):
                ACT(f4[0:n, 0:256], ps_ap, AF.Square, [pskey], ["f4"])
                RED(st4[0:n, 4:8], v3(f4[0:n, 0:256], 4), ["f4"], ["st4"])
                rsqrt_act(st4[0:n, 8:12], st4[0:n, 4:8], 1.0 / 64, ["st4"], ["st4"])
                TT("dve", v3(f4[0:n, 0:256], 4), v3(ps_ap, 4), bc_l(st4[0:n, 8:12], 64), ALU.mult, [pskey, "st4"], ["f4"])
                TT("dve", y_bf[0:n, ycols], f4[0:n, 0:256], gate[0:n, gcols], ALU.mult, ["f4", "gate"], ["y_bf"])

            head_norm(pO[0:n, 0:256], kO, slice(0, 256), slice(0, 256))
            if mid_hook is not None:
                mid_hook()

            pD0, kD0 = proj_tok(3084, 3596)
            pD1, kD1 = proj_tok(3596, 4108)
            cosb = rt[0:n, 0:32].unsqueeze(1).to_broadcast([n, 16, 32])
            sinb = rt[0:n, 32:64].unsqueeze(1).to_broadcast([n, 16, 32])
            qk4 = pD0[0:n, :].rearrange("p (a b) -> p a b", a=16)
            TT("dve", f1[0:n, 0:512].rearrange("p (a b) -> p a b", a=16), qk4, cosb, ALU.mult, [kD0, rtk], ["f1"])
            TT("dve", f2[0:n, 0:512].rearrange("p (a b) -> p a b", a=16), qk4, sinb, ALU.mult, [kD0, rtk], ["f2"])
            c4 = f1[0:n, 0:512].rearrange("p (a s b) -> p a s b", a=8, s=2)
            s4 = f2[0:n, 0:512].rearrange("p (a s b) -> p a s b", a=8, s=2)
            qkr = b1[0:n, :, :].rearrange("p a (s b) -> p a s b", s=4)
            qkr8 = b1[0:n, :, :].rearrange("p a b -> p (a b)").rearrange("p (a s b) -> p a s b", a=8, s=2)
            TT("dve", qkr8[:, :, 0, :], c4[:, :, 0, :], s4[:, :, 1, :], ALU.subtract, ["f1", "f2"], ["b1"])
            TT("dve", qkr8[:, :, 1, :], c4[:, :, 1, :], s4[:, :, 0, :], ALU.add, ["f1", "f2"], ["b1"])
            ACT(v_bf[0:n, :, :].rearrange("p a b -> p (a b)"), pD1[0:n, 0:256], AF.Copy, [kD1], ["v_bf"])
            ACT(f1[0:n, 512:768], pD1[0:n, 256:512], AF.Exp, [kD1], ["f1"], scale=-1.0)
            sigmoid_from_exp(f1[0:n, 512:768], "f1")
            TT("dve", gate[0:n, 768:1024], pD1[0:n, 256:512], f1[0:n, 512:768], ALU.mult, [kD1, "f1"], ["gate"])
            pT, kT = bank()
            for blk in range(4):
                MM(pT[:, blk * 128:blk * 128 + n], b1[0:n, blk, :], ident_bf[0:n, 0:n], ["b1", "ident_bf"], [kT])
            CP("act", qkT[:, :, 0:n], v3(pT[:, :], 4)[:, :, 0:n], [kT], ["qkT"])
            egq = C("egq").rearrange("p (a b) -> p a b", a=2)
            TT("dve", qpT[:, 0:2, 0:n], qkT[:, 0:2, 0:n], egq[:, :, 0:n], ALU.mult, ["qkT", "cst"], ["qpT"])
            egrev = C("egrev16" if t == 0 else "egrev64", slice(0, n))
            TT("dve", kp_bf[0:n, :, 0:64], b1[0:n, 2:4, :].rearrange("p a (s b) -> p (a s) b", s=2), bc_l(egrev, 64), ALU.mult,
               ["b1", "cst"], ["kp_bf"])
            pS, kS = bank()
            for hd in (0, 2, 1, 3):
                hp, hh = hd // 2, hd % 2
                rows = slice(hh * 64, hh * 64 + 64)
                MM(pS[0:n, hd * 128:hd * 128 + n], qkT[rows, 2 + hp, 0:n], qkT[rows, hp, 0:n], ["qkT"], [kS])
            TT("dve", AT[0:n, :, 0:n], v3(pS[0:n, :], 4)[:, :, 0:n], dtret_bf[0:n, :, 0:n], ALU.mult, [kS, "dtret_bf"], ["AT"])
            for hd in (0, 2, 1, 3):
                MM(pOD[0:n, 256 + hd * 64:256 + hd * 64 + 64], AT[0:n, hd, 0:n], v_bf[0:n, hd, :], ["AT", "v_bf"], [kOD],
                   start=(hd == 0), stop=False)
            egl = C("egl").rearrange("p (a b) -> p a b", a=2)
            for ci, (c0, c1) in enumerate(chunks):
                pK, kK = bank()
                for hd in (0, 2, 1, 3):
                    hp, hh = hd // 2, hd % 2
                    rows = slice(hh * 64, hh * 64 + 64)
                    MM(pOD[c0:c1, 256 + hd * 64:256 + hd * 64 + 64], qpT[rows, hp, c0:c1], Sb_D[rows, hp, :], ["qpT", "Sb_D"], [kOD],
                       start=False, stop=True)
                    MM(pK[rows, hp * 64:hp * 64 + 64], kp_bf[c0:c1, hd, 0:64], v_bf[c0:c1, hd, :], ["kp_bf", "v_bf"], [kK])
                TT("dve", tmpS[:, 0:2, :], S_D[:], bc_l(egl[:, :, (0 if t == 0 else 1)], 64), ALU.mult, ["S_D", "cst"], ["tmpS"])
                TT("dve", S_D[:], tmpS[:, 0:2, :], v3(pK[:, 0:128], 2), ALU.add, ["tmpS", kK], ["S_D"])
                CP("act", Sb_D[:], S_D[:], ["S_D"], ["Sb_D"])
            oD = pOD[0:n, 256:512]
            kO_ = kOD
            RED(st4[0:n, 4:8], v3(oD, 4), [kO_], ["st4"])
            TS("dve", st4[0:n, 4:8], st4[0:n, 4:8], -1.0 / 64, ALU.mult, ["st4"], ["st4"])
            TT("dve", v3(f3[0:n, 0:256], 4), v3(oD, 4), bc_l(st4[0:n, 4:8], 64), ALU.add, [kO_, "st4"], ["f3"])
            ACT(f4[0:n, 0:256], f3[0:n, 0:256], AF.Square, ["f3"], ["f4"])
            RED(st4[0:n, 4:8], v3(f4[0:n, 0:256], 4), ["f4"], ["st4"])
            rsqrt_act(st4[0:n, 8:12], st4[0:n, 4:8], 1.0 / 64, ["st4"], ["st4"])
            TT("dve", v3(f3[0:n, 0:256], 4), v3(f3[0:n, 0:256], 4), bc_l(st4[0:n, 8:12], 64), ALU.mult, ["f3", "st4"], ["f3"])
            TT("dve", f3[0:n, 0:256], f3[0:n, 0:256], prm[0:n, 768:1024], ALU.mult, ["f3", "prm"], ["f3"])
            TT("dve", f3[0:n, 0:256], f3[0:n, 0:256], prm[0:n, 1024:1280], ALU.add, ["f3", "prm"], ["f3"])
            TT("dve", y_bf[0:n, 768:1024], f3[0:n, 0:256], gate[0:n, 768:1024], ALU.mult, ["f3", "gate"], ["y_bf"])

            pBz, kBz = proj_tok(1792, 2056, (256, 1288))
            pCz, kCz = proj_tok(2824, 3084, (256, 1292))
            ACT(f1[0:n, 512:768], pBz[0:n, 0:256], AF.Exp, [kBz], ["f1"], scale=-1.0)
            ACT(f2[0:n, 0:256], pCz[0:n, 0:256], AF.Exp, [kCz], ["f2"], scale=-1.0)
            ACT(beta[0:n, 0:4], pBz[0:n, 260:264], AF.Exp, [kBz], ["beta"], scale=-1.0)
            ACT(g8[0:n, 0:4], pBz[0:n, 256:260], AF.Exp, [kBz], ["g8"])
            ACT(g8[0:n, 4:8], pCz[0:n, 256:260], AF.Exp, [kCz], ["g8"])
            sigmoid_from_exp(f1[0:n, 512:768], "f1")
            sigmoid_from_exp(f2[0:n, 0:256], "f2")
            TT("dve", f1[0:n, 512:768], f1[0:n, 512:768], prm[0:n, 256:512], ALU.mult, ["f1", "prm"], ["f1"])
            TT("dve", gate[0:n, 256:512], pBz[0:n, 0:256], f1[0:n, 512:768], ALU.mult, [kBz, "f1"], ["gate"])
            TT("dve", gate[0:n, 512:768], pCz[0:n, 0:256], f2[0:n, 0:256], ALU.mult, [kCz, "f2"], ["gate"])
            for cv, cols0 in ((0, 1024), (1, 2056)):
                for part, (b0, nb_) in enumerate(((0, 4), (4, 2))):
                    pf, kf = proj_feat(cols0 + b0 * 128, nb_)
                    src = v3(pf[:, 0:nb_ * 128], nb_)[:, :, 0:n]
                    CP("act", cur[:, cv * 6 + b0:cv * 6 + b0 + nb_, 3:3 + n], src, [kf], [curk])
                    if last_tile:
                        CP("dve", cvst[:, cv * 6 + b0:cv * 6 + b0 + nb_, :], src[:, :, n - 3:n], [kf], ["cvst"])
            if not last_tile:
                CP("pool", nxt[:, :, 0:3], cur[:, :, n:n + 3], [curk], [nxtk])
            for cv in range(2):
                for part, (b0, nb_) in enumerate(((0, 4), (4, 2))):
                    pc, kc_ = bank()
                    for b_ in range(nb_):
                        blk = cv * 6 + b0 + b_
                        for w in range(4):
                            MM(pc[:, b_ * 128:b_ * 128 + n], dg[:, blk, w, :], cur[:, blk, w:w + n], ["dg", curk], [kc_],
                               start=(w == 0), stop=(w == 3 and cv == 0))
                        if cv == 1:
                            MM(pc[:, b_ * 128:b_ * 128 + n], cbias_bf[0:1, (b0 + b_) * 128:(b0 + b_ + 1) * 128], ones_bf[0:1, 0:n],
                               ["cbias_bf", "ones_bf"], [kc_], start=False, stop=True)
                    src = v3(pc[:, 0:nb_ * 128], nb_)[:, :, 0:n]
                    dstf = v3(f1[:, 0:nb_ * 128], nb_)[:, :, 0:n]
                    ACT(dstf, src, AF.Exp, [kc_], ["f1"], scale=-1.0)
                    sigmoid_from_exp(dstf, "f1")
                    if cv == 0 and part == 0:
                        TT("dve", xsf[:, :, 0:n], src, dstf, ALU.mult, [kc_, "f1"], ["xsf"])
                    else:
                        TT("dve", xs[:, cv * 6 + b0:cv * 6 + b0 + nb_, 0:n], src, dstf, ALU.mult, [kc_, "f1"], ["xs"])
            ACT(b1[:, :, 0:n], xsf[:, :, 0:n], AF.Square, ["xsf"], ["b1"])
            pN, kN = bank()
            for blk in range(4):
                MM(pN[:, blk * 128:blk * 128 + n], bones_bf[:], b1[:, blk, 0:n], ["bones_bf", "b1"], [kN])
            srcN = v3(pN[:, :], 4)[:, :, 0:n]
            dstN = v3(f2[:, 0:512], 4)[:, :, 0:n]
            ACT(dstN, srcN, AF.Ln, [kN], ["f2"], bias=eps_t[:, 0:1])
            ACT(dstN, dstN, AF.Exp, ["f2"], ["f2"], scale=-0.5)
            STT(xs[:, 0:2, 0:n], xsf[:, 0:2, 0:n], QK, dstN[:, 0:2, :], ALU.mult, ALU.mult, ["xsf", "f2"], ["xs"])
            TT("dve", xs[:, 2:4, 0:n], xsf[:, 2:4, 0:n], dstN[:, 2:4, :], ALU.mult, ["xsf", "f2"], ["xs"])

            ACT(g8[0:n, 0:8], g8[0:n, 0:8], AF.Ln, ["g8"], ["g8"], bias=eps_t[0:n, 1:2])
            CP("dve", dtb[0:n, :], g8[0:n, :], ["g8"], ["dtb"])
            TT("dve", g8[0:n, :], g8[0:n, :], nega[0:n, :], ALU.mult, ["g8", "nega"], ["g8"])
            sigmoid_from_exp(beta[0:n, :], "beta")
            pDc, kDc = bank()
            MM(pDc[0:n, 0:32], C("maskT", slice(0, n), 0, n), g8[0:n, 0:32], ["cst", "g8"], [kDc])
            MM(pDc[0:n, 32:64], C("urev", slice(0, n), 0, n), g8[0:n, 0:32], ["cst", "g8"], [kDc])
            CP("dve", gc[0:n, :], pDc[0:n, 0:64], [kDc], ["gc"])
            ACT(egc[0:n, :], gc[0:n, :], AF.Exp, ["gc"], ["egc"])
            for half in range(2):
                pB_, kB_ = bank()
                CP("dve", ghi[0:n, :, :], bc_l(g8[0:n, half * 4:half * 4 + 4], 128), ["g8"], ["ghi"])
                TT("dve", glo[0:n, :, :], bc_l(g8[0:n, half * 4:half * 4 + 4], 128), ghi[0:n, :, :], ALU.subtract,
                   ["g8", "ghi"], ["glo"])
                for hd in range(4):
                    MM(pB_[:, hd * 128:hd * 128 + n], ghi[0:n, hd, :], maskT_bf[0:n, 0:n], ["maskT_bf", "ghi"], [kB_],
                       start=True, stop=False)
                    MM(pB_[:, hd * 128:hd * 128 + n], glo[0:n, hd, :], maskT_bf[0:n, 0:n], ["maskT_bf", "glo"], [kB_],
                       start=False, stop=True)
                srcB = v3(pB_[:, :], 4)[:, :, 0:n]
                ACT(eGbc[:, half * 4:half * 4 + 4, 0:n], srcB, AF.Exp, [kB_], ["eGbc"])
                TT("dve", tt[0:n, half * 4:half * 4 + 4, 0:n], srcB[0:n], bc_l(gc[0:n, half * 4:half * 4 + 4], n), ALU.subtract,
                   [kB_, "gc"], ["tt"])
            if True:
                TT("dve", v3(f1[0:n, 0:512], 4)[:, :, 0:n], tt[0:n, 0:4, 0:n], bc_h(C("negS", slice(0, n), 0, n)), ALU.subtract,
                   ["tt", "cst"], ["f1"])
                ACT(Dst[0:n, :, 0:n], v3(f1[0:n, 0:512], 4)[:, :, 0:n], AF.Exp, ["f1"], ["Dst"], scale=-1.0)
                TT("dve", tt[0:n, :, 0:n], tt[0:n, :, 0:n], bc_h(C("negT", slice(0, n), 0, n), 8), ALU.add, ["tt", "cst"], ["tt"])
                ACT(DT[0:n, :, 0:n], tt[0:n, :, 0:n], AF.Exp, ["tt"], ["DT"])

            pS, kS = bank()
            pKK, kKK = bank()
            for hd in (0, 2, 1, 3):
                hp, hh = hd // 2, hd % 2
                rows = slice(hh * 64, hh * 64 + 64)
                MM(pS[0:n, hd * 128:hd * 128 + n], xs[rows, 2 + hp, 0:n], xs[rows, hp, 0:n], ["xs"], [kS])
                MM(pKK[0:n, hd * 128:hd * 128 + n], xs[rows, 2 + hp, 0:n], xs[rows, 2 + hp, 0:n], ["xs"], [kKK])
            TT("dve", AT[0:n, :, 0:n], v3(pS[0:n, :], 4)[:, :, 0:n], DT[0:n, 0:4, 0:n], ALU.mult, [kS, "DT"], ["AT"])
            TS("dve", nb[0:n, :], beta[0:n, :], -1.0, ALU.mult, ["beta"], ["nb"])
            TT("dve", v3(f1[0:n, 0:512], 4)[:, :, 0:n], v3(pKK[0:n, :], 4)[:, :, 0:n], Dst[0:n, :, 0:n], ALU.mult, [kKK, "Dst"], ["f1"])
            TT("dve", X_bf[0][0:n, :, 0:n], v3(f1[0:n, 0:512], 4)[:, :, 0:n], bc_l(nb[0:n, 0:4], n), ALU.mult,
               ["f1", "nb"], ["X_bf0"])
            def t_chain():
                pY, kY = bank()
                for hd in (0, 2, 1, 3):
                    MM(pY[0:n, hd * 128:hd * 128 + n], X_bf[0][0:n, hd, 0:n], ident_bf[0:n, 0:n], ["X_bf0", "ident_bf"], [kY])
                CP("act", Y_bf[0][0:n, :, 0:n], v3(pY[0:n, :], 4)[:, :, 0:n], [kY], ["Y_bf0"])
                TT("dve", P_bf[0:n, :, 0:n], v3(pY[0:n, :], 4)[:, :, 0:n], bc_h(C("ident", slice(0, n), 0, n)), ALU.add,
                   [kY, "cst"], ["P_bf"])
                yield
                nlev = int(math.ceil(math.log2(clen))) - 1
                ci_ = 0
                for lev in range(nlev):
                    ni_ = 1 - ci_
                    pX2, kX2 = bank()
                    for hd in (0, 2, 1, 3):
                        MM(pX2[0:n, hd * 128:hd * 128 + n], Y_bf[ci_][0:n, hd, 0:n], X_bf[ci_][0:n, hd, 0:n],
                           [f"Y_bf{ci_}", f"X_bf{ci_}"], [kX2])
                    CP("act", X_bf[ni_][0:n, :, 0:n], v3(pX2[0:n, :], 4)[:, :, 0:n], [kX2], [f"X_bf{ni_}"])
                    if lev < nlev - 1:
                        pY2, kY2 = bank()
                        for hd in (0, 2, 1, 3):
                            MM(pY2[0:n, hd * 128:hd * 128 + n], X_bf[ci_][0:n, hd, 0:n], Y_bf[ci_][0:n, hd, 0:n],
                               [f"Y_bf{ci_}", f"X_bf{ci_}"], [kY2])
                        CP("dve", Y_bf[ni_][0:n, :, 0:n], v3(pY2[0:n, :], 4)[:, :, 0:n], [kY2], [f"Y_bf{ni_}"])
                    pP, kP = bank()
                    for hd in (0, 2, 1, 3):
                        MM(pP[0:n, hd * 128:hd * 128 + n], X_bf[ni_][0:n, hd, 0:n], P_bf[0:n, hd, 0:n], [f"X_bf{ni_}", "P_bf"], [kP])
                    TT("dve", P_bf[0:n, :, 0:n], P_bf[0:n, :, 0:n], v3(pP[0:n, :], 4)[:, :, 0:n], ALU.add, ["P_bf", kP], ["P_bf"])
                    ci_ = ni_
                    yield

            tgen = t_chain()

            def tstep():
                next(tgen, None)

            tstep()

            pS, kS = bank()
            for g in range(2):
                MM(pS[0:n, g * 128:g * 128 + n], xs[:, 8 + g, 0:n], xs[:, 10 + g, 0:n], ["xs"], [kS])
            for g in range(2):
                TT("dve", AT2[0:n, 2 * g:2 * g + 2, 0:n], pS[0:n, g * 128:g * 128 + n].unsqueeze(1).to_broadcast([n, 2, n]),
                   DT[0:n, 4 + 2 * g:4 + 2 * g + 2, 0:n], ALU.mult, [kS, "DT"], ["AT2"])
            tstep()
            pT, kT = bank()
            for blk in range(4):
                MM(pT[0:n, blk * 128:(blk + 1) * 128], xs[:, 6 + blk, 0:n], ident_bf[:, :], ["xs", "ident_bf"], [kT])
            TT("dve", v_bf[0:n, :, :], v3(pT[0:n, 0:256], 4), bc_l(dtb[0:n, 4:8], 64), ALU.mult, [kT, "dtb"], ["v_bf"])
            TT("dve", xd_bf[0:n, :, :], v3(pT[0:n, 0:256], 4), bc_l(prm[0:n, 1296:1300], 64), ALU.mult, [kT, "prm"], ["xd_bf"])
            for g in range(2):
                TT("dve", kp_bf[0:n, 2 * g:2 * g + 2, :], pT[0:n, 256 + g * 128:256 + (g + 1) * 128].unsqueeze(1).to_broadcast([n, 2, 128]),
                   bc_l(egc[0:n, 36 + 2 * g:36 + 2 * g + 2], 128), ALU.mult, [kT, "egc"], ["kp_bf"])
                TT("dve", qpT[:, 2 * g:2 * g + 2, 0:n], xs[:, 10 + g, 0:n].unsqueeze(1).to_broadcast([128, 2, n]),
                   eGbc[:, 4 + 2 * g:4 + 2 * g + 2, 0:n], ALU.mult, ["xs", "eGbc"], ["qpT"])
            for hd in (0, 2, 1, 3):
                MM(pOC[0:n, 256 + hd * 64:256 + hd * 64 + 64], AT2[0:n, hd, 0:n], v_bf[0:n, hd, :], ["AT2", "v_bf"], [kOC],
                   start=(hd == 0), stop=False)
            MM(pOC[0:n, 256:512], ident_bf[0:n, 0:n], xd_bf[0:n, :, :].rearrange("p a b -> p (a b)"), ["ident_bf", "xd_bf"], [kOC],
               start=False, stop=False)
            for ci, (c0, c1) in enumerate(chunks):
                tstep()
                pK, kK = bank()
                for hd in (0, 2, 1, 3):
                    MM(pOC[c0:c1, 256 + hd * 64:256 + hd * 64 + 64], qpT[:, hd, c0:c1], Sb_C[:, hd, :], ["qpT", "Sb_C"], [kOC],
                       start=False, stop=True)
                    MM(pK[:, hd * 64:hd * 64 + 64], kp_bf[c0:c1, hd, :], v_bf[c0:c1, hd, :], ["kp_bf", "v_bf"], [kK])
                TT("dve", tmpS[:, :, :], S_C[:], eGbc[:, 4:8, c1 - 1:c1].to_broadcast([128, 4, 64]), ALU.mult, ["S_C", "eGbc"], ["tmpS"])
                TT("dve", S_C[:], tmpS[:, :, :], v3(pK[:, 0:256], 4), ALU.add, ["tmpS", kK], ["S_C"])
                CP("act", Sb_C[:], S_C[:], ["S_C"], ["Sb_C"])
            tstep()
            TT("dve", f3[0:n, 0:256], pOC[0:n, 256:512], gate[0:n, 512:768], ALU.mult, [kOC, "gate"], ["f3"])
            ACT(f4[0:n, 0:256], f3[0:n, 0:256], AF.Square, ["f3"], ["f4"])
            RED(st4[0:n, 4:6], v3(f4[0:n, 0:256], 2), ["f4"], ["st4"])
            rsqrt_act(st4[0:n, 8:10], st4[0:n, 4:6], 1.0 / 128, ["st4"], ["st4"])
            TT("dve", v3(f3[0:n, 0:256], 2), v3(f3[0:n, 0:256], 2), bc_l(st4[0:n, 8:10], 128), ALU.mult, ["f3", "st4"], ["f3"])
            TT("dve", y_bf[0:n, 512:768], f3[0:n, 0:256], prm[0:n, 512:768], ALU.mult, ["f3", "prm"], ["y_bf"])

            for _ in tgen:
                pass

            pT, kT = bank()
            for blk in range(4):
                MM(pT[0:n, blk * 128:(blk + 1) * 128], xs[:, 2 + blk, 0:n], ident_bf[:, :], ["xs", "ident_bf"], [kT])
            TT("dve", kp_bf[0:n, :, 0:64], v3(pT[0:n, 0:256], 4), bc_l(egc[0:n, 32:36], 64), ALU.mult, [kT, "egc"], ["kp_bf"])
            TT("dve", bv[0:n, :, :], v3(pT[0:n, 256:512], 4), bc_l(beta[0:n, 0:4], 64), ALU.mult, [kT, "beta"], ["bv"])
            TT("dve", nb2[0:n, :], nb[0:n, :], egc[0:n, :], ALU.mult, ["nb", "egc"], ["nb2"])
            for hh in range(2):
                rows = slice(hh * 64, hh * 64 + 64)
                TT("dve", qpT[rows, 0:2, 0:n], xs[rows, 0:2, 0:n], eGbc[rows, hh:4:2, 0:n], ALU.mult, ["xs", "eGbc"], ["qpT"])
            for ci, (c0, c1) in enumerate(chunks):
                pW, kW = bank()
                for hd in (0, 2, 1, 3):
                    hp, hh = hd // 2, hd % 2
                    rows = slice(hh * 64, hh * 64 + 64)
                    MM(pW[c0:c1, hd * 64:hd * 64 + 64], xs[rows, 2 + hp, c0:c1], Sb_B[rows, hp, :], ["xs", "Sb_B"], [kW])
                TT("dve", v3(f4[c0:c1, 0:256], 4), v3(pW[c0:c1, 0:256], 4), bc_l(nb2[c0:c1, 0:4], 64), ALU.mult,
                   [kW, "nb2"], ["f4"])
                TT("dve", r_bf[c0:c1, :, :], v3(f4[c0:c1, 0:256], 4), bv[c0:c1, :, :], ALU.add, ["f4", "bv"], ["r_bf"])
                pU, kU = bank()
                for hd in (0, 2, 1, 3):
                    MM(pU[c0:c1, hd * 64:hd * 64 + 64], P_bf[c0:c1, hd, c0:c1], r_bf[c0:c1, hd, :], ["P_bf", "r_bf"], [kU])
                CP("act", u_bf[c0:c1, :, :], v3(pU[c0:c1, 0:256], 4), [kU], ["u_bf"])
                pK, kK = bank()
                for hd in (0, 2, 1, 3):
                    hp, hh = hd // 2, hd % 2
                    rows = slice(hh * 64, hh * 64 + 64)
                    MM(pO2[c0:c1, hd * 64:hd * 64 + 64], AT[c0:c1, hd, c0:c1], u_bf[c0:c1, hd, :], ["AT", "u_bf"], [kO2],
                       start=(hd == 0), stop=False)
                    MM(pO2[c0:c1, hd * 64:hd * 64 + 64], qpT[rows, hp, c0:c1], Sb_B[rows, hp, :], ["qpT", "Sb_B"], [kO2],
                       start=False, stop=True)
                    MM(pK[rows, hp * 64:hp * 64 + 64], kp_bf[c0:c1, hd, 0:64], u_bf[c0:c1, hd, :], ["kp_bf", "u_bf"], [kK])
                for hh in range(2):
                    rows = slice(hh * 64, hh * 64 + 64)
                    TT("dve", tmpS[rows, 0:2, :], S_B[rows, :, :], eGbc[rows, hh:4:2, c1 - 1:c1].to_broadcast([64, 2, 64]), ALU.mult,
                       ["S_B", "eGbc"], ["tmpS"])
                TT("dve", S_B[:], tmpS[:, 0:2, :], v3(pK[:, 0:128], 2), ALU.add, ["tmpS", kK], ["S_B"])
                CP("act", Sb_B[:], S_B[:], ["S_B"], ["Sb_B"])
            head_norm(pO2[0:n, 0:256], kO2, slice(256, 512), slice(256, 512))

            if late_hook is not None:
                late_hook()
            for half in range(2):
                pt, pk = bank()
                for kk in range(4):
                    kc = half * 4 + kk
                    MM(pt[:, kk * 128:kk * 128 + n], y_bf[0:n, kc * 128:(kc + 1) * 128], ident_bf[0:n, 0:n],
                       ["y_bf", "ident_bf"], [pk])
                CP("act" if half else "dve", yT[:, half * 4:half * 4 + 4, 0:n], v3(pt[:, :], 4)[:, :, 0:n], [pk], ["yT"])
            for cg in range(2):
                pt, pk = bank()
                for kc in range(8):
                    MM(pt[0:n, :], yT[:, kc, 0:n], wout[:, kc, cg * 512:(cg + 1) * 512], ["yT", "wout"], [pk],
                       start=(kc == 0), stop=(kc == 7))
                TT("dve", ht[:, cg * 512:(cg + 1) * 512], ht[:, cg * 512:(cg + 1) * 512], pt[0:n, :], ALU.add, [hk, pk], [hk])

            if last_tile:
                for nm, S_, dst in (("A", S_A, st_hg), ("B", S_B, st_gd), ("D", S_D, st_rt)):
                    for hh in range(2):
                        DMA(dst[l, hh:4:2, :, :].rearrange("a k v -> k a v"), S_[hh * 64:(hh + 1) * 64, :, :], ["S_" + nm], [])
                DMA(st_sd[l].rearrange("a k v -> k a v"), S_C[:, :, :], ["S_C"], [])
                for blk in range(6):
                    DMA(st_gc[l][:, blk * 128:(blk + 1) * 128].rearrange("w p -> p w"), cvst[:, blk, :], ["cvst"], [], slow=True)
                    DMA(st_sc[l][:, blk * 128:(blk + 1) * 128].rearrange("w p -> p w"), cvst[:, 6 + blk, :], ["cvst"], [], slow=True)
            if l < depth - 1:
                DMA(hscr[t * 128:t * 128 + n, :], ht, [hk], [f"hd{t}"])
            if l == depth - 1 and t > 0:
                ACT(hn_bf[0:n, :], ht, AF.Square, [hk], ["hn_bf", "st4"], accum=st4[0:n, 0:1])
                rsqrt_act(st4[0:n, 1:2], st4[0:n, 0:1], 1.0 / D, ["st4"], ["st4"])
                STT(ht, ht, st4[0:n, 1:2], finw[0:n, :], ALU.mult, ALU.mult, [hk, "st4", "lgt"], [hk])
                DMA(y_p[(t - 1) * 128:t * 128, :], ht, [hk], [])


        hs = sb("hs", [NS, D])
        DMA(hs[:, :], xs_d, [], ["hs"])

        def sample_fwd(l, last):
            n = NS
            DMA(rot[0][:], rot_d[NT], [], ["rot0"])
            rt = rot[0]; rtk = "rot0"

            def v3(ps_ap, a):
                return ps_ap.rearrange("p (a b) -> p a b", a=a)

            def bc_l(ap2d, m):
                return ap2d.unsqueeze(2).to_broadcast([ap2d.shape[0], ap2d.shape[1], m])

            ACT(hn_bf[0:n, :], hs[:, :], AF.Square, ["hs"], ["hn_bf", "st4"], accum=st4[0:n, 0:1])
            rsqrt_act(st4[0:n, 1:2], st4[0:n, 0:1], 1.0 / D, ["st4"], ["st4"])
            ACT(hn_bf[0:n, :], hs[:, :], AF.Copy, ["hs", "st4"], ["hn_bf"], scale=st4[0:n, 1:2])
            for half in range(2):
                pt, pk = bank()
                for kk in range(4):
                    kc = half * 4 + kk
                    MM(pt[:, kk * 128:kk * 128 + n], hn_bf[0:n, kc * 128:(kc + 1) * 128], ident_bf[0:n, 0:n],
                       ["hn_bf", "ident_bf"], [pk])
                CP("act", hnT[:, half * 4:half * 4 + 4, 0:n], v3(pt[:, :], 4)[:, :, 0:n], [pk], ["hnT1"])

            def proj_tok(c0, c1, extra=None):
                pt, pk = bank()
                for kc in range(8):
                    MM(pt[0:n, 0:c1 - c0], hnT[:, kc, 0:n], win[:, kc, c0:c1], ["hnT1", "win"], [pk],
                       start=(kc == 0), stop=(kc == 7 and extra is None))
                if extra is not None:
                    MM(pt[0:n, extra[0]:extra[0] + 4], C("ident", slice(0, n), 0, n), prm[0:n, extra[1]:extra[1] + 4],
                       ["cst", "prm"], [pk], start=False, stop=True)
                return pt, pk

            sel = C("sel").rearrange("p (a b) -> p a b", a=4)
            selT = C("selT").rearrange("p (a b) -> p a b", a=4)
            pvs = f4
            Sbuf = [tt, eGbc]; Skey = ["tt", "eGbc"]
            Tbuf = gate; Tkey = "gate"
            slot = [0]

            def select(fields):
                pv, pvk = bank()
                first = True
                for (c0, wd, fn) in fields:
                    for hd in range(4):
                        ap, key = fn(hd)
                        P.op("pe", (lambda o_, l_, r_, st_: (lambda e: e.matmul(o_, lhsT=l_, rhs=r_, start=st_, stop=False,
                                                                                 skip_group_check=True)))(
                            pv[0:64, c0:c0 + wd], sel[0:n, hd, :], ap, first), reads=["cst", key], writes=[pvk])
                        first = False
                wtot = max(c0 + wd for (c0, wd, _) in fields)
                CP("dve", pvs[0:64, 0:wtot], pv[0:64, 0:wtot], [pvk], ["f4"])

            def unselect(o_ap, okey):
                po, pok = bank()
                for hd in range(4):
                    MM(po[0:n, hd * 64:hd * 64 + 64], selT[0:64, hd, :], o_ap, ["cst", okey], [pok])
                return po, pok

            def state_io(st_in, st_out, K):
                ks = 16
                for k0 in range(0, K, ks):
                    yield k0, ks

            def load_slice(st_in, k0, ks):
                i = slot[0] % 2
                slot[0] += 1
                Sv = Sbuf[i][0:64, :, :].rearrange("p a b -> p (a b)")[:, 0:ks * 64].rearrange("p (k v) -> p k v", k=ks)
                for hd in range(4):
                    DMA(Sv[hd * 16:(hd + 1) * 16, :, :], st_in[l, :, hd, k0:k0 + ks, :], [], [Skey[i]])
                return Sv, Skey[i]

            def store_slice(st_out, Sv, sk, k0, ks):
                for hd in range(4):
                    DMA(st_out[l, :, hd, k0:k0 + ks, :], Sv[hd * 16:(hd + 1) * 16, :, :], [sk], [])

            o_sb = tmpS[0:64, 0, :]; w_sb = tmpS[0:64, 1, :]; op_sb = tmpS[0:64, 2, :]; u_sb = tmpS[0:64, 3, :]

            def Tview(ks):
                return Tbuf[0:64, 0:ks * 64].rearrange("p (k v) -> p k v", k=ks)

            def q_reduce(Sv, sk, q_ap, k0, ks, first, acc):
                T = Tview(ks)
                TT("dve", T, Sv, bc_l(q_ap[:, k0:k0 + ks], 64), ALU.mult, [sk, "f4"], [Tkey])
                RED(op_sb, T.rearrange("p k v -> p v k"), [Tkey], ["tmpS"])
                if first:
                    CP("dve", acc, op_sb, ["tmpS"], ["tmpS"])
                else:
                    TT("dve", acc, acc, op_sb, ALU.add, ["tmpS"], ["tmpS"])

            def step_plain(st_in, st_out, K, q_ap, k_ap, v_ap, vec_f=None, sc=None):
                for k0, ks in state_io(st_in, st_out, K):
                    Sv, sk = load_slice(st_in, k0, ks)
                    T = Tview(ks)
                    TT("dve", T, bc_l(k_ap[:, k0:k0 + ks], 64), v_ap.unsqueeze(1).to_broadcast([64, ks, 64]), ALU.mult,
                       ["f4", "tmpS"], [Tkey])
                    if vec_f is not None:
                        TT("dve", Sv, Sv, bc_l(vec_f[:, k0:k0 + ks], 64), ALU.mult, [sk, "f4"], [sk])
                        TT("dve", Sv, Sv, T, ALU.add, [sk, Tkey], [sk])
                    else:
                        STT(Sv, Sv, sc, T, ALU.mult, ALU.add, [sk, Tkey, "f4", "cst"], [sk])
                    store_slice(st_out, Sv, sk, k0, ks)
                    q_reduce(Sv, sk, q_ap, k0, ks, k0 == 0, o_sb)

            def head_norm(ps_ap, pskey, gcols, ycols):
                ACT(f3[0:n, 256:512], ps_ap, AF.Square, [pskey], ["f3"])
                RED(st4[0:n, 4:8], v3(f3[0:n, 256:512], 4), ["f3"], ["st4"])
                rsqrt_act(st4[0:n, 8:12], st4[0:n, 4:8], 1.0 / 64, ["st4"], ["st4"])
                TT("dve", v3(f3[0:n, 256:512], 4), v3(ps_ap, 4), bc_l(st4[0:n, 8:12], 64), ALU.mult, [pskey, "st4"], ["f3"])
                TT("dve", y_bf[0:n, ycols], f3[0:n, 256:512], hb[1][0:n, gcols], ALU.mult, ["f3", "hb1"], ["y_bf"])

            gs = hb[1]; gsk = "hb1"

            pA0, kA0 = proj_tok(0, 512)
            pA1, kA1 = proj_tok(512, 1024)
            ACT(f1[0:n, 0:256], pA0[0:n, 0:256], AF.Exp, [kA0], ["f1"], scale=-1.0)
            ACT(f1[0:n, 256:512], pA0[0:n, 256:512], AF.Exp, [kA0], ["f1"])
            ACT(f1[0:n, 512:768], pA1[0:n, 256:512], AF.Exp, [kA1], ["f1"], scale=-1.0)
            sigmoid_from_exp(f1[0:n, :], "f1")
            STT(f2[0:n, 0:256], pA0[0:n, 0:256], QK, f1[0:n, 0:256], ALU.mult, ALU.mult, [kA0, "f1"], ["f2"])
            TT("dve", f2[0:n, 256:512], f1[0:n, 256:512], oml[0:n, l, :], ALU.mult, ["f1", "oml"], ["f2"])
            TT("dve", f1[0:n, 512:768], f1[0:n, 512:768], prm[0:n, 0:256], ALU.mult, ["f1", "prm"], ["f1"])
            TT("dve", gs[0:n, 0:256], pA1[0:n, 256:512], f1[0:n, 512:768], ALU.mult, [kA1, "f1"], [gsk])
            CP("dve", f3[0:n, 0:256], pA1[0:n, 0:256], [kA1], ["f3"])
            TS("dve", f1[0:n, 0:256], f2[0:n, 256:512], -1.0, ALU.mult, ["f2"], ["f1"], s2=1.0, op1=ALU.add)
            select([(0, 64, lambda hd: (f2[0:n, hd * 64:hd * 64 + 64], "f2")),
                    (64, 64, lambda hd: (f2[0:n, 256 + hd * 64:256 + hd * 64 + 64], "f2")),
                    (128, 64, lambda hd: (f3[0:n, hd * 64:hd * 64 + 64], "f3")),
                    (192, 64, lambda hd: (f1[0:n, hd * 64:hd * 64 + 64], "f1"))])
            step_plain(si_hg, so_hg, 64, pvs[0:64, 0:64], pvs[0:64, 64:128], pvs[0:64, 128:192], vec_f=pvs[0:64, 192:256])
            po, pok = unselect(o_sb, "tmpS")
            head_norm(po[0:n, 0:256], pok, slice(0, 256), slice(0, 256))

            pD0, kD0 = proj_tok(3084, 3596)
            pD1, kD1 = proj_tok(3596, 4108)
            cosb = rt[0:n, 0:32].unsqueeze(1).to_broadcast([n, 16, 32])
            sinb = rt[0:n, 32:64].unsqueeze(1).to_broadcast([n, 16, 32])
            qk4 = pD0[0:n, :].rearrange("p (a b) -> p a b", a=16)
            TT("dve", f1[0:n, 0:512].rearrange("p (a b) -> p a b", a=16), qk4, cosb, ALU.mult, [kD0, rtk], ["f1"])
            TT("dve", f2[0:n, 0:512].rearrange("p (a b) -> p a b", a=16), qk4, sinb, ALU.mult, [kD0, rtk], ["f2"])
            c4 = f1[0:n, 0:512].rearrange("p (a s b) -> p a s b", a=8, s=2)
            s4 = f2[0:n, 0:512].rearrange("p (a s b) -> p a s b", a=8, s=2)
            r4 = f3[0:n, 0:512].rearrange("p (a s b) -> p a s b", a=8, s=2)
            TT("dve", r4[:, :, 0, :], c4[:, :, 0, :], s4[:, :, 1, :], ALU.subtract, ["f1", "f2"], ["f3"])
            TT("dve", r4[:, :, 1, :], c4[:, :, 1, :], s4[:, :, 0, :], ALU.add, ["f1", "f2"], ["f3"])
            TS("dve", f3[0:n, 256:512], f3[0:n, 256:512], QK, ALU.mult, ["f3"], ["f3"])
            CP("dve", f1[0:n, 0:256], pD1[0:n, 0:256], [kD1], ["f1"])
            ACT(f1[0:n, 512:768], pD1[0:n, 256:512], AF.Exp, [kD1], ["f1"], scale=-1.0)
            sigmoid_from_exp(f1[0:n, 512:768], "f1")
            TT("dve", gs[0:n, 768:1024], pD1[0:n, 256:512], f1[0:n, 512:768], ALU.mult, [kD1, "f1"], [gsk])
            select([(0, 64, lambda hd: (f3[0:n, hd * 64:hd * 64 + 64], "f3")),
                    (64, 64, lambda hd: (f3[0:n, 256 + hd * 64:256 + hd * 64 + 64], "f3")),
                    (128, 64, lambda hd: (f1[0:n, hd * 64:hd * 64 + 64], "f1"))])
            step_plain(si_rt, so_rt, 64, pvs[0:64, 0:64], pvs[0:64, 64:128], pvs[0:64, 128:192], sc=C("gam64", slice(0, 64), 0, 1))
            po, pok = unselect(o_sb, "tmpS")
            oD = po[0:n, 0:256]
            RED(st4[0:n, 4:8], v3(oD, 4), [pok], ["st4"])
            TS("dve", st4[0:n, 4:8], st4[0:n, 4:8], -1.0 / 64, ALU.mult, ["st4"], ["st4"])
            TT("dve", v3(f3[0:n, 0:256], 4), v3(oD, 4), bc_l(st4[0:n, 4:8], 64), ALU.add, [pok, "st4"], ["f3"])
            ACT(f3[0:n, 256:512], f3[0:n, 0:256], AF.Square, ["f3"], ["f3"])
            RED(st4[0:n, 4:8], v3(f3[0:n, 256:512], 4), ["f3"], ["st4"])
            rsqrt_act(st4[0:n, 8:12], st4[0:n, 4:8], 1.0 / 64, ["st4"], ["st4"])
            TT("dve", v3(f3[0:n, 0:256], 4), v3(f3[0:n, 0:256], 4), bc_l(st4[0:n, 8:12], 64), ALU.mult, ["f3", "st4"], ["f3"])
            TT("dve", f3[0:n, 0:256], f3[0:n, 0:256], prm[0:n, 768:1024], ALU.mult, ["f3", "prm"], ["f3"])
            TT("dve", f3[0:n, 0:256], f3[0:n, 0:256], prm[0:n, 1024:1280], ALU.add, ["f3", "prm"], ["f3"])
            TT("dve", y_bf[0:n, 768:1024], f3[0:n, 0:256], gs[0:n, 768:1024], ALU.mult, ["f3", gsk], ["y_bf"])

            pBz, kBz = proj_tok(1792, 2056, (256, 1288))
            ACT(f1[0:n, 512:768], pBz[0:n, 0:256], AF.Exp, [kBz], ["f1"], scale=-1.0)
            ACT(beta[0:n, 0:4], pBz[0:n, 260:264], AF.Exp, [kBz], ["beta"], scale=-1.0)
            ACT(g8[0:n, 0:4], pBz[0:n, 256:260], AF.Exp, [kBz], ["g8"])
            sigmoid_from_exp(f1[0:n, 512:768], "f1")
            TT("dve", f1[0:n, 512:768], f1[0:n, 512:768], prm[0:n, 256:512], ALU.mult, ["f1", "prm"], ["f1"])
            TT("dve", gs[0:n, 256:512], pBz[0:n, 0:256], f1[0:n, 512:768], ALU.mult, [kBz, "f1"], [gsk])
            pCz, kCz = proj_tok(2824, 3084, (256, 1292))
            ACT(f1[0:n, 512:768], pCz[0:n, 0:256], AF.Exp, [kCz], ["f1"], scale=-1.0)
            ACT(g8[0:n, 4:8], pCz[0:n, 256:260], AF.Exp, [kCz], ["g8"])
            sigmoid_from_exp(f1[0:n, 512:768], "f1")
            TT("dve", gs[0:n, 512:768], pCz[0:n, 0:256], f1[0:n, 512:768], ALU.mult, [kCz, "f1"], [gsk])
            ACT(g8[0:n, 0:8], g8[0:n, 0:8], AF.Ln, ["g8"], ["g8"], bias=eps_t[0:n, 1:2])
            CP("dve", dtb[0:n, :], g8[0:n, :], ["g8"], ["dtb"])
            TT("dve", g8[0:n, :], g8[0:n, :], nega[0:n, :], ALU.mult, ["g8", "nega"], ["g8"])
            ACT(egc[0:n, 0:8], g8[0:n, 0:8], AF.Exp, ["g8"], ["egc"])
            sigmoid_from_exp(beta[0:n, :], "beta")

            def conv_tok(cv, groups, st_in, st_out, bias):
                U = hb[0]; Uk = "hb0"
                for (c0, c1, o0) in groups:
                    pu, puk = proj_tok(c0, c1)
                    CP("dve", U[0:n, o0:o0 + (c1 - c0)], pu[0:n, 0:c1 - c0], [puk], [Uk])
                DMA(st_out[l, :, 2, :], U[0:n, 0:768], [Uk], [])
                Wt = eGbc[0:n, :, :].rearrange("p a b -> p (a b)")[:, 0:768]
                Ct = tt[0:n, :, :].rearrange("p a b -> p (a b)")[:, 0:768]
                DMA(Wt, cwrow_d[l, cv, 3], [], ["eGbc"])
                TT("dve", f1[0:n, 0:768], U[0:n, 0:768], Wt, ALU.mult, [Uk, "eGbc"], ["f1"])
                for w in range(3):
                    DMA(Ct, st_in[l, :, w, :], [], ["tt"])
                    DMA(Wt, cwrow_d[l, cv, w], [], ["eGbc"])
                    if w >= 1:
                        DMA(st_out[l, :, w - 1, :], Ct, ["tt"], [])
                    TT("dve", Wt, Ct, Wt, ALU.mult, ["tt", "eGbc"], ["eGbc"])
                    TT("dve", f1[0:n, 0:768], f1[0:n, 0:768], Wt, ALU.add, ["f1", "eGbc"], ["f1"])
                if bias:
                    DMA(Ct, cbrow_d[l], [], ["tt"])
                    TT("dve", f1[0:n, 0:768], f1[0:n, 0:768], Ct, ALU.add, ["f1", "tt"], ["f1"])
                ACT(Ct, f1[0:n, 0:768], AF.Exp, ["f1"], ["tt"], scale=-1.0)
                sigmoid_from_exp(Ct, "tt")
                TT("dve", f1[0:n, 0:768], f1[0:n, 0:768], Ct, ALU.mult, ["f1", "tt"], ["f1"])

            conv_tok(0, [(1024, 1536, 0), (1536, 1792, 512)], si_gc, so_gc, False)
            Ct = tt[0:n, :, :].rearrange("p a b -> p (a b)")[:, 0:512]
            ACT(Ct, f1[0:n, 0:512], AF.Square, ["f1"], ["tt"])
            RED(nb[0:n, 0:8], v3(Ct, 8), ["tt"], ["nb"])
            ACT(nb[0:n, 8:16], nb[0:n, 0:8], AF.Ln, ["nb"], ["nb"], bias=eps_t[0:n, 0:1])
            ACT(nb[0:n, 8:16], nb[0:n, 8:16], AF.Exp, ["nb"], ["nb"], scale=-0.5)
            TT("dve", v3(f1[0:n, 0:512], 8), v3(f1[0:n, 0:512], 8), bc_l(nb[0:n, 8:16], 64), ALU.mult, ["f1", "nb"], ["f1"])
            TS("dve", f1[0:n, 0:256], f1[0:n, 0:256], QK, ALU.mult, ["f1"], ["f1"])
            select([(0, 64, lambda hd: (f1[0:n, hd * 64:hd * 64 + 64], "f1")),
                    (64, 64, lambda hd: (f1[0:n, 256 + hd * 64:256 + hd * 64 + 64], "f1")),
                    (128, 64, lambda hd: (f1[0:n, 512 + hd * 64:512 + hd * 64 + 64], "f1")),
                    (192, 1, lambda hd: (egc[0:n, hd:hd + 1], "egc")),
                    (193, 1, lambda hd: (beta[0:n, hd:hd + 1], "beta"))])
            qB, kB, vB = pvs[0:64, 0:64], pvs[0:64, 64:128], pvs[0:64, 128:192]
            egB, btB = pvs[0:64, 192:193], pvs[0:64, 193:194]
            for k0, ks in state_io(si_gd, so_gd, 64):
                Sv, sk = load_slice(si_gd, k0, ks)
                T = Tview(ks)
                TT("dve", T, Sv, bc_l(kB[:, k0:k0 + ks], 64), ALU.mult, [sk, "f4"], [Tkey])
                RED(op_sb, T.rearrange("p k v -> p v k"), [Tkey], ["tmpS"])
                if k0 == 0:
                    CP("dve", w_sb, op_sb, ["tmpS"], ["tmpS"])
                else:
                    TT("dve", w_sb, w_sb, op_sb, ALU.add, ["tmpS"], ["tmpS"])
            TS("dve", w_sb, w_sb, egB, ALU.mult, ["tmpS", "f4"], ["tmpS"])
            TT("dve", u_sb, vB, w_sb, ALU.subtract, ["f4", "tmpS"], ["tmpS"])
            TS("dve", u_sb, u_sb, btB, ALU.mult, ["tmpS", "f4"], ["tmpS"])
            step_plain(si_gd, so_gd, 64, qB, kB, u_sb, sc=egB)
            po, pok = unselect(o_sb, "tmpS")
            head_norm(po[0:n, 0:256], pok, slice(256, 512), slice(256, 512))

            conv_tok(1, [(2056, 2568, 0), (2568, 2824, 512)], si_sc, so_sc, True)
            TT("dve", v3(f2[0:n, 0:256], 4), v3(f1[0:n, 0:256], 4), bc_l(dtb[0:n, 4:8], 64), ALU.mult, ["f1", "dtb"], ["f2"])
            select([(0, 128, lambda hd: (f1[0:n, 512 + (hd // 2) * 128:512 + (hd // 2) * 128 + 128], "f1")),
                    (128, 128, lambda hd: (f1[0:n, 256 + (hd // 2) * 128:256 + (hd // 2) * 128 + 128], "f1")),
                    (256, 64, lambda hd: (f2[0:n, hd * 64:hd * 64 + 64], "f2")),
                    (320, 1, lambda hd: (egc[0:n, 4 + hd:5 + hd], "egc"))])
            step_plain(si_sd, so_sd, 128, pvs[0:64, 0:128], pvs[0:64, 128:256], pvs[0:64, 256:320], sc=pvs[0:64, 320:321])
            po, pok = unselect(o_sb, "tmpS")
            TT("dve", v3(f3[0:n, 0:256], 4), v3(f1[0:n, 0:256], 4), bc_l(prm[0:n, 1296:1300], 64), ALU.mult, ["f1", "prm"], ["f3"])
            TT("dve", f3[0:n, 0:256], f3[0:n, 0:256], po[0:n, 0:256], ALU.add, ["f3", pok], ["f3"])
            TT("dve", f3[0:n, 0:256], f3[0:n, 0:256], gs[0:n, 512:768], ALU.mult, ["f3", gsk], ["f3"])
            ACT(f3[0:n, 256:512], f3[0:n, 0:256], AF.Square, ["f3"], ["f3"])
            RED(st4[0:n, 4:6], v3(f3[0:n, 256:512], 2), ["f3"], ["st4"])
            rsqrt_act(st4[0:n, 8:10], st4[0:n, 4:6], 1.0 / 128, ["st4"], ["st4"])
            TT("dve", v3(f3[0:n, 0:256], 2), v3(f3[0:n, 0:256], 2), bc_l(st4[0:n, 8:10], 128), ALU.mult, ["f3", "st4"], ["f3"])
            TT("dve", y_bf[0:n, 512:768], f3[0:n, 0:256], prm[0:n, 512:768], ALU.mult, ["f3", "prm"], ["y_bf"])

            for half in range(2):
                pt, pk = bank()
                for kk in range(4):
                    kc = half * 4 + kk
                    MM(pt[:, kk * 128:kk * 128 + n], y_bf[0:n, kc * 128:(kc + 1) * 128], ident_bf[0:n, 0:n],
                       ["y_bf", "ident_bf"], [pk])
                CP("act", yT[:, half * 4:half * 4 + 4, 0:n], v3(pt[:, :], 4)[:, :, 0:n], [pk], ["yT"])
            for cg in range(2):
                pt, pk = bank()
                for kc in range(8):
                    MM(pt[0:n, :], yT[:, kc, 0:n], wout[:, kc, cg * 512:(cg + 1) * 512], ["yT", "wout"], [pk],
                       start=(kc == 0), stop=(kc == 7))
                TT("dve", hs[:, cg * 512:(cg + 1) * 512], hs[:, cg * 512:(cg + 1) * 512], pt[0:n, :], ALU.add, ["hs", pk], ["hs"])
            if last:
                ACT(hn_bf[0:n, :], hs[:, :], AF.Square, ["hs"], ["hn_bf", "st4"], accum=st4[0:n, 0:1])
                rsqrt_act(st4[0:n, 1:2], st4[0:n, 0:1], 1.0 / D, ["st4"], ["st4"])
                STT(hs[:, :], hs[:, :], st4[0:n, 1:2], finw[0:n, :], ALU.mult, ALU.mult, ["hs", "st4", "lgt"], ["hs"])
                DMA(y_s, hs[:, :], ["hs"], [])

        for l in range(depth):
            load_layer(l)
            if not _os0.environ.get("NO_SAMPLE"):
                sample_fwd(l, l == depth - 1)
            if ntiles > 0:
                stage0(l, 0)
                stage0_pe(l, 0)
            for t in range(ntiles):
                more = t + 1 < ntiles
                tile_fwd(l, t, (lambda l_=l, t_=t: stage0(l_, t_ + 1)) if more else None,
                         (lambda l_=l, t_=t: stage0_pe(l_, t_ + 1)) if more else None)

        if _AUDIT:
            for b_ in sorted(_bad, key=str):
                print("AUDIT missing key:", b_)
        P.emit(es)
    return nc


_NC_CACHE = {}


def _prep_inputs(inp, c):
    f = np.float32
    prm = np.zeros((DEPTH, 128, NPRM), f)
    for l in range(DEPTH):
        row = np.concatenate([inp["hgrn_norm_w"][l], inp["gdn_norm_w"][l], inp["ssd_norm_w"][l], inp["ret_norm_w"][l],
                              inp["ret_norm_b"][l], inp["gdn_a_log"][l], inp["ssd_a_log"][l], inp["gdn_dt_bias"][l],
                              inp["ssd_dt_bias"][l], inp["ssd_d"][l]]).astype(f)
        prm[l] = np.broadcast_to(row[None, :], (128, NPRM))
    lgt = np.ascontiguousarray(np.broadcast_to(inp["hgrn_lb_logits"].reshape(1, 1024), (128, 1024))).astype(f)
    finw = np.ascontiguousarray(np.broadcast_to(inp["final_norm_w"].reshape(1, 1024), (128, 1024))).astype(f)
    featp = np.zeros((128, 32 + 192), f)
    featp[:, 0:32] = inp["norm_w"].reshape(DEPTH, 8, 128).transpose(2, 0, 1).reshape(128, 32)
    for l in range(DEPTH):
        for cv, key in enumerate(("gdn_conv_w", "ssd_conv_w")):
            w = inp[key][l].reshape(4, 6, 128)
            featp[:, 32 + l * 48 + cv * 24:32 + l * 48 + cv * 24 + 24] = w.transpose(2, 1, 0).reshape(128, 24)
    cbias = np.ascontiguousarray(inp["ssd_conv_b"].reshape(1, DEPTH * 768)).astype(f)
    cw = np.stack([inp["gdn_conv_w"], inp["ssd_conv_w"]], 1).astype(f)
    cwrow = np.ascontiguousarray(np.broadcast_to(cw[:, :, :, None, :], (DEPTH, 2, 4, NS, 768)))
    cbrow = np.ascontiguousarray(np.broadcast_to(inp["ssd_conv_b"].astype(f)[:, None, :], (DEPTH, NS, 768)))
    return {
        "xp": np.ascontiguousarray(inp["x_prompt"][c]).astype(f),
        "meta": np.ascontiguousarray(inp["meta_tokens"]).astype(f),
        "w_in": np.ascontiguousarray(inp["w_in"]).astype(f),
        "w_out": np.ascontiguousarray(inp["w_out"]).astype(f),
        "cst": CST, "rot": ROT, "prm": prm, "lgt": lgt, "finw": finw, "featp": featp, "cbias": cbias,
        "cwrow": cwrow, "cbrow": cbrow, **_sample_inputs(inp, c),
    }


def _sample_inputs(inp, c):
    f = np.float32
    sl = slice(c * NS, (c + 1) * NS)
    return {
        "xs_in": np.ascontiguousarray(inp["x_sample"][sl, 0, :]).astype(f),
        "si_hg": np.ascontiguousarray(inp["state_hgrn"][:, sl]).astype(f),
        "si_gd": np.ascontiguousarray(inp["state_gdn"][:, sl]).astype(f),
        "si_gc": np.ascontiguousarray(inp["state_gdn_conv"][:, sl]).astype(f),
        "si_sd": np.ascontiguousarray(inp["state_ssd"][:, sl]).astype(f),
        "si_sc": np.ascontiguousarray(inp["state_ssd_conv"][:, sl]).astype(f),
        "si_rt": np.ascontiguousarray(inp["state_ret"][:, sl]).astype(f),
    }


def kernel(**inp):
    inp = {k: np.asarray(v) for k, v in inp.items()}
    if "nc" not in _NC_CACHE:
        _NC_CACHE["nc"] = build_program()
    nc = _NC_CACHE["nc"]
    shared = None
    in_maps = []
    for c in range(8):
        m = _prep_inputs(inp, c) if shared is None else dict(shared)
        if shared is None:
            shared = m
        else:
            m["xp"] = np.ascontiguousarray(inp["x_prompt"][c]).astype(np.float32)
            m.update(_sample_inputs(inp, c))
        in_maps.append(m)
    res = run_bass_kernel_spmd(nc, in_maps, core_ids=list(range(8)))
    R = res.results
    y_prompt = np.stack([R[c]["y_p"] for c in range(8)], 0)
    def stk(name):
        return np.ascontiguousarray(np.stack([R[c][name] for c in range(8)], 1))
    y_sample = np.concatenate([R[c]["y_s"] for c in range(8)], 0)[:, None, :]
    def cat(name):
        return np.ascontiguousarray(np.concatenate([R[c][name] for c in range(8)], 1))
    outs = (y_prompt, np.ascontiguousarray(y_sample),
            stk("st_hg"), stk("st_gd"), stk("st_gc"), stk("st_sd"), stk("st_sc"), stk("st_rt"),
            cat("so_hg"), cat("so_gd"), cat("so_gc"), cat("so_sd"), cat("so_sc"), cat("so_rt"))
    return outs
```

```python
import contextlib
import math
import numpy as np
import concourse.bass as bass
import concourse.mybir as mybir
from concourse.bass_utils import run_bass_kernel_spmd

F32 = mybir.dt.float32
BF16 = mybir.dt.bfloat16
AF = mybir.ActivationFunctionType
ALU = mybir.AluOpType
AX = mybir.AxisListType

D = 1024
DEPTH = 4
SEQ = 2048
NT = 17
IN_DIM = 4108
EPS = 1e-6
QK = 0.125
NS = 16
NPRM = 1300
NEGV = -30000.0
import os as _os0
EMBED_WAIT = not _os0.environ.get("NO_EMBED")
ANNOTATE = bool(_os0.environ.get("ANNOTATE"))


class Prog:
    ENG = ("pe", "act", "dve", "pool", "sp")

    def __init__(self, nc, n_dma_sems=8):
        self.nc = nc
        self.ops = []
        self.cnt = {}
        self.clock = {e: {} for e in self.ENG}
        self.tok_clock = {}
        self.last_w = {}
        self.readers = {}
        self.n_dma = n_dma_sems
        self.dma_rr = {e: 0 for e in self.ENG}
        self.dma_last = {}
        self.anns = []
        import os
        self.pe_skip = not os.environ.get("PE_SELFWAIT")
        self.strict_same = not os.environ.get("RELAX_SAME")

    def _need(self, eng, tok, waits, force=False):
        key, idx = tok
        if key == "pe" and eng == "pe" and self.pe_skip and not force:
            return
        if self.clock[eng].get(key, 0) >= idx:
            return
        if waits.get(key, 0) < idx:
            waits[key] = idx

    def op(self, eng, fn, reads=(), writes=(), dma=False, pe_serial=False):
        waits = {}
        for b in reads:
            t = self.last_w.get(b)
            if t:
                self._need(eng, t, waits)
            if b.startswith("ps"):
                for r in self.readers.get(b, ()):
                    if r[0] != eng:
                        self._need(eng, r, waits)
        for b in writes:
            t = self.last_w.get(b)
            if t and (t[0] != eng or pe_serial or dma or self.strict_same):
                self._need(eng, t, waits, force=pe_serial)
            for r in self.readers.get(b, ()):
                if r[0] != eng or dma or self.strict_same:
                    self._need(eng, r, waits)
        if dma:
            key = ("dma", eng, self.dma_rr[eng] % self.n_dma)
            self.dma_rr[eng] += 1
            prev = self.dma_last.get(key)
            if prev:
                self._need(eng, prev, waits)
        else:
            key = eng
        ck = self.clock[eng]
        for kk, ii in waits.items():
            for k2, i2 in self.tok_clock.get((kk, ii), {}).items():
                if ck.get(k2, 0) < i2:
                    ck[k2] = i2
            if ck.get(kk, 0) < ii:
                ck[kk] = ii
        self.cnt[key] = self.cnt.get(key, 0) + 1
        tok = (key, self.cnt[key])
        if dma:
            self.dma_last[key] = tok
            snap = dict(ck)
            snap[key] = tok[1]
            self.tok_clock[tok] = snap
        else:
            snap = dict(ck)
            snap[key] = tok[1]
            self.tok_clock[tok] = snap
        for b in writes:
            self.last_w[b] = tok
            self.readers[b] = []
        for b in reads:
            if b not in writes:
                self.readers.setdefault(b, []).append(tok)
        ann = None
        if ANNOTATE:
            import sys as _sys
            f = _sys._getframe(1)
            while f:
                if f.f_code.co_name in ("tile_fwd", "sample_fwd", "load_layer"):
                    ann = "L%d" % f.f_lineno
                    break
                f = f.f_back
        self.anns.append(ann)
        self.ops.append((eng, fn, list(waits.items()), tok, dma))
        return tok

    def emit(self, es, final_wait_eng="sp"):
        nc = self.nc
        import os
        km = int(os.environ.get("KMAX", "0"))
        if km:
            self.ops = self.ops[:km]
            self.dma_last = {}
            for (e_, f_, w_, tok_, d_) in self.ops:
                if d_:
                    self.dma_last[tok_[0]] = tok_
        needed = set()
        for (_, _, waits, _, _) in self.ops:
            for w in waits:
                needed.add(w)
        finals = []
        for k, t in self.dma_last.items():
            finals.append(t)
            needed.add(t)
        per_key = {}
        for (k, i) in needed:
            per_key.setdefault(k, []).append(i)
        sigcount = {}
        for k, lst in per_key.items():
            for n, i in enumerate(sorted(lst)):
                sigcount[(k, i)] = n + 1
        sems = {}
        for k in sorted(per_key.keys(), key=str):
            nm = "s_" + "_".join(str(x) for x in (k if isinstance(k, tuple) else (k,)))
            sems[k] = es.enter_context(nc.semaphore(nm))
        per_eng = {e: [] for e in self.ENG}
        for j, o in enumerate(self.ops):
            per_eng[o[0]].append(o + (self.anns[j] if j < len(self.anns) else None,))
        blk = es.enter_context(nc.Block())

        def run(e, engobj):
            for (_, fn, waits, tok, dma, ann) in per_eng[e]:
                emb = None
                if waits and EMBED_WAIT and not dma:
                    emb = waits[-1]
                    waits = waits[:-1]
                for (k, i) in waits:
                    mult = 16 if isinstance(k, tuple) else 1
                    engobj.wait_ge(sems[k], sigcount[(k, i)] * mult)
                ins = fn(engobj)
                if ann is not None:
                    ins.annotate(ann)
                if emb is not None:
                    k, i = emb
                    ins._wait_ge(sems[k], sigcount[(k, i)] * (16 if isinstance(k, tuple) else 1))
                if tok in sigcount:
                    ins.then_inc(sems[tok[0]], 16 if dma else 1)
            if e == final_wait_eng:
                for t in finals:
                    engobj.wait_ge(sems[t[0]], sigcount[t] * 16)

        @blk.tensor
        def _(e):
            run("pe", e)

        @blk.scalar
        def _(e):
            run("act", e)

        @blk.vector
        def _(e):
            run("dve", e)

        @blk.gpsimd
        def _(e):
            run("pool", e)

        @blk.sync
        def _(e):
            run("sp", e)


def host_consts():
    idx = np.arange(128)
    ch = idx // 64
    same = ch[:, None] == ch[None, :]
    ident = np.eye(128, dtype=np.float32)
    maskT = (same & (idx[:, None] <= idx[None, :])).astype(np.float32)
    negT = np.where(maskT > 0, 0.0, NEGV).astype(np.float32)
    strict = (same & (idx[None, :] < idx[:, None]))
    negS = np.where(strict, 0.0, NEGV).astype(np.float32)
    mid = ch * 64 + 31
    uprime = (same & (idx[:, None] <= idx[None, :])).astype(np.float32) - \
             (same & (idx[:, None] <= mid[None, :])).astype(np.float32)
    urev = (same & (idx[:, None] > idx[None, :])).astype(np.float32)
    wc = np.zeros((128, 8), np.float32)
    wc[:, 0] = (idx <= 31)
    wc[:, 1] = (idx >= 64) & (idx <= 95)
    wc[:, 2] = (idx >= 32) & (idx <= 63)
    wc[:, 3] = (idx >= 96)
    wc[:, 4] = (idx <= 63)
    wc[:, 5] = (idx >= 64)
    blockones = same.astype(np.float32)
    lg = np.log1p(-np.exp2(-5.0 - np.arange(4, dtype=np.float64)))
    loc = idx % 64
    dt_ret = np.zeros((128, 4, 128), np.float64)
    for h in range(4):
        dt_ret[:, h, :] = np.where(maskT > 0, np.exp(lg[h] * (idx[None, :] - idx[:, None])), 0.0) * QK
    egq = np.zeros((128, 2, 128), np.float64)
    for hp in range(2):
        for hh in range(2):
            egq[hh * 64:(hh + 1) * 64, hp, :] = np.exp(lg[2 * hp + hh] * (loc[None, :] + 1))
    egrev64 = np.zeros((128, 4), np.float64)
    egrev16 = np.zeros((128, 4), np.float64)
    for h in range(4):
        egrev64[:, h] = np.exp(lg[h] * (63 - loc)) * QK
        egrev16[:, h] = np.exp(lg[h] * np.maximum(15 - idx, 0)) * QK
    egl = np.zeros((128, 2, 2), np.float64)
    for hp in range(2):
        for hh in range(2):
            egl[hh * 64:(hh + 1) * 64, hp, 0] = np.exp(lg[2 * hp + hh] * 16)
            egl[hh * 64:(hh + 1) * 64, hp, 1] = np.exp(lg[2 * hp + hh] * 64)
    sel = np.zeros((128, 4, 64), np.float32)
    selT = np.zeros((128, 4, 16), np.float32)
    gam64 = np.zeros((128, 4), np.float32)
    for h in range(4):
        for b in range(16):
            sel[b, h, h * 16 + b] = 1.0
            selT[h * 16 + b, h, b] = 1.0
            gam64[h * 16 + b, 0] = np.exp(lg[h])
    parts = [ident, maskT, negT, negS, uprime, urev, wc, blockones,
             dt_ret.reshape(128, 512), egq.reshape(128, 256), egrev64, egrev16, egl.reshape(128, 4),
             sel.reshape(128, 256), selT.reshape(128, 64), gam64]
    offs = {}
    names = ["ident", "maskT", "negT", "negS", "uprime", "urev", "wc", "blockones",
             "dt_ret", "egq", "egrev64", "egrev16", "egl", "sel", "selT", "gam64"]
    o = 0
    for nm, p in zip(names, parts):
        offs[nm] = (o, p.shape[1])
        o += p.shape[1]
    cst = np.concatenate([p.astype(np.float32) for p in parts], axis=1)
    half = 32
    inv_freq = (1.0 / (np.float32(10000.0) ** np.linspace(0.0, 1.0, half, dtype=np.float32))).astype(np.float32)
    rot = np.zeros((NT + 1, 128, 64), np.float32)
    for t in range(NT):
        pos = (np.arange(128) if t == 0 else 16 + (t - 1) * 128 + np.arange(128)).astype(np.float32)
        ang = (pos[:, None] * inv_freq[None, :]).astype(np.float32)
        rot[t, :, 0:32] = np.cos(ang)
        rot[t, :, 32:64] = np.sin(ang)
    ang = (np.full((128, 1), 16384.0, np.float32) * inv_freq[None, :]).astype(np.float32)
    rot[NT, :, 0:32] = np.cos(ang)
    rot[NT, :, 32:64] = np.sin(ang)
    return cst, offs, rot


CST, COFF, ROT = host_consts()
NCST = CST.shape[1]


def build_program(depth=DEPTH, ntiles=NT):
    nc = bass.Bass("TRN2", target_bir_lowering=False)

    def din(name, shape):
        return nc.dram_tensor(name, list(shape), F32, kind="ExternalInput").ap()

    def dout(name, shape):
        return nc.dram_tensor(name, list(shape), F32, kind="ExternalOutput").ap()

    xp = din("xp", [SEQ, D])
    meta = din("meta", [16, D])
    w_in = din("w_in", [DEPTH, D, IN_DIM])
    w_out = din("w_out", [DEPTH, D, D])
    cst_d = din("cst", [128, NCST])
    rot_d = din("rot", [NT + 1, 128, 64])
    prm_d = din("prm", [DEPTH, 128, NPRM])
    lgt_d = din("lgt", [128, 1024])
    finw_d = din("finw", [128, 1024])
    featp_d = din("featp", [128, 32 + 192])
    cbias_d = din("cbias", [1, DEPTH * 768])

    y_p = dout("y_p", [SEQ, D])
    st_hg = dout("st_hg", [DEPTH, 4, 64, 64])
    st_gd = dout("st_gd", [DEPTH, 4, 64, 64])
    st_gc = dout("st_gc", [DEPTH, 3, 768])
    st_sd = dout("st_sd", [DEPTH, 4, 128, 64])
    st_sc = dout("st_sc", [DEPTH, 3, 768])
    st_rt = dout("st_rt", [DEPTH, 4, 64, 64])
    xs_d = din("xs_in", [NS, D])
    si_hg = din("si_hg", [DEPTH, NS, 4, 64, 64]); si_gd = din("si_gd", [DEPTH, NS, 4, 64, 64])
    si_gc = din("si_gc", [DEPTH, NS, 3, 768]); si_sd = din("si_sd", [DEPTH, NS, 4, 128, 64])
    si_sc = din("si_sc", [DEPTH, NS, 3, 768]); si_rt = din("si_rt", [DEPTH, NS, 4, 64, 64])
    cwrow_d = din("cwrow", [DEPTH, 2, 4, NS, 768])
    cbrow_d = din("cbrow", [DEPTH, NS, 768])
    y_s = dout("y_s", [NS, D])
    so_hg = dout("so_hg", [DEPTH, NS, 4, 64, 64]); so_gd = dout("so_gd", [DEPTH, NS, 4, 64, 64])
    so_gc = dout("so_gc", [DEPTH, NS, 3, 768]); so_sd = dout("so_sd", [DEPTH, NS, 4, 128, 64])
    so_sc = dout("so_sc", [DEPTH, NS, 3, 768]); so_rt = dout("so_rt", [DEPTH, NS, 4, 64, 64])

    with contextlib.ExitStack() as es:
        def sb(name, shape, dt=F32):
            return es.enter_context(nc.sbuf_tensor("sb_" + name, list(shape), dt))

        P = Prog(nc)

        hscr = nc.dram_tensor("hscr", [NT * 128, D], F32, kind="Internal").ap()
        hb = [sb(f"hb{i}", [128, D]) for i in range(2)]
        win = sb("win", [128, 8, IN_DIM], BF16)
        wout = sb("wout", [128, 8, D], BF16)
        WCH = 1027
        wst = [sb(f"wst{i}", [128, WCH]) for i in range(2)]
        cst = sb("cst", [128, NCST])
        ident_bf = sb("ident_bf", [128, 128], BF16)
        bones_bf = sb("bones_bf", [128, 128], BF16)
        maskT_bf = sb("maskT_bf", [128, 128], BF16)
        ghi = sb("ghi", [128, 4, 128], BF16)
        glo = sb("glo", [128, 4, 128], BF16)
        dtret_bf = sb("dtret_bf", [128, 4, 128], BF16)
        ones_bf = sb("ones_bf", [1, 128], BF16)
        prm = sb("prm", [128, NPRM])
        oml = sb("oml", [128, 4, 256])
        lgt = sb("lgt", [128, 4, 256])
        featp = sb("featp", [128, 32 + 192])
        dg = sb("dg", [128, 12, 4, 128], BF16)
        cbias_bf = sb("cbias_bf", [1, 768], BF16)
        nega = sb("nega", [128, 64])
        rot = [sb(f"rot{i}", [128, 64]) for i in range(2)]

        def C(name, rows=slice(0, 128), lo=0, hi=None):
            o, w = COFF[name]
            hi = w if hi is None else hi
            return cst[rows, o + lo:o + hi]

        psb = [es.enter_context(nc.psum_tensor(f"ps{i}", [128, 512], F32)) for i in range(8)]
        ps_rr = [0]

        def bank():
            i = ps_rr[0] % 4 if ps_rr[0] < 0 else (0, 1, 2, 3, 6, 7)[ps_rr[0] % 6]
            ps_rr[0] += 1
            return psb[i], f"ps{i}"

        import os as _os
        _AUDIT = bool(_os.environ.get("AUDIT"))
        _bad = set()

        def _chk(r, w, outs, ins):
            if not _AUDIT:
                return
            for grp, keys, what in ((outs, list(w), "W"), (ins, list(r) + list(w), "R")):
                for ap in grp:
                    nm = getattr(ap, "name", None)
                    if not isinstance(nm, str):
                        continue
                    key = nm[3:] if nm.startswith("sb_") else nm
                    if key not in keys:
                        import traceback
                        fr = traceback.extract_stack()[-3]
                        _bad.add((what, key, fr.lineno))

        _last_rb = {}

        def MM(out, lhsT, rhs, r, w, start=True, stop=True):
            skip = any(k in ("ps4", "ps5") for k in w)
            _chk(r, w, [out], [lhsT, rhs])
            rb = lhsT.base_partition()
            ser = False
            for k in w:
                if _last_rb.get(k, rb) != rb:
                    ser = True
                _last_rb[k] = rb
            P.op("pe", lambda e: e.matmul(out, lhsT=lhsT, rhs=rhs, start=start, stop=stop, skip_group_check=skip),
                 reads=r, writes=w, pe_serial=ser)

        def ACT(out, in_, func, r, w, scale=1.0, bias=None, accum=None):
            kw = {}
            if bias is not None:
                kw["bias"] = bias
            if accum is not None:
                kw["accum_out"] = accum
            if hasattr(bias, "name") and "eps_t" not in r:
                r = list(r) + ["eps_t"]
            _chk(r, w, [out] + ([accum] if accum is not None else []), [in_] + [x for x in (scale, bias) if hasattr(x, "name")])
            P.op("act", lambda e: e.activation(out=out, in_=in_, func=func, scale=scale, **kw), reads=r, writes=w)

        def TT(eng, out, in0, in1, op, r, w):
            _chk(r, w, [out], [in0, in1])
            P.op(eng, lambda e: e.tensor_tensor(out=out, in0=in0, in1=in1, op=op), reads=r, writes=w)

        def TS(eng, out, in0, s1, op0, r, w, s2=None, op1=None):
            _chk(r, w, [out], [in0] + [x for x in (s1, s2) if hasattr(x, "name")])
            if op1 is None:
                P.op(eng, lambda e: e.tensor_scalar(out=out, in0=in0, scalar1=s1, scalar2=None, op0=op0), reads=r, writes=w)
            else:
                P.op(eng, lambda e: e.tensor_scalar(out=out, in0=in0, scalar1=s1, scalar2=s2, op0=op0, op1=op1), reads=r, writes=w)

        def STT(out, in0, scalar, in1, op0, op1, r, w):
            _chk(r, w, [out], [in0, in1] + [x for x in (scalar,) if hasattr(x, "name")])
            P.op("dve", lambda e: e.scalar_tensor_tensor(out=out, in0=in0, scalar=scalar, in1=in1, op0=op0, op1=op1),
                 reads=r, writes=w)

        def RED(out, in_, r, w):
            _chk(r, w, [out], [in_])
            P.op("dve", lambda e: e.tensor_reduce(out=out, in_=in_, axis=AX.X, op=ALU.add), reads=r, writes=w)

        def RECIP(out, in_, r, w):
            _chk(r, w, [out], [in_])
            P.op("dve", lambda e: e.reciprocal(out=out, in_=in_), reads=r, writes=w)

        def CP(eng, out, in_, r, w):
            if eng == "act":
                ACT(out, in_, AF.Copy, r, w)
            else:
                _chk(r, w, [out], [in_])
                P.op(eng, lambda e: e.tensor_copy(out=out, in_=in_), reads=r, writes=w)

        def MEMSET(eng, ap, val, w):
            P.op(eng, lambda e: e.memset(ap, val), reads=(), writes=w)

        def DMA(out, in_, r, w, slow=False):
            if slow:
                P.op("sp", lambda e: e.dma_start(out=out, in_=in_, allow_slow_non_contiguous=True), reads=r, writes=w, dma=True)
            else:
                P.op("sp", lambda e: e.dma_start(out=out, in_=in_), reads=r, writes=w, dma=True)

        def sigmoid_from_exp(buf, key):
            ACT(buf, buf, AF.Ln, [key], [key], bias=eps_t[0:buf.shape[0], 1:2])
            ACT(buf, buf, AF.Exp, [key], [key], scale=-1.0)

        def rsqrt_act(out, in_, scale, r, w):
            ACT(out, in_, AF.Ln, r, w, scale=scale, bias=eps_t[0:out.shape[0], 0:1])
            ACT(out, out, AF.Exp, w, w, scale=-0.5)

        eps_t = sb("eps_t", [128, 2])
        MEMSET("pool", eps_t[:, 0:1], EPS, ["eps_t"])
        MEMSET("pool", eps_t[:, 1:2], 1.0, ["eps_t"])
        DMA(cst[:], cst_d, [], ["cst"])
        DMA(lgt[:].rearrange("p a b -> p (a b)"), lgt_d, [], ["lgt"])
        DMA(featp[:], featp_d, [], ["featp"])
        CP("dve", ident_bf[:], C("ident"), ["cst"], ["ident_bf"])
        CP("dve", bones_bf[:], C("blockones"), ["cst"], ["bones_bf"])
        CP("dve", maskT_bf[:], C("maskT"), ["cst"], ["maskT_bf"])
        CP("dve", dtret_bf[:].rearrange("p a b -> p (a b)"), C("dt_ret"), ["cst"], ["dtret_bf"])
        MEMSET("pool", ones_bf[:], 1.0, ["ones_bf"])
        mx = wst[1][:, 0:256]
        TT("dve", mx, lgt[:, 0, :], lgt[:, 1, :], ALU.max, ["lgt"], ["wst1"])
        TT("dve", mx, mx, lgt[:, 2, :], ALU.max, ["lgt", "wst1"], ["wst1"])
        TT("dve", mx, mx, lgt[:, 3, :], ALU.max, ["lgt", "wst1"], ["wst1"])
        TT("dve", lgt[:], lgt[:], mx.unsqueeze(1).to_broadcast([128, 4, 256]), ALU.subtract, ["lgt", "wst1"], ["lgt"])
        ACT(lgt[:], lgt[:], AF.Exp, ["lgt"], ["lgt"])
        TT("dve", mx, lgt[:, 0, :], lgt[:, 1, :], ALU.add, ["lgt"], ["wst1"])
        TT("dve", mx, mx, lgt[:, 2, :], ALU.add, ["lgt", "wst1"], ["wst1"])
        TT("dve", mx, mx, lgt[:, 3, :], ALU.add, ["lgt", "wst1"], ["wst1"])
        RECIP(mx, mx, ["wst1"], ["wst1"])
        TT("dve", lgt[:], lgt[:], mx.unsqueeze(1).to_broadcast([128, 4, 256]), ALU.mult, ["lgt", "wst1"], ["lgt"])
        MEMSET("dve", oml[:, 0, :], 0.0, ["oml"])
        CP("dve", oml[:, 1, :], lgt[:, 1, :], ["lgt"], ["oml"])
        TT("dve", oml[:, 2, :], oml[:, 1, :], lgt[:, 2, :], ALU.add, ["lgt", "oml"], ["oml"])
        TT("dve", oml[:, 3, :], oml[:, 2, :], lgt[:, 3, :], ALU.add, ["lgt", "oml"], ["oml"])
        TS("dve", oml[:], oml[:], 0.0, ALU.max, ["oml"], ["oml"])
        TS("dve", oml[:], oml[:], -1.0, ALU.mult, ["oml"], ["oml"], s2=1.0, op1=ALU.add)
        DMA(lgt[:].rearrange("p a b -> p (a b)"), finw_d, ["lgt"], ["lgt"])
        finw = lgt[:].rearrange("p a b -> p (a b)")

        hn_bf = sb("hn_bf", [128, D], BF16)
        hnTs = [sb(f"hnT{i}", [128, 8, 128], BF16) for i in range(2)]
        hnT = hnTs[1]
        st0 = sb("st0", [128, 2])
        st4 = sb("st4", [128, 16])
        f1 = sb("f1", [128, 768])
        f2 = sb("f2", [128, 512])
        f3 = sb("f3", [128, 512])
        f4 = sb("f4", [128, 512])
        gate = sb("gate", [128, D])
        y_bf = sb("y_bf", [128, D], BF16)
        yT = sb("yT", [128, 8, 128], BF16)
        b1 = sb("b1", [128, 4, 128], BF16)
        qkT = sb("qkT", [128, 4, 128], BF16)
        AT = sb("AT", [128, 4, 128], BF16)
        AT2 = sb("AT2", [128, 4, 128], BF16)
        v_bf = sb("v_bf", [128, 4, 64], BF16)
        kp_bf = sb("kp_bf", [128, 4, 128], BF16)
        qpT = sb("qpT", [128, 4, 128], BF16)
        ecs = sb("ecs", [128, 2, 8])
        S_A = sb("S_A", [128, 2, 64]); Sb_A = sb("Sb_A", [128, 2, 64], BF16); Sd_A = sb("Sd_A", [128, 2, 64])
        S_B = sb("S_B", [128, 2, 64]); Sb_B = sb("Sb_B", [128, 2, 64], BF16)
        S_C = sb("S_C", [128, 4, 64]); Sb_C = sb("Sb_C", [128, 4, 64], BF16)
        S_D = sb("S_D", [128, 2, 64]); Sb_D = sb("Sb_D", [128, 2, 64], BF16)
        tmpS = sb("tmpS", [128, 4, 64])
        uT = [sb(f"uT{i}", [128, 12, 131], BF16) for i in range(2)]
        cvst = sb("cvst", [128, 12, 3])
        xs = sb("xs", [128, 12, 128], BF16)
        xsf = sb("xsf", [128, 4, 128])
        g8 = sb("g8", [128, 64])
        gc = sb("gc", [128, 64])
        egc = sb("egc", [128, 64])
        beta = sb("beta", [128, 64])
        dtb = sb("dtb", [128, 64])
        nb = sb("nb", [128, 64])
        nb2 = sb("nb2", [128, 64])
        for _t, _k in ((g8, "g8"), (beta, "beta"), (nega, "nega")):
            MEMSET("pool", _t[:], 0.0, [_k])
        tt = sb("tt", [128, 8, 128])
        DT = sb("DT", [128, 8, 128], BF16)
        Dst = sb("Dst", [128, 4, 128], BF16)
        eGbc = sb("eGbc", [128, 8, 128])
        eGlB = sb("eGlB", [128, 2, 2])
        X_bf = [sb(f"X_bf{i}", [128, 4, 128], BF16) for i in range(2)]
        Y_bf = [sb(f"Y_bf{i}", [128, 4, 128], BF16) for i in range(2)]
        P_bf = sb("P_bf", [128, 4, 128], BF16)
        bv = sb("bv", [128, 4, 64])
        r_bf = sb("r_bf", [128, 4, 64], BF16)
        u_bf = sb("u_bf", [128, 4, 64], BF16)
        xd_bf = sb("xd_bf", [128, 4, 64], BF16)

        def load_layer(l):
            DMA(prm[:], prm_d[l], [], ["prm"])
            DMA(wst[0][0:1, 0:768], cbias_d[0:1, l * 768:(l + 1) * 768], [], ["wst0"])
            CP("pool", cbias_bf[:], wst[0][0:1, 0:768], ["wst0"], ["cbias_bf"])
            ACT(nega[:, 0:8], prm[:, 1280:1288], AF.Exp, ["prm"], ["nega"])
            TS("dve", nega[:, 0:64], nega[:, 0:64], -1.0, ALU.mult, ["nega"], ["nega"])
            for cv in range(2):
                for blk in range(6):
                    for w in range(4):
                        col = 32 + l * 48 + cv * 24 + blk * 4 + w
                        if (blk + w) % 2 == 0:
                            ACT(dg[:, cv * 6 + blk, w, :], ident_bf[:], AF.Copy, ["ident_bf", "featp"], ["dg"],
                                scale=featp[:, col:col + 1])
                        else:
                            TS("dve", dg[:, cv * 6 + blk, w, :], ident_bf[:], featp[:, col:col + 1], ALU.mult,
                               ["ident_bf", "featp"], ["dg"])
            i = 0
            for kc in range(8):
                for c0 in range(0, IN_DIM, WCH):
                    st = wst[i % 2]; sk = f"wst{i % 2}"
                    DMA(st[:, 0:WCH], w_in[l, kc * 128:(kc + 1) * 128, c0:c0 + WCH], [], [sk])
                    if i % 2 == 0:
                        ACT(win[:, kc, c0:c0 + WCH], st[:, 0:WCH], AF.Copy, [sk, "featp"], ["win"],
                            scale=featp[:, l * 8 + kc:l * 8 + kc + 1])
                    else:
                        TS("dve", win[:, kc, c0:c0 + WCH], st[:, 0:WCH], featp[:, l * 8 + kc:l * 8 + kc + 1], ALU.mult,
                           [sk, "featp"], ["win"])
                    i += 1
            for kc in range(8):
                st = wst[i % 2]; sk = f"wst{i % 2}"
                DMA(st[:, 0:1024], w_out[l, kc * 128:(kc + 1) * 128, :], [], [sk])
                CP("act" if i % 2 == 0 else "dve", wout[:, kc, :], st[:, 0:1024], [sk], ["wout"])
                i += 1
            for nm, S_, Sb_ in (("A", S_A, Sb_A), ("B", S_B, Sb_B), ("C", S_C, Sb_C), ("D", S_D, Sb_D)):
                MEMSET("pool", S_[:], 0.0, ["S_" + nm])
                MEMSET("pool", Sb_[:], 0.0, ["Sb_" + nm])
            MEMSET("pool", uT[0][:, :, 0:3], 0.0, ["uT0"])

        def stage0(l, t):
            n = 16 if t == 0 else 128
            hk = f"hb{t % 2}"
            ht = hb[t % 2][0:n, :]
            hnT = hnTs[t % 2]; hnTk = f"hnT{t % 2}"
            if l == 0:
                DMA(ht, meta if t == 0 else xp[(t - 1) * 128:t * 128, :], [], [hk])
            else:
                DMA(ht, hscr[t * 128:t * 128 + n, :], [f"hd{t}"], [hk])
            ACT(hn_bf[0:n, :], ht, AF.Square, [hk], ["hn_bf", "st0"], accum=st0[0:n, 0:1])
            ACT(st0[0:n, 1:2], st0[0:n, 0:1], AF.Ln, ["st0"], ["st0"], scale=1.0 / D, bias=eps_t[0:n, 0:1])
            ACT(st0[0:n, 1:2], st0[0:n, 1:2], AF.Exp, ["st0"], ["st0"], scale=-0.5)
            ACT(hn_bf[0:n, :], ht, AF.Copy, [hk, "st0"], ["hn_bf"], scale=st0[0:n, 1:2])

        def stage0_pe(l, t):
            n = 16 if t == 0 else 128
            hnT = hnTs[t % 2]; hnTk = f"hnT{t % 2}"
            for half in range(2):
                pt, pk = bank()
                for kk in range(4):
                    kc = half * 4 + kk
                    MM(pt[:, kk * 128:kk * 128 + n], hn_bf[0:n, kc * 128:(kc + 1) * 128], ident_bf[0:n, 0:n],
                       ["hn_bf", "ident_bf"], [pk])
                CP("act" if half else "dve", hnT[:, half * 4:half * 4 + 4, 0:n],
                   pt[:, :].rearrange("p (a b) -> p a b", a=4)[:, :, 0:n], [pk], [hnTk])

        def tile_fwd(l, t, mid_hook=None, late_hook=None):
            n = 16 if t == 0 else 128
            chunks = [(0, 16)] if t == 0 else [(0, 64), (64, 128)]
            nch = len(chunks)
            clen = chunks[0][1]
            last_tile = (t == ntiles - 1)
            hk = f"hb{t % 2}"
            ht = hb[t % 2][0:n, :]
            hnT = hnTs[t % 2]; hnTk = f"hnT{t % 2}"
            cur = uT[t % 2]; curk = f"uT{t % 2}"
            pO2, kO2 = psb[5], "ps5"
            pOC, kOC = psb[5], "ps5"
            nxt = uT[(t + 1) % 2]; nxtk = f"uT{(t + 1) % 2}"
            rt = rot[t % 2]; rtk = f"rot{t % 2}"
            DMA(rt[:], rot_d[t], [], [rtk])

            def bc_h(ap2d, nh=4):
                return ap2d.unsqueeze(1).to_broadcast([ap2d.shape[0], nh, ap2d.shape[1]])

            def bc_l(ap2d, m):
                return ap2d.unsqueeze(2).to_broadcast([ap2d.shape[0], ap2d.shape[1], m])

            def proj_tok(c0, c1, extra=None):
                pt, pk = bank()
                for kc in range(8):
                    MM(pt[0:n, 0:c1 - c0], hnT[:, kc, 0:n], win[:, kc, c0:c1], [hnTk, "win"], [pk],
                       start=(kc == 0), stop=(kc == 7 and extra is None))
                if extra is not None:
                    MM(pt[0:n, extra[0]:extra[0] + 4], C("ident", slice(0, n), 0, n), prm[0:n, extra[1]:extra[1] + 4],
                       ["cst", "prm"], [pk], start=False, stop=True)
                return pt, pk

            def proj_feat(cols0, nblk):
                pt, pk = bank()
                for b_ in range(nblk):
                    for kc in range(8):
                        MM(pt[:, b_ * 128:b_ * 128 + n], win[:, kc, cols0 + b_ * 128:cols0 + (b_ + 1) * 128], hnT[:, kc, 0:n],
                           [hnTk, "win"], [pk], start=(kc == 0), stop=(kc == 7))
                return pt, pk

            def v3(ps_ap, a):
                return ps_ap.rearrange("p (a b) -> p a b", a=a)

            pA0, kA0 = proj_tok(0, 512)
            pA1, kA1 = proj_tok(512, 1024)
            ACT(f1[0:n, 0:256], pA0[0:n, 0:256], AF.Exp, [kA0], ["f1"], scale=-1.0)
            ACT(f1[0:n, 256:512], pA0[0:n, 256:512], AF.Exp, [kA0], ["f1"])
            ACT(f1[0:n, 512:768], pA1[0:n, 256:512], AF.Exp, [kA1], ["f1"], scale=-1.0)
            sigmoid_from_exp(f1[0:n, :], "f1")
            STT(f2[0:n, 0:256], pA0[0:n, 0:256], QK, f1[0:n, 0:256], ALU.mult, ALU.mult, [kA0, "f1"], ["f2"])
            TT("dve", f2[0:n, 256:512], f1[0:n, 256:512], oml[0:n, l, :], ALU.mult, ["f1", "oml"], ["f2"])
            TT("dve", f1[0:n, 512:768], f1[0:n, 512:768], prm[0:n, 0:256], ALU.mult, ["f1", "prm"], ["f1"])
            TT("dve", gate[0:n, 0:256], pA1[0:n, 256:512], f1[0:n, 512:768], ALU.mult, [kA1, "f1"], ["gate"])
            ACT(v_bf[0:n, :, :].rearrange("p a b -> p (a b)"), pA1[0:n, 0:256], AF.Copy, [kA1], ["v_bf"])
            ACT(f3[0:n, 0:256], f2[0:n, 256:512], AF.Ln, ["f2"], ["f3"], scale=-1.0, bias=eps_t[0:n, 1:2])
            pG, kG = bank()
            MM(pG[0:n, 0:256], C("uprime", slice(0, n), 0, n), f3[0:n, 0:256], ["cst", "f3"], [kG])
            for hp in range(2):
                MM(pG[:, 256 + hp * 8:256 + hp * 8 + 8], f3[0:n, hp * 128:(hp + 1) * 128], C("wc", slice(0, n)),
                   ["cst", "f3"], [kG])
            ACT(f3[0:n, 0:256], pG[0:n, 0:256], AF.Exp, [kG], ["f3"])
            ACT(f3[0:n, 256:512], pG[0:n, 0:256], AF.Exp, [kG], ["f3"], scale=-1.0)
            ACT(ecs[:].rearrange("p a b -> p (a b)"), pG[:, 256:272], AF.Exp, [kG], ["ecs"])
            TT("dve", b1[0:n, 0:2, :].rearrange("p a b -> p (a b)"), f2[0:n, 0:256], f3[0:n, 0:256], ALU.mult,
               ["f2", "f3"], ["b1"])
            TT("dve", b1[0:n, 2:4, :].rearrange("p a b -> p (a b)"), f2[0:n, 256:512], f3[0:n, 256:512], ALU.mult,
               ["f2", "f3"], ["b1"])
            pT, kT = bank()
            for blk in range(4):
                MM(pT[:, blk * 128:blk * 128 + n], b1[0:n, blk, :], ident_bf[0:n, 0:n], ["b1", "ident_bf"], [kT])
            CP("act", qkT[:, :, 0:n], v3(pT[:, :], 4)[:, :, 0:n], [kT], ["qkT"])
            pS, kS = bank()
            for hd in (0, 2, 1, 3):
                hp, hh = hd // 2, hd % 2
                rows = slice(hh * 64, hh * 64 + 64)
                MM(pS[0:n, hd * 128:hd * 128 + n], qkT[rows, 2 + hp, 0:n], qkT[rows, hp, 0:n], ["qkT"], [kS])
            TT("dve", AT[0:n, :, 0:n], v3(pS[0:n, :], 4)[:, :, 0:n], bc_h(C("maskT", slice(0, n), 0, n)), ALU.mult,
               [kS, "cst"], ["AT"])
            pO, kO = psb[4], "ps4"
            pOD, kOD = psb[4], "ps4"
            for hd in (0, 2, 1, 3):
                MM(pO[0:n, hd * 64:hd * 64 + 64], AT[0:n, hd, 0:n], v_bf[0:n, hd, :], ["AT", "v_bf"], [kO],
                   start=(hd == 0), stop=False)
            for ci, (c0, c1) in enumerate(chunks):
                TT("dve", Sb_A[:], S_A[:], bc_l(ecs[:, :, ci], 64), ALU.mult, ["S_A", "ecs"], ["Sb_A"])
                TT("dve", Sd_A[:], S_A[:], bc_l(ecs[:, :, 4 + ci], 64), ALU.mult, ["S_A", "ecs"], ["Sd_A"])
                pK, kK = bank()
                for hd in (0, 2, 1, 3):
                    hp, hh = hd // 2, hd % 2
                    rows = slice(hh * 64, hh * 64 + 64)
                    MM(pO[c0:c1, hd * 64:hd * 64 + 64], qkT[rows, hp, c0:c1], Sb_A[rows, hp, :], ["qkT", "Sb_A"], [kO],
                       start=False, stop=True)
                    MM(pK[rows, hp * 64:hp * 64 + 64], b1[c0:c1, 2 + hp, hh * 64:hh * 64 + 64], v_bf[c0:c1, hd, :],
                       ["b1", "v_bf"], [kK])
                TT("dve", tmpS[:, 0:2, :], v3(pK[:, 0:128], 2), bc_l(ecs[:, :, 2 + ci], 64), ALU.mult, [kK, "ecs"], ["tmpS"])
                TT("dve", S_A[:], tmpS[:, 0:2, :], Sd_A[:], ALU.add, ["tmpS", "Sd_A"], ["S_A"])

            def head_norm(ps_ap, pskey, gcols, ycols):
                ACT(f4[0:n, 0:256], ps_ap, AF.Square, [pskey], ["f4"])
                RED(st4[0:n, 4:8], v3(f4[0:n, 0:256], 4), ["f4"], ["st4"])
                rsqrt_act(st4[0:n, 8:12], st4[0:n, 4:8], 1.0 / 64, ["st4"], ["st4"])
                TT("dve", v3(f4[0:n, 0:256], 4), v3(ps_ap, 4), bc_l(st4[0:n, 8:12], 64), ALU.mult, [pskey, "st4"], ["f4"])
                TT("dve", y_bf[0:n, ycols], f4[0:n, 0:256], gate[0:n, gcols], ALU.mult, ["f4", "gate"], ["y_bf"])

            head_norm(pO[0:n, 0:256], kO, slice(0, 256), slice(0, 256))
            if mid_hook is not None:
                mid_hook()

            pD0, kD0 = proj_tok(3084, 3596)
            pD1, kD1 = proj_tok(3596, 4108)
            cosb = rt[0:n, 0:32].unsqueeze(1).to_broadcast([n, 16, 32])
            sinb = rt[0:n, 32:64].unsqueeze(1).to_broadcast([n, 16, 32])
            qk4 = pD0[0:n, :].rearrange("p (a b) -> p a b", a=16)
            TT("dve", f1[0:n, 0:512].rearrange("p (a b) -> p a b", a=16), qk4, cosb, ALU.mult, [kD0, rtk], ["f1"])
            TT("dve", f2[0:n, 0:512].rearrange("p (a b) -> p a b", a=16), qk4, sinb, ALU.mult, [kD0, rtk], ["f2"])
            c4 = f1[0:n, 0:512].rearrange("p (a s b) -> p a s b", a=8, s=2)
            s4 = f2[0:n, 0:512].rearrange("p (a s b) -> p a s b", a=8, s=2)
            qkr = b1[0:n, :, :].rearrange("p a (s b) -> p a s b", s=4)
            qkr8 = b1[0:n, :, :].rearrange("p a b -> p (a b)").rearrange("p (a s b) -> p a s b", a=8, s=2)
            TT("dve", qkr8[:, :, 0, :], c4[:, :, 0, :], s4[:, :, 1, :], ALU.subtract, ["f1", "f2"], ["b1"])
            TT("dve", qkr8[:, :, 1, :], c4[:, :, 1, :], s4[:, :, 0, :], ALU.add, ["f1", "f2"], ["b1"])
            ACT(v_bf[0:n, :, :].rearrange("p a b -> p (a b)"), pD1[0:n, 0:256], AF.Copy, [kD1], ["v_bf"])
            ACT(f1[0:n, 512:768], pD1[0:n, 256:512], AF.Exp, [kD1], ["f1"], scale=-1.0)
            sigmoid_from_exp(f1[0:n, 512:768], "f1")
            TT("dve", gate[0:n, 768:1024], pD1[0:n, 256:512], f1[0:n, 512:768], ALU.mult, [kD1, "f1"], ["gate"])
            pT, kT = bank()
            for blk in range(4):
                MM(pT[:, blk * 128:blk * 128 + n], b1[0:n, blk, :], ident_bf[0:n, 0:n], ["b1", "ident_bf"], [kT])
            CP("act", qkT[:, :, 0:n], v3(pT[:, :], 4)[:, :, 0:n], [kT], ["qkT"])
            egq = C("egq").rearrange("p (a b) -> p a b", a=2)
            TT("dve", qpT[:, 0:2, 0:n], qkT[:, 0:2, 0:n], egq[:, :, 0:n], ALU.mult, ["qkT", "cst"], ["qpT"])
            egrev = C("egrev16" if t == 0 else "egrev64", slice(0, n))
            TT("dve", kp_bf[0:n, :, 0:64], b1[0:n, 2:4, :].rearrange("p a (s b) -> p (a s) b", s=2), bc_l(egrev, 64), ALU.mult,
               ["b1", "cst"], ["kp_bf"])
            pS, kS = bank()
            for hd in (0, 2, 1, 3):
                hp, hh = hd // 2, hd % 2
                rows = slice(hh * 64, hh * 64 + 64)
                MM(pS[0:n, hd * 128:hd * 128 + n], qkT[rows, 2 + hp, 0:n], qkT[rows, hp, 0:n], ["qkT"], [kS])
            TT("dve", AT[0:n, :, 0:n], v3(pS[0:n, :], 4)[:, :, 0:n], dtret_bf[0:n, :, 0:n], ALU.mult, [kS, "dtret_bf"], ["AT"])
            for hd in (0, 2, 1, 3):
                MM(pOD[0:n, 256 + hd * 64:256 + hd * 64 + 64], AT[0:n, hd, 0:n], v_bf[0:n, hd, :], ["AT", "v_bf"], [kOD],
                   start=(hd == 0), stop=False)
            egl = C("egl").rearrange("p (a b) -> p a b", a=2)
            for ci, (c0, c1) in enumerate(chunks):
                pK, kK = bank()
                for hd in (0, 2, 1, 3):
                    hp, hh = hd // 2, hd % 2
                    rows = slice(hh * 64, hh * 64 + 64)
                    MM(pOD[c0:c1, 256 + hd * 64:256 + hd * 64 + 64], qpT[rows, hp, c0:c1], Sb_D[rows, hp, :], ["qpT", "Sb_D"], [kOD],
                       start=False, stop=True)
                    MM(pK[rows, hp * 64:hp * 64 + 64], kp_bf[c0:c1, hd, 0:64], v_bf[c0:c1, hd, :], ["kp_bf", "v_bf"], [kK])
                TT("dve", tmpS[:, 0:2, :], S_D[:], bc_l(egl[:, :, (0 if t == 0 else 1)], 64), ALU.mult, ["S_D", "cst"], ["tmpS"])
                TT("dve", S_D[:], tmpS[:, 0:2, :], v3(pK[:, 0:128], 2), ALU.add, ["tmpS", kK], ["S_D"])
                CP("act", Sb_D[:], S_D[:], ["S_D"], ["Sb_D"])
            oD = pOD[0:n, 256:512]
            kO_ = kOD
            RED(st4[0:n, 4:8], v3(oD, 4), [kO_], ["st4"])
            TS("dve", st4[0:n, 4:8], st4[0:n, 4:8], -1.0 / 64, ALU.mult, ["st4"], ["st4"])
            TT("dve", v3(f3[0:n, 0:256], 4), v3(oD, 4), bc_l(st4[0:n, 4:8], 64), ALU.add, [kO_, "st4"], ["f3"])
            ACT(f4[0:n, 0:256], f3[0:n, 0:256], AF.Square, ["f3"], ["f4"])
            RED(st4[0:n, 4:8], v3(f4[0:n, 0:256], 4), ["f4"], ["st4"])
            rsqrt_act(st4[0:n, 8:12], st4[0:n, 4:8], 1.0 / 64, ["st4"], ["st4"])
            TT("dve", v3(f3[0:n, 0:256], 4), v3(f3[0:n, 0:256], 4), bc_l(st4[0:n, 8:12], 64), ALU.mult, ["f3", "st4"], ["f3"])
            TT("dve", f3[0:n, 0:256], f3[0:n, 0:256], prm[0:n, 768:1024], ALU.mult, ["f3", "prm"], ["f3"])
            TT("dve", f3[0:n, 0:256], f3[0:n, 0:256], prm[0:n, 1024:1280], ALU.add, ["f3", "prm"], ["f3"])
            TT("dve", y_bf[0:n, 768:1024], f3[0:n, 0:256], gate[0:n, 768:1024], ALU.mult, ["f3", "gate"], ["y_bf"])

            pBz, kBz = proj_tok(1792, 2056, (256, 1288))
            pCz, kCz = proj_tok(2824, 3084, (256, 1292))
            ACT(f1[0:n, 512:768], pBz[0:n, 0:256], AF.Exp, [kBz], ["f1"], scale=-1.0)
            ACT(f2[0:n, 0:256], pCz[0:n, 0:256], AF.Exp, [kCz], ["f2"], scale=-1.0)
            ACT(beta[0:n, 0:4], pBz[0:n, 260:264], AF.Exp, [kBz], ["beta"], scale=-1.0)
            ACT(g8[0:n, 0:4], pBz[0:n, 256:260], AF.Exp, [kBz], ["g8"])
            ACT(g8[0:n, 4:8], pCz[0:n, 256:260], AF.Exp, [kCz], ["g8"])
            sigmoid_from_exp(f1[0:n, 512:768], "f1")
            sigmoid_from_exp(f2[0:n, 0:256], "f2")
            TT("dve", f1[0:n, 512:768], f1[0:n, 512:768], prm[0:n, 256:512], ALU.mult, ["f1", "prm"], ["f1"])
            TT("dve", gate[0:n, 256:512], pBz[0:n, 0:256], f1[0:n, 512:768], ALU.mult, [kBz, "f1"], ["gate"])
            TT("dve", gate[0:n, 512:768], pCz[0:n, 0:256], f2[0:n, 0:256], ALU.mult, [kCz, "f2"], ["gate"])
            for cv, cols0 in ((0, 1024), (1, 2056)):
                for part, (b0, nb_) in enumerate(((0, 4), (4, 2))):
                    pf, kf = proj_feat(cols0 + b0 * 128, nb_)
                    src = v3(pf[:, 0:nb_ * 128], nb_)[:, :, 0:n]
                    CP("act", cur[:, cv * 6 + b0:cv * 6 + b0 + nb_, 3:3 + n], src, [kf], [curk])
                    if last_tile:
                        CP("dve", cvst[:, cv * 6 + b0:cv * 6 + b0 + nb_, :], src[:, :, n - 3:n], [kf], ["cvst"])
            if not last_tile:
                CP("pool", nxt[:, :, 0:3], cur[:, :, n:n + 3], [curk], [nxtk])
            for cv in range(2):
                for part, (b0, nb_) in enumerate(((0, 4), (4, 2))):
                    pc, kc_ = bank()
                    for b_ in range(nb_):
                        blk = cv * 6 + b0 + b_
                        for w in range(4):
                            MM(pc[:, b_ * 128:b_ * 128 + n], dg[:, blk, w, :], cur[:, blk, w:w + n], ["dg", curk], [kc_],
                               start=(w == 0), stop=(w == 3 and cv == 0))
                        if cv == 1:
                            MM(pc[:, b_ * 128:b_ * 128 + n], cbias_bf[0:1, (b0 + b_) * 128:(b0 + b_ + 1) * 128], ones_bf[0:1, 0:n],
                               ["cbias_bf", "ones_bf"], [kc_], start=False, stop=True)
                    src = v3(pc[:, 0:nb_ * 128], nb_)[:, :, 0:n]
                    dstf = v3(f1[:, 0:nb_ * 128], nb_)[:, :, 0:n]
                    ACT(dstf, src, AF.Exp, [kc_], ["f1"], scale=-1.0)
                    sigmoid_from_exp(dstf, "f1")
                    if cv == 0 and part == 0:
                        TT("dve", xsf[:, :, 0:n], src, dstf, ALU.mult, [kc_, "f1"], ["xsf"])
                        ACT(b1[:, :, 0:n], xsf[:, :, 0:n], AF.Square, ["xsf"], ["b1"])
                        pN, kN = bank()
                        for blk in range(4):
                            MM(pN[:, blk * 128:blk * 128 + n], bones_bf[:], b1[:, blk, 0:n], ["bones_bf", "b1"], [kN])
                        srcN = v3(pN[:, :], 4)[:, :, 0:n]
                        dstN = v3(f2[:, 0:512], 4)[:, :, 0:n]
                        ACT(dstN, srcN, AF.Ln, [kN], ["f2"], bias=eps_t[:, 0:1])
                        ACT(dstN, dstN, AF.Exp, ["f2"], ["f2"], scale=-0.5)
                        STT(xs[:, 0:2, 0:n], xsf[:, 0:2, 0:n], QK, dstN[:, 0:2, :], ALU.mult, ALU.mult, ["xsf", "f2"], ["xs"])
                        TT("dve", xs[:, 2:4, 0:n], xsf[:, 2:4, 0:n], dstN[:, 2:4, :], ALU.mult, ["xsf", "f2"], ["xs"])
                    else:
                        TT("dve", xs[:, cv * 6 + b0:cv * 6 + b0 + nb_, 0:n], src, dstf, ALU.mult, [kc_, "f1"], ["xs"])
            ACT(g8[0:n, 0:8], g8[0:n, 0:8], AF.Ln, ["g8"], ["g8"], bias=eps_t[0:n, 1:2])
            CP("dve", dtb[0:n, :], g8[0:n, :], ["g8"], ["dtb"])
            TT("dve", g8[0:n, :], g8[0:n, :], nega[0:n, :], ALU.mult, ["g8", "nega"], ["g8"])
            sigmoid_from_exp(beta[0:n, :], "beta")
            pDc, kDc = bank()
            MM(pDc[0:n, 0:32], C("maskT", slice(0, n), 0, n), g8[0:n, 0:32], ["cst", "g8"], [kDc])
            MM(pDc[0:n, 32:64], C("urev", slice(0, n), 0, n), g8[0:n, 0:32], ["cst", "g8"], [kDc])
            CP("dve", gc[0:n, :], pDc[0:n, 0:64], [kDc], ["gc"])
            ACT(egc[0:n, :], gc[0:n, :], AF.Exp, ["gc"], ["egc"])
            for half in range(2):
                pB_, kB_ = bank()
                CP("dve", ghi[0:n, :, :], bc_l(g8[0:n, half * 4:half * 4 + 4], 128), ["g8"], ["ghi"])
                TT("dve", glo[0:n, :, :], bc_l(g8[0:n, half * 4:half * 4 + 4], 128), ghi[0:n, :, :], ALU.subtract,
                   ["g8", "ghi"], ["glo"])
                for hd in range(4):
                    MM(pB_[:, hd * 128:hd * 128 + n], ghi[0:n, hd, :], maskT_bf[0:n, 0:n], ["maskT_bf", "ghi"], [kB_],
                       start=True, stop=False)
                    MM(pB_[:, hd * 128:hd * 128 + n], glo[0:n, hd, :], maskT_bf[0:n, 0:n], ["maskT_bf", "glo"], [kB_],
                       start=False, stop=True)
                srcB = v3(pB_[:, :], 4)[:, :, 0:n]
                ACT(eGbc[:, half * 4:half * 4 + 4, 0:n], srcB, AF.Exp, [kB_], ["eGbc"])
                TT("dve", tt[0:n, half * 4:half * 4 + 4, 0:n], srcB[0:n], bc_l(gc[0:n, half * 4:half * 4 + 4], n), ALU.subtract,
                   [kB_, "gc"], ["tt"])
            if True:
                TT("dve", v3(f1[0:n, 0:512], 4)[:, :, 0:n], tt[0:n, 0:4, 0:n], bc_h(C("negS", slice(0, n), 0, n)), ALU.subtract,
                   ["tt", "cst"], ["f1"])
                ACT(Dst[0:n, :, 0:n], v3(f1[0:n, 0:512], 4)[:, :, 0:n], AF.Exp, ["f1"], ["Dst"], scale=-1.0)
                TT("dve", tt[0:n, :, 0:n], tt[0:n, :, 0:n], bc_h(C("negT", slice(0, n), 0, n), 8), ALU.add, ["tt", "cst"], ["tt"])
                ACT(DT[0:n, :, 0:n], tt[0:n, :, 0:n], AF.Exp, ["tt"], ["DT"])

            pS, kS = bank()
            pKK, kKK = bank()
            for hd in (0, 2, 1, 3):
                hp, hh = hd // 2, hd % 2
                rows = slice(hh * 64, hh * 64 + 64)
                MM(pS[0:n, hd * 128:hd * 128 + n], xs[rows, 2 + hp, 0:n], xs[rows, hp, 0:n], ["xs"], [kS])
                MM(pKK[0:n, hd * 128:hd * 128 + n], xs[rows, 2 + hp, 0:n], xs[rows, 2 + hp, 0:n], ["xs"], [kKK])
            TS("dve", nb[0:n, :], beta[0:n, :], -1.0, ALU.mult, ["beta"], ["nb"])
            TT("dve", v3(f1[0:n, 0:512], 4)[:, :, 0:n], v3(pKK[0:n, :], 4)[:, :, 0:n], Dst[0:n, :, 0:n], ALU.mult, [kKK, "Dst"], ["f1"])
            TT("dve", X_bf[0][0:n, :, 0:n], v3(f1[0:n, 0:512], 4)[:, :, 0:n], bc_l(nb[0:n, 0:4], n), ALU.mult,
               ["f1", "nb"], ["X_bf0"])
            TT("dve", AT[0:n, :, 0:n], v3(pS[0:n, :], 4)[:, :, 0:n], DT[0:n, 0:4, 0:n], ALU.mult, [kS, "DT"], ["AT"])
            def t_chain():
                pY, kY = bank()
                for hd in (0, 2, 1, 3):
                    MM(pY[0:n, hd * 128:hd * 128 + n], X_bf[0][0:n, hd, 0:n], ident_bf[0:n, 0:n], ["X_bf0", "ident_bf"], [kY])
                CP("act", Y_bf[0][0:n, :, 0:n], v3(pY[0:n, :], 4)[:, :, 0:n], [kY], ["Y_bf0"])
                TT("dve", P_bf[0:n, :, 0:n], v3(pY[0:n, :], 4)[:, :, 0:n], bc_h(C("ident", slice(0, n), 0, n)), ALU.add,
                   [kY, "cst"], ["P_bf"])
                yield
                nlev = int(math.ceil(math.log2(clen))) - 1
                ci_ = 0
                for lev in range(nlev):
                    ni_ = 1 - ci_
                    pX2, kX2 = bank()
                    for hd in (0, 2, 1, 3):
                        MM(pX2[0:n, hd * 128:hd * 128 + n], Y_bf[ci_][0:n, hd, 0:n], X_bf[ci_][0:n, hd, 0:n],
                           [f"Y_bf{ci_}", f"X_bf{ci_}"], [kX2])
                    CP("act", X_bf[ni_][0:n, :, 0:n], v3(pX2[0:n, :], 4)[:, :, 0:n], [kX2], [f"X_bf{ni_}"])
                    if lev < nlev - 1:
                        pY2, kY2 = bank()
                        for hd in (0, 2, 1, 3):
                            MM(pY2[0:n, hd * 128:hd * 128 + n], X_bf[ci_][0:n, hd, 0:n], Y_bf[ci_][0:n, hd, 0:n],
                               [f"Y_bf{ci_}", f"X_bf{ci_}"], [kY2])
                        CP("dve", Y_bf[ni_][0:n, :, 0:n], v3(pY2[0:n, :], 4)[:, :, 0:n], [kY2], [f"Y_bf{ni_}"])
                    pP, kP = bank()
                    for hd in (0, 2, 1, 3):
                        MM(pP[0:n, hd * 128:hd * 128 + n], X_bf[ni_][0:n, hd, 0:n], P_bf[0:n, hd, 0:n], [f"X_bf{ni_}", "P_bf"], [kP])
                    TT("dve", P_bf[0:n, :, 0:n], P_bf[0:n, :, 0:n], v3(pP[0:n, :], 4)[:, :, 0:n], ALU.add, ["P_bf", kP], ["P_bf"])
                    ci_ = ni_
                    yield

            tgen = t_chain()

            def tstep():
                next(tgen, None)

            tstep()

            pS, kS = bank()
            for g in range(2):
                MM(pS[0:n, g * 128:g * 128 + n], xs[:, 8 + g, 0:n], xs[:, 10 + g, 0:n], ["xs"], [kS])
            for g in range(2):
                TT("dve", AT2[0:n, 2 * g:2 * g + 2, 0:n], pS[0:n, g * 128:g * 128 + n].unsqueeze(1).to_broadcast([n, 2, n]),
                   DT[0:n, 4 + 2 * g:4 + 2 * g + 2, 0:n], ALU.mult, [kS, "DT"], ["AT2"])
            tstep()
            pT, kT = bank()
            for blk in range(4):
                MM(pT[0:n, blk * 128:(blk + 1) * 128], xs[:, 6 + blk, 0:n], ident_bf[:, :], ["xs", "ident_bf"], [kT])
            TT("dve", v_bf[0:n, :, :], v3(pT[0:n, 0:256], 4), bc_l(dtb[0:n, 4:8], 64), ALU.mult, [kT, "dtb"], ["v_bf"])
            TT("dve", xd_bf[0:n, :, :], v3(pT[0:n, 0:256], 4), bc_l(prm[0:n, 1296:1300], 64), ALU.mult, [kT, "prm"], ["xd_bf"])
            for g in range(2):
                TT("dve", kp_bf[0:n, 2 * g:2 * g + 2, :], pT[0:n, 256 + g * 128:256 + (g + 1) * 128].unsqueeze(1).to_broadcast([n, 2, 128]),
                   bc_l(egc[0:n, 36 + 2 * g:36 + 2 * g + 2], 128), ALU.mult, [kT, "egc"], ["kp_bf"])
                TT("dve", qpT[:, 2 * g:2 * g + 2, 0:n], xs[:, 10 + g, 0:n].unsqueeze(1).to_broadcast([128, 2, n]),
                   eGbc[:, 4 + 2 * g:4 + 2 * g + 2, 0:n], ALU.mult, ["xs", "eGbc"], ["qpT"])
            for hd in (0, 2, 1, 3):
                MM(pOC[0:n, 256 + hd * 64:256 + hd * 64 + 64], AT2[0:n, hd, 0:n], v_bf[0:n, hd, :], ["AT2", "v_bf"], [kOC],
                   start=(hd == 0), stop=False)
            MM(pOC[0:n, 256:512], ident_bf[0:n, 0:n], xd_bf[0:n, :, :].rearrange("p a b -> p (a b)"), ["ident_bf", "xd_bf"], [kOC],
               start=False, stop=False)
            for ci, (c0, c1) in enumerate(chunks):
                tstep()
                pK, kK = bank()
                for hd in (0, 2, 1, 3):
                    MM(pOC[c0:c1, 256 + hd * 64:256 + hd * 64 + 64], qpT[:, hd, c0:c1], Sb_C[:, hd, :], ["qpT", "Sb_C"], [kOC],
                       start=False, stop=True)
                    MM(pK[:, hd * 64:hd * 64 + 64], kp_bf[c0:c1, hd, :], v_bf[c0:c1, hd, :], ["kp_bf", "v_bf"], [kK])
                TT("dve", tmpS[:, :, :], S_C[:], eGbc[:, 4:8, c1 - 1:c1].to_broadcast([128, 4, 64]), ALU.mult, ["S_C", "eGbc"], ["tmpS"])
                TT("dve", S_C[:], tmpS[:, :, :], v3(pK[:, 0:256], 4), ALU.add, ["tmpS", kK], ["S_C"])
                CP("act", Sb_C[:], S_C[:], ["S_C"], ["Sb_C"])
            tstep()
            TT("dve", f3[0:n, 0:256], pOC[0:n, 256:512], gate[0:n, 512:768], ALU.mult, [kOC, "gate"], ["f3"])
            ACT(f4[0:n, 0:256], f3[0:n, 0:256], AF.Square, ["f3"], ["f4"])
            RED(st4[0:n, 4:6], v3(f4[0:n, 0:256], 2), ["f4"], ["st4"])
            rsqrt_act(st4[0:n, 8:10], st4[0:n, 4:6], 1.0 / 128, ["st4"], ["st4"])
            TT("dve", v3(f3[0:n, 0:256], 2), v3(f3[0:n, 0:256], 2), bc_l(st4[0:n, 8:10], 128), ALU.mult, ["f3", "st4"], ["f3"])
            TT("dve", y_bf[0:n, 512:768], f3[0:n, 0:256], prm[0:n, 512:768], ALU.mult, ["f3", "prm"], ["y_bf"])

            for _ in tgen:
                pass

            pT, kT = bank()
            for blk in range(4):
                MM(pT[0:n, blk * 128:(blk + 1) * 128], xs[:, 2 + blk, 0:n], ident_bf[:, :], ["xs", "ident_bf"], [kT])
            TT("dve", kp_bf[0:n, :, 0:64], v3(pT[0:n, 0:256], 4), bc_l(egc[0:n, 32:36], 64), ALU.mult, [kT, "egc"], ["kp_bf"])
            TT("dve", bv[0:n, :, :], v3(pT[0:n, 256:512], 4), bc_l(beta[0:n, 0:4], 64), ALU.mult, [kT, "beta"], ["bv"])
            TT("dve", nb2[0:n, :], nb[0:n, :], egc[0:n, :], ALU.mult, ["nb", "egc"], ["nb2"])
            for hh in range(2):
                rows = slice(hh * 64, hh * 64 + 64)
                TT("dve", qpT[rows, 0:2, 0:n], xs[rows, 0:2, 0:n], eGbc[rows, hh:4:2, 0:n], ALU.mult, ["xs", "eGbc"], ["qpT"])
            for ci, (c0, c1) in enumerate(chunks):
                pW, kW = bank()
                for hd in (0, 2, 1, 3):
                    hp, hh = hd // 2, hd % 2
                    rows = slice(hh * 64, hh * 64 + 64)
                    MM(pW[c0:c1, hd * 64:hd * 64 + 64], xs[rows, 2 + hp, c0:c1], Sb_B[rows, hp, :], ["xs", "Sb_B"], [kW])
                TT("dve", v3(f4[c0:c1, 0:256], 4), v3(pW[c0:c1, 0:256], 4), bc_l(nb2[c0:c1, 0:4], 64), ALU.mult,
                   [kW, "nb2"], ["f4"])
                TT("dve", r_bf[c0:c1, :, :], v3(f4[c0:c1, 0:256], 4), bv[c0:c1, :, :], ALU.add, ["f4", "bv"], ["r_bf"])
                pU, kU = bank()
                for hd in (0, 2, 1, 3):
                    MM(pU[c0:c1, hd * 64:hd * 64 + 64], P_bf[c0:c1, hd, c0:c1], r_bf[c0:c1, hd, :], ["P_bf", "r_bf"], [kU])
                CP("act", u_bf[c0:c1, :, :], v3(pU[c0:c1, 0:256], 4), [kU], ["u_bf"])
                pK, kK = bank()
                for hd in (0, 2, 1, 3):
                    hp, hh = hd // 2, hd % 2
                    rows = slice(hh * 64, hh * 64 + 64)
                    MM(pO2[c0:c1, hd * 64:hd * 64 + 64], AT[c0:c1, hd, c0:c1], u_bf[c0:c1, hd, :], ["AT", "u_bf"], [kO2],
                       start=(hd == 0), stop=False)
                    MM(pO2[c0:c1, hd * 64:hd * 64 + 64], qpT[rows, hp, c0:c1], Sb_B[rows, hp, :], ["qpT", "Sb_B"], [kO2],
                       start=False, stop=True)
                    MM(pK[rows, hp * 64:hp * 64 + 64], kp_bf[c0:c1, hd, 0:64], u_bf[c0:c1, hd, :], ["kp_bf", "u_bf"], [kK])
                for hh in range(2):
                    rows = slice(hh * 64, hh * 64 + 64)
                    TT("dve", tmpS[rows, 0:2, :], S_B[rows, :, :], eGbc[rows, hh:4:2, c1 - 1:c1].to_broadcast([64, 2, 64]), ALU.mult,
                       ["S_B", "eGbc"], ["tmpS"])
                TT("dve", S_B[:], tmpS[:, 0:2, :], v3(pK[:, 0:128], 2), ALU.add, ["tmpS", kK], ["S_B"])
                CP("act", Sb_B[:], S_B[:], ["S_B"], ["Sb_B"])
            head_norm(pO2[0:n, 0:256], kO2, slice(256, 512), slice(256, 512))

            if late_hook is not None:
                late_hook()
            for half in range(2):
                pt, pk = bank()
                for kk in range(4):
                    kc = half * 4 + kk
                    MM(pt[:, kk * 128:kk * 128 + n], y_bf[0:n, kc * 128:(kc + 1) * 128], ident_bf[0:n, 0:n],
                       ["y_bf", "ident_bf"], [pk])
                CP("act" if half else "dve", yT[:, half * 4:half * 4 + 4, 0:n], v3(pt[:, :], 4)[:, :, 0:n], [pk], ["yT"])
            for cg in range(2):
                pt, pk = bank()
                for kc in range(8):
                    MM(pt[0:n, :], yT[:, kc, 0:n], wout[:, kc, cg * 512:(cg + 1) * 512], ["yT", "wout"], [pk],
                       start=(kc == 0), stop=(kc == 7))
                TT("dve", ht[:, cg * 512:(cg + 1) * 512], ht[:, cg * 512:(cg + 1) * 512], pt[0:n, :], ALU.add, [hk, pk], [hk])

            if last_tile:
                for nm, S_, dst in (("A", S_A, st_hg), ("B", S_B, st_gd), ("D", S_D, st_rt)):
                    for hh in range(2):
                        DMA(dst[l, hh:4:2, :, :].rearrange("a k v -> k a v"), S_[hh * 64:(hh + 1) * 64, :, :], ["S_" + nm], [])
                DMA(st_sd[l].rearrange("a k v -> k a v"), S_C[:, :, :], ["S_C"], [])
                for blk in range(6):
                    DMA(st_gc[l][:, blk * 128:(blk + 1) * 128].rearrange("w p -> p w"), cvst[:, blk, :], ["cvst"], [], slow=True)
                    DMA(st_sc[l][:, blk * 128:(blk + 1) * 128].rearrange("w p -> p w"), cvst[:, 6 + blk, :], ["cvst"], [], slow=True)
            if l < depth - 1:
                DMA(hscr[t * 128:t * 128 + n, :], ht, [hk], [f"hd{t}"])
            if l == depth - 1 and t > 0:
                ACT(hn_bf[0:n, :], ht, AF.Square, [hk], ["hn_bf", "st4"], accum=st4[0:n, 0:1])
                rsqrt_act(st4[0:n, 1:2], st4[0:n, 0:1], 1.0 / D, ["st4"], ["st4"])
                STT(ht, ht, st4[0:n, 1:2], finw[0:n, :], ALU.mult, ALU.mult, [hk, "st4", "lgt"], [hk])
                DMA(y_p[(t - 1) * 128:t * 128, :], ht, [hk], [])


        hs = sb("hs", [NS, D])
        DMA(hs[:, :], xs_d, [], ["hs"])

        def sample_fwd(l, last):
            n = NS
            DMA(rot[0][:], rot_d[NT], [], ["rot0"])
            rt = rot[0]; rtk = "rot0"

            def v3(ps_ap, a):
                return ps_ap.rearrange("p (a b) -> p a b", a=a)

            def bc_l(ap2d, m):
                return ap2d.unsqueeze(2).to_broadcast([ap2d.shape[0], ap2d.shape[1], m])

            ACT(hn_bf[0:n, :], hs[:, :], AF.Square, ["hs"], ["hn_bf", "st4"], accum=st4[0:n, 0:1])
            rsqrt_act(st4[0:n, 1:2], st4[0:n, 0:1], 1.0 / D, ["st4"], ["st4"])
            ACT(hn_bf[0:n, :], hs[:, :], AF.Copy, ["hs", "st4"], ["hn_bf"], scale=st4[0:n, 1:2])
            for half in range(2):
                pt, pk = bank()
                for kk in range(4):
                    kc = half * 4 + kk
                    MM(pt[:, kk * 128:kk * 128 + n], hn_bf[0:n, kc * 128:(kc + 1) * 128], ident_bf[0:n, 0:n],
                       ["hn_bf", "ident_bf"], [pk])
                CP("act", hnT[:, half * 4:half * 4 + 4, 0:n], v3(pt[:, :], 4)[:, :, 0:n], [pk], ["hnT1"])

            def proj_tok(c0, c1, extra=None):
                pt, pk = bank()
                for kc in range(8):
                    MM(pt[0:n, 0:c1 - c0], hnT[:, kc, 0:n], win[:, kc, c0:c1], ["hnT1", "win"], [pk],
                       start=(kc == 0), stop=(kc == 7 and extra is None))
                if extra is not None:
                    MM(pt[0:n, extra[0]:extra[0] + 4], C("ident", slice(0, n), 0, n), prm[0:n, extra[1]:extra[1] + 4],
                       ["cst", "prm"], [pk], start=False, stop=True)
                return pt, pk

            sel = C("sel").rearrange("p (a b) -> p a b", a=4)
            selT = C("selT").rearrange("p (a b) -> p a b", a=4)
            pvs = f4
            Sbuf = [tt, eGbc]; Skey = ["tt", "eGbc"]
            Tbuf = gate; Tkey = "gate"
            slot = [0]

            def select(fields):
                pv, pvk = bank()
                first = True
                for (c0, wd, fn) in fields:
                    for hd in range(4):
                        ap, key = fn(hd)
                        P.op("pe", (lambda o_, l_, r_, st_: (lambda e: e.matmul(o_, lhsT=l_, rhs=r_, start=st_, stop=False,
                                                                                 skip_group_check=True)))(
                            pv[0:64, c0:c0 + wd], sel[0:n, hd, :], ap, first), reads=["cst", key], writes=[pvk])
                        first = False
                wtot = max(c0 + wd for (c0, wd, _) in fields)
                CP("dve", pvs[0:64, 0:wtot], pv[0:64, 0:wtot], [pvk], ["f4"])

            def unselect(o_ap, okey):
                po, pok = bank()
                for hd in range(4):
                    MM(po[0:n, hd * 64:hd * 64 + 64], selT[0:64, hd, :], o_ap, ["cst", okey], [pok])
                return po, pok

            def state_io(st_in, st_out, K):
                ks = 16
                for k0 in range(0, K, ks):
                    yield k0, ks

            def load_slice(st_in, k0, ks):
                i = slot[0] % 2
                slot[0] += 1
                Sv = Sbuf[i][0:64, :, :].rearrange("p a b -> p (a b)")[:, 0:ks * 64].rearrange("p (k v) -> p k v", k=ks)
                for hd in range(4):
                    DMA(Sv[hd * 16:(hd + 1) * 16, :, :], st_in[l, :, hd, k0:k0 + ks, :], [], [Skey[i]])
                return Sv, Skey[i]

            def store_slice(st_out, Sv, sk, k0, ks):
                for hd in range(4):
                    DMA(st_out[l, :, hd, k0:k0 + ks, :], Sv[hd * 16:(hd + 1) * 16, :, :], [sk], [])

            o_sb = tmpS[0:64, 0, :]; w_sb = tmpS[0:64, 1, :]; op_sb = tmpS[0:64, 2, :]; u_sb = tmpS[0:64, 3, :]

            def Tview(ks):
                return Tbuf[0:64, 0:ks * 64].rearrange("p (k v) -> p k v", k=ks)

            def q_reduce(Sv, sk, q_ap, k0, ks, first, acc):
                T = Tview(ks)
                TT("dve", T, Sv, bc_l(q_ap[:, k0:k0 + ks], 64), ALU.mult, [sk, "f4"], [Tkey])
                RED(op_sb, T.rearrange("p k v -> p v k"), [Tkey], ["tmpS"])
                if first:
                    CP("dve", acc, op_sb, ["tmpS"], ["tmpS"])
                else:
                    TT("dve", acc, acc, op_sb, ALU.add, ["tmpS"], ["tmpS"])

            def step_plain(st_in, st_out, K, q_ap, k_ap, v_ap, vec_f=None, sc=None):
                for k0, ks in state_io(st_in, st_out, K):
                    Sv, sk = load_slice(st_in, k0, ks)
                    T = Tview(ks)
                    TT("dve", T, bc_l(k_ap[:, k0:k0 + ks], 64), v_ap.unsqueeze(1).to_broadcast([64, ks, 64]), ALU.mult,
                       ["f4", "tmpS"], [Tkey])
                    if vec_f is not None:
                        TT("dve", Sv, Sv, bc_l(vec_f[:, k0:k0 + ks], 64), ALU.mult, [sk, "f4"], [sk])
                        TT("dve", Sv, Sv, T, ALU.add, [sk, Tkey], [sk])
                    else:
                        STT(Sv, Sv, sc, T, ALU.mult, ALU.add, [sk, Tkey, "f4", "cst"], [sk])
                    store_slice(st_out, Sv, sk, k0, ks)
                    q_reduce(Sv, sk, q_ap, k0, ks, k0 == 0, o_sb)

            def head_norm(ps_ap, pskey, gcols, ycols):
                ACT(f3[0:n, 256:512], ps_ap, AF.Square, [pskey], ["f3"])
                RED(st4[0:n, 4:8], v3(f3[0:n, 256:512], 4), ["f3"], ["st4"])
                rsqrt_act(st4[0:n, 8:12], st4[0:n, 4:8], 1.0 / 64, ["st4"], ["st4"])
                TT("dve", v3(f3[0:n, 256:512], 4), v3(ps_ap, 4), bc_l(st4[0:n, 8:12], 64), ALU.mult, [pskey, "st4"], ["f3"])
                TT("dve", y_bf[0:n, ycols], f3[0:n, 256:512], hb[1][0:n, gcols], ALU.mult, ["f3", "hb1"], ["y_bf"])

            gs = hb[1]; gsk = "hb1"

            pA0, kA0 = proj_tok(0, 512)
            pA1, kA1 = proj_tok(512, 1024)
            ACT(f1[0:n, 0:256], pA0[0:n, 0:256], AF.Exp, [kA0], ["f1"], scale=-1.0)
            ACT(f1[0:n, 256:512], pA0[0:n, 256:512], AF.Exp, [kA0], ["f1"])
            ACT(f1[0:n, 512:768], pA1[0:n, 256:512], AF.Exp, [kA1], ["f1"], scale=-1.0)
            sigmoid_from_exp(f1[0:n, :], "f1")
            STT(f2[0:n, 0:256], pA0[0:n, 0:256], QK, f1[0:n, 0:256], ALU.mult, ALU.mult, [kA0, "f1"], ["f2"])
            TT("dve", f2[0:n, 256:512], f1[0:n, 256:512], oml[0:n, l, :], ALU.mult, ["f1", "oml"], ["f2"])
            TT("dve", f1[0:n, 512:768], f1[0:n, 512:768], prm[0:n, 0:256], ALU.mult, ["f1", "prm"], ["f1"])
            TT("dve", gs[0:n, 0:256], pA1[0:n, 256:512], f1[0:n, 512:768], ALU.mult, [kA1, "f1"], [gsk])
            CP("dve", f3[0:n, 0:256], pA1[0:n, 0:256], [kA1], ["f3"])
            TS("dve", f1[0:n, 0:256], f2[0:n, 256:512], -1.0, ALU.mult, ["f2"], ["f1"], s2=1.0, op1=ALU.add)
            select([(0, 64, lambda hd: (f2[0:n, hd * 64:hd * 64 + 64], "f2")),
                    (64, 64, lambda hd: (f2[0:n, 256 + hd * 64:256 + hd * 64 + 64], "f2")),
                    (128, 64, lambda hd: (f3[0:n, hd * 64:hd * 64 + 64], "f3")),
                    (192, 64, lambda hd: (f1[0:n, hd * 64:hd * 64 + 64], "f1"))])
            step_plain(si_hg, so_hg, 64, pvs[0:64, 0:64], pvs[0:64, 64:128], pvs[0:64, 128:192], vec_f=pvs[0:64, 192:256])
            po, pok = unselect(o_sb, "tmpS")
            head_norm(po[0:n, 0:256], pok, slice(0, 256), slice(0, 256))

            pD0, kD0 = proj_tok(3084, 3596)
            pD1, kD1 = proj_tok(3596, 4108)
            cosb = rt[0:n, 0:32].unsqueeze(1).to_broadcast([n, 16, 32])
            sinb = rt[0:n, 32:64].unsqueeze(1).to_broadcast([n, 16, 32])
            qk4 = pD0[0:n, :].rearrange("p (a b) -> p a b", a=16)
            TT("dve", f1[0:n, 0:512].rearrange("p (a b) -> p a b", a=16), qk4, cosb, ALU.mult, [kD0, rtk], ["f1"])
            TT("dve", f2[0:n, 0:512].rearrange("p (a b) -> p a b", a=16), qk4, sinb, ALU.mult, [kD0, rtk], ["f2"])
            c4 = f1[0:n, 0:512].rearrange("p (a s b) -> p a s b", a=8, s=2)
            s4 = f2[0:n, 0:512].rearrange("p (a s b) -> p a s b", a=8, s=2)
            r4 = f3[0:n, 0:512].rearrange("p (a s b) -> p a s b", a=8, s=2)
            TT("dve", r4[:, :, 0, :], c4[:, :, 0, :], s4[:, :, 1, :], ALU.subtract, ["f1", "f2"], ["f3"])
            TT("dve", r4[:, :, 1, :], c4[:, :, 1, :], s4[:, :, 0, :], ALU.add, ["f1", "f2"], ["f3"])
            TS("dve", f3[0:n, 256:512], f3[0:n, 256:512], QK, ALU.mult, ["f3"], ["f3"])
            CP("dve", f1[0:n, 0:256], pD1[0:n, 0:256], [kD1], ["f1"])
            ACT(f1[0:n, 512:768], pD1[0:n, 256:512], AF.Exp, [kD1], ["f1"], scale=-1.0)
            sigmoid_from_exp(f1[0:n, 512:768], "f1")
            TT("dve", gs[0:n, 768:1024], pD1[0:n, 256:512], f1[0:n, 512:768], ALU.mult, [kD1, "f1"], [gsk])
            select([(0, 64, lambda hd: (f3[0:n, hd * 64:hd * 64 + 64], "f3")),
                    (64, 64, lambda hd: (f3[0:n, 256 + hd * 64:256 + hd * 64 + 64], "f3")),
                    (128, 64, lambda hd: (f1[0:n, hd * 64:hd * 64 + 64], "f1"))])
            step_plain(si_rt, so_rt, 64, pvs[0:64, 0:64], pvs[0:64, 64:128], pvs[0:64, 128:192], sc=C("gam64", slice(0, 64), 0, 1))
            po, pok = unselect(o_sb, "tmpS")
            oD = po[0:n, 0:256]
            RED(st4[0:n, 4:8], v3(oD, 4), [pok], ["st4"])
            TS("dve", st4[0:n, 4:8], st4[0:n, 4:8], -1.0 / 64, ALU.mult, ["st4"], ["st4"])
            TT("dve", v3(f3[0:n, 0:256], 4), v3(oD, 4), bc_l(st4[0:n, 4:8], 64), ALU.add, [pok, "st4"], ["f3"])
            ACT(f3[0:n, 256:512], f3[0:n, 0:256], AF.Square, ["f3"], ["f3"])
            RED(st4[0:n, 4:8], v3(f3[0:n, 256:512], 4), ["f3"], ["st4"])
            rsqrt_act(st4[0:n, 8:12], st4[0:n, 4:8], 1.0 / 64, ["st4"], ["st4"])
            TT("dve", v3(f3[0:n, 0:256], 4), v3(f3[0:n, 0:256], 4), bc_l(st4[0:n, 8:12], 64), ALU.mult, ["f3", "st4"], ["f3"])
            TT("dve", f3[0:n, 0:256], f3[0:n, 0:256], prm[0:n, 768:1024], ALU.mult, ["f3", "prm"], ["f3"])
            TT("dve", f3[0:n, 0:256], f3[0:n, 0:256], prm[0:n, 1024:1280], ALU.add, ["f3", "prm"], ["f3"])
            TT("dve", y_bf[0:n, 768:1024], f3[0:n, 0:256], gs[0:n, 768:1024], ALU.mult, ["f3", gsk], ["y_bf"])

            pBz, kBz = proj_tok(1792, 2056, (256, 1288))
            ACT(f1[0:n, 512:768], pBz[0:n, 0:256], AF.Exp, [kBz], ["f1"], scale=-1.0)
            ACT(beta[0:n, 0:4], pBz[0:n, 260:264], AF.Exp, [kBz], ["beta"], scale=-1.0)
            ACT(g8[0:n, 0:4], pBz[0:n, 256:260], AF.Exp, [kBz], ["g8"])
            sigmoid_from_exp(f1[0:n, 512:768], "f1")
            TT("dve", f1[0:n, 512:768], f1[0:n, 512:768], prm[0:n, 256:512], ALU.mult, ["f1", "prm"], ["f1"])
            TT("dve", gs[0:n, 256:512], pBz[0:n, 0:256], f1[0:n, 512:768], ALU.mult, [kBz, "f1"], [gsk])
            pCz, kCz = proj_tok(2824, 3084, (256, 1292))
            ACT(f1[0:n, 512:768], pCz[0:n, 0:256], AF.Exp, [kCz], ["f1"], scale=-1.0)
            ACT(g8[0:n, 4:8], pCz[0:n, 256:260], AF.Exp, [kCz], ["g8"])
            sigmoid_from_exp(f1[0:n, 512:768], "f1")
            TT("dve", gs[0:n, 512:768], pCz[0:n, 0:256], f1[0:n, 512:768], ALU.mult, [kCz, "f1"], [gsk])
            ACT(g8[0:n, 0:8], g8[0:n, 0:8], AF.Ln, ["g8"], ["g8"], bias=eps_t[0:n, 1:2])
            CP("dve", dtb[0:n, :], g8[0:n, :], ["g8"], ["dtb"])
            TT("dve", g8[0:n, :], g8[0:n, :], nega[0:n, :], ALU.mult, ["g8", "nega"], ["g8"])
            ACT(egc[0:n, 0:8], g8[0:n, 0:8], AF.Exp, ["g8"], ["egc"])
            sigmoid_from_exp(beta[0:n, :], "beta")

            def conv_tok(cv, groups, st_in, st_out, bias):
                U = hb[0]; Uk = "hb0"
                for (c0, c1, o0) in groups:
                    pu, puk = proj_tok(c0, c1)
                    CP("dve", U[0:n, o0:o0 + (c1 - c0)], pu[0:n, 0:c1 - c0], [puk], [Uk])
                DMA(st_out[l, :, 2, :], U[0:n, 0:768], [Uk], [])
                Wt = eGbc[0:n, :, :].rearrange("p a b -> p (a b)")[:, 0:768]
                Ct = tt[0:n, :, :].rearrange("p a b -> p (a b)")[:, 0:768]
                DMA(Wt, cwrow_d[l, cv, 3], [], ["eGbc"])
                TT("dve", f1[0:n, 0:768], U[0:n, 0:768], Wt, ALU.mult, [Uk, "eGbc"], ["f1"])
                for w in range(3):
                    DMA(Ct, st_in[l, :, w, :], [], ["tt"])
                    DMA(Wt, cwrow_d[l, cv, w], [], ["eGbc"])
                    if w >= 1:
                        DMA(st_out[l, :, w - 1, :], Ct, ["tt"], [])
                    TT("dve", Wt, Ct, Wt, ALU.mult, ["tt", "eGbc"], ["eGbc"])
                    TT("dve", f1[0:n, 0:768], f1[0:n, 0:768], Wt, ALU.add, ["f1", "eGbc"], ["f1"])
                if bias:
                    DMA(Ct, cbrow_d[l], [], ["tt"])
                    TT("dve", f1[0:n, 0:768], f1[0:n, 0:768], Ct, ALU.add, ["f1", "tt"], ["f1"])
                ACT(Ct, f1[0:n, 0:768], AF.Exp, ["f1"], ["tt"], scale=-1.0)
                sigmoid_from_exp(Ct, "tt")
                TT("dve", f1[0:n, 0:768], f1[0:n, 0:768], Ct, ALU.mult, ["f1", "tt"], ["f1"])

            conv_tok(0, [(1024, 1536, 0), (1536, 1792, 512)], si_gc, so_gc, False)
            Ct = tt[0:n, :, :].rearrange("p a b -> p (a b)")[:, 0:512]
            ACT(Ct, f1[0:n, 0:512], AF.Square, ["f1"], ["tt"])
            RED(nb[0:n, 0:8], v3(Ct, 8), ["tt"], ["nb"])
            ACT(nb[0:n, 8:16], nb[0:n, 0:8], AF.Ln, ["nb"], ["nb"], bias=eps_t[0:n, 0:1])
            ACT(nb[0:n, 8:16], nb[0:n, 8:16], AF.Exp, ["nb"], ["nb"], scale=-0.5)
            TT("dve", v3(f1[0:n, 0:512], 8), v3(f1[0:n, 0:512], 8), bc_l(nb[0:n, 8:16], 64), ALU.mult, ["f1", "nb"], ["f1"])
            TS("dve", f1[0:n, 0:256], f1[0:n, 0:256], QK, ALU.mult, ["f1"], ["f1"])
            select([(0, 64, lambda hd: (f1[0:n, hd * 64:hd * 64 + 64], "f1")),
                    (64, 64, lambda hd: (f1[0:n, 256 + hd * 64:256 + hd * 64 + 64], "f1")),
                    (128, 64, lambda hd: (f1[0:n, 512 + hd * 64:512 + hd * 64 + 64], "f1")),
                    (192, 1, lambda hd: (egc[0:n, hd:hd + 1], "egc")),
                    (193, 1, lambda hd: (beta[0:n, hd:hd + 1], "beta"))])
            qB, kB, vB = pvs[0:64, 0:64], pvs[0:64, 64:128], pvs[0:64, 128:192]
            egB, btB = pvs[0:64, 192:193], pvs[0:64, 193:194]
            for k0, ks in state_io(si_gd, so_gd, 64):
                Sv, sk = load_slice(si_gd, k0, ks)
                T = Tview(ks)
                TT("dve", T, Sv, bc_l(kB[:, k0:k0 + ks], 64), ALU.mult, [sk, "f4"], [Tkey])
                RED(op_sb, T.rearrange("p k v -> p v k"), [Tkey], ["tmpS"])
                if k0 == 0:
                    CP("dve", w_sb, op_sb, ["tmpS"], ["tmpS"])
                else:
                    TT("dve", w_sb, w_sb, op_sb, ALU.add, ["tmpS"], ["tmpS"])
            TS("dve", w_sb, w_sb, egB, ALU.mult, ["tmpS", "f4"], ["tmpS"])
            TT("dve", u_sb, vB, w_sb, ALU.subtract, ["f4", "tmpS"], ["tmpS"])
            TS("dve", u_sb, u_sb, btB, ALU.mult, ["tmpS", "f4"], ["tmpS"])
            step_plain(si_gd, so_gd, 64, qB, kB, u_sb, sc=egB)
            po, pok = unselect(o_sb, "tmpS")
            head_norm(po[0:n, 0:256], pok, slice(256, 512), slice(256, 512))

            conv_tok(1, [(2056, 2568, 0), (2568, 2824, 512)], si_sc, so_sc, True)
            TT("dve", v3(f2[0:n, 0:256], 4), v3(f1[0:n, 0:256], 4), bc_l(dtb[0:n, 4:8], 64), ALU.mult, ["f1", "dtb"], ["f2"])
            select([(0, 128, lambda hd: (f1[0:n, 512 + (hd // 2) * 128:512 + (hd // 2) * 128 + 128], "f1")),
                    (128, 128, lambda hd: (f1[0:n, 256 + (hd // 2) * 128:256 + (hd // 2) * 128 + 128], "f1")),
                    (256, 64, lambda hd: (f2[0:n, hd * 64:hd * 64 + 64], "f2")),
                    (320, 1, lambda hd: (egc[0:n, 4 + hd:5 + hd], "egc"))])
            step_plain(si_sd, so_sd, 128, pvs[0:64, 0:128], pvs[0:64, 128:256], pvs[0:64, 256:320], sc=pvs[0:64, 320:321])
            po, pok = unselect(o_sb, "tmpS")
            TT("dve", v3(f3[0:n, 0:256], 4), v3(f1[0:n, 0:256], 4), bc_l(prm[0:n, 1296:1300], 64), ALU.mult, ["f1", "prm"], ["f3"])
            TT("dve", f3[0:n, 0:256], f3[0:n, 0:256], po[0:n, 0:256], ALU.add, ["f3", pok], ["f3"])
            TT("dve", f3[0:n, 0:256], f3[0:n, 0:256], gs[0:n, 512:768], ALU.mult, ["f3", gsk], ["f3"])
            ACT(f3[0:n, 256:512], f3[0:n, 0:256], AF.Square, ["f3"], ["f3"])
            RED(st4[0:n, 4:6], v3(f3[0:n, 256:512], 2), ["f3"], ["st4"])
            rsqrt_act(st4[0:n, 8:10], st4[0:n, 4:6], 1.0 / 128, ["st4"], ["st4"])
            TT("dve", v3(f3[0:n, 0:256], 2), v3(f3[0:n, 0:256], 2), bc_l(st4[0:n, 8:10], 128), ALU.mult, ["f3", "st4"], ["f3"])
            TT("dve", y_bf[0:n, 512:768], f3[0:n, 0:256], prm[0:n, 512:768], ALU.mult, ["f3", "prm"], ["y_bf"])

            for half in range(2):
                pt, pk = bank()
                for kk in range(4):
                    kc = half * 4 + kk
                    MM(pt[:, kk * 128:kk * 128 + n], y_bf[0:n, kc * 128:(kc + 1) * 128], ident_bf[0:n, 0:n],
                       ["y_bf", "ident_bf"], [pk])
                CP("act", yT[:, half * 4:half * 4 + 4, 0:n], v3(pt[:, :], 4)[:, :, 0:n], [pk], ["yT"])
            for cg in range(2):
                pt, pk = bank()
                for kc in range(8):
                    MM(pt[0:n, :], yT[:, kc, 0:n], wout[:, kc, cg * 512:(cg + 1) * 512], ["yT", "wout"], [pk],
                       start=(kc == 0), stop=(kc == 7))
                TT("dve", hs[:, cg * 512:(cg + 1) * 512], hs[:, cg * 512:(cg + 1) * 512], pt[0:n, :], ALU.add, ["hs", pk], ["hs"])
            if last:
                ACT(hn_bf[0:n, :], hs[:, :], AF.Square, ["hs"], ["hn_bf", "st4"], accum=st4[0:n, 0:1])
                rsqrt_act(st4[0:n, 1:2], st4[0:n, 0:1], 1.0 / D, ["st4"], ["st4"])
                STT(hs[:, :], hs[:, :], st4[0:n, 1:2], finw[0:n, :], ALU.mult, ALU.mult, ["hs", "st4", "lgt"], ["hs"])
                DMA(y_s, hs[:, :], ["hs"], [])

        for l in range(depth):
            load_layer(l)
            if not _os0.environ.get("NO_SAMPLE"):
                sample_fwd(l, l == depth - 1)
            if ntiles > 0:
                stage0(l, 0)
                stage0_pe(l, 0)
            for t in range(ntiles):
                more = t + 1 < ntiles
                tile_fwd(l, t, (lambda l_=l, t_=t: stage0(l_, t_ + 1)) if more else None,
                         (lambda l_=l, t_=t: stage0_pe(l_, t_ + 1)) if more else None)

        if _AUDIT:
            for b_ in sorted(_bad, key=str):
                print("AUDIT missing key:", b_)
        P.emit(es)
    return nc


_NC_CACHE = {}


def _prep_inputs(inp, c):
    f = np.float32
    prm = np.zeros((DEPTH, 128, NPRM), f)
    for l in range(DEPTH):
        row = np.concatenate([inp["hgrn_norm_w"][l], inp["gdn_norm_w"][l], inp["ssd_norm_w"][l], inp["ret_norm_w"][l],
                              inp["ret_norm_b"][l], inp["gdn_a_log"][l], inp["ssd_a_log"][l], inp["gdn_dt_bias"][l],
                              inp["ssd_dt_bias"][l], inp["ssd_d"][l]]).astype(f)
        prm[l] = np.broadcast_to(row[None, :], (128, NPRM))
    lgt = np.ascontiguousarray(np.broadcast_to(inp["hgrn_lb_logits"].reshape(1, 1024), (128, 1024))).astype(f)
    finw = np.ascontiguousarray(np.broadcast_to(inp["final_norm_w"].reshape(1, 1024), (128, 1024))).astype(f)
    featp = np.zeros((128, 32 + 192), f)
    featp[:, 0:32] = inp["norm_w"].reshape(DEPTH, 8, 128).transpose(2, 0, 1).reshape(128, 32)
    for l in range(DEPTH):
        for cv, key in enumerate(("gdn_conv_w", "ssd_conv_w")):
            w = inp[key][l].reshape(4, 6, 128)
            featp[:, 32 + l * 48 + cv * 24:32 + l * 48 + cv * 24 + 24] = w.transpose(2, 1, 0).reshape(128, 24)
    cbias = np.ascontiguousarray(inp["ssd_conv_b"].reshape(1, DEPTH * 768)).astype(f)
    cw = np.stack([inp["gdn_conv_w"], inp["ssd_conv_w"]], 1).astype(f)
    cwrow = np.ascontiguousarray(np.broadcast_to(cw[:, :, :, None, :], (DEPTH, 2, 4, NS, 768)))
    cbrow = np.ascontiguousarray(np.broadcast_to(inp["ssd_conv_b"].astype(f)[:, None, :], (DEPTH, NS, 768)))
    return {
        "xp": np.ascontiguousarray(inp["x_prompt"][c]).astype(f),
        "meta": np.ascontiguousarray(inp["meta_tokens"]).astype(f),
        "w_in": np.ascontiguousarray(inp["w_in"]).astype(f),
        "w_out": np.ascontiguousarray(inp["w_out"]).astype(f),
        "cst": CST, "rot": ROT, "prm": prm, "lgt": lgt, "finw": finw, "featp": featp, "cbias": cbias,
        "cwrow": cwrow, "cbrow": cbrow, **_sample_inputs(inp, c),
    }


def _sample_inputs(inp, c):
    f = np.float32
    sl = slice(c * NS, (c + 1) * NS)
    return {
        "xs_in": np.ascontiguousarray(inp["x_sample"][sl, 0, :]).astype(f),
        "si_hg": np.ascontiguousarray(inp["state_hgrn"][:, sl]).astype(f),
        "si_gd": np.ascontiguousarray(inp["state_gdn"][:, sl]).astype(f),
        "si_gc": np.ascontiguousarray(inp["state_gdn_conv"][:, sl]).astype(f),
        "si_sd": np.ascontiguousarray(inp["state_ssd"][:, sl]).astype(f),
        "si_sc": np.ascontiguousarray(inp["state_ssd_conv"][:, sl]).astype(f),
        "si_rt": np.ascontiguousarray(inp["state_ret"][:, sl]).astype(f),
    }


def kernel(**inp):
    inp = {k: np.asarray(v) for k, v in inp.items()}
    if "nc" not in _NC_CACHE:
        _NC_CACHE["nc"] = build_program()
    nc = _NC_CACHE["nc"]
    shared = None
    in_maps = []
    for c in range(8):
        m = _prep_inputs(inp, c) if shared is None else dict(shared)
        if shared is None:
            shared = m
        else:
            m["xp"] = np.ascontiguousarray(inp["x_prompt"][c]).astype(np.float32)
            m.update(_sample_inputs(inp, c))
        in_maps.append(m)
    res = run_bass_kernel_spmd(nc, in_maps, core_ids=list(range(8)))
    R = res.results
    y_prompt = np.stack([R[c]["y_p"] for c in range(8)], 0)
    def stk(name):
        return np.ascontiguousarray(np.stack([R[c][name] for c in range(8)], 1))
    y_sample = np.concatenate([R[c]["y_s"] for c in range(8)], 0)[:, None, :]
    def cat(name):
        return np.ascontiguousarray(np.concatenate([R[c][name] for c in range(8)], 1))
    outs = (y_prompt, np.ascontiguousarray(y_sample),
            stk("st_hg"), stk("st_gd"), stk("st_gc"), stk("st_sd"), stk("st_sc"), stk("st_rt"),
            cat("so_hg"), cat("so_gd"), cat("so_gc"), cat("so_sd"), cat("so_sc"), cat("so_rt"))
    return outs
```

```python
import contextlib
import math
import numpy as np
import concourse.bass as bass
import concourse.mybir as mybir
from concourse.bass_utils import run_bass_kernel_spmd

F32 = mybir.dt.float32
BF16 = mybir.dt.bfloat16
AF = mybir.ActivationFunctionType
ALU = mybir.AluOpType
AX = mybir.AxisListType

D = 1024
DEPTH = 4
SEQ = 2048
NT = 17
IN_DIM = 4108
EPS = 1e-6
QK = 0.125
NS = 16
NPRM = 1300
NEGV = -30000.0
import os as _os0
EMBED_WAIT = not _os0.environ.get("NO_EMBED")
ANNOTATE = bool(_os0.environ.get("ANNOTATE"))


class Prog:
    ENG = ("pe", "act", "dve", "pool", "sp")

    def __init__(self, nc, n_dma_sems=8):
        self.nc = nc
        self.ops = []
        self.cnt = {}
        self.clock = {e: {} for e in self.ENG}
        self.tok_clock = {}
        self.last_w = {}
        self.readers = {}
        self.n_dma = n_dma_sems
        self.dma_rr = {e: 0 for e in self.ENG}
        self.dma_last = {}
        self.anns = []
        import os
        self.pe_skip = not os.environ.get("PE_SELFWAIT")
        self.strict_same = not os.environ.get("RELAX_SAME")

    def _need(self, eng, tok, waits, force=False):
        key, idx = tok
        if key == "pe" and eng == "pe" and self.pe_skip and not force:
            return
        if self.clock[eng].get(key, 0) >= idx:
            return
        if waits.get(key, 0) < idx:
            waits[key] = idx

    def op(self, eng, fn, reads=(), writes=(), dma=False, pe_serial=False):
        waits = {}
        for b in reads:
            t = self.last_w.get(b)
            if t:
                self._need(eng, t, waits)
            if b.startswith("ps"):
                for r in self.readers.get(b, ()):
                    if r[0] != eng:
                        self._need(eng, r, waits)
        for b in writes:
            t = self.last_w.get(b)
            if t and (t[0] != eng or pe_serial or dma or self.strict_same):
                self._need(eng, t, waits, force=pe_serial)
            for r in self.readers.get(b, ()):
                if r[0] != eng or dma or self.strict_same:
                    self._need(eng, r, waits)
        if dma:
            key = ("dma", eng, self.dma_rr[eng] % self.n_dma)
            self.dma_rr[eng] += 1
            prev = self.dma_last.get(key)
            if prev:
                self._need(eng, prev, waits)
        else:
            key = eng
        ck = self.clock[eng]
        for kk, ii in waits.items():
            for k2, i2 in self.tok_clock.get((kk, ii), {}).items():
                if ck.get(k2, 0) < i2:
                    ck[k2] = i2
            if ck.get(kk, 0) < ii:
                ck[kk] = ii
        self.cnt[key] = self.cnt.get(key, 0) + 1
        tok = (key, self.cnt[key])
        if dma:
            self.dma_last[key] = tok
            snap = dict(ck)
            snap[key] = tok[1]
            self.tok_clock[tok] = snap
        else:
            snap = dict(ck)
            snap[key] = tok[1]
            self.tok_clock[tok] = snap
        for b in writes:
            self.last_w[b] = tok
            self.readers[b] = []
        for b in reads:
            if b not in writes:
                self.readers.setdefault(b, []).append(tok)
        ann = None
        if ANNOTATE:
            import sys as _sys
            f = _sys._getframe(1)
            while f:
                if f.f_code.co_name in ("tile_fwd", "sample_fwd", "load_layer"):
                    ann = "L%d" % f.f_lineno
                    break
                f = f.f_back
        self.anns.append(ann)
        self.ops.append((eng, fn, list(waits.items()), tok, dma))
        return tok

    def emit(self, es, final_wait_eng="sp"):
        nc = self.nc
        import os
        km = int(os.environ.get("KMAX", "0"))
        if km:
            self.ops = self.ops[:km]
            self.dma_last = {}
            for (e_, f_, w_, tok_, d_) in self.ops:
                if d_:
                    self.dma_last[tok_[0]] = tok_
        needed = set()
        for (_, _, waits, _, _) in self.ops:
            for w in waits:
                needed.add(w)
        finals = []
        for k, t in self.dma_last.items():
            finals.append(t)
            needed.add(t)
        per_key = {}
        for (k, i) in needed:
            per_key.setdefault(k, []).append(i)
        sigcount = {}
        for k, lst in per_key.items():
            for n, i in enumerate(sorted(lst)):
                sigcount[(k, i)] = n + 1
        sems = {}
        for k in sorted(per_key.keys(), key=str):
            nm = "s_" + "_".join(str(x) for x in (k if isinstance(k, tuple) else (k,)))
            sems[k] = es.enter_context(nc.semaphore(nm))
        per_eng = {e: [] for e in self.ENG}
        for j, o in enumerate(self.ops):
            per_eng[o[0]].append(o + (self.anns[j] if j < len(self.anns) else None,))
        blk = es.enter_context(nc.Block())

        def run(e, engobj):
            for (_, fn, waits, tok, dma, ann) in per_eng[e]:
                emb = None
                if waits and EMBED_WAIT and not dma:
                    emb = waits[-1]
                    waits = waits[:-1]
                for (k, i) in waits:
                    mult = 16 if isinstance(k, tuple) else 1
                    engobj.wait_ge(sems[k], sigcount[(k, i)] * mult)
                ins = fn(engobj)
                if ann is not None:
                    ins.annotate(ann)
                if emb is not None:
                    k, i = emb
                    ins._wait_ge(sems[k], sigcount[(k, i)] * (16 if isinstance(k, tuple) else 1))
                if tok in sigcount:
                    ins.then_inc(sems[tok[0]], 16 if dma else 1)
            if e == final_wait_eng:
                for t in finals:
                    engobj.wait_ge(sems[t[0]], sigcount[t] * 16)

        @blk.tensor
        def _(e):
            run("pe", e)

        @blk.scalar
        def _(e):
            run("act", e)

        @blk.vector
        def _(e):
            run("dve", e)

        @blk.gpsimd
        def _(e):
            run("pool", e)

        @blk.sync
        def _(e):
            run("sp", e)


def host_consts():
    idx = np.arange(128)
    ch = idx // 64
    same = ch[:, None] == ch[None, :]
    ident = np.eye(128, dtype=np.float32)
    maskT = (same & (idx[:, None] <= idx[None, :])).astype(np.float32)
    negT = np.where(maskT > 0, 0.0, NEGV).astype(np.float32)
    strict = (same & (idx[None, :] < idx[:, None]))
    negS = np.where(strict, 0.0, NEGV).astype(np.float32)
    mid = ch * 64 + 31
    uprime = (same & (idx[:, None] <= idx[None, :])).astype(np.float32) - \
             (same & (idx[:, None] <= mid[None, :])).astype(np.float32)
    urev = (same & (idx[:, None] > idx[None, :])).astype(np.float32)
    wc = np.zeros((128, 8), np.float32)
    wc[:, 0] = (idx <= 31)
    wc[:, 1] = (idx >= 64) & (idx <= 95)
    wc[:, 2] = (idx >= 32) & (idx <= 63)
    wc[:, 3] = (idx >= 96)
    wc[:, 4] = (idx <= 63)
    wc[:, 5] = (idx >= 64)
    blockones = same.astype(np.float32)
    lg = np.log1p(-np.exp2(-5.0 - np.arange(4, dtype=np.float64)))
    loc = idx % 64
    dt_ret = np.zeros((128, 4, 128), np.float64)
    for h in range(4):
        dt_ret[:, h, :] = np.where(maskT > 0, np.exp(lg[h] * (idx[None, :] - idx[:, None])), 0.0) * QK
    egq = np.zeros((128, 2, 128), np.float64)
    for hp in range(2):
        for hh in range(2):
            egq[hh * 64:(hh + 1) * 64, hp, :] = np.exp(lg[2 * hp + hh] * (loc[None, :] + 1))
    egrev64 = np.zeros((128, 4), np.float64)
    egrev16 = np.zeros((128, 4), np.float64)
    for h in range(4):
        egrev64[:, h] = np.exp(lg[h] * (63 - loc)) * QK
        egrev16[:, h] = np.exp(lg[h] * np.maximum(15 - idx, 0)) * QK
    egl = np.zeros((128, 2, 2), np.float64)
    for hp in range(2):
        for hh in range(2):
            egl[hh * 64:(hh + 1) * 64, hp, 0] = np.exp(lg[2 * hp + hh] * 16)
            egl[hh * 64:(hh + 1) * 64, hp, 1] = np.exp(lg[2 * hp + hh] * 64)
    sel = np.zeros((128, 4, 64), np.float32)
    selT = np.zeros((128, 4, 16), np.float32)
    gam64 = np.zeros((128, 4), np.float32)
    for h in range(4):
        for b in range(16):
            sel[b, h, h * 16 + b] = 1.0
            selT[h * 16 + b, h, b] = 1.0
            gam64[h * 16 + b, 0] = np.exp(lg[h])
    parts = [ident, maskT, negT, negS, uprime, urev, wc, blockones,
             dt_ret.reshape(128, 512), egq.reshape(128, 256), egrev64, egrev16, egl.reshape(128, 4),
             sel.reshape(128, 256), selT.reshape(128, 64), gam64]
    offs = {}
    names = ["ident", "maskT", "negT", "negS", "uprime", "urev", "wc", "blockones",
             "dt_ret", "egq", "egrev64", "egrev16", "egl", "sel", "selT", "gam64"]
    o = 0
    for nm, p in zip(names, parts):
        offs[nm] = (o, p.shape[1])
        o += p.shape[1]
    cst = np.concatenate([p.astype(np.float32) for p in parts], axis=1)
    half = 32
    inv_freq = (1.0 / (np.float32(10000.0) ** np.linspace(0.0, 1.0, half, dtype=np.float32))).astype(np.float32)
    rot = np.zeros((NT + 1, 128, 64), np.float32)
    for t in range(NT):
        pos = (np.arange(128) if t == 0 else 16 + (t - 1) * 128 + np.arange(128)).astype(np.float32)
        ang = (pos[:, None] * inv_freq[None, :]).astype(np.float32)
        rot[t, :, 0:32] = np.cos(ang)
        rot[t, :, 32:64] = np.sin(ang)
    ang = (np.full((128, 1), 16384.0, np.float32) * inv_freq[None, :]).astype(np.float32)
    rot[NT, :, 0:32] = np.cos(ang)
    rot[NT, :, 32:64] = np.sin(ang)
    return cst, offs, rot


CST, COFF, ROT = host_consts()
NCST = CST.shape[1]


def build_program(depth=DEPTH, ntiles=NT):
    nc = bass.Bass("TRN2", target_bir_lowering=False)

    def din(name, shape):
        return nc.dram_tensor(name, list(shape), F32, kind="ExternalInput").ap()

    def dout(name, shape):
        return nc.dram_tensor(name, list(shape), F32, kind="ExternalOutput").ap()

    xp = din("xp", [SEQ, D])
    meta = din("meta", [16, D])
    w_in = din("w_in", [DEPTH, D, IN_DIM])
    w_out = din("w_out", [DEPTH, D, D])
    cst_d = din("cst", [128, NCST])
    rot_d = din("rot", [NT + 1, 128, 64])
    prm_d = din("prm", [DEPTH, 128, NPRM])
    lgt_d = din("lgt", [128, 1024])
    finw_d = din("finw", [128, 1024])
    featp_d = din("featp", [128, 32 + 192])
    cbias_d = din("cbias", [1, DEPTH * 768])

    y_p = dout("y_p", [SEQ, D])
    st_hg = dout("st_hg", [DEPTH, 4, 64, 64])
    st_gd = dout("st_gd", [DEPTH, 4, 64, 64])
    st_gc = dout("st_gc", [DEPTH, 3, 768])
    st_sd = dout("st_sd", [DEPTH, 4, 128, 64])
    st_sc = dout("st_sc", [DEPTH, 3, 768])
    st_rt = dout("st_rt", [DEPTH, 4, 64, 64])
    xs_d = din("xs_in", [NS, D])
    si_hg = din("si_hg", [DEPTH, NS, 4, 64, 64]); si_gd = din("si_gd", [DEPTH, NS, 4, 64, 64])
    si_gc = din("si_gc", [DEPTH, NS, 3, 768]); si_sd = din("si_sd", [DEPTH, NS, 4, 128, 64])
    si_sc = din("si_sc", [DEPTH, NS, 3, 768]); si_rt = din("si_rt", [DEPTH, NS, 4, 64, 64])
    cwrow_d = din("cwrow", [DEPTH, 2, 4, NS, 768])
    cbrow_d = din("cbrow", [DEPTH, NS, 768])
    y_s = dout("y_s", [NS, D])
    so_hg = dout("so_hg", [DEPTH, NS, 4, 64, 64]); so_gd = dout("so_gd", [DEPTH, NS, 4, 64, 64])
    so_gc = dout("so_gc", [DEPTH, NS, 3, 768]); so_sd = dout("so_sd", [DEPTH, NS, 4, 128, 64])
    so_sc = dout("so_sc", [DEPTH, NS, 3, 768]); so_rt = dout("so_rt", [DEPTH, NS, 4, 64, 64])

    with contextlib.ExitStack() as es:
        def sb(name, shape, dt=F32):
            return es.enter_context(nc.sbuf_tensor("sb_" + name, list(shape), dt))

        P = Prog(nc)

        hscr = nc.dram_tensor("hscr", [NT * 128, D], F32, kind="Internal").ap()
        hb = [sb(f"hb{i}", [128, D]) for i in range(2)]
        win = sb("win", [128, 8, IN_DIM], BF16)
        wout = sb("wout", [128, 8, D], BF16)
        WCH = 1027
        wst = [sb(f"wst{i}", [128, WCH]) for i in range(2)]
        cst = sb("cst", [128, NCST])
        ident_bf = sb("ident_bf", [128, 128], BF16)
        bones_bf = sb("bones_bf", [128, 128], BF16)
        maskT_bf = sb("maskT_bf", [128, 128], BF16)
        ghi = sb("ghi", [128, 4, 128], BF16)
        glo = sb("glo", [128, 4, 128], BF16)
        dtret_bf = sb("dtret_bf", [128, 4, 128], BF16)
        ones_bf = sb("ones_bf", [1, 128], BF16)
        prm = sb("prm", [128, NPRM])
        oml = sb("oml", [128, 4, 256])
        lgt = sb("lgt", [128, 4, 256])
        featp = sb("featp", [128, 32 + 192])
        dg = sb("dg", [128, 12, 4, 128], BF16)
        cbias_bf = sb("cbias_bf", [1, 768], BF16)
        nega = sb("nega", [128, 64])
        rot = [sb(f"rot{i}", [128, 64]) for i in range(2)]

        def C(name, rows=slice(0, 128), lo=0, hi=None):
            o, w = COFF[name]
            hi = w if hi is None else hi
            return cst[rows, o + lo:o + hi]

        psb = [es.enter_context(nc.psum_tensor(f"ps{i}", [128, 512], F32)) for i in range(8)]
        ps_rr = [0]

        def bank():
            i = ps_rr[0] % 4 if ps_rr[0] < 0 else (0, 1, 2, 3, 6, 7)[ps_rr[0] % 6]
            ps_rr[0] += 1
            return psb[i], f"ps{i}"

        import os as _os
        _AUDIT = bool(_os.environ.get("AUDIT"))
        _bad = set()

        def _chk(r, w, outs, ins):
            if not _AUDIT:
                return
            for grp, keys, what in ((outs, list(w), "W"), (ins, list(r) + list(w), "R")):
                for ap in grp:
                    nm = getattr(ap, "name", None)
                    if not isinstance(nm, str):
                        continue
                    key = nm[3:] if nm.startswith("sb_") else nm
                    if key not in keys:
                        import traceback
                        fr = traceback.extract_stack()[-3]
                        _bad.add((what, key, fr.lineno))

        _last_rb = {}

        def MM(out, lhsT, rhs, r, w, start=True, stop=True):
            skip = any(k in ("ps4", "ps5") for k in w)
            _chk(r, w, [out], [lhsT, rhs])
            rb = lhsT.base_partition()
            ser = False
            for k in w:
                if _last_rb.get(k, rb) != rb:
                    ser = True
                _last_rb[k] = rb
            P.op("pe", lambda e: e.matmul(out, lhsT=lhsT, rhs=rhs, start=start, stop=stop, skip_group_check=skip),
                 reads=r, writes=w, pe_serial=ser)

        def ACT(out, in_, func, r, w, scale=1.0, bias=None, accum=None):
            kw = {}
            if bias is not None:
                kw["bias"] = bias
            if accum is not None:
                kw["accum_out"] = accum
            if hasattr(bias, "name") and "eps_t" not in r:
                r = list(r) + ["eps_t"]
            _chk(r, w, [out] + ([accum] if accum is not None else []), [in_] + [x for x in (scale, bias) if hasattr(x, "name")])
            P.op("act", lambda e: e.activation(out=out, in_=in_, func=func, scale=scale, **kw), reads=r, writes=w)

        def TT(eng, out, in0, in1, op, r, w):
            _chk(r, w, [out], [in0, in1])
            P.op(eng, lambda e: e.tensor_tensor(out=out, in0=in0, in1=in1, op=op), reads=r, writes=w)

        def TS(eng, out, in0, s1, op0, r, w, s2=None, op1=None):
            _chk(r, w, [out], [in0] + [x for x in (s1, s2) if hasattr(x, "name")])
            if op1 is None:
                P.op(eng, lambda e: e.tensor_scalar(out=out, in0=in0, scalar1=s1, scalar2=None, op0=op0), reads=r, writes=w)
            else:
                P.op(eng, lambda e: e.tensor_scalar(out=out, in0=in0, scalar1=s1, scalar2=s2, op0=op0, op1=op1), reads=r, writes=w)

        def STT(out, in0, scalar, in1, op0, op1, r, w):
            _chk(r, w, [out], [in0, in1] + [x for x in (scalar,) if hasattr(x, "name")])
            P.op("dve", lambda e: e.scalar_tensor_tensor(out=out, in0=in0, scalar=scalar, in1=in1, op0=op0, op1=op1),
                 reads=r, writes=w)

        def RED(out, in_, r, w):
            _chk(r, w, [out], [in_])
            P.op("dve", lambda e: e.tensor_reduce(out=out, in_=in_, axis=AX.X, op=ALU.add), reads=r, writes=w)

        def RECIP(out, in_, r, w):
            _chk(r, w, [out], [in_])
            P.op("dve", lambda e: e.reciprocal(out=out, in_=in_), reads=r, writes=w)

        def CP(eng, out, in_, r, w):
            if eng == "act":
                ACT(out, in_, AF.Copy, r, w)
            else:
                _chk(r, w, [out], [in_])
                P.op(eng, lambda e: e.tensor_copy(out=out, in_=in_), reads=r, writes=w)

        def MEMSET(eng, ap, val, w):
            P.op(eng, lambda e: e.memset(ap, val), reads=(), writes=w)

        def DMA(out, in_, r, w, slow=False):
            if slow:
                P.op("sp", lambda e: e.dma_start(out=out, in_=in_, allow_slow_non_contiguous=True), reads=r, writes=w, dma=True)
            else:
                P.op("sp", lambda e: e.dma_start(out=out, in_=in_), reads=r, writes=w, dma=True)

        def sigmoid_from_exp(buf, key):
            ACT(buf, buf, AF.Ln, [key], [key], bias=eps_t[0:buf.shape[0], 1:2])
            ACT(buf, buf, AF.Exp, [key], [key], scale=-1.0)

        def rsqrt_act(out, in_, scale, r, w):
            ACT(out, in_, AF.Ln, r, w, scale=scale, bias=eps_t[0:out.shape[0], 0:1])
            ACT(out, out, AF.Exp, w, w, scale=-0.5)

        eps_t = sb("eps_t", [128, 2])
        MEMSET("pool", eps_t[:, 0:1], EPS, ["eps_t"])
        MEMSET("pool", eps_t[:, 1:2], 1.0, ["eps_t"])
        DMA(cst[:], cst_d, [], ["cst"])
        DMA(lgt[:].rearrange("p a b -> p (a b)"), lgt_d, [], ["lgt"])
        DMA(featp[:], featp_d, [], ["featp"])
        CP("dve", ident_bf[:], C("ident"), ["cst"], ["ident_bf"])
        CP("dve", bones_bf[:], C("blockones"), ["cst"], ["bones_bf"])
        CP("dve", maskT_bf[:], C("maskT"), ["cst"], ["maskT_bf"])
        CP("dve", dtret_bf[:].rearrange("p a b -> p (a b)"), C("dt_ret"), ["cst"], ["dtret_bf"])
        MEMSET("pool", ones_bf[:], 1.0, ["ones_bf"])
        mx = wst[1][:, 0:256]
        TT("dve", mx, lgt[:, 0, :], lgt[:, 1, :], ALU.max, ["lgt"], ["wst1"])
        TT("dve", mx, mx, lgt[:, 2, :], ALU.max, ["lgt", "wst1"], ["wst1"])
        TT("dve", mx, mx, lgt[:, 3, :], ALU.max, ["lgt", "wst1"], ["wst1"])
        TT("dve", lgt[:], lgt[:], mx.unsqueeze(1).to_broadcast([128, 4, 256]), ALU.subtract, ["lgt", "wst1"], ["lgt"])
        ACT(lgt[:], lgt[:], AF.Exp, ["lgt"], ["lgt"])
        TT("dve", mx, lgt[:, 0, :], lgt[:, 1, :], ALU.add, ["lgt"], ["wst1"])
        TT("dve", mx, mx, lgt[:, 2, :], ALU.add, ["lgt", "wst1"], ["wst1"])
        TT("dve", mx, mx, lgt[:, 3, :], ALU.add, ["lgt", "wst1"], ["wst1"])
        RECIP(mx, mx, ["wst1"], ["wst1"])
        TT("dve", lgt[:], lgt[:], mx.unsqueeze(1).to_broadcast([128, 4, 256]), ALU.mult, ["lgt", "wst1"], ["lgt"])
        MEMSET("dve", oml[:, 0, :], 0.0, ["oml"])
        CP("dve", oml[:, 1, :], lgt[:, 1, :], ["lgt"], ["oml"])
        TT("dve", oml[:, 2, :], oml[:, 1, :], lgt[:, 2, :], ALU.add, ["lgt", "oml"], ["oml"])
        TT("dve", oml[:, 3, :], oml[:, 2, :], lgt[:, 3, :], ALU.add, ["lgt", "oml"], ["oml"])
        TS("dve", oml[:], oml[:], 0.0, ALU.max, ["oml"], ["oml"])
        TS("dve", oml[:], oml[:], -1.0, ALU.mult, ["oml"], ["oml"], s2=1.0, op1=ALU.add)
        DMA(lgt[:].rearrange("p a b -> p (a b)"), finw_d, ["lgt"], ["lgt"])
        finw = lgt[:].rearrange("p a b -> p (a b)")

        hn_bf = sb("hn_bf", [128, D], BF16)
        hnTs = [sb(f"hnT{i}", [128, 8, 128], BF16) for i in range(2)]
        hnT = hnTs[1]
        st0 = sb("st0", [128, 2])
        st4 = sb("st4", [128, 16])
        f1 = sb("f1", [128, 768])
        f2 = sb("f2", [128, 512])
        f3 = sb("f3", [128, 512])
        f4 = sb("f4", [128, 512])
        gate = sb("gate", [128, D])
        y_bf = sb("y_bf", [128, D], BF16)
        yT = sb("yT", [128, 8, 128], BF16)
        b1 = sb("b1", [128, 4, 128], BF16)
        qkT = sb("qkT", [128, 4, 128], BF16)
        AT = sb("AT", [128, 4, 128], BF16)
        AT2 = sb("AT2", [128, 4, 128], BF16)
        v_bf = sb("v_bf", [128, 4, 64], BF16)
        kp_bf = sb("kp_bf", [128, 4, 128], BF16)
        qpT = sb("qpT", [128, 4, 128], BF16)
        ecs = sb("ecs", [128, 2, 8])
        S_A = sb("S_A", [128, 2, 64]); Sb_A = sb("Sb_A", [128, 2, 64], BF16); Sd_A = sb("Sd_A", [128, 2, 64])
        S_B = sb("S_B", [128, 2, 64]); Sb_B = sb("Sb_B", [128, 2, 64], BF16)
        S_C = sb("S_C", [128, 4, 64]); Sb_C = sb("Sb_C", [128, 4, 64], BF16)
        S_D = sb("S_D", [128, 2, 64]); Sb_D = sb("Sb_D", [128, 2, 64], BF16)
        tmpS = sb("tmpS", [128, 4, 64])
        uT = [sb(f"uT{i}", [128, 12, 131], BF16) for i in range(2)]
        cvst = sb("cvst", [128, 12, 3])
        xs = sb("xs", [128, 12, 128], BF16)
        xsf = sb("xsf", [128, 4, 128])
        g8 = sb("g8", [128, 64])
        gc = sb("gc", [128, 64])
        egc = sb("egc", [128, 64])
        beta = sb("beta", [128, 64])
        dtb = sb("dtb", [128, 64])
        nb = sb("nb", [128, 64])
        nb2 = sb("nb2", [128, 64])
        for _t, _k in ((g8, "g8"), (beta, "beta"), (nega, "nega")):
            MEMSET("pool", _t[:], 0.0, [_k])
        tt = sb("tt", [128, 8, 128])
        DT = sb("DT", [128, 8, 128], BF16)
        Dst = sb("Dst", [128, 4, 128], BF16)
        eGbc = sb("eGbc", [128, 8, 128])
        eGlB = sb("eGlB", [128, 2, 2])
        X_bf = [sb(f"X_bf{i}", [128, 4, 128], BF16) for i in range(2)]
        Y_bf = [sb(f"Y_bf{i}", [128, 4, 128], BF16) for i in range(2)]
        P_bf = sb("P_bf", [128, 4, 128], BF16)
        bv = sb("bv", [128, 4, 64])
        r_bf = sb("r_bf", [128, 4, 64], BF16)
        u_bf = sb("u_bf", [128, 4, 64], BF16)
        xd_bf = sb("xd_bf", [128, 4, 64], BF16)

        def load_layer(l):
            DMA(prm[:], prm_d[l], [], ["prm"])
            DMA(wst[0][0:1, 0:768], cbias_d[0:1, l * 768:(l + 1) * 768], [], ["wst0"])
            CP("pool", cbias_bf[:], wst[0][0:1, 0:768], ["wst0"], ["cbias_bf"])
            ACT(nega[:, 0:8], prm[:, 1280:1288], AF.Exp, ["prm"], ["nega"])
            TS("dve", nega[:, 0:64], nega[:, 0:64], -1.0, ALU.mult, ["nega"], ["nega"])
            for cv in range(2):
                for blk in range(6):
                    for w in range(4):
                        col = 32 + l * 48 + cv * 24 + blk * 4 + w
                        if (blk + w) % 2 == 0:
                            ACT(dg[:, cv * 6 + blk, w, :], ident_bf[:], AF.Copy, ["ident_bf", "featp"], ["dg"],
                                scale=featp[:, col:col + 1])
                        else:
                            TS("dve", dg[:, cv * 6 + blk, w, :], ident_bf[:], featp[:, col:col + 1], ALU.mult,
                               ["ident_bf", "featp"], ["dg"])
            i = 0
            for kc in range(8):
                for c0 in range(0, IN_DIM, WCH):
                    st = wst[i % 2]; sk = f"wst{i % 2}"
                    DMA(st[:, 0:WCH], w_in[l, kc * 128:(kc + 1) * 128, c0:c0 + WCH], [], [sk])
                    if i % 2 == 0:
                        ACT(win[:, kc, c0:c0 + WCH], st[:, 0:WCH], AF.Copy, [sk, "featp"], ["win"],
                            scale=featp[:, l * 8 + kc:l * 8 + kc + 1])
                    else:
                        TS("dve", win[:, kc, c0:c0 + WCH], st[:, 0:WCH], featp[:, l * 8 + kc:l * 8 + kc + 1], ALU.mult,
                           [sk, "featp"], ["win"])
                    i += 1
            for kc in range(8):
                st = wst[i % 2]; sk = f"wst{i % 2}"
                DMA(st[:, 0:1024], w_out[l, kc * 128:(kc + 1) * 128, :], [], [sk])
                CP("act" if i % 2 == 0 else "dve", wout[:, kc, :], st[:, 0:1024], [sk], ["wout"])
                i += 1
            for nm, S_, Sb_ in (("A", S_A, Sb_A), ("B", S_B, Sb_B), ("C", S_C, Sb_C), ("D", S_D, Sb_D)):
                MEMSET("pool", S_[:], 0.0, ["S_" + nm])
                MEMSET("pool", Sb_[:], 0.0, ["Sb_" + nm])
            MEMSET("pool", uT[0][:, :, 0:3], 0.0, ["uT0"])

        def stage0(l, t):
            n = 16 if t == 0 else 128
            hk = f"hb{t % 2}"
            ht = hb[t % 2][0:n, :]
            hnT = hnTs[t % 2]; hnTk = f"hnT{t % 2}"
            if l == 0:
                DMA(ht, meta if t == 0 else xp[(t - 1) * 128:t * 128, :], [], [hk])
            else:
                DMA(ht, hscr[t * 128:t * 128 + n, :], [f"hd{t}"], [hk])
            ACT(hn_bf[0:n, :], ht, AF.Square, [hk], ["hn_bf", "st0"], accum=st0[0:n, 0:1])
            ACT(st0[0:n, 1:2], st0[0:n, 0:1], AF.Ln, ["st0"], ["st0"], scale=1.0 / D, bias=eps_t[0:n, 0:1])
            ACT(st0[0:n, 1:2], st0[0:n, 1:2], AF.Exp, ["st0"], ["st0"], scale=-0.5)
            ACT(hn_bf[0:n, :], ht, AF.Copy, [hk, "st0"], ["hn_bf"], scale=st0[0:n, 1:2])

        def stage0_pe(l, t):
            n = 16 if t == 0 else 128
            hnT = hnTs[t % 2]; hnTk = f"hnT{t % 2}"
            for half in range(2):
                pt, pk = bank()
                for kk in range(4):
                    kc = half * 4 + kk
                    MM(pt[:, kk * 128:kk * 128 + n], hn_bf[0:n, kc * 128:(kc + 1) * 128], ident_bf[0:n, 0:n],
                       ["hn_bf", "ident_bf"], [pk])
                CP("act" if half else "dve", hnT[:, half * 4:half * 4 + 4, 0:n],
                   pt[:, :].rearrange("p (a b) -> p a b", a=4)[:, :, 0:n], [pk], [hnTk])

        def tile_fwd(l, t, mid_hook=None, late_hook=None):
            n = 16 if t == 0 else 128
            chunks = [(0, 16)] if t == 0 else [(0, 64), (64, 128)]
            nch = len(chunks)
            clen = chunks[0][1]
            last_tile = (t == ntiles - 1)
            hk = f"hb{t % 2}"
            ht = hb[t % 2][0:n, :]
            hnT = hnTs[t % 2]; hnTk = f"hnT{t % 2}"
            cur = uT[t % 2]; curk = f"uT{t % 2}"
            pO2, kO2 = psb[5], "ps5"
            pOC, kOC = psb[5], "ps5"
            nxt = uT[(t + 1) % 2]; nxtk = f"uT{(t + 1) % 2}"
            rt = rot[t % 2]; rtk = f"rot{t % 2}"
            DMA(rt[:], rot_d[t], [], [rtk])

            def bc_h(ap2d, nh=4):
                return ap2d.unsqueeze(1).to_broadcast([ap2d.shape[0], nh, ap2d.shape[1]])

            def bc_l(ap2d, m):
                return ap2d.unsqueeze(2).to_broadcast([ap2d.shape[0], ap2d.shape[1], m])

            def proj_tok(c0, c1, extra=None):
                pt, pk = bank()
                for kc in range(8):
                    MM(pt[0:n, 0:c1 - c0], hnT[:, kc, 0:n], win[:, kc, c0:c1], [hnTk, "win"], [pk],
                       start=(kc == 0), stop=(kc == 7 and extra is None))
                if extra is not None:
                    MM(pt[0:n, extra[0]:extra[0] + 4], C("ident", slice(0, n), 0, n), prm[0:n, extra[1]:extra[1] + 4],
                       ["cst", "prm"], [pk], start=False, stop=True)
                return pt, pk

            def proj_feat(cols0, nblk):
                pt, pk = bank()
                for b_ in range(nblk):
                    for kc in range(8):
                        MM(pt[:, b_ * 128:b_ * 128 + n], win[:, kc, cols0 + b_ * 128:cols0 + (b_ + 1) * 128], hnT[:, kc, 0:n],
                           [hnTk, "win"], [pk], start=(kc == 0), stop=(kc == 7))
                return pt, pk

            def v3(ps_ap, a):
                return ps_ap.rearrange("p (a b) -> p a b", a=a)

            pA0, kA0 = proj_tok(0, 512)
            pA1, kA1 = proj_tok(512, 1024)
            ACT(f1[0:n, 256:512], pA0[0:n, 256:512], AF.Exp, [kA0], ["f1"])
            sigmoid_from_exp(f1[0:n, 256:512], "f1")
            TT("dve", f2[0:n, 256:512], f1[0:n, 256:512], oml[0:n, l, :], ALU.mult, ["f1", "oml"], ["f2"])
            ACT(f3[0:n, 0:256], f2[0:n, 256:512], AF.Ln, ["f2"], ["f3"], scale=-1.0, bias=eps_t[0:n, 1:2])
            pG, kG = bank()
            MM(pG[0:n, 0:256], C("uprime", slice(0, n), 0, n), f3[0:n, 0:256], ["cst", "f3"], [kG])
            for hp in range(2):
                MM(pG[:, 256 + hp * 8:256 + hp * 8 + 8], f3[0:n, hp * 128:(hp + 1) * 128], C("wc", slice(0, n)),
                   ["cst", "f3"], [kG])
            ACT(f1[0:n, 0:256], pA0[0:n, 0:256], AF.Exp, [kA0], ["f1"], scale=-1.0)
            ACT(f1[0:n, 512:768], pA1[0:n, 256:512], AF.Exp, [kA1], ["f1"], scale=-1.0)
            sigmoid_from_exp(f1[0:n, 0:256], "f1")
            sigmoid_from_exp(f1[0:n, 512:768], "f1")
            STT(f2[0:n, 0:256], pA0[0:n, 0:256], QK, f1[0:n, 0:256], ALU.mult, ALU.mult, [kA0, "f1"], ["f2"])
            TT("dve", f1[0:n, 512:768], f1[0:n, 512:768], prm[0:n, 0:256], ALU.mult, ["f1", "prm"], ["f1"])
            TT("dve", gate[0:n, 0:256], pA1[0:n, 256:512], f1[0:n, 512:768], ALU.mult, [kA1, "f1"], ["gate"])
            ACT(v_bf[0:n, :, :].rearrange("p a b -> p (a b)"), pA1[0:n, 0:256], AF.Copy, [kA1], ["v_bf"])
            ACT(f3[0:n, 0:256], pG[0:n, 0:256], AF.Exp, [kG], ["f3"])
            ACT(f3[0:n, 256:512], pG[0:n, 0:256], AF.Exp, [kG], ["f3"], scale=-1.0)
            ACT(ecs[:].rearrange("p a b -> p (a b)"), pG[:, 256:272], AF.Exp, [kG], ["ecs"])
            TT("dve", b1[0:n, 0:2, :].rearrange("p a b -> p (a b)"), f2[0:n, 0:256], f3[0:n, 0:256], ALU.mult,
               ["f2", "f3"], ["b1"])
            TT("dve", b1[0:n, 2:4, :].rearrange("p a b -> p (a b)"), f2[0:n, 256:512], f3[0:n, 256:512], ALU.mult,
               ["f2", "f3"], ["b1"])
            pT, kT = bank()
            for blk in range(4):
                MM(pT[:, blk * 128:blk * 128 + n], b1[0:n, blk, :], ident_bf[0:n, 0:n], ["b1", "ident_bf"], [kT])
            CP("act", qkT[:, :, 0:n], v3(pT[:, :], 4)[:, :, 0:n], [kT], ["qkT"])
            pS, kS = bank()
            for hd in (0, 2, 1, 3):
                hp, hh = hd // 2, hd % 2
                rows = slice(hh * 64, hh * 64 + 64)
                MM(pS[0:n, hd * 128:hd * 128 + n], qkT[rows, 2 + hp, 0:n], qkT[rows, hp, 0:n], ["qkT"], [kS])
            TT("dve", AT[0:n, :, 0:n], v3(pS[0:n, :], 4)[:, :, 0:n], bc_h(C("maskT", slice(0, n), 0, n)), ALU.mult,
               [kS, "cst"], ["AT"])
            pO, kO = psb[4], "ps4"
            pOD, kOD = psb[4], "ps4"
            for hd in (0, 2, 1, 3):
                MM(pO[0:n, hd * 64:hd * 64 + 64], AT[0:n, hd, 0:n], v_bf[0:n, hd, :], ["AT", "v_bf"], [kO],
                   start=(hd == 0), stop=False)
            for ci, (c0, c1) in enumerate(chunks):
                TT("dve", Sb_A[:], S_A[:], bc_l(ecs[:, :, ci], 64), ALU.mult, ["S_A", "ecs"], ["Sb_A"])
                TT("dve", Sd_A[:], S_A[:], bc_l(ecs[:, :, 4 + ci], 64), ALU.mult, ["S_A", "ecs"], ["Sd_A"])
                pK, kK = bank()
                for hd in (0, 2, 1, 3):
                    hp, hh = hd // 2, hd % 2
                    rows = slice(hh * 64, hh * 64 + 64)
                    MM(pO[c0:c1, hd * 64:hd * 64 + 64], qkT[rows, hp, c0:c1], Sb_A[rows, hp, :], ["qkT", "Sb_A"], [kO],
                       start=False, stop=True)
                    MM(pK[rows, hp * 64:hp * 64 + 64], b1[c0:c1, 2 + hp, hh * 64:hh * 64 + 64], v_bf[c0:c1, hd, :],
                       ["b1", "v_bf"], [kK])
                TT("dve", tmpS[:, 0:2, :], v3(pK[:, 0:128], 2), bc_l(ecs[:, :, 2 + ci], 64), ALU.mult, [kK, "ecs"], ["tmpS"])
                TT("dve", S_A[:], tmpS[:, 0:2, :], Sd_A[:], ALU.add, ["tmpS", "Sd_A"], ["S_A"])

            def head_norm(ps_ap, pskey, gcols, ycols):
                ACT(f4[0:n, 0:256], ps_ap, AF.Square, [pskey], ["f4"])
                RED(st4[0:n, 4:8], v3(f4[0:n, 0:256], 4), ["f4"], ["st4"])
                rsqrt_act(st4[0:n, 8:12], st4[0:n, 4:8], 1.0 / 64, ["st4"], ["st4"])
                TT("dve", v3(f4[0:n, 0:256], 4), v3(ps_ap, 4), bc_l(st4[0:n, 8:12], 64), ALU.mult, [pskey, "st4"], ["f4"])
                TT("dve", y_bf[0:n, ycols], f4[0:n, 0:256], gate[0:n, gcols], ALU.mult, ["f4", "gate"], ["y_bf"])

            head_norm(pO[0:n, 0:256], kO, slice(0, 256), slice(0, 256))
            if mid_hook is not None:
                mid_hook()

            pD0, kD0 = proj_tok(3084, 3596)
            pD1, kD1 = proj_tok(3596, 4108)
            cosb = rt[0:n, 0:32].unsqueeze(1).to_broadcast([n, 16, 32])
            sinb = rt[0:n, 32:64].unsqueeze(1).to_broadcast([n, 16, 32])
            qk4 = pD0[0:n, :].rearrange("p (a b) -> p a b", a=16)
            TT("dve", f1[0:n, 0:512].rearrange("p (a b) -> p a b", a=16), qk4, cosb, ALU.mult, [kD0, rtk], ["f1"])
            TT("dve", f2[0:n, 0:512].rearrange("p (a b) -> p a b", a=16), qk4, sinb, ALU.mult, [kD0, rtk], ["f2"])
            c4 = f1[0:n, 0:512].rearrange("p (a s b) -> p a s b", a=8, s=2)
            s4 = f2[0:n, 0:512].rearrange("p (a s b) -> p a s b", a=8, s=2)
            qkr = b1[0:n, :, :].rearrange("p a (s b) -> p a s b", s=4)
            qkr8 = b1[0:n, :, :].rearrange("p a b -> p (a b)").rearrange("p (a s b) -> p a s b", a=8, s=2)
            TT("dve", qkr8[:, :, 0, :], c4[:, :, 0, :], s4[:, :, 1, :], ALU.subtract, ["f1", "f2"], ["b1"])
            TT("dve", qkr8[:, :, 1, :], c4[:, :, 1, :], s4[:, :, 0, :], ALU.add, ["f1", "f2"], ["b1"])
            ACT(v_bf[0:n, :, :].rearrange("p a b -> p (a b)"), pD1[0:n, 0:256], AF.Copy, [kD1], ["v_bf"])
            ACT(f1[0:n, 512:768], pD1[0:n, 256:512], AF.Exp, [kD1], ["f1"], scale=-1.0)
            sigmoid_from_exp(f1[0:n, 512:768], "f1")
            TT("dve", gate[0:n, 768:1024], pD1[0:n, 256:512], f1[0:n, 512:768], ALU.mult, [kD1, "f1"], ["gate"])
            pT, kT = bank()
            for blk in range(4):
                MM(pT[:, blk * 128:blk * 128 + n], b1[0:n, blk, :], ident_bf[0:n, 0:n], ["b1", "ident_bf"], [kT])
            CP("act", qkT[:, :, 0:n], v3(pT[:, :], 4)[:, :, 0:n], [kT], ["qkT"])
            egq = C("egq").rearrange("p (a b) -> p a b", a=2)
            TT("dve", qpT[:, 0:2, 0:n], qkT[:, 0:2, 0:n], egq[:, :, 0:n], ALU.mult, ["qkT", "cst"], ["qpT"])
            egrev = C("egrev16" if t == 0 else "egrev64", slice(0, n))
            TT("dve", kp_bf[0:n, :, 0:64], b1[0:n, 2:4, :].rearrange("p a (s b) -> p (a s) b", s=2), bc_l(egrev, 64), ALU.mult,
               ["b1", "cst"], ["kp_bf"])
            pS, kS = bank()
            for hd in (0, 2, 1, 3):
                hp, hh = hd // 2, hd % 2
                rows = slice(hh * 64, hh * 64 + 64)
                MM(pS[0:n, hd * 128:hd * 128 + n], qkT[rows, 2 + hp, 0:n], qkT[rows, hp, 0:n], ["qkT"], [kS])
            TT("dve", AT[0:n, :, 0:n], v3(pS[0:n, :], 4)[:, :, 0:n], dtret_bf[0:n, :, 0:n], ALU.mult, [kS, "dtret_bf"], ["AT"])
            for hd in (0, 2, 1, 3):
                MM(pOD[0:n, 256 + hd * 64:256 + hd * 64 + 64], AT[0:n, hd, 0:n], v_bf[0:n, hd, :], ["AT", "v_bf"], [kOD],
                   start=(hd == 0), stop=False)
            egl = C("egl").rearrange("p (a b) -> p a b", a=2)
            for ci, (c0, c1) in enumerate(chunks):
                pK, kK = bank()
                for hd in (0, 2, 1, 3):
                    hp, hh = hd // 2, hd % 2
                    rows = slice(hh * 64, hh * 64 + 64)
                    MM(pOD[c0:c1, 256 + hd * 64:256 + hd * 64 + 64], qpT[rows, hp, c0:c1], Sb_D[rows, hp, :], ["qpT", "Sb_D"], [kOD],
                       start=False, stop=True)
                    MM(pK[rows, hp * 64:hp * 64 + 64], kp_bf[c0:c1, hd, 0:64], v_bf[c0:c1, hd, :], ["kp_bf", "v_bf"], [kK])
                TT("dve", tmpS[:, 0:2, :], S_D[:], bc_l(egl[:, :, (0 if t == 0 else 1)], 64), ALU.mult, ["S_D", "cst"], ["tmpS"])
                TT("dve", S_D[:], tmpS[:, 0:2, :], v3(pK[:, 0:128], 2), ALU.add, ["tmpS", kK], ["S_D"])
                CP("act", Sb_D[:], S_D[:], ["S_D"], ["Sb_D"])
            oD = pOD[0:n, 256:512]
            kO_ = kOD
            RED(st4[0:n, 4:8], v3(oD, 4), [kO_], ["st4"])
            TS("dve", st4[0:n, 4:8], st4[0:n, 4:8], -1.0 / 64, ALU.mult, ["st4"], ["st4"])
            TT("dve", v3(f3[0:n, 0:256], 4), v3(oD, 4), bc_l(st4[0:n, 4:8], 64), ALU.add, [kO_, "st4"], ["f3"])
            ACT(f4[0:n, 0:256], f3[0:n, 0:256], AF.Square, ["f3"], ["f4"])
            RED(st4[0:n, 4:8], v3(f4[0:n, 0:256], 4), ["f4"], ["st4"])
            rsqrt_act(st4[0:n, 8:12], st4[0:n, 4:8], 1.0 / 64, ["st4"], ["st4"])
            TT("dve", v3(f3[0:n, 0:256], 4), v3(f3[0:n, 0:256], 4), bc_l(st4[0:n, 8:12], 64), ALU.mult, ["f3", "st4"], ["f3"])
            TT("dve", f3[0:n, 0:256], f3[0:n, 0:256], prm[0:n, 768:1024], ALU.mult, ["f3", "prm"], ["f3"])
            TT("dve", f3[0:n, 0:256], f3[0:n, 0:256], prm[0:n, 1024:1280], ALU.add, ["f3", "prm"], ["f3"])
            TT("dve", y_bf[0:n, 768:1024], f3[0:n, 0:256], gate[0:n, 768:1024], ALU.mult, ["f3", "gate"], ["y_bf"])

            pBz, kBz = proj_tok(1792, 2056, (256, 1288))
            pCz, kCz = proj_tok(2824, 3084, (256, 1292))
            ACT(f1[0:n, 512:768], pBz[0:n, 0:256], AF.Exp, [kBz], ["f1"], scale=-1.0)
            ACT(f2[0:n, 0:256], pCz[0:n, 0:256], AF.Exp, [kCz], ["f2"], scale=-1.0)
            ACT(beta[0:n, 0:4], pBz[0:n, 260:264], AF.Exp, [kBz], ["beta"], scale=-1.0)
            ACT(g8[0:n, 0:4], pBz[0:n, 256:260], AF.Exp, [kBz], ["g8"])
            ACT(g8[0:n, 4:8], pCz[0:n, 256:260], AF.Exp, [kCz], ["g8"])
            sigmoid_from_exp(f1[0:n, 512:768], "f1")
            sigmoid_from_exp(f2[0:n, 0:256], "f2")
            TT("dve", f1[0:n, 512:768], f1[0:n, 512:768], prm[0:n, 256:512], ALU.mult, ["f1", "prm"], ["f1"])
            TT("dve", gate[0:n, 256:512], pBz[0:n, 0:256], f1[0:n, 512:768], ALU.mult, [kBz, "f1"], ["gate"])
            TT("dve", gate[0:n, 512:768], pCz[0:n, 0:256], f2[0:n, 0:256], ALU.mult, [kCz, "f2"], ["gate"])
            for cv, cols0 in ((0, 1024), (1, 2056)):
                for part, (b0, nb_) in enumerate(((0, 4), (4, 2))):
                    pf, kf = proj_feat(cols0 + b0 * 128, nb_)
                    src = v3(pf[:, 0:nb_ * 128], nb_)[:, :, 0:n]
                    CP("act", cur[:, cv * 6 + b0:cv * 6 + b0 + nb_, 3:3 + n], src, [kf], [curk])
                    if last_tile:
                        CP("dve", cvst[:, cv * 6 + b0:cv * 6 + b0 + nb_, :], src[:, :, n - 3:n], [kf], ["cvst"])
            if not last_tile:
                CP("pool", nxt[:, :, 0:3], cur[:, :, n:n + 3], [curk], [nxtk])
            for cv in range(2):
                for part, (b0, nb_) in enumerate(((0, 4), (4, 2))):
                    pc, kc_ = bank()
                    for b_ in range(nb_):
                        blk = cv * 6 + b0 + b_
                        for w in range(4):
                            MM(pc[:, b_ * 128:b_ * 128 + n], dg[:, blk, w, :], cur[:, blk, w:w + n], ["dg", curk], [kc_],
                               start=(w == 0), stop=(w == 3 and cv == 0))
                        if cv == 1:
                            MM(pc[:, b_ * 128:b_ * 128 + n], cbias_bf[0:1, (b0 + b_) * 128:(b0 + b_ + 1) * 128], ones_bf[0:1, 0:n],
                               ["cbias_bf", "ones_bf"], [kc_], start=False, stop=True)
                    src = v3(pc[:, 0:nb_ * 128], nb_)[:, :, 0:n]
                    dstf = v3(f1[:, 0:nb_ * 128], nb_)[:, :, 0:n]
                    ACT(dstf, src, AF.Exp, [kc_], ["f1"], scale=-1.0)
                    sigmoid_from_exp(dstf, "f1")
                    if cv == 0 and part == 0:
                        TT("dve", xsf[:, :, 0:n], src, dstf, ALU.mult, [kc_, "f1"], ["xsf"])
                        ACT(b1[:, :, 0:n], xsf[:, :, 0:n], AF.Square, ["xsf"], ["b1"])
                        pN, kN = bank()
                        for blk in range(4):
                            MM(pN[:, blk * 128:blk * 128 + n], bones_bf[:], b1[:, blk, 0:n], ["bones_bf", "b1"], [kN])
                        srcN = v3(pN[:, :], 4)[:, :, 0:n]
                        dstN = v3(f2[:, 0:512], 4)[:, :, 0:n]
                        ACT(dstN, srcN, AF.Ln, [kN], ["f2"], bias=eps_t[:, 0:1])
                        ACT(dstN, dstN, AF.Exp, ["f2"], ["f2"], scale=-0.5)
                        STT(xs[:, 0:2, 0:n], xsf[:, 0:2, 0:n], QK, dstN[:, 0:2, :], ALU.mult, ALU.mult, ["xsf", "f2"], ["xs"])
                        TT("dve", xs[:, 2:4, 0:n], xsf[:, 2:4, 0:n], dstN[:, 2:4, :], ALU.mult, ["xsf", "f2"], ["xs"])
                    else:
                        TT("dve", xs[:, cv * 6 + b0:cv * 6 + b0 + nb_, 0:n], src, dstf, ALU.mult, [kc_, "f1"], ["xs"])
            ACT(g8[0:n, 0:8], g8[0:n, 0:8], AF.Ln, ["g8"], ["g8"], bias=eps_t[0:n, 1:2])
            CP("dve", dtb[0:n, :], g8[0:n, :], ["g8"], ["dtb"])
            TT("dve", g8[0:n, :], g8[0:n, :], nega[0:n, :], ALU.mult, ["g8", "nega"], ["g8"])
            sigmoid_from_exp(beta[0:n, :], "beta")
            pDc, kDc = bank()
            MM(pDc[0:n, 0:32], C("maskT", slice(0, n), 0, n), g8[0:n, 0:32], ["cst", "g8"], [kDc])
            MM(pDc[0:n, 32:64], C("urev", slice(0, n), 0, n), g8[0:n, 0:32], ["cst", "g8"], [kDc])
            CP("dve", gc[0:n, :], pDc[0:n, 0:64], [kDc], ["gc"])
            ACT(egc[0:n, :], gc[0:n, :], AF.Exp, ["gc"], ["egc"])
            for half in range(2):
                pB_, kB_ = bank()
                CP("dve", ghi[0:n, :, :], bc_l(g8[0:n, half * 4:half * 4 + 4], 128), ["g8"], ["ghi"])
                TT("dve", glo[0:n, :, :], bc_l(g8[0:n, half * 4:half * 4 + 4], 128), ghi[0:n, :, :], ALU.subtract,
                   ["g8", "ghi"], ["glo"])
                for hd in range(4):
                    MM(pB_[:, hd * 128:hd * 128 + n], ghi[0:n, hd, :], maskT_bf[0:n, 0:n], ["maskT_bf", "ghi"], [kB_],
                       start=True, stop=False)
                    MM(pB_[:, hd * 128:hd * 128 + n], glo[0:n, hd, :], maskT_bf[0:n, 0:n], ["maskT_bf", "glo"], [kB_],
                       start=False, stop=True)
                srcB = v3(pB_[:, :], 4)[:, :, 0:n]
                ACT(eGbc[:, half * 4:half * 4 + 4, 0:n], srcB, AF.Exp, [kB_], ["eGbc"])
                TT("dve", tt[0:n, half * 4:half * 4 + 4, 0:n], srcB[0:n], bc_l(gc[0:n, half * 4:half * 4 + 4], n), ALU.subtract,
                   [kB_, "gc"], ["tt"])
            if True:
                TT("dve", v3(f1[0:n, 0:512], 4)[:, :, 0:n], tt[0:n, 0:4, 0:n], bc_h(C("negS", slice(0, n), 0, n)), ALU.subtract,
                   ["tt", "cst"], ["f1"])
                ACT(Dst[0:n, :, 0:n], v3(f1[0:n, 0:512], 4)[:, :, 0:n], AF.Exp, ["f1"], ["Dst"], scale=-1.0)
                TT("dve", tt[0:n, :, 0:n], tt[0:n, :, 0:n], bc_h(C("negT", slice(0, n), 0, n), 8), ALU.add, ["tt", "cst"], ["tt"])
                ACT(DT[0:n, :, 0:n], tt[0:n, :, 0:n], AF.Exp, ["tt"], ["DT"])

            pS, kS = bank()
            pKK, kKK = bank()
            for hd in (0, 2, 1, 3):
                hp, hh = hd // 2, hd % 2
                rows = slice(hh * 64, hh * 64 + 64)
                MM(pS[0:n, hd * 128:hd * 128 + n], xs[rows, 2 + hp, 0:n], xs[rows, hp, 0:n], ["xs"], [kS])
                MM(pKK[0:n, hd * 128:hd * 128 + n], xs[rows, 2 + hp, 0:n], xs[rows, 2 + hp, 0:n], ["xs"], [kKK])
            TS("dve", nb[0:n, :], beta[0:n, :], -1.0, ALU.mult, ["beta"], ["nb"])
            TT("dve", v3(f1[0:n, 0:512], 4)[:, :, 0:n], v3(pKK[0:n, :], 4)[:, :, 0:n], Dst[0:n, :, 0:n], ALU.mult, [kKK, "Dst"], ["f1"])
            TT("dve", X_bf[0][0:n, :, 0:n], v3(f1[0:n, 0:512], 4)[:, :, 0:n], bc_l(nb[0:n, 0:4], n), ALU.mult,
               ["f1", "nb"], ["X_bf0"])
            TT("dve", AT[0:n, :, 0:n], v3(pS[0:n, :], 4)[:, :, 0:n], DT[0:n, 0:4, 0:n], ALU.mult, [kS, "DT"], ["AT"])
            def t_chain():
                pY, kY = bank()
                for hd in (0, 2, 1, 3):
                    MM(pY[0:n, hd * 128:hd * 128 + n], X_bf[0][0:n, hd, 0:n], ident_bf[0:n, 0:n], ["X_bf0", "ident_bf"], [kY])
                CP("act", Y_bf[0][0:n, :, 0:n], v3(pY[0:n, :], 4)[:, :, 0:n], [kY], ["Y_bf0"])
                TT("dve", P_bf[0:n, :, 0:n], v3(pY[0:n, :], 4)[:, :, 0:n], bc_h(C("ident", slice(0, n), 0, n)), ALU.add,
                   [kY, "cst"], ["P_bf"])
                yield
                nlev = int(math.ceil(math.log2(clen))) - 1
                ci_ = 0
                for lev in range(nlev):
                    ni_ = 1 - ci_
                    pX2, kX2 = bank()
                    for hd in (0, 2, 1, 3):
                        MM(pX2[0:n, hd * 128:hd * 128 + n], Y_bf[ci_][0:n, hd, 0:n], X_bf[ci_][0:n, hd, 0:n],
                           [f"Y_bf{ci_}", f"X_bf{ci_}"], [kX2])
                    CP("act", X_bf[ni_][0:n, :, 0:n], v3(pX2[0:n, :], 4)[:, :, 0:n], [kX2], [f"X_bf{ni_}"])
                    if lev < nlev - 1:
                        pY2, kY2 = bank()
                        for hd in (0, 2, 1, 3):
                            MM(pY2[0:n, hd * 128:hd * 128 + n], X_bf[ci_][0:n, hd, 0:n], Y_bf[ci_][0:n, hd, 0:n],
                               [f"Y_bf{ci_}", f"X_bf{ci_}"], [kY2])
                        CP("dve", Y_bf[ni_][0:n, :, 0:n], v3(pY2[0:n, :], 4)[:, :, 0:n], [kY2], [f"Y_bf{ni_}"])
                    pP, kP = bank()
                    for hd in (0, 2, 1, 3):
                        MM(pP[0:n, hd * 128:hd * 128 + n], X_bf[ni_][0:n, hd, 0:n], P_bf[0:n, hd, 0:n], [f"X_bf{ni_}", "P_bf"], [kP])
                    TT("dve", P_bf[0:n, :, 0:n], P_bf[0:n, :, 0:n], v3(pP[0:n, :], 4)[:, :, 0:n], ALU.add, ["P_bf", kP], ["P_bf"])
                    ci_ = ni_
                    yield

            tgen = t_chain()

            def tstep():
                next(tgen, None)

            tstep()

            pS, kS = bank()
            for g in range(2):
                MM(pS[0:n, g * 128:g * 128 + n], xs[:, 8 + g, 0:n], xs[:, 10 + g, 0:n], ["xs"], [kS])
            for g in range(2):
                TT("dve", AT2[0:n, 2 * g:2 * g + 2, 0:n], pS[0:n, g * 128:g * 128 + n].unsqueeze(1).to_broadcast([n, 2, n]),
                   DT[0:n, 4 + 2 * g:4 + 2 * g + 2, 0:n], ALU.mult, [kS, "DT"], ["AT2"])
            tstep()
            pT, kT = bank()
            for blk in range(4):
                MM(pT[0:n, blk * 128:(blk + 1) * 128], xs[:, 6 + blk, 0:n], ident_bf[:, :], ["xs", "ident_bf"], [kT])
            TT("dve", v_bf[0:n, :, :], v3(pT[0:n, 0:256], 4), bc_l(dtb[0:n, 4:8], 64), ALU.mult, [kT, "dtb"], ["v_bf"])
            TT("dve", xd_bf[0:n, :, :], v3(pT[0:n, 0:256], 4), bc_l(prm[0:n, 1296:1300], 64), ALU.mult, [kT, "prm"], ["xd_bf"])
            for g in range(2):
                TT("dve", kp_bf[0:n, 2 * g:2 * g + 2, :], pT[0:n, 256 + g * 128:256 + (g + 1) * 128].unsqueeze(1).to_broadcast([n, 2, 128]),
                   bc_l(egc[0:n, 36 + 2 * g:36 + 2 * g + 2], 128), ALU.mult, [kT, "egc"], ["kp_bf"])
                TT("dve", qpT[:, 2 * g:2 * g + 2, 0:n], xs[:, 10 + g, 0:n].unsqueeze(1).to_broadcast([128, 2, n]),
                   eGbc[:, 4 + 2 * g:4 + 2 * g + 2, 0:n], ALU.mult, ["xs", "eGbc"], ["qpT"])
            for hd in (0, 2, 1, 3):
                MM(pOC[0:n, 256 + hd * 64:256 + hd * 64 + 64], AT2[0:n, hd, 0:n], v_bf[0:n, hd, :], ["AT2", "v_bf"], [kOC],
                   start=(hd == 0), stop=False)
            MM(pOC[0:n, 256:512], ident_bf[0:n, 0:n], xd_bf[0:n, :, :].rearrange("p a b -> p (a b)"), ["ident_bf", "xd_bf"], [kOC],
               start=False, stop=False)
            for ci, (c0, c1) in enumerate(chunks):
                tstep()
                pK, kK = bank()
                for hd in (0, 2, 1, 3):
                    MM(pOC[c0:c1, 256 + hd * 64:256 + hd * 64 + 64], qpT[:, hd, c0:c1], Sb_C[:, hd, :], ["qpT", "Sb_C"], [kOC],
                       start=False, stop=True)
                    MM(pK[:, hd * 64:hd * 64 + 64], kp_bf[c0:c1, hd, :], v_bf[c0:c1, hd, :], ["kp_bf", "v_bf"], [kK])
                TT("dve", tmpS[:, :, :], S_C[:], eGbc[:, 4:8, c1 - 1:c1].to_broadcast([128, 4, 64]), ALU.mult, ["S_C", "eGbc"], ["tmpS"])
                TT("dve", S_C[:], tmpS[:, :, :], v3(pK[:, 0:256], 4), ALU.add, ["tmpS", kK], ["S_C"])
                CP("act", Sb_C[:], S_C[:], ["S_C"], ["Sb_C"])
            tstep()
            TT("dve", f3[0:n, 0:256], pOC[0:n, 256:512], gate[0:n, 512:768], ALU.mult, [kOC, "gate"], ["f3"])
            ACT(f4[0:n, 0:256], f3[0:n, 0:256], AF.Square, ["f3"], ["f4"])
            RED(st4[0:n, 4:6], v3(f4[0:n, 0:256], 2), ["f4"], ["st4"])
            rsqrt_act(st4[0:n, 8:10], st4[0:n, 4:6], 1.0 / 128, ["st4"], ["st4"])
            TT("dve", v3(f3[0:n, 0:256], 2), v3(f3[0:n, 0:256], 2), bc_l(st4[0:n, 8:10], 128), ALU.mult, ["f3", "st4"], ["f3"])
            TT("dve", y_bf[0:n, 512:768], f3[0:n, 0:256], prm[0:n, 512:768], ALU.mult, ["f3", "prm"], ["y_bf"])

            for _ in tgen:
                pass

            pT, kT = bank()
            for blk in range(4):
                MM(pT[0:n, blk * 128:(blk + 1) * 128], xs[:, 2 + blk, 0:n], ident_bf[:, :], ["xs", "ident_bf"], [kT])
            TT("dve", kp_bf[0:n, :, 0:64], v3(pT[0:n, 0:256], 4), bc_l(egc[0:n, 32:36], 64), ALU.mult, [kT, "egc"], ["kp_bf"])
            TT("dve", bv[0:n, :, :], v3(pT[0:n, 256:512], 4), bc_l(beta[0:n, 0:4], 64), ALU.mult, [kT, "beta"], ["bv"])
            TT("dve", nb2[0:n, :], nb[0:n, :], egc[0:n, :], ALU.mult, ["nb", "egc"], ["nb2"])
            for hh in range(2):
                rows = slice(hh * 64, hh * 64 + 64)
                TT("dve", qpT[rows, 0:2, 0:n], xs[rows, 0:2, 0:n], eGbc[rows, hh:4:2, 0:n], ALU.mult, ["xs", "eGbc"], ["qpT"])
            for ci, (c0, c1) in enumerate(chunks):
                pW, kW = bank()
                for hd in (0, 2, 1, 3):
                    hp, hh = hd // 2, hd % 2
                    rows = slice(hh * 64, hh * 64 + 64)
                    MM(pW[c0:c1, hd * 64:hd * 64 + 64], xs[rows, 2 + hp, c0:c1], Sb_B[rows, hp, :], ["xs", "Sb_B"], [kW])
                TT("dve", v3(f4[c0:c1, 0:256], 4), v3(pW[c0:c1, 0:256], 4), bc_l(nb2[c0:c1, 0:4], 64), ALU.mult,
                   [kW, "nb2"], ["f4"])
                TT("dve", r_bf[c0:c1, :, :], v3(f4[c0:c1, 0:256], 4), bv[c0:c1, :, :], ALU.add, ["f4", "bv"], ["r_bf"])
                pU, kU = bank()
                for hd in (0, 2, 1, 3):
                    MM(pU[c0:c1, hd * 64:hd * 64 + 64], P_bf[c0:c1, hd, c0:c1], r_bf[c0:c1, hd, :], ["P_bf", "r_bf"], [kU])
                CP("act", u_bf[c0:c1, :, :], v3(pU[c0:c1, 0:256], 4), [kU], ["u_bf"])
                pK, kK = bank()
                for hd in (0, 2, 1, 3):
                    hp, hh = hd // 2, hd % 2
                    rows = slice(hh * 64, hh * 64 + 64)
                    MM(pO2[c0:c1, hd * 64:hd * 64 + 64], AT[c0:c1, hd, c0:c1], u_bf[c0:c1, hd, :], ["AT", "u_bf"], [kO2],
                       start=(hd == 0), stop=False)
                    MM(pO2[c0:c1, hd * 64:hd * 64 + 64], qpT[rows, hp, c0:c1], Sb_B[rows, hp, :], ["qpT", "Sb_B"], [kO2],
                       start=False, stop=True)
                    MM(pK[rows, hp * 64:hp * 64 + 64], kp_bf[c0:c1, hd, 0:64], u_bf[c0:c1, hd, :], ["kp_bf", "u_bf"], [kK])
                for hh in range(2):
                    rows = slice(hh * 64, hh * 64 + 64)
                    TT("dve", tmpS[rows, 0:2, :], S_B[rows, :, :], eGbc[rows, hh:4:2, c1 - 1:c1].to_broadcast([64, 2, 64]), ALU.mult,
                       ["S_B", "eGbc"], ["tmpS"])
                TT("dve", S_B[:], tmpS[:, 0:2, :], v3(pK[:, 0:128], 2), ALU.add, ["tmpS", kK], ["S_B"])
                CP("act", Sb_B[:], S_B[:], ["S_B"], ["Sb_B"])
            head_norm(pO2[0:n, 0:256], kO2, slice(256, 512), slice(256, 512))

            if late_hook is not None:
                late_hook()
            for half in range(2):
                pt, pk = bank()
                for kk in range(4):
                    kc = half * 4 + kk
                    MM(pt[:, kk * 128:kk * 128 + n], y_bf[0:n, kc * 128:(kc + 1) * 128], ident_bf[0:n, 0:n],
                       ["y_bf", "ident_bf"], [pk])
                CP("act" if half else "dve", yT[:, half * 4:half * 4 + 4, 0:n], v3(pt[:, :], 4)[:, :, 0:n], [pk], ["yT"])
            for cg in range(2):
                pt, pk = bank()
                for kc in range(8):
                    MM(pt[0:n, :], yT[:, kc, 0:n], wout[:, kc, cg * 512:(cg + 1) * 512], ["yT", "wout"], [pk],
                       start=(kc == 0), stop=(kc == 7))
                TT("dve", ht[:, cg * 512:(cg + 1) * 512], ht[:, cg * 512:(cg + 1) * 512], pt[0:n, :], ALU.add, [hk, pk], [hk])

            if last_tile:
                for nm, S_, dst in (("A", S_A, st_hg), ("B", S_B, st_gd), ("D", S_D, st_rt)):
                    for hh in range(2):
                        DMA(dst[l, hh:4:2, :, :].rearrange("a k v -> k a v"), S_[hh * 64:(hh + 1) * 64, :, :], ["S_" + nm], [])
                DMA(st_sd[l].rearrange("a k v -> k a v"), S_C[:, :, :], ["S_C"], [])
                for blk in range(6):
                    DMA(st_gc[l][:, blk * 128:(blk + 1) * 128].rearrange("w p -> p w"), cvst[:, blk, :], ["cvst"], [], slow=True)
                    DMA(st_sc[l][:, blk * 128:(blk + 1) * 128].rearrange("w p -> p w"), cvst[:, 6 + blk, :], ["cvst"], [], slow=True)
            if l < depth - 1:
                DMA(hscr[t * 128:t * 128 + n, :], ht, [hk], [f"hd{t}"])
            if l == depth - 1 and t > 0:
                ACT(hn_bf[0:n, :], ht, AF.Square, [hk], ["hn_bf", "st4"], accum=st4[0:n, 0:1])
                rsqrt_act(st4[0:n, 1:2], st4[0:n, 0:1], 1.0 / D, ["st4"], ["st4"])
                STT(ht, ht, st4[0:n, 1:2], finw[0:n, :], ALU.mult, ALU.mult, [hk, "st4", "lgt"], [hk])
                DMA(y_p[(t - 1) * 128:t * 128, :], ht, [hk], [])


        hs = sb("hs", [NS, D])
        DMA(hs[:, :], xs_d, [], ["hs"])

        def sample_fwd(l, last):
            n = NS
            DMA(rot[0][:], rot_d[NT], [], ["rot0"])
            rt = rot[0]; rtk = "rot0"

            def v3(ps_ap, a):
                return ps_ap.rearrange("p (a b) -> p a b", a=a)

            def bc_l(ap2d, m):
                return ap2d.unsqueeze(2).to_broadcast([ap2d.shape[0], ap2d.shape[1], m])

            ACT(hn_bf[0:n, :], hs[:, :], AF.Square, ["hs"], ["hn_bf", "st4"], accum=st4[0:n, 0:1])
            rsqrt_act(st4[0:n, 1:2], st4[0:n, 0:1], 1.0 / D, ["st4"], ["st4"])
            ACT(hn_bf[0:n, :], hs[:, :], AF.Copy, ["hs", "st4"], ["hn_bf"], scale=st4[0:n, 1:2])
            for half in range(2):
                pt, pk = bank()
                for kk in range(4):
                    kc = half * 4 + kk
                    MM(pt[:, kk * 128:kk * 128 + n], hn_bf[0:n, kc * 128:(kc + 1) * 128], ident_bf[0:n, 0:n],
                       ["hn_bf", "ident_bf"], [pk])
                CP("act", hnT[:, half * 4:half * 4 + 4, 0:n], v3(pt[:, :], 4)[:, :, 0:n], [pk], ["hnT1"])

            def proj_tok(c0, c1, extra=None):
                pt, pk = bank()
                for kc in range(8):
                    MM(pt[0:n, 0:c1 - c0], hnT[:, kc, 0:n], win[:, kc, c0:c1], ["hnT1", "win"], [pk],
                       start=(kc == 0), stop=(kc == 7 and extra is None))
                if extra is not None:
                    MM(pt[0:n, extra[0]:extra[0] + 4], C("ident", slice(0, n), 0, n), prm[0:n, extra[1]:extra[1] + 4],
                       ["cst", "prm"], [pk], start=False, stop=True)
                return pt, pk

            sel = C("sel").rearrange("p (a b) -> p a b", a=4)
            selT = C("selT").rearrange("p (a b) -> p a b", a=4)
            pvs = f4
            Sbuf = [tt, eGbc]; Skey = ["tt", "eGbc"]
            Tbuf = gate; Tkey = "gate"
            slot = [0]

            def select(fields):
                pv, pvk = bank()
                first = True
                for (c0, wd, fn) in fields:
                    for hd in range(4):
                        ap, key = fn(hd)
                        P.op("pe", (lambda o_, l_, r_, st_: (lambda e: e.matmul(o_, lhsT=l_, rhs=r_, start=st_, stop=False,
                                                                                 skip_group_check=True)))(
                            pv[0:64, c0:c0 + wd], sel[0:n, hd, :], ap, first), reads=["cst", key], writes=[pvk])
                        first = False
                wtot = max(c0 + wd for (c0, wd, _) in fields)
                CP("dve", pvs[0:64, 0:wtot], pv[0:64, 0:wtot], [pvk], ["f4"])

            def unselect(o_ap, okey):
                po, pok = bank()
                for hd in range(4):
                    MM(po[0:n, hd * 64:hd * 64 + 64], selT[0:64, hd, :], o_ap, ["cst", okey], [pok])
                return po, pok

            def state_io(st_in, st_out, K):
                ks = 16
                for k0 in range(0, K, ks):
                    yield k0, ks

            def load_slice(st_in, k0, ks):
                i = slot[0] % 2
                slot[0] += 1
                Sv = Sbuf[i][0:64, :, :].rearrange("p a b -> p (a b)")[:, 0:ks * 64].rearrange("p (k v) -> p k v", k=ks)
                for hd in range(4):
                    DMA(Sv[hd * 16:(hd + 1) * 16, :, :], st_in[l, :, hd, k0:k0 + ks, :], [], [Skey[i]])
                return Sv, Skey[i]

            def store_slice(st_out, Sv, sk, k0, ks):
                for hd in range(4):
                    DMA(st_out[l, :, hd, k0:k0 + ks, :], Sv[hd * 16:(hd + 1) * 16, :, :], [sk], [])

            o_sb = tmpS[0:64, 0, :]; w_sb = tmpS[0:64, 1, :]; op_sb = tmpS[0:64, 2, :]; u_sb = tmpS[0:64, 3, :]

            def Tview(ks):
                return Tbuf[0:64, 0:ks * 64].rearrange("p (k v) -> p k v", k=ks)

            def q_reduce(Sv, sk, q_ap, k0, ks, first, acc):
                T = Tview(ks)
                TT("dve", T, Sv, bc_l(q_ap[:, k0:k0 + ks], 64), ALU.mult, [sk, "f4"], [Tkey])
                RED(op_sb, T.rearrange("p k v -> p v k"), [Tkey], ["tmpS"])
                if first:
                    CP("dve", acc, op_sb, ["tmpS"], ["tmpS"])
                else:
                    TT("dve", acc, acc, op_sb, ALU.add, ["tmpS"], ["tmpS"])

            def step_plain(st_in, st_out, K, q_ap, k_ap, v_ap, vec_f=None, sc=None):
                for k0, ks in state_io(st_in, st_out, K):
                    Sv, sk = load_slice(st_in, k0, ks)
                    T = Tview(ks)
                    TT("dve", T, bc_l(k_ap[:, k0:k0 + ks], 64), v_ap.unsqueeze(1).to_broadcast([64, ks, 64]), ALU.mult,
                       ["f4", "tmpS"], [Tkey])
                    if vec_f is not None:
                        TT("dve", Sv, Sv, bc_l(vec_f[:, k0:k0 + ks], 64), ALU.mult, [sk, "f4"], [sk])
                        TT("dve", Sv, Sv, T, ALU.add, [sk, Tkey], [sk])
                    else:
                        STT(Sv, Sv, sc, T, ALU.mult, ALU.add, [sk, Tkey, "f4", "cst"], [sk])
                    store_slice(st_out, Sv, sk, k0, ks)
                    q_reduce(Sv, sk, q_ap, k0, ks, k0 == 0, o_sb)

            def head_norm(ps_ap, pskey, gcols, ycols):
                ACT(f3[0:n, 256:512], ps_ap, AF.Square, [pskey], ["f3"])
                RED(st4[0:n, 4:8], v3(f3[0:n, 256:512], 4), ["f3"], ["st4"])
                rsqrt_act(st4[0:n, 8:12], st4[0:n, 4:8], 1.0 / 64, ["st4"], ["st4"])
                TT("dve", v3(f3[0:n, 256:512], 4), v3(ps_ap, 4), bc_l(st4[0:n, 8:12], 64), ALU.mult, [pskey, "st4"], ["f3"])
                TT("dve", y_bf[0:n, ycols], f3[0:n, 256:512], hb[1][0:n, gcols], ALU.mult, ["f3", "hb1"], ["y_bf"])

            gs = hb[1]; gsk = "hb1"

            pA0, kA0 = proj_tok(0, 512)
            pA1, kA1 = proj_tok(512, 1024)
            ACT(f1[0:n, 0:256], pA0[0:n, 0:256], AF.Exp, [kA0], ["f1"], scale=-1.0)
            ACT(f1[0:n, 256:512], pA0[0:n, 256:512], AF.Exp, [kA0], ["f1"])
            ACT(f1[0:n, 512:768], pA1[0:n, 256:512], AF.Exp, [kA1], ["f1"], scale=-1.0)
            sigmoid_from_exp(f1[0:n, :], "f1")
            STT(f2[0:n, 0:256], pA0[0:n, 0:256], QK, f1[0:n, 0:256], ALU.mult, ALU.mult, [kA0, "f1"], ["f2"])
            TT("dve", f2[0:n, 256:512], f1[0:n, 256:512], oml[0:n, l, :], ALU.mult, ["f1", "oml"], ["f2"])
            TT("dve", f1[0:n, 512:768], f1[0:n, 512:768], prm[0:n, 0:256], ALU.mult, ["f1", "prm"], ["f1"])
            TT("dve", gs[0:n, 0:256], pA1[0:n, 256:512], f1[0:n, 512:768], ALU.mult, [kA1, "f1"], [gsk])
            CP("dve", f3[0:n, 0:256], pA1[0:n, 0:256], [kA1], ["f3"])
            TS("dve", f1[0:n, 0:256], f2[0:n, 256:512], -1.0, ALU.mult, ["f2"], ["f1"], s2=1.0, op1=ALU.add)
            select([(0, 64, lambda hd: (f2[0:n, hd * 64:hd * 64 + 64], "f2")),
                    (64, 64, lambda hd: (f2[0:n, 256 + hd * 64:256 + hd * 64 + 64], "f2")),
                    (128, 64, lambda hd: (f3[0:n, hd * 64:hd * 64 + 64], "f3")),
                    (192, 64, lambda hd: (f1[0:n, hd * 64:hd * 64 + 64], "f1"))])
            step_plain(si_hg, so_hg, 64, pvs[0:64, 0:64], pvs[0:64, 64:128], pvs[0:64, 128:192], vec_f=pvs[0:64, 192:256])
            po, pok = unselect(o_sb, "tmpS")
            head_norm(po[0:n, 0:256], pok, slice(0, 256), slice(0, 256))

            pD0, kD0 = proj_tok(3084, 3596)
            pD1, kD1 = proj_tok(3596, 4108)
            cosb = rt[0:n, 0:32].unsqueeze(1).to_broadcast([n, 16, 32])
            sinb = rt[0:n, 32:64].unsqueeze(1).to_broadcast([n, 16, 32])
            qk4 = pD0[0:n, :].rearrange("p (a b) -> p a b", a=16)
            TT("dve", f1[0:n, 0:512].rearrange("p (a b) -> p a b", a=16), qk4, cosb, ALU.mult, [kD0, rtk], ["f1"])
            TT("dve", f2[0:n, 0:512].rearrange("p (a b) -> p a b", a=16), qk4, sinb, ALU.mult, [kD0, rtk], ["f2"])
            c4 = f1[0:n, 0:512].rearrange("p (a s b) -> p a s b", a=8, s=2)
            s4 = f2[0:n, 0:512].rearrange("p (a s b) -> p a s b", a=8, s=2)
            r4 = f3[0:n, 0:512].rearrange("p (a s b) -> p a s b", a=8, s=2)
            TT("dve", r4[:, :, 0, :], c4[:, :, 0, :], s4[:, :, 1, :], ALU.subtract, ["f1", "f2"], ["f3"])
            TT("dve", r4[:, :, 1, :], c4[:, :, 1, :], s4[:, :, 0, :], ALU.add, ["f1", "f2"], ["f3"])
            TS("dve", f3[0:n, 256:512], f3[0:n, 256:512], QK, ALU.mult, ["f3"], ["f3"])
            CP("dve", f1[0:n, 0:256], pD1[0:n, 0:256], [kD1], ["f1"])
            ACT(f1[0:n, 512:768], pD1[0:n, 256:512], AF.Exp, [kD1], ["f1"], scale=-1.0)
            sigmoid_from_exp(f1[0:n, 512:768], "f1")
            TT("dve", gs[0:n, 768:1024], pD1[0:n, 256:512], f1[0:n, 512:768], ALU.mult, [kD1, "f1"], [gsk])
            select([(0, 64, lambda hd: (f3[0:n, hd * 64:hd * 64 + 64], "f3")),
                    (64, 64, lambda hd: (f3[0:n, 256 + hd * 64:256 + hd * 64 + 64], "f3")),
                    (128, 64, lambda hd: (f1[0:n, hd * 64:hd * 64 + 64], "f1"))])
            step_plain(si_rt, so_rt, 64, pvs[0:64, 0:64], pvs[0:64, 64:128], pvs[0:64, 128:192], sc=C("gam64", slice(0, 64), 0, 1))
            po, pok = unselect(o_sb, "tmpS")
            oD = po[0:n, 0:256]
            RED(st4[0:n, 4:8], v3(oD, 4), [pok], ["st4"])
            TS("dve", st4[0:n, 4:8], st4[0:n, 4:8], -1.0 / 64, ALU.mult, ["st4"], ["st4"])
            TT("dve", v3(f3[0:n, 0:256], 4), v3(oD, 4), bc_l(st4[0:n, 4:8], 64), ALU.add, [pok, "st4"], ["f3"])
            ACT(f3[0:n, 256:512], f3[0:n, 0:256], AF.Square, ["f3"], ["f3"])
            RED(st4[0:n, 4:8], v3(f3[0:n, 256:512], 4), ["f3"], ["st4"])
            rsqrt_act(st4[0:n, 8:12], st4[0:n, 4:8], 1.0 / 64, ["st4"], ["st4"])
            TT("dve", v3(f3[0:n, 0:256], 4), v3(f3[0:n, 0:256], 4), bc_l(st4[0:n, 8:12], 64), ALU.mult, ["f3", "st4"], ["f3"])
            TT("dve", f3[0:n, 0:256], f3[0:n, 0:256], prm[0:n, 768:1024], ALU.mult, ["f3", "prm"], ["f3"])
            TT("dve", f3[0:n, 0:256], f3[0:n, 0:256], prm[0:n, 1024:1280], ALU.add, ["f3", "prm"], ["f3"])
            TT("dve", y_bf[0:n, 768:1024], f3[0:n, 0:256], gs[0:n, 768:1024], ALU.mult, ["f3", gsk], ["y_bf"])

            pBz, kBz = proj_tok(1792, 2056, (256, 1288))
            ACT(f1[0:n, 512:768], pBz[0:n, 0:256], AF.Exp, [kBz], ["f1"], scale=-1.0)
            ACT(beta[0:n, 0:4], pBz[0:n, 260:264], AF.Exp, [kBz], ["beta"], scale=-1.0)
            ACT(g8[0:n, 0:4], pBz[0:n, 256:260], AF.Exp, [kBz], ["g8"])
            sigmoid_from_exp(f1[0:n, 512:768], "f1")
            TT("dve", f1[0:n, 512:768], f1[0:n, 512:768], prm[0:n, 256:512], ALU.mult, ["f1", "prm"], ["f1"])
            TT("dve", gs[0:n, 256:512], pBz[0:n, 0:256], f1[0:n, 512:768], ALU.mult, [kBz, "f1"], [gsk])
            pCz, kCz = proj_tok(2824, 3084, (256, 1292))
            ACT(f1[0:n, 512:768], pCz[0:n, 0:256], AF.Exp, [kCz], ["f1"], scale=-1.0)
            ACT(g8[0:n, 4:8], pCz[0:n, 256:260], AF.Exp, [kCz], ["g8"])
            sigmoid_from_exp(f1[0:n, 512:768], "f1")
            TT("dve", gs[0:n, 512:768], pCz[0:n, 0:256], f1[0:n, 512:768], ALU.mult, [kCz, "f1"], [gsk])
            ACT(g8[0:n, 0:8], g8[0:n, 0:8], AF.Ln, ["g8"], ["g8"], bias=eps_t[0:n, 1:2])
            CP("dve", dtb[0:n, :], g8[0:n, :], ["g8"], ["dtb"])
            TT("dve", g8[0:n, :], g8[0:n, :], nega[0:n, :], ALU.mult, ["g8", "nega"], ["g8"])
            ACT(egc[0:n, 0:8], g8[0:n, 0:8], AF.Exp, ["g8"], ["egc"])
            sigmoid_from_exp(beta[0:n, :], "beta")

            def conv_tok(cv, groups, st_in, st_out, bias):
                U = hb[0]; Uk = "hb0"
                for (c0, c1, o0) in groups:
                    pu, puk = proj_tok(c0, c1)
                    CP("dve", U[0:n, o0:o0 + (c1 - c0)], pu[0:n, 0:c1 - c0], [puk], [Uk])
                DMA(st_out[l, :, 2, :], U[0:n, 0:768], [Uk], [])
                Wt = eGbc[0:n, :, :].rearrange("p a b -> p (a b)")[:, 0:768]
                Ct = tt[0:n, :, :].rearrange("p a b -> p (a b)")[:, 0:768]
                DMA(Wt, cwrow_d[l, cv, 3], [], ["eGbc"])
                TT("dve", f1[0:n, 0:768], U[0:n, 0:768], Wt, ALU.mult, [Uk, "eGbc"], ["f1"])
                for w in range(3):
                    DMA(Ct, st_in[l, :, w, :], [], ["tt"])
                    DMA(Wt, cwrow_d[l, cv, w], [], ["eGbc"])
                    if w >= 1:
                        DMA(st_out[l, :, w - 1, :], Ct, ["tt"], [])
                    TT("dve", Wt, Ct, Wt, ALU.mult, ["tt", "eGbc"], ["eGbc"])
                    TT("dve", f1[0:n, 0:768], f1[0:n, 0:768], Wt, ALU.add, ["f1", "eGbc"], ["f1"])
                if bias:
                    DMA(Ct, cbrow_d[l], [], ["tt"])
                    TT("dve", f1[0:n, 0:768], f1[0:n, 0:768], Ct, ALU.add, ["f1", "tt"], ["f1"])
                ACT(Ct, f1[0:n, 0:768], AF.Exp, ["f1"], ["tt"], scale=-1.0)
                sigmoid_from_exp(Ct, "tt")
                TT("dve", f1[0:n, 0:768], f1[0:n, 0:768], Ct, ALU.mult, ["f1", "tt"], ["f1"])

            conv_tok(0, [(1024, 1536, 0), (1536, 1792, 512)], si_gc, so_gc, False)
            Ct = tt[0:n, :, :].rearrange("p a b -> p (a b)")[:, 0:512]
            ACT(Ct, f1[0:n, 0:512], AF.Square, ["f1"], ["tt"])
            RED(nb[0:n, 0:8], v3(Ct, 8), ["tt"], ["nb"])
            ACT(nb[0:n, 8:16], nb[0:n, 0:8], AF.Ln, ["nb"], ["nb"], bias=eps_t[0:n, 0:1])
            ACT(nb[0:n, 8:16], nb[0:n, 8:16], AF.Exp, ["nb"], ["nb"], scale=-0.5)
            TT("dve", v3(f1[0:n, 0:512], 8), v3(f1[0:n, 0:512], 8), bc_l(nb[0:n, 8:16], 64), ALU.mult, ["f1", "nb"], ["f1"])
            TS("dve", f1[0:n, 0:256], f1[0:n, 0:256], QK, ALU.mult, ["f1"], ["f1"])
            select([(0, 64, lambda hd: (f1[0:n, hd * 64:hd * 64 + 64], "f1")),
                    (64, 64, lambda hd: (f1[0:n, 256 + hd * 64:256 + hd * 64 + 64], "f1")),
                    (128, 64, lambda hd: (f1[0:n, 512 + hd * 64:512 + hd * 64 + 64], "f1")),
                    (192, 1, lambda hd: (egc[0:n, hd:hd + 1], "egc")),
                    (193, 1, lambda hd: (beta[0:n, hd:hd + 1], "beta"))])
            qB, kB, vB = pvs[0:64, 0:64], pvs[0:64, 64:128], pvs[0:64, 128:192]
            egB, btB = pvs[0:64, 192:193], pvs[0:64, 193:194]
            for k0, ks in state_io(si_gd, so_gd, 64):
                Sv, sk = load_slice(si_gd, k0, ks)
                T = Tview(ks)
                TT("dve", T, Sv, bc_l(kB[:, k0:k0 + ks], 64), ALU.mult, [sk, "f4"], [Tkey])
                RED(op_sb, T.rearrange("p k v -> p v k"), [Tkey], ["tmpS"])
                if k0 == 0:
                    CP("dve", w_sb, op_sb, ["tmpS"], ["tmpS"])
                else:
                    TT("dve", w_sb, w_sb, op_sb, ALU.add, ["tmpS"], ["tmpS"])
            TS("dve", w_sb, w_sb, egB, ALU.mult, ["tmpS", "f4"], ["tmpS"])
            TT("dve", u_sb, vB, w_sb, ALU.subtract, ["f4", "tmpS"], ["tmpS"])
            TS("dve", u_sb, u_sb, btB, ALU.mult, ["tmpS", "f4"], ["tmpS"])
            step_plain(si_gd, so_gd, 64, qB, kB, u_sb, sc=egB)
            po, pok = unselect(o_sb, "tmpS")
            head_norm(po[0:n, 0:256], pok, slice(256, 512), slice(256, 512))

            conv_tok(1, [(2056, 2568, 0), (2568, 2824, 512)], si_sc, so_sc, True)
            TT("dve", v3(f2[0:n, 0:256], 4), v3(f1[0:n, 0:256], 4), bc_l(dtb[0:n, 4:8], 64), ALU.mult, ["f1", "dtb"], ["f2"])
            select([(0, 128, lambda hd: (f1[0:n, 512 + (hd // 2) * 128:512 + (hd // 2) * 128 + 128], "f1")),
                    (128, 128, lambda hd: (f1[0:n, 256 + (hd // 2) * 128:256 + (hd // 2) * 128 + 128], "f1")),
                    (256, 64, lambda hd: (f2[0:n, hd * 64:hd * 64 + 64], "f2")),
                    (320, 1, lambda hd: (egc[0:n, 4 + hd:5 + hd], "egc"))])
            step_plain(si_sd, so_sd, 128, pvs[0:64, 0:128], pvs[0:64, 128:256], pvs[0:64, 256:320], sc=pvs[0:64, 320:321])
            po, pok = unselect(o_sb, "tmpS")
            TT("dve", v3(f3[0:n, 0:256], 4), v3(f1[0:n, 0:256], 4), bc_l(prm[0:n, 1296:1300], 64), ALU.mult, ["f1", "prm"], ["f3"])
            TT("dve", f3[0:n, 0:256], f3[0:n, 0:256], po[0:n, 0:256], ALU.add, ["f3", pok], ["f3"])
            TT("dve", f3[0:n, 0:256], f3[0:n, 0:256], gs[0:n, 512:768], ALU.mult, ["f3", gsk], ["f3"])
            ACT(f3[0:n, 256:512], f3[0:n, 0:256], AF.Square, ["f3"], ["f3"])
            RED(st4[0:n, 4:6], v3(f3[0:n, 256:512], 2), ["f3"], ["st4"])
            rsqrt_act(st4[0:n, 8:10], st4[0:n, 4:6], 1.0 / 128, ["st4"], ["st4"])
            TT("dve", v3(f3[0:n, 0:256], 2), v3(f3[0:n, 0:256], 2), bc_l(st4[0:n, 8:10], 128), ALU.mult, ["f3", "st4"], ["f3"])
            TT("dve", y_bf[0:n, 512:768], f3[0:n, 0:256], prm[0:n, 512:768], ALU.mult, ["f3", "prm"], ["y_bf"])

            for half in range(2):
                pt, pk = bank()
                for kk in range(4):
                    kc = half * 4 + kk
                    MM(pt[:, kk * 128:kk * 128 + n], y_bf[0:n, kc * 128:(kc + 1) * 128], ident_bf[0:n, 0:n],
                       ["y_bf", "ident_bf"], [pk])
                CP("act", yT[:, half * 4:half * 4 + 4, 0:n], v3(pt[:, :], 4)[:, :, 0:n], [pk], ["yT"])
            for cg in range(2):
                pt, pk = bank()
                for kc in range(8):
                    MM(pt[0:n, :], yT[:, kc, 0:n], wout[:, kc, cg * 512:(cg + 1) * 512], ["yT", "wout"], [pk],
                       start=(kc == 0), stop=(kc == 7))
                TT("dve", hs[:, cg * 512:(cg + 1) * 512], hs[:, cg * 512:(cg + 1) * 512], pt[0:n, :], ALU.add, ["hs", pk], ["hs"])
            if last:
                ACT(hn_bf[0:n, :], hs[:, :], AF.Square, ["hs"], ["hn_bf", "st4"], accum=st4[0:n, 0:1])
                rsqrt_act(st4[0:n, 1:2], st4[0:n, 0:1], 1.0 / D, ["st4"], ["st4"])
                STT(hs[:, :], hs[:, :], st4[0:n, 1:2], finw[0:n, :], ALU.mult, ALU.mult, ["hs", "st4", "lgt"], ["hs"])
                DMA(y_s, hs[:, :], ["hs"], [])

        for l in range(depth):
            load_layer(l)
            if not _os0.environ.get("NO_SAMPLE"):
                sample_fwd(l, l == depth - 1)
            if ntiles > 0:
                stage0(l, 0)
                stage0_pe(l, 0)
            for t in range(ntiles):
                more = t + 1 < ntiles
                tile_fwd(l, t, (lambda l_=l, t_=t: stage0(l_, t_ + 1)) if more else None,
                         (lambda l_=l, t_=t: stage0_pe(l_, t_ + 1)) if more else None)

        if _AUDIT:
            for b_ in sorted(_bad, key=str):
                print("AUDIT missing key:", b_)
        P.emit(es)
    return nc


_NC_CACHE = {}


def _prep_inputs(inp, c):
    f = np.float32
    prm = np.zeros((DEPTH, 128, NPRM), f)
    for l in range(DEPTH):
        row = np.concatenate([inp["hgrn_norm_w"][l], inp["gdn_norm_w"][l], inp["ssd_norm_w"][l], inp["ret_norm_w"][l],
                              inp["ret_norm_b"][l], inp["gdn_a_log"][l], inp["ssd_a_log"][l], inp["gdn_dt_bias"][l],
                              inp["ssd_dt_bias"][l], inp["ssd_d"][l]]).astype(f)
        prm[l] = np.broadcast_to(row[None, :], (128, NPRM))
    lgt = np.ascontiguousarray(np.broadcast_to(inp["hgrn_lb_logits"].reshape(1, 1024), (128, 1024))).astype(f)
    finw = np.ascontiguousarray(np.broadcast_to(inp["final_norm_w"].reshape(1, 1024), (128, 1024))).astype(f)
    featp = np.zeros((128, 32 + 192), f)
    featp[:, 0:32] = inp["norm_w"].reshape(DEPTH, 8, 128).transpose(2, 0, 1).reshape(128, 32)
    for l in range(DEPTH):
        for cv, key in enumerate(("gdn_conv_w", "ssd_conv_w")):
            w = inp[key][l].reshape(4, 6, 128)
            featp[:, 32 + l * 48 + cv * 24:32 + l * 48 + cv * 24 + 24] = w.transpose(2, 1, 0).reshape(128, 24)
    cbias = np.ascontiguousarray(inp["ssd_conv_b"].reshape(1, DEPTH * 768)).astype(f)
    cw = np.stack([inp["gdn_conv_w"], inp["ssd_conv_w"]], 1).astype(f)
    cwrow = np.ascontiguousarray(np.broadcast_to(cw[:, :, :, None, :], (DEPTH, 2, 4, NS, 768)))
    cbrow = np.ascontiguousarray(np.broadcast_to(inp["ssd_conv_b"].astype(f)[:, None, :], (DEPTH, NS, 768)))
    return {
        "xp": np.ascontiguousarray(inp["x_prompt"][c]).astype(f),
        "meta": np.ascontiguousarray(inp["meta_tokens"]).astype(f),
        "w_in": np.ascontiguousarray(inp["w_in"]).astype(f),
        "w_out": np.ascontiguousarray(inp["w_out"]).astype(f),
        "cst": CST, "rot": ROT, "prm": prm, "lgt": lgt, "finw": finw, "featp": featp, "cbias": cbias,
        "cwrow": cwrow, "cbrow": cbrow, **_sample_inputs(inp, c),
    }


def _sample_inputs(inp, c):
    f = np.float32
    sl = slice(c * NS, (c + 1) * NS)
    return {
        "xs_in": np.ascontiguousarray(inp["x_sample"][sl, 0, :]).astype(f),
        "si_hg": np.ascontiguousarray(inp["state_hgrn"][:, sl]).astype(f),
        "si_gd": np.ascontiguousarray(inp["state_gdn"][:, sl]).astype(f),
        "si_gc": np.ascontiguousarray(inp["state_gdn_conv"][:, sl]).astype(f),
        "si_sd": np.ascontiguousarray(inp["state_ssd"][:, sl]).astype(f),
        "si_sc": np.ascontiguousarray(inp["state_ssd_conv"][:, sl]).astype(f),
        "si_rt": np.ascontiguousarray(inp["state_ret"][:, sl]).astype(f),
    }


def kernel(**inp):
    inp = {k: np.asarray(v) for k, v in inp.items()}
    if "nc" not in _NC_CACHE:
        _NC_CACHE["nc"] = build_program()
    nc = _NC_CACHE["nc"]
    shared = None
    in_maps = []
    for c in range(8):
        m = _prep_inputs(inp, c) if shared is None else dict(shared)
        if shared is None:
            shared = m
        else:
            m["xp"] = np.ascontiguousarray(inp["x_prompt"][c]).astype(np.float32)
            m.update(_sample_inputs(inp, c))
        in_maps.append(m)
    res = run_bass_kernel_spmd(nc, in_maps, core_ids=list(range(8)))
    R = res.results
    y_prompt = np.stack([R[c]["y_p"] for c in range(8)], 0)
    def stk(name):
        return np.ascontiguousarray(np.stack([R[c][name] for c in range(8)], 1))
    y_sample = np.concatenate([R[c]["y_s"] for c in range(8)], 0)[:, None, :]
    def cat(name):
        return np.ascontiguousarray(np.concatenate([R[c][name] for c in range(8)], 1))
    outs = (y_prompt, np.ascontiguousarray(y_sample),
            stk("st_hg"), stk("st_gd"), stk("st_gc"), stk("st_sd"), stk("st_sc"), stk("st_rt"),
            cat("so_hg"), cat("so_gd"), cat("so_gc"), cat("so_sd"), cat("so_sc"), cat("so_rt"))
    return outs
```

```python
import contextlib
import math
import numpy as np
import concourse.bass as bass
import concourse.mybir as mybir
from concourse.bass_utils import run_bass_kernel_spmd

F32 = mybir.dt.float32
BF16 = mybir.dt.bfloat16
AF = mybir.ActivationFunctionType
ALU = mybir.AluOpType
AX = mybir.AxisListType

D = 1024
DEPTH = 4
SEQ = 2048
NT = 17
IN_DIM = 4108
EPS = 1e-6
QK = 0.125
NS = 16
NPRM = 1300
NEGV = -30000.0
import os as _os0
EMBED_WAIT = not _os0.environ.get("NO_EMBED")
ANNOTATE = bool(_os0.environ.get("ANNOTATE"))


class Prog:
    ENG = ("pe", "act", "dve", "pool", "sp")

    def __init__(self, nc, n_dma_sems=8):
        self.nc = nc
        self.ops = []
        self.cnt = {}
        self.clock = {e: {} for e in self.ENG}
        self.tok_clock = {}
        self.last_w = {}
        self.readers = {}
        self.n_dma = n_dma_sems
        self.dma_rr = {e: 0 for e in self.ENG}
        self.dma_last = {}
        self.anns = []
        import os
        self.pe_skip = not os.environ.get("PE_SELFWAIT")
        self.strict_same = not os.environ.get("RELAX_SAME")

    def _need(self, eng, tok, waits, force=False):
        key, idx = tok
        if key == "pe" and eng == "pe" and self.pe_skip and not force:
            return
        if self.clock[eng].get(key, 0) >= idx:
            return
        if waits.get(key, 0) < idx:
            waits[key] = idx

    def op(self, eng, fn, reads=(), writes=(), dma=False, pe_serial=False):
        waits = {}
        for b in reads:
            t = self.last_w.get(b)
            if t:
                self._need(eng, t, waits)
            if b.startswith("ps"):
                for r in self.readers.get(b, ()):
                    if r[0] != eng:
                        self._need(eng, r, waits)
        for b in writes:
            t = self.last_w.get(b)
            if t and (t[0] != eng or pe_serial or dma or self.strict_same):
                self._need(eng, t, waits, force=pe_serial)
            for r in self.readers.get(b, ()):
                if r[0] != eng or dma or self.strict_same:
                    self._need(eng, r, waits)
        if dma:
            key = ("dma", eng, self.dma_rr[eng] % self.n_dma)
            self.dma_rr[eng] += 1
            prev = self.dma_last.get(key)
            if prev:
                self._need(eng, prev, waits)
        else:
            key = eng
        ck = self.clock[eng]
        for kk, ii in waits.items():
            for k2, i2 in self.tok_clock.get((kk, ii), {}).items():
                if ck.get(k2, 0) < i2:
                    ck[k2] = i2
            if ck.get(kk, 0) < ii:
                ck[kk] = ii
        self.cnt[key] = self.cnt.get(key, 0) + 1
        tok = (key, self.cnt[key])
        if dma:
            self.dma_last[key] = tok
            snap = dict(ck)
            snap[key] = tok[1]
            self.tok_clock[tok] = snap
        else:
            snap = dict(ck)
            snap[key] = tok[1]
            self.tok_clock[tok] = snap
        for b in writes:
            self.last_w[b] = tok
            self.readers[b] = []
        for b in reads:
            if b not in writes:
                self.readers.setdefault(b, []).append(tok)
        ann = None
        if ANNOTATE:
            import sys as _sys
            f = _sys._getframe(1)
            while f:
                if f.f_code.co_name in ("tile_fwd", "sample_fwd", "load_layer"):
                    ann = "L%d" % f.f_lineno
                    break
                f = f.f_back
        self.anns.append(ann)
        self.ops.append((eng, fn, list(waits.items()), tok, dma))
        return tok

    def emit(self, es, final_wait_eng="sp"):
        nc = self.nc
        import os
        km = int(os.environ.get("KMAX", "0"))
        if km:
            self.ops = self.ops[:km]
            self.dma_last = {}
            for (e_, f_, w_, tok_, d_) in self.ops:
                if d_:
                    self.dma_last[tok_[0]] = tok_
        needed = set()
        for (_, _, waits, _, _) in self.ops:
            for w in waits:
                needed.add(w)
        finals = []
        for k, t in self.dma_last.items():
            finals.append(t)
            needed.add(t)
        per_key = {}
        for (k, i) in needed:
            per_key.setdefault(k, []).append(i)
        sigcount = {}
        for k, lst in per_key.items():
            for n, i in enumerate(sorted(lst)):
                sigcount[(k, i)] = n + 1
        sems = {}
        for k in sorted(per_key.keys(), key=str):
            nm = "s_" + "_".join(str(x) for x in (k if isinstance(k, tuple) else (k,)))
            sems[k] = es.enter_context(nc.semaphore(nm))
        per_eng = {e: [] for e in self.ENG}
        for j, o in enumerate(self.ops):
            per_eng[o[0]].append(o + (self.anns[j] if j < len(self.anns) else None,))
        blk = es.enter_context(nc.Block())

        def run(e, engobj):
            for (_, fn, waits, tok, dma, ann) in per_eng[e]:
                emb = None
                if waits and EMBED_WAIT and not dma:
                    emb = waits[-1]
                    waits = waits[:-1]
                for (k, i) in waits:
                    mult = 16 if isinstance(k, tuple) else 1
                    engobj.wait_ge(sems[k], sigcount[(k, i)] * mult)
                ins = fn(engobj)
                if ann is not None:
                    ins.annotate(ann)
                if emb is not None:
                    k, i = emb
                    ins._wait_ge(sems[k], sigcount[(k, i)] * (16 if isinstance(k, tuple) else 1))
                if tok in sigcount:
                    ins.then_inc(sems[tok[0]], 16 if dma else 1)
            if e == final_wait_eng:
                for t in finals:
                    engobj.wait_ge(sems[t[0]], sigcount[t] * 16)

        @blk.tensor
        def _(e):
            run("pe", e)

        @blk.scalar
        def _(e):
            run("act", e)

        @blk.vector
        def _(e):
            run("dve", e)

        @blk.gpsimd
        def _(e):
            run("pool", e)

        @blk.sync
        def _(e):
            run("sp", e)


def host_consts():
    idx = np.arange(128)
    ch = idx // 64
    same = ch[:, None] == ch[None, :]
    ident = np.eye(128, dtype=np.float32)
    maskT = (same & (idx[:, None] <= idx[None, :])).astype(np.float32)
    negT = np.where(maskT > 0, 0.0, NEGV).astype(np.float32)
    strict = (same & (idx[None, :] < idx[:, None]))
    negS = np.where(strict, 0.0, NEGV).astype(np.float32)
    mid = ch * 64 + 31
    uprime = (same & (idx[:, None] <= idx[None, :])).astype(np.float32) - \
             (same & (idx[:, None] <= mid[None, :])).astype(np.float32)
    urev = (same & (idx[:, None] > idx[None, :])).astype(np.float32)
    wc = np.zeros((128, 8), np.float32)
    wc[:, 0] = (idx <= 31)
    wc[:, 1] = (idx >= 64) & (idx <= 95)
    wc[:, 2] = (idx >= 32) & (idx <= 63)
    wc[:, 3] = (idx >= 96)
    wc[:, 4] = (idx <= 63)
    wc[:, 5] = (idx >= 64)
    blockones = same.astype(np.float32)
    lg = np.log1p(-np.exp2(-5.0 - np.arange(4, dtype=np.float64)))
    loc = idx % 64
    dt_ret = np.zeros((128, 4, 128), np.float64)
    for h in range(4):
        dt_ret[:, h, :] = np.where(maskT > 0, np.exp(lg[h] * (idx[None, :] - idx[:, None])), 0.0) * QK
    egq = np.zeros((128, 2, 128), np.float64)
    for hp in range(2):
        for hh in range(2):
            egq[hh * 64:(hh + 1) * 64, hp, :] = np.exp(lg[2 * hp + hh] * (loc[None, :] + 1))
    egrev64 = np.zeros((128, 4), np.float64)
    egrev16 = np.zeros((128, 4), np.float64)
    for h in range(4):
        egrev64[:, h] = np.exp(lg[h] * (63 - loc)) * QK
        egrev16[:, h] = np.exp(lg[h] * np.maximum(15 - idx, 0)) * QK
    egl = np.zeros((128, 2, 2), np.float64)
    for hp in range(2):
        for hh in range(2):
            egl[hh * 64:(hh + 1) * 64, hp, 0] = np.exp(lg[2 * hp + hh] * 16)
            egl[hh * 64:(hh + 1) * 64, hp, 1] = np.exp(lg[2 * hp + hh] * 64)
    sel = np.zeros((128, 4, 64), np.float32)
    selT = np.zeros((128, 4, 16), np.float32)
    gam64 = np.zeros((128, 4), np.float32)
    for h in range(4):
        for b in range(16):
            sel[b, h, h * 16 + b] = 1.0
            selT[h * 16 + b, h, b] = 1.0
            gam64[h * 16 + b, 0] = np.exp(lg[h])
    parts = [ident, maskT, negT, negS, uprime, urev, wc, blockones,
             dt_ret.reshape(128, 512), egq.reshape(128, 256), egrev64, egrev16, egl.reshape(128, 4),
             sel.reshape(128, 256), selT.reshape(128, 64), gam64]
    offs = {}
    names = ["ident", "maskT", "negT", "negS", "uprime", "urev", "wc", "blockones",
             "dt_ret", "egq", "egrev64", "egrev16", "egl", "sel", "selT", "gam64"]
    o = 0
    for nm, p in zip(names, parts):
        offs[nm] = (o, p.shape[1])
        o += p.shape[1]
    cst = np.concatenate([p.astype(np.float32) for p in parts], axis=1)
    half = 32
    inv_freq = (1.0 / (np.float32(10000.0) ** np.linspace(0.0, 1.0, half, dtype=np.float32))).astype(np.float32)
    rot = np.zeros((NT + 1, 128, 64), np.float32)
    for t in range(NT):
        pos = (np.arange(128) if t == 0 else 16 + (t - 1) * 128 + np.arange(128)).astype(np.float32)
        ang = (pos[:, None] * inv_freq[None, :]).astype(np.float32)
        rot[t, :, 0:32] = np.cos(ang)
        rot[t, :, 32:64] = np.sin(ang)
    ang = (np.full((128, 1), 16384.0, np.float32) * inv_freq[None, :]).astype(np.float32)
    rot[NT, :, 0:32] = np.cos(ang)
    rot[NT, :, 32:64] = np.sin(ang)
    return cst, offs, rot


CST, COFF, ROT = host_consts()
NCST = CST.shape[1]


def build_program(depth=DEPTH, ntiles=NT):
    nc = bass.Bass("TRN2", target_bir_lowering=False)

    def din(name, shape):
        return nc.dram_tensor(name, list(shape), F32, kind="ExternalInput").ap()

    def dout(name, shape):
        return nc.dram_tensor(name, list(shape), F32, kind="ExternalOutput").ap()

    xp = din("xp", [SEQ, D])
    meta = din("meta", [16, D])
    w_in = din("w_in", [DEPTH, D, IN_DIM])
    w_out = din("w_out", [DEPTH, D, D])
    cst_d = din("cst", [128, NCST])
    rot_d = din("rot", [NT + 1, 128, 64])
    prm_d = din("prm", [DEPTH, 128, NPRM])
    lgt_d = din("lgt", [128, 1024])
    finw_d = din("finw", [128, 1024])
    featp_d = din("featp", [128, 32 + 192])
    cbias_d = din("cbias", [1, DEPTH * 768])

    y_p = dout("y_p", [SEQ, D])
    st_hg = dout("st_hg", [DEPTH, 4, 64, 64])
    st_gd = dout("st_gd", [DEPTH, 4, 64, 64])
    st_gc = dout("st_gc", [DEPTH, 3, 768])
    st_sd = dout("st_sd", [DEPTH, 4, 128, 64])
    st_sc = dout("st_sc", [DEPTH, 3, 768])
    st_rt = dout("st_rt", [DEPTH, 4, 64, 64])
    xs_d = din("xs_in", [NS, D])
    si_hg = din("si_hg", [DEPTH, NS, 4, 64, 64]); si_gd = din("si_gd", [DEPTH, NS, 4, 64, 64])
    si_gc = din("si_gc", [DEPTH, NS, 3, 768]); si_sd = din("si_sd", [DEPTH, NS, 4, 128, 64])
    si_sc = din("si_sc", [DEPTH, NS, 3, 768]); si_rt = din("si_rt", [DEPTH, NS, 4, 64, 64])
    cwrow_d = din("cwrow", [DEPTH, 2, 4, NS, 768])
    cbrow_d = din("cbrow", [DEPTH, NS, 768])
    y_s = dout("y_s", [NS, D])
    so_hg = dout("so_hg", [DEPTH, NS, 4, 64, 64]); so_gd = dout("so_gd", [DEPTH, NS, 4, 64, 64])
    so_gc = dout("so_gc", [DEPTH, NS, 3, 768]); so_sd = dout("so_sd", [DEPTH, NS, 4, 128, 64])
    so_sc = dout("so_sc", [DEPTH, NS, 3, 768]); so_rt = dout("so_rt", [DEPTH, NS, 4, 64, 64])

    with contextlib.ExitStack() as es:
        def sb(name, shape, dt=F32):
            return es.enter_context(nc.sbuf_tensor("sb_" + name, list(shape), dt))

        P = Prog(nc)

        hscr = nc.dram_tensor("hscr", [NT * 128, D], F32, kind="Internal").ap()
        hb = [sb(f"hb{i}", [128, D]) for i in range(2)]
        win = sb("win", [128, 8, IN_DIM], BF16)
        wout = sb("wout", [128, 8, D], BF16)
        WCH = 1027
        wst = [sb(f"wst{i}", [128, WCH]) for i in range(2)]
        cst = sb("cst", [128, NCST])
        ident_bf = sb("ident_bf", [128, 128], BF16)
        bones_bf = sb("bones_bf", [128, 128], BF16)
        maskT_bf = sb("maskT_bf", [128, 128], BF16)
        ghi = sb("ghi", [128, 4, 128], BF16)
        glo = sb("glo", [128, 4, 128], BF16)
        dtret_bf = sb("dtret_bf", [128, 4, 128], BF16)
        ones_bf = sb("ones_bf", [1, 128], BF16)
        prm = sb("prm", [128, NPRM])
        oml = sb("oml", [128, 4, 256])
        lgt = sb("lgt", [128, 4, 256])
        featp = sb("featp", [128, 32 + 192])
        dg = sb("dg", [128, 12, 4, 128], BF16)
        cbias_bf = sb("cbias_bf", [1, 768], BF16)
        nega = sb("nega", [128, 64])
        rot = [sb(f"rot{i}", [128, 64]) for i in range(2)]

        def C(name, rows=slice(0, 128), lo=0, hi=None):
            o, w = COFF[name]
            hi = w if hi is None else hi
            return cst[rows, o + lo:o + hi]

        psb = [es.enter_context(nc.psum_tensor(f"ps{i}", [128, 512], F32)) for i in range(8)]
        ps_rr = [0]

        def bank():
            i = ps_rr[0] % 4 if ps_rr[0] < 0 else (0, 1, 2, 3, 6, 7)[ps_rr[0] % 6]
            ps_rr[0] += 1
            return psb[i], f"ps{i}"

        import os as _os
        _AUDIT = bool(_os.environ.get("AUDIT"))
        _bad = set()

        def _chk(r, w, outs, ins):
            if not _AUDIT:
                return
            for grp, keys, what in ((outs, list(w), "W"), (ins, list(r) + list(w), "R")):
                for ap in grp:
                    nm = getattr(ap, "name", None)
                    if not isinstance(nm, str):
                        continue
                    key = nm[3:] if nm.startswith("sb_") else nm
                    if key not in keys:
                        import traceback
                        fr = traceback.extract_stack()[-3]
                        _bad.add((what, key, fr.lineno))

        _last_rb = {}

        def MM(out, lhsT, rhs, r, w, start=True, stop=True):
            skip = any(k in ("ps4", "ps5") for k in w)
            _chk(r, w, [out], [lhsT, rhs])
            rb = lhsT.base_partition()
            ser = False
            for k in w:
                if _last_rb.get(k, rb) != rb:
                    ser = True
                _last_rb[k] = rb
            P.op("pe", lambda e: e.matmul(out, lhsT=lhsT, rhs=rhs, start=start, stop=stop, skip_group_check=skip),
                 reads=r, writes=w, pe_serial=ser)

        def ACT(out, in_, func, r, w, scale=1.0, bias=None, accum=None):
            kw = {}
            if bias is not None:
                kw["bias"] = bias
            if accum is not None:
                kw["accum_out"] = accum
            if hasattr(bias, "name") and "eps_t" not in r:
                r = list(r) + ["eps_t"]
            _chk(r, w, [out] + ([accum] if accum is not None else []), [in_] + [x for x in (scale, bias) if hasattr(x, "name")])
            P.op("act", lambda e: e.activation(out=out, in_=in_, func=func, scale=scale, **kw), reads=r, writes=w)

        def TT(eng, out, in0, in1, op, r, w):
            _chk(r, w, [out], [in0, in1])
            P.op(eng, lambda e: e.tensor_tensor(out=out, in0=in0, in1=in1, op=op), reads=r, writes=w)

        def TS(eng, out, in0, s1, op0, r, w, s2=None, op1=None):
            _chk(r, w, [out], [in0] + [x for x in (s1, s2) if hasattr(x, "name")])
            if op1 is None:
                P.op(eng, lambda e: e.tensor_scalar(out=out, in0=in0, scalar1=s1, scalar2=None, op0=op0), reads=r, writes=w)
            else:
                P.op(eng, lambda e: e.tensor_scalar(out=out, in0=in0, scalar1=s1, scalar2=s2, op0=op0, op1=op1), reads=r, writes=w)

        def STT(out, in0, scalar, in1, op0, op1, r, w):
            _chk(r, w, [out], [in0, in1] + [x for x in (scalar,) if hasattr(x, "name")])
            P.op("dve", lambda e: e.scalar_tensor_tensor(out=out, in0=in0, scalar=scalar, in1=in1, op0=op0, op1=op1),
                 reads=r, writes=w)

        def RED(out, in_, r, w):
            _chk(r, w, [out], [in_])
            P.op("dve", lambda e: e.tensor_reduce(out=out, in_=in_, axis=AX.X, op=ALU.add), reads=r, writes=w)

        def RECIP(out, in_, r, w):
            _chk(r, w, [out], [in_])
            P.op("dve", lambda e: e.reciprocal(out=out, in_=in_), reads=r, writes=w)

        def CP(eng, out, in_, r, w):
            if eng == "act":
                ACT(out, in_, AF.Copy, r, w)
            else:
                _chk(r, w, [out], [in_])
                P.op(eng, lambda e: e.tensor_copy(out=out, in_=in_), reads=r, writes=w)

        def MEMSET(eng, ap, val, w):
            P.op(eng, lambda e: e.memset(ap, val), reads=(), writes=w)

        def DMA(out, in_, r, w, slow=False):
            if slow:
                P.op("sp", lambda e: e.dma_start(out=out, in_=in_, allow_slow_non_contiguous=True), reads=r, writes=w, dma=True)
            else:
                P.op("sp", lambda e: e.dma_start(out=out, in_=in_), reads=r, writes=w, dma=True)

        def sigmoid_from_exp(buf, key):
            ACT(buf, buf, AF.Ln, [key], [key], bias=eps_t[0:buf.shape[0], 1:2])
            ACT(buf, buf, AF.Exp, [key], [key], scale=-1.0)

        def rsqrt_act(out, in_, scale, r, w):
            ACT(out, in_, AF.Ln, r, w, scale=scale, bias=eps_t[0:out.shape[0], 0:1])
            ACT(out, out, AF.Exp, w, w, scale=-0.5)

        eps_t = sb("eps_t", [128, 2])
        MEMSET("pool", eps_t[:, 0:1], EPS, ["eps_t"])
        MEMSET("pool", eps_t[:, 1:2], 1.0, ["eps_t"])
        DMA(cst[:], cst_d, [], ["cst"])
        DMA(lgt[:].rearrange("p a b -> p (a b)"), lgt_d, [], ["lgt"])
        DMA(featp[:], featp_d, [], ["featp"])
        CP("dve", ident_bf[:], C("ident"), ["cst"], ["ident_bf"])
        CP("dve", bones_bf[:], C("blockones"), ["cst"], ["bones_bf"])
        CP("dve", maskT_bf[:], C("maskT"), ["cst"], ["maskT_bf"])
        CP("dve", dtret_bf[:].rearrange("p a b -> p (a b)"), C("dt_ret"), ["cst"], ["dtret_bf"])
        MEMSET("pool", ones_bf[:], 1.0, ["ones_bf"])
        mx = wst[1][:, 0:256]
        TT("dve", mx, lgt[:, 0, :], lgt[:, 1, :], ALU.max, ["lgt"], ["wst1"])
        TT("dve", mx, mx, lgt[:, 2, :], ALU.max, ["lgt", "wst1"], ["wst1"])
        TT("dve", mx, mx, lgt[:, 3, :], ALU.max, ["lgt", "wst1"], ["wst1"])
        TT("dve", lgt[:], lgt[:], mx.unsqueeze(1).to_broadcast([128, 4, 256]), ALU.subtract, ["lgt", "wst1"], ["lgt"])
        ACT(lgt[:], lgt[:], AF.Exp, ["lgt"], ["lgt"])
        TT("dve", mx, lgt[:, 0, :], lgt[:, 1, :], ALU.add, ["lgt"], ["wst1"])
        TT("dve", mx, mx, lgt[:, 2, :], ALU.add, ["lgt", "wst1"], ["wst1"])
        TT("dve", mx, mx, lgt[:, 3, :], ALU.add, ["lgt", "wst1"], ["wst1"])
        RECIP(mx, mx, ["wst1"], ["wst1"])
        TT("dve", lgt[:], lgt[:], mx.unsqueeze(1).to_broadcast([128, 4, 256]), ALU.mult, ["lgt", "wst1"], ["lgt"])
        MEMSET("dve", oml[:, 0, :], 0.0, ["oml"])
        CP("dve", oml[:, 1, :], lgt[:, 1, :], ["lgt"], ["oml"])
        TT("dve", oml[:, 2, :], oml[:, 1, :], lgt[:, 2, :], ALU.add, ["lgt", "oml"], ["oml"])
        TT("dve", oml[:, 3, :], oml[:, 2, :], lgt[:, 3, :], ALU.add, ["lgt", "oml"], ["oml"])
        TS("dve", oml[:], oml[:], 0.0, ALU.max, ["oml"], ["oml"])
        TS("dve", oml[:], oml[:], -1.0, ALU.mult, ["oml"], ["oml"], s2=1.0, op1=ALU.add)
        DMA(lgt[:].rearrange("p a b -> p (a b)"), finw_d, ["lgt"], ["lgt"])
        finw = lgt[:].rearrange("p a b -> p (a b)")

        hn_bf = sb("hn_bf", [128, D], BF16)
        hnTs = [sb(f"hnT{i}", [128, 8, 128], BF16) for i in range(2)]
        hnT = hnTs[1]
        st0 = sb("st0", [128, 2])
        st4 = sb("st4", [128, 16])
        f1 = sb("f1", [128, 768])
        f2 = sb("f2", [128, 512])
        f3 = sb("f3", [128, 512])
        f4 = sb("f4", [128, 512])
        gate = sb("gate", [128, D])
        y_bf = sb("y_bf", [128, D], BF16)
        yT = sb("yT", [128, 8, 128], BF16)
        b1 = sb("b1", [128, 4, 128], BF16)
        qkT = sb("qkT", [128, 4, 128], BF16)
        AT = sb("AT", [128, 4, 128], BF16)
        AT2 = sb("AT2", [128, 4, 128], BF16)
        v_bf = sb("v_bf", [128, 4, 64], BF16)
        kp_bf = sb("kp_bf", [128, 4, 128], BF16)
        qpT = sb("qpT", [128, 4, 128], BF16)
        ecs = sb("ecs", [128, 2, 8])
        S_A = sb("S_A", [128, 2, 64]); Sb_A = sb("Sb_A", [128, 2, 64], BF16); Sd_A = sb("Sd_A", [128, 2, 64])
        S_B = sb("S_B", [128, 2, 64]); Sb_B = sb("Sb_B", [128, 2, 64], BF16)
        S_C = sb("S_C", [128, 4, 64]); Sb_C = sb("Sb_C", [128, 4, 64], BF16)
        S_D = sb("S_D", [128, 2, 64]); Sb_D = sb("Sb_D", [128, 2, 64], BF16)
        tmpS = sb("tmpS", [128, 4, 64])
        uT = [sb(f"uT{i}", [128, 12, 131], BF16) for i in range(2)]
        cvst = sb("cvst", [128, 12, 3])
        xs = sb("xs", [128, 12, 128], BF16)
        xsf = sb("xsf", [128, 4, 128])
        g8 = sb("g8", [128, 64])
        gc = sb("gc", [128, 64])
        egc = sb("egc", [128, 64])
        beta = sb("beta", [128, 64])
        dtb = sb("dtb", [128, 64])
        nb = sb("nb", [128, 64])
        nb2 = sb("nb2", [128, 64])
        for _t, _k in ((g8, "g8"), (beta, "beta"), (nega, "nega")):
            MEMSET("pool", _t[:], 0.0, [_k])
        tt = sb("tt", [128, 8, 128])
        DT = sb("DT", [128, 8, 128], BF16)
        Dst = sb("Dst", [128, 4, 128], BF16)
        eGbc = sb("eGbc", [128, 8, 128])
        eGlB = sb("eGlB", [128, 2, 2])
        X_bf = [sb(f"X_bf{i}", [128, 4, 128], BF16) for i in range(2)]
        Y_bf = [sb(f"Y_bf{i}", [128, 4, 128], BF16) for i in range(2)]
        P_bf = sb("P_bf", [128, 4, 128], BF16)
        bv = sb("bv", [128, 4, 64])
        r_bf = sb("r_bf", [128, 4, 64], BF16)
        u_bf = sb("u_bf", [128, 4, 64], BF16)
        xd_bf = sb("xd_bf", [128, 4, 64], BF16)

        def load_layer(l):
            DMA(prm[:], prm_d[l], [], ["prm"])
            DMA(wst[0][0:1, 0:768], cbias_d[0:1, l * 768:(l + 1) * 768], [], ["wst0"])
            CP("pool", cbias_bf[:], wst[0][0:1, 0:768], ["wst0"], ["cbias_bf"])
            ACT(nega[:, 0:8], prm[:, 1280:1288], AF.Exp, ["prm"], ["nega"])
            TS("dve", nega[:, 0:64], nega[:, 0:64], -1.0, ALU.mult, ["nega"], ["nega"])
            for cv in range(2):
                for blk in range(6):
                    for w in range(4):
                        col = 32 + l * 48 + cv * 24 + blk * 4 + w
                        if (blk + w) % 2 == 0:
                            ACT(dg[:, cv * 6 + blk, w, :], ident_bf[:], AF.Copy, ["ident_bf", "featp"], ["dg"],
                                scale=featp[:, col:col + 1])
                        else:
                            TS("dve", dg[:, cv * 6 + blk, w, :], ident_bf[:], featp[:, col:col + 1], ALU.mult,
                               ["ident_bf", "featp"], ["dg"])
            i = 0
            for kc in range(8):
                for c0 in range(0, IN_DIM, WCH):
                    st = wst[i % 2]; sk = f"wst{i % 2}"
                    DMA(st[:, 0:WCH], w_in[l, kc * 128:(kc + 1) * 128, c0:c0 + WCH], [], [sk])
                    if i % 2 == 0:
                        ACT(win[:, kc, c0:c0 + WCH], st[:, 0:WCH], AF.Copy, [sk, "featp"], ["win"],
                            scale=featp[:, l * 8 + kc:l * 8 + kc + 1])
                    else:
                        TS("dve", win[:, kc, c0:c0 + WCH], st[:, 0:WCH], featp[:, l * 8 + kc:l * 8 + kc + 1], ALU.mult,
                           [sk, "featp"], ["win"])
                    i += 1
            for kc in range(8):
                st = wst[i % 2]; sk = f"wst{i % 2}"
                DMA(st[:, 0:1024], w_out[l, kc * 128:(kc + 1) * 128, :], [], [sk])
                CP("act" if i % 2 == 0 else "dve", wout[:, kc, :], st[:, 0:1024], [sk], ["wout"])
                i += 1
            for nm, S_, Sb_ in (("A", S_A, Sb_A), ("B", S_B, Sb_B), ("C", S_C, Sb_C), ("D", S_D, Sb_D)):
                MEMSET("pool", S_[:], 0.0, ["S_" + nm])
                MEMSET("pool", Sb_[:], 0.0, ["Sb_" + nm])
            MEMSET("pool", uT[0][:, :, 0:3], 0.0, ["uT0"])

        def stage0(l, t):
            n = 16 if t == 0 else 128
            hk = f"hb{t % 2}"
            ht = hb[t % 2][0:n, :]
            hnT = hnTs[t % 2]; hnTk = f"hnT{t % 2}"
            if l == 0:
                DMA(ht, meta if t == 0 else xp[(t - 1) * 128:t * 128, :], [], [hk])
            else:
                DMA(ht, hscr[t * 128:t * 128 + n, :], [f"hd{t}"], [hk])
            ACT(hn_bf[0:n, :], ht, AF.Square, [hk], ["hn_bf", "st0"], accum=st0[0:n, 0:1])
            ACT(st0[0:n, 1:2], st0[0:n, 0:1], AF.Ln, ["st0"], ["st0"], scale=1.0 / D, bias=eps_t[0:n, 0:1])
            ACT(st0[0:n, 1:2], st0[0:n, 1:2], AF.Exp, ["st0"], ["st0"], scale=-0.5)
            ACT(hn_bf[0:n, :], ht, AF.Copy, [hk, "st0"], ["hn_bf"], scale=st0[0:n, 1:2])

        def stage0_pe(l, t):
            n = 16 if t == 0 else 128
            hnT = hnTs[t % 2]; hnTk = f"hnT{t % 2}"
            for half in range(2):
                pt, pk = bank()
                for kk in range(4):
                    kc = half * 4 + kk
                    MM(pt[:, kk * 128:kk * 128 + n], hn_bf[0:n, kc * 128:(kc + 1) * 128], ident_bf[0:n, 0:n],
                       ["hn_bf", "ident_bf"], [pk])
                CP("act" if half else "dve", hnT[:, half * 4:half * 4 + 4, 0:n],
                   pt[:, :].rearrange("p (a b) -> p a b", a=4)[:, :, 0:n], [pk], [hnTk])

        def tile_fwd(l, t, mid_hook=None, late_hook=None):
            n = 16 if t == 0 else 128
            chunks = [(0, 16)] if t == 0 else [(0, 64), (64, 128)]
            nch = len(chunks)
            clen = chunks[0][1]
            last_tile = (t == ntiles - 1)
            hk = f"hb{t % 2}"
            ht = hb[t % 2][0:n, :]
            hnT = hnTs[t % 2]; hnTk = f"hnT{t % 2}"
            cur = uT[t % 2]; curk = f"uT{t % 2}"
            pO2, kO2 = psb[5], "ps5"
            pOC, kOC = psb[5], "ps5"
            nxt = uT[(t + 1) % 2]; nxtk = f"uT{(t + 1) % 2}"
            rt = rot[t % 2]; rtk = f"rot{t % 2}"
            DMA(rt[:], rot_d[t], [], [rtk])

            def bc_h(ap2d, nh=4):
                return ap2d.unsqueeze(1).to_broadcast([ap2d.shape[0], nh, ap2d.shape[1]])

            def bc_l(ap2d, m):
                return ap2d.unsqueeze(2).to_broadcast([ap2d.shape[0], ap2d.shape[1], m])

            def proj_tok(c0, c1, extra=None):
                pt, pk = bank()
                for kc in range(8):
                    MM(pt[0:n, 0:c1 - c0], hnT[:, kc, 0:n], win[:, kc, c0:c1], [hnTk, "win"], [pk],
                       start=(kc == 0), stop=(kc == 7 and extra is None))
                if extra is not None:
                    MM(pt[0:n, extra[0]:extra[0] + 4], C("ident", slice(0, n), 0, n), prm[0:n, extra[1]:extra[1] + 4],
                       ["cst", "prm"], [pk], start=False, stop=True)
                return pt, pk

            def proj_feat(cols0, nblk):
                pt, pk = bank()
                for b_ in range(nblk):
                    for kc in range(8):
                        MM(pt[:, b_ * 128:b_ * 128 + n], win[:, kc, cols0 + b_ * 128:cols0 + (b_ + 1) * 128], hnT[:, kc, 0:n],
                           [hnTk, "win"], [pk], start=(kc == 0), stop=(kc == 7))
                return pt, pk

            def v3(ps_ap, a):
                return ps_ap.rearrange("p (a b) -> p a b", a=a)

            pA0, kA0 = proj_tok(0, 512)
            pA1, kA1 = proj_tok(512, 1024)
            ACT(f1[0:n, 256:512], pA0[0:n, 256:512], AF.Exp, [kA0], ["f1"])
            sigmoid_from_exp(f1[0:n, 256:512], "f1")
            TT("dve", f2[0:n, 256:512], f1[0:n, 256:512], oml[0:n, l, :], ALU.mult, ["f1", "oml"], ["f2"])
            ACT(f3[0:n, 0:256], f2[0:n, 256:512], AF.Ln, ["f2"], ["f3"], scale=-1.0, bias=eps_t[0:n, 1:2])
            pG, kG = bank()
            MM(pG[0:n, 0:256], C("uprime", slice(0, n), 0, n), f3[0:n, 0:256], ["cst", "f3"], [kG])
            for hp in range(2):
                MM(pG[:, 256 + hp * 8:256 + hp * 8 + 8], f3[0:n, hp * 128:(hp + 1) * 128], C("wc", slice(0, n)),
                   ["cst", "f3"], [kG])
            ACT(f1[0:n, 0:256], pA0[0:n, 0:256], AF.Exp, [kA0], ["f1"], scale=-1.0)
            ACT(f1[0:n, 512:768], pA1[0:n, 256:512], AF.Exp, [kA1], ["f1"], scale=-1.0)
            sigmoid_from_exp(f1[0:n, 0:256], "f1")
            sigmoid_from_exp(f1[0:n, 512:768], "f1")
            STT(f2[0:n, 0:256], pA0[0:n, 0:256], QK, f1[0:n, 0:256], ALU.mult, ALU.mult, [kA0, "f1"], ["f2"])
            TT("dve", f1[0:n, 512:768], f1[0:n, 512:768], prm[0:n, 0:256], ALU.mult, ["f1", "prm"], ["f1"])
            TT("dve", gate[0:n, 0:256], pA1[0:n, 256:512], f1[0:n, 512:768], ALU.mult, [kA1, "f1"], ["gate"])
            ACT(v_bf[0:n, :, :].rearrange("p a b -> p (a b)"), pA1[0:n, 0:256], AF.Copy, [kA1], ["v_bf"])
            ACT(f3[0:n, 0:256], pG[0:n, 0:256], AF.Exp, [kG], ["f3"])
            ACT(f3[0:n, 256:512], pG[0:n, 0:256], AF.Exp, [kG], ["f3"], scale=-1.0)
            ACT(ecs[:].rearrange("p a b -> p (a b)"), pG[:, 256:272], AF.Exp, [kG], ["ecs"])
            TT("dve", b1[0:n, 0:2, :].rearrange("p a b -> p (a b)"), f2[0:n, 0:256], f3[0:n, 0:256], ALU.mult,
               ["f2", "f3"], ["b1"])
            TT("dve", b1[0:n, 2:4, :].rearrange("p a b -> p (a b)"), f2[0:n, 256:512], f3[0:n, 256:512], ALU.mult,
               ["f2", "f3"], ["b1"])
            pT, kT = bank()
            for blk in range(4):
                MM(pT[:, blk * 128:blk * 128 + n], b1[0:n, blk, :], ident_bf[0:n, 0:n], ["b1", "ident_bf"], [kT])
            CP("act", qkT[:, :, 0:n], v3(pT[:, :], 4)[:, :, 0:n], [kT], ["qkT"])
            pS, kS = bank()
            for hd in (0, 2, 1, 3):
                hp, hh = hd // 2, hd % 2
                rows = slice(hh * 64, hh * 64 + 64)
                MM(pS[0:n, hd * 128:hd * 128 + n], qkT[rows, 2 + hp, 0:n], qkT[rows, hp, 0:n], ["qkT"], [kS])
            TT("dve", AT[0:n, :, 0:n], v3(pS[0:n, :], 4)[:, :, 0:n], bc_h(C("maskT", slice(0, n), 0, n)), ALU.mult,
               [kS, "cst"], ["AT"])
            pO, kO = psb[4], "ps4"
            pOD, kOD = psb[4], "ps4"
            for hd in (0, 2, 1, 3):
                MM(pO[0:n, hd * 64:hd * 64 + 64], AT[0:n, hd, 0:n], v_bf[0:n, hd, :], ["AT", "v_bf"], [kO],
                   start=(hd == 0), stop=False)
            for ci, (c0, c1) in enumerate(chunks):
                TT("dve", Sb_A[:], S_A[:], bc_l(ecs[:, :, ci], 64), ALU.mult, ["S_A", "ecs"], ["Sb_A"])
                TT("dve", Sd_A[:], S_A[:], bc_l(ecs[:, :, 4 + ci], 64), ALU.mult, ["S_A", "ecs"], ["Sd_A"])
                pK, kK = bank()
                for hd in (0, 2, 1, 3):
                    hp, hh = hd // 2, hd % 2
                    rows = slice(hh * 64, hh * 64 + 64)
                    MM(pO[c0:c1, hd * 64:hd * 64 + 64], qkT[rows, hp, c0:c1], Sb_A[rows, hp, :], ["qkT", "Sb_A"], [kO],
                       start=False, stop=True)
                    MM(pK[rows, hp * 64:hp * 64 + 64], b1[c0:c1, 2 + hp, hh * 64:hh * 64 + 64], v_bf[c0:c1, hd, :],
                       ["b1", "v_bf"], [kK])
                TT("dve", tmpS[:, 0:2, :], v3(pK[:, 0:128], 2), bc_l(ecs[:, :, 2 + ci], 64), ALU.mult, [kK, "ecs"], ["tmpS"])
                TT("dve", S_A[:], tmpS[:, 0:2, :], Sd_A[:], ALU.add, ["tmpS", "Sd_A"], ["S_A"])

            def head_norm(ps_ap, pskey, gcols, ycols):
                ACT(f4[0:n, 0:256], ps_ap, AF.Square, [pskey], ["f4"])
                RED(st4[0:n, 4:8], v3(f4[0:n, 0:256], 4), ["f4"], ["st4"])
                rsqrt_act(st4[0:n, 8:12], st4[0:n, 4:8], 1.0 / 64, ["st4"], ["st4"])
                TT("dve", v3(f4[0:n, 0:256], 4), v3(ps_ap, 4), bc_l(st4[0:n, 8:12], 64), ALU.mult, [pskey, "st4"], ["f4"])
                TT("dve", y_bf[0:n, ycols], f4[0:n, 0:256], gate[0:n, gcols], ALU.mult, ["f4", "gate"], ["y_bf"])

            head_norm(pO[0:n, 0:256], kO, slice(0, 256), slice(0, 256))
            if mid_hook is not None:
                mid_hook()

            pD0, kD0 = proj_tok(3084, 3596)
            pD1, kD1 = proj_tok(3596, 4108)
            cosb = rt[0:n, 0:32].unsqueeze(1).to_broadcast([n, 16, 32])
            sinb = rt[0:n, 32:64].unsqueeze(1).to_broadcast([n, 16, 32])
            qk4 = pD0[0:n, :].rearrange("p (a b) -> p a b", a=16)
            TT("dve", f1[0:n, 0:512].rearrange("p (a b) -> p a b", a=16), qk4, cosb, ALU.mult, [kD0, rtk], ["f1"])
            TT("dve", f2[0:n, 0:512].rearrange("p (a b) -> p a b", a=16), qk4, sinb, ALU.mult, [kD0, rtk], ["f2"])
            c4 = f1[0:n, 0:512].rearrange("p (a s b) -> p a s b", a=8, s=2)
            s4 = f2[0:n, 0:512].rearrange("p (a s b) -> p a s b", a=8, s=2)
            qkr = b1[0:n, :, :].rearrange("p a (s b) -> p a s b", s=4)
            qkr8 = b1[0:n, :, :].rearrange("p a b -> p (a b)").rearrange("p (a s b) -> p a s b", a=8, s=2)
            TT("dve", qkr8[:, :, 0, :], c4[:, :, 0, :], s4[:, :, 1, :], ALU.subtract, ["f1", "f2"], ["b1"])
            TT("dve", qkr8[:, :, 1, :], c4[:, :, 1, :], s4[:, :, 0, :], ALU.add, ["f1", "f2"], ["b1"])
            ACT(v_bf[0:n, :, :].rearrange("p a b -> p (a b)"), pD1[0:n, 0:256], AF.Copy, [kD1], ["v_bf"])
            ACT(f1[0:n, 512:768], pD1[0:n, 256:512], AF.Exp, [kD1], ["f1"], scale=-1.0)
            sigmoid_from_exp(f1[0:n, 512:768], "f1")
            TT("dve", gate[0:n, 768:1024], pD1[0:n, 256:512], f1[0:n, 512:768], ALU.mult, [kD1, "f1"], ["gate"])
            pT, kT = bank()
            for blk in range(4):
                MM(pT[:, blk * 128:blk * 128 + n], b1[0:n, blk, :], ident_bf[0:n, 0:n], ["b1", "ident_bf"], [kT])
            CP("act", qkT[:, :, 0:n], v3(pT[:, :], 4)[:, :, 0:n], [kT], ["qkT"])
            egq = C("egq").rearrange("p (a b) -> p a b", a=2)
            TT("dve", qpT[:, 0:2, 0:n], qkT[:, 0:2, 0:n], egq[:, :, 0:n], ALU.mult, ["qkT", "cst"], ["qpT"])
            egrev = C("egrev16" if t == 0 else "egrev64", slice(0, n))
            TT("dve", kp_bf[0:n, :, 0:64], b1[0:n, 2:4, :].rearrange("p a (s b) -> p (a s) b", s=2), bc_l(egrev, 64), ALU.mult,
               ["b1", "cst"], ["kp_bf"])
            pS, kS = bank()
            for hd in (0, 2, 1, 3):
                hp, hh = hd // 2, hd % 2
                rows = slice(hh * 64, hh * 64 + 64)
                MM(pS[0:n, hd * 128:hd * 128 + n], qkT[rows, 2 + hp, 0:n], qkT[rows, hp, 0:n], ["qkT"], [kS])
            TT("dve", AT[0:n, :, 0:n], v3(pS[0:n, :], 4)[:, :, 0:n], dtret_bf[0:n, :, 0:n], ALU.mult, [kS, "dtret_bf"], ["AT"])
            for hd in (0, 2, 1, 3):
                MM(pOD[0:n, 256 + hd * 64:256 + hd * 64 + 64], AT[0:n, hd, 0:n], v_bf[0:n, hd, :], ["AT", "v_bf"], [kOD],
                   start=(hd == 0), stop=False)
            egl = C("egl").rearrange("p (a b) -> p a b", a=2)
            for ci, (c0, c1) in enumerate(chunks):
                pK, kK = bank()
                for hd in (0, 2, 1, 3):
                    hp, hh = hd // 2, hd % 2
                    rows = slice(hh * 64, hh * 64 + 64)
                    MM(pOD[c0:c1, 256 + hd * 64:256 + hd * 64 + 64], qpT[rows, hp, c0:c1], Sb_D[rows, hp, :], ["qpT", "Sb_D"], [kOD],
                       start=False, stop=True)
                    MM(pK[rows, hp * 64:hp * 64 + 64], kp_bf[c0:c1, hd, 0:64], v_bf[c0:c1, hd, :], ["kp_bf", "v_bf"], [kK])
                TT("dve", tmpS[:, 0:2, :], S_D[:], bc_l(egl[:, :, (0 if t == 0 else 1)], 64), ALU.mult, ["S_D", "cst"], ["tmpS"])
                TT("dve", S_D[:], tmpS[:, 0:2, :], v3(pK[:, 0:128], 2), ALU.add, ["tmpS", kK], ["S_D"])
                CP("act", Sb_D[:], S_D[:], ["S_D"], ["Sb_D"])
            oD = pOD[0:n, 256:512]
            kO_ = kOD
            RED(st4[0:n, 4:8], v3(oD, 4), [kO_], ["st4"])
            TS("dve", st4[0:n, 4:8], st4[0:n, 4:8], -1.0 / 64, ALU.mult, ["st4"], ["st4"])
            TT("dve", v3(f3[0:n, 0:256], 4), v3(oD, 4), bc_l(st4[0:n, 4:8], 64), ALU.add, [kO_, "st4"], ["f3"])
            ACT(f4[0:n, 0:256], f3[0:n, 0:256], AF.Square, ["f3"], ["f4"])
            RED(st4[0:n, 4:8], v3(f4[0:n, 0:256], 4), ["f4"], ["st4"])
            rsqrt_act(st4[0:n, 8:12], st4[0:n, 4:8], 1.0 / 64, ["st4"], ["st4"])
            TT("dve", v3(f3[0:n, 0:256], 4), v3(f3[0:n, 0:256], 4), bc_l(st4[0:n, 8:12], 64), ALU.mult, ["f3", "st4"], ["f3"])
            TT("dve", f3[0:n, 0:256], f3[0:n, 0:256], prm[0:n, 768:1024], ALU.mult, ["f3", "prm"], ["f3"])
            TT("dve", f3[0:n, 0:256], f3[0:n, 0:256], prm[0:n, 1024:1280], ALU.add, ["f3", "prm"], ["f3"])
            TT("dve", y_bf[0:n, 768:1024], f3[0:n, 0:256], gate[0:n, 768:1024], ALU.mult, ["f3", "gate"], ["y_bf"])

            pBz, kBz = proj_tok(1792, 2056, (256, 1288))
            pCz, kCz = proj_tok(2824, 3084, (256, 1292))
            ACT(f1[0:n, 512:768], pBz[0:n, 0:256], AF.Exp, [kBz], ["f1"], scale=-1.0)
            ACT(f2[0:n, 0:256], pCz[0:n, 0:256], AF.Exp, [kCz], ["f2"], scale=-1.0)
            ACT(beta[0:n, 0:4], pBz[0:n, 260:264], AF.Exp, [kBz], ["beta"], scale=-1.0)
            ACT(g8[0:n, 0:4], pBz[0:n, 256:260], AF.Exp, [kBz], ["g8"])
            ACT(g8[0:n, 4:8], pCz[0:n, 256:260], AF.Exp, [kCz], ["g8"])
            sigmoid_from_exp(f1[0:n, 512:768], "f1")
            sigmoid_from_exp(f2[0:n, 0:256], "f2")
            TT("dve", f1[0:n, 512:768], f1[0:n, 512:768], prm[0:n, 256:512], ALU.mult, ["f1", "prm"], ["f1"])
            TT("dve", gate[0:n, 256:512], pBz[0:n, 0:256], f1[0:n, 512:768], ALU.mult, [kBz, "f1"], ["gate"])
            TT("dve", gate[0:n, 512:768], pCz[0:n, 0:256], f2[0:n, 0:256], ALU.mult, [kCz, "f2"], ["gate"])
            ACT(g8[0:n, 0:8], g8[0:n, 0:8], AF.Ln, ["g8"], ["g8"], bias=eps_t[0:n, 1:2])
            CP("dve", dtb[0:n, :], g8[0:n, :], ["g8"], ["dtb"])
            TT("dve", g8[0:n, :], g8[0:n, :], nega[0:n, :], ALU.mult, ["g8", "nega"], ["g8"])
            sigmoid_from_exp(beta[0:n, :], "beta")
            for cv, cols0 in ((0, 1024), (1, 2056)):
                for part, (b0, nb_) in enumerate(((0, 4), (4, 2))):
                    pf, kf = proj_feat(cols0 + b0 * 128, nb_)
                    src = v3(pf[:, 0:nb_ * 128], nb_)[:, :, 0:n]
                    CP("act", cur[:, cv * 6 + b0:cv * 6 + b0 + nb_, 3:3 + n], src, [kf], [curk])
                    if last_tile:
                        CP("dve", cvst[:, cv * 6 + b0:cv * 6 + b0 + nb_, :], src[:, :, n - 3:n], [kf], ["cvst"])
            if not last_tile:
                CP("pool", nxt[:, :, 0:3], cur[:, :, n:n + 3], [curk], [nxtk])
            pDc, kDc = bank()
            MM(pDc[0:n, 0:32], C("maskT", slice(0, n), 0, n), g8[0:n, 0:32], ["cst", "g8"], [kDc])
            MM(pDc[0:n, 32:64], C("urev", slice(0, n), 0, n), g8[0:n, 0:32], ["cst", "g8"], [kDc])
            CP("dve", gc[0:n, :], pDc[0:n, 0:64], [kDc], ["gc"])
            ACT(egc[0:n, :], gc[0:n, :], AF.Exp, ["gc"], ["egc"])
            for half in range(2):
                pB_, kB_ = bank()
                CP("dve", ghi[0:n, :, :], bc_l(g8[0:n, half * 4:half * 4 + 4], 128), ["g8"], ["ghi"])
                TT("dve", glo[0:n, :, :], bc_l(g8[0:n, half * 4:half * 4 + 4], 128), ghi[0:n, :, :], ALU.subtract,
                   ["g8", "ghi"], ["glo"])
                for hd in range(4):
                    MM(pB_[:, hd * 128:hd * 128 + n], ghi[0:n, hd, :], maskT_bf[0:n, 0:n], ["maskT_bf", "ghi"], [kB_],
                       start=True, stop=False)
                    MM(pB_[:, hd * 128:hd * 128 + n], glo[0:n, hd, :], maskT_bf[0:n, 0:n], ["maskT_bf", "glo"], [kB_],
                       start=False, stop=True)
                srcB = v3(pB_[:, :], 4)[:, :, 0:n]
                ACT(eGbc[:, half * 4:half * 4 + 4, 0:n], srcB, AF.Exp, [kB_], ["eGbc"])
                TT("dve", tt[0:n, half * 4:half * 4 + 4, 0:n], srcB[0:n], bc_l(gc[0:n, half * 4:half * 4 + 4], n), ALU.subtract,
                   [kB_, "gc"], ["tt"])
            if True:
                TT("dve", v3(f1[0:n, 0:512], 4)[:, :, 0:n], tt[0:n, 0:4, 0:n], bc_h(C("negS", slice(0, n), 0, n)), ALU.subtract,
                   ["tt", "cst"], ["f1"])
                ACT(Dst[0:n, :, 0:n], v3(f1[0:n, 0:512], 4)[:, :, 0:n], AF.Exp, ["f1"], ["Dst"], scale=-1.0)
                TT("dve", tt[0:n, :, 0:n], tt[0:n, :, 0:n], bc_h(C("negT", slice(0, n), 0, n), 8), ALU.add, ["tt", "cst"], ["tt"])
                ACT(DT[0:n, :, 0:n], tt[0:n, :, 0:n], AF.Exp, ["tt"], ["DT"])

            for cv in range(2):
                for part, (b0, nb_) in enumerate(((0, 4), (4, 2))):
                    pc, kc_ = bank()
                    for b_ in range(nb_):
                        blk = cv * 6 + b0 + b_
                        for w in range(4):
                            MM(pc[:, b_ * 128:b_ * 128 + n], dg[:, blk, w, :], cur[:, blk, w:w + n], ["dg", curk], [kc_],
                               start=(w == 0), stop=(w == 3 and cv == 0))
                        if cv == 1:
                            MM(pc[:, b_ * 128:b_ * 128 + n], cbias_bf[0:1, (b0 + b_) * 128:(b0 + b_ + 1) * 128], ones_bf[0:1, 0:n],
                               ["cbias_bf", "ones_bf"], [kc_], start=False, stop=True)
                    src = v3(pc[:, 0:nb_ * 128], nb_)[:, :, 0:n]
                    dstf = v3(f1[:, 0:nb_ * 128], nb_)[:, :, 0:n]
                    ACT(dstf, src, AF.Exp, [kc_], ["f1"], scale=-1.0)
                    sigmoid_from_exp(dstf, "f1")
                    if cv == 0 and part == 0:
                        TT("dve", xsf[:, :, 0:n], src, dstf, ALU.mult, [kc_, "f1"], ["xsf"])
                        ACT(b1[:, :, 0:n], xsf[:, :, 0:n], AF.Square, ["xsf"], ["b1"])
                        pN, kN = bank()
                        for blk in range(4):
                            MM(pN[:, blk * 128:blk * 128 + n], bones_bf[:], b1[:, blk, 0:n], ["bones_bf", "b1"], [kN])
                        srcN = v3(pN[:, :], 4)[:, :, 0:n]
                        dstN = v3(f2[:, 0:512], 4)[:, :, 0:n]
                        ACT(dstN, srcN, AF.Ln, [kN], ["f2"], bias=eps_t[:, 0:1])
                        ACT(dstN, dstN, AF.Exp, ["f2"], ["f2"], scale=-0.5)
                        STT(xs[:, 0:2, 0:n], xsf[:, 0:2, 0:n], QK, dstN[:, 0:2, :], ALU.mult, ALU.mult, ["xsf", "f2"], ["xs"])
                        TT("dve", xs[:, 2:4, 0:n], xsf[:, 2:4, 0:n], dstN[:, 2:4, :], ALU.mult, ["xsf", "f2"], ["xs"])
                    else:
                        TT("dve", xs[:, cv * 6 + b0:cv * 6 + b0 + nb_, 0:n], src, dstf, ALU.mult, [kc_, "f1"], ["xs"])
            pS, kS = bank()
            pKK, kKK = bank()
            for hd in (0, 2, 1, 3):
                hp, hh = hd // 2, hd % 2
                rows = slice(hh * 64, hh * 64 + 64)
                MM(pS[0:n, hd * 128:hd * 128 + n], xs[rows, 2 + hp, 0:n], xs[rows, hp, 0:n], ["xs"], [kS])
                MM(pKK[0:n, hd * 128:hd * 128 + n], xs[rows, 2 + hp, 0:n], xs[rows, 2 + hp, 0:n], ["xs"], [kKK])
            TS("dve", nb[0:n, :], beta[0:n, :], -1.0, ALU.mult, ["beta"], ["nb"])
            TT("dve", v3(f1[0:n, 0:512], 4)[:, :, 0:n], v3(pKK[0:n, :], 4)[:, :, 0:n], Dst[0:n, :, 0:n], ALU.mult, [kKK, "Dst"], ["f1"])
            TT("dve", X_bf[0][0:n, :, 0:n], v3(f1[0:n, 0:512], 4)[:, :, 0:n], bc_l(nb[0:n, 0:4], n), ALU.mult,
               ["f1", "nb"], ["X_bf0"])
            TT("dve", AT[0:n, :, 0:n], v3(pS[0:n, :], 4)[:, :, 0:n], DT[0:n, 0:4, 0:n], ALU.mult, [kS, "DT"], ["AT"])
            def t_chain():
                pY, kY = bank()
                for hd in (0, 2, 1, 3):
                    MM(pY[0:n, hd * 128:hd * 128 + n], X_bf[0][0:n, hd, 0:n], ident_bf[0:n, 0:n], ["X_bf0", "ident_bf"], [kY])
                CP("act", Y_bf[0][0:n, :, 0:n], v3(pY[0:n, :], 4)[:, :, 0:n], [kY], ["Y_bf0"])
                TT("dve", P_bf[0:n, :, 0:n], v3(pY[0:n, :], 4)[:, :, 0:n], bc_h(C("ident", slice(0, n), 0, n)), ALU.add,
                   [kY, "cst"], ["P_bf"])
                yield
                nlev = int(math.ceil(math.log2(clen))) - 1
                ci_ = 0
                for lev in range(nlev):
                    ni_ = 1 - ci_
                    pX2, kX2 = bank()
                    for hd in (0, 2, 1, 3):
                        MM(pX2[0:n, hd * 128:hd * 128 + n], Y_bf[ci_][0:n, hd, 0:n], X_bf[ci_][0:n, hd, 0:n],
                           [f"Y_bf{ci_}", f"X_bf{ci_}"], [kX2])
                    CP("act", X_bf[ni_][0:n, :, 0:n], v3(pX2[0:n, :], 4)[:, :, 0:n], [kX2], [f"X_bf{ni_}"])
                    if lev < nlev - 1:
                        pY2, kY2 = bank()
                        for hd in (0, 2, 1, 3):
                            MM(pY2[0:n, hd * 128:hd * 128 + n], X_bf[ci_][0:n, hd, 0:n], Y_bf[ci_][0:n, hd, 0:n],
                               [f"Y_bf{ci_}", f"X_bf{ci_}"], [kY2])
                        CP("dve", Y_bf[ni_][0:n, :, 0:n], v3(pY2[0:n, :], 4)[:, :, 0:n], [kY2], [f"Y_bf{ni_}"])
                    pP, kP = bank()
                    for hd in (0, 2, 1, 3):
                        MM(pP[0:n, hd * 128:hd * 128 + n], X_bf[ni_][0:n, hd, 0:n], P_bf[0:n, hd, 0:n], [f"X_bf{ni_}", "P_bf"], [kP])
                    TT("dve", P_bf[0:n, :, 0:n], P_bf[0:n, :, 0:n], v3(pP[0:n, :], 4)[:, :, 0:n], ALU.add, ["P_bf", kP], ["P_bf"])
                    ci_ = ni_
                    yield

            tgen = t_chain()

            def tstep():
                next(tgen, None)

            tstep()

            pS, kS = bank()
            for g in range(2):
                MM(pS[0:n, g * 128:g * 128 + n], xs[:, 8 + g, 0:n], xs[:, 10 + g, 0:n], ["xs"], [kS])
            for g in range(2):
                TT("dve", AT2[0:n, 2 * g:2 * g + 2, 0:n], pS[0:n, g * 128:g * 128 + n].unsqueeze(1).to_broadcast([n, 2, n]),
                   DT[0:n, 4 + 2 * g:4 + 2 * g + 2, 0:n], ALU.mult, [kS, "DT"], ["AT2"])
            tstep()
            pT, kT = bank()
            for blk in range(4):
                MM(pT[0:n, blk * 128:(blk + 1) * 128], xs[:, 6 + blk, 0:n], ident_bf[:, :], ["xs", "ident_bf"], [kT])
            TT("dve", v_bf[0:n, :, :], v3(pT[0:n, 0:256], 4), bc_l(dtb[0:n, 4:8], 64), ALU.mult, [kT, "dtb"], ["v_bf"])
            TT("dve", xd_bf[0:n, :, :], v3(pT[0:n, 0:256], 4), bc_l(prm[0:n, 1296:1300], 64), ALU.mult, [kT, "prm"], ["xd_bf"])
            for g in range(2):
                TT("dve", kp_bf[0:n, 2 * g:2 * g + 2, :], pT[0:n, 256 + g * 128:256 + (g + 1) * 128].unsqueeze(1).to_broadcast([n, 2, 128]),
                   bc_l(egc[0:n, 36 + 2 * g:36 + 2 * g + 2], 128), ALU.mult, [kT, "egc"], ["kp_bf"])
                TT("dve", qpT[:, 2 * g:2 * g + 2, 0:n], xs[:, 10 + g, 0:n].unsqueeze(1).to_broadcast([128, 2, n]),
                   eGbc[:, 4 + 2 * g:4 + 2 * g + 2, 0:n], ALU.mult, ["xs", "eGbc"], ["qpT"])
            for hd in (0, 2, 1, 3):
                MM(pOC[0:n, 256 + hd * 64:256 + hd * 64 + 64], AT2[0:n, hd, 0:n], v_bf[0:n, hd, :], ["AT2", "v_bf"], [kOC],
                   start=(hd == 0), stop=False)
            MM(pOC[0:n, 256:512], ident_bf[0:n, 0:n], xd_bf[0:n, :, :].rearrange("p a b -> p (a b)"), ["ident_bf", "xd_bf"], [kOC],
               start=False, stop=False)
            for ci, (c0, c1) in enumerate(chunks):
                tstep()
                pK, kK = bank()
                for hd in (0, 2, 1, 3):
                    MM(pOC[c0:c1, 256 + hd * 64:256 + hd * 64 + 64], qpT[:, hd, c0:c1], Sb_C[:, hd, :], ["qpT", "Sb_C"], [kOC],
                       start=False, stop=True)
                    MM(pK[:, hd * 64:hd * 64 + 64], kp_bf[c0:c1, hd, :], v_bf[c0:c1, hd, :], ["kp_bf", "v_bf"], [kK])
                TT("dve", tmpS[:, :, :], S_C[:], eGbc[:, 4:8, c1 - 1:c1].to_broadcast([128, 4, 64]), ALU.mult, ["S_C", "eGbc"], ["tmpS"])
                TT("dve", S_C[:], tmpS[:, :, :], v3(pK[:, 0:256], 4), ALU.add, ["tmpS", kK], ["S_C"])
                CP("act", Sb_C[:], S_C[:], ["S_C"], ["Sb_C"])
            tstep()
            TT("dve", f3[0:n, 0:256], pOC[0:n, 256:512], gate[0:n, 512:768], ALU.mult, [kOC, "gate"], ["f3"])
            ACT(f4[0:n, 0:256], f3[0:n, 0:256], AF.Square, ["f3"], ["f4"])
            RED(st4[0:n, 4:6], v3(f4[0:n, 0:256], 2), ["f4"], ["st4"])
            rsqrt_act(st4[0:n, 8:10], st4[0:n, 4:6], 1.0 / 128, ["st4"], ["st4"])
            TT("dve", v3(f3[0:n, 0:256], 2), v3(f3[0:n, 0:256], 2), bc_l(st4[0:n, 8:10], 128), ALU.mult, ["f3", "st4"], ["f3"])
            TT("dve", y_bf[0:n, 512:768], f3[0:n, 0:256], prm[0:n, 512:768], ALU.mult, ["f3", "prm"], ["y_bf"])

            for _ in tgen:
                pass

            pT, kT = bank()
            for blk in range(4):
                MM(pT[0:n, blk * 128:(blk + 1) * 128], xs[:, 2 + blk, 0:n], ident_bf[:, :], ["xs", "ident_bf"], [kT])
            TT("dve", kp_bf[0:n, :, 0:64], v3(pT[0:n, 0:256], 4), bc_l(egc[0:n, 32:36], 64), ALU.mult, [kT, "egc"], ["kp_bf"])
            TT("dve", bv[0:n, :, :], v3(pT[0:n, 256:512], 4), bc_l(beta[0:n, 0:4], 64), ALU.mult, [kT, "beta"], ["bv"])
            TT("dve", nb2[0:n, :], nb[0:n, :], egc[0:n, :], ALU.mult, ["nb", "egc"], ["nb2"])
            for hh in range(2):
                rows = slice(hh * 64, hh * 64 + 64)
                TT("dve", qpT[rows, 0:2, 0:n], xs[rows, 0:2, 0:n], eGbc[rows, hh:4:2, 0:n], ALU.mult, ["xs", "eGbc"], ["qpT"])
            for ci, (c0, c1) in enumerate(chunks):
                pW, kW = bank()
                for hd in (0, 2, 1, 3):
                    hp, hh = hd // 2, hd % 2
                    rows = slice(hh * 64, hh * 64 + 64)
                    MM(pW[c0:c1, hd * 64:hd * 64 + 64], xs[rows, 2 + hp, c0:c1], Sb_B[rows, hp, :], ["xs", "Sb_B"], [kW])
                TT("dve", v3(f4[c0:c1, 0:256], 4), v3(pW[c0:c1, 0:256], 4), bc_l(nb2[c0:c1, 0:4], 64), ALU.mult,
                   [kW, "nb2"], ["f4"])
                TT("dve", r_bf[c0:c1, :, :], v3(f4[c0:c1, 0:256], 4), bv[c0:c1, :, :], ALU.add, ["f4", "bv"], ["r_bf"])
                pU, kU = bank()
                for hd in (0, 2, 1, 3):
                    MM(pU[c0:c1, hd * 64:hd * 64 + 64], P_bf[c0:c1, hd, c0:c1], r_bf[c0:c1, hd, :], ["P_bf", "r_bf"], [kU])
                CP("act", u_bf[c0:c1, :, :], v3(pU[c0:c1, 0:256], 4), [kU], ["u_bf"])
                pK, kK = bank()
                for hd in (0, 2, 1, 3):
                    hp, hh = hd // 2, hd % 2
                    rows = slice(hh * 64, hh * 64 + 64)
                    MM(pO2[c0:c1, hd * 64:hd * 64 + 64], AT[c0:c1, hd, c0:c1], u_bf[c0:c1, hd, :], ["AT", "u_bf"], [kO2],
                       start=(hd == 0), stop=False)
                    MM(pO2[c0:c1, hd * 64:hd * 64 + 64], qpT[rows, hp, c0:c1], Sb_B[rows, hp, :], ["qpT", "Sb_B"], [kO2],
                       start=False, stop=True)
                    MM(pK[rows, hp * 64:hp * 64 + 64], kp_bf[c0:c1, hd, 0:64], u_bf[c0:c1, hd, :], ["kp_bf", "u_bf"], [kK])
                for hh in range(2):
                    rows = slice(hh * 64, hh * 64 + 64)
                    TT("dve", tmpS[rows, 0:2, :], S_B[rows, :, :], eGbc[rows, hh:4:2, c1 - 1:c1].to_broadcast([64, 2, 64]), ALU.mult,
                       ["S_B", "eGbc"], ["tmpS"])
                TT("dve", S_B[:], tmpS[:, 0:2, :], v3(pK[:, 0:128], 2), ALU.add, ["tmpS", kK], ["S_B"])
                CP("act", Sb_B[:], S_B[:], ["S_B"], ["Sb_B"])
            head_norm(pO2[0:n, 0:256], kO2, slice(256, 512), slice(256, 512))

            if late_hook is not None:
                late_hook()
            for half in range(2):
                pt, pk = bank()
                for kk in range(4):
                    kc = half * 4 + kk
                    MM(pt[:, kk * 128:kk * 128 + n], y_bf[0:n, kc * 128:(kc + 1) * 128], ident_bf[0:n, 0:n],
                       ["y_bf", "ident_bf"], [pk])
                CP("act" if half else "dve", yT[:, half * 4:half * 4 + 4, 0:n], v3(pt[:, :], 4)[:, :, 0:n], [pk], ["yT"])
            for cg in range(2):
                pt, pk = bank()
                for kc in range(8):
                    MM(pt[0:n, :], yT[:, kc, 0:n], wout[:, kc, cg * 512:(cg + 1) * 512], ["yT", "wout"], [pk],
                       start=(kc == 0), stop=(kc == 7))
                TT("dve", ht[:, cg * 512:(cg + 1) * 512], ht[:, cg * 512:(cg + 1) * 512], pt[0:n, :], ALU.add, [hk, pk], [hk])

            if last_tile:
                for nm, S_, dst in (("A", S_A, st_hg), ("B", S_B, st_gd), ("D", S_D, st_rt)):
                    for hh in range(2):
                        DMA(dst[l, hh:4:2, :, :].rearrange("a k v -> k a v"), S_[hh * 64:(hh + 1) * 64, :, :], ["S_" + nm], [])
                DMA(st_sd[l].rearrange("a k v -> k a v"), S_C[:, :, :], ["S_C"], [])
                for blk in range(6):
                    DMA(st_gc[l][:, blk * 128:(blk + 1) * 128].rearrange("w p -> p w"), cvst[:, blk, :], ["cvst"], [], slow=True)
                    DMA(st_sc[l][:, blk * 128:(blk + 1) * 128].rearrange("w p -> p w"), cvst[:, 6 + blk, :], ["cvst"], [], slow=True)
            if l < depth - 1:
                DMA(hscr[t * 128:t * 128 + n, :], ht, [hk], [f"hd{t}"])
            if l == depth - 1 and t > 0:
                ACT(hn_bf[0:n, :], ht, AF.Square, [hk], ["hn_bf", "st4"], accum=st4[0:n, 0:1])
                rsqrt_act(st4[0:n, 1:2], st4[0:n, 0:1], 1.0 / D, ["st4"], ["st4"])
                STT(ht, ht, st4[0:n, 1:2], finw[0:n, :], ALU.mult, ALU.mult, [hk, "st4", "lgt"], [hk])
                DMA(y_p[(t - 1) * 128:t * 128, :], ht, [hk], [])


        hs = sb("hs", [NS, D])
        DMA(hs[:, :], xs_d, [], ["hs"])

        def sample_fwd(l, last):
            n = NS
            DMA(rot[0][:], rot_d[NT], [], ["rot0"])
            rt = rot[0]; rtk = "rot0"

            def v3(ps_ap, a):
                return ps_ap.rearrange("p (a b) -> p a b", a=a)

            def bc_l(ap2d, m):
                return ap2d.unsqueeze(2).to_broadcast([ap2d.shape[0], ap2d.shape[1], m])

            ACT(hn_bf[0:n, :], hs[:, :], AF.Square, ["hs"], ["hn_bf", "st4"], accum=st4[0:n, 0:1])
            rsqrt_act(st4[0:n, 1:2], st4[0:n, 0:1], 1.0 / D, ["st4"], ["st4"])
            ACT(hn_bf[0:n, :], hs[:, :], AF.Copy, ["hs", "st4"], ["hn_bf"], scale=st4[0:n, 1:2])
            for half in range(2):
                pt, pk = bank()
                for kk in range(4):
                    kc = half * 4 + kk
                    MM(pt[:, kk * 128:kk * 128 + n], hn_bf[0:n, kc * 128:(kc + 1) * 128], ident_bf[0:n, 0:n],
                       ["hn_bf", "ident_bf"], [pk])
                CP("act", hnT[:, half * 4:half * 4 + 4, 0:n], v3(pt[:, :], 4)[:, :, 0:n], [pk], ["hnT1"])

            def proj_tok(c0, c1, extra=None):
                pt, pk = bank()
                for kc in range(8):
                    MM(pt[0:n, 0:c1 - c0], hnT[:, kc, 0:n], win[:, kc, c0:c1], ["hnT1", "win"], [pk],
                       start=(kc == 0), stop=(kc == 7 and extra is None))
                if extra is not None:
                    MM(pt[0:n, extra[0]:extra[0] + 4], C("ident", slice(0, n), 0, n), prm[0:n, extra[1]:extra[1] + 4],
                       ["cst", "prm"], [pk], start=False, stop=True)
                return pt, pk

            sel = C("sel").rearrange("p (a b) -> p a b", a=4)
            selT = C("selT").rearrange("p (a b) -> p a b", a=4)
            pvs = f4
            Sbuf = [tt, eGbc]; Skey = ["tt", "eGbc"]
            Tbuf = gate; Tkey = "gate"
            slot = [0]

            def select(fields):
                pv, pvk = bank()
                first = True
                for (c0, wd, fn) in fields:
                    for hd in range(4):
                        ap, key = fn(hd)
                        P.op("pe", (lambda o_, l_, r_, st_: (lambda e: e.matmul(o_, lhsT=l_, rhs=r_, start=st_, stop=False,
                                                                                 skip_group_check=True)))(
                            pv[0:64, c0:c0 + wd], sel[0:n, hd, :], ap, first), reads=["cst", key], writes=[pvk])
                        first = False
                wtot = max(c0 + wd for (c0, wd, _) in fields)
                CP("dve", pvs[0:64, 0:wtot], pv[0:64, 0:wtot], [pvk], ["f4"])

            def unselect(o_ap, okey):
                po, pok = bank()
                for hd in range(4):
                    MM(po[0:n, hd * 64:hd * 64 + 64], selT[0:64, hd, :], o_ap, ["cst", okey], [pok])
                return po, pok

            def state_io(st_in, st_out, K):
                ks = 16
                for k0 in range(0, K, ks):
                    yield k0, ks

            def load_slice(st_in, k0, ks):
                i = slot[0] % 2
                slot[0] += 1
                Sv = Sbuf[i][0:64, :, :].rearrange("p a b -> p (a b)")[:, 0:ks * 64].rearrange("p (k v) -> p k v", k=ks)
                for hd in range(4):
                    DMA(Sv[hd * 16:(hd + 1) * 16, :, :], st_in[l, :, hd, k0:k0 + ks, :], [], [Skey[i]])
                return Sv, Skey[i]

            def store_slice(st_out, Sv, sk, k0, ks):
                for hd in range(4):
                    DMA(st_out[l, :, hd, k0:k0 + ks, :], Sv[hd * 16:(hd + 1) * 16, :, :], [sk], [])

            o_sb = tmpS[0:64, 0, :]; w_sb = tmpS[0:64, 1, :]; op_sb = tmpS[0:64, 2, :]; u_sb = tmpS[0:64, 3, :]

            def Tview(ks):
                return Tbuf[0:64, 0:ks * 64].rearrange("p (k v) -> p k v", k=ks)

            def q_reduce(Sv, sk, q_ap, k0, ks, first, acc):
                T = Tview(ks)
                TT("dve", T, Sv, bc_l(q_ap[:, k0:k0 + ks], 64), ALU.mult, [sk, "f4"], [Tkey])
                RED(op_sb, T.rearrange("p k v -> p v k"), [Tkey], ["tmpS"])
                if first:
                    CP("dve", acc, op_sb, ["tmpS"], ["tmpS"])
                else:
                    TT("dve", acc, acc, op_sb, ALU.add, ["tmpS"], ["tmpS"])

            def step_plain(st_in, st_out, K, q_ap, k_ap, v_ap, vec_f=None, sc=None):
                for k0, ks in state_io(st_in, st_out, K):
                    Sv, sk = load_slice(st_in, k0, ks)
                    T = Tview(ks)
                    TT("dve", T, bc_l(k_ap[:, k0:k0 + ks], 64), v_ap.unsqueeze(1).to_broadcast([64, ks, 64]), ALU.mult,
                       ["f4", "tmpS"], [Tkey])
                    if vec_f is not None:
                        TT("dve", Sv, Sv, bc_l(vec_f[:, k0:k0 + ks], 64), ALU.mult, [sk, "f4"], [sk])
                        TT("dve", Sv, Sv, T, ALU.add, [sk, Tkey], [sk])
                    else:
                        STT(Sv, Sv, sc, T, ALU.mult, ALU.add, [sk, Tkey, "f4", "cst"], [sk])
                    store_slice(st_out, Sv, sk, k0, ks)
                    q_reduce(Sv, sk, q_ap, k0, ks, k0 == 0, o_sb)

            def head_norm(ps_ap, pskey, gcols, ycols):
                ACT(f3[0:n, 256:512], ps_ap, AF.Square, [pskey], ["f3"])
                RED(st4[0:n, 4:8], v3(f3[0:n, 256:512], 4), ["f3"], ["st4"])
                rsqrt_act(st4[0:n, 8:12], st4[0:n, 4:8], 1.0 / 64, ["st4"], ["st4"])
                TT("dve", v3(f3[0:n, 256:512], 4), v3(ps_ap, 4), bc_l(st4[0:n, 8:12], 64), ALU.mult, [pskey, "st4"], ["f3"])
                TT("dve", y_bf[0:n, ycols], f3[0:n, 256:512], hb[1][0:n, gcols], ALU.mult, ["f3", "hb1"], ["y_bf"])

            gs = hb[1]; gsk = "hb1"

            pA0, kA0 = proj_tok(0, 512)
            pA1, kA1 = proj_tok(512, 1024)
            ACT(f1[0:n, 0:256], pA0[0:n, 0:256], AF.Exp, [kA0], ["f1"], scale=-1.0)
            ACT(f1[0:n, 256:512], pA0[0:n, 256:512], AF.Exp, [kA0], ["f1"])
            ACT(f1[0:n, 512:768], pA1[0:n, 256:512], AF.Exp, [kA1], ["f1"], scale=-1.0)
            sigmoid_from_exp(f1[0:n, :], "f1")
            STT(f2[0:n, 0:256], pA0[0:n, 0:256], QK, f1[0:n, 0:256], ALU.mult, ALU.mult, [kA0, "f1"], ["f2"])
            TT("dve", f2[0:n, 256:512], f1[0:n, 256:512], oml[0:n, l, :], ALU.mult, ["f1", "oml"], ["f2"])
            TT("dve", f1[0:n, 512:768], f1[0:n, 512:768], prm[0:n, 0:256], ALU.mult, ["f1", "prm"], ["f1"])
            TT("dve", gs[0:n, 0:256], pA1[0:n, 256:512], f1[0:n, 512:768], ALU.mult, [kA1, "f1"], [gsk])
            CP("dve", f3[0:n, 0:256], pA1[0:n, 0:256], [kA1], ["f3"])
            TS("dve", f1[0:n, 0:256], f2[0:n, 256:512], -1.0, ALU.mult, ["f2"], ["f1"], s2=1.0, op1=ALU.add)
            select([(0, 64, lambda hd: (f2[0:n, hd * 64:hd * 64 + 64], "f2")),
                    (64, 64, lambda hd: (f2[0:n, 256 + hd * 64:256 + hd * 64 + 64], "f2")),
                    (128, 64, lambda hd: (f3[0:n, hd * 64:hd * 64 + 64], "f3")),
                    (192, 64, lambda hd: (f1[0:n, hd * 64:hd * 64 + 64], "f1"))])
            step_plain(si_hg, so_hg, 64, pvs[0:64, 0:64], pvs[0:64, 64:128], pvs[0:64, 128:192], vec_f=pvs[0:64, 192:256])
            po, pok = unselect(o_sb, "tmpS")
            head_norm(po[0:n, 0:256], pok, slice(0, 256), slice(0, 256))

            pD0, kD0 = proj_tok(3084, 3596)
            pD1, kD1 = proj_tok(3596, 4108)
            cosb = rt[0:n, 0:32].unsqueeze(1).to_broadcast([n, 16, 32])
            sinb = rt[0:n, 32:64].unsqueeze(1).to_broadcast([n, 16, 32])
            qk4 = pD0[0:n, :].rearrange("p (a b) -> p a b", a=16)
            TT("dve", f1[0:n, 0:512].rearrange("p (a b) -> p a b", a=16), qk4, cosb, ALU.mult, [kD0, rtk], ["f1"])
            TT("dve", f2[0:n, 0:512].rearrange("p (a b) -> p a b", a=16), qk4, sinb, ALU.mult, [kD0, rtk], ["f2"])
            c4 = f1[0:n, 0:512].rearrange("p (a s b) -> p a s b", a=8, s=2)
            s4 = f2[0:n, 0:512].rearrange("p (a s b) -> p a s b", a=8, s=2)
            r4 = f3[0:n, 0:512].rearrange("p (a s b) -> p a s b", a=8, s=2)
            TT("dve", r4[:, :, 0, :], c4[:, :, 0, :], s4[:, :, 1, :], ALU.subtract, ["f1", "f2"], ["f3"])
            TT("dve", r4[:, :, 1, :], c4[:, :, 1, :], s4[:, :, 0, :], ALU.add, ["f1", "f2"], ["f3"])
            TS("dve", f3[0:n, 256:512], f3[0:n, 256:512], QK, ALU.mult, ["f3"], ["f3"])
            CP("dve", f1[0:n, 0:256], pD1[0:n, 0:256], [kD1], ["f1"])
            ACT(f1[0:n, 512:768], pD1[0:n, 256:512], AF.Exp, [kD1], ["f1"], scale=-1.0)
            sigmoid_from_exp(f1[0:n, 512:768], "f1")
            TT("dve", gs[0:n, 768:1024], pD1[0:n, 256:512], f1[0:n, 512:768], ALU.mult, [kD1, "f1"], [gsk])
            select([(0, 64, lambda hd: (f3[0:n, hd * 64:hd * 64 + 64], "f3")),
                    (64, 64, lambda hd: (f3[0:n, 256 + hd * 64:256 + hd * 64 + 64], "f3")),
                    (128, 64, lambda hd: (f1[0:n, hd * 64:hd * 64 + 64], "f1"))])
            step_plain(si_rt, so_rt, 64, pvs[0:64, 0:64], pvs[0:64, 64:128], pvs[0:64, 128:192], sc=C("gam64", slice(0, 64), 0, 1))
            po, pok = unselect(o_sb, "tmpS")
            oD = po[0:n, 0:256]
            RED(st4[0:n, 4:8], v3(oD, 4), [pok], ["st4"])
            TS("dve", st4[0:n, 4:8], st4[0:n, 4:8], -1.0 / 64, ALU.mult, ["st4"], ["st4"])
            TT("dve", v3(f3[0:n, 0:256], 4), v3(oD, 4), bc_l(st4[0:n, 4:8], 64), ALU.add, [pok, "st4"], ["f3"])
            ACT(f3[0:n, 256:512], f3[0:n, 0:256], AF.Square, ["f3"], ["f3"])
            RED(st4[0:n, 4:8], v3(f3[0:n, 256:512], 4), ["f3"], ["st4"])
            rsqrt_act(st4[0:n, 8:12], st4[0:n, 4:8], 1.0 / 64, ["st4"], ["st4"])
            TT("dve", v3(f3[0:n, 0:256], 4), v3(f3[0:n, 0:256], 4), bc_l(st4[0:n, 8:12], 64), ALU.mult, ["f3", "st4"], ["f3"])
            TT("dve", f3[0:n, 0:256], f3[0:n, 0:256], prm[0:n, 768:1024], ALU.mult, ["f3", "prm"], ["f3"])
            TT("dve", f3[0:n, 0:256], f3[0:n, 0:256], prm[0:n, 1024:1280], ALU.add, ["f3", "prm"], ["f3"])
            TT("dve", y_bf[0:n, 768:1024], f3[0:n, 0:256], gs[0:n, 768:1024], ALU.mult, ["f3", gsk], ["y_bf"])

            pBz, kBz = proj_tok(1792, 2056, (256, 1288))
            ACT(f1[0:n, 512:768], pBz[0:n, 0:256], AF.Exp, [kBz], ["f1"], scale=-1.0)
            ACT(beta[0:n, 0:4], pBz[0:n, 260:264], AF.Exp, [kBz], ["beta"], scale=-1.0)
            ACT(g8[0:n, 0:4], pBz[0:n, 256:260], AF.Exp, [kBz], ["g8"])
            sigmoid_from_exp(f1[0:n, 512:768], "f1")
            TT("dve", f1[0:n, 512:768], f1[0:n, 512:768], prm[0:n, 256:512], ALU.mult, ["f1", "prm"], ["f1"])
            TT("dve", gs[0:n, 256:512], pBz[0:n, 0:256], f1[0:n, 512:768], ALU.mult, [kBz, "f1"], [gsk])
            pCz, kCz = proj_tok(2824, 3084, (256, 1292))
            ACT(f1[0:n, 512:768], pCz[0:n, 0:256], AF.Exp, [kCz], ["f1"], scale=-1.0)
            ACT(g8[0:n, 4:8], pCz[0:n, 256:260], AF.Exp, [kCz], ["g8"])
            sigmoid_from_exp(f1[0:n, 512:768], "f1")
            TT("dve", gs[0:n, 512:768], pCz[0:n, 0:256], f1[0:n, 512:768], ALU.mult, [kCz, "f1"], [gsk])
            ACT(g8[0:n, 0:8], g8[0:n, 0:8], AF.Ln, ["g8"], ["g8"], bias=eps_t[0:n, 1:2])
            CP("dve", dtb[0:n, :], g8[0:n, :], ["g8"], ["dtb"])
            TT("dve", g8[0:n, :], g8[0:n, :], nega[0:n, :], ALU.mult, ["g8", "nega"], ["g8"])
            ACT(egc[0:n, 0:8], g8[0:n, 0:8], AF.Exp, ["g8"], ["egc"])
            sigmoid_from_exp(beta[0:n, :], "beta")

            def conv_tok(cv, groups, st_in, st_out, bias):
                U = hb[0]; Uk = "hb0"
                for (c0, c1, o0) in groups:
                    pu, puk = proj_tok(c0, c1)
                    CP("dve", U[0:n, o0:o0 + (c1 - c0)], pu[0:n, 0:c1 - c0], [puk], [Uk])
                DMA(st_out[l, :, 2, :], U[0:n, 0:768], [Uk], [])
                Wt = eGbc[0:n, :, :].rearrange("p a b -> p (a b)")[:, 0:768]
                Ct = tt[0:n, :, :].rearrange("p a b -> p (a b)")[:, 0:768]
                DMA(Wt, cwrow_d[l, cv, 3], [], ["eGbc"])
                TT("dve", f1[0:n, 0:768], U[0:n, 0:768], Wt, ALU.mult, [Uk, "eGbc"], ["f1"])
                for w in range(3):
                    DMA(Ct, st_in[l, :, w, :], [], ["tt"])
                    DMA(Wt, cwrow_d[l, cv, w], [], ["eGbc"])
                    if w >= 1:
                        DMA(st_out[l, :, w - 1, :], Ct, ["tt"], [])
                    TT("dve", Wt, Ct, Wt, ALU.mult, ["tt", "eGbc"], ["eGbc"])
                    TT("dve", f1[0:n, 0:768], f1[0:n, 0:768], Wt, ALU.add, ["f1", "eGbc"], ["f1"])
                if bias:
                    DMA(Ct, cbrow_d[l], [], ["tt"])
                    TT("dve", f1[0:n, 0:768], f1[0:n, 0:768], Ct, ALU.add, ["f1", "tt"], ["f1"])
                ACT(Ct, f1[0:n, 0:768], AF.Exp, ["f1"], ["tt"], scale=-1.0)
                sigmoid_from_exp(Ct, "tt")
                TT("dve", f1[0:n, 0:768], f1[0:n, 0:768], Ct, ALU.mult, ["f1", "tt"], ["f1"])

            conv_tok(0, [(1024, 1536, 0), (1536, 1792, 512)], si_gc, so_gc, False)
            Ct = tt[0:n, :, :].rearrange("p a b -> p (a b)")[:, 0:512]
            ACT(Ct, f1[0:n, 0:512], AF.Square, ["f1"], ["tt"])
            RED(nb[0:n, 0:8], v3(Ct, 8), ["tt"], ["nb"])
            ACT(nb[0:n, 8:16], nb[0:n, 0:8], AF.Ln, ["nb"], ["nb"], bias=eps_t[0:n, 0:1])
            ACT(nb[0:n, 8:16], nb[0:n, 8:16], AF.Exp, ["nb"], ["nb"], scale=-0.5)
            TT("dve", v3(f1[0:n, 0:512], 8), v3(f1[0:n, 0:512], 8), bc_l(nb[0:n, 8:16], 64), ALU.mult, ["f1", "nb"], ["f1"])
            TS("dve", f1[0:n, 0:256], f1[0:n, 0:256], QK, ALU.mult, ["f1"], ["f1"])
            select([(0, 64, lambda hd: (f1[0:n, hd * 64:hd * 64 + 64], "f1")),
                    (64, 64, lambda hd: (f1[0:n, 256 + hd * 64:256 + hd * 64 + 64], "f1")),
                    (128, 64, lambda hd: (f1[0:n, 512 + hd * 64:512 + hd * 64 + 64], "f1")),
                    (192, 1, lambda hd: (egc[0:n, hd:hd + 1], "egc")),
                    (193, 1, lambda hd: (beta[0:n, hd:hd + 1], "beta"))])
            qB, kB, vB = pvs[0:64, 0:64], pvs[0:64, 64:128], pvs[0:64, 128:192]
            egB, btB = pvs[0:64, 192:193], pvs[0:64, 193:194]
            for k0, ks in state_io(si_gd, so_gd, 64):
                Sv, sk = load_slice(si_gd, k0, ks)
                T = Tview(ks)
                TT("dve", T, Sv, bc_l(kB[:, k0:k0 + ks], 64), ALU.mult, [sk, "f4"], [Tkey])
                RED(op_sb, T.rearrange("p k v -> p v k"), [Tkey], ["tmpS"])
                if k0 == 0:
                    CP("dve", w_sb, op_sb, ["tmpS"], ["tmpS"])
                else:
                    TT("dve", w_sb, w_sb, op_sb, ALU.add, ["tmpS"], ["tmpS"])
            TS("dve", w_sb, w_sb, egB, ALU.mult, ["tmpS", "f4"], ["tmpS"])
            TT("dve", u_sb, vB, w_sb, ALU.subtract, ["f4", "tmpS"], ["tmpS"])
            TS("dve", u_sb, u_sb, btB, ALU.mult, ["tmpS", "f4"], ["tmpS"])
            step_plain(si_gd, so_gd, 64, qB, kB, u_sb, sc=egB)
            po, pok = unselect(o_sb, "tmpS")
            head_norm(po[0:n, 0:256], pok, slice(256, 512), slice(256, 512))

            conv_tok(1, [(2056, 2568, 0), (2568, 2824, 512)], si_sc, so_sc, True)
            TT("dve", v3(f2[0:n, 0:256], 4), v3(f1[0:n, 0:256], 4), bc_l(dtb[0:n, 4:8], 64), ALU.mult, ["f1", "dtb"], ["f2"])
            select([(0, 128, lambda hd: (f1[0:n, 512 + (hd // 2) * 128:512 + (hd // 2) * 128 + 128], "f1")),
                    (128, 128, lambda hd: (f1[0:n, 256 + (hd // 2) * 128:256 + (hd // 2) * 128 + 128], "f1")),
                    (256, 64, lambda hd: (f2[0:n, hd * 64:hd * 64 + 64], "f2")),
                    (320, 1, lambda hd: (egc[0:n, 4 + hd:5 + hd], "egc"))])
            step_plain(si_sd, so_sd, 128, pvs[0:64, 0:128], pvs[0:64, 128:256], pvs[0:64, 256:320], sc=pvs[0:64, 320:321])
            po, pok = unselect(o_sb, "tmpS")
            TT("dve", v3(f3[0:n, 0:256], 4), v3(f1[0:n, 0:256], 4), bc_l(prm[0:n, 1296:1300], 64), ALU.mult, ["f1", "prm"], ["f3"])
            TT("dve", f3[0:n, 0:256], f3[0:n, 0:256], po[0:n, 0:256], ALU.add, ["f3", pok], ["f3"])
            TT("dve", f3[0:n, 0:256], f3[0:n, 0:256], gs[0:n, 512:768], ALU.mult, ["f3", gsk], ["f3"])
            ACT(f3[0:n, 256:512], f3[0:n, 0:256], AF.Square, ["f3"], ["f3"])
            RED(st4[0:n, 4:6], v3(f3[0:n, 256:512], 2), ["f3"], ["st4"])
            rsqrt_act(st4[0:n, 8:10], st4[0:n, 4:6], 1.0 / 128, ["st4"], ["st4"])
            TT("dve", v3(f3[0:n, 0:256], 2), v3(f3[0:n, 0:256], 2), bc_l(st4[0:n, 8:10], 128), ALU.mult, ["f3", "st4"], ["f3"])
            TT("dve", y_bf[0:n, 512:768], f3[0:n, 0:256], prm[0:n, 512:768], ALU.mult, ["f3", "prm"], ["y_bf"])

            for half in range(2):
                pt, pk = bank()
                for kk in range(4):
                    kc = half * 4 + kk
                    MM(pt[:, kk * 128:kk * 128 + n], y_bf[0:n, kc * 128:(kc + 1) * 128], ident_bf[0:n, 0:n],
                       ["y_bf", "ident_bf"], [pk])
                CP("act", yT[:, half * 4:half * 4 + 4, 0:n], v3(pt[:, :], 4)[:, :, 0:n], [pk], ["yT"])
            for cg in range(2):
                pt, pk = bank()
                for kc in range(8):
                    MM(pt[0:n, :], yT[:, kc, 0:n], wout[:, kc, cg * 512:(cg + 1) * 512], ["yT", "wout"], [pk],
                       start=(kc == 0), stop=(kc == 7))
                TT("dve", hs[:, cg * 512:(cg + 1) * 512], hs[:, cg * 512:(cg + 1) * 512], pt[0:n, :], ALU.add, ["hs", pk], ["hs"])
            if last:
                ACT(hn_bf[0:n, :], hs[:, :], AF.Square, ["hs"], ["hn_bf", "st4"], accum=st4[0:n, 0:1])
                rsqrt_act(st4[0:n, 1:2], st4[0:n, 0:1], 1.0 / D, ["st4"], ["st4"])
                STT(hs[:, :], hs[:, :], st4[0:n, 1:2], finw[0:n, :], ALU.mult, ALU.mult, ["hs", "st4", "lgt"], ["hs"])
                DMA(y_s, hs[:, :], ["hs"], [])

        for l in range(depth):
            load_layer(l)
            if not _os0.environ.get("NO_SAMPLE"):
                sample_fwd(l, l == depth - 1)
            if ntiles > 0:
                stage0(l, 0)
                stage0_pe(l, 0)
            for t in range(ntiles):
                more = t + 1 < ntiles
                tile_fwd(l, t, (lambda l_=l, t_=t: stage0(l_, t_ + 1)) if more else None,
                         (lambda l_=l, t_=t: stage0_pe(l_, t_ + 1)) if more else None)

        if _AUDIT:
            for b_ in sorted(_bad, key=str):
                print("AUDIT missing key:", b_)
        P.emit(es)
    return nc


_NC_CACHE = {}


def _prep_inputs(inp, c):
    f = np.float32
    prm = np.zeros((DEPTH, 128, NPRM), f)
    for l in range(DEPTH):
        row = np.concatenate([inp["hgrn_norm_w"][l], inp["gdn_norm_w"][l], inp["ssd_norm_w"][l], inp["ret_norm_w"][l],
                              inp["ret_norm_b"][l], inp["gdn_a_log"][l], inp["ssd_a_log"][l], inp["gdn_dt_bias"][l],
                              inp["ssd_dt_bias"][l], inp["ssd_d"][l]]).astype(f)
        prm[l] = np.broadcast_to(row[None, :], (128, NPRM))
    lgt = np.ascontiguousarray(np.broadcast_to(inp["hgrn_lb_logits"].reshape(1, 1024), (128, 1024))).astype(f)
    finw = np.ascontiguousarray(np.broadcast_to(inp["final_norm_w"].reshape(1, 1024), (128, 1024))).astype(f)
    featp = np.zeros((128, 32 + 192), f)
    featp[:, 0:32] = inp["norm_w"].reshape(DEPTH, 8, 128).transpose(2, 0, 1).reshape(128, 32)
    for l in range(DEPTH):
        for cv, key in enumerate(("gdn_conv_w", "ssd_conv_w")):
            w = inp[key][l].reshape(4, 6, 128)
            featp[:, 32 + l * 48 + cv * 24:32 + l * 48 + cv * 24 + 24] = w.transpose(2, 1, 0).reshape(128, 24)
    cbias = np.ascontiguousarray(inp["ssd_conv_b"].reshape(1, DEPTH * 768)).astype(f)
    cw = np.stack([inp["gdn_conv_w"], inp["ssd_conv_w"]], 1).astype(f)
    cwrow = np.ascontiguousarray(np.broadcast_to(cw[:, :, :, None, :], (DEPTH, 2, 4, NS, 768)))
    cbrow = np.ascontiguousarray(np.broadcast_to(inp["ssd_conv_b"].astype(f)[:, None, :], (DEPTH, NS, 768)))
    return {
        "xp": np.ascontiguousarray(inp["x_prompt"][c]).astype(f),
        "meta": np.ascontiguousarray(inp["meta_tokens"]).astype(f),
        "w_in": np.ascontiguousarray(inp["w_in"]).astype(f),
        "w_out": np.ascontiguousarray(inp["w_out"]).astype(f),
        "cst": CST, "rot": ROT, "prm": prm, "lgt": lgt, "finw": finw, "featp": featp, "cbias": cbias,
        "cwrow": cwrow, "cbrow": cbrow, **_sample_inputs(inp, c),
    }


def _sample_inputs(inp, c):
    f = np.float32
    sl = slice(c * NS, (c + 1) * NS)
    return {
        "xs_in": np.ascontiguousarray(inp["x_sample"][sl, 0, :]).astype(f),
        "si_hg": np.ascontiguousarray(inp["state_hgrn"][:, sl]).astype(f),
        "si_gd": np.ascontiguousarray(inp["state_gdn"][:, sl]).astype(f),
        "si_gc": np.ascontiguousarray(inp["state_gdn_conv"][:, sl]).astype(f),
        "si_sd": np.ascontiguousarray(inp["state_ssd"][:, sl]).astype(f),
        "si_sc": np.ascontiguousarray(inp["state_ssd_conv"][:, sl]).astype(f),
        "si_rt": np.ascontiguousarray(inp["state_ret"][:, sl]).astype(f),
    }


def kernel(**inp):
    inp = {k: np.asarray(v) for k, v in inp.items()}
    if "nc" not in _NC_CACHE:
        _NC_CACHE["nc"] = build_program()
    nc = _NC_CACHE["nc"]
    shared = None
    in_maps = []
    for c in range(8):
        m = _prep_inputs(inp, c) if shared is None else dict(shared)
        if shared is None:
            shared = m
        else:
            m["xp"] = np.ascontiguousarray(inp["x_prompt"][c]).astype(np.float32)
            m.update(_sample_inputs(inp, c))
        in_maps.append(m)
    res = run_bass_kernel_spmd(nc, in_maps, core_ids=list(range(8)))
    R = res.results
    y_prompt = np.stack([R[c]["y_p"] for c in range(8)], 0)
    def stk(name):
        return np.ascontiguousarray(np.stack([R[c][name] for c in range(8)], 1))
    y_sample = np.concatenate([R[c]["y_s"] for c in range(8)], 0)[:, None, :]
    def cat(name):
        return np.ascontiguousarray(np.concatenate([R[c][name] for c in range(8)], 1))
    outs = (y_prompt, np.ascontiguousarray(y_sample),
            stk("st_hg"), stk("st_gd"), stk("st_gc"), stk("st_sd"), stk("st_sc"), stk("st_rt"),
            cat("so_hg"), cat("so_gd"), cat("so_gc"), cat("so_sd"), cat("so_sc"), cat("so_rt"))
    return outs
```
